# Optimizing a Trainium2 kernel written in Bass

```python
import jax, jax.numpy as jnp
from jax import lax
import numpy as np

D_MODEL = 2048
BATCH = 4
SEQ = 4096
DEPTH = 2

N_META = 16
CHUNK = 64
EPS = 1e-6

A_HEAD_DIM = 64
A_WIDTH = D_MODEL // 4
A_HEADS = A_WIDTH // A_HEAD_DIM
A_DECAY_RANK = 32
A_ICL_RANK = 32
A_GATE_RANK = 96
A_GN_EPS = 64e-5
B_DV = 128
B_WIDTH = D_MODEL // 4
B_HEADS = B_WIDTH // B_DV
B_DK = B_DV // 2
B_QK = B_HEADS * B_DK
B_GATE_RANK = 16
B_GATE_NORMALIZER = 16.0
C_DV = 256
C_WIDTH = D_MODEL // 2
C_HEADS = C_WIDTH // C_DV
C_DQK = C_DV // 2
C_QK = C_HEADS * C_DQK
C_CONV = 4
C_GATE_CAP = 15.0

MIX_WIDTH = A_WIDTH + B_WIDTH + C_WIDTH
A_COLS = 3 * A_WIDTH + A_DECAY_RANK + A_ICL_RANK + A_GATE_RANK
B_COLS = 2 * B_QK + B_WIDTH + B_GATE_RANK + B_WIDTH
C_COLS = 2 * C_QK + C_WIDTH + 2 * C_HEADS + C_WIDTH
N_IN = A_COLS + B_COLS + C_COLS
D_FF = ((8 * D_MODEL // 3 + 255) // 256) * 256

kernel_name = "hymba_rwkv7_gla_mlstm_trunk"


def _split(a, sizes):
    offs = [int(o) for o in np.cumsum(sizes)[:-1]]
    return jnp.split(a, offs, axis=-1)


def rmsnorm(x, g):
    xf = x.astype(jnp.float32)
    y = xf * lax.rsqrt(jnp.mean(xf * xf, axis=-1, keepdims=True) + EPS)
    return (y * g.astype(jnp.float32)).astype(x.dtype)


def _heads(a, n_heads):
    b, l = a.shape[:2]
    return a.reshape(b, l, n_heads, -1).transpose(0, 2, 1, 3)


def _unheads(a):
    b, h, l, d = a.shape
    return a.transpose(0, 2, 1, 3).reshape(b, l, h * d)


def _softcap(z, cap):
    return cap * jnp.tanh(z / cap)


def _token_shift(p):
    return jnp.pad(p, ((0, 0), (1, 0), (0, 0)))[:, :-1]


def _causal_conv(x, w, b):
    k_w = w.shape[0]
    l = x.shape[1]
    xp = jnp.pad(x, ((0, 0), (k_w - 1, 0), (0, 0)))
    y = b
    for j in range(k_w):
        y = y + xp[:, j:j + l] * w[j]
    return y


def _to_chunks(a):
    b, h, t = a.shape[:3]
    a = a.reshape((b, h, t // CHUNK, CHUNK) + a.shape[3:])
    return jnp.moveaxis(a, 2, 0)


def _from_chunks(y):
    nc, b, h, c, e = y.shape
    return jnp.moveaxis(y, 0, 2).reshape(b, h, nc * c, e)


def _run_chunked(step, state0, xs):
    meta = tuple(a[:, :, :N_META] for a in xs)
    real = tuple(_to_chunks(a[:, :, N_META:]) for a in xs)
    state, y_meta = step(state0, meta)
    _, y_real = lax.scan(step, state, real)
    return jnp.concatenate([y_meta, _from_chunks(y_real)], axis=2)


def rwkv7_mixer(p, mu, w0, w2, a0, a2, g2, k_k, k_a, r_k, ln_w, ln_b):
    dt = p.dtype
    p = p.astype(jnp.float32)
    bsz, l = p.shape[:2]
    p = p + mu * (_token_shift(p) - p)
    r, k, v, xw, xa, xg = _split(p, [A_WIDTH, A_WIDTH, A_WIDTH, A_DECAY_RANK, A_ICL_RANK, A_GATE_RANK])
    w_log = -jax.nn.softplus(-(w0 + jnp.tanh(xw) @ w2)) - 0.5
    decay = jnp.exp(-jnp.exp(w_log))
    a = jax.nn.sigmoid(a0 + xa @ a2)
    g = jax.nn.sigmoid(xg) @ g2
    hs = lambda t: t.reshape(bsz, l, A_HEADS, A_HEAD_DIM)
    kk = hs(k * k_k)
    kk = kk / jnp.maximum(jnp.linalg.norm(kk, axis=-1, keepdims=True), 1e-12)
    k = k * (1.0 + (a - 1.0) * k_a)
    r4, k4, v4, w4, a4 = hs(r), hs(k), hs(v), hs(decay), hs(a)
    bb = kk * a4
    tm = lambda t: jnp.moveaxis(t, 1, 0)

    def step(S, inp):
        r_t, w_t, k_t, v_t, kk_t, b_t = inp
        sa = jnp.einsum('bhvk,bhk->bhv', S, -kk_t)
        S = S * w_t[:, :, None, :] + sa[..., None] * b_t[:, :, None, :] + v_t[..., None] * k_t[:, :, None, :]
        return S, jnp.einsum('bhvk,bhk->bhv', S, r_t)

    S0 = jnp.zeros((bsz, A_HEADS, A_HEAD_DIM, A_HEAD_DIM), jnp.float32)
    _, y = lax.scan(step, S0, (tm(r4), tm(w4), tm(k4), tm(v4), tm(kk), tm(bb)))
    y = jnp.moveaxis(y, 0, 1)
    mean = jnp.mean(y, axis=-1, keepdims=True)
    var = jnp.mean(jnp.square(y - mean), axis=-1, keepdims=True)
    y = (y - mean) * lax.rsqrt(var + A_GN_EPS)
    y = y * ln_w.reshape(A_HEADS, A_HEAD_DIM) + ln_b.reshape(A_HEADS, A_HEAD_DIM)
    bonus = jnp.sum(r4 * k4 * r_k, axis=-1, keepdims=True) * v4
    out = (y + bonus).reshape(bsz, l, A_WIDTH) * g
    return out.astype(dt)


def _gla_step(S, inp):
    q, k, v, lg = inp
    c = q.shape[2]
    b = jnp.cumsum(lg, axis=2)
    b_last = b[:, :, -1:]
    o_inter = jnp.einsum('bhcd,bhde->bhce', q * jnp.exp(b), S)
    mask = jnp.tril(jnp.ones((c, c), bool))[:, :, None]
    dec = jnp.exp(jnp.where(mask, b[:, :, :, None, :] - b[:, :, None, :, :], -jnp.inf))
    scores = jnp.einsum('bhid,bhjd,bhijd->bhij', q, k, dec)
    o = o_inter + jnp.einsum('bhij,bhje->bhie', scores, v)
    S = jnp.exp(b_last[:, :, 0])[..., None] * S + jnp.einsum('bhcd,bhce->bhde', k * jnp.exp(b_last - b), v)
    return S, o


def gla_mixer(p, a2, ab, norm_w):
    dt = p.dtype
    p = p.astype(jnp.float32)
    bsz = p.shape[0]
    q, k, v, xa, g = _split(p, [B_QK, B_QK, B_WIDTH, B_GATE_RANK, B_WIDTH])
    q = _heads(q, B_HEADS) * (B_DK ** -0.5)
    k = _heads(k, B_HEADS)
    v = _heads(v, B_HEADS)
    lg = _heads(jax.nn.log_sigmoid(xa @ a2 + ab) / B_GATE_NORMALIZER, B_HEADS)
    S0 = jnp.zeros((bsz, B_HEADS, B_DK, B_DV), jnp.float32)
    o = _run_chunked(_gla_step, S0, (q, k, v, lg))
    o = o * lax.rsqrt(jnp.mean(o * o, axis=-1, keepdims=True) + EPS) * norm_w
    out = _unheads(o) * jax.nn.silu(g)
    return out.astype(dt)


def _mlstm_step(state, inp):
    Cm, n, m = state
    q, k, v, ig, lf = inp
    c = q.shape[2]
    b = jnp.cumsum(lf, axis=-1)
    mask = jnp.tril(jnp.ones((c, c), bool))
    log_w = jnp.where(mask, b[..., :, None] - b[..., None, :] + ig[..., None, :], -jnp.inf)
    log_prev = b + m[..., None]
    m_t = jnp.maximum(log_prev, jnp.max(log_w, axis=-1))
    w = jnp.exp(log_w - m_t[..., None])
    w_prev = jnp.exp(log_prev - m_t)
    s = jnp.einsum('bhtd,bhjd->bhtj', q, k) * w
    num = w_prev[..., None] * jnp.einsum('bhtd,bhde->bhte', q, Cm) + jnp.einsum('bhtj,bhje->bhte', s, v)
    den = w_prev * jnp.einsum('bhtd,bhd->bht', q, n) + jnp.sum(s, axis=-1)
    h = num / jnp.maximum(jnp.abs(den), jnp.exp(-m_t))[..., None]
    m_new = m_t[..., -1]
    w_end = jnp.exp(b[..., -1:] - b + ig - m_new[..., None])
    f_end = jnp.exp(b[..., -1] + m - m_new)
    Cm = f_end[..., None, None] * Cm + jnp.einsum('bhc,bhcd,bhce->bhde', w_end, k, v)
    n = f_end[..., None] * n + jnp.einsum('bhc,bhcd->bhd', w_end, k)
    return (Cm, n, m_new), h


def mlstm_mixer(p, conv_w, conv_b, ib, fb, norm_w):
    dt = p.dtype
    p = p.astype(jnp.float32)
    bsz = p.shape[0]
    q, k, v, ig, fg, o = _split(p, [C_QK, C_QK, C_WIDTH, C_HEADS, C_HEADS, C_WIDTH])
    qk = jax.nn.silu(_causal_conv(jnp.concatenate([q, k], axis=-1), conv_w, conv_b))
    q, k = _split(qk, [C_QK, C_QK])
    q = _heads(q, C_HEADS)
    k = _heads(k, C_HEADS) * (C_DQK ** -0.5)
    v = _heads(v, C_HEADS)
    ig = jnp.swapaxes(_softcap(ig + ib, C_GATE_CAP), 1, 2)
    lf = jnp.swapaxes(jax.nn.log_sigmoid(_softcap(fg + fb, C_GATE_CAP)), 1, 2)
    state0 = (jnp.zeros((bsz, C_HEADS, C_DQK, C_DV), jnp.float32),
              jnp.zeros((bsz, C_HEADS, C_DQK), jnp.float32),
              jnp.zeros((bsz, C_HEADS), jnp.float32))
    h = _run_chunked(_mlstm_step, state0, (q, k, v, ig, lf))
    h = h * lax.rsqrt(jnp.mean(h * h, axis=-1, keepdims=True) + EPS)
    out = _unheads(h) * norm_w * jax.nn.sigmoid(o)
    return out.astype(dt)


def setup_inputs(seed: int = 0) -> dict:
    key = jax.random.key(seed)
    ks = iter(jax.random.split(key, 40))
    nrm = lambda shape, s: jax.random.normal(next(ks), shape, jnp.float32) * s
    uni = lambda shape, lo, hi: jax.random.uniform(next(ks), shape, jnp.float32, lo, hi)
    gain = lambda shape: 1.0 + nrm(shape, 0.05)
    L = DEPTH
    return {
        "x": nrm((BATCH, SEQ, D_MODEL), 1.0),
        "meta_tokens": nrm((N_META, D_MODEL), 1.0),
        "norm_mix": gain((L, D_MODEL)),
        "w_in": nrm((L, D_MODEL, N_IN), D_MODEL ** -0.5),
        "rw_mu": uni((L, A_COLS), 0.0, 1.0),
        "rw_w0": uni((L, A_WIDTH), -5.0, 0.0),
        "rw_w2": nrm((L, A_DECAY_RANK, A_WIDTH), 0.1),
        "rw_a0": nrm((L, A_WIDTH), 0.1),
        "rw_a2": nrm((L, A_ICL_RANK, A_WIDTH), 0.5 * A_ICL_RANK ** -0.5),
        "rw_g2": nrm((L, A_GATE_RANK, A_WIDTH), A_GATE_RANK ** -0.5),
        "rw_kk": 0.85 + nrm((L, A_WIDTH), 0.05),
        "rw_ka": 1.0 + nrm((L, A_WIDTH), 0.05),
        "rw_rk": nrm((L, A_HEADS, A_HEAD_DIM), 0.1),
        "rw_ln_w": gain((L, A_WIDTH)),
        "rw_ln_b": nrm((L, A_WIDTH), 0.02),
        "gla_a2": nrm((L, B_GATE_RANK, B_QK), B_GATE_RANK ** -0.5),
        "gla_ab": nrm((L, B_QK), 0.1),
        "gla_norm": gain((L, B_DV)),
        "ml_conv_w": nrm((L, C_CONV, 2 * C_QK), C_CONV ** -0.5),
        "ml_conv_b": nrm((L, 2 * C_QK), 0.02),
        "ml_ib": nrm((L, C_HEADS), 0.1),
        "ml_fb": uni((L, C_HEADS), 3.0, 6.0),
        "ml_norm": gain((L, C_WIDTH)),
        "w_out": nrm((L, MIX_WIDTH, D_MODEL), MIX_WIDTH ** -0.5),
        "norm_ffn": gain((L, D_MODEL)),
        "ffn_w1": nrm((L, D_MODEL, D_FF), D_MODEL ** -0.5),
        "ffn_w3": nrm((L, D_MODEL, D_FF), D_MODEL ** -0.5),
        "ffn_w2": nrm((L, D_FF, D_MODEL), D_FF ** -0.5),
        "norm_final": gain((D_MODEL,)),
    }


def reference(x, meta_tokens, norm_mix, w_in, rw_mu, rw_w0, rw_w2, rw_a0, rw_a2, rw_g2,
              rw_kk, rw_ka, rw_rk, rw_ln_w, rw_ln_b, gla_a2, gla_ab, gla_norm,
              ml_conv_w, ml_conv_b, ml_ib, ml_fb, ml_norm, w_out, norm_ffn,
              ffn_w1, ffn_w3, ffn_w2, norm_final):
    bsz = x.shape[0]
    meta = jnp.broadcast_to(meta_tokens[None].astype(x.dtype), (bsz, N_META, D_MODEL))
    h = jnp.concatenate([meta, x], axis=1)
    for l in range(DEPTH):
        u = rmsnorm(h, norm_mix[l])
        p = u @ w_in[l]
        p_a, p_b, p_c = _split(p, [A_COLS, B_COLS, C_COLS])
        y_a = rwkv7_mixer(p_a, rw_mu[l], rw_w0[l], rw_w2[l], rw_a0[l], rw_a2[l], rw_g2[l],
                          rw_kk[l], rw_ka[l], rw_rk[l], rw_ln_w[l], rw_ln_b[l])
        y_b = gla_mixer(p_b, gla_a2[l], gla_ab[l], gla_norm[l])
        y_c = mlstm_mixer(p_c, ml_conv_w[l], ml_conv_b[l], ml_ib[l], ml_fb[l], ml_norm[l])
        h = h + jnp.concatenate([y_a, y_b, y_c], axis=-1) @ w_out[l]
        u = rmsnorm(h, norm_ffn[l])
        h = h + (jax.nn.silu(u @ ffn_w1[l]) * (u @ ffn_w3[l])) @ ffn_w2[l]
    return rmsnorm(h, norm_final)[:, N_META:]
```

```python
import contextlib
import numpy as np
import concourse.bass as bass
import concourse.mybir as mybir

F32 = mybir.dt.float32
BF16 = mybir.dt.bfloat16
AF = mybir.ActivationFunctionType
ALU = mybir.AluOpType
AX = mybir.AxisListType

PE, ACT, DVE, POOL, SP = "pe", "act", "dve", "pool", "sp"
ENGS = [PE, ACT, DVE, POOL, SP]
SEG = 30000
NSLOT = 6


class Buf:
    __slots__ = ("name", "w", "ws", "rs")

    def __init__(self, name=""):
        self.name = name
        self.w = None
        self.ws = {}
        self.rs = {}


def _key(o):
    return (o.eng, o.slot if o.dma else None)


class Op:
    __slots__ = ("eng", "fn", "deps", "dma", "idx", "signal", "ev", "slot", "name")

    def __init__(self, eng, fn, dma):
        self.eng = eng
        self.fn = fn
        self.dma = dma
        self.deps = []
        self.signal = False
        self.ev = None
        self.slot = None
        self.name = ""


class Prog:
    def __init__(self, nc, same_engine_sync=True):
        self.nc = nc
        self.ops = {e: [] for e in ENGS}
        self.same = same_engine_sync
        self.ndma = {e: 0 for e in ENGS}
        self.final_deps = []
        self.pending_barrier = None

    def op(self, eng, fn, reads=(), writes=(), dma=False, name="", pwrites=()):
        o = Op(eng, fn, dma)
        o.name = name
        if dma:
            o.slot = self.ndma[eng] % NSLOT
            self.ndma[eng] += 1
            o.signal = True
        o.idx = len(self.ops[eng])
        deps = []
        for r in reads:
            if r.w is not None:
                deps.append(r.w)
            deps.extend(r.ws.values())
        for w in writes:
            if w.w is not None:
                deps.append(w.w)
            deps.extend(w.ws.values())
            deps.extend(w.rs.values())
        for w in pwrites:
            if w.w is not None:
                deps.append(w.w)
            deps.extend(w.rs.values())
        if self.pending_barrier and self.pending_barrier.get(eng):
            deps.extend(self.pending_barrier[eng])
            self.pending_barrier[eng] = []
        best = {}
        for d in deps:
            if d is o:
                continue
            if d.eng == eng and not d.dma:
                if eng == PE or not self.same:
                    continue
            k = _key(d)
            if k not in best or best[k].idx < d.idx:
                best[k] = d
        for d in best.values():
            o.deps.append(d)
            d.signal = True
        for w in writes:
            w.w = o
            w.ws = {}
            w.rs = {}
        for w in pwrites:
            w.ws[_key(o)] = o
        for r in reads:
            r.rs[_key(o)] = o
        self.ops[eng].append(o)
        return o

    def dma(self, eng, out, in_, reads=(), writes=(), pwrites=(), **kw):
        return self.op(eng, lambda e: e.dma_start(out=out, in_=in_, **kw), reads, writes, dma=True, pwrites=pwrites)

    def finish(self, bufs):
        for b in bufs:
            for o in ([b.w] if b.w is not None else []) + list(b.ws.values()):
                self.final_deps.append(o)
                o.signal = True

    def emit(self):
        nc = self.nc
        with contextlib.ExitStack() as st:
            csem = {}
            for e in (PE, ACT, DVE, POOL):
                n = sum(1 for o in self.ops[e] if o.signal and not o.dma)
                nseg = n // SEG + 1
                csem[e] = [st.enter_context(nc.semaphore(f"c_{e}_{i}")) for i in range(nseg)]
            dsem = {}
            for e in (ACT, POOL, SP):
                if self.ndma[e] > 0:
                    dsem[e] = [st.enter_context(nc.semaphore(f"d_{e}_{i}")) for i in range(NSLOT)]
            for e in ENGS:
                cnt = 0
                dcur = [0] * NSLOT
                for o in self.ops[e]:
                    if o.dma:
                        prev = dcur[o.slot]
                        dcur[o.slot] += 16
                        o.ev = (dsem[e][o.slot], dcur[o.slot], prev)
                    elif o.signal:
                        seg, v = divmod(cnt, SEG)
                        o.ev = (csem[e][seg], v + 1, None)
                        cnt += 1
            block = st.enter_context(nc.Block())
            handles = {PE: block.tensor, ACT: block.scalar, DVE: block.vector,
                       POOL: block.gpsimd, SP: block.sync}
            for e in ENGS:
                ops = self.ops[e]
                fdeps = self.final_deps if e == SP else []
                if not ops and not fdeps:
                    continue

                def body(eng, ops=ops, fdeps=fdeps):
                    known = {}

                    def wait(sem, val):
                        k = id(sem)
                        if known.get(k, 0) >= val:
                            return
                        eng.wait_ge(sem, val)
                        known[k] = val

                    for o in ops:
                        for d in o.deps:
                            wait(d.ev[0], d.ev[1])
                        if o.dma and o.ev[2] > 0:
                            wait(o.ev[0], o.ev[2])
                        ins = o.fn(eng)
                        if o.dma:
                            ins.then_inc(o.ev[0], 16)
                        elif o.signal:
                            ins.then_inc(o.ev[0], 1)
                    for d in fdeps:
                        wait(d.ev[0], d.ev[1])

                handles[e](body)


def _barrier(self):
    lasts = []
    for e in ENGS:
        ops = self.ops[e]
        if not ops:
            continue
        for o in reversed(ops):
            if not o.dma:
                lasts.append(o)
                break
        seen = set()
        for o in reversed(ops):
            if o.dma and o.slot not in seen:
                seen.add(o.slot)
                lasts.append(o)
            if len(seen) == NSLOT:
                break
    for o in lasts:
        o.signal = True
    self.pending_barrier = {e: list(lasts) for e in ENGS}


Prog.barrier = _barrier


class Ring:
    def __init__(self, tiles):
        self.tiles = tiles
        self.bufs = [Buf() for _ in tiles]
        self.i = 0

    def next(self):
        k = self.i % len(self.tiles)
        self.i += 1
        return self.tiles[k], self.bufs[k]


class _HP:
    pass
hp = _HP()


import numpy as np

A0, B0, C0 = 0, 1696, 3248


def fm_blocks():
    blks = []
    for seg in range(3):
        for i in range(4):
            blks.append(list(range(A0 + seg * 512 + i * 128, A0 + seg * 512 + (i + 1) * 128)))
    blks.append(list(range(A0 + 1536, A0 + 1600)))
    blks.append(list(range(A0 + 1600, A0 + 1696)))
    for seg in range(2):
        for i in range(2):
            blks.append(list(range(B0 + seg * 256 + i * 128, B0 + seg * 256 + (i + 1) * 128)))
    blks.append(list(range(B0 + 1024, B0 + 1040)))
    for seg in range(2):
        for i in range(4):
            blks.append(list(range(C0 + seg * 512 + i * 128, C0 + seg * 512 + (i + 1) * 128)))
    blks.append(list(range(C0 + 2048, C0 + 2056)))
    assert len(blks) == 28
    return blks


def tm_cols():
    cols = []
    cols += list(range(B0 + 512, B0 + 1024))
    cols += list(range(B0 + 1040, B0 + 1552))
    cols += list(range(C0 + 1024, C0 + 2048))
    cols += list(range(C0 + 2056, C0 + 3080))
    assert len(cols) == 3072
    return cols


def tile_k(W, ncol=512):
    K = W.shape[0]
    return np.ascontiguousarray(W.reshape(K // 128, 128, ncol).transpose(1, 0, 2).reshape(128, (K // 128) * ncol))


def prep_win(w):
    blks = fm_blocks()
    out = np.zeros((13, 128, 8192), np.float32)
    for wi in range(7):
        Wt = np.zeros((2048, 512), np.float32)
        for bi in range(4):
            cols = blks[wi * 4 + bi]
            Wt[:, bi * 128:bi * 128 + len(cols)] = w[:, cols]
        out[wi] = tile_k(Wt)
    tc = tm_cols()
    for ci in range(6):
        out[7 + ci] = tile_k(w[:, tc[ci * 512:(ci + 1) * 512]])
    return out


def prep_sq(w, ncb):
    return np.stack([tile_k(w[:, i * 512:(i + 1) * 512]) for i in range(ncb)])


def prep_w2(w):
    out = np.zeros((4, 4, 128, 11 * 512), np.float32)
    for cb in range(4):
        for pc in range(4):
            out[cb, pc] = tile_k(w[pc * 1408:(pc + 1) * 1408, cb * 512:(cb + 1) * 512])
    return out


def gT(g):
    return np.ascontiguousarray(g.reshape(16, 128).T)


for _n in ['fm_blocks','tm_cols','tile_k','prep_win','prep_sq','prep_w2','gT']:
    setattr(hp, _n, globals()[_n])


import contextlib

D = 2048
NTOK = 2064
NMETA = 16
TP = 2070
DFF = 5632
NFMB = 28
NTMC = 3072
EPS = 1e-6

TG = [(0, 16)] + [(16 + 512 * i, 512) for i in range(4)]
TT = [(0, 16)] + [(16 + 128 * i, 128) for i in range(16)]


def pfcol(t):
    return 3 + t if t < 16 else t + 6


class Ctx:
    pass


_uid = [0]


def sbuf(C, st, name, shape, dt=F32):
    _uid[0] += 1
    return st.enter_context(C.nc.sbuf_tensor(f"{name}_{_uid[0]}", shape, dt))


def psum(C, st, name, shape, dt=F32):
    _uid[0] += 1
    return st.enter_context(C.nc.psum_tensor(f"{name}_{_uid[0]}", shape, dt))


def make_wloader(C, st, n_stage=2, n_wb=2, stage_elems=8192):
    stage = Ring([sbuf(C, st, f"wst{i}", [128, stage_elems], F32) for i in range(n_stage)])
    return stage


def load_w(C, stage, dst_ap, dst_buf, src_ap, nelem, cast_eng=POOL):
    P = C.P
    stt, stb = stage.next()
    P.dma(SP, stt[:, 0:nelem], src_ap, writes=[stb])
    P.op(cast_eng, lambda e: e.tensor_copy(out=dst_ap, in_=stt[:, 0:nelem]), reads=[stb], writes=[dst_buf])


def phase_norm(C, st, hsrc, hbuf, gT, gbuf, uT, ubuf, ps_t, ident_bf, idbuf):
    P = C.P
    hring = Ring([sbuf(C, st, f"nh{i}", [128, D], F32) for i in range(2)])
    hnring = Ring([sbuf(C, st, f"nhn{i}", [128, D], BF16) for i in range(2)])
    junk = sbuf(C, st, "njunk", [128, D], BF16)
    jb = Buf()
    stat = Ring([sbuf(C, st, f"nst{i}", [128, 4], F32) for i in range(2)])
    mh = sbuf(C, st, "nmh", [128, 1], F32)
    mhb = Buf()
    P.op(POOL, lambda e: e.memset(mh[:], -0.5), writes=[mhb])
    for (t0, nt) in TT:
        ht, hb = hring.next()
        hn, hnb = hnring.next()
        s, sb_ = stat.next()
        P.dma(SP, ht[0:nt, :], hsrc[t0:t0 + nt, :], reads=[hbuf], writes=[hb])
        P.op(ACT, lambda e, ht=ht, s=s, nt=nt: e.activation(out=junk[0:nt, :], in_=ht[0:nt, :], func=AF.Square,
                                                            accum_out=s[0:nt, 0:1]), reads=[hb], writes=[jb, sb_])
        P.op(DVE, lambda e, s=s, nt=nt: e.tensor_scalar(out=s[0:nt, 1:2], in0=s[0:nt, 0:1], scalar1=1.0 / D, scalar2=EPS,
                                                        op0=ALU.mult, op1=ALU.add), reads=[sb_], writes=[sb_])
        P.op(POOL, lambda e, s=s, nt=nt: e.tensor_tensor(out=s[0:nt, 2:3], in0=s[0:nt, 1:2], in1=mh[0:nt, :], op=ALU.pow),
             reads=[sb_, mhb], writes=[sb_])
        P.op(DVE, lambda e, ht=ht, hn=hn, s=s, nt=nt: e.tensor_scalar(out=hn[0:nt, :], in0=ht[0:nt, :], scalar1=s[0:nt, 2:3],
                                                                      scalar2=None, op0=ALU.mult), reads=[hb, sb_], writes=[hnb])
        for half in range(2):
            pt_, ptb = ps_t.next()
            for k in range(8):
                kb = half * 8 + k
                P.op(PE, lambda e, pt_=pt_, hn=hn, kb=kb, k=k, nt=nt: e.transpose(
                    out=pt_[:, k * 128:k * 128 + nt], in_=hn[0:nt, kb * 128:(kb + 1) * 128], identity=ident_bf[0:nt, 0:nt]),
                    reads=[hnb, idbuf], writes=[ptb])
            eng = DVE if half == 0 else POOL
            if half == 0:
                P.op(DVE, lambda e, pt_=pt_, nt=nt, t0=t0, half=half: e.tensor_tensor(
                    out=uT[:, half * 8:half * 8 + 8, t0:t0 + nt],
                    in0=pt_[:].rearrange("p (k t) -> p k t", k=8)[:, :, 0:nt],
                    in1=gT[:, half * 8:half * 8 + 8].unsqueeze(2).broadcast_to([128, 8, nt]), op=ALU.mult),
                    reads=[ptb, gbuf], pwrites=[ubuf])
            else:
                P.op(DVE, lambda e, pt_=pt_, nt=nt, t0=t0, half=half: e.tensor_tensor(
                    out=uT[:, half * 8:half * 8 + 8, t0:t0 + nt],
                    in0=pt_[:].rearrange("p (k t) -> p k t", k=8)[:, :, 0:nt],
                    in1=gT[:, half * 8:half * 8 + 8].unsqueeze(2).broadcast_to([128, 8, nt]), op=ALU.mult),
                    reads=[ptb, gbuf], pwrites=[ubuf])


def phase_proj(C, st, uT, ubuf, w_dram, pf, pfbuf, pt, ptbuf, ps_mm, hist_out=None, hob=None):
    P = C.P
    stage = make_wloader(C, st)
    wb = Ring([sbuf(C, st, f"pwb{i}", [128, 16, 512], BF16) for i in range(2)])
    ev = Ring([sbuf(C, st, f"pev{i}", [128, 512], F32) for i in range(4)])
    cnt = 0
    for wi in range(13):
        wt, wbuf = wb.next()
        load_w(C, stage, wt[:].rearrange("p k c -> p (k c)"), wbuf, w_dram[wi], 8192)
        if wi < 7:
            for bi in range(4):
                blk = wi * 4 + bi
                for (t0, nt) in TG:
                    pm, pmb = ps_mm.next()
                    for kb in range(16):
                        P.op(PE, lambda e, pm=pm, wt=wt, kb=kb, bi=bi, t0=t0, nt=nt: e.matmul(
                            pm[:, 0:nt], lhsT=wt[:, kb, bi * 128:(bi + 1) * 128], rhs=uT[:, kb, t0:t0 + nt],
                            start=(kb == 0), stop=(kb == 15)), reads=[wbuf, ubuf], writes=[pmb])
                    et, eb = ev.next()
                    if cnt % 2 == 0:
                        P.op(ACT, lambda e, et=et, pm=pm, nt=nt: e.activation(out=et[:, 0:nt], in_=pm[:, 0:nt], func=AF.Copy),
                             reads=[pmb], writes=[eb])
                    else:
                        P.op(DVE, lambda e, et=et, pm=pm, nt=nt: e.tensor_copy(out=et[:, 0:nt], in_=pm[:, 0:nt]),
                             reads=[pmb], writes=[eb])
                    cnt += 1
                    c0 = pfcol(t0)
                    P.dma(POOL, pf[blk * 128:(blk + 1) * 128, c0:c0 + nt], et[:, 0:nt], reads=[eb], pwrites=[pfbuf])
                    if hist_out is not None and t0 + nt == NTOK:
                        P.dma(POOL, hist_out[blk * 128:(blk + 1) * 128, :], et[:, nt - 3:nt], reads=[eb], pwrites=[hob])
        else:
            ci = wi - 7
            for (t0, nt) in TT:
                pm, pmb = ps_mm.next()
                for kb in range(16):
                    P.op(PE, lambda e, pm=pm, wt=wt, kb=kb, t0=t0, nt=nt: e.matmul(
                        pm[0:nt, :], lhsT=uT[:, kb, t0:t0 + nt], rhs=wt[:, kb, :],
                        start=(kb == 0), stop=(kb == 15)), reads=[wbuf, ubuf], writes=[pmb])
                et, eb = ev.next()
                if cnt % 2 == 0:
                    P.op(ACT, lambda e, et=et, pm=pm, nt=nt: e.activation(out=et[0:nt, :], in_=pm[0:nt, :], func=AF.Copy),
                         reads=[pmb], writes=[eb])
                else:
                    P.op(DVE, lambda e, et=et, pm=pm, nt=nt: e.tensor_copy(out=et[0:nt, :], in_=pm[0:nt, :]),
                         reads=[pmb], writes=[eb])
                cnt += 1
                P.dma(POOL, pt[t0:t0 + nt, ci * 512:(ci + 1) * 512], et[0:nt, :], reads=[eb], pwrites=[ptbuf])


def phase_wout(C, st, y, ybuf, hsrc, hbuf, hdst, hdbuf, w_dram, uT, ubuf, ps_mm, ps_t, ident_bf, idbuf):
    P = C.P
    yr = Ring([sbuf(C, st, f"oy{i}", [128, D], BF16) for i in range(2)])
    for (t0, nt) in TT:
        yt, yb = yr.next()
        P.dma(SP, yt[0:nt, :], y[t0:t0 + nt, :], reads=[ybuf], writes=[yb])
        for half in range(2):
            pt_, ptb = ps_t.next()
            for k in range(8):
                kb = half * 8 + k
                P.op(PE, lambda e, pt_=pt_, yt=yt, kb=kb, k=k, nt=nt: e.transpose(
                    out=pt_[:, k * 128:k * 128 + nt], in_=yt[0:nt, kb * 128:(kb + 1) * 128], identity=ident_bf[0:nt, 0:nt]),
                    reads=[yb, idbuf], writes=[ptb])
            eng = ACT if half == 0 else DVE
            if half == 0:
                P.op(ACT, lambda e, pt_=pt_, nt=nt, t0=t0, half=half: e.activation(
                    out=uT[:, half * 8:half * 8 + 8, t0:t0 + nt],
                    in_=pt_[:].rearrange("p (k t) -> p k t", k=8)[:, :, 0:nt], func=AF.Copy), reads=[ptb], pwrites=[ubuf])
            else:
                P.op(DVE, lambda e, pt_=pt_, nt=nt, t0=t0, half=half: e.tensor_copy(
                    out=uT[:, half * 8:half * 8 + 8, t0:t0 + nt],
                    in_=pt_[:].rearrange("p (k t) -> p k t", k=8)[:, :, 0:nt]), reads=[ptb], pwrites=[ubuf])
    stage = make_wloader(C, st)
    wb = Ring([sbuf(C, st, f"owb{i}", [128, 16, 512], BF16) for i in range(2)])
    hr = Ring([sbuf(C, st, f"ohr{i}", [128, 512], F32) for i in range(3)])
    ev = Ring([sbuf(C, st, f"oev{i}", [128, 512], F32) for i in range(3)])
    for ci in range(4):
        wt, wbuf = wb.next()
        load_w(C, stage, wt[:].rearrange("p k c -> p (k c)"), wbuf, w_dram[ci], 8192)
        for (t0, nt) in TT:
            ho, hob = hr.next()
            P.dma(SP, ho[0:nt, :], hsrc[t0:t0 + nt, ci * 512:(ci + 1) * 512], reads=[hbuf], writes=[hob])
            pm, pmb = ps_mm.next()
            for kb in range(16):
                P.op(PE, lambda e, pm=pm, wt=wt, kb=kb, t0=t0, nt=nt: e.matmul(
                    pm[0:nt, :], lhsT=uT[:, kb, t0:t0 + nt], rhs=wt[:, kb, :],
                    start=(kb == 0), stop=(kb == 15)), reads=[wbuf, ubuf], writes=[pmb])
            et, eb = ev.next()
            P.op(DVE, lambda e, et=et, pm=pm, ho=ho, nt=nt: e.tensor_tensor(out=et[0:nt, :], in0=pm[0:nt, :], in1=ho[0:nt, :],
                                                                            op=ALU.add), reads=[pmb, hob], writes=[eb])
            P.dma(POOL, hdst[t0:t0 + nt, ci * 512:(ci + 1) * 512], et[0:nt, :], reads=[eb], pwrites=[hdbuf])


def phase_ffn1(C, st, uT, ubuf, w1_dram, w3_dram, aT, abuf, ps_mm):
    P = C.P
    stage = make_wloader(C, st)
    w1b = Ring([sbuf(C, st, f"f1w{i}", [128, 16, 512], BF16) for i in range(2)])
    w3b = Ring([sbuf(C, st, f"f3w{i}", [128, 16, 512], BF16) for i in range(2)])
    sg = Ring([sbuf(C, st, f"fsg{i}", [128, 512], F32) for i in range(3)])
    av = Ring([sbuf(C, st, f"fav{i}", [128, 512], BF16) for i in range(3)])
    for gi in range(11):
        w1t, w1buf = w1b.next()
        w3t, w3buf = w3b.next()
        load_w(C, stage, w1t[:].rearrange("p k c -> p (k c)"), w1buf, w1_dram[gi], 8192)
        load_w(C, stage, w3t[:].rearrange("p k c -> p (k c)"), w3buf, w3_dram[gi], 8192)
        for bi in range(4):
            j = gi * 4 + bi
            for gidx, (t0, nt) in enumerate(TG):
                pa, pab = ps_mm.next()
                for kb in range(16):
                    P.op(PE, lambda e, pa=pa, w1t=w1t, kb=kb, bi=bi, t0=t0, nt=nt: e.matmul(
                        pa[:, 0:nt], lhsT=w1t[:, kb, bi * 128:(bi + 1) * 128], rhs=uT[:, kb, t0:t0 + nt],
                        start=(kb == 0), stop=(kb == 15)), reads=[w1buf, ubuf], writes=[pab])
                pb_, pbb = ps_mm.next()
                for kb in range(16):
                    P.op(PE, lambda e, pb_=pb_, w3t=w3t, kb=kb, bi=bi, t0=t0, nt=nt: e.matmul(
                        pb_[:, 0:nt], lhsT=w3t[:, kb, bi * 128:(bi + 1) * 128], rhs=uT[:, kb, t0:t0 + nt],
                        start=(kb == 0), stop=(kb == 15)), reads=[w3buf, ubuf], writes=[pbb])
                s, sb_ = sg.next()
                a, ab_ = av.next()
                P.op(ACT, lambda e, s=s, pa=pa, nt=nt: e.activation(out=s[:, 0:nt], in_=pa[:, 0:nt], func=AF.Silu),
                     reads=[pab], writes=[sb_])
                P.op(DVE, lambda e, a=a, s=s, pb_=pb_, nt=nt: e.tensor_tensor(out=a[:, 0:nt], in0=pb_[:, 0:nt], in1=s[:, 0:nt],
                                                                              op=ALU.mult), reads=[pbb, sb_], writes=[ab_])
                P.dma(POOL, aT[gidx, :, j, 0:nt], a[:, 0:nt], reads=[ab_], pwrites=[abuf])


def phase_ffn2(C, st, aT, abuf, w2_dram, hsrc, hbuf, hdst, hdbuf, ps_mm):
    P = C.P
    stage = Ring([sbuf(C, st, f"gst{i}", [128, 11 * 512], F32) for i in range(2)])
    w2b = Ring([sbuf(C, st, f"gw{i}", [128, 44, 512], BF16) for i in range(1)])
    ar = Ring([sbuf(C, st, f"gar{i}", [128, 44, 512], BF16) for i in range(2)])
    hr = Ring([sbuf(C, st, f"ghr{i}", [128, 512], F32) for i in range(3)])
    ev = Ring([sbuf(C, st, f"gev{i}", [128, 512], F32) for i in range(3)])
    for cb in range(4):
        wt, wbuf = w2b.next()
        for pc in range(4):
            load_w(C, stage, wt[:, pc * 11:(pc + 1) * 11, :].rearrange("p k c -> p (k c)"), wbuf, w2_dram[cb, pc], 11 * 512)
        for gidx, (g0, gn) in enumerate(TG):
            at, atb = ar.next()
            P.dma(SP, at[:, :, 0:gn], aT[gidx, :, :, 0:gn], reads=[abuf], writes=[atb])
            for s0 in range(0, gn, 128):
                nt = min(128, gn - s0)
                t0 = g0 + s0
                ho, hob = hr.next()
                P.dma(SP, ho[0:nt, :], hsrc[t0:t0 + nt, cb * 512:(cb + 1) * 512], reads=[hbuf], writes=[hob])
                pm, pmb = ps_mm.next()
                for j in range(44):
                    P.op(PE, lambda e, pm=pm, at=at, wt=wt, j=j, s0=s0, nt=nt: e.matmul(
                        pm[0:nt, :], lhsT=at[:, j, s0:s0 + nt], rhs=wt[:, j, :],
                        start=(j == 0), stop=(j == 43)), reads=[wbuf, atb], writes=[pmb])
                et, eb = ev.next()
                P.op(DVE, lambda e, et=et, pm=pm, ho=ho, nt=nt: e.tensor_tensor(out=et[0:nt, :], in0=pm[0:nt, :], in1=ho[0:nt, :],
                                                                                op=ALU.add), reads=[pmb, hob], writes=[eb])
                P.dma(POOL, hdst[t0:t0 + nt, cb * 512:(cb + 1) * 512], et[0:nt, :], reads=[eb], pwrites=[hdbuf])


def phase_final_norm(C, st, hsrc, hbuf, gbc, gbcb, out, obuf):
    P = C.P
    hring = Ring([sbuf(C, st, f"zh{i}", [128, D], F32) for i in range(2)])
    oring = Ring([sbuf(C, st, f"zo{i}", [128, D], F32) for i in range(2)])
    junk = sbuf(C, st, "zjunk", [128, D], BF16)
    jb = Buf()
    stat = Ring([sbuf(C, st, f"zst{i}", [128, 4], F32) for i in range(2)])
    mh = sbuf(C, st, "zmh", [128, 1], F32)
    mhb = Buf()
    P.op(POOL, lambda e: e.memset(mh[:], -0.5), writes=[mhb])
    for (t0, nt) in TT[1:]:
        ht, hb = hring.next()
        ot, ob = oring.next()
        s, sb_ = stat.next()
        P.dma(SP, ht[0:nt, :], hsrc[t0:t0 + nt, :], reads=[hbuf], writes=[hb])
        P.op(ACT, lambda e, ht=ht, s=s, nt=nt: e.activation(out=junk[0:nt, :], in_=ht[0:nt, :], func=AF.Square,
                                                            accum_out=s[0:nt, 0:1]), reads=[hb], writes=[jb, sb_])
        P.op(DVE, lambda e, s=s, nt=nt: e.tensor_scalar(out=s[0:nt, 1:2], in0=s[0:nt, 0:1], scalar1=1.0 / D, scalar2=EPS,
                                                        op0=ALU.mult, op1=ALU.add), reads=[sb_], writes=[sb_])
        P.op(POOL, lambda e, s=s, nt=nt: e.tensor_tensor(out=s[0:nt, 2:3], in0=s[0:nt, 1:2], in1=mh[0:nt, :], op=ALU.pow),
             reads=[sb_, mhb], writes=[sb_])
        P.op(DVE, lambda e, ht=ht, ot=ot, s=s, nt=nt: e.scalar_tensor_tensor(
            out=ot[0:nt, :], in0=ht[0:nt, :], scalar=s[0:nt, 2:3], in1=gbc[0:nt, :], op0=ALU.mult, op1=ALU.mult),
            reads=[hb, sb_, gbcb], writes=[ob])
        P.dma(POOL, out[t0 - 16:t0 - 16 + nt, :], ot[0:nt, :], reads=[ob], pwrites=[obuf])


import contextlib

SEGS = [(3, 16, 0), (22, 2048, 16)]
LDK = 0.6065306597126334


def chunks_of(seg):
    c0, ncol, t0 = seg
    if ncol == 16:
        return [(c0, 16, t0)]
    return [(c0 + 64 * i, 64, t0 + 64 * i) for i in range(ncol // 64)]


def fix_gap(C, x, xb, hist_src, flagE, fb, tmp_ring, rows):
    P = C.P
    t, tb = tmp_ring.next()
    P.dma(SP, t[0:rows, 0:3], hist_src, writes=[tb])
    P.op(DVE, lambda e: e.scalar_tensor_tensor(out=x[0:rows, 19:22], in0=x[0:rows, 16:19], scalar=flagE[0:rows, 0:1],
                                               in1=t[0:rows, 0:3], op0=ALU.mult, op1=ALU.add), reads=[xb, tb, fb], writes=[xb])


def chunk_rel(C, out, ob, src, sb_, rows):
    P = C.P
    P.op(DVE, lambda e: e.tensor_copy(out=out[0:rows, 3:19], in_=src[0:rows, 3:19]), reads=[sb_], writes=[ob])
    P.op(DVE, lambda e: e.tensor_copy(out=out[0:rows, 22:86], in_=src[0:rows, 22:86]), reads=[sb_], writes=[ob])
    o3 = out[0:rows, 86:2070].rearrange("p (c j) -> p c j", j=64)
    s3 = src[0:rows, 86:2070].rearrange("p (c j) -> p c j", j=64)
    pv = src[0:rows, 22:2006].rearrange("p (c j) -> p c j", j=64)[:, :, 63:64].broadcast_to([rows, 31, 64])
    P.op(DVE, lambda e: e.tensor_tensor(out=o3, in0=s3, in1=pv, op=ALU.subtract), reads=[sb_], writes=[ob])


def gate_prepass(C, st, pt, ptb):
    P = C.P
    r = Ring([sbuf(C, st, f"gp{i}", [128, 1536], F32) for i in range(3)])
    for (t0, nt) in TT:
        t, tb = r.next()
        P.dma(SP, t[0:nt, 0:512], pt[t0:t0 + nt, 512:1024], reads=[ptb], writes=[tb])
        P.dma(SP, t[0:nt, 512:1536], pt[t0:t0 + nt, 2048:3072], reads=[ptb], writes=[tb])
        P.op(ACT, lambda e, t=t, nt=nt: e.activation(out=t[0:nt, 0:512], in_=t[0:nt, 0:512], func=AF.Silu), reads=[tb], writes=[tb])
        P.op(ACT, lambda e, t=t, nt=nt: e.activation(out=t[0:nt, 512:1536], in_=t[0:nt, 512:1536], func=AF.Sigmoid),
             reads=[tb], writes=[tb])
        P.dma(POOL, pt[t0:t0 + nt, 512:1024], t[0:nt, 0:512], reads=[tb], writes=[ptb])
        P.dma(POOL, pt[t0:t0 + nt, 2048:3072], t[0:nt, 512:1536], reads=[tb], writes=[ptb])


def mixer_gla(C, st, pf, pfb, pt, ptb, y, yb, prm, K):
    P = C.P
    ones, onesb, ident, idb, mask_i, mib, flagE, fb = K.ones, K.onesb, K.ident, K.idb, K.mask_i, K.mib, K.flagE, K.fb
    a2 = sbuf(C, st, "ga2", [32, 256]); a2b = Buf()
    P.op(DVE, lambda e: e.memset(a2[:], 0.0), writes=[a2b])
    nab = sbuf(C, st, "gnab", [64, 4]); nabb = Buf()
    nbc = sbuf(C, st, "gnbc", [64, 128]); nbcb = Buf()
    P.dma(SP, a2[0:16, :], prm["gla_a2"], reads=[a2b], writes=[a2b])
    P.dma(SP, nab[:], prm["gla_ab"], writes=[nabb])
    P.op(DVE, lambda e: e.tensor_scalar(out=nab[:], in0=nab[:], scalar1=-1.0, scalar2=None, op0=ALU.mult), reads=[nabb], writes=[nabb])
    P.dma(SP, nbc[:], prm["gla_normbc"], writes=[nbcb])
    xa = sbuf(C, st, "gxa", [32, TP]); xab = Buf()
    P.dma(SP, xa[:], pf[18 * 128:18 * 128 + 32, :], reads=[pfb], writes=[xab])
    q = sbuf(C, st, "gq", [64, TP]); k = sbuf(C, st, "gk", [64, TP]); sp = sbuf(C, st, "gsp", [64, TP])
    spc = sbuf(C, st, "gspc", [64, TP]); e1 = sbuf(C, st, "ge1", [64, TP]); e2 = sbuf(C, st, "ge2", [64, TP])
    qb, kb_, spb, spcb, e1b, e2b = [Buf() for _ in range(6)]
    S = sbuf(C, st, "gS", [64, 128]); Sb = Buf()
    Sin = sbuf(C, st, "gSin", [64, 128]); Sinb = Buf()
    psA = Ring([psum(C, st, f"gpa{i}", [128, 512], F32) for i in range(2)])
    psB = Ring([psum(C, st, f"gpb{i}", [128, 512], F32) for i in range(6)])
    vr = Ring([sbuf(C, st, f"gv{i}", [64, 256], F32) for i in range(3)])
    ktr = Ring([sbuf(C, st, f"gkt{i}", [64, 64], F32) for i in range(2)])
    scr = Ring([sbuf(C, st, f"gsc{i}", [64, 64], F32) for i in range(2)])
    str_ = Ring([sbuf(C, st, f"gst{i}", [64, 4], F32) for i in range(2)])
    junk = sbuf(C, st, "gjunk", [64, 128], F32); jb = Buf()
    mh = sbuf(C, st, "gmh", [64, 1], F32); mhb = Buf()
    P.op(POOL, lambda e: e.memset(mh[:], -0.5), writes=[mhb])
    t1r = Ring([sbuf(C, st, f"gt1{i}", [64, 128], F32) for i in range(2)])
    yor = Ring([sbuf(C, st, f"gyo{i}", [64, 128], BF16) for i in range(2)])
    for h in range(4):
        r0 = (14 + h // 2) * 128 + (h % 2) * 64
        r1 = (16 + h // 2) * 128 + (h % 2) * 64
        P.dma(SP, q[:], pf[r0:r0 + 64, :], reads=[pfb], writes=[qb])
        P.dma(SP, k[:], pf[r1:r1 + 64, :], reads=[pfb], writes=[kb_])
        for c0 in range(3, TP, 512):
            n = min(512, TP - c0)
            pa, pab = psA.next()
            P.op(PE, lambda e, pa=pa, c0=c0, n=n, h=h: e.matmul(pa[0:64, 0:n], lhsT=a2[0:32, h * 64:(h + 1) * 64], rhs=xa[0:32, c0:c0 + n],
                                                               start=True, stop=True), reads=[a2b, xab], writes=[pab])
            P.op(ACT, lambda e, pa=pa, c0=c0, n=n, h=h: e.activation(out=sp[:, c0:c0 + n], in_=pa[0:64, 0:n], func=AF.Exp, scale=-1.0,
                                                                    bias=nab[:, h:h + 1]), reads=[pab, nabb], writes=[spb])
        P.op(ACT, lambda e: e.activation(out=sp[:, 3:TP], in_=sp[:, 3:TP], func=AF.Ln, bias=1.0), reads=[spb], writes=[spb])
        for (c0, ncol, t0) in SEGS:
            P.op(DVE, lambda e, c0=c0, ncol=ncol: e.tensor_tensor_scan(out=spc[:, c0:c0 + ncol], data0=ones[0:64, c0:c0 + ncol],
                                                                       data1=sp[:, c0:c0 + ncol], initial=0.0, op0=ALU.mult, op1=ALU.add),
                 reads=[spb, onesb], writes=[spcb])
        chunk_rel(C, sp, spb, spc, spcb, 64)
        P.op(ACT, lambda e: e.activation(out=e1[:, 3:TP], in_=sp[:, 3:TP], func=AF.Exp, scale=-1.0 / 16), reads=[spb], writes=[e1b])
        P.op(ACT, lambda e: e.activation(out=e2[:, 3:TP], in_=sp[:, 3:TP], func=AF.Exp, scale=1.0 / 16), reads=[spb], writes=[e2b])
        P.op(DVE, lambda e: e.scalar_tensor_tensor(out=q[:, 3:TP], in0=q[:, 3:TP], scalar=0.125, in1=e1[:, 3:TP], op0=ALU.mult,
                                                   op1=ALU.mult), reads=[qb, e1b], writes=[qb])
        P.op(DVE, lambda e: e.tensor_tensor(out=k[:, 3:TP], in0=k[:, 3:TP], in1=e2[:, 3:TP], op=ALU.mult), reads=[kb_, e2b], writes=[kb_])
        P.op(DVE, lambda e: e.memset(S[:], 0.0), writes=[Sb])
        for si, seg in enumerate(SEGS):
            if si == 1:
                P.dma(SP, Sin[:], prm["sB_in"][h], writes=[Sinb])
                P.op(DVE, lambda e: e.scalar_tensor_tensor(out=S[:], in0=S[:], scalar=flagE[0:64, 0:1], in1=Sin[:], op0=ALU.mult,
                                                           op1=ALU.add), reads=[Sb, Sinb, fb], writes=[Sb])
            for (c0, n, t0) in chunks_of(seg):
                vt, vb = vr.next()
                P.dma(SP, vt[0:n, 0:128], pt[t0:t0 + n, h * 128:(h + 1) * 128], reads=[ptb], writes=[vb])
                P.dma(SP, vt[0:n, 128:256], pt[t0:t0 + n, 512 + h * 128:512 + (h + 1) * 128], reads=[ptb], writes=[vb])
                pb, pbb = psB.next()
                P.op(PE, lambda e, pb=pb, c0=c0, n=n: e.transpose(out=pb[0:n, 0:64], in_=k[:, c0:c0 + n], identity=ident[0:64, 0:64]),
                     reads=[kb_, idb], writes=[pbb])
                P.op(PE, lambda e, pb=pb, c0=c0, n=n: e.matmul(pb[0:n, 64:64 + n], lhsT=k[:, c0:c0 + n], rhs=q[:, c0:c0 + n], start=True, stop=True),
                     reads=[kb_, qb], writes=[pbb])
                kt, ktb = ktr.next()
                sc, scb = scr.next()
                P.op(ACT, lambda e, kt=kt, pb=pb, n=n: e.activation(out=kt[0:n, :], in_=pb[0:n, 0:64], func=AF.Copy), reads=[pbb], writes=[ktb])
                P.op(DVE, lambda e, sc=sc, pb=pb, n=n: e.tensor_tensor(out=sc[0:n, 0:n], in0=pb[0:n, 64:64 + n], in1=mask_i[0:n, 0:n], op=ALU.mult),
                     reads=[pbb, mib], writes=[scb])
                po, pob = psB.next()
                P.op(PE, lambda e, po=po, c0=c0, n=n: e.matmul(po[0:n, 0:128], lhsT=q[:, c0:c0 + n], rhs=S[:, :], start=True, stop=False),
                     reads=[qb, Sb], writes=[pob])
                P.op(PE, lambda e, po=po, sc=sc, vt=vt, n=n: e.matmul(po[0:n, 0:128], lhsT=sc[0:n, 0:n], rhs=vt[0:n, 0:128], start=False, stop=True),
                     reads=[scb, vb], writes=[pob])
                pc, pcb = psB.next()
                P.op(PE, lambda e, pc=pc: e.matmul(pc[0:64, 0:128], lhsT=ident[0:64, 0:64], rhs=S[:, :], start=True, stop=False),
                     reads=[idb, Sb], writes=[pcb])
                P.op(PE, lambda e, pc=pc, kt=kt, vt=vt, n=n: e.matmul(pc[0:64, 0:128], lhsT=kt[0:n, 0:64], rhs=vt[0:n, 0:128], start=False, stop=True),
                     reads=[ktb, vb], writes=[pcb])
                ce = c0 + n - 1
                P.op(DVE, lambda e, pc=pc, ce=ce: e.tensor_scalar(out=S[:], in0=pc[0:64, 0:128], scalar1=e1[:, ce:ce + 1], scalar2=None, op0=ALU.mult),
                     reads=[pcb, e1b], writes=[Sb])
                if C.emit_out and not False:
                    s_, sb2 = str_.next()
                    t1, t1b = t1r.next()
                    yo, yob = yor.next()
                    P.op(ACT, lambda e, t1=t1, po=po, n=n: e.activation(out=t1[0:n, :], in_=po[0:n, 0:128], func=AF.Copy), reads=[pob], writes=[t1b])
                    P.op(DVE, lambda e, t1=t1, n=n: e.tensor_tensor(out=junk[0:n, :], in0=t1[0:n, :], in1=t1[0:n, :], op=ALU.mult), reads=[t1b], writes=[jb])
                    P.op(DVE, lambda e, s_=s_, n=n: e.tensor_reduce(out=s_[0:n, 0:1], in_=junk[0:n, :], axis=AX.X, op=ALU.add), reads=[jb], writes=[sb2])
                    P.op(DVE, lambda e, s_=s_, n=n: e.tensor_scalar(out=s_[0:n, 1:2], in0=s_[0:n, 0:1], scalar1=1.0 / 128, scalar2=EPS, op0=ALU.mult,
                                                                   op1=ALU.add), reads=[sb2], writes=[sb2])
                    P.op(POOL, lambda e, s_=s_, n=n: e.tensor_tensor(out=s_[0:n, 2:3], in0=s_[0:n, 1:2], in1=mh[0:n, :], op=ALU.pow),
                         reads=[sb2, mhb], writes=[sb2])
                    P.op(DVE, lambda e, t1=t1, s_=s_, n=n: e.scalar_tensor_tensor(out=t1[0:n, :], in0=t1[0:n, :], scalar=s_[0:n, 2:3],
                                                                               in1=nbc[0:n, :], op0=ALU.mult, op1=ALU.mult),
                         reads=[sb2, nbcb, t1b], writes=[t1b])
                    P.op(DVE, lambda e, yo=yo, t1=t1, vt=vt, n=n: e.tensor_tensor(out=yo[0:n, :], in0=t1[0:n, :], in1=vt[0:n, 128:256], op=ALU.mult),
                         reads=[t1b, vb], writes=[yob])
                    P.dma(SP, y[t0:t0 + n, 512 + h * 128:512 + (h + 1) * 128], yo[0:n, :], reads=[yob], pwrites=[yb])
        P.dma(POOL, prm["sB_out"][h], S[:], reads=[Sb], pwrites=[K.sob])


def mixer_mlstm(C, st, pf, pfb, pt, ptb, y, yb, prm, K):
    P = C.P
    ones, onesb, ident, idb, mask_i, mib, flagE, fb = K.ones, K.onesb, K.ident, K.idb, K.mask_i, K.mib, K.flagE, K.fb
    cw = sbuf(C, st, "mcw", [128, 8, 4]); cb = sbuf(C, st, "mcb", [128, 8]); cwb = Buf()
    ib = sbuf(C, st, "mib", [4, 2]); fbb = sbuf(C, st, "mfb", [4, 2]); gbb = Buf()
    nbc = sbuf(C, st, "mnbc", [64, 1024]); nbcb = Buf()
    oh = sbuf(C, st, "moh", [4, 4, 128]); ohb = Buf()
    P.dma(SP, cw[:], prm["ml_cw"], writes=[cwb]); P.dma(SP, cb[:], prm["ml_cb"], writes=[cwb])
    P.dma(SP, ib[:, 0:1], prm["ml_ib"], writes=[gbb]); P.dma(SP, fbb[:, 0:1], prm["ml_fb"], writes=[gbb])
    P.dma(SP, nbc[:], prm["ml_normbc"], writes=[nbcb]); P.dma(SP, oh[:], prm["onehot"], writes=[ohb])
    P.op(DVE, lambda e: e.tensor_scalar(out=ib[:, 1:2], in0=ib[:, 0:1], scalar1=1.0 / 15, scalar2=None, op0=ALU.mult), reads=[gbb], writes=[gbb])
    P.op(DVE, lambda e: e.tensor_scalar(out=fbb[:, 1:2], in0=fbb[:, 0:1], scalar1=1.0 / 15, scalar2=None, op0=ALU.mult), reads=[gbb], writes=[gbb])
    gi = sbuf(C, st, "mgi", [4, TP]); gf = sbuf(C, st, "mgf", [4, TP]); SPc = sbuf(C, st, "mSP", [4, TP]); av = sbuf(C, st, "mav", [4, TP])
    MU = sbuf(C, st, "mMU", [4, TP]); MUS = sbuf(C, st, "mMUS", [4, TP])
    G8 = sbuf(C, st, "mG8", [36, TP]); FE = sbuf(C, st, "mFE", [4, 64]); min_ = sbuf(C, st, "mmin", [4, 4])
    gib, gfb_, SPb, avb, MUb, MUSb, G8b, FEb, minb = [Buf() for _ in range(9)]
    P.dma(SP, gi[:], pf[27 * 128:27 * 128 + 4, :], reads=[pfb], writes=[gib])
    P.dma(SP, gf[:], pf[27 * 128 + 4:27 * 128 + 8, :], reads=[pfb], writes=[gfb_])
    P.op(DVE, lambda e: e.memset(G8[:], 0.0), writes=[G8b])
    P.op(ACT, lambda e: e.activation(out=gi[:], in_=gi[:], func=AF.Tanh, scale=1.0 / 15, bias=ib[:, 1:2]), reads=[gib, gbb], writes=[gib])
    P.op(ACT, lambda e: e.activation(out=gf[:], in_=gf[:], func=AF.Tanh, scale=1.0 / 15, bias=fbb[:, 1:2]), reads=[gfb_, gbb], writes=[gfb_])
    P.op(ACT, lambda e: e.activation(out=gf[:], in_=gf[:], func=AF.Exp, scale=-15.0), reads=[gfb_], writes=[gfb_])
    P.op(ACT, lambda e: e.activation(out=gf[:], in_=gf[:], func=AF.Ln, bias=1.0), reads=[gfb_], writes=[gfb_])
    P.dma(SP, min_[:, 0:1], prm["mC_in"], writes=[minb])
    for si, (c0, ncol, t0) in enumerate(SEGS):
        P.op(DVE, lambda e, c0=c0, ncol=ncol: e.tensor_tensor_scan(out=SPc[:, c0:c0 + ncol], data0=ones[0:4, c0:c0 + ncol], data1=gf[:, c0:c0 + ncol],
                                                                   initial=0.0, op0=ALU.mult, op1=ALU.add), reads=[gfb_, onesb], writes=[SPb])
        P.op(DVE, lambda e, c0=c0, ncol=ncol: e.scalar_tensor_tensor(out=av[:, c0:c0 + ncol], in0=gi[:, c0:c0 + ncol], scalar=15.0,
                                                                     in1=SPc[:, c0:c0 + ncol], op0=ALU.mult, op1=ALU.add), reads=[gib, SPb], writes=[avb])
        if si == 0:
            P.op(DVE, lambda e, c0=c0, ncol=ncol: e.tensor_tensor_scan(out=MU[:, c0:c0 + ncol], data0=av[:, c0:c0 + ncol], data1=av[:, c0:c0 + ncol],
                                                                       initial=0.0, op0=ALU.max, op1=ALU.max), reads=[avb], writes=[MUb])
            P.op(DVE, lambda e: e.memset(MUS[:, 3:19], 0.0), writes=[MUSb])
            P.op(DVE, lambda e: e.tensor_tensor(out=min_[:, 1:2], in0=MU[:, 18:19], in1=SPc[:, 18:19], op=ALU.subtract), reads=[MUb, SPb, minb], writes=[minb])
            P.op(DVE, lambda e: e.scalar_tensor_tensor(out=min_[:, 2:3], in0=min_[:, 1:2], scalar=flagE[0:4, 0:1], in1=min_[:, 0:1], op0=ALU.mult,
                                                       op1=ALU.add), reads=[minb, fb], writes=[minb])
        else:
            P.op(DVE, lambda e, c0=c0, ncol=ncol: e.tensor_tensor_scan(out=MU[:, c0:c0 + ncol], data0=av[:, c0:c0 + ncol], data1=av[:, c0:c0 + ncol],
                                                                       initial=min_[:, 2:3], op0=ALU.max, op1=ALU.max), reads=[avb, minb], writes=[MUb])
            P.op(DVE, lambda e: e.tensor_copy(out=MUS[:, 22:86], in_=min_[:, 2:3].broadcast_to([4, 64])), reads=[minb], writes=[MUSb])
            P.op(DVE, lambda e: e.tensor_copy(out=MUS[:, 86:2070].rearrange("p (c j) -> p c j", j=64),
                                              in_=MU[:, 22:2006].rearrange("p (c j) -> p c j", j=64)[:, :, 63:64].broadcast_to([4, 31, 64])),
                 reads=[MUb], writes=[MUSb])
    P.op(DVE, lambda e: e.tensor_tensor(out=av[:, 3:TP], in0=av[:, 3:TP], in1=MUS[:, 3:TP], op=ALU.subtract), reads=[avb, MUSb], writes=[avb])
    P.op(ACT, lambda e: e.activation(out=G8[0:4, 3:TP], in_=av[:, 3:TP], func=AF.Exp), reads=[avb, G8b], writes=[G8b])
    P.op(DVE, lambda e: e.tensor_tensor(out=av[:, 3:TP], in0=SPc[:, 3:TP], in1=MUS[:, 3:TP], op=ALU.subtract), reads=[SPb, MUSb, G8b], writes=[avb])
    P.op(ACT, lambda e: e.activation(out=G8[32:36, 3:TP], in_=av[:, 3:TP], func=AF.Exp), reads=[avb, G8b], writes=[G8b])
    P.op(DVE, lambda e: e.tensor_tensor(out=FE[:, 0:1], in0=MUS[:, 3:4], in1=MU[:, 18:19], op=ALU.subtract), reads=[MUSb, MUb], writes=[FEb])
    P.op(DVE, lambda e: e.tensor_tensor(out=FE[:, 1:33], in0=MUS[:, 22:2070].rearrange("p (c j) -> p c j", j=64)[:, :, 0],
                                        in1=MU[:, 22:2070].rearrange("p (c j) -> p c j", j=64)[:, :, 63], op=ALU.subtract),
         reads=[MUSb, MUb, FEb], writes=[FEb])
    P.op(ACT, lambda e: e.activation(out=FE[:, 0:33], in_=FE[:, 0:33], func=AF.Exp), reads=[FEb], writes=[FEb])
    P.op(DVE, lambda e: e.tensor_tensor(out=min_[:, 3:4], in0=MU[:, TP - 1:TP], in1=SPc[:, TP - 1:TP], op=ALU.subtract), reads=[MUb, SPb, minb], writes=[minb])
    P.dma(POOL, prm["mC_out"], min_[:, 3:4], reads=[minb], pwrites=[K.sob])
    xq = sbuf(C, st, "mxq", [128, TP]); xk = sbuf(C, st, "mxk", [128, TP]); q = sbuf(C, st, "mq", [128, TP]); k = sbuf(C, st, "mk", [128, TP])
    xqb, xkb, qb, kb_ = [Buf() for _ in range(4)]
    tmpr = Ring([sbuf(C, st, f"mtmp{i}", [128, 4], F32) for i in range(2)])
    CX = sbuf(C, st, "mCX", [128, 257]); CXb = Buf()
    CXin = sbuf(C, st, "mCXin", [128, 257]); CXinb = Buf()
    FB = sbuf(C, st, "mFB", [128, 64]); FBb = Buf()
    psA = Ring([psum(C, st, f"mpa{i}", [128, 512], F32) for i in range(4)])
    psG = Ring([psum(C, st, f"mpg{i}", [128, 512], F32) for i in range(2)])
    vr = Ring([sbuf(C, st, f"mv{i}", [64, 512], F32) for i in range(3)])
    vxr = Ring([sbuf(C, st, f"mvx{i}", [64, 257], F32) for i in range(2)])
    ktr = Ring([sbuf(C, st, f"mkt{i}", [64, 128], F32) for i in range(2)])
    scr = Ring([sbuf(C, st, f"msc{i}", [64, 64], F32) for i in range(2)])
    gtr = Ring([sbuf(C, st, f"mgt{i}", [64, 36], F32) for i in range(2)])
    str_ = Ring([sbuf(C, st, f"mst{i}", [64, 8], F32) for i in range(2)])
    junk = sbuf(C, st, "mjunk", [64, 256], F32); jb = Buf()
    mh = sbuf(C, st, "mmh", [64, 1], F32); mhb = Buf()
    P.op(POOL, lambda e: e.memset(mh[:], -0.5), writes=[mhb])
    t1r = Ring([sbuf(C, st, f"mt1{i}", [64, 256], F32) for i in range(2)])
    yor = Ring([sbuf(C, st, f"myo{i}", [64, 256], BF16) for i in range(2)])
    for h in range(4):
        P.dma(SP, xq[:], pf[(19 + h) * 128:(20 + h) * 128, :], reads=[pfb], writes=[xqb])
        P.dma(SP, xk[:], pf[(23 + h) * 128:(24 + h) * 128, :], reads=[pfb], writes=[xkb])
        P.op(DVE, lambda e: e.memset(xq[:, 0:3], 0.0), reads=[xqb], writes=[xqb])
        P.op(DVE, lambda e: e.memset(xk[:, 0:3], 0.0), reads=[xkb], writes=[xkb])
        fix_gap(C, xq, xqb, prm["hist_in"][(19 + h) * 128:(20 + h) * 128, :], flagE, fb, tmpr, 128)
        fix_gap(C, xk, xkb, prm["hist_in"][(23 + h) * 128:(24 + h) * 128, :], flagE, fb, tmpr, 128)
        for (x, xb, o, ob, j) in ((xq, xqb, q, qb, h), (xk, xkb, k, kb_, 4 + h)):
            P.op(DVE, lambda e, x=x, o=o, j=j: e.tensor_scalar(out=o[:, 3:TP], in0=x[:, 0:TP - 3], scalar1=cw[:, j, 0:1], scalar2=cb[:, j:j + 1],
                                                               op0=ALU.mult, op1=ALU.add), reads=[xb, cwb], writes=[ob])
            for tap in range(1, 4):
                P.op(DVE, lambda e, x=x, o=o, j=j, tap=tap: e.scalar_tensor_tensor(out=o[:, 3:TP], in0=x[:, tap:TP - 3 + tap], scalar=cw[:, j, tap:tap + 1],
                                                                                 in1=o[:, 3:TP], op0=ALU.mult, op1=ALU.add), reads=[xb, cwb, ob], writes=[ob])
            P.op(ACT, lambda e, o=o: e.activation(out=o[:, 3:TP], in_=o[:, 3:TP], func=AF.Silu), reads=[ob], writes=[ob])
        P.op(DVE, lambda e: e.tensor_scalar(out=k[:, 3:TP], in0=k[:, 3:TP], scalar1=128 ** -0.5, scalar2=None, op0=ALU.mult), reads=[kb_], writes=[kb_])
        pg, pgb = psG.next()
        P.op(PE, lambda e, pg=pg, h=h: e.matmul(pg[:, 0:33], lhsT=oh[0:4, h, :], rhs=FE[0:4, 0:33], start=True, stop=True), reads=[ohb, FEb], writes=[pgb])
        P.op(ACT, lambda e, pg=pg: e.activation(out=FB[:, 0:33], in_=pg[:, 0:33], func=AF.Copy), reads=[pgb], writes=[FBb])
        P.op(DVE, lambda e: e.memset(CX[:], 0.0), writes=[CXb])
        ci = 0
        for si, seg in enumerate(SEGS):
            if si == 1:
                P.dma(SP, CXin[:], prm["sC_in"][h], writes=[CXinb])
                P.op(DVE, lambda e: e.scalar_tensor_tensor(out=CX[:], in0=CX[:], scalar=flagE[:, 0:1], in1=CXin[:], op0=ALU.mult, op1=ALU.add),
                     reads=[CXb, CXinb, fb], writes=[CXb])
            for (c0, n, t0) in chunks_of(seg):
                vt, vb = vr.next()
                P.dma(SP, vt[0:n, 0:256], pt[t0:t0 + n, 1024 + h * 256:1024 + (h + 1) * 256], reads=[ptb], writes=[vb])
                P.dma(SP, vt[0:n, 256:512], pt[t0:t0 + n, 2048 + h * 256:2048 + (h + 1) * 256], reads=[ptb], writes=[vb])
                pa, pab = psA.next()
                P.op(PE, lambda e, pa=pa, c0=c0, n=n: e.transpose(out=pa[0:n, 0:128], in_=k[:, c0:c0 + n], identity=ident[:, :]), reads=[kb_, idb], writes=[pab])
                P.op(PE, lambda e, pa=pa, c0=c0, n=n: e.matmul(pa[0:n, 128:128 + n], lhsT=k[:, c0:c0 + n], rhs=q[:, c0:c0 + n], start=True, stop=True),
                     reads=[kb_, qb], writes=[pab])
                P.op(PE, lambda e, pa=pa, c0=c0, n=n: e.transpose(out=pa[0:n, 192:228], in_=G8[0:36, c0:c0 + n], identity=ident[0:36, 0:36]),
                     reads=[G8b, idb], writes=[pab])
                kt, ktb = ktr.next(); sc, scb = scr.next(); gt, gtb = gtr.next()
                P.op(ACT, lambda e, kt=kt, pa=pa, n=n: e.activation(out=kt[0:n, :], in_=pa[0:n, 0:128], func=AF.Copy), reads=[pab], writes=[ktb])
                P.op(DVE, lambda e, sc=sc, pa=pa, n=n: e.tensor_tensor(out=sc[0:n, 0:n], in0=pa[0:n, 128:128 + n], in1=mask_i[0:n, 0:n], op=ALU.mult),
                     reads=[pab, mib], writes=[scb])
                P.op(ACT, lambda e, gt=gt, pa=pa, n=n: e.activation(out=gt[0:n, :], in_=pa[0:n, 192:228], func=AF.Copy), reads=[pab], writes=[gtb])
                vx, vxb = vxr.next()
                P.op(DVE, lambda e, vx=vx, vt=vt, gt=gt, n=n, h=h: e.tensor_scalar(out=vx[0:n, 0:256], in0=vt[0:n, 0:256], scalar1=gt[0:n, h:h + 1], scalar2=None,
                                                                                 op0=ALU.mult), reads=[vb, gtb], writes=[vxb])
                P.op(ACT, lambda e, vx=vx, gt=gt, n=n, h=h: e.activation(out=vx[0:n, 256:257], in_=gt[0:n, h:h + 1], func=AF.Copy), reads=[gtb, vxb], writes=[vxb])
                pn, pnb = psA.next()
                P.op(PE, lambda e, pn=pn, c0=c0, n=n: e.matmul(pn[0:n, 0:257], lhsT=q[:, c0:c0 + n], rhs=CX[:, :], start=True, stop=False), reads=[qb, CXb], writes=[pnb])
                P.op(PE, lambda e, pn=pn, sc=sc, vx=vx, n=n: e.matmul(pn[0:n, 0:257], lhsT=sc[0:n, 0:n], rhs=vx[0:n, :], start=False, stop=True),
                     reads=[scb, vxb], writes=[pnb])
                pc, pcb = psA.next()
                P.op(PE, lambda e, pc=pc: e.matmul(pc[:, 0:257], lhsT=ident[:, :], rhs=CX[:, :], start=True, stop=False), reads=[idb, CXb], writes=[pcb])
                P.op(PE, lambda e, pc=pc, kt=kt, vx=vx, n=n: e.matmul(pc[:, 0:257], lhsT=kt[0:n, :], rhs=vx[0:n, :], start=False, stop=True),
                     reads=[ktb, vxb], writes=[pcb])
                P.op(DVE, lambda e, pc=pc, ci=ci: e.tensor_scalar(out=CX[:], in0=pc[:, 0:257], scalar1=FB[:, ci:ci + 1], scalar2=None, op0=ALU.mult),
                     reads=[pcb, FBb], writes=[CXb])
                if C.emit_out:
                    s_, sb2 = str_.next()
                    P.op(ACT, lambda e, s_=s_, pn=pn, n=n: e.activation(out=s_[0:n, 0:1], in_=pn[0:n, 256:257], func=AF.Abs),
                         reads=[pnb], writes=[sb2])
                    P.op(DVE, lambda e, s_=s_, gt=gt, n=n, h=h: e.tensor_tensor(out=s_[0:n, 0:1], in0=s_[0:n, 0:1], in1=gt[0:n, 32 + h:33 + h], op=ALU.max),
                         reads=[sb2, gtb], writes=[sb2])
                    P.op(DVE, lambda e, s_=s_, n=n: e.reciprocal(out=s_[0:n, 1:2], in_=s_[0:n, 0:1]), reads=[sb2], writes=[sb2])
                    P.op(ACT, lambda e, s_=s_, pn=pn, n=n: e.activation(out=junk[0:n, :], in_=pn[0:n, 0:256], func=AF.Square, scale=s_[0:n, 1:2],
                                                                       accum_out=s_[0:n, 2:3]), reads=[pnb, sb2], writes=[jb, sb2])
                    P.op(DVE, lambda e, s_=s_, n=n: e.tensor_scalar(out=s_[0:n, 3:4], in0=s_[0:n, 2:3], scalar1=1.0 / 256, scalar2=EPS, op0=ALU.mult,
                                                                   op1=ALU.add), reads=[sb2], writes=[sb2])
                    P.op(POOL, lambda e, s_=s_, n=n: e.tensor_tensor(out=s_[0:n, 4:5], in0=s_[0:n, 3:4], in1=mh[0:n, :], op=ALU.pow), reads=[sb2, mhb], writes=[sb2])
                    P.op(DVE, lambda e, s_=s_, n=n: e.tensor_tensor(out=s_[0:n, 5:6], in0=s_[0:n, 4:5], in1=s_[0:n, 1:2], op=ALU.mult), reads=[sb2], writes=[sb2])
                    t1, t1b = t1r.next(); yo, yob = yor.next()
                    P.op(DVE, lambda e, t1=t1, pn=pn, s_=s_, n=n, h=h: e.scalar_tensor_tensor(out=t1[0:n, :], in0=pn[0:n, 0:256], scalar=s_[0:n, 5:6],
                                                                                          in1=nbc[0:n, h * 256:(h + 1) * 256], op0=ALU.mult, op1=ALU.mult),
                         reads=[pnb, sb2, nbcb], writes=[t1b])
                    P.op(DVE, lambda e, yo=yo, t1=t1, vt=vt, n=n: e.tensor_tensor(out=yo[0:n, :], in0=t1[0:n, :], in1=vt[0:n, 256:512], op=ALU.mult),
                         reads=[t1b, vb], writes=[yob])
                    P.dma(POOL, y[t0:t0 + n, 1024 + h * 256:1024 + (h + 1) * 256], yo[0:n, :], reads=[yob], pwrites=[yb])
                ci += 1
        P.dma(POOL, prm["sC_out"][h], CX[:], reads=[CXb], pwrites=[K.sob])


def mixer_rwkv(C, st, pf, pfb, y, yb, prm, K):
    P = C.P
    ones, onesb, ident, idb, flagE, fb = K.ones, K.onesb, K.ident, K.idb, K.flagE, K.fb
    mask5, m5b = K.mask5, K.m5b
    muA = sbuf(C, st, "amuA", [64, 3, 8]); muL = sbuf(C, st, "amuL", [96, 3]); w2 = sbuf(C, st, "aw2", [32, 512]); a2 = sbuf(C, st, "aa2", [32, 512])
    g2 = sbuf(C, st, "ag2", [96, 512]); ch = sbuf(C, st, "ach", [64, 5, 8]); rk = sbuf(C, st, "ark", [64, 8, 2])
    lnw = sbuf(C, st, "alnw", [64, 512]); lnb = sbuf(C, st, "alnb", [64, 512])
    pb_ = Buf()
    for t, n_ in ((muA, "rw_muA"), (muL, "rw_muL"), (w2, "rw_w2"), (a2, "rw_a2"), (g2, "rw_g2"), (rk, "rw_rk"), (lnw, "rw_lnw_bc"), (lnb, "rw_lnb_bc")):
        P.dma(SP, t[:], prm[n_], pwrites=[pb_])
    P.dma(SP, ch[:, 0:4, :], prm["rw_ch"], pwrites=[pb_])
    P.op(DVE, lambda e: e.tensor_scalar(out=ch[:, 4, :], in0=ch[:, 3, :], scalar1=-1.0, scalar2=1.0, op0=ALU.mult, op1=ALU.add), reads=[pb_], writes=[pb_])
    mh = sbuf(C, st, "amh", [64, TP], F32); mhb = Buf()
    P.op(POOL, lambda e: e.memset(mh[:], -0.5), writes=[mhb])
    tmpr = Ring([sbuf(C, st, f"atmp{i}", [128, 4], F32) for i in range(2)])
    raw = sbuf(C, st, "araw", [96, TP]); rawb = Buf()
    thw = sbuf(C, st, "athw", [32, TP]); xal = sbuf(C, st, "axal", [32, TP]); sg = sbuf(C, st, "asg", [96, TP])
    thwb, xalb, sgb = Buf(), Buf(), Buf()
    for (dst, dstb, r0, nr, mcol, fn) in ((thw, thwb, 12 * 128, 32, 0, AF.Tanh), (xal, xalb, 12 * 128 + 32, 32, 1, None), (sg, sgb, 13 * 128, 96, 2, AF.Sigmoid)):
        P.dma(SP, raw[0:nr, :], pf[r0:r0 + nr, :], reads=[pfb], writes=[rawb])
        P.op(DVE, lambda e, nr=nr: e.memset(raw[0:nr, 0:3], 0.0), reads=[rawb], writes=[rawb])
        fix_gap(C, raw, rawb, prm["hist_in"][r0:r0 + nr, :], flagE, fb, tmpr, nr)
        P.op(DVE, lambda e, dst=dst, nr=nr: e.tensor_tensor(out=dst[0:nr, 3:TP], in0=raw[0:nr, 2:TP - 1], in1=raw[0:nr, 3:TP], op=ALU.subtract),
             reads=[rawb], writes=[dstb])
        P.op(DVE, lambda e, dst=dst, nr=nr, mcol=mcol: e.scalar_tensor_tensor(out=dst[0:nr, 3:TP], in0=dst[0:nr, 3:TP], scalar=muL[0:nr, mcol:mcol + 1],
                                                                            in1=raw[0:nr, 3:TP], op0=ALU.mult, op1=ALU.add), reads=[rawb, dstb, pb_], writes=[dstb])
        if fn is not None:
            P.op(ACT, lambda e, dst=dst, nr=nr, fn=fn: e.activation(out=dst[0:nr, 3:TP], in_=dst[0:nr, 3:TP], func=fn), reads=[dstb], writes=[dstb])
    A = [sbuf(C, st, f"aA{i}", [64, TP]) for i in range(10)]
    Ab = [Buf() for _ in range(10)]
    H = sbuf(C, st, "aH", [64, 64]); Hb = Buf()
    Hin = sbuf(C, st, "aHin", [64, 64]); Hinb = Buf()
    G = 8
    MM = sbuf(C, st, "aMM", [64, G, 5, 64]); MMb = Buf()
    TM = sbuf(C, st, "aTM", [64, G, 3, 64]); TMb = Buf()
    NN = [sbuf(C, st, f"aNN{i}", [64, G, 2, 64]) for i in range(2)]; NNb = [Buf(), Buf()]
    Pm = sbuf(C, st, "aPm", [64, G, 64]); Pmb = Buf()
    GB = sbuf(C, st, "aGB", [64, G, 66]); GBb = Buf()
    ps1 = Ring([psum(C, st, f"ap1{i}", [128, 512], F32) for i in range(4)])
    ps2 = Ring([psum(C, st, f"ap2{i}", [128, 512], F32) for i in range(4)])
    w0r = Ring([sbuf(C, st, f"aw0{i}", [64, 64], F32) for i in range(2)])
    ur = Ring([sbuf(C, st, f"au{i}", [64, 64], F32) for i in range(2)])
    str_ = Ring([sbuf(C, st, f"ast{i}", [64, 8], F32) for i in range(2)])
    junk = sbuf(C, st, "ajunk", [64, 64], F32); jb = Buf()
    t1r = Ring([sbuf(C, st, f"at1{i}", [64, 64], F32) for i in range(2)])
    yor = Ring([sbuf(C, st, f"ayo{i}", [64, 64], BF16) for i in range(2)])
    for h in range(8):
        rows = [(sg_ * 4 + h // 2) * 128 + (h % 2) * 64 for sg_ in range(3)]
        for i in range(3):
            P.dma(SP, A[i][:], pf[rows[i]:rows[i] + 64, :], reads=[pfb], writes=[Ab[i]])
            P.op(DVE, lambda e, i=i: e.memset(A[i][:, 0:3], 0.0), reads=[Ab[i]], writes=[Ab[i]])
            fix_gap(C, A[i], Ab[i], prm["hist_in"][rows[i]:rows[i] + 64, :], flagE, fb, tmpr, 64)
            P.op(DVE, lambda e, i=i: e.tensor_tensor(out=A[3 + i][:, 3:TP], in0=A[i][:, 2:TP - 1], in1=A[i][:, 3:TP], op=ALU.subtract),
                 reads=[Ab[i]], writes=[Ab[3 + i]])
            P.op(DVE, lambda e, i=i, h=h: e.scalar_tensor_tensor(out=A[3 + i][:, 3:TP], in0=A[3 + i][:, 3:TP], scalar=muA[:, i, h:h + 1], in1=A[i][:, 3:TP],
                                                                op0=ALU.mult, op1=ALU.add), reads=[Ab[i], Ab[3 + i], pb_], writes=[Ab[3 + i]])
        xr, xk, xv = A[3], A[4], A[5]
        for c0 in range(3, TP, 512):
            n = min(512, TP - c0)
            p_, p_b = ps1.next()
            P.op(PE, lambda e, p_=p_, c0=c0, n=n, h=h: e.matmul(p_[0:64, 0:n], lhsT=w2[0:32, h * 64:(h + 1) * 64], rhs=thw[0:32, c0:c0 + n], start=True, stop=True),
                 reads=[pb_, thwb], writes=[p_b])
            P.op(ACT, lambda e, p_=p_, c0=c0, n=n, h=h: e.activation(out=A[0][:, c0:c0 + n], in_=p_[0:64, 0:n], func=AF.Sigmoid, bias=ch[:, 0, h:h + 1]),
                 reads=[p_b, pb_], writes=[Ab[0]])
            p_, p_b = ps1.next()
            P.op(PE, lambda e, p_=p_, c0=c0, n=n, h=h: e.matmul(p_[0:64, 0:n], lhsT=a2[0:32, h * 64:(h + 1) * 64], rhs=xal[0:32, c0:c0 + n], start=True, stop=True),
                 reads=[pb_, xalb], writes=[p_b])
            P.op(ACT, lambda e, p_=p_, c0=c0, n=n, h=h: e.activation(out=A[1][:, c0:c0 + n], in_=p_[0:64, 0:n], func=AF.Sigmoid, bias=ch[:, 1, h:h + 1]),
                 reads=[p_b, pb_], writes=[Ab[1]])
        P.op(DVE, lambda e, h=h: e.tensor_scalar(out=A[2][:, 3:TP], in0=xk[:, 3:TP], scalar1=ch[:, 2, h:h + 1], scalar2=None, op0=ALU.mult),
             reads=[Ab[4], pb_], writes=[Ab[2]])
        P.op(DVE, lambda e: e.tensor_tensor(out=A[6][:, 3:TP], in0=A[2][:, 3:TP], in1=A[2][:, 3:TP], op=ALU.mult), reads=[Ab[2]], writes=[Ab[6]])
        for c0 in range(3, TP, 512):
            n = min(512, TP - c0)
            p_, p_b = ps1.next()
            P.op(PE, lambda e, p_=p_, c0=c0, n=n: e.matmul(p_[0:64, 0:n], lhsT=ones[0:64, 0:64], rhs=A[6][:, c0:c0 + n], start=True, stop=True),
                 reads=[onesb, Ab[6]], writes=[p_b])
            P.op(DVE, lambda e, p_=p_, c0=c0, n=n: e.tensor_scalar(out=A[8][:, c0:c0 + n], in0=p_[0:64, 0:n], scalar1=1e-24, scalar2=None, op0=ALU.max),
                 reads=[p_b], writes=[Ab[8]])
        P.op(POOL, lambda e: e.tensor_tensor(out=A[8][:, 3:TP], in0=A[8][:, 3:TP], in1=mh[:, 3:TP], op=ALU.pow), reads=[Ab[8], mhb], writes=[Ab[8]])
        P.op(DVE, lambda e: e.tensor_tensor(out=A[2][:, 3:TP], in0=A[2][:, 3:TP], in1=A[8][:, 3:TP], op=ALU.mult), reads=[Ab[2], Ab[8]], writes=[Ab[2]])
        P.op(DVE, lambda e, h=h: e.tensor_scalar(out=A[6][:, 3:TP], in0=A[1][:, 3:TP], scalar1=ch[:, 3, h:h + 1], scalar2=ch[:, 4, h:h + 1], op0=ALU.mult,
                                                op1=ALU.add), reads=[Ab[1], pb_], writes=[Ab[6]])
        P.op(DVE, lambda e: e.tensor_tensor(out=xk[:, 3:TP], in0=xk[:, 3:TP], in1=A[6][:, 3:TP], op=ALU.mult), reads=[Ab[4], Ab[6]], writes=[Ab[4]])
        P.op(DVE, lambda e: e.tensor_tensor(out=A[1][:, 3:TP], in0=A[1][:, 3:TP], in1=A[2][:, 3:TP], op=ALU.mult), reads=[Ab[1], Ab[2]], writes=[Ab[1]])
        P.op(DVE, lambda e: e.tensor_tensor(out=A[6][:, 3:TP], in0=xr[:, 3:TP], in1=xk[:, 3:TP], op=ALU.mult), reads=[Ab[3], Ab[4]], writes=[Ab[6]])
        for (c0, ncol, t0) in SEGS:
            P.op(DVE, lambda e, c0=c0, ncol=ncol: e.tensor_tensor_scan(out=A[7][:, c0:c0 + ncol], data0=ones[0:64, c0:c0 + ncol], data1=A[0][:, c0:c0 + ncol],
                                                                       initial=0.0, op0=ALU.mult, op1=ALU.add), reads=[Ab[0], onesb], writes=[Ab[7]])
        P.op(DVE, lambda e: e.memset(A[8][:, 19:22], 0.0), reads=[Ab[8]], writes=[Ab[8]])
        chunk_rel(C, A[8], Ab[8], A[7], Ab[7], 64)
        P.op(ACT, lambda e: e.activation(out=A[7][:, 3:TP], in_=A[8][:, 3:TP], func=AF.Exp, scale=-LDK), reads=[Ab[8]], writes=[Ab[7]])
        P.op(ACT, lambda e: e.activation(out=A[9][:, 3:TP], in_=A[8][:, 3:TP], func=AF.Exp, scale=LDK), reads=[Ab[8]], writes=[Ab[9]])
        P.op(DVE, lambda e: e.tensor_tensor(out=A[8][:, 3:TP], in0=A[8][:, 3:TP], in1=A[0][:, 3:TP], op=ALU.subtract), reads=[Ab[8], Ab[0]], writes=[Ab[8]])
        P.op(ACT, lambda e: e.activation(out=A[8][:, 3:TP], in_=A[8][:, 3:TP], func=AF.Exp, scale=-LDK), reads=[Ab[8]], writes=[Ab[8]])
        P.op(DVE, lambda e: e.tensor_tensor(out=xr[:, 3:TP], in0=xr[:, 3:TP], in1=A[7][:, 3:TP], op=ALU.mult), reads=[Ab[3], Ab[7]], writes=[Ab[3]])
        P.op(DVE, lambda e: e.tensor_tensor(out=xk[:, 3:TP], in0=xk[:, 3:TP], in1=A[9][:, 3:TP], op=ALU.mult), reads=[Ab[4], Ab[9]], writes=[Ab[4]])
        P.op(DVE, lambda e: e.tensor_tensor(out=A[1][:, 3:TP], in0=A[1][:, 3:TP], in1=A[9][:, 3:TP], op=ALU.mult), reads=[Ab[1], Ab[9]], writes=[Ab[1]])
        P.op(DVE, lambda e: e.scalar_tensor_tensor(out=A[2][:, 3:TP], in0=A[2][:, 3:TP], scalar=-1.0, in1=A[8][:, 3:TP], op0=ALU.mult, op1=ALU.mult),
             reads=[Ab[2], Ab[8]], writes=[Ab[2]])
        rt, kt_, bt, at, prod, G1 = A[3], A[4], A[1], A[2], A[6], A[7]
        rtb, ktb_, btb, atb, prodb, G1b = Ab[3], Ab[4], Ab[1], Ab[2], Ab[6], Ab[7]
        xvb = Ab[5]
        P.op(DVE, lambda e: e.memset(H[:], 0.0), writes=[Hb])
        for si, seg in enumerate(SEGS):
            if si == 1:
                P.dma(SP, Hin[:], prm["sA_in"][h], writes=[Hinb])
                P.op(DVE, lambda e: e.scalar_tensor_tensor(out=H[:], in0=H[:], scalar=flagE[0:64, 0:1], in1=Hin[:], op0=ALU.mult, op1=ALU.add),
                     reads=[Hb, Hinb, fb], writes=[Hb])
            chs_all = chunks_of(seg)
            for g0 in range(0, len(chs_all), G):
                chs = chs_all[g0:g0 + G]
                ng = len(chs)
                n = chs[0][1]
                nlev = 5 if n == 64 else 3
                for g, (c0, n, t0) in enumerate(chs):
                    p_, p_b = ps1.next()
                    for j, (src, srcb) in enumerate(((xv, xvb), (kt_, ktb_), (bt, btb))):
                        P.op(PE, lambda e, p_=p_, src=src, c0=c0, n=n, j=j: e.transpose(out=p_[0:n, j * 64:(j + 1) * 64], in_=src[:, c0:c0 + n], identity=ident[0:64, 0:64]),
                             reads=[srcb, idb], writes=[p_b])
                    P.op(ACT, lambda e, p_=p_, g=g, n=n: e.activation(out=TM[0:n, g, :, :], in_=p_[0:n, 0:192].rearrange("p (j d) -> p j d", j=3), func=AF.Copy),
                         reads=[p_b], pwrites=[TMb])
                    q_, q_b = ps1.next()
                    pairs = ((kt_, ktb_, at, atb), (kt_, ktb_, rt, rtb), (bt, btb, at, atb), (bt, btb, rt, rtb), (at, atb, bt, btb))
                    for j, (l, lb, r, rb) in enumerate(pairs):
                        P.op(PE, lambda e, q_=q_, l=l, r=r, c0=c0, n=n, j=j: e.matmul(q_[0:n, j * 64:j * 64 + n], lhsT=l[:, c0:c0 + n], rhs=r[:, c0:c0 + n], start=True, stop=True),
                             reads=[lb, rb], writes=[q_b])
                    P.op(DVE, lambda e, q_=q_, g=g, n=n: e.tensor_tensor(out=MM[0:n, g, :, 0:n], in0=q_[0:n, 0:320].rearrange("p (j d) -> p j d", j=5)[:, :, 0:n],
                                                                       in1=mask5[0:n, :, 0:n], op=ALU.mult), reads=[q_b, m5b], pwrites=[MMb])
                P.op(DVE, lambda e, ng=ng, n=n: e.tensor_tensor(out=Pm[0:n, 0:ng, 0:n], in0=MM[0:n, 0:ng, 2, 0:n],
                                                                in1=ident[0:n, 0:n].unsqueeze(1).broadcast_to([n, ng, n]), op=ALU.add),
                     reads=[MMb, idb], writes=[Pmb])
                curN = lambda g, n=n: MM[0:n, g, 2, 0:n]
                curNT = lambda g, n=n: MM[0:n, g, 4, 0:n]
                curb = MMb
                for lev in range(nlev):
                    nn, nnb = NN[lev % 2], NNb[lev % 2]
                    for g4 in range(0, ng, 4):
                        m4 = min(4, ng - g4)
                        p2, p2b = ps2.next()
                        for g in range(g4, g4 + m4):
                            gg = g - g4
                            P.op(PE, lambda e, p2=p2, gg=gg, n=n, a_=curNT(g), b_=curN(g): e.matmul(p2[0:n, gg * 128:gg * 128 + n], lhsT=a_, rhs=b_, start=True, stop=True),
                                 reads=[curb], writes=[p2b])
                            P.op(PE, lambda e, p2=p2, gg=gg, n=n, a_=curN(g), b_=curNT(g): e.matmul(p2[0:n, gg * 128 + 64:gg * 128 + 64 + n], lhsT=a_, rhs=b_, start=True, stop=True),
                                 reads=[curb], writes=[p2b])
                        P.op(ACT, lambda e, p2=p2, nn=nn, g4=g4, m4=m4, n=n: e.activation(out=nn[0:n, g4:g4 + m4, :, 0:n],
                                                                                 in_=p2[0:n, 0:m4 * 128].rearrange("p (g j d) -> p g j d", g=m4, j=2)[:, :, :, 0:n], func=AF.Copy),
                             reads=[p2b], pwrites=[nnb])
                    curN = lambda g, nn=nn, n=n: nn[0:n, g, 0, 0:n]
                    curNT = lambda g, nn=nn, n=n: nn[0:n, g, 1, 0:n]
                    curb = nnb
                    p1, p1b = ps1.next()
                    for g in range(ng):
                        P.op(PE, lambda e, p1=p1, g=g, n=n, a_=curNT(g): e.matmul(p1[0:n, g * 64:g * 64 + n], lhsT=a_, rhs=Pm[0:n, g, 0:n], start=True, stop=True),
                             reads=[curb, Pmb], writes=[p1b])
                    P.op(DVE, lambda e, p1=p1, ng=ng, n=n: e.tensor_tensor(out=Pm[0:n, 0:ng, 0:n], in0=Pm[0:n, 0:ng, 0:n],
                                                                         in1=p1[0:n, 0:ng * 64].rearrange("p (g d) -> p g d", g=ng)[:, :, 0:n], op=ALU.add),
                         reads=[p1b, Pmb], writes=[Pmb])
                if C.emit_out:
                    for g, (c0, n, t0) in enumerate(chs):
                        p_, p_b = ps1.next()
                        P.op(PE, lambda e, p_=p_, c0=c0, n=n, h=h: e.matmul(p_[0:n, 0:64], lhsT=sg[0:96, c0:c0 + n], rhs=g2[0:96, h * 64:(h + 1) * 64], start=True, stop=True),
                             reads=[sgb, pb_], writes=[p_b])
                        P.op(PE, lambda e, p_=p_, c0=c0, n=n, h=h: e.matmul(p_[0:n, 64:66], lhsT=prod[:, c0:c0 + n], rhs=rk[:, h, :], start=True, stop=True),
                             reads=[prodb, pb_], writes=[p_b])
                        P.op(ACT, lambda e, p_=p_, g=g, n=n: e.activation(out=GB[0:n, g, :], in_=p_[0:n, 0:66], func=AF.Copy), reads=[p_b], pwrites=[GBb])
                for g, (c0, n, t0) in enumerate(chs):
                    vtm = TM[0:n, g, 0, :]; ktm = TM[0:n, g, 1, :]; btm = TM[0:n, g, 2, :]
                    LakT = MM[0:n, g, 0, 0:n]; MrkT = MM[0:n, g, 1, 0:n]; MrbT = MM[0:n, g, 3, 0:n]
                    TT_ = Pm[0:n, g, 0:n]
                    pw, pwb = ps1.next()
                    P.op(PE, lambda e, pw=pw, c0=c0, n=n: e.matmul(pw[0:n, 0:64], lhsT=at[:, c0:c0 + n], rhs=H[:, :], start=True, stop=False), reads=[atb, Hb], writes=[pwb])
                    P.op(PE, lambda e, pw=pw, n=n, LakT=LakT, vtm=vtm: e.matmul(pw[0:n, 0:64], lhsT=LakT, rhs=vtm, start=False, stop=True), reads=[MMb, TMb], writes=[pwb])
                    w0, w0b = w0r.next()
                    P.op(ACT, lambda e, w0=w0, pw=pw, n=n: e.activation(out=w0[0:n, :], in_=pw[0:n, 0:64], func=AF.Copy), reads=[pwb], writes=[w0b])
                    P.op(PE, lambda e, pw=pw, n=n, TT_=TT_, w0=w0: e.matmul(pw[0:n, 64:128], lhsT=TT_, rhs=w0[0:n, :], start=True, stop=True), reads=[Pmb, w0b], writes=[pwb])
                    u, ub_ = ur.next()
                    P.op(DVE, lambda e, u=u, pw=pw, n=n: e.tensor_copy(out=u[0:n, :], in_=pw[0:n, 64:128]), reads=[pwb], writes=[ub_])
                    if C.emit_out:
                        P.op(PE, lambda e, pw=pw, c0=c0, n=n: e.matmul(pw[0:n, 128:192], lhsT=rt[:, c0:c0 + n], rhs=H[:, :], start=True, stop=False), reads=[rtb, Hb], writes=[pwb])
                        P.op(PE, lambda e, pw=pw, n=n, MrbT=MrbT, u=u: e.matmul(pw[0:n, 128:192], lhsT=MrbT, rhs=u[0:n, :], start=False, stop=False), reads=[MMb, ub_], writes=[pwb])
                        P.op(PE, lambda e, pw=pw, n=n, MrkT=MrkT, vtm=vtm: e.matmul(pw[0:n, 128:192], lhsT=MrkT, rhs=vtm, start=False, stop=True), reads=[MMb, TMb], writes=[pwb])
                    ph, phb = ps1.next()
                    P.op(PE, lambda e, ph=ph: e.matmul(ph[0:64, 0:64], lhsT=ident[0:64, 0:64], rhs=H[:, :], start=True, stop=False), reads=[idb, Hb], writes=[phb])
                    P.op(PE, lambda e, ph=ph, n=n, btm=btm, u=u: e.matmul(ph[0:64, 0:64], lhsT=btm, rhs=u[0:n, :], start=False, stop=False), reads=[TMb, ub_], writes=[phb])
                    P.op(PE, lambda e, ph=ph, n=n, ktm=ktm, vtm=vtm: e.matmul(ph[0:64, 0:64], lhsT=ktm, rhs=vtm, start=False, stop=True), reads=[TMb], writes=[phb])
                    ce = c0 + n - 1
                    P.op(DVE, lambda e, ph=ph, ce=ce: e.tensor_scalar(out=H[:], in0=ph[0:64, 0:64], scalar1=G1[:, ce:ce + 1], scalar2=None, op0=ALU.mult),
                         reads=[phb, G1b], writes=[Hb])
                    if C.emit_out:
                        s_, sb2 = str_.next()
                        yps = pw[0:n, 128:192]
                        P.op(DVE, lambda e, s_=s_, yps=yps, n=n: e.tensor_reduce(out=s_[0:n, 0:1], in_=yps, axis=AX.X, op=ALU.add), reads=[pwb], writes=[sb2])
                        P.op(ACT, lambda e, s_=s_, yps=yps, n=n: e.activation(out=junk[0:n, :], in_=yps, func=AF.Square, accum_out=s_[0:n, 1:2]), reads=[pwb], writes=[jb, sb2])
                        P.op(DVE, lambda e, s_=s_, n=n: e.tensor_scalar(out=s_[0:n, 2:3], in0=s_[0:n, 0:1], scalar1=1.0 / 64, scalar2=None, op0=ALU.mult), reads=[sb2], writes=[sb2])
                        P.op(DVE, lambda e, s_=s_, n=n: e.tensor_tensor(out=s_[0:n, 3:4], in0=s_[0:n, 2:3], in1=s_[0:n, 2:3], op=ALU.mult), reads=[sb2], writes=[sb2])
                        P.op(DVE, lambda e, s_=s_, n=n: e.tensor_scalar(out=s_[0:n, 4:5], in0=s_[0:n, 1:2], scalar1=1.0 / 64, scalar2=64e-5, op0=ALU.mult, op1=ALU.add),
                             reads=[sb2], writes=[sb2])
                        P.op(DVE, lambda e, s_=s_, n=n: e.tensor_tensor(out=s_[0:n, 4:5], in0=s_[0:n, 4:5], in1=s_[0:n, 3:4], op=ALU.subtract), reads=[sb2], writes=[sb2])
                        P.op(POOL, lambda e, s_=s_, n=n: e.tensor_tensor(out=s_[0:n, 5:6], in0=s_[0:n, 4:5], in1=mh[0:n, 0:1], op=ALU.pow), reads=[sb2, mhb], writes=[sb2])
                        t1, t1b = t1r.next(); yo, yob = yor.next()
                        P.op(DVE, lambda e, t1=t1, yps=yps, s_=s_, n=n: e.tensor_scalar(out=t1[0:n, :], in0=yps, scalar1=s_[0:n, 2:3], scalar2=s_[0:n, 5:6],
                                                                                       op0=ALU.subtract, op1=ALU.mult), reads=[pwb, sb2], writes=[t1b])
                        P.op(DVE, lambda e, t1=t1, n=n, h=h: e.tensor_tensor(out=t1[0:n, :], in0=t1[0:n, :], in1=lnw[0:n, h * 64:(h + 1) * 64], op=ALU.mult),
                             reads=[t1b, pb_], writes=[t1b])
                        P.op(DVE, lambda e, t1=t1, n=n, h=h: e.tensor_tensor(out=t1[0:n, :], in0=t1[0:n, :], in1=lnb[0:n, h * 64:(h + 1) * 64], op=ALU.add),
                             reads=[t1b, pb_], writes=[t1b])
                        P.op(DVE, lambda e, t1=t1, n=n, g=g, vtm=vtm: e.scalar_tensor_tensor(out=t1[0:n, :], in0=vtm, scalar=GB[0:n, g, 64:65], in1=t1[0:n, :],
                                                                                           op0=ALU.mult, op1=ALU.add), reads=[t1b, TMb, GBb], writes=[t1b])
                        P.op(DVE, lambda e, yo=yo, t1=t1, n=n, g=g: e.tensor_tensor(out=yo[0:n, :], in0=t1[0:n, :], in1=GB[0:n, g, 0:64], op=ALU.mult),
                             reads=[t1b, GBb], writes=[yob])
                        P.dma(POOL, y[t0:t0 + n, h * 64:(h + 1) * 64], yo[0:n, :], reads=[yob], pwrites=[yb])
        P.dma(POOL, prm["sA_out"][h], H[:], reads=[Hb], pwrites=[K.sob])


import contextlib
import numpy as np

PRM_SHAPES = {
    "gla_a2": [16, 256], "gla_ab": [64, 4], "gla_normbc": [64, 128],
    "ml_cw": [128, 8, 4], "ml_cb": [128, 8], "ml_ib": [4, 1], "ml_fb": [4, 1], "ml_normbc": [64, 1024], "onehot": [4, 4, 128],
    "rw_muA": [64, 3, 8], "rw_muL": [96, 3], "rw_w2": [32, 512], "rw_a2": [32, 512], "rw_g2": [96, 512], "rw_ch": [64, 4, 8],
    "rw_rk": [64, 8, 2], "rw_lnw_bc": [64, 512], "rw_lnb_bc": [64, 512],
    "sA_in": [8, 64, 64], "sB_in": [4, 64, 128], "sC_in": [4, 128, 257], "mC_in": [4, 1], "hist_in": [NFMB * 128, 3],
    "flagE": [128, 1], "mask_i": [64, 64], "mask5": [64, 5, 64],
}
OUT_SHAPES = {"sA_out": [8, 64, 64], "sB_out": [4, 64, 128], "sC_out": [4, 128, 257], "mC_out": [4, 1], "hist_out": [NFMB * 128, 3]}


def host_consts():
    j = np.arange(64)
    mi = (j[None, :] >= j[:, None]).astype(np.float32)
    ms = (j[None, :] > j[:, None]).astype(np.float32)
    ml = (j[None, :] < j[:, None]).astype(np.float32)
    mask5 = np.stack([ms, mi, ms, mi, ml], 1)
    oh = np.zeros((4, 4, 128), np.float32)
    for h in range(4):
        oh[h, h, :] = 1.0
    return {"mask_i": mi, "mask5": np.ascontiguousarray(mask5), "onehot": oh}


def host_layer_params(z, l):
    f = lambda a: np.ascontiguousarray(a, dtype=np.float32)
    chT = lambda v: f(v.reshape(8, 64).T)
    mu = z["rw_mu"][l]
    d = {}
    d["gla_a2"] = f(z["gla_a2"][l]); d["gla_ab"] = f(z["gla_ab"][l].reshape(4, 64).T)
    d["gla_normbc"] = f(np.broadcast_to(z["gla_norm"][l], (64, 128)))
    cw = z["ml_conv_w"][l]
    d["ml_cw"] = f(cw.reshape(4, 8, 128).transpose(2, 1, 0)); d["ml_cb"] = f(z["ml_conv_b"][l].reshape(8, 128).T)
    d["ml_ib"] = f(z["ml_ib"][l].reshape(4, 1)); d["ml_fb"] = f(z["ml_fb"][l].reshape(4, 1))
    d["ml_normbc"] = f(np.broadcast_to(z["ml_norm"][l], (64, 1024)))
    d["rw_muA"] = f(np.stack([chT(mu[0:512]), chT(mu[512:1024]), chT(mu[1024:1536])], 1))
    muL = np.zeros((96, 3), np.float32); muL[0:32, 0] = mu[1536:1568]; muL[0:32, 1] = mu[1568:1600]; muL[0:96, 2] = mu[1600:1696]
    d["rw_muL"] = muL
    d["rw_w2"] = f(z["rw_w2"][l]); d["rw_a2"] = f(z["rw_a2"][l]); d["rw_g2"] = f(z["rw_g2"][l])
    d["rw_ch"] = f(np.stack([chT(z["rw_w0"][l]), chT(z["rw_a0"][l]), chT(z["rw_kk"][l]), chT(z["rw_ka"][l])], 1))
    rk = z["rw_rk"][l]
    d["rw_rk"] = f(np.stack([rk.T, rk.T], 2))
    d["rw_lnw_bc"] = f(np.broadcast_to(z["rw_ln_w"][l], (64, 512))); d["rw_lnb_bc"] = f(np.broadcast_to(z["rw_ln_b"][l], (64, 512)))
    return d


def host_layer_weights(z, l):
    return {"win": hp.prep_win(z["w_in"][l]), "wout": hp.prep_sq(z["w_out"][l], 4), "w1": hp.prep_sq(z["ffn_w1"][l], 11),
            "w3": hp.prep_sq(z["ffn_w3"][l], 11), "w2": hp.prep_w2(z["ffn_w2"][l]), "g1": hp.gT(z["norm_mix"][l]), "g2": hp.gT(z["norm_ffn"][l])}


def build_layer(debug=False, emit_out=True, do_final=True):
    nc = bass.Bass("TRN2", target_bir_lowering=False)
    C = Ctx(); C.nc = nc; C.P = Prog(nc); C.emit_out = emit_out
    P = C.P
    dr = lambda n, s, dt=F32, kind="ExternalInput": nc.dram_tensor(n, s, dt, kind=kind).ap()
    hin = dr("hin", [NTOK, D])
    win = dr("win", [13, 128, 8192]); wout = dr("wout", [4, 128, 8192])
    w1 = dr("w1", [11, 128, 8192]); w3 = dr("w3", [11, 128, 8192]); w2 = dr("w2", [4, 4, 128, 11 * 512])
    g1 = dr("g1", [128, 16]); g2 = dr("g2", [128, 16]); gf = dr("gf", [128, D])
    prm = {k: dr(k, s) for k, s in PRM_SHAPES.items()}
    for k, s in OUT_SHAPES.items():
        prm[k] = dr(k, s, F32, "ExternalOutput")
    dk = "ExternalOutput" if debug else "Internal"
    pf = dr("pf", [NFMB * 128, TP], F32, dk)
    pt = dr("pt", [NTOK, NTMC], F32, dk)
    y = dr("y", [NTOK, D], BF16, dk)
    hmid = dr("hmid", [NTOK, D], F32, dk)
    hout = dr("hout", [NTOK, D], F32, "ExternalOutput")
    out = dr("out", [NTOK - 16, D], F32, "ExternalOutput")
    aT = dr("aT", [5, 128, 44, 512], BF16, "Internal")
    hb, pfb, ptb, yb, hmb, hob, ob, ab = [Buf() for _ in range(8)]
    K = Ctx(); K.sob = Buf()
    with contextlib.ExitStack() as st0:
        K.ident = sbuf(C, st0, "ident", [128, 128], F32); identb = sbuf(C, st0, "identb", [128, 128], BF16)
        g1t = sbuf(C, st0, "g1t", [128, 16]); g2t = sbuf(C, st0, "g2t", [128, 16])
        K.idb, idbb, g1b, g2b = [Buf() for _ in range(4)]
        P.op(POOL, lambda e: e.memset(K.ident[:], 1.0), writes=[K.idb])
        P.op(POOL, lambda e: e.affine_select(out=K.ident[:], in_=K.ident[:], pattern=[[-1, 128]], base=0, channel_multiplier=1,
                                             compare_op=ALU.is_equal, fill=0.0), reads=[K.idb], writes=[K.idb])
        P.op(POOL, lambda e: e.tensor_copy(out=identb[:], in_=K.ident[:]), reads=[K.idb], writes=[idbb])
        P.dma(SP, g1t[:], g1, writes=[g1b]); P.dma(SP, g2t[:], g2, writes=[g2b])
        with contextlib.ExitStack() as st1:
            uT = sbuf(C, st1, "uT", [128, 16, NTOK], BF16); ub = Buf()
            pst = Ring([psum(C, st1, f"pst{i}", [128, 1024], BF16) for i in range(2)])
            psm = Ring([psum(C, st1, f"psm{i}", [128, 512], F32) for i in range(6)])
            with contextlib.ExitStack() as st:
                phase_norm(C, st, hin, hb, g1t, g1b, uT, ub, pst, identb, idbb)
            P.barrier()
            with contextlib.ExitStack() as st:
                phase_proj(C, st, uT, ub, win, pf, pfb, pt, ptb, psm, prm["hist_out"], K.sob)
            P.barrier()
        with contextlib.ExitStack() as st1:
            K.ones = sbuf(C, st1, "ones", [64, TP]); K.onesb = Buf()
            K.mask_i = sbuf(C, st1, "mask_i", [64, 64]); K.mib = Buf()
            K.mask5 = sbuf(C, st1, "mask5", [64, 5, 64]); K.m5b = Buf()
            K.flagE = sbuf(C, st1, "flagE", [128, 1]); K.fb = Buf()
            P.op(POOL, lambda e: e.memset(K.ones[:], 1.0), writes=[K.onesb])
            P.dma(SP, K.mask_i[:], prm["mask_i"], writes=[K.mib]); P.dma(SP, K.mask5[:], prm["mask5"], writes=[K.m5b])
            P.dma(SP, K.flagE[:], prm["flagE"], writes=[K.fb])
            with contextlib.ExitStack() as st:
                gate_prepass(C, st, pt, ptb)
            P.barrier()
            if True:
              with contextlib.ExitStack() as st:
                mixer_gla(C, st, pf, pfb, pt, ptb, y, yb, prm, K)
            P.barrier()
            if True:
              with contextlib.ExitStack() as st:
                mixer_mlstm(C, st, pf, pfb, pt, ptb, y, yb, prm, K)
            P.barrier()
            if True:
              with contextlib.ExitStack() as st:
                mixer_rwkv(C, st, pf, pfb, y, yb, prm, K)
            P.barrier()
        if emit_out:
            with contextlib.ExitStack() as st1:
                uT = sbuf(C, st1, "uT2", [128, 16, NTOK], BF16); ub = Buf()
                pst = Ring([psum(C, st1, f"pst{i}", [128, 1024], BF16) for i in range(2)])
                psm = Ring([psum(C, st1, f"psm{i}", [128, 512], F32) for i in range(6)])
                with contextlib.ExitStack() as st:
                    phase_wout(C, st, y, yb, hin, hb, hmid, hmb, wout, uT, ub, psm, pst, identb, idbb)
                P.barrier()
                with contextlib.ExitStack() as st:
                    phase_norm(C, st, hmid, hmb, g2t, g2b, uT, ub, pst, identb, idbb)
                P.barrier()
                with contextlib.ExitStack() as st:
                    phase_ffn1(C, st, uT, ub, w1, w3, aT, ab, psm)
                P.barrier()
    if emit_out:
        with contextlib.ExitStack() as st:
            psm = Ring([psum(C, st, f"psn{i}", [128, 512], F32) for i in range(6)])
            phase_ffn2(C, st, aT, ab, w2, hmid, hmb, hout, hob, psm)
        P.barrier()
        if do_final:
            with contextlib.ExitStack() as st:
                gft = sbuf(C, st, "gft2", [128, D]); gfb = Buf()
                P.dma(SP, gft[:], gf, writes=[gfb])
                phase_final_norm(C, st, hout, hob, gft, gfb, out, ob)
    fin = [K.sob, hob, ob]
    if debug:
        fin += [pfb, ptb, yb, hmb]
    P.finish(fin)
    P.emit()
    C.counts = {e: (len(P.ops[e]), sum(1 for o in P.ops[e] if o.signal)) for e in ENGS}
    return nc, C


import contextlib
import numpy as np

LAYER_KEYS = ["gla_a2", "gla_ab", "gla_normbc", "ml_cw", "ml_cb", "ml_ib", "ml_fb", "ml_normbc", "rw_muA", "rw_muL", "rw_w2", "rw_a2",
              "rw_g2", "rw_ch", "rw_rk", "rw_lnw_bc", "rw_lnb_bc"]
STATE_KEYS = ["sA", "sB", "sC", "mC", "hist"]


def emit_half(C, K, T, l, half):
    P = C.P
    hin, hinb = T["hin"][(l, half)]
    hout, houtb = T["hout"][(l, half)]
    prm = {k: T["lp"][k][l] for k in LAYER_KEYS}
    for k in ("mask_i", "mask5", "onehot"):
        prm[k] = T["const"][k]
    prm["flagE"] = T["flag1"] if half == 0 else T["flag0"]
    for k in STATE_KEYS:
        prm[k + "_in"] = T["zstate"][k] if half == 0 else T["state"][k][l]
        prm[k + "_out"] = T["state"][k][l] if half == 0 else T["sdump"][k]
    K.sob = T["stateb"][l] if half == 0 else T["sdumpb"]
    K.fb = Buf()
    win, wout, w1, w3, w2 = T["win"][l], T["wout"][l], T["w1"][l], T["w3"][l], T["w2"][l]
    pf, pfb, pt, ptb, y, yb, hmid, hmb, aT, ab = T["pf"], T["pfb"], T["pt"], T["ptb"], T["y"], T["yb"], T["hmid"], T["hmb"], T["aT"], T["ab"]
    identb, idbb = K.identb, K.idbb
    with contextlib.ExitStack() as st1:
        uT = sbuf(C, st1, "uT", [128, 16, NTOK], BF16); ub = Buf()
        pst = Ring([psum(C, st1, f"pst{i}", [128, 1024], BF16) for i in range(2)])
        psm = Ring([psum(C, st1, f"psm{i}", [128, 512], F32) for i in range(6)])
        with contextlib.ExitStack() as st:
            phase_norm(C, st, hin, hinb, K.g1t[l], K.g1b, uT, ub, pst, identb, idbb)
        P.barrier()
        with contextlib.ExitStack() as st:
            phase_proj(C, st, uT, ub, win, pf, pfb, pt, ptb, psm, prm["hist_out"], K.sob)
        P.barrier()
    with contextlib.ExitStack() as st1:
        K.flagE = sbuf(C, st1, "flagE", [128, 1])
        P.dma(SP, K.flagE[:], prm["flagE"], writes=[K.fb])
        K.ones = sbuf(C, st1, "ones", [64, TP]); K.onesb = Buf()
        K.mask_i = sbuf(C, st1, "mask_i", [64, 64]); K.mib = Buf()
        K.mask5 = sbuf(C, st1, "mask5", [64, 5, 64]); K.m5b = Buf()
        P.op(POOL, lambda e: e.memset(K.ones[:], 1.0), writes=[K.onesb])
        P.dma(SP, K.mask_i[:], T["const"]["mask_i"], writes=[K.mib]); P.dma(SP, K.mask5[:], T["const"]["mask5"], writes=[K.m5b])
        with contextlib.ExitStack() as st:
            gate_prepass(C, st, pt, ptb)
        P.barrier()
        with contextlib.ExitStack() as st:
            mixer_gla(C, st, pf, pfb, pt, ptb, y, yb, prm, K)
        P.barrier()
        with contextlib.ExitStack() as st:
            mixer_mlstm(C, st, pf, pfb, pt, ptb, y, yb, prm, K)
        P.barrier()
        with contextlib.ExitStack() as st:
            mixer_rwkv(C, st, pf, pfb, y, yb, prm, K)
        P.barrier()
    with contextlib.ExitStack() as st1:
        uT = sbuf(C, st1, "uT2", [128, 16, NTOK], BF16); ub = Buf()
        pst = Ring([psum(C, st1, f"pst{i}", [128, 1024], BF16) for i in range(2)])
        psm = Ring([psum(C, st1, f"psm{i}", [128, 512], F32) for i in range(6)])
        with contextlib.ExitStack() as st:
            phase_wout(C, st, y, yb, hin, hinb, hmid, hmb, wout, uT, ub, psm, pst, identb, idbb)
        P.barrier()
        with contextlib.ExitStack() as st:
            phase_norm(C, st, hmid, hmb, K.g2t[l], K.g1b, uT, ub, pst, identb, idbb)
        P.barrier()
        with contextlib.ExitStack() as st:
            phase_ffn1(C, st, uT, ub, w1, w3, aT, ab, psm)
        P.barrier()
    with contextlib.ExitStack() as st:
        psm = Ring([psum(C, st, f"psn{i}", [128, 512], F32) for i in range(6)])
        phase_ffn2(C, st, aT, ab, w2, hmid, hmb, hout, houtb, psm)
    P.barrier()
    if l == 1:
        with contextlib.ExitStack() as st:
            gft = sbuf(C, st, "gft2", [128, D]); gfb = Buf()
            P.dma(SP, gft[:], T["gf"], writes=[gfb])
            phase_final_norm(C, st, hout, houtb, gft, gfb, T["out"][half], T["outb"])
        P.barrier()


def build_fused(nlayers=2, halves=(0, 1)):
    nc = bass.Bass("TRN2", target_bir_lowering=False)
    C = Ctx(); C.nc = nc; C.P = Prog(nc); C.emit_out = True
    P = C.P
    dr = lambda n, s, dt=F32, kind="ExternalInput": nc.dram_tensor(n, s, dt, kind=kind).ap()
    T = {}
    xin = [dr("xE", [NTOK, D]), dr("xO", [NTOK, D])]
    T["win"] = dr("win", [2, 13, 128, 8192]); T["wout"] = dr("wout", [2, 4, 128, 8192])
    T["w1"] = dr("w1", [2, 11, 128, 8192]); T["w3"] = dr("w3", [2, 11, 128, 8192]); T["w2"] = dr("w2", [2, 4, 4, 128, 11 * 512])
    g1 = dr("g1", [2, 128, 16]); g2 = dr("g2", [2, 128, 16]); T["gf"] = dr("gf", [128, D])
    T["lp"] = {k: dr(k, [2] + PRM_SHAPES[k]) for k in LAYER_KEYS}
    T["const"] = {k: dr(k, PRM_SHAPES[k]) for k in ("mask_i", "mask5", "onehot")}
    T["flag1"] = dr("flag1", [128, 1]); T["flag0"] = dr("flag0", [128, 1])
    T["zstate"] = {k: dr("z_" + k, PRM_SHAPES[k + "_in"]) for k in STATE_KEYS}
    T["state"] = {k: dr("st_" + k, [2] + PRM_SHAPES[k + "_in"], F32, "Internal") for k in STATE_KEYS}
    T["sdump"] = {k: dr("sd_" + k, PRM_SHAPES[k + "_in"], F32, "Internal") for k in STATE_KEYS}
    T["stateb"] = [Buf(), Buf()]; T["sdumpb"] = Buf()
    T["pf"] = dr("pf", [NFMB * 128, TP], F32, "Internal"); T["pt"] = dr("pt", [NTOK, NTMC], F32, "Internal")
    T["y"] = dr("y", [NTOK, D], BF16, "Internal"); T["hmid"] = dr("hmid", [NTOK, D], F32, "Internal")
    T["aT"] = dr("aT", [5, 128, 44, 512], BF16, "Internal")
    for k in ("pfb", "ptb", "yb", "hmb", "ab", "outb"):
        T[k] = Buf()
    h1 = [dr("h1E", [NTOK, D], F32, "Internal"), dr("h1O", [NTOK, D], F32, "Internal")]
    h2 = [dr("h2E", [NTOK, D], F32, "Internal"), dr("h2O", [NTOK, D], F32, "Internal")]
    T["out"] = [dr("outE", [2048, D], F32, "ExternalOutput"), dr("outO", [2048, D], F32, "ExternalOutput")]
    xb = [Buf(), Buf()]; h1b = [Buf(), Buf()]; h2b = [Buf(), Buf()]
    T["hin"] = {(0, 0): (xin[0], xb[0]), (0, 1): (xin[1], xb[1]), (1, 0): (h1[0], h1b[0]), (1, 1): (h1[1], h1b[1])}
    T["hout"] = {(0, 0): (h1[0], h1b[0]), (0, 1): (h1[1], h1b[1]), (1, 0): (h2[0], h2b[0]), (1, 1): (h2[1], h2b[1])}
    K = Ctx()
    with contextlib.ExitStack() as st0:
        K.ident = sbuf(C, st0, "ident", [128, 128], F32); K.identb = sbuf(C, st0, "identb", [128, 128], BF16)
        K.g1t = [sbuf(C, st0, f"g1t{l}", [128, 16]) for l in range(2)]; K.g2t = [sbuf(C, st0, f"g2t{l}", [128, 16]) for l in range(2)]
        K.idb, K.idbb, K.g1b = Buf(), Buf(), Buf()
        P.op(POOL, lambda e: e.memset(K.ident[:], 1.0), writes=[K.idb])
        P.op(POOL, lambda e: e.affine_select(out=K.ident[:], in_=K.ident[:], pattern=[[-1, 128]], base=0, channel_multiplier=1,
                                             compare_op=ALU.is_equal, fill=0.0), reads=[K.idb], writes=[K.idb])
        P.op(POOL, lambda e: e.tensor_copy(out=K.identb[:], in_=K.ident[:]), reads=[K.idb], writes=[K.idbb])
        for l in range(2):
            P.dma(SP, K.g1t[l][:], g1[l], pwrites=[K.g1b]); P.dma(SP, K.g2t[l][:], g2[l], pwrites=[K.g1b])
        for l in range(nlayers):
            for half in halves:
                emit_half(C, K, T, l, half)
    P.finish([T["outb"], T["sdumpb"], T["stateb"][0], T["stateb"][1]])
    P.emit()
    C.counts = {e: (len(P.ops[e]), sum(1 for o in P.ops[e] if o.signal)) for e in ENGS}
    return nc, C


from concourse.bass_utils import run_bass_kernel_spmd

_PROG = {}


def kernel(**z):
    x = np.asarray(z["x"], np.float32)
    meta = np.asarray(z["meta_tokens"], np.float32)
    if "nc" not in _PROG:
        _PROG["nc"] = build_fused()[0]
    nc = _PROG["nc"]
    shared = {}
    shared.update(host_consts())
    shared["gf"] = np.ascontiguousarray(np.broadcast_to(np.asarray(z["norm_final"], np.float32), (128, D)))
    Ws = [host_layer_weights(z, l) for l in range(2)]
    for k in ("win", "wout", "w1", "w3", "w2", "g1", "g2"):
        shared[k] = np.stack([Ws[0][k], Ws[1][k]])
    Ps = [host_layer_params(z, l) for l in range(2)]
    for k in LAYER_KEYS:
        shared[k] = np.stack([Ps[0][k], Ps[1][k]])
    shared["flag1"] = np.ones((128, 1), np.float32)
    shared["flag0"] = np.zeros((128, 1), np.float32)
    for k in STATE_KEYS:
        shared["z_" + k] = np.zeros(PRM_SHAPES[k + "_in"], np.float32)
    in_maps = []
    for c in range(8):
        b = c % 4
        im = dict(shared)
        im["xE"] = np.ascontiguousarray(np.concatenate([meta, x[b, :2048]], 0))
        im["xO"] = np.ascontiguousarray(np.concatenate([meta, x[b, 2048:]], 0))
        in_maps.append(im)
    res = run_bass_kernel_spmd(nc, in_maps, core_ids=list(range(8))).results
    out = np.zeros((4, 4096, D), np.float32)
    for b in range(4):
        out[b, :2048] = np.asarray(res[b]["outE"], np.float32)
        out[b, 2048:] = np.asarray(res[b]["outO"], np.float32)
    return out
```

```python
import contextlib
import numpy as np
import concourse.bass as bass
import concourse.mybir as mybir

F32 = mybir.dt.float32
BF16 = mybir.dt.bfloat16
AF = mybir.ActivationFunctionType
ALU = mybir.AluOpType
AX = mybir.AxisListType

PE, ACT, DVE, POOL, SP = "pe", "act", "dve", "pool", "sp"
ENGS = [PE, ACT, DVE, POOL, SP]
SEG = 30000
NSLOT = 6


class Buf:
    __slots__ = ("name", "w", "ws", "rs")

    def __init__(self, name=""):
        self.name = name
        self.w = None
        self.ws = {}
        self.rs = {}


def _key(o):
    return (o.eng, o.slot if o.dma else None)


class Op:
    __slots__ = ("eng", "fn", "deps", "dma", "idx", "signal", "ev", "slot", "name")

    def __init__(self, eng, fn, dma):
        self.eng = eng
        self.fn = fn
        self.dma = dma
        self.deps = []
        self.signal = False
        self.ev = None
        self.slot = None
        self.name = ""


class Prog:
    def __init__(self, nc, same_engine_sync=True):
        self.nc = nc
        self.ops = {e: [] for e in ENGS}
        self.same = same_engine_sync
        self.ndma = {e: 0 for e in ENGS}
        self.final_deps = []
        self.pending_barrier = None
        self.scopes = False
        self.phase = ""

    def op(self, eng, fn, reads=(), writes=(), dma=False, name="", pwrites=()):
        o = Op(eng, fn, dma)
        o.name = getattr(self, "phase", "")
        if dma:
            o.slot = self.ndma[eng] % NSLOT
            self.ndma[eng] += 1
            o.signal = True
        o.idx = len(self.ops[eng])
        deps = []
        for r in reads:
            if r.w is not None:
                deps.append(r.w)
            deps.extend(r.ws.values())
        for w in writes:
            if w.w is not None:
                deps.append(w.w)
            deps.extend(w.ws.values())
            deps.extend(w.rs.values())
        for w in pwrites:
            if w.w is not None:
                deps.append(w.w)
            deps.extend(w.rs.values())
        if self.pending_barrier and self.pending_barrier.get(eng):
            deps.extend(self.pending_barrier[eng])
            self.pending_barrier[eng] = []
        best = {}
        for d in deps:
            if d is o:
                continue
            if d.eng == eng and not d.dma:
                if eng == PE or not self.same:
                    continue
            k = _key(d)
            if k not in best or best[k].idx < d.idx:
                best[k] = d
        for d in best.values():
            o.deps.append(d)
            d.signal = True
        for w in writes:
            w.w = o
            w.ws = {}
            w.rs = {}
        for w in pwrites:
            w.ws[_key(o)] = o
        for r in reads:
            r.rs[_key(o)] = o
        self.ops[eng].append(o)
        return o

    def dma(self, eng, out, in_, reads=(), writes=(), pwrites=(), **kw):
        return self.op(eng, lambda e: e.dma_start(out=out, in_=in_, **kw), reads, writes, dma=True, pwrites=pwrites)

    def finish(self, bufs):
        for b in bufs:
            for o in ([b.w] if b.w is not None else []) + list(b.ws.values()):
                self.final_deps.append(o)
                o.signal = True

    def emit(self):
        nc = self.nc
        with contextlib.ExitStack() as st:
            csem = {}
            for e in (PE, ACT, DVE, POOL):
                n = sum(1 for o in self.ops[e] if o.signal and not o.dma)
                nseg = n // SEG + 1
                csem[e] = [st.enter_context(nc.semaphore(f"c_{e}_{i}")) for i in range(nseg)]
            dsem = {}
            for e in (ACT, POOL, SP):
                if self.ndma[e] > 0:
                    dsem[e] = [st.enter_context(nc.semaphore(f"d_{e}_{i}")) for i in range(NSLOT)]
            for e in ENGS:
                cnt = 0
                dcur = [0] * NSLOT
                for o in self.ops[e]:
                    if o.dma:
                        prev = dcur[o.slot]
                        dcur[o.slot] += 16
                        o.ev = (dsem[e][o.slot], dcur[o.slot], prev)
                    elif o.signal:
                        seg, v = divmod(cnt, SEG)
                        o.ev = (csem[e][seg], v + 1, None)
                        cnt += 1
            block = st.enter_context(nc.Block())
            handles = {PE: block.tensor, ACT: block.scalar, DVE: block.vector,
                       POOL: block.gpsimd, SP: block.sync}
            for e in ENGS:
                ops = self.ops[e]
                fdeps = self.final_deps if e == SP else []
                if not ops and not fdeps:
                    continue

                def body(eng, ops=ops, fdeps=fdeps):
                    known = {}

                    def wait(sem, val):
                        k = id(sem)
                        if known.get(k, 0) >= val:
                            return
                        eng.wait_ge(sem, val)
                        known[k] = val

                    cur_ph, sid = None, None
                    for o in ops:
                        if self.scopes and o.name != cur_ph:
                            if cur_ph:
                                nc.leave_named_scope(cur_ph, sid, False)
                            cur_ph = o.name
                            if cur_ph:
                                sid, _ = nc.enter_named_scope(cur_ph, False)
                        for d in o.deps:
                            wait(d.ev[0], d.ev[1])
                        if o.dma and o.ev[2] > 0:
                            wait(o.ev[0], o.ev[2])
                        ins = o.fn(eng)
                        if o.dma:
                            ins.then_inc(o.ev[0], 16)
                        elif o.signal:
                            ins.then_inc(o.ev[0], 1)
                    if self.scopes and cur_ph:
                        nc.leave_named_scope(cur_ph, sid, False)
                    for d in fdeps:
                        wait(d.ev[0], d.ev[1])

                handles[e](body)


def _barrier(self):
    lasts = []
    for e in ENGS:
        ops = self.ops[e]
        if not ops:
            continue
        for o in reversed(ops):
            if not o.dma:
                lasts.append(o)
                break
        seen = set()
        for o in reversed(ops):
            if o.dma and o.slot not in seen:
                seen.add(o.slot)
                lasts.append(o)
            if len(seen) == NSLOT:
                break
    for o in lasts:
        o.signal = True
    self.pending_barrier = {e: list(lasts) for e in ENGS}


Prog.barrier = _barrier


class Ring:
    def __init__(self, tiles):
        self.tiles = tiles
        self.bufs = [Buf() for _ in tiles]
        self.i = 0

    def next(self):
        k = self.i % len(self.tiles)
        self.i += 1
        return self.tiles[k], self.bufs[k]


class _HP:
    pass
hp = _HP()


import numpy as np

A0, B0, C0 = 0, 1696, 3248


def fm_blocks():
    blks = []
    for seg in range(3):
        for i in range(4):
            blks.append(list(range(A0 + seg * 512 + i * 128, A0 + seg * 512 + (i + 1) * 128)))
    blks.append(list(range(A0 + 1536, A0 + 1600)))
    blks.append(list(range(A0 + 1600, A0 + 1696)))
    for seg in range(2):
        for i in range(2):
            blks.append(list(range(B0 + seg * 256 + i * 128, B0 + seg * 256 + (i + 1) * 128)))
    blks.append(list(range(B0 + 1024, B0 + 1040)))
    for seg in range(2):
        for i in range(4):
            blks.append(list(range(C0 + seg * 512 + i * 128, C0 + seg * 512 + (i + 1) * 128)))
    blks.append(list(range(C0 + 2048, C0 + 2056)))
    assert len(blks) == 28
    return blks


def tm_cols():
    cols = []
    cols += list(range(B0 + 512, B0 + 1024))
    cols += list(range(B0 + 1040, B0 + 1552))
    cols += list(range(C0 + 1024, C0 + 2048))
    cols += list(range(C0 + 2056, C0 + 3080))
    assert len(cols) == 3072
    return cols


def tile_k(W, ncol=512):
    K = W.shape[0]
    return np.ascontiguousarray(W.reshape(K // 128, 128, ncol).transpose(1, 0, 2).reshape(128, (K // 128) * ncol))


def prep_win(w):
    blks = fm_blocks()
    out = np.zeros((13, 128, 8192), np.float32)
    for wi in range(7):
        Wt = np.zeros((2048, 512), np.float32)
        for bi in range(4):
            cols = blks[wi * 4 + bi]
            Wt[:, bi * 128:bi * 128 + len(cols)] = w[:, cols]
        out[wi] = tile_k(Wt)
    tc = tm_cols()
    for ci in range(6):
        out[7 + ci] = tile_k(w[:, tc[ci * 512:(ci + 1) * 512]])
    return out


def prep_sq(w, ncb):
    return np.stack([tile_k(w[:, i * 512:(i + 1) * 512]) for i in range(ncb)])


def prep_w2(w):
    out = np.zeros((4, 4, 128, 11 * 512), np.float32)
    for cb in range(4):
        for pc in range(4):
            out[cb, pc] = tile_k(w[pc * 1408:(pc + 1) * 1408, cb * 512:(cb + 1) * 512])
    return out


def gT(g):
    return np.ascontiguousarray(g.reshape(16, 128).T)


for _n in ['fm_blocks','tm_cols','tile_k','prep_win','prep_sq','prep_w2','gT']:
    setattr(hp, _n, globals()[_n])


import contextlib

D = 2048
NTOK = 2064
NMETA = 16
TP = 2070
DFF = 5632
NFMB = 28
NTMC = 3072
EPS = 1e-6

TG = [(0, 16)] + [(16 + 512 * i, 512) for i in range(4)]
TT = [(0, 16)] + [(16 + 128 * i, 128) for i in range(16)]


def pfcol(t):
    return 3 + t if t < 16 else t + 6


class Ctx:
    pass


_uid = [0]


def sbuf(C, st, name, shape, dt=F32):
    _uid[0] += 1
    return st.enter_context(C.nc.sbuf_tensor(f"{name}_{_uid[0]}", shape, dt))


def psum(C, st, name, shape, dt=F32):
    _uid[0] += 1
    return st.enter_context(C.nc.psum_tensor(f"{name}_{_uid[0]}", shape, dt))


def make_wloader(C, st, n_stage=2, n_wb=2, stage_elems=8192):
    stage = Ring([sbuf(C, st, f"wst{i}", [128, stage_elems], F32) for i in range(n_stage)])
    return stage


def load_w(C, stage, dst_ap, dst_buf, src_ap, nelem, cast_eng=POOL):
    P = C.P
    stt, stb = stage.next()
    P.dma(SP, stt[:, 0:nelem], src_ap, writes=[stb])
    P.op(cast_eng, lambda e: e.tensor_copy(out=dst_ap, in_=stt[:, 0:nelem]), reads=[stb], writes=[dst_buf])


def phase_norm(C, st, hsrc, hbuf, gT, gbuf, uT, ubuf, ps_t, ident_bf, idbuf):
    P = C.P
    hring = Ring([sbuf(C, st, f"nh{i}", [128, D], F32) for i in range(2)])
    hnring = Ring([sbuf(C, st, f"nhn{i}", [128, D], BF16) for i in range(2)])
    junk = sbuf(C, st, "njunk", [128, D], BF16)
    jb = Buf()
    stat = Ring([sbuf(C, st, f"nst{i}", [128, 4], F32) for i in range(2)])
    mh = sbuf(C, st, "nmh", [128, 1], F32)
    mhb = Buf()
    P.op(POOL, lambda e: e.memset(mh[:], -0.5), writes=[mhb])
    for (t0, nt) in TT:
        ht, hb = hring.next()
        hn, hnb = hnring.next()
        s, sb_ = stat.next()
        P.dma(SP, ht[0:nt, :], hsrc[t0:t0 + nt, :], reads=[hbuf], writes=[hb])
        P.op(ACT, lambda e, ht=ht, s=s, nt=nt: e.activation(out=junk[0:nt, :], in_=ht[0:nt, :], func=AF.Square,
                                                            accum_out=s[0:nt, 0:1]), reads=[hb], writes=[jb, sb_])
        P.op(DVE, lambda e, s=s, nt=nt: e.tensor_scalar(out=s[0:nt, 1:2], in0=s[0:nt, 0:1], scalar1=1.0 / D, scalar2=EPS,
                                                        op0=ALU.mult, op1=ALU.add), reads=[sb_], writes=[sb_])
        P.op(POOL, lambda e, s=s, nt=nt: e.tensor_tensor(out=s[0:nt, 2:3], in0=s[0:nt, 1:2], in1=mh[0:nt, :], op=ALU.pow),
             reads=[sb_, mhb], writes=[sb_])
        P.op(DVE, lambda e, ht=ht, hn=hn, s=s, nt=nt: e.tensor_scalar(out=hn[0:nt, :], in0=ht[0:nt, :], scalar1=s[0:nt, 2:3],
                                                                      scalar2=None, op0=ALU.mult), reads=[hb, sb_], writes=[hnb])
        for half in range(2):
            pt_, ptb = ps_t.next()
            for k in range(8):
                kb = half * 8 + k
                P.op(PE, lambda e, pt_=pt_, hn=hn, kb=kb, k=k, nt=nt: e.transpose(
                    out=pt_[:, k * 128:k * 128 + nt], in_=hn[0:nt, kb * 128:(kb + 1) * 128], identity=ident_bf[0:nt, 0:nt]),
                    reads=[hnb, idbuf], writes=[ptb])
            eng = DVE if half == 0 else POOL
            if half == 0:
                P.op(DVE, lambda e, pt_=pt_, nt=nt, t0=t0, half=half: e.tensor_tensor(
                    out=uT[:, half * 8:half * 8 + 8, t0:t0 + nt],
                    in0=pt_[:].rearrange("p (k t) -> p k t", k=8)[:, :, 0:nt],
                    in1=gT[:, half * 8:half * 8 + 8].unsqueeze(2).broadcast_to([128, 8, nt]), op=ALU.mult),
                    reads=[ptb, gbuf], pwrites=[ubuf])
            else:
                P.op(DVE, lambda e, pt_=pt_, nt=nt, t0=t0, half=half: e.tensor_tensor(
                    out=uT[:, half * 8:half * 8 + 8, t0:t0 + nt],
                    in0=pt_[:].rearrange("p (k t) -> p k t", k=8)[:, :, 0:nt],
                    in1=gT[:, half * 8:half * 8 + 8].unsqueeze(2).broadcast_to([128, 8, nt]), op=ALU.mult),
                    reads=[ptb, gbuf], pwrites=[ubuf])


def phase_proj(C, st, uT, ubuf, w_dram, pf, pfbuf, pt, ptbuf, ps_mm, hist_out=None, hob=None):
    P = C.P
    stage = make_wloader(C, st)
    wb = Ring([sbuf(C, st, f"pwb{i}", [128, 16, 512], BF16) for i in range(2)])
    ev = Ring([sbuf(C, st, f"pev{i}", [128, 512], F32) for i in range(4)])
    cnt = 0
    for wi in range(13):
        wt, wbuf = wb.next()
        load_w(C, stage, wt[:].rearrange("p k c -> p (k c)"), wbuf, w_dram[wi], 8192)
        if wi < 7:
            for bi in range(4):
                blk = wi * 4 + bi
                for (t0, nt) in TG:
                    pm, pmb = ps_mm.next()
                    for kb in range(16):
                        P.op(PE, lambda e, pm=pm, wt=wt, kb=kb, bi=bi, t0=t0, nt=nt: e.matmul(
                            pm[:, 0:nt], lhsT=wt[:, kb, bi * 128:(bi + 1) * 128], rhs=uT[:, kb, t0:t0 + nt],
                            start=(kb == 0), stop=(kb == 15)), reads=[wbuf, ubuf], writes=[pmb])
                    et, eb = ev.next()
                    if cnt % 2 == 0:
                        P.op(ACT, lambda e, et=et, pm=pm, nt=nt: e.activation(out=et[:, 0:nt], in_=pm[:, 0:nt], func=AF.Copy),
                             reads=[pmb], writes=[eb])
                    else:
                        P.op(DVE, lambda e, et=et, pm=pm, nt=nt: e.tensor_copy(out=et[:, 0:nt], in_=pm[:, 0:nt]),
                             reads=[pmb], writes=[eb])
                    cnt += 1
                    c0 = pfcol(t0)
                    P.dma(POOL, pf[blk * 128:(blk + 1) * 128, c0:c0 + nt], et[:, 0:nt], reads=[eb], pwrites=[pfbuf])
                    if hist_out is not None and t0 + nt == NTOK:
                        P.dma(POOL, hist_out[blk * 128:(blk + 1) * 128, :], et[:, nt - 3:nt], reads=[eb], pwrites=[hob])
        else:
            ci = wi - 7
            for (t0, nt) in TT:
                pm, pmb = ps_mm.next()
                for kb in range(16):
                    P.op(PE, lambda e, pm=pm, wt=wt, kb=kb, t0=t0, nt=nt: e.matmul(
                        pm[0:nt, :], lhsT=uT[:, kb, t0:t0 + nt], rhs=wt[:, kb, :],
                        start=(kb == 0), stop=(kb == 15)), reads=[wbuf, ubuf], writes=[pmb])
                et, eb = ev.next()
                if cnt % 2 == 0:
                    P.op(ACT, lambda e, et=et, pm=pm, nt=nt: e.activation(out=et[0:nt, :], in_=pm[0:nt, :], func=AF.Copy),
                         reads=[pmb], writes=[eb])
                else:
                    P.op(DVE, lambda e, et=et, pm=pm, nt=nt: e.tensor_copy(out=et[0:nt, :], in_=pm[0:nt, :]),
                         reads=[pmb], writes=[eb])
                cnt += 1
                P.dma(POOL, pt[t0:t0 + nt, ci * 512:(ci + 1) * 512], et[0:nt, :], reads=[eb], pwrites=[ptbuf])


def phase_wout(C, st, y, ybuf, hsrc, hbuf, hdst, hdbuf, w_dram, uT, ubuf, ps_mm, ps_t, ident_bf, idbuf):
    P = C.P
    yr = Ring([sbuf(C, st, f"oy{i}", [128, D], BF16) for i in range(2)])
    for (t0, nt) in TT:
        yt, yb = yr.next()
        P.dma(SP, yt[0:nt, :], y[t0:t0 + nt, :], reads=[ybuf], writes=[yb])
        for half in range(2):
            pt_, ptb = ps_t.next()
            for k in range(8):
                kb = half * 8 + k
                P.op(PE, lambda e, pt_=pt_, yt=yt, kb=kb, k=k, nt=nt: e.transpose(
                    out=pt_[:, k * 128:k * 128 + nt], in_=yt[0:nt, kb * 128:(kb + 1) * 128], identity=ident_bf[0:nt, 0:nt]),
                    reads=[yb, idbuf], writes=[ptb])
            eng = ACT if half == 0 else DVE
            if half == 0:
                P.op(ACT, lambda e, pt_=pt_, nt=nt, t0=t0, half=half: e.activation(
                    out=uT[:, half * 8:half * 8 + 8, t0:t0 + nt],
                    in_=pt_[:].rearrange("p (k t) -> p k t", k=8)[:, :, 0:nt], func=AF.Copy), reads=[ptb], pwrites=[ubuf])
            else:
                P.op(DVE, lambda e, pt_=pt_, nt=nt, t0=t0, half=half: e.tensor_copy(
                    out=uT[:, half * 8:half * 8 + 8, t0:t0 + nt],
                    in_=pt_[:].rearrange("p (k t) -> p k t", k=8)[:, :, 0:nt]), reads=[ptb], pwrites=[ubuf])
    stage = make_wloader(C, st)
    wb = Ring([sbuf(C, st, f"owb{i}", [128, 16, 512], BF16) for i in range(2)])
    hr = Ring([sbuf(C, st, f"ohr{i}", [128, 512], F32) for i in range(3)])
    ev = Ring([sbuf(C, st, f"oev{i}", [128, 512], F32) for i in range(3)])
    for ci in range(4):
        wt, wbuf = wb.next()
        load_w(C, stage, wt[:].rearrange("p k c -> p (k c)"), wbuf, w_dram[ci], 8192)
        for (t0, nt) in TT:
            ho, hob = hr.next()
            P.dma(SP, ho[0:nt, :], hsrc[t0:t0 + nt, ci * 512:(ci + 1) * 512], reads=[hbuf], writes=[hob])
            pm, pmb = ps_mm.next()
            for kb in range(16):
                P.op(PE, lambda e, pm=pm, wt=wt, kb=kb, t0=t0, nt=nt: e.matmul(
                    pm[0:nt, :], lhsT=uT[:, kb, t0:t0 + nt], rhs=wt[:, kb, :],
                    start=(kb == 0), stop=(kb == 15)), reads=[wbuf, ubuf], writes=[pmb])
            et, eb = ev.next()
            P.op(DVE, lambda e, et=et, pm=pm, ho=ho, nt=nt: e.tensor_tensor(out=et[0:nt, :], in0=pm[0:nt, :], in1=ho[0:nt, :],
                                                                            op=ALU.add), reads=[pmb, hob], writes=[eb])
            P.dma(POOL, hdst[t0:t0 + nt, ci * 512:(ci + 1) * 512], et[0:nt, :], reads=[eb], pwrites=[hdbuf])


def phase_ffn1(C, st, uT, ubuf, w1_dram, w3_dram, aT, abuf, ps_mm):
    P = C.P
    stage = make_wloader(C, st)
    w1b = Ring([sbuf(C, st, f"f1w{i}", [128, 16, 512], BF16) for i in range(2)])
    w3b = Ring([sbuf(C, st, f"f3w{i}", [128, 16, 512], BF16) for i in range(2)])
    sg = Ring([sbuf(C, st, f"fsg{i}", [128, 512], F32) for i in range(3)])
    av = Ring([sbuf(C, st, f"fav{i}", [128, 512], BF16) for i in range(3)])
    for gi in range(11):
        w1t, w1buf = w1b.next()
        w3t, w3buf = w3b.next()
        load_w(C, stage, w1t[:].rearrange("p k c -> p (k c)"), w1buf, w1_dram[gi], 8192)
        load_w(C, stage, w3t[:].rearrange("p k c -> p (k c)"), w3buf, w3_dram[gi], 8192)
        for bi in range(4):
            j = gi * 4 + bi
            for gidx, (t0, nt) in enumerate(TG):
                pa, pab = ps_mm.next()
                for kb in range(16):
                    P.op(PE, lambda e, pa=pa, w1t=w1t, kb=kb, bi=bi, t0=t0, nt=nt: e.matmul(
                        pa[:, 0:nt], lhsT=w1t[:, kb, bi * 128:(bi + 1) * 128], rhs=uT[:, kb, t0:t0 + nt],
                        start=(kb == 0), stop=(kb == 15)), reads=[w1buf, ubuf], writes=[pab])
                pb_, pbb = ps_mm.next()
                for kb in range(16):
                    P.op(PE, lambda e, pb_=pb_, w3t=w3t, kb=kb, bi=bi, t0=t0, nt=nt: e.matmul(
                        pb_[:, 0:nt], lhsT=w3t[:, kb, bi * 128:(bi + 1) * 128], rhs=uT[:, kb, t0:t0 + nt],
                        start=(kb == 0), stop=(kb == 15)), reads=[w3buf, ubuf], writes=[pbb])
                s, sb_ = sg.next()
                a, ab_ = av.next()
                P.op(ACT, lambda e, s=s, pa=pa, nt=nt: e.activation(out=s[:, 0:nt], in_=pa[:, 0:nt], func=AF.Silu),
                     reads=[pab], writes=[sb_])
                P.op(DVE, lambda e, a=a, s=s, pb_=pb_, nt=nt: e.tensor_tensor(out=a[:, 0:nt], in0=pb_[:, 0:nt], in1=s[:, 0:nt],
                                                                              op=ALU.mult), reads=[pbb, sb_], writes=[ab_])
                P.dma(POOL, aT[gidx, :, j, 0:nt], a[:, 0:nt], reads=[ab_], pwrites=[abuf])


def phase_ffn2(C, st, aT, abuf, w2_dram, hsrc, hbuf, hdst, hdbuf, ps_mm):
    P = C.P
    stage = Ring([sbuf(C, st, f"gst{i}", [128, 11 * 512], F32) for i in range(2)])
    w2b = Ring([sbuf(C, st, f"gw{i}", [128, 44, 512], BF16) for i in range(1)])
    ar = Ring([sbuf(C, st, f"gar{i}", [128, 44, 512], BF16) for i in range(2)])
    hr = Ring([sbuf(C, st, f"ghr{i}", [128, 512], F32) for i in range(3)])
    ev = Ring([sbuf(C, st, f"gev{i}", [128, 512], F32) for i in range(3)])
    for cb in range(4):
        wt, wbuf = w2b.next()
        for pc in range(4):
            load_w(C, stage, wt[:, pc * 11:(pc + 1) * 11, :].rearrange("p k c -> p (k c)"), wbuf, w2_dram[cb, pc], 11 * 512)
        for gidx, (g0, gn) in enumerate(TG):
            at, atb = ar.next()
            P.dma(SP, at[:, :, 0:gn], aT[gidx, :, :, 0:gn], reads=[abuf], writes=[atb])
            for s0 in range(0, gn, 128):
                nt = min(128, gn - s0)
                t0 = g0 + s0
                ho, hob = hr.next()
                P.dma(SP, ho[0:nt, :], hsrc[t0:t0 + nt, cb * 512:(cb + 1) * 512], reads=[hbuf], writes=[hob])
                pm, pmb = ps_mm.next()
                for j in range(44):
                    P.op(PE, lambda e, pm=pm, at=at, wt=wt, j=j, s0=s0, nt=nt: e.matmul(
                        pm[0:nt, :], lhsT=at[:, j, s0:s0 + nt], rhs=wt[:, j, :],
                        start=(j == 0), stop=(j == 43)), reads=[wbuf, atb], writes=[pmb])
                et, eb = ev.next()
                P.op(DVE, lambda e, et=et, pm=pm, ho=ho, nt=nt: e.tensor_tensor(out=et[0:nt, :], in0=pm[0:nt, :], in1=ho[0:nt, :],
                                                                                op=ALU.add), reads=[pmb, hob], writes=[eb])
                P.dma(POOL, hdst[t0:t0 + nt, cb * 512:(cb + 1) * 512], et[0:nt, :], reads=[eb], pwrites=[hdbuf])


def phase_final_norm(C, st, hsrc, hbuf, gbc, gbcb, out, obuf):
    P = C.P
    hring = Ring([sbuf(C, st, f"zh{i}", [128, D], F32) for i in range(2)])
    oring = Ring([sbuf(C, st, f"zo{i}", [128, D], F32) for i in range(2)])
    junk = sbuf(C, st, "zjunk", [128, D], BF16)
    jb = Buf()
    stat = Ring([sbuf(C, st, f"zst{i}", [128, 4], F32) for i in range(2)])
    mh = sbuf(C, st, "zmh", [128, 1], F32)
    mhb = Buf()
    P.op(POOL, lambda e: e.memset(mh[:], -0.5), writes=[mhb])
    for (t0, nt) in TT[1:]:
        ht, hb = hring.next()
        ot, ob = oring.next()
        s, sb_ = stat.next()
        P.dma(SP, ht[0:nt, :], hsrc[t0:t0 + nt, :], reads=[hbuf], writes=[hb])
        P.op(ACT, lambda e, ht=ht, s=s, nt=nt: e.activation(out=junk[0:nt, :], in_=ht[0:nt, :], func=AF.Square,
                                                            accum_out=s[0:nt, 0:1]), reads=[hb], writes=[jb, sb_])
        P.op(DVE, lambda e, s=s, nt=nt: e.tensor_scalar(out=s[0:nt, 1:2], in0=s[0:nt, 0:1], scalar1=1.0 / D, scalar2=EPS,
                                                        op0=ALU.mult, op1=ALU.add), reads=[sb_], writes=[sb_])
        P.op(POOL, lambda e, s=s, nt=nt: e.tensor_tensor(out=s[0:nt, 2:3], in0=s[0:nt, 1:2], in1=mh[0:nt, :], op=ALU.pow),
             reads=[sb_, mhb], writes=[sb_])
        P.op(DVE, lambda e, ht=ht, ot=ot, s=s, nt=nt: e.scalar_tensor_tensor(
            out=ot[0:nt, :], in0=ht[0:nt, :], scalar=s[0:nt, 2:3], in1=gbc[0:nt, :], op0=ALU.mult, op1=ALU.mult),
            reads=[hb, sb_, gbcb], writes=[ob])
        P.dma(POOL, out[t0 - 16:t0 - 16 + nt, :], ot[0:nt, :], reads=[ob], pwrites=[obuf])


import contextlib

SEGS = [(3, 16, 0), (22, 2048, 16)]
LDK = 0.6065306597126334


def chunks_of(seg):
    c0, ncol, t0 = seg
    if ncol == 16:
        return [(c0, 16, t0)]
    return [(c0 + 64 * i, 64, t0 + 64 * i) for i in range(ncol // 64)]


def fix_gap(C, x, xb, hist_src, flagE, fb, tmp_ring, rows):
    P = C.P
    t, tb = tmp_ring.next()
    P.dma(SP, t[0:rows, 0:3], hist_src, writes=[tb])
    P.op(DVE, lambda e: e.scalar_tensor_tensor(out=x[0:rows, 19:22], in0=x[0:rows, 16:19], scalar=flagE[0:rows, 0:1],
                                               in1=t[0:rows, 0:3], op0=ALU.mult, op1=ALU.add), reads=[xb, tb, fb], writes=[xb])


def chunk_rel(C, out, ob, src, sb_, rows):
    P = C.P
    P.op(DVE, lambda e: e.tensor_copy(out=out[0:rows, 3:19], in_=src[0:rows, 3:19]), reads=[sb_], writes=[ob])
    P.op(DVE, lambda e: e.tensor_copy(out=out[0:rows, 22:86], in_=src[0:rows, 22:86]), reads=[sb_], writes=[ob])
    o3 = out[0:rows, 86:2070].rearrange("p (c j) -> p c j", j=64)
    s3 = src[0:rows, 86:2070].rearrange("p (c j) -> p c j", j=64)
    pv = src[0:rows, 22:2006].rearrange("p (c j) -> p c j", j=64)[:, :, 63:64].broadcast_to([rows, 31, 64])
    P.op(DVE, lambda e: e.tensor_tensor(out=o3, in0=s3, in1=pv, op=ALU.subtract), reads=[sb_], writes=[ob])


def gate_prepass(C, st, pt, ptb):
    P = C.P
    r = Ring([sbuf(C, st, f"gp{i}", [128, 1536], F32) for i in range(3)])
    for (t0, nt) in TT:
        t, tb = r.next()
        P.dma(SP, t[0:nt, 0:512], pt[t0:t0 + nt, 512:1024], reads=[ptb], writes=[tb])
        P.dma(SP, t[0:nt, 512:1536], pt[t0:t0 + nt, 2048:3072], reads=[ptb], writes=[tb])
        P.op(ACT, lambda e, t=t, nt=nt: e.activation(out=t[0:nt, 0:512], in_=t[0:nt, 0:512], func=AF.Silu), reads=[tb], writes=[tb])
        P.op(ACT, lambda e, t=t, nt=nt: e.activation(out=t[0:nt, 512:1536], in_=t[0:nt, 512:1536], func=AF.Sigmoid),
             reads=[tb], writes=[tb])
        P.dma(POOL, pt[t0:t0 + nt, 512:1024], t[0:nt, 0:512], reads=[tb], writes=[ptb])
        P.dma(POOL, pt[t0:t0 + nt, 2048:3072], t[0:nt, 512:1536], reads=[tb], writes=[ptb])


def mixer_gla(C, st, pf, pfb, pt, ptb, y, yb, prm, K):
    P = C.P
    ones, onesb, ident, idb, mask_i, mib, flagE, fb = K.ones, K.onesb, K.ident, K.idb, K.mask_i, K.mib, K.flagE, K.fb
    a2 = sbuf(C, st, "ga2", [32, 256]); a2b = Buf()
    P.op(DVE, lambda e: e.memset(a2[:], 0.0), writes=[a2b])
    nab = sbuf(C, st, "gnab", [64, 4]); nabb = Buf()
    nbc = sbuf(C, st, "gnbc", [64, 128]); nbcb = Buf()
    P.dma(SP, a2[0:16, :], prm["gla_a2"], reads=[a2b], writes=[a2b])
    P.dma(SP, nab[:], prm["gla_ab"], writes=[nabb])
    P.op(DVE, lambda e: e.tensor_scalar(out=nab[:], in0=nab[:], scalar1=-1.0, scalar2=None, op0=ALU.mult), reads=[nabb], writes=[nabb])
    P.dma(SP, nbc[:], prm["gla_normbc"], writes=[nbcb])
    xa = sbuf(C, st, "gxa", [32, TP]); xab = Buf()
    P.dma(SP, xa[:], pf[18 * 128:18 * 128 + 32, :], reads=[pfb], writes=[xab])
    q = sbuf(C, st, "gq", [64, TP]); k = sbuf(C, st, "gk", [64, TP]); sp = sbuf(C, st, "gsp", [64, TP])
    spc = sbuf(C, st, "gspc", [64, TP]); e1 = sbuf(C, st, "ge1", [64, TP]); e2 = sbuf(C, st, "ge2", [64, TP])
    qb, kb_, spb, spcb, e1b, e2b = [Buf() for _ in range(6)]
    S = sbuf(C, st, "gS", [64, 128]); Sb = Buf()
    Sin = sbuf(C, st, "gSin", [64, 128]); Sinb = Buf()
    psA = Ring([psum(C, st, f"gpa{i}", [128, 512], F32) for i in range(2)])
    psB = Ring([psum(C, st, f"gpb{i}", [128, 512], F32) for i in range(6)])
    vr = Ring([sbuf(C, st, f"gv{i}", [64, 256], F32) for i in range(3)])
    ktr = Ring([sbuf(C, st, f"gkt{i}", [64, 64], F32) for i in range(2)])
    scr = Ring([sbuf(C, st, f"gsc{i}", [64, 64], F32) for i in range(2)])
    str_ = Ring([sbuf(C, st, f"gst{i}", [64, 4], F32) for i in range(2)])
    junk = sbuf(C, st, "gjunk", [64, 128], F32); jb = Buf()
    mh = sbuf(C, st, "gmh", [64, 1], F32); mhb = Buf()
    P.op(POOL, lambda e: e.memset(mh[:], -0.5), writes=[mhb])
    t1r = Ring([sbuf(C, st, f"gt1{i}", [64, 128], F32) for i in range(2)])
    yor = Ring([sbuf(C, st, f"gyo{i}", [64, 128], BF16) for i in range(2)])
    for h in range(4):
        r0 = (14 + h // 2) * 128 + (h % 2) * 64
        r1 = (16 + h // 2) * 128 + (h % 2) * 64
        P.dma(SP, q[:], pf[r0:r0 + 64, :], reads=[pfb], writes=[qb])
        P.dma(SP, k[:], pf[r1:r1 + 64, :], reads=[pfb], writes=[kb_])
        for c0 in range(3, TP, 512):
            n = min(512, TP - c0)
            pa, pab = psA.next()
            P.op(PE, lambda e, pa=pa, c0=c0, n=n, h=h: e.matmul(pa[0:64, 0:n], lhsT=a2[0:32, h * 64:(h + 1) * 64], rhs=xa[0:32, c0:c0 + n],
                                                               start=True, stop=True), reads=[a2b, xab], writes=[pab])
            P.op(ACT, lambda e, pa=pa, c0=c0, n=n, h=h: e.activation(out=sp[:, c0:c0 + n], in_=pa[0:64, 0:n], func=AF.Exp, scale=-1.0,
                                                                    bias=nab[:, h:h + 1]), reads=[pab, nabb], writes=[spb])
        P.op(ACT, lambda e: e.activation(out=sp[:, 3:TP], in_=sp[:, 3:TP], func=AF.Ln, bias=1.0), reads=[spb], writes=[spb])
        for (c0, ncol, t0) in SEGS:
            P.op(DVE, lambda e, c0=c0, ncol=ncol: e.tensor_tensor_scan(out=spc[:, c0:c0 + ncol], data0=ones[0:64, c0:c0 + ncol],
                                                                       data1=sp[:, c0:c0 + ncol], initial=0.0, op0=ALU.mult, op1=ALU.add),
                 reads=[spb, onesb], writes=[spcb])
        chunk_rel(C, sp, spb, spc, spcb, 64)
        P.op(ACT, lambda e: e.activation(out=e1[:, 3:TP], in_=sp[:, 3:TP], func=AF.Exp, scale=-1.0 / 16), reads=[spb], writes=[e1b])
        P.op(ACT, lambda e: e.activation(out=e2[:, 3:TP], in_=sp[:, 3:TP], func=AF.Exp, scale=1.0 / 16), reads=[spb], writes=[e2b])
        P.op(DVE, lambda e: e.scalar_tensor_tensor(out=q[:, 3:TP], in0=q[:, 3:TP], scalar=0.125, in1=e1[:, 3:TP], op0=ALU.mult,
                                                   op1=ALU.mult), reads=[qb, e1b], writes=[qb])
        P.op(DVE, lambda e: e.tensor_tensor(out=k[:, 3:TP], in0=k[:, 3:TP], in1=e2[:, 3:TP], op=ALU.mult), reads=[kb_, e2b], writes=[kb_])
        P.op(DVE, lambda e: e.memset(S[:], 0.0), writes=[Sb])
        for si, seg in enumerate(SEGS):
            if si == 1:
                P.dma(SP, Sin[:], prm["sB_in"][h], writes=[Sinb])
                P.op(DVE, lambda e: e.scalar_tensor_tensor(out=S[:], in0=S[:], scalar=flagE[0:64, 0:1], in1=Sin[:], op0=ALU.mult,
                                                           op1=ALU.add), reads=[Sb, Sinb, fb], writes=[Sb])
            for (c0, n, t0) in chunks_of(seg):
                vt, vb = vr.next()
                P.dma(SP, vt[0:n, 0:128], pt[t0:t0 + n, h * 128:(h + 1) * 128], reads=[ptb], writes=[vb])
                P.dma(SP, vt[0:n, 128:256], pt[t0:t0 + n, 512 + h * 128:512 + (h + 1) * 128], reads=[ptb], writes=[vb])
                pb, pbb = psB.next()
                P.op(PE, lambda e, pb=pb, c0=c0, n=n: e.transpose(out=pb[0:n, 0:64], in_=k[:, c0:c0 + n], identity=ident[0:64, 0:64]),
                     reads=[kb_, idb], writes=[pbb])
                P.op(PE, lambda e, pb=pb, c0=c0, n=n: e.matmul(pb[0:n, 64:64 + n], lhsT=k[:, c0:c0 + n], rhs=q[:, c0:c0 + n], start=True, stop=True),
                     reads=[kb_, qb], writes=[pbb])
                kt, ktb = ktr.next()
                sc, scb = scr.next()
                P.op(ACT, lambda e, kt=kt, pb=pb, n=n: e.activation(out=kt[0:n, :], in_=pb[0:n, 0:64], func=AF.Copy), reads=[pbb], writes=[ktb])
                P.op(DVE, lambda e, sc=sc, pb=pb, n=n: e.tensor_tensor(out=sc[0:n, 0:n], in0=pb[0:n, 64:64 + n], in1=mask_i[0:n, 0:n], op=ALU.mult),
                     reads=[pbb, mib], writes=[scb])
                po, pob = psB.next()
                P.op(PE, lambda e, po=po, c0=c0, n=n: e.matmul(po[0:n, 0:128], lhsT=q[:, c0:c0 + n], rhs=S[:, :], start=True, stop=False),
                     reads=[qb, Sb], writes=[pob])
                P.op(PE, lambda e, po=po, sc=sc, vt=vt, n=n: e.matmul(po[0:n, 0:128], lhsT=sc[0:n, 0:n], rhs=vt[0:n, 0:128], start=False, stop=True),
                     reads=[scb, vb], writes=[pob])
                pc, pcb = psB.next()
                P.op(PE, lambda e, pc=pc: e.matmul(pc[0:64, 0:128], lhsT=ident[0:64, 0:64], rhs=S[:, :], start=True, stop=False),
                     reads=[idb, Sb], writes=[pcb])
                P.op(PE, lambda e, pc=pc, kt=kt, vt=vt, n=n: e.matmul(pc[0:64, 0:128], lhsT=kt[0:n, 0:64], rhs=vt[0:n, 0:128], start=False, stop=True),
                     reads=[ktb, vb], writes=[pcb])
                ce = c0 + n - 1
                P.op(DVE, lambda e, pc=pc, ce=ce: e.tensor_scalar(out=S[:], in0=pc[0:64, 0:128], scalar1=e1[:, ce:ce + 1], scalar2=None, op0=ALU.mult),
                     reads=[pcb, e1b], writes=[Sb])
                if C.emit_out and not False:
                    s_, sb2 = str_.next()
                    t1, t1b = t1r.next()
                    yo, yob = yor.next()
                    P.op(ACT, lambda e, t1=t1, po=po, n=n: e.activation(out=t1[0:n, :], in_=po[0:n, 0:128], func=AF.Copy), reads=[pob], writes=[t1b])
                    P.op(DVE, lambda e, t1=t1, n=n: e.tensor_tensor(out=junk[0:n, :], in0=t1[0:n, :], in1=t1[0:n, :], op=ALU.mult), reads=[t1b], writes=[jb])
                    P.op(DVE, lambda e, s_=s_, n=n: e.tensor_reduce(out=s_[0:n, 0:1], in_=junk[0:n, :], axis=AX.X, op=ALU.add), reads=[jb], writes=[sb2])
                    P.op(DVE, lambda e, s_=s_, n=n: e.tensor_scalar(out=s_[0:n, 1:2], in0=s_[0:n, 0:1], scalar1=1.0 / 128, scalar2=EPS, op0=ALU.mult,
                                                                   op1=ALU.add), reads=[sb2], writes=[sb2])
                    P.op(POOL, lambda e, s_=s_, n=n: e.tensor_tensor(out=s_[0:n, 2:3], in0=s_[0:n, 1:2], in1=mh[0:n, :], op=ALU.pow),
                         reads=[sb2, mhb], writes=[sb2])
                    P.op(DVE, lambda e, t1=t1, s_=s_, n=n: e.scalar_tensor_tensor(out=t1[0:n, :], in0=t1[0:n, :], scalar=s_[0:n, 2:3],
                                                                               in1=nbc[0:n, :], op0=ALU.mult, op1=ALU.mult),
                         reads=[sb2, nbcb, t1b], writes=[t1b])
                    P.op(DVE, lambda e, yo=yo, t1=t1, vt=vt, n=n: e.tensor_tensor(out=yo[0:n, :], in0=t1[0:n, :], in1=vt[0:n, 128:256], op=ALU.mult),
                         reads=[t1b, vb], writes=[yob])
                    P.dma(SP, y[t0:t0 + n, 512 + h * 128:512 + (h + 1) * 128], yo[0:n, :], reads=[yob], pwrites=[yb])
        P.dma(POOL, prm["sB_out"][h], S[:], reads=[Sb], pwrites=[K.sob])


def mixer_mlstm(C, st, pf, pfb, pt, ptb, y, yb, prm, K):
    P = C.P
    ones, onesb, ident, idb, mask_i, mib, flagE, fb = K.ones, K.onesb, K.ident, K.idb, K.mask_i, K.mib, K.flagE, K.fb
    cw = sbuf(C, st, "mcw", [128, 8, 4]); cb = sbuf(C, st, "mcb", [128, 8]); cwb = Buf()
    ib = sbuf(C, st, "mib", [4, 2]); fbb = sbuf(C, st, "mfb", [4, 2]); gbb = Buf()
    nbc = sbuf(C, st, "mnbc", [64, 1024]); nbcb = Buf()
    oh = sbuf(C, st, "moh", [4, 4, 128]); ohb = Buf()
    P.dma(SP, cw[:], prm["ml_cw"], writes=[cwb]); P.dma(SP, cb[:], prm["ml_cb"], writes=[cwb])
    P.dma(SP, ib[:, 0:1], prm["ml_ib"], writes=[gbb]); P.dma(SP, fbb[:, 0:1], prm["ml_fb"], writes=[gbb])
    P.dma(SP, nbc[:], prm["ml_normbc"], writes=[nbcb]); P.dma(SP, oh[:], prm["onehot"], writes=[ohb])
    P.op(DVE, lambda e: e.tensor_scalar(out=ib[:, 1:2], in0=ib[:, 0:1], scalar1=1.0 / 15, scalar2=None, op0=ALU.mult), reads=[gbb], writes=[gbb])
    P.op(DVE, lambda e: e.tensor_scalar(out=fbb[:, 1:2], in0=fbb[:, 0:1], scalar1=1.0 / 15, scalar2=None, op0=ALU.mult), reads=[gbb], writes=[gbb])
    gi = sbuf(C, st, "mgi", [4, TP]); gf = sbuf(C, st, "mgf", [4, TP]); SPc = sbuf(C, st, "mSP", [4, TP]); av = sbuf(C, st, "mav", [4, TP])
    MU = sbuf(C, st, "mMU", [4, TP]); MUS = sbuf(C, st, "mMUS", [4, TP])
    G8 = sbuf(C, st, "mG8", [36, TP]); FE = sbuf(C, st, "mFE", [4, 64]); min_ = sbuf(C, st, "mmin", [4, 4])
    gib, gfb_, SPb, avb, MUb, MUSb, G8b, FEb, minb = [Buf() for _ in range(9)]
    P.dma(SP, gi[:], pf[27 * 128:27 * 128 + 4, :], reads=[pfb], writes=[gib])
    P.dma(SP, gf[:], pf[27 * 128 + 4:27 * 128 + 8, :], reads=[pfb], writes=[gfb_])
    P.op(DVE, lambda e: e.memset(G8[:], 0.0), writes=[G8b])
    P.op(ACT, lambda e: e.activation(out=gi[:], in_=gi[:], func=AF.Tanh, scale=1.0 / 15, bias=ib[:, 1:2]), reads=[gib, gbb], writes=[gib])
    P.op(ACT, lambda e: e.activation(out=gf[:], in_=gf[:], func=AF.Tanh, scale=1.0 / 15, bias=fbb[:, 1:2]), reads=[gfb_, gbb], writes=[gfb_])
    P.op(ACT, lambda e: e.activation(out=gf[:], in_=gf[:], func=AF.Exp, scale=-15.0), reads=[gfb_], writes=[gfb_])
    P.op(ACT, lambda e: e.activation(out=gf[:], in_=gf[:], func=AF.Ln, bias=1.0), reads=[gfb_], writes=[gfb_])
    P.dma(SP, min_[:, 0:1], prm["mC_in"], writes=[minb])
    for si, (c0, ncol, t0) in enumerate(SEGS):
        P.op(DVE, lambda e, c0=c0, ncol=ncol: e.tensor_tensor_scan(out=SPc[:, c0:c0 + ncol], data0=ones[0:4, c0:c0 + ncol], data1=gf[:, c0:c0 + ncol],
                                                                   initial=0.0, op0=ALU.mult, op1=ALU.add), reads=[gfb_, onesb], writes=[SPb])
        P.op(DVE, lambda e, c0=c0, ncol=ncol: e.scalar_tensor_tensor(out=av[:, c0:c0 + ncol], in0=gi[:, c0:c0 + ncol], scalar=15.0,
                                                                     in1=SPc[:, c0:c0 + ncol], op0=ALU.mult, op1=ALU.add), reads=[gib, SPb], writes=[avb])
        if si == 0:
            P.op(DVE, lambda e, c0=c0, ncol=ncol: e.tensor_tensor_scan(out=MU[:, c0:c0 + ncol], data0=av[:, c0:c0 + ncol], data1=av[:, c0:c0 + ncol],
                                                                       initial=0.0, op0=ALU.max, op1=ALU.max), reads=[avb], writes=[MUb])
            P.op(DVE, lambda e: e.memset(MUS[:, 3:19], 0.0), writes=[MUSb])
            P.op(DVE, lambda e: e.tensor_tensor(out=min_[:, 1:2], in0=MU[:, 18:19], in1=SPc[:, 18:19], op=ALU.subtract), reads=[MUb, SPb, minb], writes=[minb])
            P.op(DVE, lambda e: e.scalar_tensor_tensor(out=min_[:, 2:3], in0=min_[:, 1:2], scalar=flagE[0:4, 0:1], in1=min_[:, 0:1], op0=ALU.mult,
                                                       op1=ALU.add), reads=[minb, fb], writes=[minb])
        else:
            P.op(DVE, lambda e, c0=c0, ncol=ncol: e.tensor_tensor_scan(out=MU[:, c0:c0 + ncol], data0=av[:, c0:c0 + ncol], data1=av[:, c0:c0 + ncol],
                                                                       initial=min_[:, 2:3], op0=ALU.max, op1=ALU.max), reads=[avb, minb], writes=[MUb])
            P.op(DVE, lambda e: e.tensor_copy(out=MUS[:, 22:86], in_=min_[:, 2:3].broadcast_to([4, 64])), reads=[minb], writes=[MUSb])
            P.op(DVE, lambda e: e.tensor_copy(out=MUS[:, 86:2070].rearrange("p (c j) -> p c j", j=64),
                                              in_=MU[:, 22:2006].rearrange("p (c j) -> p c j", j=64)[:, :, 63:64].broadcast_to([4, 31, 64])),
                 reads=[MUb], writes=[MUSb])
    P.op(DVE, lambda e: e.tensor_tensor(out=av[:, 3:TP], in0=av[:, 3:TP], in1=MUS[:, 3:TP], op=ALU.subtract), reads=[avb, MUSb], writes=[avb])
    P.op(ACT, lambda e: e.activation(out=G8[0:4, 3:TP], in_=av[:, 3:TP], func=AF.Exp), reads=[avb, G8b], writes=[G8b])
    P.op(DVE, lambda e: e.tensor_tensor(out=av[:, 3:TP], in0=SPc[:, 3:TP], in1=MUS[:, 3:TP], op=ALU.subtract), reads=[SPb, MUSb, G8b], writes=[avb])
    P.op(ACT, lambda e: e.activation(out=G8[32:36, 3:TP], in_=av[:, 3:TP], func=AF.Exp), reads=[avb, G8b], writes=[G8b])
    P.op(DVE, lambda e: e.tensor_tensor(out=FE[:, 0:1], in0=MUS[:, 3:4], in1=MU[:, 18:19], op=ALU.subtract), reads=[MUSb, MUb], writes=[FEb])
    P.op(DVE, lambda e: e.tensor_tensor(out=FE[:, 1:33], in0=MUS[:, 22:2070].rearrange("p (c j) -> p c j", j=64)[:, :, 0],
                                        in1=MU[:, 22:2070].rearrange("p (c j) -> p c j", j=64)[:, :, 63], op=ALU.subtract),
         reads=[MUSb, MUb, FEb], writes=[FEb])
    P.op(ACT, lambda e: e.activation(out=FE[:, 0:33], in_=FE[:, 0:33], func=AF.Exp), reads=[FEb], writes=[FEb])
    P.op(DVE, lambda e: e.tensor_tensor(out=min_[:, 3:4], in0=MU[:, TP - 1:TP], in1=SPc[:, TP - 1:TP], op=ALU.subtract), reads=[MUb, SPb, minb], writes=[minb])
    P.dma(POOL, prm["mC_out"], min_[:, 3:4], reads=[minb], pwrites=[K.sob])
    xq = sbuf(C, st, "mxq", [128, TP]); xk = sbuf(C, st, "mxk", [128, TP]); q = sbuf(C, st, "mq", [128, TP]); k = sbuf(C, st, "mk", [128, TP])
    xqb, xkb, qb, kb_ = [Buf() for _ in range(4)]
    tmpr = Ring([sbuf(C, st, f"mtmp{i}", [128, 4], F32) for i in range(2)])
    CX = sbuf(C, st, "mCX", [128, 257]); CXb = Buf()
    CXin = sbuf(C, st, "mCXin", [128, 257]); CXinb = Buf()
    FB = sbuf(C, st, "mFB", [128, 64]); FBb = Buf()
    psA = Ring([psum(C, st, f"mpa{i}", [128, 512], F32) for i in range(4)])
    psG = Ring([psum(C, st, f"mpg{i}", [128, 512], F32) for i in range(2)])
    vr = Ring([sbuf(C, st, f"mv{i}", [64, 512], F32) for i in range(3)])
    vxr = Ring([sbuf(C, st, f"mvx{i}", [64, 257], F32) for i in range(2)])
    ktr = Ring([sbuf(C, st, f"mkt{i}", [64, 128], F32) for i in range(2)])
    scr = Ring([sbuf(C, st, f"msc{i}", [64, 64], F32) for i in range(2)])
    gtr = Ring([sbuf(C, st, f"mgt{i}", [64, 36], F32) for i in range(2)])
    str_ = Ring([sbuf(C, st, f"mst{i}", [64, 8], F32) for i in range(2)])
    junk = sbuf(C, st, "mjunk", [64, 256], F32); jb = Buf()
    mh = sbuf(C, st, "mmh", [64, 1], F32); mhb = Buf()
    P.op(POOL, lambda e: e.memset(mh[:], -0.5), writes=[mhb])
    t1r = Ring([sbuf(C, st, f"mt1{i}", [64, 256], F32) for i in range(2)])
    yor = Ring([sbuf(C, st, f"myo{i}", [64, 256], BF16) for i in range(2)])
    for h in range(4):
        P.dma(SP, xq[:], pf[(19 + h) * 128:(20 + h) * 128, :], reads=[pfb], writes=[xqb])
        P.dma(SP, xk[:], pf[(23 + h) * 128:(24 + h) * 128, :], reads=[pfb], writes=[xkb])
        P.op(DVE, lambda e: e.memset(xq[:, 0:3], 0.0), reads=[xqb], writes=[xqb])
        P.op(DVE, lambda e: e.memset(xk[:, 0:3], 0.0), reads=[xkb], writes=[xkb])
        fix_gap(C, xq, xqb, prm["hist_in"][(19 + h) * 128:(20 + h) * 128, :], flagE, fb, tmpr, 128)
        fix_gap(C, xk, xkb, prm["hist_in"][(23 + h) * 128:(24 + h) * 128, :], flagE, fb, tmpr, 128)
        for (x, xb, o, ob, j) in ((xq, xqb, q, qb, h), (xk, xkb, k, kb_, 4 + h)):
            P.op(DVE, lambda e, x=x, o=o, j=j: e.tensor_scalar(out=o[:, 3:TP], in0=x[:, 0:TP - 3], scalar1=cw[:, j, 0:1], scalar2=cb[:, j:j + 1],
                                                               op0=ALU.mult, op1=ALU.add), reads=[xb, cwb], writes=[ob])
            for tap in range(1, 4):
                P.op(DVE, lambda e, x=x, o=o, j=j, tap=tap: e.scalar_tensor_tensor(out=o[:, 3:TP], in0=x[:, tap:TP - 3 + tap], scalar=cw[:, j, tap:tap + 1],
                                                                                 in1=o[:, 3:TP], op0=ALU.mult, op1=ALU.add), reads=[xb, cwb, ob], writes=[ob])
            P.op(ACT, lambda e, o=o: e.activation(out=o[:, 3:TP], in_=o[:, 3:TP], func=AF.Silu), reads=[ob], writes=[ob])
        P.op(DVE, lambda e: e.tensor_scalar(out=k[:, 3:TP], in0=k[:, 3:TP], scalar1=128 ** -0.5, scalar2=None, op0=ALU.mult), reads=[kb_], writes=[kb_])
        pg, pgb = psG.next()
        P.op(PE, lambda e, pg=pg, h=h: e.matmul(pg[:, 0:33], lhsT=oh[0:4, h, :], rhs=FE[0:4, 0:33], start=True, stop=True), reads=[ohb, FEb], writes=[pgb])
        P.op(ACT, lambda e, pg=pg: e.activation(out=FB[:, 0:33], in_=pg[:, 0:33], func=AF.Copy), reads=[pgb], writes=[FBb])
        P.op(DVE, lambda e: e.memset(CX[:], 0.0), writes=[CXb])
        ci = 0
        for si, seg in enumerate(SEGS):
            if si == 1:
                P.dma(SP, CXin[:], prm["sC_in"][h], writes=[CXinb])
                P.op(DVE, lambda e: e.scalar_tensor_tensor(out=CX[:], in0=CX[:], scalar=flagE[:, 0:1], in1=CXin[:], op0=ALU.mult, op1=ALU.add),
                     reads=[CXb, CXinb, fb], writes=[CXb])
            for (c0, n, t0) in chunks_of(seg):
                vt, vb = vr.next()
                P.dma(SP, vt[0:n, 0:256], pt[t0:t0 + n, 1024 + h * 256:1024 + (h + 1) * 256], reads=[ptb], writes=[vb])
                P.dma(SP, vt[0:n, 256:512], pt[t0:t0 + n, 2048 + h * 256:2048 + (h + 1) * 256], reads=[ptb], writes=[vb])
                pa, pab = psA.next()
                P.op(PE, lambda e, pa=pa, c0=c0, n=n: e.transpose(out=pa[0:n, 0:128], in_=k[:, c0:c0 + n], identity=ident[:, :]), reads=[kb_, idb], writes=[pab])
                P.op(PE, lambda e, pa=pa, c0=c0, n=n: e.matmul(pa[0:n, 128:128 + n], lhsT=k[:, c0:c0 + n], rhs=q[:, c0:c0 + n], start=True, stop=True),
                     reads=[kb_, qb], writes=[pab])
                P.op(PE, lambda e, pa=pa, c0=c0, n=n: e.transpose(out=pa[0:n, 192:228], in_=G8[0:36, c0:c0 + n], identity=ident[0:36, 0:36]),
                     reads=[G8b, idb], writes=[pab])
                kt, ktb = ktr.next(); sc, scb = scr.next(); gt, gtb = gtr.next()
                P.op(ACT, lambda e, kt=kt, pa=pa, n=n: e.activation(out=kt[0:n, :], in_=pa[0:n, 0:128], func=AF.Copy), reads=[pab], writes=[ktb])
                P.op(DVE, lambda e, sc=sc, pa=pa, n=n: e.tensor_tensor(out=sc[0:n, 0:n], in0=pa[0:n, 128:128 + n], in1=mask_i[0:n, 0:n], op=ALU.mult),
                     reads=[pab, mib], writes=[scb])
                P.op(ACT, lambda e, gt=gt, pa=pa, n=n: e.activation(out=gt[0:n, :], in_=pa[0:n, 192:228], func=AF.Copy), reads=[pab], writes=[gtb])
                vx, vxb = vxr.next()
                P.op(DVE, lambda e, vx=vx, vt=vt, gt=gt, n=n, h=h: e.tensor_scalar(out=vx[0:n, 0:256], in0=vt[0:n, 0:256], scalar1=gt[0:n, h:h + 1], scalar2=None,
                                                                                 op0=ALU.mult), reads=[vb, gtb], writes=[vxb])
                P.op(ACT, lambda e, vx=vx, gt=gt, n=n, h=h: e.activation(out=vx[0:n, 256:257], in_=gt[0:n, h:h + 1], func=AF.Copy), reads=[gtb, vxb], writes=[vxb])
                pn, pnb = psA.next()
                P.op(PE, lambda e, pn=pn, c0=c0, n=n: e.matmul(pn[0:n, 0:257], lhsT=q[:, c0:c0 + n], rhs=CX[:, :], start=True, stop=False), reads=[qb, CXb], writes=[pnb])
                P.op(PE, lambda e, pn=pn, sc=sc, vx=vx, n=n: e.matmul(pn[0:n, 0:257], lhsT=sc[0:n, 0:n], rhs=vx[0:n, :], start=False, stop=True),
                     reads=[scb, vxb], writes=[pnb])
                pc, pcb = psA.next()
                P.op(PE, lambda e, pc=pc: e.matmul(pc[:, 0:257], lhsT=ident[:, :], rhs=CX[:, :], start=True, stop=False), reads=[idb, CXb], writes=[pcb])
                P.op(PE, lambda e, pc=pc, kt=kt, vx=vx, n=n: e.matmul(pc[:, 0:257], lhsT=kt[0:n, :], rhs=vx[0:n, :], start=False, stop=True),
                     reads=[ktb, vxb], writes=[pcb])
                P.op(DVE, lambda e, pc=pc, ci=ci: e.tensor_scalar(out=CX[:], in0=pc[:, 0:257], scalar1=FB[:, ci:ci + 1], scalar2=None, op0=ALU.mult),
                     reads=[pcb, FBb], writes=[CXb])
                if C.emit_out:
                    s_, sb2 = str_.next()
                    P.op(ACT, lambda e, s_=s_, pn=pn, n=n: e.activation(out=s_[0:n, 0:1], in_=pn[0:n, 256:257], func=AF.Abs),
                         reads=[pnb], writes=[sb2])
                    P.op(DVE, lambda e, s_=s_, gt=gt, n=n, h=h: e.tensor_tensor(out=s_[0:n, 0:1], in0=s_[0:n, 0:1], in1=gt[0:n, 32 + h:33 + h], op=ALU.max),
                         reads=[sb2, gtb], writes=[sb2])
                    P.op(DVE, lambda e, s_=s_, n=n: e.reciprocal(out=s_[0:n, 1:2], in_=s_[0:n, 0:1]), reads=[sb2], writes=[sb2])
                    P.op(ACT, lambda e, s_=s_, pn=pn, n=n: e.activation(out=junk[0:n, :], in_=pn[0:n, 0:256], func=AF.Square, scale=s_[0:n, 1:2],
                                                                       accum_out=s_[0:n, 2:3]), reads=[pnb, sb2], writes=[jb, sb2])
                    P.op(DVE, lambda e, s_=s_, n=n: e.tensor_scalar(out=s_[0:n, 3:4], in0=s_[0:n, 2:3], scalar1=1.0 / 256, scalar2=EPS, op0=ALU.mult,
                                                                   op1=ALU.add), reads=[sb2], writes=[sb2])
                    P.op(POOL, lambda e, s_=s_, n=n: e.tensor_tensor(out=s_[0:n, 4:5], in0=s_[0:n, 3:4], in1=mh[0:n, :], op=ALU.pow), reads=[sb2, mhb], writes=[sb2])
                    P.op(DVE, lambda e, s_=s_, n=n: e.tensor_tensor(out=s_[0:n, 5:6], in0=s_[0:n, 4:5], in1=s_[0:n, 1:2], op=ALU.mult), reads=[sb2], writes=[sb2])
                    t1, t1b = t1r.next(); yo, yob = yor.next()
                    P.op(DVE, lambda e, t1=t1, pn=pn, s_=s_, n=n, h=h: e.scalar_tensor_tensor(out=t1[0:n, :], in0=pn[0:n, 0:256], scalar=s_[0:n, 5:6],
                                                                                          in1=nbc[0:n, h * 256:(h + 1) * 256], op0=ALU.mult, op1=ALU.mult),
                         reads=[pnb, sb2, nbcb], writes=[t1b])
                    P.op(DVE, lambda e, yo=yo, t1=t1, vt=vt, n=n: e.tensor_tensor(out=yo[0:n, :], in0=t1[0:n, :], in1=vt[0:n, 256:512], op=ALU.mult),
                         reads=[t1b, vb], writes=[yob])
                    P.dma(POOL, y[t0:t0 + n, 1024 + h * 256:1024 + (h + 1) * 256], yo[0:n, :], reads=[yob], pwrites=[yb])
                ci += 1
        P.dma(POOL, prm["sC_out"][h], CX[:], reads=[CXb], pwrites=[K.sob])


def mixer_rwkv(C, st, pf, pfb, y, yb, prm, K):
    P = C.P
    ones, onesb, ident, idb, flagE, fb = K.ones, K.onesb, K.ident, K.idb, K.flagE, K.fb
    mask5, m5b = K.mask5, K.m5b
    muA = sbuf(C, st, "amuA", [64, 3, 8]); muL = sbuf(C, st, "amuL", [96, 3]); w2 = sbuf(C, st, "aw2", [32, 512]); a2 = sbuf(C, st, "aa2", [32, 512])
    g2 = sbuf(C, st, "ag2", [96, 512]); ch = sbuf(C, st, "ach", [64, 5, 8]); rk = sbuf(C, st, "ark", [64, 8, 2])
    lnw = sbuf(C, st, "alnw", [64, 512]); lnb = sbuf(C, st, "alnb", [64, 512])
    pb_ = Buf()
    for t, n_ in ((muA, "rw_muA"), (muL, "rw_muL"), (w2, "rw_w2"), (a2, "rw_a2"), (g2, "rw_g2"), (rk, "rw_rk"), (lnw, "rw_lnw_bc"), (lnb, "rw_lnb_bc")):
        P.dma(SP, t[:], prm[n_], pwrites=[pb_])
    P.dma(SP, ch[:, 0:4, :], prm["rw_ch"], pwrites=[pb_])
    P.op(DVE, lambda e: e.tensor_scalar(out=ch[:, 4, :], in0=ch[:, 3, :], scalar1=-1.0, scalar2=1.0, op0=ALU.mult, op1=ALU.add), reads=[pb_], writes=[pb_])
    mh = sbuf(C, st, "amh", [64, TP], F32); mhb = Buf()
    P.op(POOL, lambda e: e.memset(mh[:], -0.5), writes=[mhb])
    tmpr = Ring([sbuf(C, st, f"atmp{i}", [128, 4], F32) for i in range(2)])
    raw = sbuf(C, st, "araw", [96, TP]); rawb = Buf()
    thw = sbuf(C, st, "athw", [32, TP]); xal = sbuf(C, st, "axal", [32, TP]); sg = sbuf(C, st, "asg", [96, TP])
    thwb, xalb, sgb = Buf(), Buf(), Buf()
    for (dst, dstb, r0, nr, mcol, fn) in ((thw, thwb, 12 * 128, 32, 0, AF.Tanh), (xal, xalb, 12 * 128 + 32, 32, 1, None), (sg, sgb, 13 * 128, 96, 2, AF.Sigmoid)):
        P.dma(SP, raw[0:nr, :], pf[r0:r0 + nr, :], reads=[pfb], writes=[rawb])
        P.op(DVE, lambda e, nr=nr: e.memset(raw[0:nr, 0:3], 0.0), reads=[rawb], writes=[rawb])
        fix_gap(C, raw, rawb, prm["hist_in"][r0:r0 + nr, :], flagE, fb, tmpr, nr)
        P.op(DVE, lambda e, dst=dst, nr=nr: e.tensor_tensor(out=dst[0:nr, 3:TP], in0=raw[0:nr, 2:TP - 1], in1=raw[0:nr, 3:TP], op=ALU.subtract),
             reads=[rawb], writes=[dstb])
        P.op(DVE, lambda e, dst=dst, nr=nr, mcol=mcol: e.scalar_tensor_tensor(out=dst[0:nr, 3:TP], in0=dst[0:nr, 3:TP], scalar=muL[0:nr, mcol:mcol + 1],
                                                                            in1=raw[0:nr, 3:TP], op0=ALU.mult, op1=ALU.add), reads=[rawb, dstb, pb_], writes=[dstb])
        if fn is not None:
            P.op(ACT, lambda e, dst=dst, nr=nr, fn=fn: e.activation(out=dst[0:nr, 3:TP], in_=dst[0:nr, 3:TP], func=fn), reads=[dstb], writes=[dstb])
    A = [sbuf(C, st, f"aA{i}", [64, TP]) for i in range(10)]
    Ab = [Buf() for _ in range(10)]
    H = sbuf(C, st, "aH", [64, 64]); Hb = Buf()
    Hin = sbuf(C, st, "aHin", [64, 64]); Hinb = Buf()
    G = 8
    MM = sbuf(C, st, "aMM", [64, G, 5, 64]); MMb = Buf()
    TM = sbuf(C, st, "aTM", [64, G, 3, 64]); TMb = Buf()
    NN = [sbuf(C, st, f"aNN{i}", [64, G, 2, 64]) for i in range(2)]; NNb = [Buf(), Buf()]
    Pm = sbuf(C, st, "aPm", [64, G, 64]); Pmb = Buf()
    GB = sbuf(C, st, "aGB", [64, G, 66]); GBb = Buf()
    ps1 = Ring([psum(C, st, f"ap1{i}", [128, 512], F32) for i in range(3)])
    pygr = Ring([psum(C, st, f"apy{i}", [128, 512], F32) for i in range(2)])
    ps2 = Ring([psum(C, st, f"ap2{i}", [128, 512], F32) for i in range(3)])
    w0r = Ring([sbuf(C, st, f"aw0{i}", [64, 64], F32) for i in range(2)])
    ur = Ring([sbuf(C, st, f"au{i}", [64, 64], F32) for i in range(2)])
    str_ = Ring([sbuf(C, st, f"ast{i}", [64, 8], F32) for i in range(2)])
    junk = sbuf(C, st, "ajunk", [64, 64], F32); jb = Buf()
    T1 = sbuf(C, st, "aT1", [64, G, 64]); T1b = Buf()
    SQ = sbuf(C, st, "aSQ", [64, G, 64]); SQb = Buf()
    YO = sbuf(C, st, "aYO", [64, G, 64], BF16); YOb = Buf()
    ST = sbuf(C, st, "aST", [64, 6, G]); STb = Buf()
    for h in range(8):
        rows = [(sg_ * 4 + h // 2) * 128 + (h % 2) * 64 for sg_ in range(3)]
        for i in range(3):
            P.dma(SP, A[i][:], pf[rows[i]:rows[i] + 64, :], reads=[pfb], writes=[Ab[i]])
            P.op(DVE, lambda e, i=i: e.memset(A[i][:, 0:3], 0.0), reads=[Ab[i]], writes=[Ab[i]])
            fix_gap(C, A[i], Ab[i], prm["hist_in"][rows[i]:rows[i] + 64, :], flagE, fb, tmpr, 64)
            P.op(DVE, lambda e, i=i: e.tensor_tensor(out=A[3 + i][:, 3:TP], in0=A[i][:, 2:TP - 1], in1=A[i][:, 3:TP], op=ALU.subtract),
                 reads=[Ab[i]], writes=[Ab[3 + i]])
            P.op(DVE, lambda e, i=i, h=h: e.scalar_tensor_tensor(out=A[3 + i][:, 3:TP], in0=A[3 + i][:, 3:TP], scalar=muA[:, i, h:h + 1], in1=A[i][:, 3:TP],
                                                                op0=ALU.mult, op1=ALU.add), reads=[Ab[i], Ab[3 + i], pb_], writes=[Ab[3 + i]])
        xr, xk, xv = A[3], A[4], A[5]
        for c0 in range(3, TP, 512):
            n = min(512, TP - c0)
            p_, p_b = ps1.next()
            P.op(PE, lambda e, p_=p_, c0=c0, n=n, h=h: e.matmul(p_[0:64, 0:n], lhsT=w2[0:32, h * 64:(h + 1) * 64], rhs=thw[0:32, c0:c0 + n], start=True, stop=True),
                 reads=[pb_, thwb], writes=[p_b])
            P.op(ACT, lambda e, p_=p_, c0=c0, n=n, h=h: e.activation(out=A[0][:, c0:c0 + n], in_=p_[0:64, 0:n], func=AF.Sigmoid, bias=ch[:, 0, h:h + 1]),
                 reads=[p_b, pb_], writes=[Ab[0]])
            p_, p_b = ps1.next()
            P.op(PE, lambda e, p_=p_, c0=c0, n=n, h=h: e.matmul(p_[0:64, 0:n], lhsT=a2[0:32, h * 64:(h + 1) * 64], rhs=xal[0:32, c0:c0 + n], start=True, stop=True),
                 reads=[pb_, xalb], writes=[p_b])
            P.op(ACT, lambda e, p_=p_, c0=c0, n=n, h=h: e.activation(out=A[1][:, c0:c0 + n], in_=p_[0:64, 0:n], func=AF.Sigmoid, bias=ch[:, 1, h:h + 1]),
                 reads=[p_b, pb_], writes=[Ab[1]])
        P.op(DVE, lambda e, h=h: e.tensor_scalar(out=A[2][:, 3:TP], in0=xk[:, 3:TP], scalar1=ch[:, 2, h:h + 1], scalar2=None, op0=ALU.mult),
             reads=[Ab[4], pb_], writes=[Ab[2]])
        P.op(DVE, lambda e: e.tensor_tensor(out=A[6][:, 3:TP], in0=A[2][:, 3:TP], in1=A[2][:, 3:TP], op=ALU.mult), reads=[Ab[2]], writes=[Ab[6]])
        for c0 in range(3, TP, 512):
            n = min(512, TP - c0)
            p_, p_b = ps1.next()
            P.op(PE, lambda e, p_=p_, c0=c0, n=n: e.matmul(p_[0:64, 0:n], lhsT=ones[0:64, 0:64], rhs=A[6][:, c0:c0 + n], start=True, stop=True),
                 reads=[onesb, Ab[6]], writes=[p_b])
            P.op(DVE, lambda e, p_=p_, c0=c0, n=n: e.tensor_scalar(out=A[8][:, c0:c0 + n], in0=p_[0:64, 0:n], scalar1=1e-24, scalar2=None, op0=ALU.max),
                 reads=[p_b], writes=[Ab[8]])
        P.op(POOL, lambda e: e.tensor_tensor(out=A[8][:, 3:TP], in0=A[8][:, 3:TP], in1=mh[:, 3:TP], op=ALU.pow), reads=[Ab[8], mhb], writes=[Ab[8]])
        P.op(DVE, lambda e: e.tensor_tensor(out=A[2][:, 3:TP], in0=A[2][:, 3:TP], in1=A[8][:, 3:TP], op=ALU.mult), reads=[Ab[2], Ab[8]], writes=[Ab[2]])
        P.op(DVE, lambda e, h=h: e.tensor_scalar(out=A[6][:, 3:TP], in0=A[1][:, 3:TP], scalar1=ch[:, 3, h:h + 1], scalar2=ch[:, 4, h:h + 1], op0=ALU.mult,
                                                op1=ALU.add), reads=[Ab[1], pb_], writes=[Ab[6]])
        P.op(DVE, lambda e: e.tensor_tensor(out=xk[:, 3:TP], in0=xk[:, 3:TP], in1=A[6][:, 3:TP], op=ALU.mult), reads=[Ab[4], Ab[6]], writes=[Ab[4]])
        P.op(DVE, lambda e: e.tensor_tensor(out=A[1][:, 3:TP], in0=A[1][:, 3:TP], in1=A[2][:, 3:TP], op=ALU.mult), reads=[Ab[1], Ab[2]], writes=[Ab[1]])
        P.op(DVE, lambda e: e.tensor_tensor(out=A[6][:, 3:TP], in0=xr[:, 3:TP], in1=xk[:, 3:TP], op=ALU.mult), reads=[Ab[3], Ab[4]], writes=[Ab[6]])
        for (c0, ncol, t0) in SEGS:
            P.op(DVE, lambda e, c0=c0, ncol=ncol: e.tensor_tensor_scan(out=A[7][:, c0:c0 + ncol], data0=ones[0:64, c0:c0 + ncol], data1=A[0][:, c0:c0 + ncol],
                                                                       initial=0.0, op0=ALU.mult, op1=ALU.add), reads=[Ab[0], onesb], writes=[Ab[7]])
        P.op(DVE, lambda e: e.memset(A[8][:, 19:22], 0.0), reads=[Ab[8]], writes=[Ab[8]])
        chunk_rel(C, A[8], Ab[8], A[7], Ab[7], 64)
        P.op(ACT, lambda e: e.activation(out=A[7][:, 3:TP], in_=A[8][:, 3:TP], func=AF.Exp, scale=-LDK), reads=[Ab[8]], writes=[Ab[7]])
        P.op(ACT, lambda e: e.activation(out=A[9][:, 3:TP], in_=A[8][:, 3:TP], func=AF.Exp, scale=LDK), reads=[Ab[8]], writes=[Ab[9]])
        P.op(DVE, lambda e: e.tensor_tensor(out=A[8][:, 3:TP], in0=A[8][:, 3:TP], in1=A[0][:, 3:TP], op=ALU.subtract), reads=[Ab[8], Ab[0]], writes=[Ab[8]])
        P.op(ACT, lambda e: e.activation(out=A[8][:, 3:TP], in_=A[8][:, 3:TP], func=AF.Exp, scale=-LDK), reads=[Ab[8]], writes=[Ab[8]])
        P.op(DVE, lambda e: e.tensor_tensor(out=xr[:, 3:TP], in0=xr[:, 3:TP], in1=A[7][:, 3:TP], op=ALU.mult), reads=[Ab[3], Ab[7]], writes=[Ab[3]])
        P.op(DVE, lambda e: e.tensor_tensor(out=xk[:, 3:TP], in0=xk[:, 3:TP], in1=A[9][:, 3:TP], op=ALU.mult), reads=[Ab[4], Ab[9]], writes=[Ab[4]])
        P.op(DVE, lambda e: e.tensor_tensor(out=A[1][:, 3:TP], in0=A[1][:, 3:TP], in1=A[9][:, 3:TP], op=ALU.mult), reads=[Ab[1], Ab[9]], writes=[Ab[1]])
        P.op(DVE, lambda e: e.scalar_tensor_tensor(out=A[2][:, 3:TP], in0=A[2][:, 3:TP], scalar=-1.0, in1=A[8][:, 3:TP], op0=ALU.mult, op1=ALU.mult),
             reads=[Ab[2], Ab[8]], writes=[Ab[2]])
        rt, kt_, bt, at, prod, G1 = A[3], A[4], A[1], A[2], A[6], A[7]
        rtb, ktb_, btb, atb, prodb, G1b = Ab[3], Ab[4], Ab[1], Ab[2], Ab[6], Ab[7]
        xvb = Ab[5]
        P.op(DVE, lambda e: e.memset(H[:], 0.0), writes=[Hb])
        for si, seg in enumerate(SEGS):
            if si == 1:
                P.dma(SP, Hin[:], prm["sA_in"][h], writes=[Hinb])
                P.op(DVE, lambda e: e.scalar_tensor_tensor(out=H[:], in0=H[:], scalar=flagE[0:64, 0:1], in1=Hin[:], op0=ALU.mult, op1=ALU.add),
                     reads=[Hb, Hinb, fb], writes=[Hb])
            chs_all = chunks_of(seg)
            for g0 in range(0, len(chs_all), G):
                chs = chs_all[g0:g0 + G]
                ng = len(chs)
                n = chs[0][1]
                nlev = 5 if n == 64 else 3
                for g, (c0, n, t0) in enumerate(chs):
                    p_, p_b = ps1.next()
                    for j, (src, srcb) in enumerate(((xv, xvb), (kt_, ktb_), (bt, btb))):
                        P.op(PE, lambda e, p_=p_, src=src, c0=c0, n=n, j=j: e.transpose(out=p_[0:n, j * 64:(j + 1) * 64], in_=src[:, c0:c0 + n], identity=ident[0:64, 0:64]),
                             reads=[srcb, idb], writes=[p_b])
                    P.op(ACT, lambda e, p_=p_, g=g, n=n: e.activation(out=TM[0:n, g, :, :], in_=p_[0:n, 0:192].rearrange("p (j d) -> p j d", j=3), func=AF.Copy),
                         reads=[p_b], pwrites=[TMb])
                    q_, q_b = ps1.next()
                    pairs = ((kt_, ktb_, at, atb), (kt_, ktb_, rt, rtb), (bt, btb, at, atb), (bt, btb, rt, rtb), (at, atb, bt, btb))
                    for j, (l, lb, r, rb) in enumerate(pairs):
                        P.op(PE, lambda e, q_=q_, l=l, r=r, c0=c0, n=n, j=j: e.matmul(q_[0:n, j * 64:j * 64 + n], lhsT=l[:, c0:c0 + n], rhs=r[:, c0:c0 + n], start=True, stop=True),
                             reads=[lb, rb], writes=[q_b])
                    P.op(DVE, lambda e, q_=q_, g=g, n=n: e.tensor_tensor(out=MM[0:n, g, :, 0:n], in0=q_[0:n, 0:320].rearrange("p (j d) -> p j d", j=5)[:, :, 0:n],
                                                                       in1=mask5[0:n, :, 0:n], op=ALU.mult), reads=[q_b, m5b], pwrites=[MMb])
                P.op(DVE, lambda e, ng=ng, n=n: e.tensor_tensor(out=Pm[0:n, 0:ng, 0:n], in0=MM[0:n, 0:ng, 2, 0:n],
                                                                in1=ident[0:n, 0:n].unsqueeze(1).broadcast_to([n, ng, n]), op=ALU.add),
                     reads=[MMb, idb], writes=[Pmb])
                curN = lambda g, n=n: MM[0:n, g, 2, 0:n]
                curNT = lambda g, n=n: MM[0:n, g, 4, 0:n]
                curb = MMb
                for lev in range(nlev):
                    nn, nnb = NN[lev % 2], NNb[lev % 2]
                    for g4 in range(0, ng, 4):
                        m4 = min(4, ng - g4)
                        p2, p2b = ps2.next()
                        for g in range(g4, g4 + m4):
                            gg = g - g4
                            P.op(PE, lambda e, p2=p2, gg=gg, n=n, a_=curNT(g), b_=curN(g): e.matmul(p2[0:n, gg * 128:gg * 128 + n], lhsT=a_, rhs=b_, start=True, stop=True),
                                 reads=[curb], writes=[p2b])
                            P.op(PE, lambda e, p2=p2, gg=gg, n=n, a_=curN(g), b_=curNT(g): e.matmul(p2[0:n, gg * 128 + 64:gg * 128 + 64 + n], lhsT=a_, rhs=b_, start=True, stop=True),
                                 reads=[curb], writes=[p2b])
                        P.op(ACT, lambda e, p2=p2, nn=nn, g4=g4, m4=m4, n=n: e.activation(out=nn[0:n, g4:g4 + m4, :, 0:n],
                                                                                 in_=p2[0:n, 0:m4 * 128].rearrange("p (g j d) -> p g j d", g=m4, j=2)[:, :, :, 0:n], func=AF.Copy),
                             reads=[p2b], pwrites=[nnb])
                    curN = lambda g, nn=nn, n=n: nn[0:n, g, 0, 0:n]
                    curNT = lambda g, nn=nn, n=n: nn[0:n, g, 1, 0:n]
                    curb = nnb
                    p1, p1b = ps1.next()
                    for g in range(ng):
                        P.op(PE, lambda e, p1=p1, g=g, n=n, a_=curNT(g): e.matmul(p1[0:n, g * 64:g * 64 + n], lhsT=a_, rhs=Pm[0:n, g, 0:n], start=True, stop=True),
                             reads=[curb, Pmb], writes=[p1b])
                    P.op(DVE, lambda e, p1=p1, ng=ng, n=n: e.tensor_tensor(out=Pm[0:n, 0:ng, 0:n], in0=Pm[0:n, 0:ng, 0:n],
                                                                         in1=p1[0:n, 0:ng * 64].rearrange("p (g d) -> p g d", g=ng)[:, :, 0:n], op=ALU.add),
                         reads=[p1b, Pmb], writes=[Pmb])
                if C.emit_out:
                    for g, (c0, n, t0) in enumerate(chs):
                        p_, p_b = ps1.next()
                        P.op(PE, lambda e, p_=p_, c0=c0, n=n, h=h: e.matmul(p_[0:n, 0:64], lhsT=sg[0:96, c0:c0 + n], rhs=g2[0:96, h * 64:(h + 1) * 64], start=True, stop=True),
                             reads=[sgb, pb_], writes=[p_b])
                        P.op(PE, lambda e, p_=p_, c0=c0, n=n, h=h: e.matmul(p_[0:n, 64:66], lhsT=prod[:, c0:c0 + n], rhs=rk[:, h, :], start=True, stop=True),
                             reads=[prodb, pb_], writes=[p_b])
                        P.op(ACT, lambda e, p_=p_, g=g, n=n: e.activation(out=GB[0:n, g, :], in_=p_[0:n, 0:66], func=AF.Copy), reads=[p_b], pwrites=[GBb])
                pyg, pygb = pygr.next()
                for g, (c0, n, t0) in enumerate(chs):
                    vtm = TM[0:n, g, 0, :]; ktm = TM[0:n, g, 1, :]; btm = TM[0:n, g, 2, :]
                    LakT = MM[0:n, g, 0, 0:n]; MrkT = MM[0:n, g, 1, 0:n]; MrbT = MM[0:n, g, 3, 0:n]
                    TT_ = Pm[0:n, g, 0:n]
                    pw, pwb = ps1.next()
                    P.op(PE, lambda e, pw=pw, c0=c0, n=n: e.matmul(pw[0:n, 0:64], lhsT=at[:, c0:c0 + n], rhs=H[:, :], start=True, stop=False), reads=[atb, Hb], writes=[pwb])
                    P.op(PE, lambda e, pw=pw, n=n, LakT=LakT, vtm=vtm: e.matmul(pw[0:n, 0:64], lhsT=LakT, rhs=vtm, start=False, stop=True), reads=[MMb, TMb], writes=[pwb])
                    w0, w0b = w0r.next()
                    P.op(ACT, lambda e, w0=w0, pw=pw, n=n: e.activation(out=w0[0:n, :], in_=pw[0:n, 0:64], func=AF.Copy), reads=[pwb], writes=[w0b])
                    P.op(PE, lambda e, pw=pw, n=n, TT_=TT_, w0=w0: e.matmul(pw[0:n, 64:128], lhsT=TT_, rhs=w0[0:n, :], start=True, stop=True), reads=[Pmb, w0b], writes=[pwb])
                    u, ub_ = ur.next()
                    P.op(DVE, lambda e, u=u, pw=pw, n=n: e.tensor_copy(out=u[0:n, :], in_=pw[0:n, 64:128]), reads=[pwb], writes=[ub_])
                    if C.emit_out:
                        P.op(PE, lambda e, pyg=pyg, g=g, c0=c0, n=n: e.matmul(pyg[0:n, g * 64:(g + 1) * 64], lhsT=rt[:, c0:c0 + n], rhs=H[:, :], start=True, stop=False), reads=[rtb, Hb], writes=[pygb])
                        P.op(PE, lambda e, pyg=pyg, g=g, n=n, MrbT=MrbT, u=u: e.matmul(pyg[0:n, g * 64:(g + 1) * 64], lhsT=MrbT, rhs=u[0:n, :], start=False, stop=False), reads=[MMb, ub_], writes=[pygb])
                        P.op(PE, lambda e, pyg=pyg, g=g, n=n, MrkT=MrkT, vtm=vtm: e.matmul(pyg[0:n, g * 64:(g + 1) * 64], lhsT=MrkT, rhs=vtm, start=False, stop=True), reads=[MMb, TMb], writes=[pygb])
                    ph, phb = ps1.next()
                    P.op(PE, lambda e, ph=ph: e.matmul(ph[0:64, 0:64], lhsT=ident[0:64, 0:64], rhs=H[:, :], start=True, stop=False), reads=[idb, Hb], writes=[phb])
                    P.op(PE, lambda e, ph=ph, n=n, btm=btm, u=u: e.matmul(ph[0:64, 0:64], lhsT=btm, rhs=u[0:n, :], start=False, stop=False), reads=[TMb, ub_], writes=[phb])
                    P.op(PE, lambda e, ph=ph, n=n, ktm=ktm, vtm=vtm: e.matmul(ph[0:64, 0:64], lhsT=ktm, rhs=vtm, start=False, stop=True), reads=[TMb], writes=[phb])
                    ce = c0 + n - 1
                    P.op(DVE, lambda e, ph=ph, ce=ce: e.tensor_scalar(out=H[:], in0=ph[0:64, 0:64], scalar1=G1[:, ce:ce + 1], scalar2=None, op0=ALU.mult),
                         reads=[phb, G1b], writes=[Hb])
                if C.emit_out:
                    t0g = chs[0][2]
                    YG = pyg[0:n, 0:ng * 64].rearrange("p (g d) -> p g d", g=ng)
                    bc = lambda ap, n=n, ng=ng: ap.unsqueeze(2).broadcast_to([n, ng, 64])
                    P.op(DVE, lambda e, YG=YG, n=n, ng=ng: e.tensor_reduce(out=ST[0:n, 0, 0:ng], in_=YG, axis=AX.X, op=ALU.add), reads=[pygb], writes=[STb])
                    P.op(ACT, lambda e, YG=YG, n=n, ng=ng: e.activation(out=SQ[0:n, 0:ng, :], in_=YG, func=AF.Square), reads=[pygb], writes=[SQb])
                    P.op(DVE, lambda e, n=n, ng=ng: e.tensor_reduce(out=ST[0:n, 1, 0:ng], in_=SQ[0:n, 0:ng, :], axis=AX.X, op=ALU.add), reads=[SQb, STb], writes=[STb])
                    P.op(DVE, lambda e, n=n, ng=ng: e.tensor_scalar(out=ST[0:n, 2, 0:ng], in0=ST[0:n, 0, 0:ng], scalar1=1.0 / 64, scalar2=None, op0=ALU.mult), reads=[STb], writes=[STb])
                    P.op(DVE, lambda e, n=n, ng=ng: e.tensor_tensor(out=ST[0:n, 3, 0:ng], in0=ST[0:n, 2, 0:ng], in1=ST[0:n, 2, 0:ng], op=ALU.mult), reads=[STb], writes=[STb])
                    P.op(DVE, lambda e, n=n, ng=ng: e.tensor_scalar(out=ST[0:n, 4, 0:ng], in0=ST[0:n, 1, 0:ng], scalar1=1.0 / 64, scalar2=64e-5, op0=ALU.mult, op1=ALU.add),
                         reads=[STb], writes=[STb])
                    P.op(DVE, lambda e, n=n, ng=ng: e.tensor_tensor(out=ST[0:n, 4, 0:ng], in0=ST[0:n, 4, 0:ng], in1=ST[0:n, 3, 0:ng], op=ALU.subtract), reads=[STb], writes=[STb])
                    P.op(POOL, lambda e, n=n, ng=ng: e.tensor_tensor(out=ST[0:n, 5, 0:ng], in0=ST[0:n, 4, 0:ng], in1=mh[0:n, 0:ng], op=ALU.pow), reads=[STb, mhb], writes=[STb])
                    P.op(DVE, lambda e, YG=YG, n=n, ng=ng, bc=bc: e.tensor_tensor(out=T1[0:n, 0:ng, :], in0=YG, in1=bc(ST[0:n, 2, 0:ng]), op=ALU.subtract),
                         reads=[pygb, STb], writes=[T1b])
                    P.op(DVE, lambda e, n=n, ng=ng, bc=bc: e.tensor_tensor(out=T1[0:n, 0:ng, :], in0=T1[0:n, 0:ng, :], in1=bc(ST[0:n, 5, 0:ng]), op=ALU.mult),
                         reads=[T1b, STb], writes=[T1b])
                    P.op(DVE, lambda e, n=n, ng=ng, h=h: e.tensor_tensor(out=T1[0:n, 0:ng, :], in0=T1[0:n, 0:ng, :],
                                                                      in1=lnw[0:n, h * 64:(h + 1) * 64].unsqueeze(1).broadcast_to([n, ng, 64]), op=ALU.mult),
                         reads=[T1b, pb_], writes=[T1b])
                    P.op(DVE, lambda e, n=n, ng=ng, h=h: e.tensor_tensor(out=T1[0:n, 0:ng, :], in0=T1[0:n, 0:ng, :],
                                                                      in1=lnb[0:n, h * 64:(h + 1) * 64].unsqueeze(1).broadcast_to([n, ng, 64]), op=ALU.add),
                         reads=[T1b, pb_], writes=[T1b])
                    P.op(DVE, lambda e, n=n, ng=ng: e.tensor_tensor(out=SQ[0:n, 0:ng, :], in0=TM[0:n, 0:ng, 0, :], in1=GB[0:n, 0:ng, 64:65].broadcast_to([n, ng, 64]), op=ALU.mult),
                         reads=[TMb, GBb, SQb], writes=[SQb])
                    P.op(DVE, lambda e, n=n, ng=ng: e.tensor_tensor(out=T1[0:n, 0:ng, :], in0=T1[0:n, 0:ng, :], in1=SQ[0:n, 0:ng, :], op=ALU.add), reads=[T1b, SQb], writes=[T1b])
                    P.op(DVE, lambda e, n=n, ng=ng: e.tensor_tensor(out=YO[0:n, 0:ng, :], in0=T1[0:n, 0:ng, :], in1=GB[0:n, 0:ng, 0:64], op=ALU.mult),
                         reads=[T1b, GBb], writes=[YOb])
                    P.dma(SP, y[t0g:t0g + ng * n, h * 64:(h + 1) * 64].rearrange("(g p) d -> p g d", p=n), YO[0:n, 0:ng, :], reads=[YOb], pwrites=[yb])
        P.dma(POOL, prm["sA_out"][h], H[:], reads=[Hb], pwrites=[K.sob])


def mixer_rwkv2(C, st, pf, pfb, y, yb, prm, K):
    P = C.P
    ones, onesb, ident, idb, flagE, fb = K.ones, K.onesb, K.ident, K.idb, K.flagE, K.fb
    mask5, m5b = K.mask5, K.m5b
    muA = sbuf(C, st, "bmuA", [64, 3, 8]); muL = sbuf(C, st, "bmuL", [96, 3]); w2 = sbuf(C, st, "bw2", [32, 512]); a2 = sbuf(C, st, "ba2", [32, 512])
    g2 = sbuf(C, st, "bg2", [96, 512]); ch = sbuf(C, st, "bch", [64, 5, 8]); rk = sbuf(C, st, "brk", [64, 8, 2])
    lnw = sbuf(C, st, "blnw", [64, 512]); lnb = sbuf(C, st, "blnb", [64, 512])
    pb_ = Buf()
    for t, n_ in ((muA, "rw_muA"), (muL, "rw_muL"), (w2, "rw_w2"), (a2, "rw_a2"), (g2, "rw_g2"), (rk, "rw_rk"), (lnw, "rw_lnw_bc"), (lnb, "rw_lnb_bc")):
        P.dma(SP, t[:], prm[n_], pwrites=[pb_])
    P.dma(SP, ch[:, 0:4, :], prm["rw_ch"], pwrites=[pb_])
    P.op(DVE, lambda e: e.tensor_scalar(out=ch[:, 4, :], in0=ch[:, 3, :], scalar1=-1.0, scalar2=1.0, op0=ALU.mult, op1=ALU.add), reads=[pb_], writes=[pb_])
    WM = 128
    W1M = WM + 1
    mh = sbuf(C, st, "bmh", [64, 8 * WM], F32); mhb = Buf()
    P.op(POOL, lambda e: e.memset(mh[:], -0.5), writes=[mhb])
    names = ["pr", "pk", "pv", "xr", "xk", "xv", "sgz", "asig", "kkn", "t1", "rel", "G1", "G2"]
    X = {nm: sbuf(C, st, "bX" + nm, [64, 8, W1M]) for nm in names}
    Xb = {nm: Buf() for nm in names}
    Ssc = sbuf(C, st, "bSsc", [64, 1 + 8 * WM]); Sscb = Buf()
    P.op(DVE, lambda e: e.memset(Ssc[:, 0:1], 0.0), writes=[Sscb])
    lraw = sbuf(C, st, "blraw", [96, 3, W1M]); lrawb = Buf()
    thw = sbuf(C, st, "bthw", [32, WM]); xal = sbuf(C, st, "bxal", [32, WM]); sg = sbuf(C, st, "bsg", [96, WM])
    thwb, xalb, sgb = Buf(), Buf(), Buf()
    hs = sbuf(C, st, "bhs", [96, 2, 8]); hsb = Buf()
    H = sbuf(C, st, "bH", [64, 8, 64]); Hb = Buf()
    Hin = sbuf(C, st, "bHin", [64, 8, 64]); Hinb = Buf()
    NQ = 16
    TM = sbuf(C, st, "bTM", [64, 3, NQ, 64]); TMb = Buf()
    MM = sbuf(C, st, "bMM", [64, 5, NQ, 64]); MMb = Buf()
    NN = [sbuf(C, st, f"bNN{i}", [64, NQ, 2, 64]) for i in range(2)]; NNb = [Buf(), Buf()]
    Pm = sbuf(C, st, "bPm", [64, NQ, 64]); Pmb = Buf()
    W0s = sbuf(C, st, "bW0", [64, 8, 64]); W0b = Buf()
    Us = sbuf(C, st, "bUs", [64, 8, 64]); Usb = Buf()
    GBs = sbuf(C, st, "bGB", [64, 528]); GBb = Buf()
    T1 = sbuf(C, st, "bT1", [64, 8, 64]); T1b = Buf()
    SQ = sbuf(C, st, "bSQ", [64, 8, 64]); SQb = Buf()
    YO = sbuf(C, st, "bYO", [64, 8, 64], BF16); YOb = Buf()
    ST = sbuf(C, st, "bST", [64, 6, 8]); STb = Buf()
    bank = [psum(C, st, f"bpb{i}", [128, 512], F32) for i in range(8)]
    bkb = [Buf() for _ in range(8)]
    P.op(DVE, lambda e: e.memset(H[:], 0.0), writes=[Hb])

    def bc8(ap, W):
        return ap.unsqueeze(2).broadcast_to([64, 8, W])

    def vop(fn, reads, writes, pwrites=()):
        P.op(DVE, fn, reads=reads, writes=writes, pwrites=pwrites)

    Fl = sbuf(C, st, "bFl", [64, 8 * WM]); Flb = Buf()

    scs = [(3, 16, 0, 16)] + [(22 + 128 * i, 128, 16 + 128 * i, 64) for i in range(16)]
    def do_sc(sci, c0, W, t0, n):
        W1 = W + 1
        nch = W // n
        nq = 8 * nch
        cur = lambda nm: X[nm][:, :, 1:W1]
        prev = lambda nm: X[nm][:, :, 0:W]
        P.phase = "rwkv_pre"
        for i, nm in enumerate(("pr", "pk", "pv")):
            P.dma(SP, X[nm][:, :, 0:W1], pf[i * 512:(i + 1) * 512, c0 - 1:c0 + W].rearrange("(h d) c -> d h c", d=64), reads=[pfb], writes=[Xb[nm]])
        for j, (r0, nr) in enumerate(((12 * 128, 32), (12 * 128 + 32, 32), (13 * 128, 96))):
            P.dma(SP, lraw[0:nr, j, 0:W1], pf[r0:r0 + nr, c0 - 1:c0 + W], reads=[pfb], writes=[lrawb])
        if sci == 0:
            for nm in ("pr", "pk", "pv"):
                vop(lambda e, nm=nm: e.memset(X[nm][:, :, 0:1], 0.0), [Xb[nm]], [Xb[nm]])
            vop(lambda e: e.memset(lraw[:, :, 0:1], 0.0), [lrawb], [lrawb])
        if sci == 1:
            for i, nm in enumerate(("pr", "pk", "pv")):
                P.dma(SP, hs[0:64, 0, :], pf[i * 512:(i + 1) * 512, 18:19].rearrange("(h d) c -> d (h c)", d=64), reads=[pfb], writes=[hsb], allow_slow_non_contiguous=True)
                P.dma(SP, hs[0:64, 1, :], prm["hist_in"][i * 512:(i + 1) * 512, 2:3].rearrange("(h d) c -> d (h c)", d=64), reads=[hsb], writes=[hsb], allow_slow_non_contiguous=True)
                vop(lambda e, nm=nm: e.scalar_tensor_tensor(out=X[nm][:, :, 0:1], in0=hs[0:64, 0, :].unsqueeze(2), scalar=flagE[0:64, 0:1], in1=hs[0:64, 1, :].unsqueeze(2),
                                                            op0=ALU.mult, op1=ALU.add), [hsb, fb, Xb[nm]], [Xb[nm]])
            for j, (r0, nr) in enumerate(((12 * 128, 32), (12 * 128 + 32, 32), (13 * 128, 96))):
                P.dma(SP, hs[0:nr, 0, 0:1], pf[r0:r0 + nr, 18:19], reads=[pfb, hsb], writes=[hsb], allow_slow_non_contiguous=True)
                P.dma(SP, hs[0:nr, 1, 0:1], prm["hist_in"][r0:r0 + nr, 2:3], reads=[hsb], writes=[hsb], allow_slow_non_contiguous=True)
                vop(lambda e, j=j, nr=nr: e.scalar_tensor_tensor(out=lraw[0:nr, j, 0:1], in0=hs[0:nr, 0, 0:1], scalar=flagE[0:nr, 0:1], in1=hs[0:nr, 1, 0:1],
                                                                 op0=ALU.mult, op1=ALU.add), [hsb, fb, lrawb], [lrawb])
            P.dma(SP, Hin[:], prm["sA_in"].rearrange("h k v -> k h v"), writes=[Hinb])
            vop(lambda e: e.scalar_tensor_tensor(out=H[:], in0=H[:], scalar=flagE[0:64, 0:1], in1=Hin[:], op0=ALU.mult, op1=ALU.add), [Hb, Hinb, fb], [Hb])
        for i, (src, dst) in enumerate((("pr", "xr"), ("pk", "xk"), ("pv", "xv"))):
            vop(lambda e, src=src, dst=dst: e.tensor_tensor(out=cur(dst), in0=prev(src), in1=cur(src), op=ALU.subtract), [Xb[src]], [Xb[dst]])
            vop(lambda e, dst=dst, i=i: e.tensor_tensor(out=cur(dst), in0=cur(dst), in1=bc8(muA[:, i, :], W), op=ALU.mult), [Xb[dst], pb_], [Xb[dst]])
            vop(lambda e, src=src, dst=dst: e.tensor_tensor(out=cur(dst), in0=cur(dst), in1=cur(src), op=ALU.add), [Xb[dst], Xb[src]], [Xb[dst]])
        for j, (dst, dstb, nr, fn) in enumerate(((thw, thwb, 32, AF.Tanh), (xal, xalb, 32, None), (sg, sgb, 96, AF.Sigmoid))):
            vop(lambda e, dst=dst, nr=nr, j=j: e.tensor_tensor(out=dst[0:nr, 0:W], in0=lraw[0:nr, j, 0:W], in1=lraw[0:nr, j, 1:W1], op=ALU.subtract), [lrawb], [dstb])
            vop(lambda e, dst=dst, nr=nr, j=j: e.scalar_tensor_tensor(out=dst[0:nr, 0:W], in0=dst[0:nr, 0:W], scalar=muL[0:nr, j:j + 1], in1=lraw[0:nr, j, 1:W1],
                                                                    op0=ALU.mult, op1=ALU.add), [lrawb, dstb, pb_], [dstb])
            if fn is not None:
                P.op(ACT, lambda e, dst=dst, nr=nr, fn=fn: e.activation(out=dst[0:nr, 0:W], in_=dst[0:nr, 0:W], func=fn), reads=[dstb], writes=[dstb])
        for (wt_, src, srcb, dst, chi, b0) in ((w2, thw, thwb, "sgz", 0, 0), (a2, xal, xalb, "asig", 1, 2)):
            for h in range(8):
                bk = b0 + (h * W) // 512
                off = (h * W) % 512
                P.op(PE, lambda e, bk=bk, off=off, wt_=wt_, src=src, h=h: e.matmul(bank[bk][0:64, off:off + W], lhsT=wt_[0:32, h * 64:(h + 1) * 64], rhs=src[0:32, 0:W],
                                                                                  start=True, stop=True), reads=[pb_, srcb], writes=[bkb[bk]])
            nb = (8 * W + 511) // 512
            for b in range(nb):
                h0 = b * (512 // W) if W >= 64 else 0
                nh = (512 // W) if W >= 64 else 8
                vop(lambda e, b=b, b0=b0, dst=dst, chi=chi, h0=h0, nh=nh: e.tensor_tensor(
                    out=X[dst][:, h0:h0 + nh, 1:W1], in0=bank[b0 + b][0:64, 0:nh * W].rearrange("p (h w) -> p h w", h=nh),
                    in1=ch[:, chi, h0:h0 + nh].unsqueeze(2).broadcast_to([64, nh, W]), op=ALU.add), [bkb[b0 + b], pb_], [Xb[dst]])
            P.op(ACT, lambda e, dst=dst: e.activation(out=cur(dst), in_=cur(dst), func=AF.Sigmoid), reads=[Xb[dst]], writes=[Xb[dst]])
        vop(lambda e: e.tensor_tensor(out=cur("kkn"), in0=cur("xk"), in1=bc8(ch[:, 2, :], W), op=ALU.mult), [Xb["xk"], pb_], [Xb["kkn"]])
        vop(lambda e: e.tensor_tensor(out=Fl[:, 0:8 * W].rearrange("p (h w) -> p h w", h=8), in0=cur("kkn"), in1=cur("kkn"), op=ALU.mult), [Xb["kkn"]], [Flb])
        nb = (8 * W + 511) // 512
        for b in range(nb):
            nn_ = min(512, 8 * W - b * 512)
            P.op(PE, lambda e, b=b, nn_=nn_: e.matmul(bank[4 + b][0:64, 0:nn_], lhsT=ones[0:64, 0:64], rhs=Fl[:, b * 512:b * 512 + nn_], start=True, stop=True),
                 reads=[onesb, Flb], writes=[bkb[4 + b]])
        for b in range(nb):
            nn_ = min(512, 8 * W - b * 512)
            vop(lambda e, b=b, nn_=nn_: e.tensor_scalar(out=Fl[:, b * 512:b * 512 + nn_], in0=bank[4 + b][0:64, 0:nn_],
                                                        scalar1=1e-24, scalar2=None, op0=ALU.max), [bkb[4 + b], Flb], [Flb])
        relf = Fl[:, 0:8 * W]
        P.op(POOL, lambda e, relf=relf: e.tensor_tensor(out=relf, in0=relf, in1=mh[:, 0:8 * W], op=ALU.pow), reads=[Flb, mhb], writes=[Flb])
        vop(lambda e, relf=relf: e.tensor_tensor(out=cur("kkn"), in0=cur("kkn"), in1=relf.rearrange("p (h w) -> p h w", h=8), op=ALU.mult),
            [Xb["kkn"], Flb], [Xb["kkn"]])
        vop(lambda e: e.tensor_tensor(out=cur("t1"), in0=cur("asig"), in1=bc8(ch[:, 3, :], W), op=ALU.mult), [Xb["asig"], pb_], [Xb["t1"]])
        vop(lambda e: e.tensor_tensor(out=cur("t1"), in0=cur("t1"), in1=bc8(ch[:, 4, :], W), op=ALU.add), [Xb["t1"], pb_], [Xb["t1"]])
        vop(lambda e: e.tensor_tensor(out=cur("xk"), in0=cur("xk"), in1=cur("t1"), op=ALU.mult), [Xb["xk"], Xb["t1"]], [Xb["xk"]])
        vop(lambda e: e.tensor_tensor(out=cur("asig"), in0=cur("asig"), in1=cur("kkn"), op=ALU.mult), [Xb["asig"], Xb["kkn"]], [Xb["asig"]])
        vop(lambda e: e.tensor_tensor(out=cur("t1"), in0=cur("xr"), in1=cur("xk"), op=ALU.mult), [Xb["xr"], Xb["xk"], Xb["t1"]], [Xb["t1"]])
        vop(lambda e: e.tensor_tensor(out=cur("pr"), in0=cur("t1"), in1=bc8(rk[:, :, 0], W), op=ALU.mult), [Xb["t1"], pb_, Xb["pr"], Xb["xr"]], [Xb["pr"]])
        vop(lambda e: e.tensor_copy(out=Fl[:, 0:8 * W].rearrange("p (h w) -> p h w", h=8), in_=cur("sgz")), [Xb["sgz"], Flb], [Flb])
        vop(lambda e: e.tensor_tensor_scan(out=Ssc[:, 1:1 + 8 * W], data0=ones[0:64, 0:8 * W], data1=Fl[:, 0:8 * W], initial=0.0, op0=ALU.mult, op1=ALU.add),
            [Flb, Sscb, onesb], [Sscb])
        vop(lambda e: e.tensor_tensor(out=cur("rel").rearrange("p h (c j) -> p h c j", j=n),
                                      in0=Ssc[:, 1:1 + 8 * W].rearrange("p (h c j) -> p h c j", h=8, j=n),
                                      in1=Ssc[:, 0:8 * W].rearrange("p (h c j) -> p h c j", h=8, j=n)[:, :, :, 0:1].broadcast_to([64, 8, nch, n]), op=ALU.subtract),
            [Sscb, Xb["rel"]], [Xb["rel"]])
        P.op(ACT, lambda e: e.activation(out=cur("G1"), in_=cur("rel"), func=AF.Exp, scale=-LDK), reads=[Xb["rel"]], writes=[Xb["G1"]])
        P.op(ACT, lambda e: e.activation(out=cur("G2"), in_=cur("rel"), func=AF.Exp, scale=LDK), reads=[Xb["rel"]], writes=[Xb["G2"]])
        vop(lambda e: e.tensor_tensor(out=cur("rel"), in0=cur("rel"), in1=cur("sgz"), op=ALU.subtract), [Xb["rel"], Xb["sgz"]], [Xb["rel"]])
        P.op(ACT, lambda e: e.activation(out=cur("rel"), in_=cur("rel"), func=AF.Exp, scale=-LDK), reads=[Xb["rel"]], writes=[Xb["rel"]])
        vop(lambda e: e.tensor_tensor(out=cur("xr"), in0=cur("xr"), in1=cur("G1"), op=ALU.mult), [Xb["xr"], Xb["G1"]], [Xb["xr"]])
        vop(lambda e: e.tensor_tensor(out=cur("xk"), in0=cur("xk"), in1=cur("G2"), op=ALU.mult), [Xb["xk"], Xb["G2"]], [Xb["xk"]])
        vop(lambda e: e.tensor_tensor(out=cur("asig"), in0=cur("asig"), in1=cur("G2"), op=ALU.mult), [Xb["asig"], Xb["G2"]], [Xb["asig"]])
        vop(lambda e: e.scalar_tensor_tensor(out=cur("kkn"), in0=cur("kkn"), scalar=-1.0, in1=cur("rel"), op0=ALU.mult, op1=ALU.mult),
            [Xb["kkn"], Xb["rel"]], [Xb["kkn"]])
        RT, KT, BT, AT, XV, PRK, G1 = "xr", "xk", "asig", "kkn", "xv", "pr", "G1"
        col = lambda nm, h, c: X[nm][:, h, 1 + c * n:1 + (c + 1) * n]
        qi = lambda h, c: h * nch + c
        P.phase = "rwkv_gram"
        for a, nm in enumerate((XV, KT, BT)):
            for h in range(8):
                for c in range(nch):
                    q = qi(h, c)
                    bk, off = (q * 64) // 512, (q * 64) % 512
                    P.op(PE, lambda e, bk=bk, off=off, nm=nm, h=h, c=c: e.transpose(out=bank[bk][0:n, off:off + 64], in_=col(nm, h, c), identity=ident[0:64, 0:64]),
                         reads=[Xb[nm], idb], writes=[bkb[bk]])
            for b in range((nq * 64 + 511) // 512):
                qn = min(8, nq - b * 8)
                P.op(ACT, lambda e, a=a, b=b, qn=qn: e.activation(out=TM[0:n, a, b * 8:b * 8 + qn, :], in_=bank[b][0:n, 0:qn * 64].rearrange("p (q d) -> p q d", q=qn),
                                                               func=AF.Copy), reads=[bkb[b]], pwrites=[TMb])
        pairs = ((KT, AT), (KT, RT), (BT, AT), (BT, RT), (AT, BT))
        for j, (l_, r_) in enumerate(pairs):
            b0 = 4 if j % 2 else 0
            for h in range(8):
                for c in range(nch):
                    q = qi(h, c)
                    bk, off = b0 + (q * 64) // 512, (q * 64) % 512
                    P.op(PE, lambda e, bk=bk, off=off, l_=l_, r_=r_, h=h, c=c: e.matmul(bank[bk][0:n, off:off + n], lhsT=col(l_, h, c), rhs=col(r_, h, c), start=True, stop=True),
                         reads=[Xb[l_], Xb[r_]], writes=[bkb[bk]])
            for b in range((nq * 64 + 511) // 512):
                qn = min(8, nq - b * 8)
                vop(lambda e, j=j, b=b, b0=b0, qn=qn: e.tensor_tensor(out=MM[0:n, j, b * 8:b * 8 + qn, 0:n],
                                                                    in0=bank[b0 + b][0:n, 0:qn * 64].rearrange("p (q d) -> p q d", q=qn)[:, :, 0:n],
                                                                    in1=mask5[0:n, j, 0:n].unsqueeze(1).broadcast_to([n, qn, n]), op=ALU.mult),
                    [bkb[b0 + b], m5b], [], pwrites=[MMb])
        P.phase = "rwkv_inv"
        vop(lambda e: e.tensor_tensor(out=Pm[0:n, 0:nq, 0:n], in0=MM[0:n, 2, 0:nq, 0:n], in1=ident[0:n, 0:n].unsqueeze(1).broadcast_to([n, nq, n]), op=ALU.add),
            [MMb, idb], [Pmb])
        curN = lambda q: MM[0:n, 2, q, 0:n]
        curNT = lambda q: MM[0:n, 4, q, 0:n]
        curb = MMb
        nlev = 5 if n == 64 else 3
        for lev in range(nlev):
            nn, nnb = NN[lev % 2], NNb[lev % 2]
            for q in range(nq):
                bk, off = (q * 128) // 512, (q * 128) % 512
                P.op(PE, lambda e, bk=bk, off=off, a_=curNT(q), b_=curN(q): e.matmul(bank[bk][0:n, off:off + n], lhsT=a_, rhs=b_, start=True, stop=True), reads=[curb], writes=[bkb[bk]])
                P.op(PE, lambda e, bk=bk, off=off, a_=curN(q), b_=curNT(q): e.matmul(bank[bk][0:n, off + 64:off + 64 + n], lhsT=a_, rhs=b_, start=True, stop=True), reads=[curb], writes=[bkb[bk]])
            for b in range((nq * 128 + 511) // 512):
                qn = min(4, nq - b * 4)
                P.op(ACT, lambda e, nn=nn, b=b, qn=qn: e.activation(out=nn[0:n, b * 4:b * 4 + qn, :, 0:n],
                                                                 in_=bank[b][0:n, 0:qn * 128].rearrange("p (q j d) -> p q j d", q=qn, j=2)[:, :, :, 0:n], func=AF.Copy),
                     reads=[bkb[b]], pwrites=[nnb])
            curN = lambda q, nn=nn: nn[0:n, q, 0, 0:n]
            curNT = lambda q, nn=nn: nn[0:n, q, 1, 0:n]
            curb = nnb
            for q in range(nq):
                bk, off = 4 + (q * 64) // 512, (q * 64) % 512
                P.op(PE, lambda e, bk=bk, off=off, a_=curNT(q), q=q: e.matmul(bank[bk][0:n, off:off + n], lhsT=a_, rhs=Pm[0:n, q, 0:n], start=True, stop=True), reads=[curb, Pmb], writes=[bkb[bk]])
            for b in range((nq * 64 + 511) // 512):
                qn = min(8, nq - b * 8)
                vop(lambda e, b=b, qn=qn: e.tensor_tensor(out=Pm[0:n, b * 8:b * 8 + qn, 0:n], in0=Pm[0:n, b * 8:b * 8 + qn, 0:n],
                                                          in1=bank[4 + b][0:n, 0:qn * 64].rearrange("p (q d) -> p q d", q=qn)[:, :, 0:n], op=ALU.add),
                    [bkb[4 + b], Pmb], [Pmb])
        P.phase = "rwkv_chain"
        for c in range(nch):
            tc0 = t0 + c * n
            for h in range(8):
                q = qi(h, c)
                P.op(PE, lambda e, h=h, c=c: e.matmul(bank[0][0:n, h * 64:(h + 1) * 64], lhsT=col(AT, h, c), rhs=H[:, h, :], start=True, stop=False), reads=[Xb[AT], Hb], writes=[bkb[0]])
                P.op(PE, lambda e, h=h, q=q: e.matmul(bank[0][0:n, h * 64:(h + 1) * 64], lhsT=MM[0:n, 0, q, 0:n], rhs=TM[0:n, 0, q, :], start=False, stop=True), reads=[MMb, TMb], writes=[bkb[0]])
            P.op(ACT, lambda e: e.activation(out=W0s[0:n, :, :], in_=bank[0][0:n, 0:512].rearrange("p (h d) -> p h d", h=8), func=AF.Copy), reads=[bkb[0]], writes=[W0b])
            for h in range(8):
                q = qi(h, c)
                P.op(PE, lambda e, h=h, q=q: e.matmul(bank[1][0:n, h * 64:(h + 1) * 64], lhsT=Pm[0:n, q, 0:n], rhs=W0s[0:n, h, :], start=True, stop=True), reads=[Pmb, W0b], writes=[bkb[1]])
            vop(lambda e: e.tensor_copy(out=Us[0:n, :, :], in_=bank[1][0:n, 0:512].rearrange("p (h d) -> p h d", h=8)), [bkb[1]], [Usb])
            if C.emit_out:
                for h in range(8):
                    q = qi(h, c)
                    P.op(PE, lambda e, h=h, c=c: e.matmul(bank[2][0:n, h * 64:(h + 1) * 64], lhsT=col(RT, h, c), rhs=H[:, h, :], start=True, stop=False), reads=[Xb[RT], Hb], writes=[bkb[2]])
                    P.op(PE, lambda e, h=h, q=q: e.matmul(bank[2][0:n, h * 64:(h + 1) * 64], lhsT=MM[0:n, 3, q, 0:n], rhs=Us[0:n, h, :], start=False, stop=False), reads=[MMb, Usb], writes=[bkb[2]])
                    P.op(PE, lambda e, h=h, q=q: e.matmul(bank[2][0:n, h * 64:(h + 1) * 64], lhsT=MM[0:n, 1, q, 0:n], rhs=TM[0:n, 0, q, :], start=False, stop=True), reads=[MMb, TMb], writes=[bkb[2]])
            for h in range(8):
                q = qi(h, c)
                P.op(PE, lambda e, h=h: e.matmul(bank[3][0:64, h * 64:(h + 1) * 64], lhsT=ident[0:64, 0:64], rhs=H[:, h, :], start=True, stop=False), reads=[idb, Hb], writes=[bkb[3]])
                P.op(PE, lambda e, h=h, q=q: e.matmul(bank[3][0:64, h * 64:(h + 1) * 64], lhsT=TM[0:n, 2, q, :], rhs=Us[0:n, h, :], start=False, stop=False), reads=[TMb, Usb], writes=[bkb[3]])
                P.op(PE, lambda e, h=h, q=q: e.matmul(bank[3][0:64, h * 64:(h + 1) * 64], lhsT=TM[0:n, 1, q, :], rhs=TM[0:n, 0, q, :], start=False, stop=True), reads=[TMb], writes=[bkb[3]])
            ce = 1 + (c + 1) * n - 1
            vop(lambda e, ce=ce: e.tensor_tensor(out=H[:], in0=bank[3][0:64, 0:512].rearrange("p (h d) -> p h d", h=8),
                                                 in1=X[G1][:, :, ce:ce + 1].broadcast_to([64, 8, 64]), op=ALU.mult), [bkb[3], Xb[G1]], [Hb])
            if C.emit_out:
                P.op(PE, lambda e, c=c: e.matmul(bank[4][0:n, 0:512], lhsT=sg[0:96, c * n:(c + 1) * n], rhs=g2[0:96, :], start=True, stop=True), reads=[sgb, pb_], writes=[bkb[4]])
                for h in range(8):
                    P.op(PE, lambda e, h=h, c=c: e.matmul(bank[5][0:n, 2 * h:2 * h + 2], lhsT=col(PRK, h, c), rhs=ones[0:64, 0:2], start=True, stop=True), reads=[Xb[PRK], onesb], writes=[bkb[5]])
                P.op(ACT, lambda e: e.activation(out=GBs[0:n, 0:512], in_=bank[4][0:n, 0:512], func=AF.Copy), reads=[bkb[4]], writes=[GBb])
                P.op(ACT, lambda e: e.activation(out=GBs[0:n, 512:528], in_=bank[5][0:n, 0:16], func=AF.Copy), reads=[bkb[5], GBb], writes=[GBb])
                YG = bank[2][0:n, 0:512].rearrange("p (h d) -> p h d", h=8)
                bcn = lambda ap: ap.unsqueeze(2).broadcast_to([n, 8, 64])
                vop(lambda e, YG=YG: e.tensor_reduce(out=ST[0:n, 0, :], in_=YG, axis=AX.X, op=ALU.add), [bkb[2]], [STb])
                P.op(ACT, lambda e, YG=YG: e.activation(out=SQ[0:n, :, :], in_=YG, func=AF.Square), reads=[bkb[2]], writes=[SQb])
                vop(lambda e: e.tensor_reduce(out=ST[0:n, 1, :], in_=SQ[0:n, :, :], axis=AX.X, op=ALU.add), [SQb, STb], [STb])
                vop(lambda e: e.tensor_scalar(out=ST[0:n, 2, :], in0=ST[0:n, 0, :], scalar1=1.0 / 64, scalar2=None, op0=ALU.mult), [STb], [STb])
                vop(lambda e: e.tensor_tensor(out=ST[0:n, 3, :], in0=ST[0:n, 2, :], in1=ST[0:n, 2, :], op=ALU.mult), [STb], [STb])
                vop(lambda e: e.tensor_scalar(out=ST[0:n, 4, :], in0=ST[0:n, 1, :], scalar1=1.0 / 64, scalar2=64e-5, op0=ALU.mult, op1=ALU.add), [STb], [STb])
                vop(lambda e: e.tensor_tensor(out=ST[0:n, 4, :], in0=ST[0:n, 4, :], in1=ST[0:n, 3, :], op=ALU.subtract), [STb], [STb])
                P.op(POOL, lambda e: e.tensor_tensor(out=ST[0:n, 5, :], in0=ST[0:n, 4, :], in1=mh[0:n, 0:8], op=ALU.pow), reads=[STb, mhb], writes=[STb])
                vop(lambda e, YG=YG, bcn=bcn: e.tensor_tensor(out=T1[0:n, :, :], in0=YG, in1=bcn(ST[0:n, 2, :]), op=ALU.subtract), [bkb[2], STb], [T1b])
                vop(lambda e, bcn=bcn: e.tensor_tensor(out=T1[0:n, :, :], in0=T1[0:n, :, :], in1=bcn(ST[0:n, 5, :]), op=ALU.mult), [T1b, STb], [T1b])
                vop(lambda e: e.tensor_tensor(out=T1[0:n, :, :], in0=T1[0:n, :, :], in1=lnw[0:n, :].rearrange("p (h d) -> p h d", h=8), op=ALU.mult), [T1b, pb_], [T1b])
                vop(lambda e: e.tensor_tensor(out=T1[0:n, :, :], in0=T1[0:n, :, :], in1=lnb[0:n, :].rearrange("p (h d) -> p h d", h=8), op=ALU.add), [T1b, pb_], [T1b])
                vtm_c = TM[0:n, 0, 0:nq, :].rearrange("p (h c) d -> p h c d", c=nch)[:, :, c, :]
                bs_c = GBs[0:n, 512:528].rearrange("p (h t) -> p h t", t=2)[:, :, 0:1].broadcast_to([n, 8, 64])
                vop(lambda e, vtm_c=vtm_c, bs_c=bs_c: e.tensor_tensor(out=SQ[0:n, :, :], in0=vtm_c, in1=bs_c, op=ALU.mult), [TMb, GBb, SQb], [SQb])
                vop(lambda e: e.tensor_tensor(out=T1[0:n, :, :], in0=T1[0:n, :, :], in1=SQ[0:n, :, :], op=ALU.add), [T1b, SQb], [T1b])
                vop(lambda e: e.tensor_tensor(out=YO[0:n, :, :], in0=T1[0:n, :, :], in1=GBs[0:n, 0:512].rearrange("p (h d) -> p h d", h=8), op=ALU.mult), [T1b, GBb], [YOb])
                P.dma(SP, y[tc0:tc0 + n, 0:512], YO[0:n, :, :].rearrange("p h d -> p (h d)"), reads=[YOb], pwrites=[yb])
    for sci, (c0, W, t0, n) in enumerate(scs):
        do_sc(sci, c0, W, t0, n)
    P.dma(POOL, prm["sA_out"].rearrange("h k v -> k h v"), H[:], reads=[Hb], pwrites=[K.sob])


import contextlib
import numpy as np

PRM_SHAPES = {
    "gla_a2": [16, 256], "gla_ab": [64, 4], "gla_normbc": [64, 128],
    "ml_cw": [128, 8, 4], "ml_cb": [128, 8], "ml_ib": [4, 1], "ml_fb": [4, 1], "ml_normbc": [64, 1024], "onehot": [4, 4, 128],
    "rw_muA": [64, 3, 8], "rw_muL": [96, 3], "rw_w2": [32, 512], "rw_a2": [32, 512], "rw_g2": [96, 512], "rw_ch": [64, 4, 8],
    "rw_rk": [64, 8, 2], "rw_lnw_bc": [64, 512], "rw_lnb_bc": [64, 512],
    "sA_in": [8, 64, 64], "sB_in": [4, 64, 128], "sC_in": [4, 128, 257], "mC_in": [4, 1], "hist_in": [NFMB * 128, 3],
    "flagE": [128, 1], "mask_i": [64, 64], "mask5": [64, 5, 64],
}
OUT_SHAPES = {"sA_out": [8, 64, 64], "sB_out": [4, 64, 128], "sC_out": [4, 128, 257], "mC_out": [4, 1], "hist_out": [NFMB * 128, 3]}


def host_consts():
    j = np.arange(64)
    mi = (j[None, :] >= j[:, None]).astype(np.float32)
    ms = (j[None, :] > j[:, None]).astype(np.float32)
    ml = (j[None, :] < j[:, None]).astype(np.float32)
    mask5 = np.stack([ms, mi, ms, mi, ml], 1)
    oh = np.zeros((4, 4, 128), np.float32)
    for h in range(4):
        oh[h, h, :] = 1.0
    return {"mask_i": mi, "mask5": np.ascontiguousarray(mask5), "onehot": oh}


def host_layer_params(z, l):
    f = lambda a: np.ascontiguousarray(a, dtype=np.float32)
    chT = lambda v: f(v.reshape(8, 64).T)
    mu = z["rw_mu"][l]
    d = {}
    d["gla_a2"] = f(z["gla_a2"][l]); d["gla_ab"] = f(z["gla_ab"][l].reshape(4, 64).T)
    d["gla_normbc"] = f(np.broadcast_to(z["gla_norm"][l], (64, 128)))
    cw = z["ml_conv_w"][l]
    d["ml_cw"] = f(cw.reshape(4, 8, 128).transpose(2, 1, 0)); d["ml_cb"] = f(z["ml_conv_b"][l].reshape(8, 128).T)
    d["ml_ib"] = f(z["ml_ib"][l].reshape(4, 1)); d["ml_fb"] = f(z["ml_fb"][l].reshape(4, 1))
    d["ml_normbc"] = f(np.broadcast_to(z["ml_norm"][l], (64, 1024)))
    d["rw_muA"] = f(np.stack([chT(mu[0:512]), chT(mu[512:1024]), chT(mu[1024:1536])], 1))
    muL = np.zeros((96, 3), np.float32); muL[0:32, 0] = mu[1536:1568]; muL[0:32, 1] = mu[1568:1600]; muL[0:96, 2] = mu[1600:1696]
    d["rw_muL"] = muL
    d["rw_w2"] = f(z["rw_w2"][l]); d["rw_a2"] = f(z["rw_a2"][l]); d["rw_g2"] = f(z["rw_g2"][l])
    d["rw_ch"] = f(np.stack([chT(z["rw_w0"][l]), chT(z["rw_a0"][l]), chT(z["rw_kk"][l]), chT(z["rw_ka"][l])], 1))
    rk = z["rw_rk"][l]
    d["rw_rk"] = f(np.stack([rk.T, rk.T], 2))
    d["rw_lnw_bc"] = f(np.broadcast_to(z["rw_ln_w"][l], (64, 512))); d["rw_lnb_bc"] = f(np.broadcast_to(z["rw_ln_b"][l], (64, 512)))
    return d


def host_layer_weights(z, l):
    return {"win": hp.prep_win(z["w_in"][l]), "wout": hp.prep_sq(z["w_out"][l], 4), "w1": hp.prep_sq(z["ffn_w1"][l], 11),
            "w3": hp.prep_sq(z["ffn_w3"][l], 11), "w2": hp.prep_w2(z["ffn_w2"][l]), "g1": hp.gT(z["norm_mix"][l]), "g2": hp.gT(z["norm_ffn"][l])}


def build_layer(debug=False, emit_out=True, do_final=True):
    nc = bass.Bass("TRN2", target_bir_lowering=False)
    C = Ctx(); C.nc = nc; C.P = Prog(nc, same_engine_sync=True); C.emit_out = emit_out
    C.P.scopes = False
    P = C.P
    dr = lambda n, s, dt=F32, kind="ExternalInput": nc.dram_tensor(n, s, dt, kind=kind).ap()
    hin = dr("hin", [NTOK, D])
    win = dr("win", [13, 128, 8192]); wout = dr("wout", [4, 128, 8192])
    w1 = dr("w1", [11, 128, 8192]); w3 = dr("w3", [11, 128, 8192]); w2 = dr("w2", [4, 4, 128, 11 * 512])
    g1 = dr("g1", [128, 16]); g2 = dr("g2", [128, 16]); gf = dr("gf", [128, D])
    prm = {k: dr(k, s) for k, s in PRM_SHAPES.items()}
    for k, s in OUT_SHAPES.items():
        prm[k] = dr(k, s, F32, "ExternalOutput")
    dk = "ExternalOutput" if debug else "Internal"
    pf = dr("pf", [NFMB * 128, TP], F32, dk)
    pt = dr("pt", [NTOK, NTMC], F32, dk)
    y = dr("y", [NTOK, D], BF16, dk)
    hmid = dr("hmid", [NTOK, D], F32, dk)
    hout = dr("hout", [NTOK, D], F32, "ExternalOutput")
    out = dr("out", [NTOK - 16, D], F32, "ExternalOutput")
    aT = dr("aT", [5, 128, 44, 512], BF16, "Internal")
    hb, pfb, ptb, yb, hmb, hob, ob, ab = [Buf() for _ in range(8)]
    K = Ctx(); K.sob = Buf()
    with contextlib.ExitStack() as st0:
        K.ident = sbuf(C, st0, "ident", [128, 128], F32); identb = sbuf(C, st0, "identb", [128, 128], BF16)
        g1t = sbuf(C, st0, "g1t", [128, 16]); g2t = sbuf(C, st0, "g2t", [128, 16])
        K.idb, idbb, g1b, g2b = [Buf() for _ in range(4)]
        P.op(POOL, lambda e: e.memset(K.ident[:], 1.0), writes=[K.idb])
        P.op(POOL, lambda e: e.affine_select(out=K.ident[:], in_=K.ident[:], pattern=[[-1, 128]], base=0, channel_multiplier=1,
                                             compare_op=ALU.is_equal, fill=0.0), reads=[K.idb], writes=[K.idb])
        P.op(POOL, lambda e: e.tensor_copy(out=identb[:], in_=K.ident[:]), reads=[K.idb], writes=[idbb])
        P.dma(SP, g1t[:], g1, writes=[g1b]); P.dma(SP, g2t[:], g2, writes=[g2b])
        with contextlib.ExitStack() as st1:
            uT = sbuf(C, st1, "uT", [128, 16, NTOK], BF16); ub = Buf()
            pst = Ring([psum(C, st1, f"pst{i}", [128, 1024], BF16) for i in range(2)])
            psm = Ring([psum(C, st1, f"psm{i}", [128, 512], F32) for i in range(6)])
            with contextlib.ExitStack() as st:
                P.phase = "norm"
                phase_norm(C, st, hin, hb, g1t, g1b, uT, ub, pst, identb, idbb)
            P.barrier()
            with contextlib.ExitStack() as st:
                P.phase = "proj"
                phase_proj(C, st, uT, ub, win, pf, pfb, pt, ptb, psm, prm["hist_out"], K.sob)
            P.barrier()
        with contextlib.ExitStack() as st1:
            K.ones = sbuf(C, st1, "ones", [64, TP]); K.onesb = Buf()
            K.mask_i = sbuf(C, st1, "mask_i", [64, 64]); K.mib = Buf()
            K.mask5 = sbuf(C, st1, "mask5", [64, 5, 64]); K.m5b = Buf()
            K.flagE = sbuf(C, st1, "flagE", [128, 1]); K.fb = Buf()
            P.op(POOL, lambda e: e.memset(K.ones[:], 1.0), writes=[K.onesb])
            P.dma(SP, K.mask_i[:], prm["mask_i"], writes=[K.mib]); P.dma(SP, K.mask5[:], prm["mask5"], writes=[K.m5b])
            P.dma(SP, K.flagE[:], prm["flagE"], writes=[K.fb])
            with contextlib.ExitStack() as st:
                P.phase = "prepass"
                gate_prepass(C, st, pt, ptb)
            P.barrier()
            if True:
              with contextlib.ExitStack() as st:
                P.phase = "gla"
                mixer_gla(C, st, pf, pfb, pt, ptb, y, yb, prm, K)
            P.barrier()
            if True:
              with contextlib.ExitStack() as st:
                P.phase = "mlstm"
                mixer_mlstm(C, st, pf, pfb, pt, ptb, y, yb, prm, K)
            P.barrier()
            if True:
              with contextlib.ExitStack() as st:
                P.phase = "rwkv"
                mixer_rwkv2(C, st, pf, pfb, y, yb, prm, K)
            P.barrier()
        if emit_out:
            with contextlib.ExitStack() as st1:
                uT = sbuf(C, st1, "uT2", [128, 16, NTOK], BF16); ub = Buf()
                pst = Ring([psum(C, st1, f"pst{i}", [128, 1024], BF16) for i in range(2)])
                psm = Ring([psum(C, st1, f"psm{i}", [128, 512], F32) for i in range(6)])
                with contextlib.ExitStack() as st:
                    P.phase = "wout"
                    phase_wout(C, st, y, yb, hin, hb, hmid, hmb, wout, uT, ub, psm, pst, identb, idbb)
                P.barrier()
                with contextlib.ExitStack() as st:
                    P.phase = "norm"
                    phase_norm(C, st, hmid, hmb, g2t, g2b, uT, ub, pst, identb, idbb)
                P.barrier()
                with contextlib.ExitStack() as st:
                    P.phase = "ffn1"
                    phase_ffn1(C, st, uT, ub, w1, w3, aT, ab, psm)
                P.barrier()
    if emit_out:
        with contextlib.ExitStack() as st:
            psm = Ring([psum(C, st, f"psn{i}", [128, 512], F32) for i in range(6)])
            P.phase = "ffn2"
            phase_ffn2(C, st, aT, ab, w2, hmid, hmb, hout, hob, psm)
        P.barrier()
        if do_final:
            with contextlib.ExitStack() as st:
                gft = sbuf(C, st, "gft2", [128, D]); gfb = Buf()
                P.dma(SP, gft[:], gf, writes=[gfb])
                P.phase = "final"
                phase_final_norm(C, st, hout, hob, gft, gfb, out, ob)
    fin = [K.sob, hob, ob]
    if debug:
        fin += [pfb, ptb, yb, hmb]
    P.finish(fin)
    P.emit()
    C.counts = {e: (len(P.ops[e]), sum(1 for o in P.ops[e] if o.signal)) for e in ENGS}
    return nc, C


import contextlib
import numpy as np

LAYER_KEYS = ["gla_a2", "gla_ab", "gla_normbc", "ml_cw", "ml_cb", "ml_ib", "ml_fb", "ml_normbc", "rw_muA", "rw_muL", "rw_w2", "rw_a2",
              "rw_g2", "rw_ch", "rw_rk", "rw_lnw_bc", "rw_lnb_bc"]
STATE_KEYS = ["sA", "sB", "sC", "mC", "hist"]


def emit_half(C, K, T, l, half):
    P = C.P
    hin, hinb = T["hin"][(l, half)]
    hout, houtb = T["hout"][(l, half)]
    prm = {k: T["lp"][k][l] for k in LAYER_KEYS}
    for k in ("mask_i", "mask5", "onehot"):
        prm[k] = T["const"][k]
    prm["flagE"] = T["flag1"] if half == 0 else T["flag0"]
    for k in STATE_KEYS:
        prm[k + "_in"] = T["zstate"][k] if half == 0 else T["state"][k][l]
        prm[k + "_out"] = T["state"][k][l] if half == 0 else T["sdump"][k]
    K.sob = T["stateb"][l] if half == 0 else T["sdumpb"]
    K.fb = Buf()
    win, wout, w1, w3, w2 = T["win"][l], T["wout"][l], T["w1"][l], T["w3"][l], T["w2"][l]
    pf, pfb, pt, ptb, y, yb, hmid, hmb, aT, ab = T["pf"], T["pfb"], T["pt"], T["ptb"], T["y"], T["yb"], T["hmid"], T["hmb"], T["aT"], T["ab"]
    identb, idbb = K.identb, K.idbb
    with contextlib.ExitStack() as st1:
        uT = sbuf(C, st1, "uT", [128, 16, NTOK], BF16); ub = Buf()
        pst = Ring([psum(C, st1, f"pst{i}", [128, 1024], BF16) for i in range(2)])
        psm = Ring([psum(C, st1, f"psm{i}", [128, 512], F32) for i in range(6)])
        with contextlib.ExitStack() as st:
            phase_norm(C, st, hin, hinb, K.g1t[l], K.g1b, uT, ub, pst, identb, idbb)
        P.barrier()
        with contextlib.ExitStack() as st:
            phase_proj(C, st, uT, ub, win, pf, pfb, pt, ptb, psm, prm["hist_out"], K.sob)
        P.barrier()
    with contextlib.ExitStack() as st1:
        K.flagE = sbuf(C, st1, "flagE", [128, 1])
        P.dma(SP, K.flagE[:], prm["flagE"], writes=[K.fb])
        K.ones = sbuf(C, st1, "ones", [64, TP]); K.onesb = Buf()
        K.mask_i = sbuf(C, st1, "mask_i", [64, 64]); K.mib = Buf()
        K.mask5 = sbuf(C, st1, "mask5", [64, 5, 64]); K.m5b = Buf()
        P.op(POOL, lambda e: e.memset(K.ones[:], 1.0), writes=[K.onesb])
        P.dma(SP, K.mask_i[:], T["const"]["mask_i"], writes=[K.mib]); P.dma(SP, K.mask5[:], T["const"]["mask5"], writes=[K.m5b])
        with contextlib.ExitStack() as st:
            gate_prepass(C, st, pt, ptb)
        P.barrier()
        with contextlib.ExitStack() as st:
            mixer_gla(C, st, pf, pfb, pt, ptb, y, yb, prm, K)
        P.barrier()
        with contextlib.ExitStack() as st:
            mixer_mlstm(C, st, pf, pfb, pt, ptb, y, yb, prm, K)
        P.barrier()
        with contextlib.ExitStack() as st:
            mixer_rwkv2(C, st, pf, pfb, y, yb, prm, K)
        P.barrier()
    with contextlib.ExitStack() as st1:
        uT = sbuf(C, st1, "uT2", [128, 16, NTOK], BF16); ub = Buf()
        pst = Ring([psum(C, st1, f"pst{i}", [128, 1024], BF16) for i in range(2)])
        psm = Ring([psum(C, st1, f"psm{i}", [128, 512], F32) for i in range(6)])
        with contextlib.ExitStack() as st:
            phase_wout(C, st, y, yb, hin, hinb, hmid, hmb, wout, uT, ub, psm, pst, identb, idbb)
        P.barrier()
        with contextlib.ExitStack() as st:
            phase_norm(C, st, hmid, hmb, K.g2t[l], K.g1b, uT, ub, pst, identb, idbb)
        P.barrier()
        with contextlib.ExitStack() as st:
            phase_ffn1(C, st, uT, ub, w1, w3, aT, ab, psm)
        P.barrier()
    with contextlib.ExitStack() as st:
        psm = Ring([psum(C, st, f"psn{i}", [128, 512], F32) for i in range(6)])
        phase_ffn2(C, st, aT, ab, w2, hmid, hmb, hout, houtb, psm)
    P.barrier()
    if l == 1:
        with contextlib.ExitStack() as st:
            gft = sbuf(C, st, "gft2", [128, D]); gfb = Buf()
            P.dma(SP, gft[:], T["gf"], writes=[gfb])
            phase_final_norm(C, st, hout, houtb, gft, gfb, T["out"][half], T["outb"])
        P.barrier()


def build_fused(nlayers=2, halves=(0, 1)):
    nc = bass.Bass("TRN2", target_bir_lowering=False)
    C = Ctx(); C.nc = nc; C.P = Prog(nc); C.emit_out = True
    P = C.P
    dr = lambda n, s, dt=F32, kind="ExternalInput": nc.dram_tensor(n, s, dt, kind=kind).ap()
    T = {}
    xin = [dr("xE", [NTOK, D]), dr("xO", [NTOK, D])]
    T["win"] = dr("win", [2, 13, 128, 8192]); T["wout"] = dr("wout", [2, 4, 128, 8192])
    T["w1"] = dr("w1", [2, 11, 128, 8192]); T["w3"] = dr("w3", [2, 11, 128, 8192]); T["w2"] = dr("w2", [2, 4, 4, 128, 11 * 512])
    g1 = dr("g1", [2, 128, 16]); g2 = dr("g2", [2, 128, 16]); T["gf"] = dr("gf", [128, D])
    T["lp"] = {k: dr(k, [2] + PRM_SHAPES[k]) for k in LAYER_KEYS}
    T["const"] = {k: dr(k, PRM_SHAPES[k]) for k in ("mask_i", "mask5", "onehot")}
    T["flag1"] = dr("flag1", [128, 1]); T["flag0"] = dr("flag0", [128, 1])
    T["zstate"] = {k: dr("z_" + k, PRM_SHAPES[k + "_in"]) for k in STATE_KEYS}
    T["state"] = {k: dr("st_" + k, [2] + PRM_SHAPES[k + "_in"], F32, "Internal") for k in STATE_KEYS}
    T["sdump"] = {k: dr("sd_" + k, PRM_SHAPES[k + "_in"], F32, "Internal") for k in STATE_KEYS}
    T["stateb"] = [Buf(), Buf()]; T["sdumpb"] = Buf()
    T["pf"] = dr("pf", [NFMB * 128, TP], F32, "Internal"); T["pt"] = dr("pt", [NTOK, NTMC], F32, "Internal")
    T["y"] = dr("y", [NTOK, D], BF16, "Internal"); T["hmid"] = dr("hmid", [NTOK, D], F32, "Internal")
    T["aT"] = dr("aT", [5, 128, 44, 512], BF16, "Internal")
    for k in ("pfb", "ptb", "yb", "hmb", "ab", "outb"):
        T[k] = Buf()
    h1 = [dr("h1E", [NTOK, D], F32, "Internal"), dr("h1O", [NTOK, D], F32, "Internal")]
    h2 = [dr("h2E", [NTOK, D], F32, "Internal"), dr("h2O", [NTOK, D], F32, "Internal")]
    T["out"] = [dr("outE", [2048, D], F32, "ExternalOutput"), dr("outO", [2048, D], F32, "ExternalOutput")]
    xb = [Buf(), Buf()]; h1b = [Buf(), Buf()]; h2b = [Buf(), Buf()]
    T["hin"] = {(0, 0): (xin[0], xb[0]), (0, 1): (xin[1], xb[1]), (1, 0): (h1[0], h1b[0]), (1, 1): (h1[1], h1b[1])}
    T["hout"] = {(0, 0): (h1[0], h1b[0]), (0, 1): (h1[1], h1b[1]), (1, 0): (h2[0], h2b[0]), (1, 1): (h2[1], h2b[1])}
    K = Ctx()
    with contextlib.ExitStack() as st0:
        K.ident = sbuf(C, st0, "ident", [128, 128], F32); K.identb = sbuf(C, st0, "identb", [128, 128], BF16)
        K.g1t = [sbuf(C, st0, f"g1t{l}", [128, 16]) for l in range(2)]; K.g2t = [sbuf(C, st0, f"g2t{l}", [128, 16]) for l in range(2)]
        K.idb, K.idbb, K.g1b = Buf(), Buf(), Buf()
        P.op(POOL, lambda e: e.memset(K.ident[:], 1.0), writes=[K.idb])
        P.op(POOL, lambda e: e.affine_select(out=K.ident[:], in_=K.ident[:], pattern=[[-1, 128]], base=0, channel_multiplier=1,
                                             compare_op=ALU.is_equal, fill=0.0), reads=[K.idb], writes=[K.idb])
        P.op(POOL, lambda e: e.tensor_copy(out=K.identb[:], in_=K.ident[:]), reads=[K.idb], writes=[K.idbb])
        for l in range(2):
            P.dma(SP, K.g1t[l][:], g1[l], pwrites=[K.g1b]); P.dma(SP, K.g2t[l][:], g2[l], pwrites=[K.g1b])
        for l in range(nlayers):
            for half in halves:
                emit_half(C, K, T, l, half)
    P.finish([T["outb"], T["sdumpb"], T["stateb"][0], T["stateb"][1]])
    P.emit()
    C.counts = {e: (len(P.ops[e]), sum(1 for o in P.ops[e] if o.signal)) for e in ENGS}
    return nc, C


from concourse.bass_utils import run_bass_kernel_spmd

_PROG = {}


def kernel(**z):
    x = np.asarray(z["x"], np.float32)
    meta = np.asarray(z["meta_tokens"], np.float32)
    if "nc" not in _PROG:
        _PROG["nc"] = build_fused()[0]
    nc = _PROG["nc"]
    shared = {}
    shared.update(host_consts())
    shared["gf"] = np.ascontiguousarray(np.broadcast_to(np.asarray(z["norm_final"], np.float32), (128, D)))
    Ws = [host_layer_weights(z, l) for l in range(2)]
    for k in ("win", "wout", "w1", "w3", "w2", "g1", "g2"):
        shared[k] = np.stack([Ws[0][k], Ws[1][k]])
    Ps = [host_layer_params(z, l) for l in range(2)]
    for k in LAYER_KEYS:
        shared[k] = np.stack([Ps[0][k], Ps[1][k]])
    shared["flag1"] = np.ones((128, 1), np.float32)
    shared["flag0"] = np.zeros((128, 1), np.float32)
    for k in STATE_KEYS:
        shared["z_" + k] = np.zeros(PRM_SHAPES[k + "_in"], np.float32)
    in_maps = []
    for c in range(8):
        b = c % 4
        im = dict(shared)
        im["xE"] = np.ascontiguousarray(np.concatenate([meta, x[b, :2048]], 0))
        im["xO"] = np.ascontiguousarray(np.concatenate([meta, x[b, 2048:]], 0))
        in_maps.append(im)
    res = run_bass_kernel_spmd(nc, in_maps, core_ids=list(range(8))).results
    out = np.zeros((4, 4096, D), np.float32)
    for b in range(4):
        out[b, :2048] = np.asarray(res[b]["outE"], np.float32)
        out[b, 2048:] = np.asarray(res[b]["outO"], np.float32)
    return out
```

```python
import contextlib
import numpy as np
import concourse.bass as bass
import concourse.mybir as mybir

F32 = mybir.dt.float32
BF16 = mybir.dt.bfloat16
AF = mybir.ActivationFunctionType
ALU = mybir.AluOpType
AX = mybir.AxisListType

PE, ACT, DVE, POOL, SP = "pe", "act", "dve", "pool", "sp"
ENGS = [PE, ACT, DVE, POOL, SP]
SEG = 30000
NSLOT = 6


class Buf:
    __slots__ = ("name", "w", "ws", "rs")

    def __init__(self, name=""):
        self.name = name
        self.w = None
        self.ws = {}
        self.rs = {}


def _key(o):
    return (o.eng, o.slot if o.dma else None)


class Op:
    __slots__ = ("eng", "fn", "deps", "dma", "idx", "signal", "ev", "slot", "name")

    def __init__(self, eng, fn, dma):
        self.eng = eng
        self.fn = fn
        self.dma = dma
        self.deps = []
        self.signal = False
        self.ev = None
        self.slot = None
        self.name = ""


class Prog:
    def __init__(self, nc, same_engine_sync=True):
        self.nc = nc
        self.ops = {e: [] for e in ENGS}
        self.same = same_engine_sync
        self.ndma = {e: 0 for e in ENGS}
        self.final_deps = []
        self.pending_barrier = None
        self.scopes = False
        self.phase = ""

    def op(self, eng, fn, reads=(), writes=(), dma=False, name="", pwrites=()):
        o = Op(eng, fn, dma)
        o.name = getattr(self, "phase", "")
        if dma:
            o.slot = self.ndma[eng] % NSLOT
            self.ndma[eng] += 1
            o.signal = True
        o.idx = len(self.ops[eng])
        deps = []
        for r in reads:
            if r.w is not None:
                deps.append(r.w)
            deps.extend(r.ws.values())
        for w in writes:
            if w.w is not None:
                deps.append(w.w)
            deps.extend(w.ws.values())
            deps.extend(w.rs.values())
        for w in pwrites:
            if w.w is not None:
                deps.append(w.w)
            deps.extend(w.rs.values())
        if self.pending_barrier and self.pending_barrier.get(eng):
            deps.extend(self.pending_barrier[eng])
            self.pending_barrier[eng] = []
        best = {}
        for d in deps:
            if d is o:
                continue
            if d.eng == eng and not d.dma:
                if eng == PE or not self.same:
                    continue
            k = _key(d)
            if k not in best or best[k].idx < d.idx:
                best[k] = d
        for d in best.values():
            o.deps.append(d)
            d.signal = True
        for w in writes:
            w.w = o
            w.ws = {}
            w.rs = {}
        for w in pwrites:
            w.ws[_key(o)] = o
        for r in reads:
            r.rs[_key(o)] = o
        self.ops[eng].append(o)
        return o

    def dma(self, eng, out, in_, reads=(), writes=(), pwrites=(), **kw):
        return self.op(eng, lambda e: e.dma_start(out=out, in_=in_, **kw), reads, writes, dma=True, pwrites=pwrites)

    def finish(self, bufs):
        for b in bufs:
            for o in ([b.w] if b.w is not None else []) + list(b.ws.values()):
                self.final_deps.append(o)
                o.signal = True

    def emit(self):
        nc = self.nc
        with contextlib.ExitStack() as st:
            csem = {}
            for e in (PE, ACT, DVE, POOL):
                n = sum(1 for o in self.ops[e] if o.signal and not o.dma)
                nseg = n // SEG + 1
                csem[e] = [st.enter_context(nc.semaphore(f"c_{e}_{i}")) for i in range(nseg)]
            dsem = {}
            for e in (ACT, POOL, SP):
                if self.ndma[e] > 0:
                    dsem[e] = [st.enter_context(nc.semaphore(f"d_{e}_{i}")) for i in range(NSLOT)]
            for e in ENGS:
                cnt = 0
                dcur = [0] * NSLOT
                for o in self.ops[e]:
                    if o.dma:
                        prev = dcur[o.slot]
                        dcur[o.slot] += 16
                        o.ev = (dsem[e][o.slot], dcur[o.slot], prev)
                    elif o.signal:
                        seg, v = divmod(cnt, SEG)
                        o.ev = (csem[e][seg], v + 1, None)
                        cnt += 1
            block = st.enter_context(nc.Block())
            handles = {PE: block.tensor, ACT: block.scalar, DVE: block.vector,
                       POOL: block.gpsimd, SP: block.sync}
            for e in ENGS:
                ops = self.ops[e]
                fdeps = self.final_deps if e == SP else []
                if not ops and not fdeps:
                    continue

                def body(eng, ops=ops, fdeps=fdeps):
                    known = {}

                    def wait(sem, val):
                        k = id(sem)
                        if known.get(k, 0) >= val:
                            return
                        eng.wait_ge(sem, val)
                        known[k] = val

                    cur_ph, sid = None, None
                    for o in ops:
                        if self.scopes and o.name != cur_ph:
                            if cur_ph:
                                nc.leave_named_scope(cur_ph, sid, False)
                            cur_ph = o.name
                            if cur_ph:
                                sid, _ = nc.enter_named_scope(cur_ph, False)
                        for d in o.deps:
                            wait(d.ev[0], d.ev[1])
                        if o.dma and o.ev[2] > 0:
                            wait(o.ev[0], o.ev[2])
                        ins = o.fn(eng)
                        if o.dma:
                            ins.then_inc(o.ev[0], 16)
                        elif o.signal:
                            ins.then_inc(o.ev[0], 1)
                    if self.scopes and cur_ph:
                        nc.leave_named_scope(cur_ph, sid, False)
                    for d in fdeps:
                        wait(d.ev[0], d.ev[1])

                handles[e](body)


def _barrier(self):
    lasts = []
    for e in ENGS:
        ops = self.ops[e]
        if not ops:
            continue
        for o in reversed(ops):
            if not o.dma:
                lasts.append(o)
                break
        seen = set()
        for o in reversed(ops):
            if o.dma and o.slot not in seen:
                seen.add(o.slot)
                lasts.append(o)
            if len(seen) == NSLOT:
                break
    for o in lasts:
        o.signal = True
    self.pending_barrier = {e: list(lasts) for e in ENGS}


Prog.barrier = _barrier


class Ring:
    def __init__(self, tiles):
        self.tiles = tiles
        self.bufs = [Buf() for _ in tiles]
        self.i = 0

    def next(self):
        k = self.i % len(self.tiles)
        self.i += 1
        return self.tiles[k], self.bufs[k]


class _HP:
    pass
hp = _HP()


import numpy as np

A0, B0, C0 = 0, 1696, 3248


def fm_blocks():
    blks = []
    for seg in range(3):
        for i in range(4):
            blks.append(list(range(A0 + seg * 512 + i * 128, A0 + seg * 512 + (i + 1) * 128)))
    blks.append(list(range(A0 + 1536, A0 + 1600)))
    blks.append(list(range(A0 + 1600, A0 + 1696)))
    for seg in range(2):
        for i in range(2):
            blks.append(list(range(B0 + seg * 256 + i * 128, B0 + seg * 256 + (i + 1) * 128)))
    blks.append(list(range(B0 + 1024, B0 + 1040)))
    for seg in range(2):
        for i in range(4):
            blks.append(list(range(C0 + seg * 512 + i * 128, C0 + seg * 512 + (i + 1) * 128)))
    blks.append(list(range(C0 + 2048, C0 + 2056)))
    assert len(blks) == 28
    return blks


def tm_cols():
    cols = []
    cols += list(range(B0 + 512, B0 + 1024))
    cols += list(range(B0 + 1040, B0 + 1552))
    cols += list(range(C0 + 1024, C0 + 2048))
    cols += list(range(C0 + 2056, C0 + 3080))
    assert len(cols) == 3072
    return cols


def tile_k(W, ncol=512):
    K = W.shape[0]
    return np.ascontiguousarray(W.reshape(K // 128, 128, ncol).transpose(1, 0, 2).reshape(128, (K // 128) * ncol))


def prep_win(w):
    blks = fm_blocks()
    out = np.zeros((13, 128, 8192), np.float32)
    for wi in range(7):
        Wt = np.zeros((2048, 512), np.float32)
        for bi in range(4):
            cols = blks[wi * 4 + bi]
            Wt[:, bi * 128:bi * 128 + len(cols)] = w[:, cols]
        out[wi] = tile_k(Wt)
    tc = tm_cols()
    for ci in range(6):
        out[7 + ci] = tile_k(w[:, tc[ci * 512:(ci + 1) * 512]])
    return out


def prep_sq(w, ncb):
    return np.stack([tile_k(w[:, i * 512:(i + 1) * 512]) for i in range(ncb)])


def prep_w2(w):
    out = np.zeros((4, 4, 128, 11 * 512), np.float32)
    for cb in range(4):
        for pc in range(4):
            out[cb, pc] = tile_k(w[pc * 1408:(pc + 1) * 1408, cb * 512:(cb + 1) * 512])
    return out


def gT(g):
    return np.ascontiguousarray(g.reshape(16, 128).T)


for _n in ['fm_blocks','tm_cols','tile_k','prep_win','prep_sq','prep_w2','gT']:
    setattr(hp, _n, globals()[_n])


import contextlib

D = 2048
NTOK = 2064
NMETA = 16
TP = 2070
DFF = 5632
NFMB = 28
NTMC = 3072
EPS = 1e-6

TG = [(0, 16)] + [(16 + 512 * i, 512) for i in range(4)]
TT = [(0, 16)] + [(16 + 128 * i, 128) for i in range(16)]


def pfcol(t):
    return 3 + t if t < 16 else t + 6


class Ctx:
    pass


_uid = [0]


def sbuf(C, st, name, shape, dt=F32):
    _uid[0] += 1
    return st.enter_context(C.nc.sbuf_tensor(f"{name}_{_uid[0]}", shape, dt))


def psum(C, st, name, shape, dt=F32):
    _uid[0] += 1
    return st.enter_context(C.nc.psum_tensor(f"{name}_{_uid[0]}", shape, dt))


def make_wloader(C, st, n_stage=2, n_wb=2, stage_elems=8192):
    stage = Ring([sbuf(C, st, f"wst{i}", [128, stage_elems], F32) for i in range(n_stage)])
    return stage


def load_w(C, stage, dst_ap, dst_buf, src_ap, nelem, cast_eng=POOL):
    P = C.P
    stt, stb = stage.next()
    P.dma(SP, stt[:, 0:nelem], src_ap, writes=[stb])
    P.op(cast_eng, lambda e: e.tensor_copy(out=dst_ap, in_=stt[:, 0:nelem]), reads=[stb], writes=[dst_buf])


def phase_norm(C, st, hsrc, hbuf, gT, gbuf, uT, ubuf, ps_t, ident_bf, idbuf):
    P = C.P
    hring = Ring([sbuf(C, st, f"nh{i}", [128, D], F32) for i in range(2)])
    hnring = Ring([sbuf(C, st, f"nhn{i}", [128, D], BF16) for i in range(2)])
    junk = sbuf(C, st, "njunk", [128, D], BF16)
    jb = Buf()
    stat = Ring([sbuf(C, st, f"nst{i}", [128, 4], F32) for i in range(2)])
    mh = sbuf(C, st, "nmh", [128, 1], F32)
    mhb = Buf()
    P.op(POOL, lambda e: e.memset(mh[:], -0.5), writes=[mhb])
    for (t0, nt) in TT:
        ht, hb = hring.next()
        hn, hnb = hnring.next()
        s, sb_ = stat.next()
        P.dma(SP, ht[0:nt, :], hsrc[t0:t0 + nt, :], reads=[hbuf], writes=[hb])
        P.op(ACT, lambda e, ht=ht, s=s, nt=nt: e.activation(out=junk[0:nt, :], in_=ht[0:nt, :], func=AF.Square,
                                                            accum_out=s[0:nt, 0:1]), reads=[hb], writes=[jb, sb_])
        P.op(DVE, lambda e, s=s, nt=nt: e.tensor_scalar(out=s[0:nt, 1:2], in0=s[0:nt, 0:1], scalar1=1.0 / D, scalar2=EPS,
                                                        op0=ALU.mult, op1=ALU.add), reads=[sb_], writes=[sb_])
        P.op(POOL, lambda e, s=s, nt=nt: e.tensor_tensor(out=s[0:nt, 2:3], in0=s[0:nt, 1:2], in1=mh[0:nt, :], op=ALU.pow),
             reads=[sb_, mhb], writes=[sb_])
        P.op(DVE, lambda e, ht=ht, hn=hn, s=s, nt=nt: e.tensor_scalar(out=hn[0:nt, :], in0=ht[0:nt, :], scalar1=s[0:nt, 2:3],
                                                                      scalar2=None, op0=ALU.mult), reads=[hb, sb_], writes=[hnb])
        for half in range(2):
            pt_, ptb = ps_t.next()
            for k in range(8):
                kb = half * 8 + k
                P.op(PE, lambda e, pt_=pt_, hn=hn, kb=kb, k=k, nt=nt: e.transpose(
                    out=pt_[:, k * 128:k * 128 + nt], in_=hn[0:nt, kb * 128:(kb + 1) * 128], identity=ident_bf[0:nt, 0:nt]),
                    reads=[hnb, idbuf], writes=[ptb])
            eng = DVE if half == 0 else POOL
            if half == 0:
                P.op(DVE, lambda e, pt_=pt_, nt=nt, t0=t0, half=half: e.tensor_tensor(
                    out=uT[:, half * 8:half * 8 + 8, t0:t0 + nt],
                    in0=pt_[:].rearrange("p (k t) -> p k t", k=8)[:, :, 0:nt],
                    in1=gT[:, half * 8:half * 8 + 8].unsqueeze(2).broadcast_to([128, 8, nt]), op=ALU.mult),
                    reads=[ptb, gbuf], pwrites=[ubuf])
            else:
                P.op(DVE, lambda e, pt_=pt_, nt=nt, t0=t0, half=half: e.tensor_tensor(
                    out=uT[:, half * 8:half * 8 + 8, t0:t0 + nt],
                    in0=pt_[:].rearrange("p (k t) -> p k t", k=8)[:, :, 0:nt],
                    in1=gT[:, half * 8:half * 8 + 8].unsqueeze(2).broadcast_to([128, 8, nt]), op=ALU.mult),
                    reads=[ptb, gbuf], pwrites=[ubuf])


def phase_proj(C, st, uT, ubuf, w_dram, pf, pfbuf, pt, ptbuf, ps_mm, hist_out=None, hob=None):
    P = C.P
    stage = make_wloader(C, st)
    wb = Ring([sbuf(C, st, f"pwb{i}", [128, 16, 512], BF16) for i in range(2)])
    ev = Ring([sbuf(C, st, f"pev{i}", [128, 512], F32) for i in range(4)])
    cnt = 0
    for wi in range(13):
        wt, wbuf = wb.next()
        load_w(C, stage, wt[:].rearrange("p k c -> p (k c)"), wbuf, w_dram[wi], 8192)
        if wi < 7:
            for bi in range(4):
                blk = wi * 4 + bi
                for (t0, nt) in TG:
                    pm, pmb = ps_mm.next()
                    for kb in range(16):
                        P.op(PE, lambda e, pm=pm, wt=wt, kb=kb, bi=bi, t0=t0, nt=nt: e.matmul(
                            pm[:, 0:nt], lhsT=wt[:, kb, bi * 128:(bi + 1) * 128], rhs=uT[:, kb, t0:t0 + nt],
                            start=(kb == 0), stop=(kb == 15)), reads=[wbuf, ubuf], writes=[pmb])
                    et, eb = ev.next()
                    if cnt % 2 == 0:
                        P.op(ACT, lambda e, et=et, pm=pm, nt=nt: e.activation(out=et[:, 0:nt], in_=pm[:, 0:nt], func=AF.Copy),
                             reads=[pmb], writes=[eb])
                    else:
                        P.op(DVE, lambda e, et=et, pm=pm, nt=nt: e.tensor_copy(out=et[:, 0:nt], in_=pm[:, 0:nt]),
                             reads=[pmb], writes=[eb])
                    cnt += 1
                    c0 = pfcol(t0)
                    P.dma(POOL, pf[blk * 128:(blk + 1) * 128, c0:c0 + nt], et[:, 0:nt], reads=[eb], pwrites=[pfbuf])
                    if hist_out is not None and t0 + nt == NTOK:
                        P.dma(POOL, hist_out[blk * 128:(blk + 1) * 128, :], et[:, nt - 3:nt], reads=[eb], pwrites=[hob])
        else:
            ci = wi - 7
            for (t0, nt) in TT:
                pm, pmb = ps_mm.next()
                for kb in range(16):
                    P.op(PE, lambda e, pm=pm, wt=wt, kb=kb, t0=t0, nt=nt: e.matmul(
                        pm[0:nt, :], lhsT=uT[:, kb, t0:t0 + nt], rhs=wt[:, kb, :],
                        start=(kb == 0), stop=(kb == 15)), reads=[wbuf, ubuf], writes=[pmb])
                et, eb = ev.next()
                if cnt % 2 == 0:
                    P.op(ACT, lambda e, et=et, pm=pm, nt=nt: e.activation(out=et[0:nt, :], in_=pm[0:nt, :], func=AF.Copy),
                         reads=[pmb], writes=[eb])
                else:
                    P.op(DVE, lambda e, et=et, pm=pm, nt=nt: e.tensor_copy(out=et[0:nt, :], in_=pm[0:nt, :]),
                         reads=[pmb], writes=[eb])
                cnt += 1
                P.dma(POOL, pt[t0:t0 + nt, ci * 512:(ci + 1) * 512], et[0:nt, :], reads=[eb], pwrites=[ptbuf])


def phase_wout(C, st, y, ybuf, hsrc, hbuf, hdst, hdbuf, w_dram, uT, ubuf, ps_mm, ps_t, ident_bf, idbuf):
    P = C.P
    yr = Ring([sbuf(C, st, f"oy{i}", [128, D], BF16) for i in range(2)])
    for (t0, nt) in TT:
        yt, yb = yr.next()
        P.dma(SP, yt[0:nt, :], y[t0:t0 + nt, :], reads=[ybuf], writes=[yb])
        for half in range(2):
            pt_, ptb = ps_t.next()
            for k in range(8):
                kb = half * 8 + k
                P.op(PE, lambda e, pt_=pt_, yt=yt, kb=kb, k=k, nt=nt: e.transpose(
                    out=pt_[:, k * 128:k * 128 + nt], in_=yt[0:nt, kb * 128:(kb + 1) * 128], identity=ident_bf[0:nt, 0:nt]),
                    reads=[yb, idbuf], writes=[ptb])
            eng = ACT if half == 0 else DVE
            if half == 0:
                P.op(ACT, lambda e, pt_=pt_, nt=nt, t0=t0, half=half: e.activation(
                    out=uT[:, half * 8:half * 8 + 8, t0:t0 + nt],
                    in_=pt_[:].rearrange("p (k t) -> p k t", k=8)[:, :, 0:nt], func=AF.Copy), reads=[ptb], pwrites=[ubuf])
            else:
                P.op(DVE, lambda e, pt_=pt_, nt=nt, t0=t0, half=half: e.tensor_copy(
                    out=uT[:, half * 8:half * 8 + 8, t0:t0 + nt],
                    in_=pt_[:].rearrange("p (k t) -> p k t", k=8)[:, :, 0:nt]), reads=[ptb], pwrites=[ubuf])
    stage = make_wloader(C, st)
    wb = Ring([sbuf(C, st, f"owb{i}", [128, 16, 512], BF16) for i in range(2)])
    hr = Ring([sbuf(C, st, f"ohr{i}", [128, 512], F32) for i in range(3)])
    ev = Ring([sbuf(C, st, f"oev{i}", [128, 512], F32) for i in range(3)])
    for ci in range(4):
        wt, wbuf = wb.next()
        load_w(C, stage, wt[:].rearrange("p k c -> p (k c)"), wbuf, w_dram[ci], 8192)
        for (t0, nt) in TT:
            ho, hob = hr.next()
            P.dma(SP, ho[0:nt, :], hsrc[t0:t0 + nt, ci * 512:(ci + 1) * 512], reads=[hbuf], writes=[hob])
            pm, pmb = ps_mm.next()
            for kb in range(16):
                P.op(PE, lambda e, pm=pm, wt=wt, kb=kb, t0=t0, nt=nt: e.matmul(
                    pm[0:nt, :], lhsT=uT[:, kb, t0:t0 + nt], rhs=wt[:, kb, :],
                    start=(kb == 0), stop=(kb == 15)), reads=[wbuf, ubuf], writes=[pmb])
            et, eb = ev.next()
            P.op(DVE, lambda e, et=et, pm=pm, ho=ho, nt=nt: e.tensor_tensor(out=et[0:nt, :], in0=pm[0:nt, :], in1=ho[0:nt, :],
                                                                            op=ALU.add), reads=[pmb, hob], writes=[eb])
            P.dma(POOL, hdst[t0:t0 + nt, ci * 512:(ci + 1) * 512], et[0:nt, :], reads=[eb], pwrites=[hdbuf])


def phase_ffn1(C, st, uT, ubuf, w1_dram, w3_dram, aT, abuf, ps_mm):
    P = C.P
    stage = make_wloader(C, st)
    w1b = Ring([sbuf(C, st, f"f1w{i}", [128, 16, 512], BF16) for i in range(2)])
    w3b = Ring([sbuf(C, st, f"f3w{i}", [128, 16, 512], BF16) for i in range(2)])
    sg = Ring([sbuf(C, st, f"fsg{i}", [128, 512], F32) for i in range(3)])
    av = Ring([sbuf(C, st, f"fav{i}", [128, 512], BF16) for i in range(3)])
    for gi in range(11):
        w1t, w1buf = w1b.next()
        w3t, w3buf = w3b.next()
        load_w(C, stage, w1t[:].rearrange("p k c -> p (k c)"), w1buf, w1_dram[gi], 8192)
        load_w(C, stage, w3t[:].rearrange("p k c -> p (k c)"), w3buf, w3_dram[gi], 8192)
        for bi in range(4):
            j = gi * 4 + bi
            for gidx, (t0, nt) in enumerate(TG):
                pa, pab = ps_mm.next()
                for kb in range(16):
                    P.op(PE, lambda e, pa=pa, w1t=w1t, kb=kb, bi=bi, t0=t0, nt=nt: e.matmul(
                        pa[:, 0:nt], lhsT=w1t[:, kb, bi * 128:(bi + 1) * 128], rhs=uT[:, kb, t0:t0 + nt],
                        start=(kb == 0), stop=(kb == 15)), reads=[w1buf, ubuf], writes=[pab])
                pb_, pbb = ps_mm.next()
                for kb in range(16):
                    P.op(PE, lambda e, pb_=pb_, w3t=w3t, kb=kb, bi=bi, t0=t0, nt=nt: e.matmul(
                        pb_[:, 0:nt], lhsT=w3t[:, kb, bi * 128:(bi + 1) * 128], rhs=uT[:, kb, t0:t0 + nt],
                        start=(kb == 0), stop=(kb == 15)), reads=[w3buf, ubuf], writes=[pbb])
                s, sb_ = sg.next()
                a, ab_ = av.next()
                P.op(ACT, lambda e, s=s, pa=pa, nt=nt: e.activation(out=s[:, 0:nt], in_=pa[:, 0:nt], func=AF.Silu),
                     reads=[pab], writes=[sb_])
                P.op(DVE, lambda e, a=a, s=s, pb_=pb_, nt=nt: e.tensor_tensor(out=a[:, 0:nt], in0=pb_[:, 0:nt], in1=s[:, 0:nt],
                                                                              op=ALU.mult), reads=[pbb, sb_], writes=[ab_])
                P.dma(POOL, aT[gidx, :, j, 0:nt], a[:, 0:nt], reads=[ab_], pwrites=[abuf])


def phase_ffn2(C, st, aT, abuf, w2_dram, hsrc, hbuf, hdst, hdbuf, ps_mm):
    P = C.P
    stage = Ring([sbuf(C, st, f"gst{i}", [128, 11 * 512], F32) for i in range(2)])
    w2b = Ring([sbuf(C, st, f"gw{i}", [128, 44, 512], BF16) for i in range(1)])
    ar = Ring([sbuf(C, st, f"gar{i}", [128, 44, 512], BF16) for i in range(2)])
    hr = Ring([sbuf(C, st, f"ghr{i}", [128, 512], F32) for i in range(3)])
    ev = Ring([sbuf(C, st, f"gev{i}", [128, 512], F32) for i in range(3)])
    for cb in range(4):
        wt, wbuf = w2b.next()
        for pc in range(4):
            load_w(C, stage, wt[:, pc * 11:(pc + 1) * 11, :].rearrange("p k c -> p (k c)"), wbuf, w2_dram[cb, pc], 11 * 512)
        for gidx, (g0, gn) in enumerate(TG):
            at, atb = ar.next()
            P.dma(SP, at[:, :, 0:gn], aT[gidx, :, :, 0:gn], reads=[abuf], writes=[atb])
            for s0 in range(0, gn, 128):
                nt = min(128, gn - s0)
                t0 = g0 + s0
                ho, hob = hr.next()
                P.dma(SP, ho[0:nt, :], hsrc[t0:t0 + nt, cb * 512:(cb + 1) * 512], reads=[hbuf], writes=[hob])
                pm, pmb = ps_mm.next()
                for j in range(44):
                    P.op(PE, lambda e, pm=pm, at=at, wt=wt, j=j, s0=s0, nt=nt: e.matmul(
                        pm[0:nt, :], lhsT=at[:, j, s0:s0 + nt], rhs=wt[:, j, :],
                        start=(j == 0), stop=(j == 43)), reads=[wbuf, atb], writes=[pmb])
                et, eb = ev.next()
                P.op(DVE, lambda e, et=et, pm=pm, ho=ho, nt=nt: e.tensor_tensor(out=et[0:nt, :], in0=pm[0:nt, :], in1=ho[0:nt, :],
                                                                                op=ALU.add), reads=[pmb, hob], writes=[eb])
                P.dma(POOL, hdst[t0:t0 + nt, cb * 512:(cb + 1) * 512], et[0:nt, :], reads=[eb], pwrites=[hdbuf])


def phase_final_norm(C, st, hsrc, hbuf, gbc, gbcb, out, obuf):
    P = C.P
    hring = Ring([sbuf(C, st, f"zh{i}", [128, D], F32) for i in range(2)])
    oring = Ring([sbuf(C, st, f"zo{i}", [128, D], F32) for i in range(2)])
    junk = sbuf(C, st, "zjunk", [128, D], BF16)
    jb = Buf()
    stat = Ring([sbuf(C, st, f"zst{i}", [128, 4], F32) for i in range(2)])
    mh = sbuf(C, st, "zmh", [128, 1], F32)
    mhb = Buf()
    P.op(POOL, lambda e: e.memset(mh[:], -0.5), writes=[mhb])
    for (t0, nt) in TT[1:]:
        ht, hb = hring.next()
        ot, ob = oring.next()
        s, sb_ = stat.next()
        P.dma(SP, ht[0:nt, :], hsrc[t0:t0 + nt, :], reads=[hbuf], writes=[hb])
        P.op(ACT, lambda e, ht=ht, s=s, nt=nt: e.activation(out=junk[0:nt, :], in_=ht[0:nt, :], func=AF.Square,
                                                            accum_out=s[0:nt, 0:1]), reads=[hb], writes=[jb, sb_])
        P.op(DVE, lambda e, s=s, nt=nt: e.tensor_scalar(out=s[0:nt, 1:2], in0=s[0:nt, 0:1], scalar1=1.0 / D, scalar2=EPS,
                                                        op0=ALU.mult, op1=ALU.add), reads=[sb_], writes=[sb_])
        P.op(POOL, lambda e, s=s, nt=nt: e.tensor_tensor(out=s[0:nt, 2:3], in0=s[0:nt, 1:2], in1=mh[0:nt, :], op=ALU.pow),
             reads=[sb_, mhb], writes=[sb_])
        P.op(DVE, lambda e, ht=ht, ot=ot, s=s, nt=nt: e.scalar_tensor_tensor(
            out=ot[0:nt, :], in0=ht[0:nt, :], scalar=s[0:nt, 2:3], in1=gbc[0:nt, :], op0=ALU.mult, op1=ALU.mult),
            reads=[hb, sb_, gbcb], writes=[ob])
        P.dma(POOL, out[t0 - 16:t0 - 16 + nt, :], ot[0:nt, :], reads=[ob], pwrites=[obuf])


import contextlib

SEGS = [(3, 16, 0), (22, 2048, 16)]
LDK = 0.6065306597126334


def chunks_of(seg):
    c0, ncol, t0 = seg
    if ncol == 16:
        return [(c0, 16, t0)]
    return [(c0 + 64 * i, 64, t0 + 64 * i) for i in range(ncol // 64)]


def fix_gap(C, x, xb, hist_src, flagE, fb, tmp_ring, rows):
    P = C.P
    t, tb = tmp_ring.next()
    P.dma(SP, t[0:rows, 0:3], hist_src, writes=[tb])
    P.op(DVE, lambda e: e.scalar_tensor_tensor(out=x[0:rows, 19:22], in0=x[0:rows, 16:19], scalar=flagE[0:rows, 0:1],
                                               in1=t[0:rows, 0:3], op0=ALU.mult, op1=ALU.add), reads=[xb, tb, fb], writes=[xb])


def chunk_rel(C, out, ob, src, sb_, rows):
    P = C.P
    P.op(DVE, lambda e: e.tensor_copy(out=out[0:rows, 3:19], in_=src[0:rows, 3:19]), reads=[sb_], writes=[ob])
    P.op(DVE, lambda e: e.tensor_copy(out=out[0:rows, 22:86], in_=src[0:rows, 22:86]), reads=[sb_], writes=[ob])
    o3 = out[0:rows, 86:2070].rearrange("p (c j) -> p c j", j=64)
    s3 = src[0:rows, 86:2070].rearrange("p (c j) -> p c j", j=64)
    pv = src[0:rows, 22:2006].rearrange("p (c j) -> p c j", j=64)[:, :, 63:64].broadcast_to([rows, 31, 64])
    P.op(DVE, lambda e: e.tensor_tensor(out=o3, in0=s3, in1=pv, op=ALU.subtract), reads=[sb_], writes=[ob])


def gate_prepass(C, st, pt, ptb):
    P = C.P
    r = Ring([sbuf(C, st, f"gp{i}", [128, 1536], F32) for i in range(3)])
    for (t0, nt) in TT:
        t, tb = r.next()
        P.dma(SP, t[0:nt, 0:512], pt[t0:t0 + nt, 512:1024], reads=[ptb], writes=[tb])
        P.dma(SP, t[0:nt, 512:1536], pt[t0:t0 + nt, 2048:3072], reads=[ptb], writes=[tb])
        P.op(ACT, lambda e, t=t, nt=nt: e.activation(out=t[0:nt, 0:512], in_=t[0:nt, 0:512], func=AF.Silu), reads=[tb], writes=[tb])
        P.op(ACT, lambda e, t=t, nt=nt: e.activation(out=t[0:nt, 512:1536], in_=t[0:nt, 512:1536], func=AF.Sigmoid),
             reads=[tb], writes=[tb])
        P.dma(POOL, pt[t0:t0 + nt, 512:1024], t[0:nt, 0:512], reads=[tb], writes=[ptb])
        P.dma(POOL, pt[t0:t0 + nt, 2048:3072], t[0:nt, 512:1536], reads=[tb], writes=[ptb])


def mixer_gla(C, st, pf, pfb, pt, ptb, y, yb, prm, K):
    P = C.P
    ones, onesb, ident, idb, mask_i, mib, flagE, fb = K.ones, K.onesb, K.ident, K.idb, K.mask_i, K.mib, K.flagE, K.fb
    a2 = sbuf(C, st, "ga2", [32, 256]); a2b = Buf()
    P.op(DVE, lambda e: e.memset(a2[:], 0.0), writes=[a2b])
    nab = sbuf(C, st, "gnab", [64, 4]); nabb = Buf()
    nbc = sbuf(C, st, "gnbc", [64, 128]); nbcb = Buf()
    P.dma(SP, a2[0:16, :], prm["gla_a2"], reads=[a2b], writes=[a2b])
    P.dma(SP, nab[:], prm["gla_ab"], writes=[nabb])
    P.op(DVE, lambda e: e.tensor_scalar(out=nab[:], in0=nab[:], scalar1=-1.0, scalar2=None, op0=ALU.mult), reads=[nabb], writes=[nabb])
    P.dma(SP, nbc[:], prm["gla_normbc"], writes=[nbcb])
    xa = sbuf(C, st, "gxa", [32, TP]); xab = Buf()
    P.dma(SP, xa[:], pf[18 * 128:18 * 128 + 32, :], reads=[pfb], writes=[xab])
    q = sbuf(C, st, "gq", [64, TP]); k = sbuf(C, st, "gk", [64, TP]); sp = sbuf(C, st, "gsp", [64, TP])
    spc = sbuf(C, st, "gspc", [64, TP]); e1 = sbuf(C, st, "ge1", [64, TP]); e2 = sbuf(C, st, "ge2", [64, TP])
    qb, kb_, spb, spcb, e1b, e2b = [Buf() for _ in range(6)]
    S = sbuf(C, st, "gS", [64, 128]); Sb = Buf()
    Sin = sbuf(C, st, "gSin", [64, 128]); Sinb = Buf()
    psA = Ring([psum(C, st, f"gpa{i}", [128, 512], F32) for i in range(2)])
    psB = Ring([psum(C, st, f"gpb{i}", [128, 512], F32) for i in range(6)])
    vr = Ring([sbuf(C, st, f"gv{i}", [64, 256], F32) for i in range(3)])
    ktr = Ring([sbuf(C, st, f"gkt{i}", [64, 64], F32) for i in range(2)])
    scr = Ring([sbuf(C, st, f"gsc{i}", [64, 64], F32) for i in range(2)])
    str_ = Ring([sbuf(C, st, f"gst{i}", [64, 4], F32) for i in range(2)])
    junk = sbuf(C, st, "gjunk", [64, 128], F32); jb = Buf()
    mh = sbuf(C, st, "gmh", [64, 1], F32); mhb = Buf()
    P.op(POOL, lambda e: e.memset(mh[:], -0.5), writes=[mhb])
    t1r = Ring([sbuf(C, st, f"gt1{i}", [64, 128], F32) for i in range(2)])
    yor = Ring([sbuf(C, st, f"gyo{i}", [64, 128], BF16) for i in range(2)])
    for h in range(4):
        r0 = (14 + h // 2) * 128 + (h % 2) * 64
        r1 = (16 + h // 2) * 128 + (h % 2) * 64
        P.dma(SP, q[:], pf[r0:r0 + 64, :], reads=[pfb], writes=[qb])
        P.dma(SP, k[:], pf[r1:r1 + 64, :], reads=[pfb], writes=[kb_])
        for c0 in range(3, TP, 512):
            n = min(512, TP - c0)
            pa, pab = psA.next()
            P.op(PE, lambda e, pa=pa, c0=c0, n=n, h=h: e.matmul(pa[0:64, 0:n], lhsT=a2[0:32, h * 64:(h + 1) * 64], rhs=xa[0:32, c0:c0 + n],
                                                               start=True, stop=True), reads=[a2b, xab], writes=[pab])
            P.op(ACT, lambda e, pa=pa, c0=c0, n=n, h=h: e.activation(out=sp[:, c0:c0 + n], in_=pa[0:64, 0:n], func=AF.Exp, scale=-1.0,
                                                                    bias=nab[:, h:h + 1]), reads=[pab, nabb], writes=[spb])
        P.op(ACT, lambda e: e.activation(out=sp[:, 3:TP], in_=sp[:, 3:TP], func=AF.Ln, bias=1.0), reads=[spb], writes=[spb])
        for (c0, ncol, t0) in SEGS:
            P.op(DVE, lambda e, c0=c0, ncol=ncol: e.tensor_tensor_scan(out=spc[:, c0:c0 + ncol], data0=ones[0:64, c0:c0 + ncol],
                                                                       data1=sp[:, c0:c0 + ncol], initial=0.0, op0=ALU.mult, op1=ALU.add),
                 reads=[spb, onesb], writes=[spcb])
        chunk_rel(C, sp, spb, spc, spcb, 64)
        P.op(ACT, lambda e: e.activation(out=e1[:, 3:TP], in_=sp[:, 3:TP], func=AF.Exp, scale=-1.0 / 16), reads=[spb], writes=[e1b])
        P.op(ACT, lambda e: e.activation(out=e2[:, 3:TP], in_=sp[:, 3:TP], func=AF.Exp, scale=1.0 / 16), reads=[spb], writes=[e2b])
        P.op(DVE, lambda e: e.scalar_tensor_tensor(out=q[:, 3:TP], in0=q[:, 3:TP], scalar=0.125, in1=e1[:, 3:TP], op0=ALU.mult,
                                                   op1=ALU.mult), reads=[qb, e1b], writes=[qb])
        P.op(DVE, lambda e: e.tensor_tensor(out=k[:, 3:TP], in0=k[:, 3:TP], in1=e2[:, 3:TP], op=ALU.mult), reads=[kb_, e2b], writes=[kb_])
        P.op(DVE, lambda e: e.memset(S[:], 0.0), writes=[Sb])
        for si, seg in enumerate(SEGS):
            if si == 1:
                P.dma(SP, Sin[:], prm["sB_in"][h], writes=[Sinb])
                P.op(DVE, lambda e: e.scalar_tensor_tensor(out=S[:], in0=S[:], scalar=flagE[0:64, 0:1], in1=Sin[:], op0=ALU.mult,
                                                           op1=ALU.add), reads=[Sb, Sinb, fb], writes=[Sb])
            for (c0, n, t0) in chunks_of(seg):
                vt, vb = vr.next()
                P.dma(SP, vt[0:n, 0:128], pt[t0:t0 + n, h * 128:(h + 1) * 128], reads=[ptb], writes=[vb])
                P.dma(SP, vt[0:n, 128:256], pt[t0:t0 + n, 512 + h * 128:512 + (h + 1) * 128], reads=[ptb], writes=[vb])
                pb, pbb = psB.next()
                P.op(PE, lambda e, pb=pb, c0=c0, n=n: e.transpose(out=pb[0:n, 0:64], in_=k[:, c0:c0 + n], identity=ident[0:64, 0:64]),
                     reads=[kb_, idb], writes=[pbb])
                P.op(PE, lambda e, pb=pb, c0=c0, n=n: e.matmul(pb[0:n, 64:64 + n], lhsT=k[:, c0:c0 + n], rhs=q[:, c0:c0 + n], start=True, stop=True),
                     reads=[kb_, qb], writes=[pbb])
                kt, ktb = ktr.next()
                sc, scb = scr.next()
                P.op(ACT, lambda e, kt=kt, pb=pb, n=n: e.activation(out=kt[0:n, :], in_=pb[0:n, 0:64], func=AF.Copy), reads=[pbb], writes=[ktb])
                P.op(DVE, lambda e, sc=sc, pb=pb, n=n: e.tensor_tensor(out=sc[0:n, 0:n], in0=pb[0:n, 64:64 + n], in1=mask_i[0:n, 0:n], op=ALU.mult),
                     reads=[pbb, mib], writes=[scb])
                po, pob = psB.next()
                P.op(PE, lambda e, po=po, c0=c0, n=n: e.matmul(po[0:n, 0:128], lhsT=q[:, c0:c0 + n], rhs=S[:, :], start=True, stop=False),
                     reads=[qb, Sb], writes=[pob])
                P.op(PE, lambda e, po=po, sc=sc, vt=vt, n=n: e.matmul(po[0:n, 0:128], lhsT=sc[0:n, 0:n], rhs=vt[0:n, 0:128], start=False, stop=True),
                     reads=[scb, vb], writes=[pob])
                pc, pcb = psB.next()
                P.op(PE, lambda e, pc=pc: e.matmul(pc[0:64, 0:128], lhsT=ident[0:64, 0:64], rhs=S[:, :], start=True, stop=False),
                     reads=[idb, Sb], writes=[pcb])
                P.op(PE, lambda e, pc=pc, kt=kt, vt=vt, n=n: e.matmul(pc[0:64, 0:128], lhsT=kt[0:n, 0:64], rhs=vt[0:n, 0:128], start=False, stop=True),
                     reads=[ktb, vb], writes=[pcb])
                ce = c0 + n - 1
                P.op(DVE, lambda e, pc=pc, ce=ce: e.tensor_scalar(out=S[:], in0=pc[0:64, 0:128], scalar1=e1[:, ce:ce + 1], scalar2=None, op0=ALU.mult),
                     reads=[pcb, e1b], writes=[Sb])
                if C.emit_out and not False:
                    s_, sb2 = str_.next()
                    t1, t1b = t1r.next()
                    yo, yob = yor.next()
                    P.op(ACT, lambda e, t1=t1, po=po, n=n: e.activation(out=t1[0:n, :], in_=po[0:n, 0:128], func=AF.Copy), reads=[pob], writes=[t1b])
                    P.op(DVE, lambda e, t1=t1, n=n: e.tensor_tensor(out=junk[0:n, :], in0=t1[0:n, :], in1=t1[0:n, :], op=ALU.mult), reads=[t1b], writes=[jb])
                    P.op(DVE, lambda e, s_=s_, n=n: e.tensor_reduce(out=s_[0:n, 0:1], in_=junk[0:n, :], axis=AX.X, op=ALU.add), reads=[jb], writes=[sb2])
                    P.op(DVE, lambda e, s_=s_, n=n: e.tensor_scalar(out=s_[0:n, 1:2], in0=s_[0:n, 0:1], scalar1=1.0 / 128, scalar2=EPS, op0=ALU.mult,
                                                                   op1=ALU.add), reads=[sb2], writes=[sb2])
                    P.op(POOL, lambda e, s_=s_, n=n: e.tensor_tensor(out=s_[0:n, 2:3], in0=s_[0:n, 1:2], in1=mh[0:n, :], op=ALU.pow),
                         reads=[sb2, mhb], writes=[sb2])
                    P.op(DVE, lambda e, t1=t1, s_=s_, n=n: e.scalar_tensor_tensor(out=t1[0:n, :], in0=t1[0:n, :], scalar=s_[0:n, 2:3],
                                                                               in1=nbc[0:n, :], op0=ALU.mult, op1=ALU.mult),
                         reads=[sb2, nbcb, t1b], writes=[t1b])
                    P.op(DVE, lambda e, yo=yo, t1=t1, vt=vt, n=n: e.tensor_tensor(out=yo[0:n, :], in0=t1[0:n, :], in1=vt[0:n, 128:256], op=ALU.mult),
                         reads=[t1b, vb], writes=[yob])
                    P.dma(SP, y[t0:t0 + n, 512 + h * 128:512 + (h + 1) * 128], yo[0:n, :], reads=[yob], pwrites=[yb])
        P.dma(POOL, prm["sB_out"][h], S[:], reads=[Sb], pwrites=[K.sob])


def mixer_mlstm(C, st, pf, pfb, pt, ptb, y, yb, prm, K):
    P = C.P
    ones, onesb, ident, idb, mask_i, mib, flagE, fb = K.ones, K.onesb, K.ident, K.idb, K.mask_i, K.mib, K.flagE, K.fb
    cw = sbuf(C, st, "mcw", [128, 8, 4]); cb = sbuf(C, st, "mcb", [128, 8]); cwb = Buf()
    ib = sbuf(C, st, "mib", [4, 2]); fbb = sbuf(C, st, "mfb", [4, 2]); gbb = Buf()
    nbc = sbuf(C, st, "mnbc", [64, 1024]); nbcb = Buf()
    oh = sbuf(C, st, "moh", [4, 4, 128]); ohb = Buf()
    P.dma(SP, cw[:], prm["ml_cw"], writes=[cwb]); P.dma(SP, cb[:], prm["ml_cb"], writes=[cwb])
    P.dma(SP, ib[:, 0:1], prm["ml_ib"], writes=[gbb]); P.dma(SP, fbb[:, 0:1], prm["ml_fb"], writes=[gbb])
    P.dma(SP, nbc[:], prm["ml_normbc"], writes=[nbcb]); P.dma(SP, oh[:], prm["onehot"], writes=[ohb])
    P.op(DVE, lambda e: e.tensor_scalar(out=ib[:, 1:2], in0=ib[:, 0:1], scalar1=1.0 / 15, scalar2=None, op0=ALU.mult), reads=[gbb], writes=[gbb])
    P.op(DVE, lambda e: e.tensor_scalar(out=fbb[:, 1:2], in0=fbb[:, 0:1], scalar1=1.0 / 15, scalar2=None, op0=ALU.mult), reads=[gbb], writes=[gbb])
    gi = sbuf(C, st, "mgi", [4, TP]); gf = sbuf(C, st, "mgf", [4, TP]); SPc = sbuf(C, st, "mSP", [4, TP]); av = sbuf(C, st, "mav", [4, TP])
    MU = sbuf(C, st, "mMU", [4, TP]); MUS = sbuf(C, st, "mMUS", [4, TP])
    G8 = sbuf(C, st, "mG8", [36, TP]); FE = sbuf(C, st, "mFE", [4, 64]); min_ = sbuf(C, st, "mmin", [4, 4])
    gib, gfb_, SPb, avb, MUb, MUSb, G8b, FEb, minb = [Buf() for _ in range(9)]
    P.dma(SP, gi[:], pf[27 * 128:27 * 128 + 4, :], reads=[pfb], writes=[gib])
    P.dma(SP, gf[:], pf[27 * 128 + 4:27 * 128 + 8, :], reads=[pfb], writes=[gfb_])
    P.op(DVE, lambda e: e.memset(G8[:], 0.0), writes=[G8b])
    P.op(ACT, lambda e: e.activation(out=gi[:], in_=gi[:], func=AF.Tanh, scale=1.0 / 15, bias=ib[:, 1:2]), reads=[gib, gbb], writes=[gib])
    P.op(ACT, lambda e: e.activation(out=gf[:], in_=gf[:], func=AF.Tanh, scale=1.0 / 15, bias=fbb[:, 1:2]), reads=[gfb_, gbb], writes=[gfb_])
    P.op(ACT, lambda e: e.activation(out=gf[:], in_=gf[:], func=AF.Exp, scale=-15.0), reads=[gfb_], writes=[gfb_])
    P.op(ACT, lambda e: e.activation(out=gf[:], in_=gf[:], func=AF.Ln, bias=1.0), reads=[gfb_], writes=[gfb_])
    P.dma(SP, min_[:, 0:1], prm["mC_in"], writes=[minb])
    for si, (c0, ncol, t0) in enumerate(SEGS):
        P.op(DVE, lambda e, c0=c0, ncol=ncol: e.tensor_tensor_scan(out=SPc[:, c0:c0 + ncol], data0=ones[0:4, c0:c0 + ncol], data1=gf[:, c0:c0 + ncol],
                                                                   initial=0.0, op0=ALU.mult, op1=ALU.add), reads=[gfb_, onesb], writes=[SPb])
        P.op(DVE, lambda e, c0=c0, ncol=ncol: e.scalar_tensor_tensor(out=av[:, c0:c0 + ncol], in0=gi[:, c0:c0 + ncol], scalar=15.0,
                                                                     in1=SPc[:, c0:c0 + ncol], op0=ALU.mult, op1=ALU.add), reads=[gib, SPb], writes=[avb])
        if si == 0:
            P.op(DVE, lambda e, c0=c0, ncol=ncol: e.tensor_tensor_scan(out=MU[:, c0:c0 + ncol], data0=av[:, c0:c0 + ncol], data1=av[:, c0:c0 + ncol],
                                                                       initial=0.0, op0=ALU.max, op1=ALU.max), reads=[avb], writes=[MUb])
            P.op(DVE, lambda e: e.memset(MUS[:, 3:19], 0.0), writes=[MUSb])
            P.op(DVE, lambda e: e.tensor_tensor(out=min_[:, 1:2], in0=MU[:, 18:19], in1=SPc[:, 18:19], op=ALU.subtract), reads=[MUb, SPb, minb], writes=[minb])
            P.op(DVE, lambda e: e.scalar_tensor_tensor(out=min_[:, 2:3], in0=min_[:, 1:2], scalar=flagE[0:4, 0:1], in1=min_[:, 0:1], op0=ALU.mult,
                                                       op1=ALU.add), reads=[minb, fb], writes=[minb])
        else:
            P.op(DVE, lambda e, c0=c0, ncol=ncol: e.tensor_tensor_scan(out=MU[:, c0:c0 + ncol], data0=av[:, c0:c0 + ncol], data1=av[:, c0:c0 + ncol],
                                                                       initial=min_[:, 2:3], op0=ALU.max, op1=ALU.max), reads=[avb, minb], writes=[MUb])
            P.op(DVE, lambda e: e.tensor_copy(out=MUS[:, 22:86], in_=min_[:, 2:3].broadcast_to([4, 64])), reads=[minb], writes=[MUSb])
            P.op(DVE, lambda e: e.tensor_copy(out=MUS[:, 86:2070].rearrange("p (c j) -> p c j", j=64),
                                              in_=MU[:, 22:2006].rearrange("p (c j) -> p c j", j=64)[:, :, 63:64].broadcast_to([4, 31, 64])),
                 reads=[MUb], writes=[MUSb])
    P.op(DVE, lambda e: e.tensor_tensor(out=av[:, 3:TP], in0=av[:, 3:TP], in1=MUS[:, 3:TP], op=ALU.subtract), reads=[avb, MUSb], writes=[avb])
    P.op(ACT, lambda e: e.activation(out=G8[0:4, 3:TP], in_=av[:, 3:TP], func=AF.Exp), reads=[avb, G8b], writes=[G8b])
    P.op(DVE, lambda e: e.tensor_tensor(out=av[:, 3:TP], in0=SPc[:, 3:TP], in1=MUS[:, 3:TP], op=ALU.subtract), reads=[SPb, MUSb, G8b], writes=[avb])
    P.op(ACT, lambda e: e.activation(out=G8[32:36, 3:TP], in_=av[:, 3:TP], func=AF.Exp), reads=[avb, G8b], writes=[G8b])
    P.op(DVE, lambda e: e.tensor_tensor(out=FE[:, 0:1], in0=MUS[:, 3:4], in1=MU[:, 18:19], op=ALU.subtract), reads=[MUSb, MUb], writes=[FEb])
    P.op(DVE, lambda e: e.tensor_tensor(out=FE[:, 1:33], in0=MUS[:, 22:2070].rearrange("p (c j) -> p c j", j=64)[:, :, 0],
                                        in1=MU[:, 22:2070].rearrange("p (c j) -> p c j", j=64)[:, :, 63], op=ALU.subtract),
         reads=[MUSb, MUb, FEb], writes=[FEb])
    P.op(ACT, lambda e: e.activation(out=FE[:, 0:33], in_=FE[:, 0:33], func=AF.Exp), reads=[FEb], writes=[FEb])
    P.op(DVE, lambda e: e.tensor_tensor(out=min_[:, 3:4], in0=MU[:, TP - 1:TP], in1=SPc[:, TP - 1:TP], op=ALU.subtract), reads=[MUb, SPb, minb], writes=[minb])
    P.dma(POOL, prm["mC_out"], min_[:, 3:4], reads=[minb], pwrites=[K.sob])
    xq = sbuf(C, st, "mxq", [128, TP]); xk = sbuf(C, st, "mxk", [128, TP]); q = sbuf(C, st, "mq", [128, TP]); k = sbuf(C, st, "mk", [128, TP])
    xqb, xkb, qb, kb_ = [Buf() for _ in range(4)]
    tmpr = Ring([sbuf(C, st, f"mtmp{i}", [128, 4], F32) for i in range(2)])
    CX = sbuf(C, st, "mCX", [128, 257]); CXb = Buf()
    CXin = sbuf(C, st, "mCXin", [128, 257]); CXinb = Buf()
    FB = sbuf(C, st, "mFB", [128, 64]); FBb = Buf()
    psA = Ring([psum(C, st, f"mpa{i}", [128, 512], F32) for i in range(4)])
    psG = Ring([psum(C, st, f"mpg{i}", [128, 512], F32) for i in range(2)])
    vr = Ring([sbuf(C, st, f"mv{i}", [64, 512], F32) for i in range(3)])
    vxr = Ring([sbuf(C, st, f"mvx{i}", [64, 257], F32) for i in range(2)])
    ktr = Ring([sbuf(C, st, f"mkt{i}", [64, 128], F32) for i in range(2)])
    scr = Ring([sbuf(C, st, f"msc{i}", [64, 64], F32) for i in range(2)])
    gtr = Ring([sbuf(C, st, f"mgt{i}", [64, 36], F32) for i in range(2)])
    str_ = Ring([sbuf(C, st, f"mst{i}", [64, 8], F32) for i in range(2)])
    junk = sbuf(C, st, "mjunk", [64, 256], F32); jb = Buf()
    mh = sbuf(C, st, "mmh", [64, 1], F32); mhb = Buf()
    P.op(POOL, lambda e: e.memset(mh[:], -0.5), writes=[mhb])
    t1r = Ring([sbuf(C, st, f"mt1{i}", [64, 256], F32) for i in range(2)])
    yor = Ring([sbuf(C, st, f"myo{i}", [64, 256], BF16) for i in range(2)])
    for h in range(4):
        P.dma(SP, xq[:], pf[(19 + h) * 128:(20 + h) * 128, :], reads=[pfb], writes=[xqb])
        P.dma(SP, xk[:], pf[(23 + h) * 128:(24 + h) * 128, :], reads=[pfb], writes=[xkb])
        P.op(DVE, lambda e: e.memset(xq[:, 0:3], 0.0), reads=[xqb], writes=[xqb])
        P.op(DVE, lambda e: e.memset(xk[:, 0:3], 0.0), reads=[xkb], writes=[xkb])
        fix_gap(C, xq, xqb, prm["hist_in"][(19 + h) * 128:(20 + h) * 128, :], flagE, fb, tmpr, 128)
        fix_gap(C, xk, xkb, prm["hist_in"][(23 + h) * 128:(24 + h) * 128, :], flagE, fb, tmpr, 128)
        for (x, xb, o, ob, j) in ((xq, xqb, q, qb, h), (xk, xkb, k, kb_, 4 + h)):
            P.op(DVE, lambda e, x=x, o=o, j=j: e.tensor_scalar(out=o[:, 3:TP], in0=x[:, 0:TP - 3], scalar1=cw[:, j, 0:1], scalar2=cb[:, j:j + 1],
                                                               op0=ALU.mult, op1=ALU.add), reads=[xb, cwb], writes=[ob])
            for tap in range(1, 4):
                P.op(DVE, lambda e, x=x, o=o, j=j, tap=tap: e.scalar_tensor_tensor(out=o[:, 3:TP], in0=x[:, tap:TP - 3 + tap], scalar=cw[:, j, tap:tap + 1],
                                                                                 in1=o[:, 3:TP], op0=ALU.mult, op1=ALU.add), reads=[xb, cwb, ob], writes=[ob])
            P.op(ACT, lambda e, o=o: e.activation(out=o[:, 3:TP], in_=o[:, 3:TP], func=AF.Silu), reads=[ob], writes=[ob])
        P.op(DVE, lambda e: e.tensor_scalar(out=k[:, 3:TP], in0=k[:, 3:TP], scalar1=128 ** -0.5, scalar2=None, op0=ALU.mult), reads=[kb_], writes=[kb_])
        pg, pgb = psG.next()
        P.op(PE, lambda e, pg=pg, h=h: e.matmul(pg[:, 0:33], lhsT=oh[0:4, h, :], rhs=FE[0:4, 0:33], start=True, stop=True), reads=[ohb, FEb], writes=[pgb])
        P.op(ACT, lambda e, pg=pg: e.activation(out=FB[:, 0:33], in_=pg[:, 0:33], func=AF.Copy), reads=[pgb], writes=[FBb])
        P.op(DVE, lambda e: e.memset(CX[:], 0.0), writes=[CXb])
        ci = 0
        for si, seg in enumerate(SEGS):
            if si == 1:
                P.dma(SP, CXin[:], prm["sC_in"][h], writes=[CXinb])
                P.op(DVE, lambda e: e.scalar_tensor_tensor(out=CX[:], in0=CX[:], scalar=flagE[:, 0:1], in1=CXin[:], op0=ALU.mult, op1=ALU.add),
                     reads=[CXb, CXinb, fb], writes=[CXb])
            for (c0, n, t0) in chunks_of(seg):
                vt, vb = vr.next()
                P.dma(SP, vt[0:n, 0:256], pt[t0:t0 + n, 1024 + h * 256:1024 + (h + 1) * 256], reads=[ptb], writes=[vb])
                P.dma(SP, vt[0:n, 256:512], pt[t0:t0 + n, 2048 + h * 256:2048 + (h + 1) * 256], reads=[ptb], writes=[vb])
                pa, pab = psA.next()
                P.op(PE, lambda e, pa=pa, c0=c0, n=n: e.transpose(out=pa[0:n, 0:128], in_=k[:, c0:c0 + n], identity=ident[:, :]), reads=[kb_, idb], writes=[pab])
                P.op(PE, lambda e, pa=pa, c0=c0, n=n: e.matmul(pa[0:n, 128:128 + n], lhsT=k[:, c0:c0 + n], rhs=q[:, c0:c0 + n], start=True, stop=True),
                     reads=[kb_, qb], writes=[pab])
                P.op(PE, lambda e, pa=pa, c0=c0, n=n: e.transpose(out=pa[0:n, 192:228], in_=G8[0:36, c0:c0 + n], identity=ident[0:36, 0:36]),
                     reads=[G8b, idb], writes=[pab])
                kt, ktb = ktr.next(); sc, scb = scr.next(); gt, gtb = gtr.next()
                P.op(ACT, lambda e, kt=kt, pa=pa, n=n: e.activation(out=kt[0:n, :], in_=pa[0:n, 0:128], func=AF.Copy), reads=[pab], writes=[ktb])
                P.op(DVE, lambda e, sc=sc, pa=pa, n=n: e.tensor_tensor(out=sc[0:n, 0:n], in0=pa[0:n, 128:128 + n], in1=mask_i[0:n, 0:n], op=ALU.mult),
                     reads=[pab, mib], writes=[scb])
                P.op(ACT, lambda e, gt=gt, pa=pa, n=n: e.activation(out=gt[0:n, :], in_=pa[0:n, 192:228], func=AF.Copy), reads=[pab], writes=[gtb])
                vx, vxb = vxr.next()
                P.op(DVE, lambda e, vx=vx, vt=vt, gt=gt, n=n, h=h: e.tensor_scalar(out=vx[0:n, 0:256], in0=vt[0:n, 0:256], scalar1=gt[0:n, h:h + 1], scalar2=None,
                                                                                 op0=ALU.mult), reads=[vb, gtb], writes=[vxb])
                P.op(ACT, lambda e, vx=vx, gt=gt, n=n, h=h: e.activation(out=vx[0:n, 256:257], in_=gt[0:n, h:h + 1], func=AF.Copy), reads=[gtb, vxb], writes=[vxb])
                pn, pnb = psA.next()
                P.op(PE, lambda e, pn=pn, c0=c0, n=n: e.matmul(pn[0:n, 0:257], lhsT=q[:, c0:c0 + n], rhs=CX[:, :], start=True, stop=False), reads=[qb, CXb], writes=[pnb])
                P.op(PE, lambda e, pn=pn, sc=sc, vx=vx, n=n: e.matmul(pn[0:n, 0:257], lhsT=sc[0:n, 0:n], rhs=vx[0:n, :], start=False, stop=True),
                     reads=[scb, vxb], writes=[pnb])
                pc, pcb = psA.next()
                P.op(PE, lambda e, pc=pc: e.matmul(pc[:, 0:257], lhsT=ident[:, :], rhs=CX[:, :], start=True, stop=False), reads=[idb, CXb], writes=[pcb])
                P.op(PE, lambda e, pc=pc, kt=kt, vx=vx, n=n: e.matmul(pc[:, 0:257], lhsT=kt[0:n, :], rhs=vx[0:n, :], start=False, stop=True),
                     reads=[ktb, vxb], writes=[pcb])
                P.op(DVE, lambda e, pc=pc, ci=ci: e.tensor_scalar(out=CX[:], in0=pc[:, 0:257], scalar1=FB[:, ci:ci + 1], scalar2=None, op0=ALU.mult),
                     reads=[pcb, FBb], writes=[CXb])
                if C.emit_out:
                    s_, sb2 = str_.next()
                    P.op(ACT, lambda e, s_=s_, pn=pn, n=n: e.activation(out=s_[0:n, 0:1], in_=pn[0:n, 256:257], func=AF.Abs),
                         reads=[pnb], writes=[sb2])
                    P.op(DVE, lambda e, s_=s_, gt=gt, n=n, h=h: e.tensor_tensor(out=s_[0:n, 0:1], in0=s_[0:n, 0:1], in1=gt[0:n, 32 + h:33 + h], op=ALU.max),
                         reads=[sb2, gtb], writes=[sb2])
                    P.op(DVE, lambda e, s_=s_, n=n: e.reciprocal(out=s_[0:n, 1:2], in_=s_[0:n, 0:1]), reads=[sb2], writes=[sb2])
                    P.op(ACT, lambda e, s_=s_, pn=pn, n=n: e.activation(out=junk[0:n, :], in_=pn[0:n, 0:256], func=AF.Square, scale=s_[0:n, 1:2],
                                                                       accum_out=s_[0:n, 2:3]), reads=[pnb, sb2], writes=[jb, sb2])
                    P.op(DVE, lambda e, s_=s_, n=n: e.tensor_scalar(out=s_[0:n, 3:4], in0=s_[0:n, 2:3], scalar1=1.0 / 256, scalar2=EPS, op0=ALU.mult,
                                                                   op1=ALU.add), reads=[sb2], writes=[sb2])
                    P.op(POOL, lambda e, s_=s_, n=n: e.tensor_tensor(out=s_[0:n, 4:5], in0=s_[0:n, 3:4], in1=mh[0:n, :], op=ALU.pow), reads=[sb2, mhb], writes=[sb2])
                    P.op(DVE, lambda e, s_=s_, n=n: e.tensor_tensor(out=s_[0:n, 5:6], in0=s_[0:n, 4:5], in1=s_[0:n, 1:2], op=ALU.mult), reads=[sb2], writes=[sb2])
                    t1, t1b = t1r.next(); yo, yob = yor.next()
                    P.op(DVE, lambda e, t1=t1, pn=pn, s_=s_, n=n, h=h: e.scalar_tensor_tensor(out=t1[0:n, :], in0=pn[0:n, 0:256], scalar=s_[0:n, 5:6],
                                                                                          in1=nbc[0:n, h * 256:(h + 1) * 256], op0=ALU.mult, op1=ALU.mult),
                         reads=[pnb, sb2, nbcb], writes=[t1b])
                    P.op(DVE, lambda e, yo=yo, t1=t1, vt=vt, n=n: e.tensor_tensor(out=yo[0:n, :], in0=t1[0:n, :], in1=vt[0:n, 256:512], op=ALU.mult),
                         reads=[t1b, vb], writes=[yob])
                    P.dma(POOL, y[t0:t0 + n, 1024 + h * 256:1024 + (h + 1) * 256], yo[0:n, :], reads=[yob], pwrites=[yb])
                ci += 1
        P.dma(POOL, prm["sC_out"][h], CX[:], reads=[CXb], pwrites=[K.sob])


def mixer_rwkv(C, st, pf, pfb, y, yb, prm, K):
    P = C.P
    ones, onesb, ident, idb, flagE, fb = K.ones, K.onesb, K.ident, K.idb, K.flagE, K.fb
    mask5, m5b = K.mask5, K.m5b
    muA = sbuf(C, st, "amuA", [64, 3, 8]); muL = sbuf(C, st, "amuL", [96, 3]); w2 = sbuf(C, st, "aw2", [32, 512]); a2 = sbuf(C, st, "aa2", [32, 512])
    g2 = sbuf(C, st, "ag2", [96, 512]); ch = sbuf(C, st, "ach", [64, 5, 8]); rk = sbuf(C, st, "ark", [64, 8, 2])
    lnw = sbuf(C, st, "alnw", [64, 512]); lnb = sbuf(C, st, "alnb", [64, 512])
    pb_ = Buf()
    for t, n_ in ((muA, "rw_muA"), (muL, "rw_muL"), (w2, "rw_w2"), (a2, "rw_a2"), (g2, "rw_g2"), (rk, "rw_rk"), (lnw, "rw_lnw_bc"), (lnb, "rw_lnb_bc")):
        P.dma(SP, t[:], prm[n_], pwrites=[pb_])
    P.dma(SP, ch[:, 0:4, :], prm["rw_ch"], pwrites=[pb_])
    P.op(DVE, lambda e: e.tensor_scalar(out=ch[:, 4, :], in0=ch[:, 3, :], scalar1=-1.0, scalar2=1.0, op0=ALU.mult, op1=ALU.add), reads=[pb_], writes=[pb_])
    mh = sbuf(C, st, "amh", [64, TP], F32); mhb = Buf()
    P.op(POOL, lambda e: e.memset(mh[:], -0.5), writes=[mhb])
    tmpr = Ring([sbuf(C, st, f"atmp{i}", [128, 4], F32) for i in range(2)])
    raw = sbuf(C, st, "araw", [96, TP]); rawb = Buf()
    thw = sbuf(C, st, "athw", [32, TP]); xal = sbuf(C, st, "axal", [32, TP]); sg = sbuf(C, st, "asg", [96, TP])
    thwb, xalb, sgb = Buf(), Buf(), Buf()
    for (dst, dstb, r0, nr, mcol, fn) in ((thw, thwb, 12 * 128, 32, 0, AF.Tanh), (xal, xalb, 12 * 128 + 32, 32, 1, None), (sg, sgb, 13 * 128, 96, 2, AF.Sigmoid)):
        P.dma(SP, raw[0:nr, :], pf[r0:r0 + nr, :], reads=[pfb], writes=[rawb])
        P.op(DVE, lambda e, nr=nr: e.memset(raw[0:nr, 0:3], 0.0), reads=[rawb], writes=[rawb])
        fix_gap(C, raw, rawb, prm["hist_in"][r0:r0 + nr, :], flagE, fb, tmpr, nr)
        P.op(DVE, lambda e, dst=dst, nr=nr: e.tensor_tensor(out=dst[0:nr, 3:TP], in0=raw[0:nr, 2:TP - 1], in1=raw[0:nr, 3:TP], op=ALU.subtract),
             reads=[rawb], writes=[dstb])
        P.op(DVE, lambda e, dst=dst, nr=nr, mcol=mcol: e.scalar_tensor_tensor(out=dst[0:nr, 3:TP], in0=dst[0:nr, 3:TP], scalar=muL[0:nr, mcol:mcol + 1],
                                                                            in1=raw[0:nr, 3:TP], op0=ALU.mult, op1=ALU.add), reads=[rawb, dstb, pb_], writes=[dstb])
        if fn is not None:
            P.op(ACT, lambda e, dst=dst, nr=nr, fn=fn: e.activation(out=dst[0:nr, 3:TP], in_=dst[0:nr, 3:TP], func=fn), reads=[dstb], writes=[dstb])
    A = [sbuf(C, st, f"aA{i}", [64, TP]) for i in range(10)]
    Ab = [Buf() for _ in range(10)]
    H = sbuf(C, st, "aH", [64, 64]); Hb = Buf()
    Hin = sbuf(C, st, "aHin", [64, 64]); Hinb = Buf()
    G = 8
    MM = sbuf(C, st, "aMM", [64, G, 5, 64]); MMb = Buf()
    TM = sbuf(C, st, "aTM", [64, G, 3, 64]); TMb = Buf()
    NN = [sbuf(C, st, f"aNN{i}", [64, G, 2, 64]) for i in range(2)]; NNb = [Buf(), Buf()]
    Pm = sbuf(C, st, "aPm", [64, G, 64]); Pmb = Buf()
    GB = sbuf(C, st, "aGB", [64, G, 66]); GBb = Buf()
    ps1 = Ring([psum(C, st, f"ap1{i}", [128, 512], F32) for i in range(3)])
    pygr = Ring([psum(C, st, f"apy{i}", [128, 512], F32) for i in range(2)])
    ps2 = Ring([psum(C, st, f"ap2{i}", [128, 512], F32) for i in range(3)])
    w0r = Ring([sbuf(C, st, f"aw0{i}", [64, 64], F32) for i in range(2)])
    ur = Ring([sbuf(C, st, f"au{i}", [64, 64], F32) for i in range(2)])
    str_ = Ring([sbuf(C, st, f"ast{i}", [64, 8], F32) for i in range(2)])
    junk = sbuf(C, st, "ajunk", [64, 64], F32); jb = Buf()
    T1 = sbuf(C, st, "aT1", [64, G, 64]); T1b = Buf()
    SQ = sbuf(C, st, "aSQ", [64, G, 64]); SQb = Buf()
    YO = sbuf(C, st, "aYO", [64, G, 64], BF16); YOb = Buf()
    ST = sbuf(C, st, "aST", [64, 6, G]); STb = Buf()
    for h in range(8):
        rows = [(sg_ * 4 + h // 2) * 128 + (h % 2) * 64 for sg_ in range(3)]
        for i in range(3):
            P.dma(SP, A[i][:], pf[rows[i]:rows[i] + 64, :], reads=[pfb], writes=[Ab[i]])
            P.op(DVE, lambda e, i=i: e.memset(A[i][:, 0:3], 0.0), reads=[Ab[i]], writes=[Ab[i]])
            fix_gap(C, A[i], Ab[i], prm["hist_in"][rows[i]:rows[i] + 64, :], flagE, fb, tmpr, 64)
            P.op(DVE, lambda e, i=i: e.tensor_tensor(out=A[3 + i][:, 3:TP], in0=A[i][:, 2:TP - 1], in1=A[i][:, 3:TP], op=ALU.subtract),
                 reads=[Ab[i]], writes=[Ab[3 + i]])
            P.op(DVE, lambda e, i=i, h=h: e.scalar_tensor_tensor(out=A[3 + i][:, 3:TP], in0=A[3 + i][:, 3:TP], scalar=muA[:, i, h:h + 1], in1=A[i][:, 3:TP],
                                                                op0=ALU.mult, op1=ALU.add), reads=[Ab[i], Ab[3 + i], pb_], writes=[Ab[3 + i]])
        xr, xk, xv = A[3], A[4], A[5]
        for c0 in range(3, TP, 512):
            n = min(512, TP - c0)
            p_, p_b = ps1.next()
            P.op(PE, lambda e, p_=p_, c0=c0, n=n, h=h: e.matmul(p_[0:64, 0:n], lhsT=w2[0:32, h * 64:(h + 1) * 64], rhs=thw[0:32, c0:c0 + n], start=True, stop=True),
                 reads=[pb_, thwb], writes=[p_b])
            P.op(ACT, lambda e, p_=p_, c0=c0, n=n, h=h: e.activation(out=A[0][:, c0:c0 + n], in_=p_[0:64, 0:n], func=AF.Sigmoid, bias=ch[:, 0, h:h + 1]),
                 reads=[p_b, pb_], writes=[Ab[0]])
            p_, p_b = ps1.next()
            P.op(PE, lambda e, p_=p_, c0=c0, n=n, h=h: e.matmul(p_[0:64, 0:n], lhsT=a2[0:32, h * 64:(h + 1) * 64], rhs=xal[0:32, c0:c0 + n], start=True, stop=True),
                 reads=[pb_, xalb], writes=[p_b])
            P.op(ACT, lambda e, p_=p_, c0=c0, n=n, h=h: e.activation(out=A[1][:, c0:c0 + n], in_=p_[0:64, 0:n], func=AF.Sigmoid, bias=ch[:, 1, h:h + 1]),
                 reads=[p_b, pb_], writes=[Ab[1]])
        P.op(DVE, lambda e, h=h: e.tensor_scalar(out=A[2][:, 3:TP], in0=xk[:, 3:TP], scalar1=ch[:, 2, h:h + 1], scalar2=None, op0=ALU.mult),
             reads=[Ab[4], pb_], writes=[Ab[2]])
        P.op(DVE, lambda e: e.tensor_tensor(out=A[6][:, 3:TP], in0=A[2][:, 3:TP], in1=A[2][:, 3:TP], op=ALU.mult), reads=[Ab[2]], writes=[Ab[6]])
        for c0 in range(3, TP, 512):
            n = min(512, TP - c0)
            p_, p_b = ps1.next()
            P.op(PE, lambda e, p_=p_, c0=c0, n=n: e.matmul(p_[0:64, 0:n], lhsT=ones[0:64, 0:64], rhs=A[6][:, c0:c0 + n], start=True, stop=True),
                 reads=[onesb, Ab[6]], writes=[p_b])
            P.op(DVE, lambda e, p_=p_, c0=c0, n=n: e.tensor_scalar(out=A[8][:, c0:c0 + n], in0=p_[0:64, 0:n], scalar1=1e-24, scalar2=None, op0=ALU.max),
                 reads=[p_b], writes=[Ab[8]])
        P.op(POOL, lambda e: e.tensor_tensor(out=A[8][:, 3:TP], in0=A[8][:, 3:TP], in1=mh[:, 3:TP], op=ALU.pow), reads=[Ab[8], mhb], writes=[Ab[8]])
        P.op(DVE, lambda e: e.tensor_tensor(out=A[2][:, 3:TP], in0=A[2][:, 3:TP], in1=A[8][:, 3:TP], op=ALU.mult), reads=[Ab[2], Ab[8]], writes=[Ab[2]])
        P.op(DVE, lambda e, h=h: e.tensor_scalar(out=A[6][:, 3:TP], in0=A[1][:, 3:TP], scalar1=ch[:, 3, h:h + 1], scalar2=ch[:, 4, h:h + 1], op0=ALU.mult,
                                                op1=ALU.add), reads=[Ab[1], pb_], writes=[Ab[6]])
        P.op(DVE, lambda e: e.tensor_tensor(out=xk[:, 3:TP], in0=xk[:, 3:TP], in1=A[6][:, 3:TP], op=ALU.mult), reads=[Ab[4], Ab[6]], writes=[Ab[4]])
        P.op(DVE, lambda e: e.tensor_tensor(out=A[1][:, 3:TP], in0=A[1][:, 3:TP], in1=A[2][:, 3:TP], op=ALU.mult), reads=[Ab[1], Ab[2]], writes=[Ab[1]])
        P.op(DVE, lambda e: e.tensor_tensor(out=A[6][:, 3:TP], in0=xr[:, 3:TP], in1=xk[:, 3:TP], op=ALU.mult), reads=[Ab[3], Ab[4]], writes=[Ab[6]])
        for (c0, ncol, t0) in SEGS:
            P.op(DVE, lambda e, c0=c0, ncol=ncol: e.tensor_tensor_scan(out=A[7][:, c0:c0 + ncol], data0=ones[0:64, c0:c0 + ncol], data1=A[0][:, c0:c0 + ncol],
                                                                       initial=0.0, op0=ALU.mult, op1=ALU.add), reads=[Ab[0], onesb], writes=[Ab[7]])
        P.op(DVE, lambda e: e.memset(A[8][:, 19:22], 0.0), reads=[Ab[8]], writes=[Ab[8]])
        chunk_rel(C, A[8], Ab[8], A[7], Ab[7], 64)
        P.op(ACT, lambda e: e.activation(out=A[7][:, 3:TP], in_=A[8][:, 3:TP], func=AF.Exp, scale=-LDK), reads=[Ab[8]], writes=[Ab[7]])
        P.op(ACT, lambda e: e.activation(out=A[9][:, 3:TP], in_=A[8][:, 3:TP], func=AF.Exp, scale=LDK), reads=[Ab[8]], writes=[Ab[9]])
        P.op(DVE, lambda e: e.tensor_tensor(out=A[8][:, 3:TP], in0=A[8][:, 3:TP], in1=A[0][:, 3:TP], op=ALU.subtract), reads=[Ab[8], Ab[0]], writes=[Ab[8]])
        P.op(ACT, lambda e: e.activation(out=A[8][:, 3:TP], in_=A[8][:, 3:TP], func=AF.Exp, scale=-LDK), reads=[Ab[8]], writes=[Ab[8]])
        P.op(DVE, lambda e: e.tensor_tensor(out=xr[:, 3:TP], in0=xr[:, 3:TP], in1=A[7][:, 3:TP], op=ALU.mult), reads=[Ab[3], Ab[7]], writes=[Ab[3]])
        P.op(DVE, lambda e: e.tensor_tensor(out=xk[:, 3:TP], in0=xk[:, 3:TP], in1=A[9][:, 3:TP], op=ALU.mult), reads=[Ab[4], Ab[9]], writes=[Ab[4]])
        P.op(DVE, lambda e: e.tensor_tensor(out=A[1][:, 3:TP], in0=A[1][:, 3:TP], in1=A[9][:, 3:TP], op=ALU.mult), reads=[Ab[1], Ab[9]], writes=[Ab[1]])
        P.op(DVE, lambda e: e.scalar_tensor_tensor(out=A[2][:, 3:TP], in0=A[2][:, 3:TP], scalar=-1.0, in1=A[8][:, 3:TP], op0=ALU.mult, op1=ALU.mult),
             reads=[Ab[2], Ab[8]], writes=[Ab[2]])
        rt, kt_, bt, at, prod, G1 = A[3], A[4], A[1], A[2], A[6], A[7]
        rtb, ktb_, btb, atb, prodb, G1b = Ab[3], Ab[4], Ab[1], Ab[2], Ab[6], Ab[7]
        xvb = Ab[5]
        P.op(DVE, lambda e: e.memset(H[:], 0.0), writes=[Hb])
        for si, seg in enumerate(SEGS):
            if si == 1:
                P.dma(SP, Hin[:], prm["sA_in"][h], writes=[Hinb])
                P.op(DVE, lambda e: e.scalar_tensor_tensor(out=H[:], in0=H[:], scalar=flagE[0:64, 0:1], in1=Hin[:], op0=ALU.mult, op1=ALU.add),
                     reads=[Hb, Hinb, fb], writes=[Hb])
            chs_all = chunks_of(seg)
            for g0 in range(0, len(chs_all), G):
                chs = chs_all[g0:g0 + G]
                ng = len(chs)
                n = chs[0][1]
                nlev = 5 if n == 64 else 3
                for g, (c0, n, t0) in enumerate(chs):
                    p_, p_b = ps1.next()
                    for j, (src, srcb) in enumerate(((xv, xvb), (kt_, ktb_), (bt, btb))):
                        P.op(PE, lambda e, p_=p_, src=src, c0=c0, n=n, j=j: e.transpose(out=p_[0:n, j * 64:(j + 1) * 64], in_=src[:, c0:c0 + n], identity=ident[0:64, 0:64]),
                             reads=[srcb, idb], writes=[p_b])
                    P.op(ACT, lambda e, p_=p_, g=g, n=n: e.activation(out=TM[0:n, g, :, :], in_=p_[0:n, 0:192].rearrange("p (j d) -> p j d", j=3), func=AF.Copy),
                         reads=[p_b], pwrites=[TMb])
                    q_, q_b = ps1.next()
                    pairs = ((kt_, ktb_, at, atb), (kt_, ktb_, rt, rtb), (bt, btb, at, atb), (bt, btb, rt, rtb), (at, atb, bt, btb))
                    for j, (l, lb, r, rb) in enumerate(pairs):
                        P.op(PE, lambda e, q_=q_, l=l, r=r, c0=c0, n=n, j=j: e.matmul(q_[0:n, j * 64:j * 64 + n], lhsT=l[:, c0:c0 + n], rhs=r[:, c0:c0 + n], start=True, stop=True),
                             reads=[lb, rb], writes=[q_b])
                    P.op(DVE, lambda e, q_=q_, g=g, n=n: e.tensor_tensor(out=MM[0:n, g, :, 0:n], in0=q_[0:n, 0:320].rearrange("p (j d) -> p j d", j=5)[:, :, 0:n],
                                                                       in1=mask5[0:n, :, 0:n], op=ALU.mult), reads=[q_b, m5b], pwrites=[MMb])
                P.op(DVE, lambda e, ng=ng, n=n: e.tensor_tensor(out=Pm[0:n, 0:ng, 0:n], in0=MM[0:n, 0:ng, 2, 0:n],
                                                                in1=ident[0:n, 0:n].unsqueeze(1).broadcast_to([n, ng, n]), op=ALU.add),
                     reads=[MMb, idb], writes=[Pmb])
                curN = lambda g, n=n: MM[0:n, g, 2, 0:n]
                curNT = lambda g, n=n: MM[0:n, g, 4, 0:n]
                curb = MMb
                for lev in range(nlev):
                    nn, nnb = NN[lev % 2], NNb[lev % 2]
                    for g4 in range(0, ng, 4):
                        m4 = min(4, ng - g4)
                        p2, p2b = ps2.next()
                        for g in range(g4, g4 + m4):
                            gg = g - g4
                            P.op(PE, lambda e, p2=p2, gg=gg, n=n, a_=curNT(g), b_=curN(g): e.matmul(p2[0:n, gg * 128:gg * 128 + n], lhsT=a_, rhs=b_, start=True, stop=True),
                                 reads=[curb], writes=[p2b])
                            P.op(PE, lambda e, p2=p2, gg=gg, n=n, a_=curN(g), b_=curNT(g): e.matmul(p2[0:n, gg * 128 + 64:gg * 128 + 64 + n], lhsT=a_, rhs=b_, start=True, stop=True),
                                 reads=[curb], writes=[p2b])
                        P.op(ACT, lambda e, p2=p2, nn=nn, g4=g4, m4=m4, n=n: e.activation(out=nn[0:n, g4:g4 + m4, :, 0:n],
                                                                                 in_=p2[0:n, 0:m4 * 128].rearrange("p (g j d) -> p g j d", g=m4, j=2)[:, :, :, 0:n], func=AF.Copy),
                             reads=[p2b], pwrites=[nnb])
                    curN = lambda g, nn=nn, n=n: nn[0:n, g, 0, 0:n]
                    curNT = lambda g, nn=nn, n=n: nn[0:n, g, 1, 0:n]
                    curb = nnb
                    p1, p1b = ps1.next()
                    for g in range(ng):
                        P.op(PE, lambda e, p1=p1, g=g, n=n, a_=curNT(g): e.matmul(p1[0:n, g * 64:g * 64 + n], lhsT=a_, rhs=Pm[0:n, g, 0:n], start=True, stop=True),
                             reads=[curb, Pmb], writes=[p1b])
                    P.op(DVE, lambda e, p1=p1, ng=ng, n=n: e.tensor_tensor(out=Pm[0:n, 0:ng, 0:n], in0=Pm[0:n, 0:ng, 0:n],
                                                                         in1=p1[0:n, 0:ng * 64].rearrange("p (g d) -> p g d", g=ng)[:, :, 0:n], op=ALU.add),
                         reads=[p1b, Pmb], writes=[Pmb])
                if C.emit_out:
                    for g, (c0, n, t0) in enumerate(chs):
                        p_, p_b = ps1.next()
                        P.op(PE, lambda e, p_=p_, c0=c0, n=n, h=h: e.matmul(p_[0:n, 0:64], lhsT=sg[0:96, c0:c0 + n], rhs=g2[0:96, h * 64:(h + 1) * 64], start=True, stop=True),
                             reads=[sgb, pb_], writes=[p_b])
                        P.op(PE, lambda e, p_=p_, c0=c0, n=n, h=h: e.matmul(p_[0:n, 64:66], lhsT=prod[:, c0:c0 + n], rhs=rk[:, h, :], start=True, stop=True),
                             reads=[prodb, pb_], writes=[p_b])
                        P.op(ACT, lambda e, p_=p_, g=g, n=n: e.activation(out=GB[0:n, g, :], in_=p_[0:n, 0:66], func=AF.Copy), reads=[p_b], pwrites=[GBb])
                pyg, pygb = pygr.next()
                for g, (c0, n, t0) in enumerate(chs):
                    vtm = TM[0:n, g, 0, :]; ktm = TM[0:n, g, 1, :]; btm = TM[0:n, g, 2, :]
                    LakT = MM[0:n, g, 0, 0:n]; MrkT = MM[0:n, g, 1, 0:n]; MrbT = MM[0:n, g, 3, 0:n]
                    TT_ = Pm[0:n, g, 0:n]
                    pw, pwb = ps1.next()
                    P.op(PE, lambda e, pw=pw, c0=c0, n=n: e.matmul(pw[0:n, 0:64], lhsT=at[:, c0:c0 + n], rhs=H[:, :], start=True, stop=False), reads=[atb, Hb], writes=[pwb])
                    P.op(PE, lambda e, pw=pw, n=n, LakT=LakT, vtm=vtm: e.matmul(pw[0:n, 0:64], lhsT=LakT, rhs=vtm, start=False, stop=True), reads=[MMb, TMb], writes=[pwb])
                    w0, w0b = w0r.next()
                    P.op(ACT, lambda e, w0=w0, pw=pw, n=n: e.activation(out=w0[0:n, :], in_=pw[0:n, 0:64], func=AF.Copy), reads=[pwb], writes=[w0b])
                    P.op(PE, lambda e, pw=pw, n=n, TT_=TT_, w0=w0: e.matmul(pw[0:n, 64:128], lhsT=TT_, rhs=w0[0:n, :], start=True, stop=True), reads=[Pmb, w0b], writes=[pwb])
                    u, ub_ = ur.next()
                    P.op(DVE, lambda e, u=u, pw=pw, n=n: e.tensor_copy(out=u[0:n, :], in_=pw[0:n, 64:128]), reads=[pwb], writes=[ub_])
                    if C.emit_out:
                        P.op(PE, lambda e, pyg=pyg, g=g, c0=c0, n=n: e.matmul(pyg[0:n, g * 64:(g + 1) * 64], lhsT=rt[:, c0:c0 + n], rhs=H[:, :], start=True, stop=False), reads=[rtb, Hb], writes=[pygb])
                        P.op(PE, lambda e, pyg=pyg, g=g, n=n, MrbT=MrbT, u=u: e.matmul(pyg[0:n, g * 64:(g + 1) * 64], lhsT=MrbT, rhs=u[0:n, :], start=False, stop=False), reads=[MMb, ub_], writes=[pygb])
                        P.op(PE, lambda e, pyg=pyg, g=g, n=n, MrkT=MrkT, vtm=vtm: e.matmul(pyg[0:n, g * 64:(g + 1) * 64], lhsT=MrkT, rhs=vtm, start=False, stop=True), reads=[MMb, TMb], writes=[pygb])
                    ph, phb = ps1.next()
                    P.op(PE, lambda e, ph=ph: e.matmul(ph[0:64, 0:64], lhsT=ident[0:64, 0:64], rhs=H[:, :], start=True, stop=False), reads=[idb, Hb], writes=[phb])
                    P.op(PE, lambda e, ph=ph, n=n, btm=btm, u=u: e.matmul(ph[0:64, 0:64], lhsT=btm, rhs=u[0:n, :], start=False, stop=False), reads=[TMb, ub_], writes=[phb])
                    P.op(PE, lambda e, ph=ph, n=n, ktm=ktm, vtm=vtm: e.matmul(ph[0:64, 0:64], lhsT=ktm, rhs=vtm, start=False, stop=True), reads=[TMb], writes=[phb])
                    ce = c0 + n - 1
                    P.op(DVE, lambda e, ph=ph, ce=ce: e.tensor_scalar(out=H[:], in0=ph[0:64, 0:64], scalar1=G1[:, ce:ce + 1], scalar2=None, op0=ALU.mult),
                         reads=[phb, G1b], writes=[Hb])
                if C.emit_out:
                    t0g = chs[0][2]
                    YG = pyg[0:n, 0:ng * 64].rearrange("p (g d) -> p g d", g=ng)
                    bc = lambda ap, n=n, ng=ng: ap.unsqueeze(2).broadcast_to([n, ng, 64])
                    P.op(DVE, lambda e, YG=YG, n=n, ng=ng: e.tensor_reduce(out=ST[0:n, 0, 0:ng], in_=YG, axis=AX.X, op=ALU.add), reads=[pygb], writes=[STb])
                    P.op(ACT, lambda e, YG=YG, n=n, ng=ng: e.activation(out=SQ[0:n, 0:ng, :], in_=YG, func=AF.Square), reads=[pygb], writes=[SQb])
                    P.op(DVE, lambda e, n=n, ng=ng: e.tensor_reduce(out=ST[0:n, 1, 0:ng], in_=SQ[0:n, 0:ng, :], axis=AX.X, op=ALU.add), reads=[SQb, STb], writes=[STb])
                    P.op(DVE, lambda e, n=n, ng=ng: e.tensor_scalar(out=ST[0:n, 2, 0:ng], in0=ST[0:n, 0, 0:ng], scalar1=1.0 / 64, scalar2=None, op0=ALU.mult), reads=[STb], writes=[STb])
                    P.op(DVE, lambda e, n=n, ng=ng: e.tensor_tensor(out=ST[0:n, 3, 0:ng], in0=ST[0:n, 2, 0:ng], in1=ST[0:n, 2, 0:ng], op=ALU.mult), reads=[STb], writes=[STb])
                    P.op(DVE, lambda e, n=n, ng=ng: e.tensor_scalar(out=ST[0:n, 4, 0:ng], in0=ST[0:n, 1, 0:ng], scalar1=1.0 / 64, scalar2=64e-5, op0=ALU.mult, op1=ALU.add),
                         reads=[STb], writes=[STb])
                    P.op(DVE, lambda e, n=n, ng=ng: e.tensor_tensor(out=ST[0:n, 4, 0:ng], in0=ST[0:n, 4, 0:ng], in1=ST[0:n, 3, 0:ng], op=ALU.subtract), reads=[STb], writes=[STb])
                    P.op(POOL, lambda e, n=n, ng=ng: e.tensor_tensor(out=ST[0:n, 5, 0:ng], in0=ST[0:n, 4, 0:ng], in1=mh[0:n, 0:ng], op=ALU.pow), reads=[STb, mhb], writes=[STb])
                    P.op(DVE, lambda e, YG=YG, n=n, ng=ng, bc=bc: e.tensor_tensor(out=T1[0:n, 0:ng, :], in0=YG, in1=bc(ST[0:n, 2, 0:ng]), op=ALU.subtract),
                         reads=[pygb, STb], writes=[T1b])
                    P.op(DVE, lambda e, n=n, ng=ng, bc=bc: e.tensor_tensor(out=T1[0:n, 0:ng, :], in0=T1[0:n, 0:ng, :], in1=bc(ST[0:n, 5, 0:ng]), op=ALU.mult),
                         reads=[T1b, STb], writes=[T1b])
                    P.op(DVE, lambda e, n=n, ng=ng, h=h: e.tensor_tensor(out=T1[0:n, 0:ng, :], in0=T1[0:n, 0:ng, :],
                                                                      in1=lnw[0:n, h * 64:(h + 1) * 64].unsqueeze(1).broadcast_to([n, ng, 64]), op=ALU.mult),
                         reads=[T1b, pb_], writes=[T1b])
                    P.op(DVE, lambda e, n=n, ng=ng, h=h: e.tensor_tensor(out=T1[0:n, 0:ng, :], in0=T1[0:n, 0:ng, :],
                                                                      in1=lnb[0:n, h * 64:(h + 1) * 64].unsqueeze(1).broadcast_to([n, ng, 64]), op=ALU.add),
                         reads=[T1b, pb_], writes=[T1b])
                    P.op(DVE, lambda e, n=n, ng=ng: e.tensor_tensor(out=SQ[0:n, 0:ng, :], in0=TM[0:n, 0:ng, 0, :], in1=GB[0:n, 0:ng, 64:65].broadcast_to([n, ng, 64]), op=ALU.mult),
                         reads=[TMb, GBb, SQb], writes=[SQb])
                    P.op(DVE, lambda e, n=n, ng=ng: e.tensor_tensor(out=T1[0:n, 0:ng, :], in0=T1[0:n, 0:ng, :], in1=SQ[0:n, 0:ng, :], op=ALU.add), reads=[T1b, SQb], writes=[T1b])
                    P.op(DVE, lambda e, n=n, ng=ng: e.tensor_tensor(out=YO[0:n, 0:ng, :], in0=T1[0:n, 0:ng, :], in1=GB[0:n, 0:ng, 0:64], op=ALU.mult),
                         reads=[T1b, GBb], writes=[YOb])
                    P.dma(SP, y[t0g:t0g + ng * n, h * 64:(h + 1) * 64].rearrange("(g p) d -> p g d", p=n), YO[0:n, 0:ng, :], reads=[YOb], pwrites=[yb])
        P.dma(POOL, prm["sA_out"][h], H[:], reads=[Hb], pwrites=[K.sob])


def mixer_rwkv2(C, st, pf, pfb, y, yb, prm, K):
    P = C.P
    ones, onesb, ident, idb, flagE, fb = K.ones, K.onesb, K.ident, K.idb, K.flagE, K.fb
    mask5, m5b = K.mask5, K.m5b
    muA = sbuf(C, st, "bmuA", [64, 3, 8]); muL = sbuf(C, st, "bmuL", [96, 3]); w2 = sbuf(C, st, "bw2", [32, 512]); a2 = sbuf(C, st, "ba2", [32, 512])
    g2 = sbuf(C, st, "bg2", [96, 512]); ch = sbuf(C, st, "bch", [64, 5, 8]); rk = sbuf(C, st, "brk", [64, 8, 2])
    lnw = sbuf(C, st, "blnw", [64, 512]); lnb = sbuf(C, st, "blnb", [64, 512])
    pb_ = Buf()
    for t, n_ in ((muA, "rw_muA"), (muL, "rw_muL"), (w2, "rw_w2"), (a2, "rw_a2"), (g2, "rw_g2"), (rk, "rw_rk"), (lnw, "rw_lnw_bc"), (lnb, "rw_lnb_bc")):
        P.dma(SP, t[:], prm[n_], pwrites=[pb_])
    P.dma(SP, ch[:, 0:4, :], prm["rw_ch"], pwrites=[pb_])
    P.op(DVE, lambda e: e.tensor_scalar(out=ch[:, 4, :], in0=ch[:, 3, :], scalar1=-1.0, scalar2=1.0, op0=ALU.mult, op1=ALU.add), reads=[pb_], writes=[pb_])
    WM = 128
    W1M = WM + 1
    mh = sbuf(C, st, "bmh", [64, 8 * WM], F32); mhb = Buf()
    P.op(POOL, lambda e: e.memset(mh[:], -0.5), writes=[mhb])
    names = ["pr", "pk", "pv", "xr", "xk", "xv", "sgz", "asig", "kkn", "t1", "rel", "G1", "G2"]
    X = {nm: sbuf(C, st, "bX" + nm, [64, 8, W1M]) for nm in names}
    Xb = {nm: Buf() for nm in names}
    Ssc = sbuf(C, st, "bSsc", [64, 1 + 8 * WM]); Sscb = Buf()
    P.op(DVE, lambda e: e.memset(Ssc[:, 0:1], 0.0), writes=[Sscb])
    lraw = sbuf(C, st, "blraw", [96, 3, W1M]); lrawb = Buf()
    thw = sbuf(C, st, "bthw", [32, WM]); xal = sbuf(C, st, "bxal", [32, WM]); sg = sbuf(C, st, "bsg", [96, WM])
    thwb, xalb, sgb = Buf(), Buf(), Buf()
    hs = sbuf(C, st, "bhs", [96, 2, 8]); hsb = Buf()
    H = sbuf(C, st, "bH", [64, 8, 64]); Hb = Buf()
    Hin = sbuf(C, st, "bHin", [64, 8, 64]); Hinb = Buf()
    NQ = 16
    TM = sbuf(C, st, "bTM", [64, 3, NQ, 64]); TMb = Buf()
    MM = sbuf(C, st, "bMM", [64, 5, NQ, 64]); MMb = Buf()
    NN = [sbuf(C, st, f"bNN{i}", [64, NQ, 2, 64]) for i in range(2)]; NNb = [Buf(), Buf()]
    Pm = sbuf(C, st, "bPm", [64, NQ, 64]); Pmb = Buf()
    W0s = sbuf(C, st, "bW0", [64, 8, 64]); W0b = Buf()
    Us = sbuf(C, st, "bUs", [64, 8, 64]); Usb = Buf()
    GBs = sbuf(C, st, "bGB", [64, 528]); GBb = Buf()
    T1 = sbuf(C, st, "bT1", [64, 8, 64]); T1b = Buf()
    SQ = sbuf(C, st, "bSQ", [64, 8, 64]); SQb = Buf()
    YO = sbuf(C, st, "bYO", [64, 8, 64], BF16); YOb = Buf()
    ST = sbuf(C, st, "bST", [64, 6, 8]); STb = Buf()
    bank = [psum(C, st, f"bpb{i}", [128, 512], F32) for i in range(8)]
    bkb = [Buf() for _ in range(8)]
    P.op(DVE, lambda e: e.memset(H[:], 0.0), writes=[Hb])

    def bc8(ap, W):
        return ap.unsqueeze(2).broadcast_to([64, 8, W])

    def vop(fn, reads, writes, pwrites=()):
        P.op(DVE, fn, reads=reads, writes=writes, pwrites=pwrites)

    Fl = sbuf(C, st, "bFl", [64, 8 * WM]); Flb = Buf()

    scs = [(3, 16, 0, 16)] + [(22 + 128 * i, 128, 16 + 128 * i, 64) for i in range(16)]
    def do_sc(sci, c0, W, t0, n):
        W1 = W + 1
        nch = W // n
        nq = 8 * nch
        cur = lambda nm: X[nm][:, :, 1:W1]
        prev = lambda nm: X[nm][:, :, 0:W]
        P.phase = "rwkv_pre"
        for i, nm in enumerate(("pr", "pk", "pv")):
            P.dma(SP, X[nm][:, :, 0:W1], pf[i * 512:(i + 1) * 512, c0 - 1:c0 + W].rearrange("(h d) c -> d h c", d=64), reads=[pfb], writes=[Xb[nm]])
        for j, (r0, nr) in enumerate(((12 * 128, 32), (12 * 128 + 32, 32), (13 * 128, 96))):
            P.dma(SP, lraw[0:nr, j, 0:W1], pf[r0:r0 + nr, c0 - 1:c0 + W], reads=[pfb], writes=[lrawb])
        if sci == 0:
            for nm in ("pr", "pk", "pv"):
                vop(lambda e, nm=nm: e.memset(X[nm][:, :, 0:1], 0.0), [Xb[nm]], [Xb[nm]])
            vop(lambda e: e.memset(lraw[:, :, 0:1], 0.0), [lrawb], [lrawb])
        if sci == 1:
            for i, nm in enumerate(("pr", "pk", "pv")):
                P.dma(SP, hs[0:64, 0, :], pf[i * 512:(i + 1) * 512, 18:19].rearrange("(h d) c -> d (h c)", d=64), reads=[pfb], writes=[hsb], allow_slow_non_contiguous=True)
                P.dma(SP, hs[0:64, 1, :], prm["hist_in"][i * 512:(i + 1) * 512, 2:3].rearrange("(h d) c -> d (h c)", d=64), reads=[hsb], writes=[hsb], allow_slow_non_contiguous=True)
                vop(lambda e, nm=nm: e.scalar_tensor_tensor(out=X[nm][:, :, 0:1], in0=hs[0:64, 0, :].unsqueeze(2), scalar=flagE[0:64, 0:1], in1=hs[0:64, 1, :].unsqueeze(2),
                                                            op0=ALU.mult, op1=ALU.add), [hsb, fb, Xb[nm]], [Xb[nm]])
            for j, (r0, nr) in enumerate(((12 * 128, 32), (12 * 128 + 32, 32), (13 * 128, 96))):
                P.dma(SP, hs[0:nr, 0, 0:1], pf[r0:r0 + nr, 18:19], reads=[pfb, hsb], writes=[hsb], allow_slow_non_contiguous=True)
                P.dma(SP, hs[0:nr, 1, 0:1], prm["hist_in"][r0:r0 + nr, 2:3], reads=[hsb], writes=[hsb], allow_slow_non_contiguous=True)
                vop(lambda e, j=j, nr=nr: e.scalar_tensor_tensor(out=lraw[0:nr, j, 0:1], in0=hs[0:nr, 0, 0:1], scalar=flagE[0:nr, 0:1], in1=hs[0:nr, 1, 0:1],
                                                                 op0=ALU.mult, op1=ALU.add), [hsb, fb, lrawb], [lrawb])
            P.dma(SP, Hin[:], prm["sA_in"].rearrange("h k v -> k h v"), writes=[Hinb])
            vop(lambda e: e.scalar_tensor_tensor(out=H[:], in0=H[:], scalar=flagE[0:64, 0:1], in1=Hin[:], op0=ALU.mult, op1=ALU.add), [Hb, Hinb, fb], [Hb])
        for i, (src, dst) in enumerate((("pr", "xr"), ("pk", "xk"), ("pv", "xv"))):
            vop(lambda e, src=src, dst=dst: e.tensor_tensor(out=cur(dst), in0=prev(src), in1=cur(src), op=ALU.subtract), [Xb[src]], [Xb[dst]])
            vop(lambda e, dst=dst, i=i: e.tensor_tensor(out=cur(dst), in0=cur(dst), in1=bc8(muA[:, i, :], W), op=ALU.mult), [Xb[dst], pb_], [Xb[dst]])
            vop(lambda e, src=src, dst=dst: e.tensor_tensor(out=cur(dst), in0=cur(dst), in1=cur(src), op=ALU.add), [Xb[dst], Xb[src]], [Xb[dst]])
        for j, (dst, dstb, nr, fn) in enumerate(((thw, thwb, 32, AF.Tanh), (xal, xalb, 32, None), (sg, sgb, 96, AF.Sigmoid))):
            vop(lambda e, dst=dst, nr=nr, j=j: e.tensor_tensor(out=dst[0:nr, 0:W], in0=lraw[0:nr, j, 0:W], in1=lraw[0:nr, j, 1:W1], op=ALU.subtract), [lrawb], [dstb])
            vop(lambda e, dst=dst, nr=nr, j=j: e.scalar_tensor_tensor(out=dst[0:nr, 0:W], in0=dst[0:nr, 0:W], scalar=muL[0:nr, j:j + 1], in1=lraw[0:nr, j, 1:W1],
                                                                    op0=ALU.mult, op1=ALU.add), [lrawb, dstb, pb_], [dstb])
            if fn is not None:
                P.op(ACT, lambda e, dst=dst, nr=nr, fn=fn: e.activation(out=dst[0:nr, 0:W], in_=dst[0:nr, 0:W], func=fn), reads=[dstb], writes=[dstb])
        for (wt_, src, srcb, dst, chi, b0) in ((w2, thw, thwb, "sgz", 0, 0), (a2, xal, xalb, "asig", 1, 2)):
            for h in range(8):
                bk = b0 + (h * W) // 512
                off = (h * W) % 512
                P.op(PE, lambda e, bk=bk, off=off, wt_=wt_, src=src, h=h: e.matmul(bank[bk][0:64, off:off + W], lhsT=wt_[0:32, h * 64:(h + 1) * 64], rhs=src[0:32, 0:W],
                                                                                  start=True, stop=True), reads=[pb_, srcb], writes=[bkb[bk]])
            nb = (8 * W + 511) // 512
            for b in range(nb):
                h0 = b * (512 // W) if W >= 64 else 0
                nh = (512 // W) if W >= 64 else 8
                vop(lambda e, b=b, b0=b0, dst=dst, chi=chi, h0=h0, nh=nh: e.tensor_tensor(
                    out=X[dst][:, h0:h0 + nh, 1:W1], in0=bank[b0 + b][0:64, 0:nh * W].rearrange("p (h w) -> p h w", h=nh),
                    in1=ch[:, chi, h0:h0 + nh].unsqueeze(2).broadcast_to([64, nh, W]), op=ALU.add), [bkb[b0 + b], pb_], [Xb[dst]])
            P.op(ACT, lambda e, dst=dst: e.activation(out=cur(dst), in_=cur(dst), func=AF.Sigmoid), reads=[Xb[dst]], writes=[Xb[dst]])
        vop(lambda e: e.tensor_tensor(out=cur("kkn"), in0=cur("xk"), in1=bc8(ch[:, 2, :], W), op=ALU.mult), [Xb["xk"], pb_], [Xb["kkn"]])
        vop(lambda e: e.tensor_tensor(out=Fl[:, 0:8 * W].rearrange("p (h w) -> p h w", h=8), in0=cur("kkn"), in1=cur("kkn"), op=ALU.mult), [Xb["kkn"]], [Flb])
        nb = (8 * W + 511) // 512
        for b in range(nb):
            nn_ = min(512, 8 * W - b * 512)
            P.op(PE, lambda e, b=b, nn_=nn_: e.matmul(bank[4 + b][0:64, 0:nn_], lhsT=ones[0:64, 0:64], rhs=Fl[:, b * 512:b * 512 + nn_], start=True, stop=True),
                 reads=[onesb, Flb], writes=[bkb[4 + b]])
        for b in range(nb):
            nn_ = min(512, 8 * W - b * 512)
            vop(lambda e, b=b, nn_=nn_: e.tensor_scalar(out=Fl[:, b * 512:b * 512 + nn_], in0=bank[4 + b][0:64, 0:nn_],
                                                        scalar1=1e-24, scalar2=None, op0=ALU.max), [bkb[4 + b], Flb], [Flb])
        relf = Fl[:, 0:8 * W]
        P.op(POOL, lambda e, relf=relf: e.tensor_tensor(out=relf, in0=relf, in1=mh[:, 0:8 * W], op=ALU.pow), reads=[Flb, mhb], writes=[Flb])
        vop(lambda e, relf=relf: e.tensor_tensor(out=cur("kkn"), in0=cur("kkn"), in1=relf.rearrange("p (h w) -> p h w", h=8), op=ALU.mult),
            [Xb["kkn"], Flb], [Xb["kkn"]])
        vop(lambda e: e.tensor_tensor(out=cur("t1"), in0=cur("asig"), in1=bc8(ch[:, 3, :], W), op=ALU.mult), [Xb["asig"], pb_], [Xb["t1"]])
        vop(lambda e: e.tensor_tensor(out=cur("t1"), in0=cur("t1"), in1=bc8(ch[:, 4, :], W), op=ALU.add), [Xb["t1"], pb_], [Xb["t1"]])
        vop(lambda e: e.tensor_tensor(out=cur("xk"), in0=cur("xk"), in1=cur("t1"), op=ALU.mult), [Xb["xk"], Xb["t1"]], [Xb["xk"]])
        vop(lambda e: e.tensor_tensor(out=cur("asig"), in0=cur("asig"), in1=cur("kkn"), op=ALU.mult), [Xb["asig"], Xb["kkn"]], [Xb["asig"]])
        vop(lambda e: e.tensor_tensor(out=cur("t1"), in0=cur("xr"), in1=cur("xk"), op=ALU.mult), [Xb["xr"], Xb["xk"], Xb["t1"]], [Xb["t1"]])
        vop(lambda e: e.tensor_tensor(out=cur("pr"), in0=cur("t1"), in1=bc8(rk[:, :, 0], W), op=ALU.mult), [Xb["t1"], pb_, Xb["pr"], Xb["xr"]], [Xb["pr"]])
        vop(lambda e: e.tensor_copy(out=Fl[:, 0:8 * W].rearrange("p (h w) -> p h w", h=8), in_=cur("sgz")), [Xb["sgz"], Flb], [Flb])
        vop(lambda e: e.tensor_tensor_scan(out=Ssc[:, 1:1 + 8 * W], data0=ones[0:64, 0:8 * W], data1=Fl[:, 0:8 * W], initial=0.0, op0=ALU.mult, op1=ALU.add),
            [Flb, Sscb, onesb], [Sscb])
        vop(lambda e: e.tensor_tensor(out=cur("rel").rearrange("p h (c j) -> p h c j", j=n),
                                      in0=Ssc[:, 1:1 + 8 * W].rearrange("p (h c j) -> p h c j", h=8, j=n),
                                      in1=Ssc[:, 0:8 * W].rearrange("p (h c j) -> p h c j", h=8, j=n)[:, :, :, 0:1].broadcast_to([64, 8, nch, n]), op=ALU.subtract),
            [Sscb, Xb["rel"]], [Xb["rel"]])
        P.op(ACT, lambda e: e.activation(out=cur("G1"), in_=cur("rel"), func=AF.Exp, scale=-LDK), reads=[Xb["rel"]], writes=[Xb["G1"]])
        P.op(ACT, lambda e: e.activation(out=cur("G2"), in_=cur("rel"), func=AF.Exp, scale=LDK), reads=[Xb["rel"]], writes=[Xb["G2"]])
        vop(lambda e: e.tensor_tensor(out=cur("rel"), in0=cur("rel"), in1=cur("sgz"), op=ALU.subtract), [Xb["rel"], Xb["sgz"]], [Xb["rel"]])
        P.op(ACT, lambda e: e.activation(out=cur("rel"), in_=cur("rel"), func=AF.Exp, scale=-LDK), reads=[Xb["rel"]], writes=[Xb["rel"]])
        vop(lambda e: e.tensor_tensor(out=cur("xr"), in0=cur("xr"), in1=cur("G1"), op=ALU.mult), [Xb["xr"], Xb["G1"]], [Xb["xr"]])
        vop(lambda e: e.tensor_tensor(out=cur("xk"), in0=cur("xk"), in1=cur("G2"), op=ALU.mult), [Xb["xk"], Xb["G2"]], [Xb["xk"]])
        vop(lambda e: e.tensor_tensor(out=cur("asig"), in0=cur("asig"), in1=cur("G2"), op=ALU.mult), [Xb["asig"], Xb["G2"]], [Xb["asig"]])
        vop(lambda e: e.scalar_tensor_tensor(out=cur("kkn"), in0=cur("kkn"), scalar=-1.0, in1=cur("rel"), op0=ALU.mult, op1=ALU.mult),
            [Xb["kkn"], Xb["rel"]], [Xb["kkn"]])
        RT, KT, BT, AT, XV, PRK, G1 = "xr", "xk", "asig", "kkn", "xv", "pr", "G1"
        col = lambda nm, h, c: X[nm][:, h, 1 + c * n:1 + (c + 1) * n]
        qi = lambda h, c: h * nch + c
        P.phase = "rwkv_gram"
        for a, nm in enumerate((XV, KT, BT)):
            for h in range(8):
                for c in range(nch):
                    q = qi(h, c)
                    bk, off = (q * 64) // 512, (q * 64) % 512
                    P.op(PE, lambda e, bk=bk, off=off, nm=nm, h=h, c=c: e.transpose(out=bank[bk][0:n, off:off + 64], in_=col(nm, h, c), identity=ident[0:64, 0:64]),
                         reads=[Xb[nm], idb], writes=[bkb[bk]])
            for b in range((nq * 64 + 511) // 512):
                qn = min(8, nq - b * 8)
                P.op(ACT, lambda e, a=a, b=b, qn=qn: e.activation(out=TM[0:n, a, b * 8:b * 8 + qn, :], in_=bank[b][0:n, 0:qn * 64].rearrange("p (q d) -> p q d", q=qn),
                                                               func=AF.Copy), reads=[bkb[b]], pwrites=[TMb])
        pairs = ((KT, AT), (KT, RT), (BT, AT), (BT, RT), (AT, BT))
        for j, (l_, r_) in enumerate(pairs):
            b0 = 4 if j % 2 else 0
            for h in range(8):
                for c in range(nch):
                    q = qi(h, c)
                    bk, off = b0 + (q * 64) // 512, (q * 64) % 512
                    P.op(PE, lambda e, bk=bk, off=off, l_=l_, r_=r_, h=h, c=c: e.matmul(bank[bk][0:n, off:off + n], lhsT=col(l_, h, c), rhs=col(r_, h, c), start=True, stop=True),
                         reads=[Xb[l_], Xb[r_]], writes=[bkb[bk]])
            for b in range((nq * 64 + 511) // 512):
                qn = min(8, nq - b * 8)
                vop(lambda e, j=j, b=b, b0=b0, qn=qn: e.tensor_tensor(out=MM[0:n, j, b * 8:b * 8 + qn, 0:n],
                                                                    in0=bank[b0 + b][0:n, 0:qn * 64].rearrange("p (q d) -> p q d", q=qn)[:, :, 0:n],
                                                                    in1=mask5[0:n, j, 0:n].unsqueeze(1).broadcast_to([n, qn, n]), op=ALU.mult),
                    [bkb[b0 + b], m5b], [], pwrites=[MMb])
        P.phase = "rwkv_inv"
        vop(lambda e: e.tensor_tensor(out=Pm[0:n, 0:nq, 0:n], in0=MM[0:n, 2, 0:nq, 0:n], in1=ident[0:n, 0:n].unsqueeze(1).broadcast_to([n, nq, n]), op=ALU.add),
            [MMb, idb], [Pmb])
        curN = lambda q: MM[0:n, 2, q, 0:n]
        curNT = lambda q: MM[0:n, 4, q, 0:n]
        curb = MMb
        nlev = 5 if n == 64 else 3
        for lev in range(nlev):
            nn, nnb = NN[lev % 2], NNb[lev % 2]
            for q in range(nq):
                bk, off = (q * 128) // 512, (q * 128) % 512
                P.op(PE, lambda e, bk=bk, off=off, a_=curNT(q), b_=curN(q): e.matmul(bank[bk][0:n, off:off + n], lhsT=a_, rhs=b_, start=True, stop=True), reads=[curb], writes=[bkb[bk]])
                P.op(PE, lambda e, bk=bk, off=off, a_=curN(q), b_=curNT(q): e.matmul(bank[bk][0:n, off + 64:off + 64 + n], lhsT=a_, rhs=b_, start=True, stop=True), reads=[curb], writes=[bkb[bk]])
            for b in range((nq * 128 + 511) // 512):
                qn = min(4, nq - b * 4)
                P.op(ACT, lambda e, nn=nn, b=b, qn=qn: e.activation(out=nn[0:n, b * 4:b * 4 + qn, :, 0:n],
                                                                 in_=bank[b][0:n, 0:qn * 128].rearrange("p (q j d) -> p q j d", q=qn, j=2)[:, :, :, 0:n], func=AF.Copy),
                     reads=[bkb[b]], pwrites=[nnb])
            curN = lambda q, nn=nn: nn[0:n, q, 0, 0:n]
            curNT = lambda q, nn=nn: nn[0:n, q, 1, 0:n]
            curb = nnb
            for q in range(nq):
                bk, off = 4 + (q * 64) // 512, (q * 64) % 512
                P.op(PE, lambda e, bk=bk, off=off, a_=curNT(q), q=q: e.matmul(bank[bk][0:n, off:off + n], lhsT=a_, rhs=Pm[0:n, q, 0:n], start=True, stop=True), reads=[curb, Pmb], writes=[bkb[bk]])
            for b in range((nq * 64 + 511) // 512):
                qn = min(8, nq - b * 8)
                vop(lambda e, b=b, qn=qn: e.tensor_tensor(out=Pm[0:n, b * 8:b * 8 + qn, 0:n], in0=Pm[0:n, b * 8:b * 8 + qn, 0:n],
                                                          in1=bank[4 + b][0:n, 0:qn * 64].rearrange("p (q d) -> p q d", q=qn)[:, :, 0:n], op=ALU.add),
                    [bkb[4 + b], Pmb], [Pmb])
        P.phase = "rwkv_chain"
        for c in range(nch):
            tc0 = t0 + c * n
            for h in range(8):
                q = qi(h, c)
                P.op(PE, lambda e, h=h, c=c: e.matmul(bank[0][0:n, h * 64:(h + 1) * 64], lhsT=col(AT, h, c), rhs=H[:, h, :], start=True, stop=False), reads=[Xb[AT], Hb], writes=[bkb[0]])
                P.op(PE, lambda e, h=h, q=q: e.matmul(bank[0][0:n, h * 64:(h + 1) * 64], lhsT=MM[0:n, 0, q, 0:n], rhs=TM[0:n, 0, q, :], start=False, stop=True), reads=[MMb, TMb], writes=[bkb[0]])
            P.op(ACT, lambda e: e.activation(out=W0s[0:n, :, :], in_=bank[0][0:n, 0:512].rearrange("p (h d) -> p h d", h=8), func=AF.Copy), reads=[bkb[0]], writes=[W0b])
            for h in range(8):
                q = qi(h, c)
                P.op(PE, lambda e, h=h, q=q: e.matmul(bank[1][0:n, h * 64:(h + 1) * 64], lhsT=Pm[0:n, q, 0:n], rhs=W0s[0:n, h, :], start=True, stop=True), reads=[Pmb, W0b], writes=[bkb[1]])
            vop(lambda e: e.tensor_copy(out=Us[0:n, :, :], in_=bank[1][0:n, 0:512].rearrange("p (h d) -> p h d", h=8)), [bkb[1]], [Usb])
            if C.emit_out:
                for h in range(8):
                    q = qi(h, c)
                    P.op(PE, lambda e, h=h, c=c: e.matmul(bank[2][0:n, h * 64:(h + 1) * 64], lhsT=col(RT, h, c), rhs=H[:, h, :], start=True, stop=False), reads=[Xb[RT], Hb], writes=[bkb[2]])
                    P.op(PE, lambda e, h=h, q=q: e.matmul(bank[2][0:n, h * 64:(h + 1) * 64], lhsT=MM[0:n, 3, q, 0:n], rhs=Us[0:n, h, :], start=False, stop=False), reads=[MMb, Usb], writes=[bkb[2]])
                    P.op(PE, lambda e, h=h, q=q: e.matmul(bank[2][0:n, h * 64:(h + 1) * 64], lhsT=MM[0:n, 1, q, 0:n], rhs=TM[0:n, 0, q, :], start=False, stop=True), reads=[MMb, TMb], writes=[bkb[2]])
            for h in range(8):
                q = qi(h, c)
                P.op(PE, lambda e, h=h: e.matmul(bank[3][0:64, h * 64:(h + 1) * 64], lhsT=ident[0:64, 0:64], rhs=H[:, h, :], start=True, stop=False), reads=[idb, Hb], writes=[bkb[3]])
                P.op(PE, lambda e, h=h, q=q: e.matmul(bank[3][0:64, h * 64:(h + 1) * 64], lhsT=TM[0:n, 2, q, :], rhs=Us[0:n, h, :], start=False, stop=False), reads=[TMb, Usb], writes=[bkb[3]])
                P.op(PE, lambda e, h=h, q=q: e.matmul(bank[3][0:64, h * 64:(h + 1) * 64], lhsT=TM[0:n, 1, q, :], rhs=TM[0:n, 0, q, :], start=False, stop=True), reads=[TMb], writes=[bkb[3]])
            ce = 1 + (c + 1) * n - 1
            vop(lambda e, ce=ce: e.tensor_tensor(out=H[:], in0=bank[3][0:64, 0:512].rearrange("p (h d) -> p h d", h=8),
                                                 in1=X[G1][:, :, ce:ce + 1].broadcast_to([64, 8, 64]), op=ALU.mult), [bkb[3], Xb[G1]], [Hb])
            if C.emit_out:
                P.op(PE, lambda e, c=c: e.matmul(bank[4][0:n, 0:512], lhsT=sg[0:96, c * n:(c + 1) * n], rhs=g2[0:96, :], start=True, stop=True), reads=[sgb, pb_], writes=[bkb[4]])
                for h in range(8):
                    P.op(PE, lambda e, h=h, c=c: e.matmul(bank[5][0:n, 2 * h:2 * h + 2], lhsT=col(PRK, h, c), rhs=ones[0:64, 0:2], start=True, stop=True), reads=[Xb[PRK], onesb], writes=[bkb[5]])
                P.op(ACT, lambda e: e.activation(out=GBs[0:n, 0:512], in_=bank[4][0:n, 0:512], func=AF.Copy), reads=[bkb[4]], writes=[GBb])
                P.op(ACT, lambda e: e.activation(out=GBs[0:n, 512:528], in_=bank[5][0:n, 0:16], func=AF.Copy), reads=[bkb[5], GBb], writes=[GBb])
                YG = bank[2][0:n, 0:512].rearrange("p (h d) -> p h d", h=8)
                bcn = lambda ap: ap.unsqueeze(2).broadcast_to([n, 8, 64])
                vop(lambda e, YG=YG: e.tensor_reduce(out=ST[0:n, 0, :], in_=YG, axis=AX.X, op=ALU.add), [bkb[2]], [STb])
                P.op(ACT, lambda e, YG=YG: e.activation(out=SQ[0:n, :, :], in_=YG, func=AF.Square), reads=[bkb[2]], writes=[SQb])
                vop(lambda e: e.tensor_reduce(out=ST[0:n, 1, :], in_=SQ[0:n, :, :], axis=AX.X, op=ALU.add), [SQb, STb], [STb])
                vop(lambda e: e.tensor_scalar(out=ST[0:n, 2, :], in0=ST[0:n, 0, :], scalar1=1.0 / 64, scalar2=None, op0=ALU.mult), [STb], [STb])
                vop(lambda e: e.tensor_tensor(out=ST[0:n, 3, :], in0=ST[0:n, 2, :], in1=ST[0:n, 2, :], op=ALU.mult), [STb], [STb])
                vop(lambda e: e.tensor_scalar(out=ST[0:n, 4, :], in0=ST[0:n, 1, :], scalar1=1.0 / 64, scalar2=64e-5, op0=ALU.mult, op1=ALU.add), [STb], [STb])
                vop(lambda e: e.tensor_tensor(out=ST[0:n, 4, :], in0=ST[0:n, 4, :], in1=ST[0:n, 3, :], op=ALU.subtract), [STb], [STb])
                P.op(POOL, lambda e: e.tensor_tensor(out=ST[0:n, 5, :], in0=ST[0:n, 4, :], in1=mh[0:n, 0:8], op=ALU.pow), reads=[STb, mhb], writes=[STb])
                vop(lambda e, YG=YG, bcn=bcn: e.tensor_tensor(out=T1[0:n, :, :], in0=YG, in1=bcn(ST[0:n, 2, :]), op=ALU.subtract), [bkb[2], STb], [T1b])
                vop(lambda e, bcn=bcn: e.tensor_tensor(out=T1[0:n, :, :], in0=T1[0:n, :, :], in1=bcn(ST[0:n, 5, :]), op=ALU.mult), [T1b, STb], [T1b])
                vop(lambda e: e.tensor_tensor(out=T1[0:n, :, :], in0=T1[0:n, :, :], in1=lnw[0:n, :].rearrange("p (h d) -> p h d", h=8), op=ALU.mult), [T1b, pb_], [T1b])
                vop(lambda e: e.tensor_tensor(out=T1[0:n, :, :], in0=T1[0:n, :, :], in1=lnb[0:n, :].rearrange("p (h d) -> p h d", h=8), op=ALU.add), [T1b, pb_], [T1b])
                vtm_c = TM[0:n, 0, 0:nq, :].rearrange("p (h c) d -> p h c d", c=nch)[:, :, c, :]
                bs_c = GBs[0:n, 512:528].rearrange("p (h t) -> p h t", t=2)[:, :, 0:1].broadcast_to([n, 8, 64])
                vop(lambda e, vtm_c=vtm_c, bs_c=bs_c: e.tensor_tensor(out=SQ[0:n, :, :], in0=vtm_c, in1=bs_c, op=ALU.mult), [TMb, GBb, SQb], [SQb])
                vop(lambda e: e.tensor_tensor(out=T1[0:n, :, :], in0=T1[0:n, :, :], in1=SQ[0:n, :, :], op=ALU.add), [T1b, SQb], [T1b])
                vop(lambda e: e.tensor_tensor(out=YO[0:n, :, :], in0=T1[0:n, :, :], in1=GBs[0:n, 0:512].rearrange("p (h d) -> p h d", h=8), op=ALU.mult), [T1b, GBb], [YOb])
                P.dma(SP, y[tc0:tc0 + n, 0:512], YO[0:n, :, :].rearrange("p h d -> p (h d)"), reads=[YOb], pwrites=[yb])
    for sci, (c0, W, t0, n) in enumerate(scs):
        do_sc(sci, c0, W, t0, n)
    P.dma(POOL, prm["sA_out"].rearrange("h k v -> k h v"), H[:], reads=[Hb], pwrites=[K.sob])


def mixer_rwkv3(C, st, pf, pfb, y, yb, prm, K):
    P = C.P
    ones, onesb, ident, idb, flagE, fb = K.ones, K.onesb, K.ident, K.idb, K.flagE, K.fb
    mask5, m5b = K.mask5, K.m5b
    muA = sbuf(C, st, "cmuA", [64, 3, 8]); muL = sbuf(C, st, "cmuL", [96, 3]); w2 = sbuf(C, st, "cw2", [32, 512]); a2 = sbuf(C, st, "ca2", [32, 512])
    g2 = sbuf(C, st, "cg2", [96, 512]); ch = sbuf(C, st, "cch", [64, 5, 8]); rk = sbuf(C, st, "crk", [64, 8, 2])
    lnw = sbuf(C, st, "clnw", [64, 512]); lnb = sbuf(C, st, "clnb", [64, 512])
    pb_ = Buf()
    for t, n_ in ((muA, "rw_muA"), (muL, "rw_muL"), (w2, "rw_w2"), (a2, "rw_a2"), (g2, "rw_g2"), (rk, "rw_rk"), (lnw, "rw_lnw_bc"), (lnb, "rw_lnb_bc")):
        P.dma(SP, t[:], prm[n_], pwrites=[pb_])
    P.dma(SP, ch[:, 0:4, :], prm["rw_ch"], pwrites=[pb_])
    P.op(DVE, lambda e: e.tensor_scalar(out=ch[:, 4, :], in0=ch[:, 3, :], scalar1=-1.0, scalar2=1.0, op0=ALU.mult, op1=ALU.add), reads=[pb_], writes=[pb_])
    WM = 128
    W1M = WM + 1
    mh = sbuf(C, st, "cmh", [64, 8 * WM], F32); mhb = Buf()
    P.op(POOL, lambda e: e.memset(mh[:], -0.5), writes=[mhb])
    names = ["pr", "pk", "pv", "xr", "xk", "xv", "sgz", "asig", "kkn", "t1", "rel", "G1", "G2"]
    X = {nm: sbuf(C, st, "cX" + nm, [64, 8, W1M]) for nm in names}
    Xb = {nm: Buf() for nm in names}
    Ssc = sbuf(C, st, "cSsc", [64, 1 + 8 * WM]); Sscb = Buf()
    P.op(DVE, lambda e: e.memset(Ssc[:, 0:1], 0.0), writes=[Sscb])
    lraw = sbuf(C, st, "clraw", [96, 3, W1M]); lrawb = Buf()
    thw = sbuf(C, st, "cthw", [32, WM]); xal = sbuf(C, st, "cxal", [32, WM]); sg = sbuf(C, st, "csg", [96, WM])
    thwb, xalb, sgb = Buf(), Buf(), Buf()
    hs = sbuf(C, st, "chs", [96, 2, 8]); hsb = Buf()
    H = sbuf(C, st, "cH", [64, 8, 64]); Hb = Buf()
    Hin = sbuf(C, st, "cHin", [64, 8, 64]); Hinb = Buf()
    NQ = 16
    TM = sbuf(C, st, "cTM", [64, 3, NQ, 64], BF16); TMb = Buf()
    MM = sbuf(C, st, "cMM", [64, 5, NQ, 64], BF16); MMb = Buf()
    NN = [sbuf(C, st, f"cNN{i}", [64, NQ, 2, 64], BF16) for i in range(2)]; NNb = [Buf(), Buf()]
    Pm = sbuf(C, st, "cPm", [64, NQ, 64], BF16); Pmb = Buf()
    W0s = sbuf(C, st, "cW0", [64, 8, 64], BF16); W0b = Buf()
    Us = sbuf(C, st, "cUs", [64, 8, 64], BF16); Usb = Buf()
    GBs = sbuf(C, st, "cGB", [64, 528]); GBb = Buf()
    T1 = sbuf(C, st, "cT1", [64, 8, 64]); T1b = Buf()
    SQ = sbuf(C, st, "cSQ", [64, 8, 64]); SQb = Buf()
    YO = sbuf(C, st, "cYO", [64, 8, 64], BF16); YOb = Buf()
    ST = sbuf(C, st, "cST", [64, 6, 8]); STb = Buf()
    bank = [psum(C, st, f"cpb{i}", [128, 512], F32) for i in range(8)]
    bkb = [Buf() for _ in range(8)]
    P.op(DVE, lambda e: e.memset(H[:], 0.0), writes=[Hb])
    H16 = sbuf(C, st, "cH16", [64, 8, 64], BF16); H16b = Buf()
    P.op(DVE, lambda e: e.memset(H16[:], 0.0), writes=[H16b])
    XBF = {nm: sbuf(C, st, "cXB" + nm, [64, 8, W1M], BF16) for nm in ("xr", "xk", "asig", "kkn")}
    XBFb = {nm: Buf() for nm in XBF}

    def bc8(ap, W):
        return ap.unsqueeze(2).broadcast_to([64, 8, W])

    def vop(fn, reads, writes, pwrites=()):
        P.op(DVE, fn, reads=reads, writes=writes, pwrites=pwrites)

    Fl = sbuf(C, st, "cFl", [64, 8 * WM]); Flb = Buf()

    scs = [(3, 16, 0, 16)] + [(22 + 128 * i, 128, 16 + 128 * i, 64) for i in range(16)]
    def do_sc(sci, c0, W, t0, n):
        W1 = W + 1
        nch = W // n
        nq = 8 * nch
        cur = lambda nm: X[nm][:, :, 1:W1]
        prev = lambda nm: X[nm][:, :, 0:W]
        P.phase = "rwkv_pre"
        for i, nm in enumerate(("pr", "pk", "pv")):
            P.dma(SP, X[nm][:, :, 0:W1], pf[i * 512:(i + 1) * 512, c0 - 1:c0 + W].rearrange("(h d) c -> d h c", d=64), reads=[pfb], writes=[Xb[nm]])
        for j, (r0, nr) in enumerate(((12 * 128, 32), (12 * 128 + 32, 32), (13 * 128, 96))):
            P.dma(SP, lraw[0:nr, j, 0:W1], pf[r0:r0 + nr, c0 - 1:c0 + W], reads=[pfb], writes=[lrawb])
        if sci == 0:
            for nm in ("pr", "pk", "pv"):
                vop(lambda e, nm=nm: e.memset(X[nm][:, :, 0:1], 0.0), [Xb[nm]], [Xb[nm]])
            vop(lambda e: e.memset(lraw[:, :, 0:1], 0.0), [lrawb], [lrawb])
        if sci == 1:
            for i, nm in enumerate(("pr", "pk", "pv")):
                P.dma(SP, hs[0:64, 0, :], pf[i * 512:(i + 1) * 512, 18:19].rearrange("(h d) c -> d (h c)", d=64), reads=[pfb], writes=[hsb], allow_slow_non_contiguous=True)
                P.dma(SP, hs[0:64, 1, :], prm["hist_in"][i * 512:(i + 1) * 512, 2:3].rearrange("(h d) c -> d (h c)", d=64), reads=[hsb], writes=[hsb], allow_slow_non_contiguous=True)
                vop(lambda e, nm=nm: e.scalar_tensor_tensor(out=X[nm][:, :, 0:1], in0=hs[0:64, 0, :].unsqueeze(2), scalar=flagE[0:64, 0:1], in1=hs[0:64, 1, :].unsqueeze(2),
                                                            op0=ALU.mult, op1=ALU.add), [hsb, fb, Xb[nm]], [Xb[nm]])
            for j, (r0, nr) in enumerate(((12 * 128, 32), (12 * 128 + 32, 32), (13 * 128, 96))):
                P.dma(SP, hs[0:nr, 0, 0:1], pf[r0:r0 + nr, 18:19], reads=[pfb, hsb], writes=[hsb], allow_slow_non_contiguous=True)
                P.dma(SP, hs[0:nr, 1, 0:1], prm["hist_in"][r0:r0 + nr, 2:3], reads=[hsb], writes=[hsb], allow_slow_non_contiguous=True)
                vop(lambda e, j=j, nr=nr: e.scalar_tensor_tensor(out=lraw[0:nr, j, 0:1], in0=hs[0:nr, 0, 0:1], scalar=flagE[0:nr, 0:1], in1=hs[0:nr, 1, 0:1],
                                                                 op0=ALU.mult, op1=ALU.add), [hsb, fb, lrawb], [lrawb])
            P.dma(SP, Hin[:], prm["sA_in"].rearrange("h k v -> k h v"), writes=[Hinb])
            vop(lambda e: e.scalar_tensor_tensor(out=H[:], in0=H[:], scalar=flagE[0:64, 0:1], in1=Hin[:], op0=ALU.mult, op1=ALU.add), [Hb, Hinb, fb], [Hb])
            P.op(ACT, lambda e: e.activation(out=H16[:], in_=H[:], func=AF.Copy), reads=[Hb], writes=[H16b])
        for i, (src, dst) in enumerate((("pr", "xr"), ("pk", "xk"), ("pv", "xv"))):
            vop(lambda e, src=src, dst=dst: e.tensor_tensor(out=cur(dst), in0=prev(src), in1=cur(src), op=ALU.subtract), [Xb[src]], [Xb[dst]])
            vop(lambda e, dst=dst, i=i: e.tensor_tensor(out=cur(dst), in0=cur(dst), in1=bc8(muA[:, i, :], W), op=ALU.mult), [Xb[dst], pb_], [Xb[dst]])
            vop(lambda e, src=src, dst=dst: e.tensor_tensor(out=cur(dst), in0=cur(dst), in1=cur(src), op=ALU.add), [Xb[dst], Xb[src]], [Xb[dst]])
        for j, (dst, dstb, nr, fn) in enumerate(((thw, thwb, 32, AF.Tanh), (xal, xalb, 32, None), (sg, sgb, 96, AF.Sigmoid))):
            vop(lambda e, dst=dst, nr=nr, j=j: e.tensor_tensor(out=dst[0:nr, 0:W], in0=lraw[0:nr, j, 0:W], in1=lraw[0:nr, j, 1:W1], op=ALU.subtract), [lrawb], [dstb])
            vop(lambda e, dst=dst, nr=nr, j=j: e.scalar_tensor_tensor(out=dst[0:nr, 0:W], in0=dst[0:nr, 0:W], scalar=muL[0:nr, j:j + 1], in1=lraw[0:nr, j, 1:W1],
                                                                    op0=ALU.mult, op1=ALU.add), [lrawb, dstb, pb_], [dstb])
            if fn is not None:
                P.op(ACT, lambda e, dst=dst, nr=nr, fn=fn: e.activation(out=dst[0:nr, 0:W], in_=dst[0:nr, 0:W], func=fn), reads=[dstb], writes=[dstb])
        for (wt_, src, srcb, dst, chi, b0) in ((w2, thw, thwb, "sgz", 0, 0), (a2, xal, xalb, "asig", 1, 2)):
            for h in range(8):
                bk = b0 + (h * W) // 512
                off = (h * W) % 512
                P.op(PE, lambda e, bk=bk, off=off, wt_=wt_, src=src, h=h: e.matmul(bank[bk][0:64, off:off + W], lhsT=wt_[0:32, h * 64:(h + 1) * 64], rhs=src[0:32, 0:W],
                                                                                  start=True, stop=True), reads=[pb_, srcb], writes=[bkb[bk]])
            nb = (8 * W + 511) // 512
            for b in range(nb):
                h0 = b * (512 // W) if W >= 64 else 0
                nh = (512 // W) if W >= 64 else 8
                vop(lambda e, b=b, b0=b0, dst=dst, chi=chi, h0=h0, nh=nh: e.tensor_tensor(
                    out=X[dst][:, h0:h0 + nh, 1:W1], in0=bank[b0 + b][0:64, 0:nh * W].rearrange("p (h w) -> p h w", h=nh),
                    in1=ch[:, chi, h0:h0 + nh].unsqueeze(2).broadcast_to([64, nh, W]), op=ALU.add), [bkb[b0 + b], pb_], [Xb[dst]])
            P.op(ACT, lambda e, dst=dst: e.activation(out=cur(dst), in_=cur(dst), func=AF.Sigmoid), reads=[Xb[dst]], writes=[Xb[dst]])
        vop(lambda e: e.tensor_tensor(out=cur("kkn"), in0=cur("xk"), in1=bc8(ch[:, 2, :], W), op=ALU.mult), [Xb["xk"], pb_], [Xb["kkn"]])
        vop(lambda e: e.tensor_tensor(out=Fl[:, 0:8 * W].rearrange("p (h w) -> p h w", h=8), in0=cur("kkn"), in1=cur("kkn"), op=ALU.mult), [Xb["kkn"]], [Flb])
        nb = (8 * W + 511) // 512
        for b in range(nb):
            nn_ = min(512, 8 * W - b * 512)
            P.op(PE, lambda e, b=b, nn_=nn_: e.matmul(bank[4 + b][0:64, 0:nn_], lhsT=ones[0:64, 0:64], rhs=Fl[:, b * 512:b * 512 + nn_], start=True, stop=True),
                 reads=[onesb, Flb], writes=[bkb[4 + b]])
        for b in range(nb):
            nn_ = min(512, 8 * W - b * 512)
            vop(lambda e, b=b, nn_=nn_: e.tensor_scalar(out=Fl[:, b * 512:b * 512 + nn_], in0=bank[4 + b][0:64, 0:nn_],
                                                        scalar1=1e-24, scalar2=None, op0=ALU.max), [bkb[4 + b], Flb], [Flb])
        relf = Fl[:, 0:8 * W]
        P.op(ACT, lambda e, relf=relf: e.activation(out=relf, in_=relf, func=AF.Sqrt), reads=[Flb], writes=[Flb])
        vop(lambda e, relf=relf: e.reciprocal(out=relf, in_=relf), [Flb], [Flb])
        vop(lambda e, relf=relf: e.tensor_tensor(out=cur("kkn"), in0=cur("kkn"), in1=relf.rearrange("p (h w) -> p h w", h=8), op=ALU.mult),
            [Xb["kkn"], Flb], [Xb["kkn"]])
        vop(lambda e: e.tensor_tensor(out=cur("t1"), in0=cur("asig"), in1=bc8(ch[:, 3, :], W), op=ALU.mult), [Xb["asig"], pb_], [Xb["t1"]])
        vop(lambda e: e.tensor_tensor(out=cur("t1"), in0=cur("t1"), in1=bc8(ch[:, 4, :], W), op=ALU.add), [Xb["t1"], pb_], [Xb["t1"]])
        vop(lambda e: e.tensor_tensor(out=cur("xk"), in0=cur("xk"), in1=cur("t1"), op=ALU.mult), [Xb["xk"], Xb["t1"]], [Xb["xk"]])
        vop(lambda e: e.tensor_tensor(out=cur("asig"), in0=cur("asig"), in1=cur("kkn"), op=ALU.mult), [Xb["asig"], Xb["kkn"]], [Xb["asig"]])
        vop(lambda e: e.tensor_tensor(out=cur("t1"), in0=cur("xr"), in1=cur("xk"), op=ALU.mult), [Xb["xr"], Xb["xk"], Xb["t1"]], [Xb["t1"]])
        vop(lambda e: e.tensor_tensor(out=cur("pr"), in0=cur("t1"), in1=bc8(rk[:, :, 0], W), op=ALU.mult), [Xb["t1"], pb_, Xb["pr"], Xb["xr"]], [Xb["pr"]])
        vop(lambda e: e.tensor_copy(out=Fl[:, 0:8 * W].rearrange("p (h w) -> p h w", h=8), in_=cur("sgz")), [Xb["sgz"], Flb], [Flb])
        vop(lambda e: e.tensor_tensor_scan(out=Ssc[:, 1:1 + 8 * W], data0=ones[0:64, 0:8 * W], data1=Fl[:, 0:8 * W], initial=0.0, op0=ALU.mult, op1=ALU.add),
            [Flb, Sscb, onesb], [Sscb])
        vop(lambda e: e.tensor_tensor(out=cur("rel").rearrange("p h (c j) -> p h c j", j=n),
                                      in0=Ssc[:, 1:1 + 8 * W].rearrange("p (h c j) -> p h c j", h=8, j=n),
                                      in1=Ssc[:, 0:8 * W].rearrange("p (h c j) -> p h c j", h=8, j=n)[:, :, :, 0:1].broadcast_to([64, 8, nch, n]), op=ALU.subtract),
            [Sscb, Xb["rel"]], [Xb["rel"]])
        P.op(ACT, lambda e: e.activation(out=cur("G1"), in_=cur("rel"), func=AF.Exp, scale=-LDK), reads=[Xb["rel"]], writes=[Xb["G1"]])
        P.op(ACT, lambda e: e.activation(out=cur("G2"), in_=cur("rel"), func=AF.Exp, scale=LDK), reads=[Xb["rel"]], writes=[Xb["G2"]])
        vop(lambda e: e.tensor_tensor(out=cur("rel"), in0=cur("rel"), in1=cur("sgz"), op=ALU.subtract), [Xb["rel"], Xb["sgz"]], [Xb["rel"]])
        P.op(ACT, lambda e: e.activation(out=cur("rel"), in_=cur("rel"), func=AF.Exp, scale=-LDK), reads=[Xb["rel"]], writes=[Xb["rel"]])
        vop(lambda e: e.tensor_tensor(out=cur("xr"), in0=cur("xr"), in1=cur("G1"), op=ALU.mult), [Xb["xr"], Xb["G1"]], [Xb["xr"]])
        vop(lambda e: e.tensor_tensor(out=cur("xk"), in0=cur("xk"), in1=cur("G2"), op=ALU.mult), [Xb["xk"], Xb["G2"]], [Xb["xk"]])
        vop(lambda e: e.tensor_tensor(out=cur("asig"), in0=cur("asig"), in1=cur("G2"), op=ALU.mult), [Xb["asig"], Xb["G2"]], [Xb["asig"]])
        vop(lambda e: e.scalar_tensor_tensor(out=cur("kkn"), in0=cur("kkn"), scalar=-1.0, in1=cur("rel"), op0=ALU.mult, op1=ALU.mult),
            [Xb["kkn"], Xb["rel"]], [Xb["kkn"]])
        for nm in ("xr", "xk", "asig", "kkn"):
            P.op(ACT, lambda e, nm=nm: e.activation(out=XBF[nm][:, :, 1:W1], in_=cur(nm), func=AF.Copy), reads=[Xb[nm]], writes=[XBFb[nm]])
        RT, KT, BT, AT, XV, PRK, G1 = "xr", "xk", "asig", "kkn", "xv", "pr", "G1"
        colb = lambda nm, h, c: XBF[nm][:, h, 1 + c * n:1 + (c + 1) * n]
        col = lambda nm, h, c: X[nm][:, h, 1 + c * n:1 + (c + 1) * n]
        qi = lambda h, c: h * nch + c
        P.phase = "rwkv_gram"
        for a, nm in enumerate((XV, KT, BT)):
            for h in range(8):
                for c in range(nch):
                    q = qi(h, c)
                    bk, off = (q * 64) // 512, (q * 64) % 512
                    P.op(PE, lambda e, bk=bk, off=off, nm=nm, h=h, c=c: e.transpose(out=bank[bk][0:n, off:off + 64], in_=col(nm, h, c), identity=ident[0:64, 0:64]),
                         reads=[Xb[nm], idb], writes=[bkb[bk]])
            for b in range((nq * 64 + 511) // 512):
                qn = min(8, nq - b * 8)
                P.op(ACT, lambda e, a=a, b=b, qn=qn: e.activation(out=TM[0:n, a, b * 8:b * 8 + qn, :], in_=bank[b][0:n, 0:qn * 64].rearrange("p (q d) -> p q d", q=qn),
                                                               func=AF.Copy), reads=[bkb[b]], pwrites=[TMb])
        pairs = ((KT, AT), (KT, RT), (BT, AT), (BT, RT), (AT, BT))
        for j, (l_, r_) in enumerate(pairs):
            b0 = 4 if j % 2 else 0
            for h in range(8):
                for c in range(nch):
                    q = qi(h, c)
                    bk, off = b0 + (q * 64) // 512, (q * 64) % 512
                    P.op(PE, lambda e, bk=bk, off=off, l_=l_, r_=r_, h=h, c=c: e.matmul(bank[bk][0:n, off:off + n], lhsT=colb(l_, h, c), rhs=colb(r_, h, c), start=True, stop=True),
                         reads=[XBFb[l_], XBFb[r_]], writes=[bkb[bk]])
            for b in range((nq * 64 + 511) // 512):
                qn = min(8, nq - b * 8)
                vop(lambda e, j=j, b=b, b0=b0, qn=qn: e.tensor_tensor(out=MM[0:n, j, b * 8:b * 8 + qn, 0:n],
                                                                    in0=bank[b0 + b][0:n, 0:qn * 64].rearrange("p (q d) -> p q d", q=qn)[:, :, 0:n],
                                                                    in1=mask5[0:n, j, 0:n].unsqueeze(1).broadcast_to([n, qn, n]), op=ALU.mult),
                    [bkb[b0 + b], m5b], [], pwrites=[MMb])
        P.phase = "rwkv_inv"
        vop(lambda e: e.tensor_tensor(out=Pm[0:n, 0:nq, 0:n], in0=MM[0:n, 2, 0:nq, 0:n], in1=ident[0:n, 0:n].unsqueeze(1).broadcast_to([n, nq, n]), op=ALU.add),
            [MMb, idb], [Pmb])
        curN = lambda q: MM[0:n, 2, q, 0:n]
        curNT = lambda q: MM[0:n, 4, q, 0:n]
        curb = MMb
        nlev = 5 if n == 64 else 3
        for lev in range(nlev):
            nn, nnb = NN[lev % 2], NNb[lev % 2]
            for q in range(nq):
                bk, off = (q * 128) // 512, (q * 128) % 512
                P.op(PE, lambda e, bk=bk, off=off, a_=curNT(q), b_=curN(q): e.matmul(bank[bk][0:n, off:off + n], lhsT=a_, rhs=b_, start=True, stop=True), reads=[curb], writes=[bkb[bk]])
                P.op(PE, lambda e, bk=bk, off=off, a_=curN(q), b_=curNT(q): e.matmul(bank[bk][0:n, off + 64:off + 64 + n], lhsT=a_, rhs=b_, start=True, stop=True), reads=[curb], writes=[bkb[bk]])
            for b in range((nq * 128 + 511) // 512):
                qn = min(4, nq - b * 4)
                P.op(ACT, lambda e, nn=nn, b=b, qn=qn: e.activation(out=nn[0:n, b * 4:b * 4 + qn, :, 0:n],
                                                                 in_=bank[b][0:n, 0:qn * 128].rearrange("p (q j d) -> p q j d", q=qn, j=2)[:, :, :, 0:n], func=AF.Copy),
                     reads=[bkb[b]], pwrites=[nnb])
            curN = lambda q, nn=nn: nn[0:n, q, 0, 0:n]
            curNT = lambda q, nn=nn: nn[0:n, q, 1, 0:n]
            curb = nnb
            for q in range(nq):
                bk, off = 4 + (q * 64) // 512, (q * 64) % 512
                P.op(PE, lambda e, bk=bk, off=off, a_=curNT(q), q=q: e.matmul(bank[bk][0:n, off:off + n], lhsT=a_, rhs=Pm[0:n, q, 0:n], start=True, stop=True), reads=[curb, Pmb], writes=[bkb[bk]])
            for b in range((nq * 64 + 511) // 512):
                qn = min(8, nq - b * 8)
                vop(lambda e, b=b, qn=qn: e.tensor_tensor(out=Pm[0:n, b * 8:b * 8 + qn, 0:n], in0=Pm[0:n, b * 8:b * 8 + qn, 0:n],
                                                          in1=bank[4 + b][0:n, 0:qn * 64].rearrange("p (q d) -> p q d", q=qn)[:, :, 0:n], op=ALU.add),
                    [bkb[4 + b], Pmb], [Pmb])
        P.phase = "rwkv_chain"
        for c in range(nch):
            tc0 = t0 + c * n
            for h in range(8):
                q = qi(h, c)
                P.op(PE, lambda e, h=h, c=c: e.matmul(bank[0][0:n, h * 64:(h + 1) * 64], lhsT=colb(AT, h, c), rhs=H16[:, h, :], start=True, stop=False), reads=[XBFb[AT], H16b], writes=[bkb[0]])
                P.op(PE, lambda e, h=h, q=q: e.matmul(bank[0][0:n, h * 64:(h + 1) * 64], lhsT=MM[0:n, 0, q, 0:n], rhs=TM[0:n, 0, q, :], start=False, stop=True), reads=[MMb, TMb], writes=[bkb[0]])
            P.op(ACT, lambda e: e.activation(out=W0s[0:n, :, :], in_=bank[0][0:n, 0:512].rearrange("p (h d) -> p h d", h=8), func=AF.Copy), reads=[bkb[0]], writes=[W0b])
            for h in range(8):
                q = qi(h, c)
                P.op(PE, lambda e, h=h, q=q: e.matmul(bank[1][0:n, h * 64:(h + 1) * 64], lhsT=Pm[0:n, q, 0:n], rhs=W0s[0:n, h, :], start=True, stop=True), reads=[Pmb, W0b], writes=[bkb[1]])
            vop(lambda e: e.tensor_copy(out=Us[0:n, :, :], in_=bank[1][0:n, 0:512].rearrange("p (h d) -> p h d", h=8)), [bkb[1]], [Usb])
            if C.emit_out:
                for h in range(8):
                    q = qi(h, c)
                    P.op(PE, lambda e, h=h, c=c: e.matmul(bank[2][0:n, h * 64:(h + 1) * 64], lhsT=colb(RT, h, c), rhs=H16[:, h, :], start=True, stop=False), reads=[XBFb[RT], H16b], writes=[bkb[2]])
                    P.op(PE, lambda e, h=h, q=q: e.matmul(bank[2][0:n, h * 64:(h + 1) * 64], lhsT=MM[0:n, 3, q, 0:n], rhs=Us[0:n, h, :], start=False, stop=False), reads=[MMb, Usb], writes=[bkb[2]])
                    P.op(PE, lambda e, h=h, q=q: e.matmul(bank[2][0:n, h * 64:(h + 1) * 64], lhsT=MM[0:n, 1, q, 0:n], rhs=TM[0:n, 0, q, :], start=False, stop=True), reads=[MMb, TMb], writes=[bkb[2]])
            for h in range(8):
                q = qi(h, c)
                P.op(PE, lambda e, h=h, q=q: e.matmul(bank[3][0:64, h * 64:(h + 1) * 64], lhsT=TM[0:n, 2, q, :], rhs=Us[0:n, h, :], start=True, stop=False), reads=[TMb, Usb], writes=[bkb[3]])
                P.op(PE, lambda e, h=h, q=q: e.matmul(bank[3][0:64, h * 64:(h + 1) * 64], lhsT=TM[0:n, 1, q, :], rhs=TM[0:n, 0, q, :], start=False, stop=True), reads=[TMb], writes=[bkb[3]])
            ce = 1 + (c + 1) * n - 1
            vop(lambda e: e.tensor_tensor(out=H[:], in0=H[:], in1=bank[3][0:64, 0:512].rearrange("p (h d) -> p h d", h=8), op=ALU.add), [bkb[3], Hb], [Hb])
            vop(lambda e, ce=ce: e.tensor_tensor(out=H[:], in0=H[:], in1=X[G1][:, :, ce:ce + 1].broadcast_to([64, 8, 64]), op=ALU.mult), [Hb, Xb[G1]], [Hb])
            P.op(ACT, lambda e: e.activation(out=H16[:], in_=H[:], func=AF.Copy), reads=[Hb], writes=[H16b])
            if C.emit_out:
                P.op(PE, lambda e, c=c: e.matmul(bank[4][0:n, 0:512], lhsT=sg[0:96, c * n:(c + 1) * n], rhs=g2[0:96, :], start=True, stop=True), reads=[sgb, pb_], writes=[bkb[4]])
                for h in range(8):
                    P.op(PE, lambda e, h=h, c=c: e.matmul(bank[5][0:n, 2 * h:2 * h + 2], lhsT=col(PRK, h, c), rhs=ones[0:64, 0:2], start=True, stop=True), reads=[Xb[PRK], onesb], writes=[bkb[5]])
                P.op(ACT, lambda e: e.activation(out=GBs[0:n, 0:512], in_=bank[4][0:n, 0:512], func=AF.Copy), reads=[bkb[4]], writes=[GBb])
                P.op(ACT, lambda e: e.activation(out=GBs[0:n, 512:528], in_=bank[5][0:n, 0:16], func=AF.Copy), reads=[bkb[5], GBb], writes=[GBb])
                YG = bank[2][0:n, 0:512].rearrange("p (h d) -> p h d", h=8)
                bcn = lambda ap: ap.unsqueeze(2).broadcast_to([n, 8, 64])
                vop(lambda e, YG=YG: e.tensor_reduce(out=ST[0:n, 0, :], in_=YG, axis=AX.X, op=ALU.add), [bkb[2]], [STb])
                P.op(ACT, lambda e, YG=YG: e.activation(out=SQ[0:n, :, :], in_=YG, func=AF.Square), reads=[bkb[2]], writes=[SQb])
                vop(lambda e: e.tensor_reduce(out=ST[0:n, 1, :], in_=SQ[0:n, :, :], axis=AX.X, op=ALU.add), [SQb, STb], [STb])
                vop(lambda e: e.tensor_scalar(out=ST[0:n, 2, :], in0=ST[0:n, 0, :], scalar1=1.0 / 64, scalar2=None, op0=ALU.mult), [STb], [STb])
                vop(lambda e: e.tensor_tensor(out=ST[0:n, 3, :], in0=ST[0:n, 2, :], in1=ST[0:n, 2, :], op=ALU.mult), [STb], [STb])
                vop(lambda e: e.tensor_scalar(out=ST[0:n, 4, :], in0=ST[0:n, 1, :], scalar1=1.0 / 64, scalar2=64e-5, op0=ALU.mult, op1=ALU.add), [STb], [STb])
                vop(lambda e: e.tensor_tensor(out=ST[0:n, 4, :], in0=ST[0:n, 4, :], in1=ST[0:n, 3, :], op=ALU.subtract), [STb], [STb])
                P.op(POOL, lambda e: e.tensor_tensor(out=ST[0:n, 5, :], in0=ST[0:n, 4, :], in1=mh[0:n, 0:8], op=ALU.pow), reads=[STb, mhb], writes=[STb])
                vop(lambda e, YG=YG, bcn=bcn: e.tensor_tensor(out=T1[0:n, :, :], in0=YG, in1=bcn(ST[0:n, 2, :]), op=ALU.subtract), [bkb[2], STb], [T1b])
                vop(lambda e, bcn=bcn: e.tensor_tensor(out=T1[0:n, :, :], in0=T1[0:n, :, :], in1=bcn(ST[0:n, 5, :]), op=ALU.mult), [T1b, STb], [T1b])
                vop(lambda e: e.tensor_tensor(out=T1[0:n, :, :], in0=T1[0:n, :, :], in1=lnw[0:n, :].rearrange("p (h d) -> p h d", h=8), op=ALU.mult), [T1b, pb_], [T1b])
                vop(lambda e: e.tensor_tensor(out=T1[0:n, :, :], in0=T1[0:n, :, :], in1=lnb[0:n, :].rearrange("p (h d) -> p h d", h=8), op=ALU.add), [T1b, pb_], [T1b])
                vtm_c = TM[0:n, 0, 0:nq, :].rearrange("p (h c) d -> p h c d", c=nch)[:, :, c, :]
                bs_c = GBs[0:n, 512:528].rearrange("p (h t) -> p h t", t=2)[:, :, 0:1].broadcast_to([n, 8, 64])
                vop(lambda e, vtm_c=vtm_c, bs_c=bs_c: e.tensor_tensor(out=SQ[0:n, :, :], in0=vtm_c, in1=bs_c, op=ALU.mult), [TMb, GBb, SQb], [SQb])
                vop(lambda e: e.tensor_tensor(out=T1[0:n, :, :], in0=T1[0:n, :, :], in1=SQ[0:n, :, :], op=ALU.add), [T1b, SQb], [T1b])
                vop(lambda e: e.tensor_tensor(out=YO[0:n, :, :], in0=T1[0:n, :, :], in1=GBs[0:n, 0:512].rearrange("p (h d) -> p h d", h=8), op=ALU.mult), [T1b, GBb], [YOb])
                P.dma(SP, y[tc0:tc0 + n, 0:512], YO[0:n, :, :].rearrange("p h d -> p (h d)"), reads=[YOb], pwrites=[yb])
    for sci, (c0, W, t0, n) in enumerate(scs):
        do_sc(sci, c0, W, t0, n)
    P.dma(POOL, prm["sA_out"].rearrange("h k v -> k h v"), H[:], reads=[Hb], pwrites=[K.sob])


import contextlib
import numpy as np

PRM_SHAPES = {
    "gla_a2": [16, 256], "gla_ab": [64, 4], "gla_normbc": [64, 128],
    "ml_cw": [128, 8, 4], "ml_cb": [128, 8], "ml_ib": [4, 1], "ml_fb": [4, 1], "ml_normbc": [64, 1024], "onehot": [4, 4, 128],
    "rw_muA": [64, 3, 8], "rw_muL": [96, 3], "rw_w2": [32, 512], "rw_a2": [32, 512], "rw_g2": [96, 512], "rw_ch": [64, 4, 8],
    "rw_rk": [64, 8, 2], "rw_lnw_bc": [64, 512], "rw_lnb_bc": [64, 512],
    "sA_in": [8, 64, 64], "sB_in": [4, 64, 128], "sC_in": [4, 128, 257], "mC_in": [4, 1], "hist_in": [NFMB * 128, 3],
    "flagE": [128, 1], "mask_i": [64, 64], "mask5": [64, 5, 64],
}
OUT_SHAPES = {"sA_out": [8, 64, 64], "sB_out": [4, 64, 128], "sC_out": [4, 128, 257], "mC_out": [4, 1], "hist_out": [NFMB * 128, 3]}


def host_consts():
    j = np.arange(64)
    mi = (j[None, :] >= j[:, None]).astype(np.float32)
    ms = (j[None, :] > j[:, None]).astype(np.float32)
    ml = (j[None, :] < j[:, None]).astype(np.float32)
    mask5 = np.stack([ms, mi, ms, mi, ml], 1)
    oh = np.zeros((4, 4, 128), np.float32)
    for h in range(4):
        oh[h, h, :] = 1.0
    return {"mask_i": mi, "mask5": np.ascontiguousarray(mask5), "onehot": oh}


def host_layer_params(z, l):
    f = lambda a: np.ascontiguousarray(a, dtype=np.float32)
    chT = lambda v: f(v.reshape(8, 64).T)
    mu = z["rw_mu"][l]
    d = {}
    d["gla_a2"] = f(z["gla_a2"][l]); d["gla_ab"] = f(z["gla_ab"][l].reshape(4, 64).T)
    d["gla_normbc"] = f(np.broadcast_to(z["gla_norm"][l], (64, 128)))
    cw = z["ml_conv_w"][l]
    d["ml_cw"] = f(cw.reshape(4, 8, 128).transpose(2, 1, 0)); d["ml_cb"] = f(z["ml_conv_b"][l].reshape(8, 128).T)
    d["ml_ib"] = f(z["ml_ib"][l].reshape(4, 1)); d["ml_fb"] = f(z["ml_fb"][l].reshape(4, 1))
    d["ml_normbc"] = f(np.broadcast_to(z["ml_norm"][l], (64, 1024)))
    d["rw_muA"] = f(np.stack([chT(mu[0:512]), chT(mu[512:1024]), chT(mu[1024:1536])], 1))
    muL = np.zeros((96, 3), np.float32); muL[0:32, 0] = mu[1536:1568]; muL[0:32, 1] = mu[1568:1600]; muL[0:96, 2] = mu[1600:1696]
    d["rw_muL"] = muL
    d["rw_w2"] = f(z["rw_w2"][l]); d["rw_a2"] = f(z["rw_a2"][l]); d["rw_g2"] = f(z["rw_g2"][l])
    d["rw_ch"] = f(np.stack([chT(z["rw_w0"][l]), chT(z["rw_a0"][l]), chT(z["rw_kk"][l]), chT(z["rw_ka"][l])], 1))
    rk = z["rw_rk"][l]
    d["rw_rk"] = f(np.stack([rk.T, rk.T], 2))
    d["rw_lnw_bc"] = f(np.broadcast_to(z["rw_ln_w"][l], (64, 512))); d["rw_lnb_bc"] = f(np.broadcast_to(z["rw_ln_b"][l], (64, 512)))
    return d


def host_layer_weights(z, l):
    return {"win": hp.prep_win(z["w_in"][l]), "wout": hp.prep_sq(z["w_out"][l], 4), "w1": hp.prep_sq(z["ffn_w1"][l], 11),
            "w3": hp.prep_sq(z["ffn_w3"][l], 11), "w2": hp.prep_w2(z["ffn_w2"][l]), "g1": hp.gT(z["norm_mix"][l]), "g2": hp.gT(z["norm_ffn"][l])}


def build_layer(debug=False, emit_out=True, do_final=True):
    nc = bass.Bass("TRN2", target_bir_lowering=False)
    C = Ctx(); C.nc = nc; C.P = Prog(nc, same_engine_sync=True); C.emit_out = emit_out
    C.P.scopes = False
    P = C.P
    dr = lambda n, s, dt=F32, kind="ExternalInput": nc.dram_tensor(n, s, dt, kind=kind).ap()
    hin = dr("hin", [NTOK, D])
    win = dr("win", [13, 128, 8192]); wout = dr("wout", [4, 128, 8192])
    w1 = dr("w1", [11, 128, 8192]); w3 = dr("w3", [11, 128, 8192]); w2 = dr("w2", [4, 4, 128, 11 * 512])
    g1 = dr("g1", [128, 16]); g2 = dr("g2", [128, 16]); gf = dr("gf", [128, D])
    prm = {k: dr(k, s) for k, s in PRM_SHAPES.items()}
    for k, s in OUT_SHAPES.items():
        prm[k] = dr(k, s, F32, "ExternalOutput")
    dk = "ExternalOutput" if debug else "Internal"
    pf = dr("pf", [NFMB * 128, TP], F32, dk)
    pt = dr("pt", [NTOK, NTMC], F32, dk)
    y = dr("y", [NTOK, D], BF16, dk)
    hmid = dr("hmid", [NTOK, D], F32, dk)
    hout = dr("hout", [NTOK, D], F32, "ExternalOutput")
    out = dr("out", [NTOK - 16, D], F32, "ExternalOutput")
    aT = dr("aT", [5, 128, 44, 512], BF16, "Internal")
    hb, pfb, ptb, yb, hmb, hob, ob, ab = [Buf() for _ in range(8)]
    K = Ctx(); K.sob = Buf()
    with contextlib.ExitStack() as st0:
        K.ident = sbuf(C, st0, "ident", [128, 128], F32); identb = sbuf(C, st0, "identb", [128, 128], BF16)
        g1t = sbuf(C, st0, "g1t", [128, 16]); g2t = sbuf(C, st0, "g2t", [128, 16])
        K.idb, idbb, g1b, g2b = [Buf() for _ in range(4)]
        P.op(POOL, lambda e: e.memset(K.ident[:], 1.0), writes=[K.idb])
        P.op(POOL, lambda e: e.affine_select(out=K.ident[:], in_=K.ident[:], pattern=[[-1, 128]], base=0, channel_multiplier=1,
                                             compare_op=ALU.is_equal, fill=0.0), reads=[K.idb], writes=[K.idb])
        P.op(POOL, lambda e: e.tensor_copy(out=identb[:], in_=K.ident[:]), reads=[K.idb], writes=[idbb])
        P.dma(SP, g1t[:], g1, writes=[g1b]); P.dma(SP, g2t[:], g2, writes=[g2b])
        with contextlib.ExitStack() as st1:
            uT = sbuf(C, st1, "uT", [128, 16, NTOK], BF16); ub = Buf()
            pst = Ring([psum(C, st1, f"pst{i}", [128, 1024], BF16) for i in range(2)])
            psm = Ring([psum(C, st1, f"psm{i}", [128, 512], F32) for i in range(6)])
            with contextlib.ExitStack() as st:
                P.phase = "norm"
                phase_norm(C, st, hin, hb, g1t, g1b, uT, ub, pst, identb, idbb)
            P.barrier()
            with contextlib.ExitStack() as st:
                P.phase = "proj"
                phase_proj(C, st, uT, ub, win, pf, pfb, pt, ptb, psm, prm["hist_out"], K.sob)
            P.barrier()
        with contextlib.ExitStack() as st1:
            K.ones = sbuf(C, st1, "ones", [64, TP]); K.onesb = Buf()
            K.mask_i = sbuf(C, st1, "mask_i", [64, 64]); K.mib = Buf()
            K.mask5 = sbuf(C, st1, "mask5", [64, 5, 64]); K.m5b = Buf()
            K.flagE = sbuf(C, st1, "flagE", [128, 1]); K.fb = Buf()
            P.op(POOL, lambda e: e.memset(K.ones[:], 1.0), writes=[K.onesb])
            P.dma(SP, K.mask_i[:], prm["mask_i"], writes=[K.mib]); P.dma(SP, K.mask5[:], prm["mask5"], writes=[K.m5b])
            P.dma(SP, K.flagE[:], prm["flagE"], writes=[K.fb])
            with contextlib.ExitStack() as st:
                P.phase = "prepass"
                gate_prepass(C, st, pt, ptb)
            P.barrier()
            if True:
              with contextlib.ExitStack() as st:
                P.phase = "gla"
                mixer_gla(C, st, pf, pfb, pt, ptb, y, yb, prm, K)
            P.barrier()
            if True:
              with contextlib.ExitStack() as st:
                P.phase = "mlstm"
                mixer_mlstm(C, st, pf, pfb, pt, ptb, y, yb, prm, K)
            P.barrier()
            if True:
              with contextlib.ExitStack() as st:
                P.phase = "rwkv"
                mixer_rwkv3(C, st, pf, pfb, y, yb, prm, K)
            P.barrier()
        if emit_out:
            with contextlib.ExitStack() as st1:
                uT = sbuf(C, st1, "uT2", [128, 16, NTOK], BF16); ub = Buf()
                pst = Ring([psum(C, st1, f"pst{i}", [128, 1024], BF16) for i in range(2)])
                psm = Ring([psum(C, st1, f"psm{i}", [128, 512], F32) for i in range(6)])
                with contextlib.ExitStack() as st:
                    P.phase = "wout"
                    phase_wout(C, st, y, yb, hin, hb, hmid, hmb, wout, uT, ub, psm, pst, identb, idbb)
                P.barrier()
                with contextlib.ExitStack() as st:
                    P.phase = "norm"
                    phase_norm(C, st, hmid, hmb, g2t, g2b, uT, ub, pst, identb, idbb)
                P.barrier()
                with contextlib.ExitStack() as st:
                    P.phase = "ffn1"
                    phase_ffn1(C, st, uT, ub, w1, w3, aT, ab, psm)
                P.barrier()
    if emit_out:
        with contextlib.ExitStack() as st:
            psm = Ring([psum(C, st, f"psn{i}", [128, 512], F32) for i in range(6)])
            P.phase = "ffn2"
            phase_ffn2(C, st, aT, ab, w2, hmid, hmb, hout, hob, psm)
        P.barrier()
        if do_final:
            with contextlib.ExitStack() as st:
                gft = sbuf(C, st, "gft2", [128, D]); gfb = Buf()
                P.dma(SP, gft[:], gf, writes=[gfb])
                P.phase = "final"
                phase_final_norm(C, st, hout, hob, gft, gfb, out, ob)
    fin = [K.sob, hob, ob]
    if debug:
        fin += [pfb, ptb, yb, hmb]
    P.finish(fin)
    P.emit()
    C.counts = {e: (len(P.ops[e]), sum(1 for o in P.ops[e] if o.signal)) for e in ENGS}
    return nc, C


import contextlib
import numpy as np

LAYER_KEYS = ["gla_a2", "gla_ab", "gla_normbc", "ml_cw", "ml_cb", "ml_ib", "ml_fb", "ml_normbc", "rw_muA", "rw_muL", "rw_w2", "rw_a2",
              "rw_g2", "rw_ch", "rw_rk", "rw_lnw_bc", "rw_lnb_bc"]
STATE_KEYS = ["sA", "sB", "sC", "mC", "hist"]


def emit_half(C, K, T, l, half):
    P = C.P
    hin, hinb = T["hin"][(l, half)]
    hout, houtb = T["hout"][(l, half)]
    prm = {k: T["lp"][k][l] for k in LAYER_KEYS}
    for k in ("mask_i", "mask5", "onehot"):
        prm[k] = T["const"][k]
    prm["flagE"] = T["flag1"] if half == 0 else T["flag0"]
    for k in STATE_KEYS:
        prm[k + "_in"] = T["zstate"][k] if half == 0 else T["state"][k][l]
        prm[k + "_out"] = T["state"][k][l] if half == 0 else T["sdump"][k]
    K.sob = T["stateb"][l] if half == 0 else T["sdumpb"]
    K.fb = Buf()
    win, wout, w1, w3, w2 = T["win"][l], T["wout"][l], T["w1"][l], T["w3"][l], T["w2"][l]
    pf, pfb, pt, ptb, y, yb, hmid, hmb, aT, ab = T["pf"], T["pfb"], T["pt"], T["ptb"], T["y"], T["yb"], T["hmid"], T["hmb"], T["aT"], T["ab"]
    identb, idbb = K.identb, K.idbb
    with contextlib.ExitStack() as st1:
        uT = sbuf(C, st1, "uT", [128, 16, NTOK], BF16); ub = Buf()
        pst = Ring([psum(C, st1, f"pst{i}", [128, 1024], BF16) for i in range(2)])
        psm = Ring([psum(C, st1, f"psm{i}", [128, 512], F32) for i in range(6)])
        with contextlib.ExitStack() as st:
            phase_norm(C, st, hin, hinb, K.g1t[l], K.g1b, uT, ub, pst, identb, idbb)
        P.barrier()
        with contextlib.ExitStack() as st:
            phase_proj(C, st, uT, ub, win, pf, pfb, pt, ptb, psm, prm["hist_out"], K.sob)
        P.barrier()
    with contextlib.ExitStack() as st1:
        K.flagE = sbuf(C, st1, "flagE", [128, 1])
        P.dma(SP, K.flagE[:], prm["flagE"], writes=[K.fb])
        K.ones = sbuf(C, st1, "ones", [64, TP]); K.onesb = Buf()
        K.mask_i = sbuf(C, st1, "mask_i", [64, 64]); K.mib = Buf()
        K.mask5 = sbuf(C, st1, "mask5", [64, 5, 64]); K.m5b = Buf()
        P.op(POOL, lambda e: e.memset(K.ones[:], 1.0), writes=[K.onesb])
        P.dma(SP, K.mask_i[:], T["const"]["mask_i"], writes=[K.mib]); P.dma(SP, K.mask5[:], T["const"]["mask5"], writes=[K.m5b])
        with contextlib.ExitStack() as st:
            gate_prepass(C, st, pt, ptb)
        P.barrier()
        with contextlib.ExitStack() as st:
            mixer_gla(C, st, pf, pfb, pt, ptb, y, yb, prm, K)
        P.barrier()
        with contextlib.ExitStack() as st:
            mixer_mlstm(C, st, pf, pfb, pt, ptb, y, yb, prm, K)
        P.barrier()
        with contextlib.ExitStack() as st:
            mixer_rwkv3(C, st, pf, pfb, y, yb, prm, K)
        P.barrier()
    with contextlib.ExitStack() as st1:
        uT = sbuf(C, st1, "uT2", [128, 16, NTOK], BF16); ub = Buf()
        pst = Ring([psum(C, st1, f"pst{i}", [128, 1024], BF16) for i in range(2)])
        psm = Ring([psum(C, st1, f"psm{i}", [128, 512], F32) for i in range(6)])
        with contextlib.ExitStack() as st:
            phase_wout(C, st, y, yb, hin, hinb, hmid, hmb, wout, uT, ub, psm, pst, identb, idbb)
        P.barrier()
        with contextlib.ExitStack() as st:
            phase_norm(C, st, hmid, hmb, K.g2t[l], K.g1b, uT, ub, pst, identb, idbb)
        P.barrier()
        with contextlib.ExitStack() as st:
            phase_ffn1(C, st, uT, ub, w1, w3, aT, ab, psm)
        P.barrier()
    with contextlib.ExitStack() as st:
        psm = Ring([psum(C, st, f"psn{i}", [128, 512], F32) for i in range(6)])
        phase_ffn2(C, st, aT, ab, w2, hmid, hmb, hout, houtb, psm)
    P.barrier()
    if l == 1:
        with contextlib.ExitStack() as st:
            gft = sbuf(C, st, "gft2", [128, D]); gfb = Buf()
            P.dma(SP, gft[:], T["gf"], writes=[gfb])
            phase_final_norm(C, st, hout, houtb, gft, gfb, T["out"][half], T["outb"])
        P.barrier()


def build_fused(nlayers=2, halves=(0, 1)):
    nc = bass.Bass("TRN2", target_bir_lowering=False)
    C = Ctx(); C.nc = nc; C.P = Prog(nc); C.emit_out = True
    P = C.P
    dr = lambda n, s, dt=F32, kind="ExternalInput": nc.dram_tensor(n, s, dt, kind=kind).ap()
    T = {}
    xin = [dr("xE", [NTOK, D]), dr("xO", [NTOK, D])]
    T["win"] = dr("win", [2, 13, 128, 8192]); T["wout"] = dr("wout", [2, 4, 128, 8192])
    T["w1"] = dr("w1", [2, 11, 128, 8192]); T["w3"] = dr("w3", [2, 11, 128, 8192]); T["w2"] = dr("w2", [2, 4, 4, 128, 11 * 512])
    g1 = dr("g1", [2, 128, 16]); g2 = dr("g2", [2, 128, 16]); T["gf"] = dr("gf", [128, D])
    T["lp"] = {k: dr(k, [2] + PRM_SHAPES[k]) for k in LAYER_KEYS}
    T["const"] = {k: dr(k, PRM_SHAPES[k]) for k in ("mask_i", "mask5", "onehot")}
    T["flag1"] = dr("flag1", [128, 1]); T["flag0"] = dr("flag0", [128, 1])
    T["zstate"] = {k: dr("z_" + k, PRM_SHAPES[k + "_in"]) for k in STATE_KEYS}
    T["state"] = {k: dr("st_" + k, [2] + PRM_SHAPES[k + "_in"], F32, "Internal") for k in STATE_KEYS}
    T["sdump"] = {k: dr("sd_" + k, PRM_SHAPES[k + "_in"], F32, "Internal") for k in STATE_KEYS}
    T["stateb"] = [Buf(), Buf()]; T["sdumpb"] = Buf()
    T["pf"] = dr("pf", [NFMB * 128, TP], F32, "Internal"); T["pt"] = dr("pt", [NTOK, NTMC], F32, "Internal")
    T["y"] = dr("y", [NTOK, D], BF16, "Internal"); T["hmid"] = dr("hmid", [NTOK, D], F32, "Internal")
    T["aT"] = dr("aT", [5, 128, 44, 512], BF16, "Internal")
    for k in ("pfb", "ptb", "yb", "hmb", "ab", "outb"):
        T[k] = Buf()
    h1 = [dr("h1E", [NTOK, D], F32, "Internal"), dr("h1O", [NTOK, D], F32, "Internal")]
    h2 = [dr("h2E", [NTOK, D], F32, "Internal"), dr("h2O", [NTOK, D], F32, "Internal")]
    T["out"] = [dr("outE", [2048, D], F32, "ExternalOutput"), dr("outO", [2048, D], F32, "ExternalOutput")]
    xb = [Buf(), Buf()]; h1b = [Buf(), Buf()]; h2b = [Buf(), Buf()]
    T["hin"] = {(0, 0): (xin[0], xb[0]), (0, 1): (xin[1], xb[1]), (1, 0): (h1[0], h1b[0]), (1, 1): (h1[1], h1b[1])}
    T["hout"] = {(0, 0): (h1[0], h1b[0]), (0, 1): (h1[1], h1b[1]), (1, 0): (h2[0], h2b[0]), (1, 1): (h2[1], h2b[1])}
    K = Ctx()
    with contextlib.ExitStack() as st0:
        K.ident = sbuf(C, st0, "ident", [128, 128], F32); K.identb = sbuf(C, st0, "identb", [128, 128], BF16)
        K.g1t = [sbuf(C, st0, f"g1t{l}", [128, 16]) for l in range(2)]; K.g2t = [sbuf(C, st0, f"g2t{l}", [128, 16]) for l in range(2)]
        K.idb, K.idbb, K.g1b = Buf(), Buf(), Buf()
        P.op(POOL, lambda e: e.memset(K.ident[:], 1.0), writes=[K.idb])
        P.op(POOL, lambda e: e.affine_select(out=K.ident[:], in_=K.ident[:], pattern=[[-1, 128]], base=0, channel_multiplier=1,
                                             compare_op=ALU.is_equal, fill=0.0), reads=[K.idb], writes=[K.idb])
        P.op(POOL, lambda e: e.tensor_copy(out=K.identb[:], in_=K.ident[:]), reads=[K.idb], writes=[K.idbb])
        for l in range(2):
            P.dma(SP, K.g1t[l][:], g1[l], pwrites=[K.g1b]); P.dma(SP, K.g2t[l][:], g2[l], pwrites=[K.g1b])
        for l in range(nlayers):
            for half in halves:
                emit_half(C, K, T, l, half)
    P.finish([T["outb"], T["sdumpb"], T["stateb"][0], T["stateb"][1]])
    P.emit()
    C.counts = {e: (len(P.ops[e]), sum(1 for o in P.ops[e] if o.signal)) for e in ENGS}
    return nc, C


from concourse.bass_utils import run_bass_kernel_spmd

_PROG = {}


def kernel(**z):
    x = np.asarray(z["x"], np.float32)
    meta = np.asarray(z["meta_tokens"], np.float32)
    if "nc" not in _PROG:
        _PROG["nc"] = build_fused()[0]
    nc = _PROG["nc"]
    shared = {}
    shared.update(host_consts())
    shared["gf"] = np.ascontiguousarray(np.broadcast_to(np.asarray(z["norm_final"], np.float32), (128, D)))
    Ws = [host_layer_weights(z, l) for l in range(2)]
    for k in ("win", "wout", "w1", "w3", "w2", "g1", "g2"):
        shared[k] = np.stack([Ws[0][k], Ws[1][k]])
    Ps = [host_layer_params(z, l) for l in range(2)]
    for k in LAYER_KEYS:
        shared[k] = np.stack([Ps[0][k], Ps[1][k]])
    shared["flag1"] = np.ones((128, 1), np.float32)
    shared["flag0"] = np.zeros((128, 1), np.float32)
    for k in STATE_KEYS:
        shared["z_" + k] = np.zeros(PRM_SHAPES[k + "_in"], np.float32)
    in_maps = []
    for c in range(8):
        b = c % 4
        im = dict(shared)
        im["xE"] = np.ascontiguousarray(np.concatenate([meta, x[b, :2048]], 0))
        im["xO"] = np.ascontiguousarray(np.concatenate([meta, x[b, 2048:]], 0))
        in_maps.append(im)
    res = run_bass_kernel_spmd(nc, in_maps, core_ids=list(range(8))).results
    out = np.zeros((4, 4096, D), np.float32)
    for b in range(4):
        out[b, :2048] = np.asarray(res[b]["outE"], np.float32)
        out[b, 2048:] = np.asarray(res[b]["outO"], np.float32)
    return out
```

```python
import contextlib
import numpy as np
import concourse.bass as bass
import concourse.mybir as mybir

F32 = mybir.dt.float32
BF16 = mybir.dt.bfloat16
AF = mybir.ActivationFunctionType
ALU = mybir.AluOpType
AX = mybir.AxisListType

PE, ACT, DVE, POOL, SP = "pe", "act", "dve", "pool", "sp"
ENGS = [PE, ACT, DVE, POOL, SP]
SEG = 30000
NSLOT = 6


class Buf:
    __slots__ = ("name", "w", "ws", "rs")

    def __init__(self, name=""):
        self.name = name
        self.w = None
        self.ws = {}
        self.rs = {}


def _key(o):
    return (o.eng, o.slot if o.dma else None)


class Op:
    __slots__ = ("eng", "fn", "deps", "dma", "idx", "signal", "ev", "slot", "name")

    def __init__(self, eng, fn, dma):
        self.eng = eng
        self.fn = fn
        self.dma = dma
        self.deps = []
        self.signal = False
        self.ev = None
        self.slot = None
        self.name = ""


class Prog:
    def __init__(self, nc, same_engine_sync=True):
        self.nc = nc
        self.ops = {e: [] for e in ENGS}
        self.same = same_engine_sync
        self.ndma = {e: 0 for e in ENGS}
        self.final_deps = []
        self.pending_barrier = None
        self.scopes = False
        self.phase = ""

    def op(self, eng, fn, reads=(), writes=(), dma=False, name="", pwrites=()):
        o = Op(eng, fn, dma)
        o.name = getattr(self, "phase", "")
        if dma:
            o.slot = self.ndma[eng] % NSLOT
            self.ndma[eng] += 1
            o.signal = True
        o.idx = len(self.ops[eng])
        deps = []
        for r in reads:
            if r.w is not None:
                deps.append(r.w)
            deps.extend(r.ws.values())
        for w in writes:
            if w.w is not None:
                deps.append(w.w)
            deps.extend(w.ws.values())
            deps.extend(w.rs.values())
        for w in pwrites:
            if w.w is not None:
                deps.append(w.w)
            deps.extend(w.rs.values())
        if self.pending_barrier and self.pending_barrier.get(eng):
            deps.extend(self.pending_barrier[eng])
            self.pending_barrier[eng] = []
        best = {}
        for d in deps:
            if d is o:
                continue
            if d.eng == eng and not d.dma:
                if eng == PE or not self.same:
                    continue
            k = _key(d)
            if k not in best or best[k].idx < d.idx:
                best[k] = d
        for d in best.values():
            o.deps.append(d)
            d.signal = True
        for w in writes:
            w.w = o
            w.ws = {}
            w.rs = {}
        for w in pwrites:
            w.ws[_key(o)] = o
        for r in reads:
            r.rs[_key(o)] = o
        self.ops[eng].append(o)
        return o

    def dma(self, eng, out, in_, reads=(), writes=(), pwrites=(), **kw):
        return self.op(eng, lambda e: e.dma_start(out=out, in_=in_, **kw), reads, writes, dma=True, pwrites=pwrites)

    def finish(self, bufs):
        for b in bufs:
            for o in ([b.w] if b.w is not None else []) + list(b.ws.values()):
                self.final_deps.append(o)
                o.signal = True

    def emit(self):
        nc = self.nc
        with contextlib.ExitStack() as st:
            csem = {}
            for e in (PE, ACT, DVE, POOL):
                n = sum(1 for o in self.ops[e] if o.signal and not o.dma)
                nseg = n // SEG + 1
                csem[e] = [st.enter_context(nc.semaphore(f"c_{e}_{i}")) for i in range(nseg)]
            dsem = {}
            for e in (ACT, POOL, SP):
                if self.ndma[e] > 0:
                    dsem[e] = [st.enter_context(nc.semaphore(f"d_{e}_{i}")) for i in range(NSLOT)]
            for e in ENGS:
                cnt = 0
                dcur = [0] * NSLOT
                for o in self.ops[e]:
                    if o.dma:
                        prev = dcur[o.slot]
                        dcur[o.slot] += 16
                        o.ev = (dsem[e][o.slot], dcur[o.slot], prev)
                    elif o.signal:
                        seg, v = divmod(cnt, SEG)
                        o.ev = (csem[e][seg], v + 1, None)
                        cnt += 1
            block = st.enter_context(nc.Block())
            handles = {PE: block.tensor, ACT: block.scalar, DVE: block.vector,
                       POOL: block.gpsimd, SP: block.sync}
            for e in ENGS:
                ops = self.ops[e]
                fdeps = self.final_deps if e == SP else []
                if not ops and not fdeps:
                    continue

                def body(eng, ops=ops, fdeps=fdeps):
                    known = {}

                    def wait(sem, val):
                        k = id(sem)
                        if known.get(k, 0) >= val:
                            return
                        eng.wait_ge(sem, val)
                        known[k] = val

                    cur_ph, sid = None, None
                    for o in ops:
                        if self.scopes and o.name != cur_ph:
                            if cur_ph:
                                nc.leave_named_scope(cur_ph, sid, False)
                            cur_ph = o.name
                            if cur_ph:
                                sid, _ = nc.enter_named_scope(cur_ph, False)
                        for d in o.deps:
                            wait(d.ev[0], d.ev[1])
                        if o.dma and o.ev[2] > 0:
                            wait(o.ev[0], o.ev[2])
                        ins = o.fn(eng)
                        if o.dma:
                            ins.then_inc(o.ev[0], 16)
                        elif o.signal:
                            ins.then_inc(o.ev[0], 1)
                    if self.scopes and cur_ph:
                        nc.leave_named_scope(cur_ph, sid, False)
                    for d in fdeps:
                        wait(d.ev[0], d.ev[1])

                handles[e](body)


def _barrier(self):
    lasts = []
    for e in ENGS:
        ops = self.ops[e]
        if not ops:
            continue
        for o in reversed(ops):
            if not o.dma:
                lasts.append(o)
                break
        seen = set()
        for o in reversed(ops):
            if o.dma and o.slot not in seen:
                seen.add(o.slot)
                lasts.append(o)
            if len(seen) == NSLOT:
                break
    for o in lasts:
        o.signal = True
    self.pending_barrier = {e: list(lasts) for e in ENGS}


Prog.barrier = _barrier


class Ring:
    def __init__(self, tiles):
        self.tiles = tiles
        self.bufs = [Buf() for _ in tiles]
        self.i = 0

    def next(self):
        k = self.i % len(self.tiles)
        self.i += 1
        return self.tiles[k], self.bufs[k]


class _HP:
    pass
hp = _HP()


import numpy as np

A0, B0, C0 = 0, 1696, 3248


def fm_blocks():
    blks = []
    for seg in range(3):
        for i in range(4):
            blks.append(list(range(A0 + seg * 512 + i * 128, A0 + seg * 512 + (i + 1) * 128)))
    blks.append(list(range(A0 + 1536, A0 + 1600)))
    blks.append(list(range(A0 + 1600, A0 + 1696)))
    for seg in range(2):
        for i in range(2):
            blks.append(list(range(B0 + seg * 256 + i * 128, B0 + seg * 256 + (i + 1) * 128)))
    blks.append(list(range(B0 + 1024, B0 + 1040)))
    for seg in range(2):
        for i in range(4):
            blks.append(list(range(C0 + seg * 512 + i * 128, C0 + seg * 512 + (i + 1) * 128)))
    blks.append(list(range(C0 + 2048, C0 + 2056)))
    assert len(blks) == 28
    return blks


def tm_cols():
    cols = []
    cols += list(range(B0 + 512, B0 + 1024))
    cols += list(range(B0 + 1040, B0 + 1552))
    cols += list(range(C0 + 1024, C0 + 2048))
    cols += list(range(C0 + 2056, C0 + 3080))
    assert len(cols) == 3072
    return cols


def tile_k(W, ncol=512):
    K = W.shape[0]
    return np.ascontiguousarray(W.reshape(K // 128, 128, ncol).transpose(1, 0, 2).reshape(128, (K // 128) * ncol))


def prep_win(w):
    blks = fm_blocks()
    out = np.zeros((13, 128, 8192), np.float32)
    for wi in range(7):
        Wt = np.zeros((2048, 512), np.float32)
        for bi in range(4):
            cols = blks[wi * 4 + bi]
            Wt[:, bi * 128:bi * 128 + len(cols)] = w[:, cols]
        out[wi] = tile_k(Wt)
    tc = tm_cols()
    for ci in range(6):
        out[7 + ci] = tile_k(w[:, tc[ci * 512:(ci + 1) * 512]])
    return out


def prep_sq(w, ncb):
    return np.stack([tile_k(w[:, i * 512:(i + 1) * 512]) for i in range(ncb)])


def prep_w2(w):
    out = np.zeros((4, 4, 128, 11 * 512), np.float32)
    for cb in range(4):
        for pc in range(4):
            out[cb, pc] = tile_k(w[pc * 1408:(pc + 1) * 1408, cb * 512:(cb + 1) * 512])
    return out


def gT(g):
    return np.ascontiguousarray(g.reshape(16, 128).T)


for _n in ['fm_blocks','tm_cols','tile_k','prep_win','prep_sq','prep_w2','gT']:
    setattr(hp, _n, globals()[_n])


import contextlib

D = 2048
NTOK = 2064
NMETA = 16
TP = 2070
DFF = 5632
NFMB = 28
NTMC = 3072
EPS = 1e-6

TG = [(0, 16)] + [(16 + 512 * i, 512) for i in range(4)]
TT = [(0, 16)] + [(16 + 128 * i, 128) for i in range(16)]


def pfcol(t):
    return 3 + t if t < 16 else t + 6


class Ctx:
    pass


_uid = [0]


def sbuf(C, st, name, shape, dt=F32):
    _uid[0] += 1
    return st.enter_context(C.nc.sbuf_tensor(f"{name}_{_uid[0]}", shape, dt))


def psum(C, st, name, shape, dt=F32):
    _uid[0] += 1
    return st.enter_context(C.nc.psum_tensor(f"{name}_{_uid[0]}", shape, dt))


def make_wloader(C, st, n_stage=2, n_wb=2, stage_elems=8192):
    stage = Ring([sbuf(C, st, f"wst{i}", [128, stage_elems], F32) for i in range(n_stage)])
    return stage


def load_w(C, stage, dst_ap, dst_buf, src_ap, nelem, cast_eng=POOL):
    P = C.P
    stt, stb = stage.next()
    P.dma(SP, stt[:, 0:nelem], src_ap, writes=[stb])
    P.op(cast_eng, lambda e: e.tensor_copy(out=dst_ap, in_=stt[:, 0:nelem]), reads=[stb], writes=[dst_buf])


def phase_norm(C, st, hsrc, hbuf, gT, gbuf, uT, ubuf, ps_t, ident_bf, idbuf):
    P = C.P
    hring = Ring([sbuf(C, st, f"nh{i}", [128, D], F32) for i in range(2)])
    hnring = Ring([sbuf(C, st, f"nhn{i}", [128, D], BF16) for i in range(2)])
    junk = sbuf(C, st, "njunk", [128, D], BF16)
    jb = Buf()
    stat = Ring([sbuf(C, st, f"nst{i}", [128, 4], F32) for i in range(2)])
    mh = sbuf(C, st, "nmh", [128, 1], F32)
    mhb = Buf()
    P.op(POOL, lambda e: e.memset(mh[:], -0.5), writes=[mhb])
    for (t0, nt) in TT:
        ht, hb = hring.next()
        hn, hnb = hnring.next()
        s, sb_ = stat.next()
        P.dma(SP, ht[0:nt, :], hsrc[t0:t0 + nt, :], reads=[hbuf], writes=[hb])
        P.op(ACT, lambda e, ht=ht, s=s, nt=nt: e.activation(out=junk[0:nt, :], in_=ht[0:nt, :], func=AF.Square,
                                                            accum_out=s[0:nt, 0:1]), reads=[hb], writes=[jb, sb_])
        P.op(DVE, lambda e, s=s, nt=nt: e.tensor_scalar(out=s[0:nt, 1:2], in0=s[0:nt, 0:1], scalar1=1.0 / D, scalar2=EPS,
                                                        op0=ALU.mult, op1=ALU.add), reads=[sb_], writes=[sb_])
        P.op(POOL, lambda e, s=s, nt=nt: e.tensor_tensor(out=s[0:nt, 2:3], in0=s[0:nt, 1:2], in1=mh[0:nt, :], op=ALU.pow),
             reads=[sb_, mhb], writes=[sb_])
        P.op(DVE, lambda e, ht=ht, hn=hn, s=s, nt=nt: e.tensor_scalar(out=hn[0:nt, :], in0=ht[0:nt, :], scalar1=s[0:nt, 2:3],
                                                                      scalar2=None, op0=ALU.mult), reads=[hb, sb_], writes=[hnb])
        for half in range(2):
            pt_, ptb = ps_t.next()
            for k in range(8):
                kb = half * 8 + k
                P.op(PE, lambda e, pt_=pt_, hn=hn, kb=kb, k=k, nt=nt: e.transpose(
                    out=pt_[:, k * 128:k * 128 + nt], in_=hn[0:nt, kb * 128:(kb + 1) * 128], identity=ident_bf[0:nt, 0:nt]),
                    reads=[hnb, idbuf], writes=[ptb])
            eng = DVE if half == 0 else POOL
            if half == 0:
                P.op(DVE, lambda e, pt_=pt_, nt=nt, t0=t0, half=half: e.tensor_tensor(
                    out=uT[:, half * 8:half * 8 + 8, t0:t0 + nt],
                    in0=pt_[:].rearrange("p (k t) -> p k t", k=8)[:, :, 0:nt],
                    in1=gT[:, half * 8:half * 8 + 8].unsqueeze(2).broadcast_to([128, 8, nt]), op=ALU.mult),
                    reads=[ptb, gbuf], pwrites=[ubuf])
            else:
                P.op(DVE, lambda e, pt_=pt_, nt=nt, t0=t0, half=half: e.tensor_tensor(
                    out=uT[:, half * 8:half * 8 + 8, t0:t0 + nt],
                    in0=pt_[:].rearrange("p (k t) -> p k t", k=8)[:, :, 0:nt],
                    in1=gT[:, half * 8:half * 8 + 8].unsqueeze(2).broadcast_to([128, 8, nt]), op=ALU.mult),
                    reads=[ptb, gbuf], pwrites=[ubuf])


def phase_proj(C, st, uT, ubuf, w_dram, pf, pfbuf, pt, ptbuf, ps_mm, hist_out=None, hob=None):
    P = C.P
    stage = make_wloader(C, st)
    wb = Ring([sbuf(C, st, f"pwb{i}", [128, 16, 512], BF16) for i in range(2)])
    ev = Ring([sbuf(C, st, f"pev{i}", [128, 512], F32) for i in range(4)])
    cnt = 0
    for wi in range(13):
        wt, wbuf = wb.next()
        load_w(C, stage, wt[:].rearrange("p k c -> p (k c)"), wbuf, w_dram[wi], 8192)
        if wi < 7:
            for bi in range(4):
                blk = wi * 4 + bi
                for (t0, nt) in TG:
                    pm, pmb = ps_mm.next()
                    for kb in range(16):
                        P.op(PE, lambda e, pm=pm, wt=wt, kb=kb, bi=bi, t0=t0, nt=nt: e.matmul(
                            pm[:, 0:nt], lhsT=wt[:, kb, bi * 128:(bi + 1) * 128], rhs=uT[:, kb, t0:t0 + nt],
                            start=(kb == 0), stop=(kb == 15)), reads=[wbuf, ubuf], writes=[pmb])
                    et, eb = ev.next()
                    if cnt % 2 == 0:
                        P.op(ACT, lambda e, et=et, pm=pm, nt=nt: e.activation(out=et[:, 0:nt], in_=pm[:, 0:nt], func=AF.Copy),
                             reads=[pmb], writes=[eb])
                    else:
                        P.op(DVE, lambda e, et=et, pm=pm, nt=nt: e.tensor_copy(out=et[:, 0:nt], in_=pm[:, 0:nt]),
                             reads=[pmb], writes=[eb])
                    cnt += 1
                    c0 = pfcol(t0)
                    P.dma(POOL, pf[blk * 128:(blk + 1) * 128, c0:c0 + nt], et[:, 0:nt], reads=[eb], pwrites=[pfbuf])
                    if hist_out is not None and t0 + nt == NTOK:
                        P.dma(POOL, hist_out[blk * 128:(blk + 1) * 128, :], et[:, nt - 3:nt], reads=[eb], pwrites=[hob])
        else:
            ci = wi - 7
            for (t0, nt) in TT:
                pm, pmb = ps_mm.next()
                for kb in range(16):
                    P.op(PE, lambda e, pm=pm, wt=wt, kb=kb, t0=t0, nt=nt: e.matmul(
                        pm[0:nt, :], lhsT=uT[:, kb, t0:t0 + nt], rhs=wt[:, kb, :],
                        start=(kb == 0), stop=(kb == 15)), reads=[wbuf, ubuf], writes=[pmb])
                et, eb = ev.next()
                if cnt % 2 == 0:
                    P.op(ACT, lambda e, et=et, pm=pm, nt=nt: e.activation(out=et[0:nt, :], in_=pm[0:nt, :], func=AF.Copy),
                         reads=[pmb], writes=[eb])
                else:
                    P.op(DVE, lambda e, et=et, pm=pm, nt=nt: e.tensor_copy(out=et[0:nt, :], in_=pm[0:nt, :]),
                         reads=[pmb], writes=[eb])
                cnt += 1
                P.dma(POOL, pt[t0:t0 + nt, ci * 512:(ci + 1) * 512], et[0:nt, :], reads=[eb], pwrites=[ptbuf])


def phase_wout(C, st, y, ybuf, hsrc, hbuf, hdst, hdbuf, w_dram, uT, ubuf, ps_mm, ps_t, ident_bf, idbuf):
    P = C.P
    yr = Ring([sbuf(C, st, f"oy{i}", [128, D], BF16) for i in range(2)])
    for (t0, nt) in TT:
        yt, yb = yr.next()
        P.dma(SP, yt[0:nt, :], y[t0:t0 + nt, :], reads=[ybuf], writes=[yb])
        for half in range(2):
            pt_, ptb = ps_t.next()
            for k in range(8):
                kb = half * 8 + k
                P.op(PE, lambda e, pt_=pt_, yt=yt, kb=kb, k=k, nt=nt: e.transpose(
                    out=pt_[:, k * 128:k * 128 + nt], in_=yt[0:nt, kb * 128:(kb + 1) * 128], identity=ident_bf[0:nt, 0:nt]),
                    reads=[yb, idbuf], writes=[ptb])
            eng = ACT if half == 0 else DVE
            if half == 0:
                P.op(ACT, lambda e, pt_=pt_, nt=nt, t0=t0, half=half: e.activation(
                    out=uT[:, half * 8:half * 8 + 8, t0:t0 + nt],
                    in_=pt_[:].rearrange("p (k t) -> p k t", k=8)[:, :, 0:nt], func=AF.Copy), reads=[ptb], pwrites=[ubuf])
            else:
                P.op(DVE, lambda e, pt_=pt_, nt=nt, t0=t0, half=half: e.tensor_copy(
                    out=uT[:, half * 8:half * 8 + 8, t0:t0 + nt],
                    in_=pt_[:].rearrange("p (k t) -> p k t", k=8)[:, :, 0:nt]), reads=[ptb], pwrites=[ubuf])
    stage = make_wloader(C, st)
    wb = Ring([sbuf(C, st, f"owb{i}", [128, 16, 512], BF16) for i in range(2)])
    hr = Ring([sbuf(C, st, f"ohr{i}", [128, 512], F32) for i in range(3)])
    ev = Ring([sbuf(C, st, f"oev{i}", [128, 512], F32) for i in range(3)])
    for ci in range(4):
        wt, wbuf = wb.next()
        load_w(C, stage, wt[:].rearrange("p k c -> p (k c)"), wbuf, w_dram[ci], 8192)
        for (t0, nt) in TT:
            ho, hob = hr.next()
            P.dma(SP, ho[0:nt, :], hsrc[t0:t0 + nt, ci * 512:(ci + 1) * 512], reads=[hbuf], writes=[hob])
            pm, pmb = ps_mm.next()
            for kb in range(16):
                P.op(PE, lambda e, pm=pm, wt=wt, kb=kb, t0=t0, nt=nt: e.matmul(
                    pm[0:nt, :], lhsT=uT[:, kb, t0:t0 + nt], rhs=wt[:, kb, :],
                    start=(kb == 0), stop=(kb == 15)), reads=[wbuf, ubuf], writes=[pmb])
            et, eb = ev.next()
            P.op(DVE, lambda e, et=et, pm=pm, ho=ho, nt=nt: e.tensor_tensor(out=et[0:nt, :], in0=pm[0:nt, :], in1=ho[0:nt, :],
                                                                            op=ALU.add), reads=[pmb, hob], writes=[eb])
            P.dma(POOL, hdst[t0:t0 + nt, ci * 512:(ci + 1) * 512], et[0:nt, :], reads=[eb], pwrites=[hdbuf])


def phase_ffn1(C, st, uT, ubuf, w1_dram, w3_dram, aT, abuf, ps_mm):
    P = C.P
    stage = make_wloader(C, st)
    w1b = Ring([sbuf(C, st, f"f1w{i}", [128, 16, 512], BF16) for i in range(2)])
    w3b = Ring([sbuf(C, st, f"f3w{i}", [128, 16, 512], BF16) for i in range(2)])
    sg = Ring([sbuf(C, st, f"fsg{i}", [128, 512], F32) for i in range(3)])
    av = Ring([sbuf(C, st, f"fav{i}", [128, 512], BF16) for i in range(3)])
    for gi in range(11):
        w1t, w1buf = w1b.next()
        w3t, w3buf = w3b.next()
        load_w(C, stage, w1t[:].rearrange("p k c -> p (k c)"), w1buf, w1_dram[gi], 8192)
        load_w(C, stage, w3t[:].rearrange("p k c -> p (k c)"), w3buf, w3_dram[gi], 8192)
        for bi in range(4):
            j = gi * 4 + bi
            for gidx, (t0, nt) in enumerate(TG):
                pa, pab = ps_mm.next()
                for kb in range(16):
                    P.op(PE, lambda e, pa=pa, w1t=w1t, kb=kb, bi=bi, t0=t0, nt=nt: e.matmul(
                        pa[:, 0:nt], lhsT=w1t[:, kb, bi * 128:(bi + 1) * 128], rhs=uT[:, kb, t0:t0 + nt],
                        start=(kb == 0), stop=(kb == 15)), reads=[w1buf, ubuf], writes=[pab])
                pb_, pbb = ps_mm.next()
                for kb in range(16):
                    P.op(PE, lambda e, pb_=pb_, w3t=w3t, kb=kb, bi=bi, t0=t0, nt=nt: e.matmul(
                        pb_[:, 0:nt], lhsT=w3t[:, kb, bi * 128:(bi + 1) * 128], rhs=uT[:, kb, t0:t0 + nt],
                        start=(kb == 0), stop=(kb == 15)), reads=[w3buf, ubuf], writes=[pbb])
                s, sb_ = sg.next()
                a, ab_ = av.next()
                P.op(ACT, lambda e, s=s, pa=pa, nt=nt: e.activation(out=s[:, 0:nt], in_=pa[:, 0:nt], func=AF.Silu),
                     reads=[pab], writes=[sb_])
                P.op(DVE, lambda e, a=a, s=s, pb_=pb_, nt=nt: e.tensor_tensor(out=a[:, 0:nt], in0=pb_[:, 0:nt], in1=s[:, 0:nt],
                                                                              op=ALU.mult), reads=[pbb, sb_], writes=[ab_])
                P.dma(POOL, aT[gidx, :, j, 0:nt], a[:, 0:nt], reads=[ab_], pwrites=[abuf])


def phase_ffn2(C, st, aT, abuf, w2_dram, hsrc, hbuf, hdst, hdbuf, ps_mm):
    P = C.P
    stage = Ring([sbuf(C, st, f"gst{i}", [128, 11 * 512], F32) for i in range(2)])
    w2b = Ring([sbuf(C, st, f"gw{i}", [128, 44, 512], BF16) for i in range(1)])
    ar = Ring([sbuf(C, st, f"gar{i}", [128, 44, 512], BF16) for i in range(2)])
    hr = Ring([sbuf(C, st, f"ghr{i}", [128, 512], F32) for i in range(3)])
    ev = Ring([sbuf(C, st, f"gev{i}", [128, 512], F32) for i in range(3)])
    for cb in range(4):
        wt, wbuf = w2b.next()
        for pc in range(4):
            load_w(C, stage, wt[:, pc * 11:(pc + 1) * 11, :].rearrange("p k c -> p (k c)"), wbuf, w2_dram[cb, pc], 11 * 512)
        for gidx, (g0, gn) in enumerate(TG):
            at, atb = ar.next()
            P.dma(SP, at[:, :, 0:gn], aT[gidx, :, :, 0:gn], reads=[abuf], writes=[atb])
            for s0 in range(0, gn, 128):
                nt = min(128, gn - s0)
                t0 = g0 + s0
                ho, hob = hr.next()
                P.dma(SP, ho[0:nt, :], hsrc[t0:t0 + nt, cb * 512:(cb + 1) * 512], reads=[hbuf], writes=[hob])
                pm, pmb = ps_mm.next()
                for j in range(44):
                    P.op(PE, lambda e, pm=pm, at=at, wt=wt, j=j, s0=s0, nt=nt: e.matmul(
                        pm[0:nt, :], lhsT=at[:, j, s0:s0 + nt], rhs=wt[:, j, :],
                        start=(j == 0), stop=(j == 43)), reads=[wbuf, atb], writes=[pmb])
                et, eb = ev.next()
                P.op(DVE, lambda e, et=et, pm=pm, ho=ho, nt=nt: e.tensor_tensor(out=et[0:nt, :], in0=pm[0:nt, :], in1=ho[0:nt, :],
                                                                                op=ALU.add), reads=[pmb, hob], writes=[eb])
                P.dma(POOL, hdst[t0:t0 + nt, cb * 512:(cb + 1) * 512], et[0:nt, :], reads=[eb], pwrites=[hdbuf])


def phase_final_norm(C, st, hsrc, hbuf, gbc, gbcb, out, obuf):
    P = C.P
    hring = Ring([sbuf(C, st, f"zh{i}", [128, D], F32) for i in range(2)])
    oring = Ring([sbuf(C, st, f"zo{i}", [128, D], F32) for i in range(2)])
    junk = sbuf(C, st, "zjunk", [128, D], BF16)
    jb = Buf()
    stat = Ring([sbuf(C, st, f"zst{i}", [128, 4], F32) for i in range(2)])
    mh = sbuf(C, st, "zmh", [128, 1], F32)
    mhb = Buf()
    P.op(POOL, lambda e: e.memset(mh[:], -0.5), writes=[mhb])
    for (t0, nt) in TT[1:]:
        ht, hb = hring.next()
        ot, ob = oring.next()
        s, sb_ = stat.next()
        P.dma(SP, ht[0:nt, :], hsrc[t0:t0 + nt, :], reads=[hbuf], writes=[hb])
        P.op(ACT, lambda e, ht=ht, s=s, nt=nt: e.activation(out=junk[0:nt, :], in_=ht[0:nt, :], func=AF.Square,
                                                            accum_out=s[0:nt, 0:1]), reads=[hb], writes=[jb, sb_])
        P.op(DVE, lambda e, s=s, nt=nt: e.tensor_scalar(out=s[0:nt, 1:2], in0=s[0:nt, 0:1], scalar1=1.0 / D, scalar2=EPS,
                                                        op0=ALU.mult, op1=ALU.add), reads=[sb_], writes=[sb_])
        P.op(POOL, lambda e, s=s, nt=nt: e.tensor_tensor(out=s[0:nt, 2:3], in0=s[0:nt, 1:2], in1=mh[0:nt, :], op=ALU.pow),
             reads=[sb_, mhb], writes=[sb_])
        P.op(DVE, lambda e, ht=ht, ot=ot, s=s, nt=nt: e.scalar_tensor_tensor(
            out=ot[0:nt, :], in0=ht[0:nt, :], scalar=s[0:nt, 2:3], in1=gbc[0:nt, :], op0=ALU.mult, op1=ALU.mult),
            reads=[hb, sb_, gbcb], writes=[ob])
        P.dma(POOL, out[t0 - 16:t0 - 16 + nt, :], ot[0:nt, :], reads=[ob], pwrites=[obuf])


import contextlib

SEGS = [(3, 16, 0), (22, 2048, 16)]
LDK = 0.6065306597126334


def chunks_of(seg):
    c0, ncol, t0 = seg
    if ncol == 16:
        return [(c0, 16, t0)]
    return [(c0 + 64 * i, 64, t0 + 64 * i) for i in range(ncol // 64)]


def fix_gap(C, x, xb, hist_src, flagE, fb, tmp_ring, rows):
    P = C.P
    t, tb = tmp_ring.next()
    P.dma(SP, t[0:rows, 0:3], hist_src, writes=[tb])
    P.op(DVE, lambda e: e.scalar_tensor_tensor(out=x[0:rows, 19:22], in0=x[0:rows, 16:19], scalar=flagE[0:rows, 0:1],
                                               in1=t[0:rows, 0:3], op0=ALU.mult, op1=ALU.add), reads=[xb, tb, fb], writes=[xb])


def chunk_rel(C, out, ob, src, sb_, rows):
    P = C.P
    P.op(DVE, lambda e: e.tensor_copy(out=out[0:rows, 3:19], in_=src[0:rows, 3:19]), reads=[sb_], writes=[ob])
    P.op(DVE, lambda e: e.tensor_copy(out=out[0:rows, 22:86], in_=src[0:rows, 22:86]), reads=[sb_], writes=[ob])
    o3 = out[0:rows, 86:2070].rearrange("p (c j) -> p c j", j=64)
    s3 = src[0:rows, 86:2070].rearrange("p (c j) -> p c j", j=64)
    pv = src[0:rows, 22:2006].rearrange("p (c j) -> p c j", j=64)[:, :, 63:64].broadcast_to([rows, 31, 64])
    P.op(DVE, lambda e: e.tensor_tensor(out=o3, in0=s3, in1=pv, op=ALU.subtract), reads=[sb_], writes=[ob])


def gate_prepass(C, st, pt, ptb):
    P = C.P
    r = Ring([sbuf(C, st, f"gp{i}", [128, 1536], F32) for i in range(3)])
    for (t0, nt) in TT:
        t, tb = r.next()
        P.dma(SP, t[0:nt, 0:512], pt[t0:t0 + nt, 512:1024], reads=[ptb], writes=[tb])
        P.dma(SP, t[0:nt, 512:1536], pt[t0:t0 + nt, 2048:3072], reads=[ptb], writes=[tb])
        P.op(ACT, lambda e, t=t, nt=nt: e.activation(out=t[0:nt, 0:512], in_=t[0:nt, 0:512], func=AF.Silu), reads=[tb], writes=[tb])
        P.op(ACT, lambda e, t=t, nt=nt: e.activation(out=t[0:nt, 512:1536], in_=t[0:nt, 512:1536], func=AF.Sigmoid),
             reads=[tb], writes=[tb])
        P.dma(POOL, pt[t0:t0 + nt, 512:1024], t[0:nt, 0:512], reads=[tb], writes=[ptb])
        P.dma(POOL, pt[t0:t0 + nt, 2048:3072], t[0:nt, 512:1536], reads=[tb], writes=[ptb])


def mixer_gla(C, st, pf, pfb, pt, ptb, y, yb, prm, K, npsA=2, npsB=6):
    P = C.P
    ones, onesb, ident, idb, mask_i, mib, flagE, fb = K.ones, K.onesb, K.ident, K.idb, K.mask_i, K.mib, K.flagE, K.fb
    a2 = sbuf(C, st, "ga2", [32, 256]); a2b = Buf()
    P.op(DVE, lambda e: e.memset(a2[:], 0.0), writes=[a2b])
    nab = sbuf(C, st, "gnab", [64, 4]); nabb = Buf()
    nbc = sbuf(C, st, "gnbc", [64, 128]); nbcb = Buf()
    P.dma(SP, a2[0:16, :], prm["gla_a2"], reads=[a2b], writes=[a2b])
    P.dma(SP, nab[:], prm["gla_ab"], writes=[nabb])
    P.op(DVE, lambda e: e.tensor_scalar(out=nab[:], in0=nab[:], scalar1=-1.0, scalar2=None, op0=ALU.mult), reads=[nabb], writes=[nabb])
    P.dma(SP, nbc[:], prm["gla_normbc"], writes=[nbcb])
    xa = sbuf(C, st, "gxa", [32, TP]); xab = Buf()
    P.dma(SP, xa[:], pf[18 * 128:18 * 128 + 32, :], reads=[pfb], writes=[xab])
    q = sbuf(C, st, "gq", [64, TP]); k = sbuf(C, st, "gk", [64, TP]); sp = sbuf(C, st, "gsp", [64, TP])
    spc = sbuf(C, st, "gspc", [64, TP]); e1 = sbuf(C, st, "ge1", [64, TP]); e2 = sbuf(C, st, "ge2", [64, TP])
    qb, kb_, spb, spcb, e1b, e2b = [Buf() for _ in range(6)]
    S = sbuf(C, st, "gS", [64, 128]); Sb = Buf()
    S16 = sbuf(C, st, "gS16", [64, 128], BF16); S16b = Buf()
    qbf = sbuf(C, st, "gqbf", [64, TP], BF16); kbf = sbuf(C, st, "gkbf", [64, TP], BF16); qbfb, kbfb = Buf(), Buf()
    v16r = Ring([sbuf(C, st, f"gv16{i}", [64, 128], BF16) for i in range(3)])
    Sin = sbuf(C, st, "gSin", [64, 128]); Sinb = Buf()
    psA = Ring([psum(C, st, f"gpa{i}", [128, 512], F32) for i in range(npsA)])
    psB = Ring([psum(C, st, f"gpb{i}", [128, 512], F32) for i in range(npsB)])
    vr = Ring([sbuf(C, st, f"gv{i}", [64, 256], F32) for i in range(3)])
    ktr = Ring([sbuf(C, st, f"gkt{i}", [64, 64], BF16) for i in range(2)])
    scr = Ring([sbuf(C, st, f"gsc{i}", [64, 64], BF16) for i in range(2)])
    str_ = Ring([sbuf(C, st, f"gst{i}", [64, 4], F32) for i in range(2)])
    junk = sbuf(C, st, "gjunk", [64, 128], F32); jb = Buf()
    mh = sbuf(C, st, "gmh", [64, 1], F32); mhb = Buf()
    P.op(POOL, lambda e: e.memset(mh[:], -0.5), writes=[mhb])
    t1r = Ring([sbuf(C, st, f"gt1{i}", [64, 128], F32) for i in range(2)])
    yor = Ring([sbuf(C, st, f"gyo{i}", [64, 128], BF16) for i in range(2)])
    for h in range(4):
        r0 = (14 + h // 2) * 128 + (h % 2) * 64
        r1 = (16 + h // 2) * 128 + (h % 2) * 64
        P.dma(SP, q[:], pf[r0:r0 + 64, :], reads=[pfb], writes=[qb])
        P.dma(SP, k[:], pf[r1:r1 + 64, :], reads=[pfb], writes=[kb_])
        for c0 in range(3, TP, 512):
            n = min(512, TP - c0)
            pa, pab = psA.next()
            P.op(PE, lambda e, pa=pa, c0=c0, n=n, h=h: e.matmul(pa[0:64, 0:n], lhsT=a2[0:32, h * 64:(h + 1) * 64], rhs=xa[0:32, c0:c0 + n],
                                                               start=True, stop=True), reads=[a2b, xab], writes=[pab])
            P.op(ACT, lambda e, pa=pa, c0=c0, n=n, h=h: e.activation(out=sp[:, c0:c0 + n], in_=pa[0:64, 0:n], func=AF.Exp, scale=-1.0,
                                                                    bias=nab[:, h:h + 1]), reads=[pab, nabb], writes=[spb])
        P.op(ACT, lambda e: e.activation(out=sp[:, 3:TP], in_=sp[:, 3:TP], func=AF.Ln, bias=1.0), reads=[spb], writes=[spb])
        for (c0, ncol, t0) in SEGS:
            P.op(DVE, lambda e, c0=c0, ncol=ncol: e.tensor_tensor_scan(out=spc[:, c0:c0 + ncol], data0=ones[0:64, c0:c0 + ncol],
                                                                       data1=sp[:, c0:c0 + ncol], initial=0.0, op0=ALU.mult, op1=ALU.add),
                 reads=[spb, onesb], writes=[spcb])
        chunk_rel(C, sp, spb, spc, spcb, 64)
        P.op(ACT, lambda e: e.activation(out=e1[:, 3:TP], in_=sp[:, 3:TP], func=AF.Exp, scale=-1.0 / 16), reads=[spb], writes=[e1b])
        P.op(ACT, lambda e: e.activation(out=e2[:, 3:TP], in_=sp[:, 3:TP], func=AF.Exp, scale=1.0 / 16), reads=[spb], writes=[e2b])
        P.op(DVE, lambda e: e.scalar_tensor_tensor(out=q[:, 3:TP], in0=q[:, 3:TP], scalar=0.125, in1=e1[:, 3:TP], op0=ALU.mult,
                                                   op1=ALU.mult), reads=[qb, e1b], writes=[qb])
        P.op(DVE, lambda e: e.tensor_tensor(out=k[:, 3:TP], in0=k[:, 3:TP], in1=e2[:, 3:TP], op=ALU.mult), reads=[kb_, e2b], writes=[kb_])
        P.op(ACT, lambda e: e.activation(out=qbf[:, 3:TP], in_=q[:, 3:TP], func=AF.Copy), reads=[qb], writes=[qbfb])
        P.op(ACT, lambda e: e.activation(out=kbf[:, 3:TP], in_=k[:, 3:TP], func=AF.Copy), reads=[kb_], writes=[kbfb])
        P.op(DVE, lambda e: e.memset(S[:], 0.0), writes=[Sb])
        P.op(DVE, lambda e: e.memset(S16[:], 0.0), writes=[S16b])
        for si, seg in enumerate(SEGS):
            if si == 1:
                P.dma(SP, Sin[:], prm["sB_in"][h], writes=[Sinb])
                P.op(DVE, lambda e: e.scalar_tensor_tensor(out=S[:], in0=S[:], scalar=flagE[0:64, 0:1], in1=Sin[:], op0=ALU.mult,
                                                           op1=ALU.add), reads=[Sb, Sinb, fb], writes=[Sb])
                P.op(ACT, lambda e: e.activation(out=S16[:], in_=S[:], func=AF.Copy), reads=[Sb], writes=[S16b])
            for (c0, n, t0) in chunks_of(seg):
                vt, vb = vr.next()
                P.dma(SP, vt[0:n, 0:128], pt[t0:t0 + n, h * 128:(h + 1) * 128], reads=[ptb], writes=[vb])
                P.dma(SP, vt[0:n, 128:256], pt[t0:t0 + n, 512 + h * 128:512 + (h + 1) * 128], reads=[ptb], writes=[vb])
                pb, pbb = psB.next()
                P.op(PE, lambda e, pb=pb, c0=c0, n=n: e.transpose(out=pb[0:n, 0:64], in_=k[:, c0:c0 + n], identity=ident[0:64, 0:64]),
                     reads=[kb_, idb], writes=[pbb])
                P.op(PE, lambda e, pb=pb, c0=c0, n=n: e.matmul(pb[0:n, 64:64 + n], lhsT=kbf[:, c0:c0 + n], rhs=qbf[:, c0:c0 + n], start=True, stop=True),
                     reads=[kbfb, qbfb], writes=[pbb])
                v16, v16b = v16r.next()
                P.op(ACT, lambda e, v16=v16, vt=vt, n=n: e.activation(out=v16[0:n, :], in_=vt[0:n, 0:128], func=AF.Copy), reads=[vb], writes=[v16b])
                kt, ktb = ktr.next()
                sc, scb = scr.next()
                P.op(ACT, lambda e, kt=kt, pb=pb, n=n: e.activation(out=kt[0:n, :], in_=pb[0:n, 0:64], func=AF.Copy), reads=[pbb], writes=[ktb])
                P.op(DVE, lambda e, sc=sc, pb=pb, n=n: e.tensor_tensor(out=sc[0:n, 0:n], in0=pb[0:n, 64:64 + n], in1=mask_i[0:n, 0:n], op=ALU.mult),
                     reads=[pbb, mib], writes=[scb])
                po, pob = psB.next()
                P.op(PE, lambda e, po=po, c0=c0, n=n: e.matmul(po[0:n, 0:128], lhsT=qbf[:, c0:c0 + n], rhs=S16[:, :], start=True, stop=False),
                     reads=[qbfb, S16b], writes=[pob])
                P.op(PE, lambda e, po=po, sc=sc, v16=v16, n=n: e.matmul(po[0:n, 0:128], lhsT=sc[0:n, 0:n], rhs=v16[0:n, :], start=False, stop=True),
                     reads=[scb, v16b], writes=[pob])
                pc, pcb = psB.next()
                P.op(PE, lambda e, pc=pc, kt=kt, v16=v16, n=n: e.matmul(pc[0:64, 0:128], lhsT=kt[0:n, 0:64], rhs=v16[0:n, :], start=True, stop=True),
                     reads=[ktb, v16b], writes=[pcb])
                ce = c0 + n - 1
                P.op(DVE, lambda e, pc=pc: e.tensor_tensor(out=S[:], in0=S[:], in1=pc[0:64, 0:128], op=ALU.add), reads=[pcb, Sb], writes=[Sb])
                P.op(DVE, lambda e, ce=ce: e.tensor_scalar(out=S[:], in0=S[:], scalar1=e1[:, ce:ce + 1], scalar2=None, op0=ALU.mult),
                     reads=[Sb, e1b], writes=[Sb])
                P.op(ACT, lambda e: e.activation(out=S16[:], in_=S[:], func=AF.Copy), reads=[Sb], writes=[S16b])
                if C.emit_out and not False:
                    s_, sb2 = str_.next()
                    t1, t1b = t1r.next()
                    yo, yob = yor.next()
                    P.op(ACT, lambda e, t1=t1, po=po, n=n: e.activation(out=t1[0:n, :], in_=po[0:n, 0:128], func=AF.Copy), reads=[pob], writes=[t1b])
                    P.op(DVE, lambda e, t1=t1, n=n: e.tensor_tensor(out=junk[0:n, :], in0=t1[0:n, :], in1=t1[0:n, :], op=ALU.mult), reads=[t1b], writes=[jb])
                    P.op(DVE, lambda e, s_=s_, n=n: e.tensor_reduce(out=s_[0:n, 0:1], in_=junk[0:n, :], axis=AX.X, op=ALU.add), reads=[jb], writes=[sb2])
                    P.op(DVE, lambda e, s_=s_, n=n: e.tensor_scalar(out=s_[0:n, 1:2], in0=s_[0:n, 0:1], scalar1=1.0 / 128, scalar2=EPS, op0=ALU.mult,
                                                                   op1=ALU.add), reads=[sb2], writes=[sb2])
                    P.op(POOL, lambda e, s_=s_, n=n: e.tensor_tensor(out=s_[0:n, 2:3], in0=s_[0:n, 1:2], in1=mh[0:n, :], op=ALU.pow),
                         reads=[sb2, mhb], writes=[sb2])
                    P.op(DVE, lambda e, t1=t1, s_=s_, n=n: e.scalar_tensor_tensor(out=t1[0:n, :], in0=t1[0:n, :], scalar=s_[0:n, 2:3],
                                                                               in1=nbc[0:n, :], op0=ALU.mult, op1=ALU.mult),
                         reads=[sb2, nbcb, t1b], writes=[t1b])
                    P.op(DVE, lambda e, yo=yo, t1=t1, vt=vt, n=n: e.tensor_tensor(out=yo[0:n, :], in0=t1[0:n, :], in1=vt[0:n, 128:256], op=ALU.mult),
                         reads=[t1b, vb], writes=[yob])
                    P.dma(SP, y[t0:t0 + n, 512 + h * 128:512 + (h + 1) * 128], yo[0:n, :], reads=[yob], pwrites=[yb])
                yield
        P.dma(POOL, prm["sB_out"][h], S[:], reads=[Sb], pwrites=[K.sob])


def mixer_mlstm(C, st, pf, pfb, pt, ptb, y, yb, prm, K, npsA=4, npsG=2):
    P = C.P
    ones, onesb, ident, idb, mask_i, mib, flagE, fb = K.ones, K.onesb, K.ident, K.idb, K.mask_i, K.mib, K.flagE, K.fb
    cw = sbuf(C, st, "mcw", [128, 8, 4]); cb = sbuf(C, st, "mcb", [128, 8]); cwb = Buf()
    ib = sbuf(C, st, "mib", [4, 2]); fbb = sbuf(C, st, "mfb", [4, 2]); gbb = Buf()
    nbc = sbuf(C, st, "mnbc", [64, 1024]); nbcb = Buf()
    oh = sbuf(C, st, "moh", [4, 4, 128]); ohb = Buf()
    P.dma(SP, cw[:], prm["ml_cw"], writes=[cwb]); P.dma(SP, cb[:], prm["ml_cb"], writes=[cwb])
    P.dma(SP, ib[:, 0:1], prm["ml_ib"], writes=[gbb]); P.dma(SP, fbb[:, 0:1], prm["ml_fb"], writes=[gbb])
    P.dma(SP, nbc[:], prm["ml_normbc"], writes=[nbcb]); P.dma(SP, oh[:], prm["onehot"], writes=[ohb])
    P.op(DVE, lambda e: e.tensor_scalar(out=ib[:, 1:2], in0=ib[:, 0:1], scalar1=1.0 / 15, scalar2=None, op0=ALU.mult), reads=[gbb], writes=[gbb])
    P.op(DVE, lambda e: e.tensor_scalar(out=fbb[:, 1:2], in0=fbb[:, 0:1], scalar1=1.0 / 15, scalar2=None, op0=ALU.mult), reads=[gbb], writes=[gbb])
    gi = sbuf(C, st, "mgi", [4, TP]); gf = sbuf(C, st, "mgf", [4, TP]); SPc = sbuf(C, st, "mSP", [4, TP]); av = sbuf(C, st, "mav", [4, TP])
    MU = sbuf(C, st, "mMU", [4, TP]); MUS = sbuf(C, st, "mMUS", [4, TP])
    G8 = sbuf(C, st, "mG8", [36, TP]); FE = sbuf(C, st, "mFE", [4, 64]); min_ = sbuf(C, st, "mmin", [4, 4])
    gib, gfb_, SPb, avb, MUb, MUSb, G8b, FEb, minb = [Buf() for _ in range(9)]
    P.dma(SP, gi[:], pf[27 * 128:27 * 128 + 4, :], reads=[pfb], writes=[gib])
    P.dma(SP, gf[:], pf[27 * 128 + 4:27 * 128 + 8, :], reads=[pfb], writes=[gfb_])
    P.op(DVE, lambda e: e.memset(G8[:], 0.0), writes=[G8b])
    P.op(ACT, lambda e: e.activation(out=gi[:], in_=gi[:], func=AF.Tanh, scale=1.0 / 15, bias=ib[:, 1:2]), reads=[gib, gbb], writes=[gib])
    P.op(ACT, lambda e: e.activation(out=gf[:], in_=gf[:], func=AF.Tanh, scale=1.0 / 15, bias=fbb[:, 1:2]), reads=[gfb_, gbb], writes=[gfb_])
    P.op(ACT, lambda e: e.activation(out=gf[:], in_=gf[:], func=AF.Exp, scale=-15.0), reads=[gfb_], writes=[gfb_])
    P.op(ACT, lambda e: e.activation(out=gf[:], in_=gf[:], func=AF.Ln, bias=1.0), reads=[gfb_], writes=[gfb_])
    P.dma(SP, min_[:, 0:1], prm["mC_in"], writes=[minb])
    for si, (c0, ncol, t0) in enumerate(SEGS):
        P.op(DVE, lambda e, c0=c0, ncol=ncol: e.tensor_tensor_scan(out=SPc[:, c0:c0 + ncol], data0=ones[0:4, c0:c0 + ncol], data1=gf[:, c0:c0 + ncol],
                                                                   initial=0.0, op0=ALU.mult, op1=ALU.add), reads=[gfb_, onesb], writes=[SPb])
        P.op(DVE, lambda e, c0=c0, ncol=ncol: e.scalar_tensor_tensor(out=av[:, c0:c0 + ncol], in0=gi[:, c0:c0 + ncol], scalar=15.0,
                                                                     in1=SPc[:, c0:c0 + ncol], op0=ALU.mult, op1=ALU.add), reads=[gib, SPb], writes=[avb])
        if si == 0:
            P.op(DVE, lambda e, c0=c0, ncol=ncol: e.tensor_tensor_scan(out=MU[:, c0:c0 + ncol], data0=av[:, c0:c0 + ncol], data1=av[:, c0:c0 + ncol],
                                                                       initial=0.0, op0=ALU.max, op1=ALU.max), reads=[avb], writes=[MUb])
            P.op(DVE, lambda e: e.memset(MUS[:, 3:19], 0.0), writes=[MUSb])
            P.op(DVE, lambda e: e.tensor_tensor(out=min_[:, 1:2], in0=MU[:, 18:19], in1=SPc[:, 18:19], op=ALU.subtract), reads=[MUb, SPb, minb], writes=[minb])
            P.op(DVE, lambda e: e.scalar_tensor_tensor(out=min_[:, 2:3], in0=min_[:, 1:2], scalar=flagE[0:4, 0:1], in1=min_[:, 0:1], op0=ALU.mult,
                                                       op1=ALU.add), reads=[minb, fb], writes=[minb])
        else:
            P.op(DVE, lambda e, c0=c0, ncol=ncol: e.tensor_tensor_scan(out=MU[:, c0:c0 + ncol], data0=av[:, c0:c0 + ncol], data1=av[:, c0:c0 + ncol],
                                                                       initial=min_[:, 2:3], op0=ALU.max, op1=ALU.max), reads=[avb, minb], writes=[MUb])
            P.op(DVE, lambda e: e.tensor_copy(out=MUS[:, 22:86], in_=min_[:, 2:3].broadcast_to([4, 64])), reads=[minb], writes=[MUSb])
            P.op(DVE, lambda e: e.tensor_copy(out=MUS[:, 86:2070].rearrange("p (c j) -> p c j", j=64),
                                              in_=MU[:, 22:2006].rearrange("p (c j) -> p c j", j=64)[:, :, 63:64].broadcast_to([4, 31, 64])),
                 reads=[MUb], writes=[MUSb])
    P.op(DVE, lambda e: e.tensor_tensor(out=av[:, 3:TP], in0=av[:, 3:TP], in1=MUS[:, 3:TP], op=ALU.subtract), reads=[avb, MUSb], writes=[avb])
    P.op(ACT, lambda e: e.activation(out=G8[0:4, 3:TP], in_=av[:, 3:TP], func=AF.Exp), reads=[avb, G8b], writes=[G8b])
    P.op(DVE, lambda e: e.tensor_tensor(out=av[:, 3:TP], in0=SPc[:, 3:TP], in1=MUS[:, 3:TP], op=ALU.subtract), reads=[SPb, MUSb, G8b], writes=[avb])
    P.op(ACT, lambda e: e.activation(out=G8[32:36, 3:TP], in_=av[:, 3:TP], func=AF.Exp), reads=[avb, G8b], writes=[G8b])
    P.op(DVE, lambda e: e.tensor_tensor(out=FE[:, 0:1], in0=MUS[:, 3:4], in1=MU[:, 18:19], op=ALU.subtract), reads=[MUSb, MUb], writes=[FEb])
    P.op(DVE, lambda e: e.tensor_tensor(out=FE[:, 1:33], in0=MUS[:, 22:2070].rearrange("p (c j) -> p c j", j=64)[:, :, 0],
                                        in1=MU[:, 22:2070].rearrange("p (c j) -> p c j", j=64)[:, :, 63], op=ALU.subtract),
         reads=[MUSb, MUb, FEb], writes=[FEb])
    P.op(ACT, lambda e: e.activation(out=FE[:, 0:33], in_=FE[:, 0:33], func=AF.Exp), reads=[FEb], writes=[FEb])
    P.op(DVE, lambda e: e.tensor_tensor(out=min_[:, 3:4], in0=MU[:, TP - 1:TP], in1=SPc[:, TP - 1:TP], op=ALU.subtract), reads=[MUb, SPb, minb], writes=[minb])
    P.dma(POOL, prm["mC_out"], min_[:, 3:4], reads=[minb], pwrites=[K.sob])
    xq = sbuf(C, st, "mxq", [128, TP]); xk = sbuf(C, st, "mxk", [128, TP]); q = sbuf(C, st, "mq", [128, TP]); k = sbuf(C, st, "mk", [128, TP])
    xqb, xkb, qb, kb_ = [Buf() for _ in range(4)]
    tmpr = Ring([sbuf(C, st, f"mtmp{i}", [128, 4], F32) for i in range(2)])
    CX = sbuf(C, st, "mCX", [128, 257]); CXb = Buf()
    CX16 = sbuf(C, st, "mCX16", [128, 258], BF16); CX16b = Buf()
    qbf = sbuf(C, st, "mqbf", [128, TP], BF16); kbf = sbuf(C, st, "mkbf", [128, TP], BF16); qbfb, kbfb = Buf(), Buf()
    CXin = sbuf(C, st, "mCXin", [128, 257]); CXinb = Buf()
    FB = sbuf(C, st, "mFB", [128, 64]); FBb = Buf()
    psA = Ring([psum(C, st, f"mpa{i}", [128, 512], F32) for i in range(npsA)])
    psG = Ring([psum(C, st, f"mpg{i}", [128, 512], F32) for i in range(npsG)])
    vr = Ring([sbuf(C, st, f"mv{i}", [64, 512], F32) for i in range(3)])
    vxr = Ring([sbuf(C, st, f"mvx{i}", [64, 258], BF16) for i in range(2)])
    for _t, _b in zip(vxr.tiles, vxr.bufs):
        P.op(DVE, lambda e, _t=_t: e.memset(_t[:], 0.0), writes=[_b])
    ktr = Ring([sbuf(C, st, f"mkt{i}", [64, 128], BF16) for i in range(2)])
    scr = Ring([sbuf(C, st, f"msc{i}", [64, 64], BF16) for i in range(2)])
    gtr = Ring([sbuf(C, st, f"mgt{i}", [64, 36], F32) for i in range(2)])
    str_ = Ring([sbuf(C, st, f"mst{i}", [64, 8], F32) for i in range(2)])
    junk = sbuf(C, st, "mjunk", [64, 256], F32); jb = Buf()
    mh = sbuf(C, st, "mmh", [64, 1], F32); mhb = Buf()
    P.op(POOL, lambda e: e.memset(mh[:], -0.5), writes=[mhb])
    t1r = Ring([sbuf(C, st, f"mt1{i}", [64, 256], F32) for i in range(2)])
    yor = Ring([sbuf(C, st, f"myo{i}", [64, 256], BF16) for i in range(2)])
    for h in range(4):
        P.dma(SP, xq[:], pf[(19 + h) * 128:(20 + h) * 128, :], reads=[pfb], writes=[xqb])
        P.dma(SP, xk[:], pf[(23 + h) * 128:(24 + h) * 128, :], reads=[pfb], writes=[xkb])
        P.op(DVE, lambda e: e.memset(xq[:, 0:3], 0.0), reads=[xqb], writes=[xqb])
        P.op(DVE, lambda e: e.memset(xk[:, 0:3], 0.0), reads=[xkb], writes=[xkb])
        fix_gap(C, xq, xqb, prm["hist_in"][(19 + h) * 128:(20 + h) * 128, :], flagE, fb, tmpr, 128)
        fix_gap(C, xk, xkb, prm["hist_in"][(23 + h) * 128:(24 + h) * 128, :], flagE, fb, tmpr, 128)
        for (x, xb, o, ob, j) in ((xq, xqb, q, qb, h), (xk, xkb, k, kb_, 4 + h)):
            P.op(DVE, lambda e, x=x, o=o, j=j: e.tensor_scalar(out=o[:, 3:TP], in0=x[:, 0:TP - 3], scalar1=cw[:, j, 0:1], scalar2=cb[:, j:j + 1],
                                                               op0=ALU.mult, op1=ALU.add), reads=[xb, cwb], writes=[ob])
            for tap in range(1, 4):
                P.op(DVE, lambda e, x=x, o=o, j=j, tap=tap: e.scalar_tensor_tensor(out=o[:, 3:TP], in0=x[:, tap:TP - 3 + tap], scalar=cw[:, j, tap:tap + 1],
                                                                                 in1=o[:, 3:TP], op0=ALU.mult, op1=ALU.add), reads=[xb, cwb, ob], writes=[ob])
            P.op(ACT, lambda e, o=o: e.activation(out=o[:, 3:TP], in_=o[:, 3:TP], func=AF.Silu), reads=[ob], writes=[ob])
        P.op(DVE, lambda e: e.tensor_scalar(out=k[:, 3:TP], in0=k[:, 3:TP], scalar1=128 ** -0.5, scalar2=None, op0=ALU.mult), reads=[kb_], writes=[kb_])
        P.op(ACT, lambda e: e.activation(out=qbf[:, 3:TP], in_=q[:, 3:TP], func=AF.Copy), reads=[qb], writes=[qbfb])
        P.op(ACT, lambda e: e.activation(out=kbf[:, 3:TP], in_=k[:, 3:TP], func=AF.Copy), reads=[kb_], writes=[kbfb])
        pg, pgb = psG.next()
        P.op(PE, lambda e, pg=pg, h=h: e.matmul(pg[:, 0:33], lhsT=oh[0:4, h, :], rhs=FE[0:4, 0:33], start=True, stop=True), reads=[ohb, FEb], writes=[pgb])
        P.op(ACT, lambda e, pg=pg: e.activation(out=FB[:, 0:33], in_=pg[:, 0:33], func=AF.Copy), reads=[pgb], writes=[FBb])
        P.op(DVE, lambda e: e.memset(CX[:], 0.0), writes=[CXb])
        P.op(DVE, lambda e: e.memset(CX16[:], 0.0), writes=[CX16b])
        ci = 0
        for si, seg in enumerate(SEGS):
            if si == 1:
                P.dma(SP, CXin[:], prm["sC_in"][h], writes=[CXinb])
                P.op(DVE, lambda e: e.scalar_tensor_tensor(out=CX[:], in0=CX[:], scalar=flagE[:, 0:1], in1=CXin[:], op0=ALU.mult, op1=ALU.add),
                     reads=[CXb, CXinb, fb], writes=[CXb])
                P.op(ACT, lambda e: e.activation(out=CX16[:, 0:257], in_=CX[:], func=AF.Copy), reads=[CXb, CX16b], writes=[CX16b])
            for (c0, n, t0) in chunks_of(seg):
                vt, vb = vr.next()
                P.dma(SP, vt[0:n, 0:256], pt[t0:t0 + n, 1024 + h * 256:1024 + (h + 1) * 256], reads=[ptb], writes=[vb])
                P.dma(SP, vt[0:n, 256:512], pt[t0:t0 + n, 2048 + h * 256:2048 + (h + 1) * 256], reads=[ptb], writes=[vb])
                pa, pab = psA.next()
                P.op(PE, lambda e, pa=pa, c0=c0, n=n: e.transpose(out=pa[0:n, 0:128], in_=k[:, c0:c0 + n], identity=ident[:, :]), reads=[kb_, idb], writes=[pab])
                P.op(PE, lambda e, pa=pa, c0=c0, n=n: e.matmul(pa[0:n, 128:128 + n], lhsT=kbf[:, c0:c0 + n], rhs=qbf[:, c0:c0 + n], start=True, stop=True),
                     reads=[kbfb, qbfb], writes=[pab])
                P.op(PE, lambda e, pa=pa, c0=c0, n=n: e.transpose(out=pa[0:n, 192:228], in_=G8[0:36, c0:c0 + n], identity=ident[0:36, 0:36]),
                     reads=[G8b, idb], writes=[pab])
                kt, ktb = ktr.next(); sc, scb = scr.next(); gt, gtb = gtr.next()
                P.op(ACT, lambda e, kt=kt, pa=pa, n=n: e.activation(out=kt[0:n, :], in_=pa[0:n, 0:128], func=AF.Copy), reads=[pab], writes=[ktb])
                P.op(DVE, lambda e, sc=sc, pa=pa, n=n: e.tensor_tensor(out=sc[0:n, 0:n], in0=pa[0:n, 128:128 + n], in1=mask_i[0:n, 0:n], op=ALU.mult),
                     reads=[pab, mib], writes=[scb])
                P.op(ACT, lambda e, gt=gt, pa=pa, n=n: e.activation(out=gt[0:n, :], in_=pa[0:n, 192:228], func=AF.Copy), reads=[pab], writes=[gtb])
                vx, vxb = vxr.next()
                P.op(DVE, lambda e, vx=vx, vt=vt, gt=gt, n=n, h=h: e.tensor_scalar(out=vx[0:n, 0:256], in0=vt[0:n, 0:256], scalar1=gt[0:n, h:h + 1], scalar2=None,
                                                                                 op0=ALU.mult), reads=[vb, gtb], writes=[vxb])
                P.op(ACT, lambda e, vx=vx, gt=gt, n=n, h=h: e.activation(out=vx[0:n, 256:257], in_=gt[0:n, h:h + 1], func=AF.Copy), reads=[gtb, vxb], writes=[vxb])
                pn, pnb = psA.next()
                P.op(PE, lambda e, pn=pn, c0=c0, n=n: e.matmul(pn[0:n, 0:258], lhsT=qbf[:, c0:c0 + n], rhs=CX16[:, :], start=True, stop=False), reads=[qbfb, CX16b], writes=[pnb])
                P.op(PE, lambda e, pn=pn, sc=sc, vx=vx, n=n: e.matmul(pn[0:n, 0:258], lhsT=sc[0:n, 0:n], rhs=vx[0:n, :], start=False, stop=True),
                     reads=[scb, vxb], writes=[pnb])
                pc, pcb = psA.next()
                P.op(PE, lambda e, pc=pc, kt=kt, vx=vx, n=n: e.matmul(pc[:, 0:258], lhsT=kt[0:n, :], rhs=vx[0:n, :], start=True, stop=True),
                     reads=[ktb, vxb], writes=[pcb])
                P.op(DVE, lambda e, pc=pc: e.tensor_tensor(out=CX[:], in0=CX[:], in1=pc[:, 0:257], op=ALU.add), reads=[pcb, CXb], writes=[CXb])
                P.op(DVE, lambda e, ci=ci: e.tensor_scalar(out=CX[:], in0=CX[:], scalar1=FB[:, ci:ci + 1], scalar2=None, op0=ALU.mult),
                     reads=[CXb, FBb], writes=[CXb])
                P.op(ACT, lambda e: e.activation(out=CX16[:, 0:257], in_=CX[:], func=AF.Copy), reads=[CXb, CX16b], writes=[CX16b])
                if C.emit_out:
                    s_, sb2 = str_.next()
                    P.op(ACT, lambda e, s_=s_, pn=pn, n=n: e.activation(out=s_[0:n, 0:1], in_=pn[0:n, 256:257], func=AF.Abs),
                         reads=[pnb], writes=[sb2])
                    P.op(DVE, lambda e, s_=s_, gt=gt, n=n, h=h: e.tensor_tensor(out=s_[0:n, 0:1], in0=s_[0:n, 0:1], in1=gt[0:n, 32 + h:33 + h], op=ALU.max),
                         reads=[sb2, gtb], writes=[sb2])
                    P.op(DVE, lambda e, s_=s_, n=n: e.reciprocal(out=s_[0:n, 1:2], in_=s_[0:n, 0:1]), reads=[sb2], writes=[sb2])
                    P.op(ACT, lambda e, s_=s_, pn=pn, n=n: e.activation(out=junk[0:n, :], in_=pn[0:n, 0:256], func=AF.Square, scale=s_[0:n, 1:2],
                                                                       accum_out=s_[0:n, 2:3]), reads=[pnb, sb2], writes=[jb, sb2])
                    P.op(DVE, lambda e, s_=s_, n=n: e.tensor_scalar(out=s_[0:n, 3:4], in0=s_[0:n, 2:3], scalar1=1.0 / 256, scalar2=EPS, op0=ALU.mult,
                                                                   op1=ALU.add), reads=[sb2], writes=[sb2])
                    P.op(POOL, lambda e, s_=s_, n=n: e.tensor_tensor(out=s_[0:n, 4:5], in0=s_[0:n, 3:4], in1=mh[0:n, :], op=ALU.pow), reads=[sb2, mhb], writes=[sb2])
                    P.op(DVE, lambda e, s_=s_, n=n: e.tensor_tensor(out=s_[0:n, 5:6], in0=s_[0:n, 4:5], in1=s_[0:n, 1:2], op=ALU.mult), reads=[sb2], writes=[sb2])
                    t1, t1b = t1r.next(); yo, yob = yor.next()
                    P.op(DVE, lambda e, t1=t1, pn=pn, s_=s_, n=n, h=h: e.scalar_tensor_tensor(out=t1[0:n, :], in0=pn[0:n, 0:256], scalar=s_[0:n, 5:6],
                                                                                          in1=nbc[0:n, h * 256:(h + 1) * 256], op0=ALU.mult, op1=ALU.mult),
                         reads=[pnb, sb2, nbcb], writes=[t1b])
                    P.op(DVE, lambda e, yo=yo, t1=t1, vt=vt, n=n: e.tensor_tensor(out=yo[0:n, :], in0=t1[0:n, :], in1=vt[0:n, 256:512], op=ALU.mult),
                         reads=[t1b, vb], writes=[yob])
                    P.dma(POOL, y[t0:t0 + n, 1024 + h * 256:1024 + (h + 1) * 256], yo[0:n, :], reads=[yob], pwrites=[yb])
                ci += 1
                yield
        P.dma(POOL, prm["sC_out"][h], CX[:], reads=[CXb], pwrites=[K.sob])


def mixer_rwkv(C, st, pf, pfb, y, yb, prm, K):
    P = C.P
    ones, onesb, ident, idb, flagE, fb = K.ones, K.onesb, K.ident, K.idb, K.flagE, K.fb
    mask5, m5b = K.mask5, K.m5b
    muA = sbuf(C, st, "amuA", [64, 3, 8]); muL = sbuf(C, st, "amuL", [96, 3]); w2 = sbuf(C, st, "aw2", [32, 512]); a2 = sbuf(C, st, "aa2", [32, 512])
    g2 = sbuf(C, st, "ag2", [96, 512]); ch = sbuf(C, st, "ach", [64, 5, 8]); rk = sbuf(C, st, "ark", [64, 8, 2])
    lnw = sbuf(C, st, "alnw", [64, 512]); lnb = sbuf(C, st, "alnb", [64, 512])
    pb_ = Buf()
    for t, n_ in ((muA, "rw_muA"), (muL, "rw_muL"), (w2, "rw_w2"), (a2, "rw_a2"), (g2, "rw_g2"), (rk, "rw_rk"), (lnw, "rw_lnw_bc"), (lnb, "rw_lnb_bc")):
        P.dma(SP, t[:], prm[n_], pwrites=[pb_])
    P.dma(SP, ch[:, 0:4, :], prm["rw_ch"], pwrites=[pb_])
    P.op(DVE, lambda e: e.tensor_scalar(out=ch[:, 4, :], in0=ch[:, 3, :], scalar1=-1.0, scalar2=1.0, op0=ALU.mult, op1=ALU.add), reads=[pb_], writes=[pb_])
    mh = sbuf(C, st, "amh", [64, TP], F32); mhb = Buf()
    P.op(POOL, lambda e: e.memset(mh[:], -0.5), writes=[mhb])
    tmpr = Ring([sbuf(C, st, f"atmp{i}", [128, 4], F32) for i in range(2)])
    raw = sbuf(C, st, "araw", [96, TP]); rawb = Buf()
    thw = sbuf(C, st, "athw", [32, TP]); xal = sbuf(C, st, "axal", [32, TP]); sg = sbuf(C, st, "asg", [96, TP])
    thwb, xalb, sgb = Buf(), Buf(), Buf()
    for (dst, dstb, r0, nr, mcol, fn) in ((thw, thwb, 12 * 128, 32, 0, AF.Tanh), (xal, xalb, 12 * 128 + 32, 32, 1, None), (sg, sgb, 13 * 128, 96, 2, AF.Sigmoid)):
        P.dma(SP, raw[0:nr, :], pf[r0:r0 + nr, :], reads=[pfb], writes=[rawb])
        P.op(DVE, lambda e, nr=nr: e.memset(raw[0:nr, 0:3], 0.0), reads=[rawb], writes=[rawb])
        fix_gap(C, raw, rawb, prm["hist_in"][r0:r0 + nr, :], flagE, fb, tmpr, nr)
        P.op(DVE, lambda e, dst=dst, nr=nr: e.tensor_tensor(out=dst[0:nr, 3:TP], in0=raw[0:nr, 2:TP - 1], in1=raw[0:nr, 3:TP], op=ALU.subtract),
             reads=[rawb], writes=[dstb])
        P.op(DVE, lambda e, dst=dst, nr=nr, mcol=mcol: e.scalar_tensor_tensor(out=dst[0:nr, 3:TP], in0=dst[0:nr, 3:TP], scalar=muL[0:nr, mcol:mcol + 1],
                                                                            in1=raw[0:nr, 3:TP], op0=ALU.mult, op1=ALU.add), reads=[rawb, dstb, pb_], writes=[dstb])
        if fn is not None:
            P.op(ACT, lambda e, dst=dst, nr=nr, fn=fn: e.activation(out=dst[0:nr, 3:TP], in_=dst[0:nr, 3:TP], func=fn), reads=[dstb], writes=[dstb])
    A = [sbuf(C, st, f"aA{i}", [64, TP]) for i in range(10)]
    Ab = [Buf() for _ in range(10)]
    H = sbuf(C, st, "aH", [64, 64]); Hb = Buf()
    Hin = sbuf(C, st, "aHin", [64, 64]); Hinb = Buf()
    G = 8
    MM = sbuf(C, st, "aMM", [64, G, 5, 64]); MMb = Buf()
    TM = sbuf(C, st, "aTM", [64, G, 3, 64]); TMb = Buf()
    NN = [sbuf(C, st, f"aNN{i}", [64, G, 2, 64]) for i in range(2)]; NNb = [Buf(), Buf()]
    Pm = sbuf(C, st, "aPm", [64, G, 64]); Pmb = Buf()
    GB = sbuf(C, st, "aGB", [64, G, 66]); GBb = Buf()
    ps1 = Ring([psum(C, st, f"ap1{i}", [128, 512], F32) for i in range(3)])
    pygr = Ring([psum(C, st, f"apy{i}", [128, 512], F32) for i in range(2)])
    ps2 = Ring([psum(C, st, f"ap2{i}", [128, 512], F32) for i in range(3)])
    w0r = Ring([sbuf(C, st, f"aw0{i}", [64, 64], F32) for i in range(2)])
    ur = Ring([sbuf(C, st, f"au{i}", [64, 64], F32) for i in range(2)])
    str_ = Ring([sbuf(C, st, f"ast{i}", [64, 8], F32) for i in range(2)])
    junk = sbuf(C, st, "ajunk", [64, 64], F32); jb = Buf()
    T1 = sbuf(C, st, "aT1", [64, G, 64]); T1b = Buf()
    SQ = sbuf(C, st, "aSQ", [64, G, 64]); SQb = Buf()
    YO = sbuf(C, st, "aYO", [64, G, 64], BF16); YOb = Buf()
    ST = sbuf(C, st, "aST", [64, 6, G]); STb = Buf()
    for h in range(8):
        rows = [(sg_ * 4 + h // 2) * 128 + (h % 2) * 64 for sg_ in range(3)]
        for i in range(3):
            P.dma(SP, A[i][:], pf[rows[i]:rows[i] + 64, :], reads=[pfb], writes=[Ab[i]])
            P.op(DVE, lambda e, i=i: e.memset(A[i][:, 0:3], 0.0), reads=[Ab[i]], writes=[Ab[i]])
            fix_gap(C, A[i], Ab[i], prm["hist_in"][rows[i]:rows[i] + 64, :], flagE, fb, tmpr, 64)
            P.op(DVE, lambda e, i=i: e.tensor_tensor(out=A[3 + i][:, 3:TP], in0=A[i][:, 2:TP - 1], in1=A[i][:, 3:TP], op=ALU.subtract),
                 reads=[Ab[i]], writes=[Ab[3 + i]])
            P.op(DVE, lambda e, i=i, h=h: e.scalar_tensor_tensor(out=A[3 + i][:, 3:TP], in0=A[3 + i][:, 3:TP], scalar=muA[:, i, h:h + 1], in1=A[i][:, 3:TP],
                                                                op0=ALU.mult, op1=ALU.add), reads=[Ab[i], Ab[3 + i], pb_], writes=[Ab[3 + i]])
        xr, xk, xv = A[3], A[4], A[5]
        for c0 in range(3, TP, 512):
            n = min(512, TP - c0)
            p_, p_b = ps1.next()
            P.op(PE, lambda e, p_=p_, c0=c0, n=n, h=h: e.matmul(p_[0:64, 0:n], lhsT=w2[0:32, h * 64:(h + 1) * 64], rhs=thw[0:32, c0:c0 + n], start=True, stop=True),
                 reads=[pb_, thwb], writes=[p_b])
            P.op(ACT, lambda e, p_=p_, c0=c0, n=n, h=h: e.activation(out=A[0][:, c0:c0 + n], in_=p_[0:64, 0:n], func=AF.Sigmoid, bias=ch[:, 0, h:h + 1]),
                 reads=[p_b, pb_], writes=[Ab[0]])
            p_, p_b = ps1.next()
            P.op(PE, lambda e, p_=p_, c0=c0, n=n, h=h: e.matmul(p_[0:64, 0:n], lhsT=a2[0:32, h * 64:(h + 1) * 64], rhs=xal[0:32, c0:c0 + n], start=True, stop=True),
                 reads=[pb_, xalb], writes=[p_b])
            P.op(ACT, lambda e, p_=p_, c0=c0, n=n, h=h: e.activation(out=A[1][:, c0:c0 + n], in_=p_[0:64, 0:n], func=AF.Sigmoid, bias=ch[:, 1, h:h + 1]),
                 reads=[p_b, pb_], writes=[Ab[1]])
        P.op(DVE, lambda e, h=h: e.tensor_scalar(out=A[2][:, 3:TP], in0=xk[:, 3:TP], scalar1=ch[:, 2, h:h + 1], scalar2=None, op0=ALU.mult),
             reads=[Ab[4], pb_], writes=[Ab[2]])
        P.op(DVE, lambda e: e.tensor_tensor(out=A[6][:, 3:TP], in0=A[2][:, 3:TP], in1=A[2][:, 3:TP], op=ALU.mult), reads=[Ab[2]], writes=[Ab[6]])
        for c0 in range(3, TP, 512):
            n = min(512, TP - c0)
            p_, p_b = ps1.next()
            P.op(PE, lambda e, p_=p_, c0=c0, n=n: e.matmul(p_[0:64, 0:n], lhsT=ones[0:64, 0:64], rhs=A[6][:, c0:c0 + n], start=True, stop=True),
                 reads=[onesb, Ab[6]], writes=[p_b])
            P.op(DVE, lambda e, p_=p_, c0=c0, n=n: e.tensor_scalar(out=A[8][:, c0:c0 + n], in0=p_[0:64, 0:n], scalar1=1e-24, scalar2=None, op0=ALU.max),
                 reads=[p_b], writes=[Ab[8]])
        P.op(POOL, lambda e: e.tensor_tensor(out=A[8][:, 3:TP], in0=A[8][:, 3:TP], in1=mh[:, 3:TP], op=ALU.pow), reads=[Ab[8], mhb], writes=[Ab[8]])
        P.op(DVE, lambda e: e.tensor_tensor(out=A[2][:, 3:TP], in0=A[2][:, 3:TP], in1=A[8][:, 3:TP], op=ALU.mult), reads=[Ab[2], Ab[8]], writes=[Ab[2]])
        P.op(DVE, lambda e, h=h: e.tensor_scalar(out=A[6][:, 3:TP], in0=A[1][:, 3:TP], scalar1=ch[:, 3, h:h + 1], scalar2=ch[:, 4, h:h + 1], op0=ALU.mult,
                                                op1=ALU.add), reads=[Ab[1], pb_], writes=[Ab[6]])
        P.op(DVE, lambda e: e.tensor_tensor(out=xk[:, 3:TP], in0=xk[:, 3:TP], in1=A[6][:, 3:TP], op=ALU.mult), reads=[Ab[4], Ab[6]], writes=[Ab[4]])
        P.op(DVE, lambda e: e.tensor_tensor(out=A[1][:, 3:TP], in0=A[1][:, 3:TP], in1=A[2][:, 3:TP], op=ALU.mult), reads=[Ab[1], Ab[2]], writes=[Ab[1]])
        P.op(DVE, lambda e: e.tensor_tensor(out=A[6][:, 3:TP], in0=xr[:, 3:TP], in1=xk[:, 3:TP], op=ALU.mult), reads=[Ab[3], Ab[4]], writes=[Ab[6]])
        for (c0, ncol, t0) in SEGS:
            P.op(DVE, lambda e, c0=c0, ncol=ncol: e.tensor_tensor_scan(out=A[7][:, c0:c0 + ncol], data0=ones[0:64, c0:c0 + ncol], data1=A[0][:, c0:c0 + ncol],
                                                                       initial=0.0, op0=ALU.mult, op1=ALU.add), reads=[Ab[0], onesb], writes=[Ab[7]])
        P.op(DVE, lambda e: e.memset(A[8][:, 19:22], 0.0), reads=[Ab[8]], writes=[Ab[8]])
        chunk_rel(C, A[8], Ab[8], A[7], Ab[7], 64)
        P.op(ACT, lambda e: e.activation(out=A[7][:, 3:TP], in_=A[8][:, 3:TP], func=AF.Exp, scale=-LDK), reads=[Ab[8]], writes=[Ab[7]])
        P.op(ACT, lambda e: e.activation(out=A[9][:, 3:TP], in_=A[8][:, 3:TP], func=AF.Exp, scale=LDK), reads=[Ab[8]], writes=[Ab[9]])
        P.op(DVE, lambda e: e.tensor_tensor(out=A[8][:, 3:TP], in0=A[8][:, 3:TP], in1=A[0][:, 3:TP], op=ALU.subtract), reads=[Ab[8], Ab[0]], writes=[Ab[8]])
        P.op(ACT, lambda e: e.activation(out=A[8][:, 3:TP], in_=A[8][:, 3:TP], func=AF.Exp, scale=-LDK), reads=[Ab[8]], writes=[Ab[8]])
        P.op(DVE, lambda e: e.tensor_tensor(out=xr[:, 3:TP], in0=xr[:, 3:TP], in1=A[7][:, 3:TP], op=ALU.mult), reads=[Ab[3], Ab[7]], writes=[Ab[3]])
        P.op(DVE, lambda e: e.tensor_tensor(out=xk[:, 3:TP], in0=xk[:, 3:TP], in1=A[9][:, 3:TP], op=ALU.mult), reads=[Ab[4], Ab[9]], writes=[Ab[4]])
        P.op(DVE, lambda e: e.tensor_tensor(out=A[1][:, 3:TP], in0=A[1][:, 3:TP], in1=A[9][:, 3:TP], op=ALU.mult), reads=[Ab[1], Ab[9]], writes=[Ab[1]])
        P.op(DVE, lambda e: e.scalar_tensor_tensor(out=A[2][:, 3:TP], in0=A[2][:, 3:TP], scalar=-1.0, in1=A[8][:, 3:TP], op0=ALU.mult, op1=ALU.mult),
             reads=[Ab[2], Ab[8]], writes=[Ab[2]])
        rt, kt_, bt, at, prod, G1 = A[3], A[4], A[1], A[2], A[6], A[7]
        rtb, ktb_, btb, atb, prodb, G1b = Ab[3], Ab[4], Ab[1], Ab[2], Ab[6], Ab[7]
        xvb = Ab[5]
        P.op(DVE, lambda e: e.memset(H[:], 0.0), writes=[Hb])
        for si, seg in enumerate(SEGS):
            if si == 1:
                P.dma(SP, Hin[:], prm["sA_in"][h], writes=[Hinb])
                P.op(DVE, lambda e: e.scalar_tensor_tensor(out=H[:], in0=H[:], scalar=flagE[0:64, 0:1], in1=Hin[:], op0=ALU.mult, op1=ALU.add),
                     reads=[Hb, Hinb, fb], writes=[Hb])
            chs_all = chunks_of(seg)
            for g0 in range(0, len(chs_all), G):
                chs = chs_all[g0:g0 + G]
                ng = len(chs)
                n = chs[0][1]
                nlev = 5 if n == 64 else 3
                for g, (c0, n, t0) in enumerate(chs):
                    p_, p_b = ps1.next()
                    for j, (src, srcb) in enumerate(((xv, xvb), (kt_, ktb_), (bt, btb))):
                        P.op(PE, lambda e, p_=p_, src=src, c0=c0, n=n, j=j: e.transpose(out=p_[0:n, j * 64:(j + 1) * 64], in_=src[:, c0:c0 + n], identity=ident[0:64, 0:64]),
                             reads=[srcb, idb], writes=[p_b])
                    P.op(ACT, lambda e, p_=p_, g=g, n=n: e.activation(out=TM[0:n, g, :, :], in_=p_[0:n, 0:192].rearrange("p (j d) -> p j d", j=3), func=AF.Copy),
                         reads=[p_b], pwrites=[TMb])
                    q_, q_b = ps1.next()
                    pairs = ((kt_, ktb_, at, atb), (kt_, ktb_, rt, rtb), (bt, btb, at, atb), (bt, btb, rt, rtb), (at, atb, bt, btb))
                    for j, (l, lb, r, rb) in enumerate(pairs):
                        P.op(PE, lambda e, q_=q_, l=l, r=r, c0=c0, n=n, j=j: e.matmul(q_[0:n, j * 64:j * 64 + n], lhsT=l[:, c0:c0 + n], rhs=r[:, c0:c0 + n], start=True, stop=True),
                             reads=[lb, rb], writes=[q_b])
                    P.op(DVE, lambda e, q_=q_, g=g, n=n: e.tensor_tensor(out=MM[0:n, g, :, 0:n], in0=q_[0:n, 0:320].rearrange("p (j d) -> p j d", j=5)[:, :, 0:n],
                                                                       in1=mask5[0:n, :, 0:n], op=ALU.mult), reads=[q_b, m5b], pwrites=[MMb])
                P.op(DVE, lambda e, ng=ng, n=n: e.tensor_tensor(out=Pm[0:n, 0:ng, 0:n], in0=MM[0:n, 0:ng, 2, 0:n],
                                                                in1=ident[0:n, 0:n].unsqueeze(1).broadcast_to([n, ng, n]), op=ALU.add),
                     reads=[MMb, idb], writes=[Pmb])
                curN = lambda g, n=n: MM[0:n, g, 2, 0:n]
                curNT = lambda g, n=n: MM[0:n, g, 4, 0:n]
                curb = MMb
                for lev in range(nlev):
                    nn, nnb = NN[lev % 2], NNb[lev % 2]
                    for g4 in range(0, ng, 4):
                        m4 = min(4, ng - g4)
                        p2, p2b = ps2.next()
                        for g in range(g4, g4 + m4):
                            gg = g - g4
                            P.op(PE, lambda e, p2=p2, gg=gg, n=n, a_=curNT(g), b_=curN(g): e.matmul(p2[0:n, gg * 128:gg * 128 + n], lhsT=a_, rhs=b_, start=True, stop=True),
                                 reads=[curb], writes=[p2b])
                            P.op(PE, lambda e, p2=p2, gg=gg, n=n, a_=curN(g), b_=curNT(g): e.matmul(p2[0:n, gg * 128 + 64:gg * 128 + 64 + n], lhsT=a_, rhs=b_, start=True, stop=True),
                                 reads=[curb], writes=[p2b])
                        P.op(ACT, lambda e, p2=p2, nn=nn, g4=g4, m4=m4, n=n: e.activation(out=nn[0:n, g4:g4 + m4, :, 0:n],
                                                                                 in_=p2[0:n, 0:m4 * 128].rearrange("p (g j d) -> p g j d", g=m4, j=2)[:, :, :, 0:n], func=AF.Copy),
                             reads=[p2b], pwrites=[nnb])
                    curN = lambda g, nn=nn, n=n: nn[0:n, g, 0, 0:n]
                    curNT = lambda g, nn=nn, n=n: nn[0:n, g, 1, 0:n]
                    curb = nnb
                    p1, p1b = ps1.next()
                    for g in range(ng):
                        P.op(PE, lambda e, p1=p1, g=g, n=n, a_=curNT(g): e.matmul(p1[0:n, g * 64:g * 64 + n], lhsT=a_, rhs=Pm[0:n, g, 0:n], start=True, stop=True),
                             reads=[curb, Pmb], writes=[p1b])
                    P.op(DVE, lambda e, p1=p1, ng=ng, n=n: e.tensor_tensor(out=Pm[0:n, 0:ng, 0:n], in0=Pm[0:n, 0:ng, 0:n],
                                                                         in1=p1[0:n, 0:ng * 64].rearrange("p (g d) -> p g d", g=ng)[:, :, 0:n], op=ALU.add),
                         reads=[p1b, Pmb], writes=[Pmb])
                if C.emit_out:
                    for g, (c0, n, t0) in enumerate(chs):
                        p_, p_b = ps1.next()
                        P.op(PE, lambda e, p_=p_, c0=c0, n=n, h=h: e.matmul(p_[0:n, 0:64], lhsT=sg[0:96, c0:c0 + n], rhs=g2[0:96, h * 64:(h + 1) * 64], start=True, stop=True),
                             reads=[sgb, pb_], writes=[p_b])
                        P.op(PE, lambda e, p_=p_, c0=c0, n=n, h=h: e.matmul(p_[0:n, 64:66], lhsT=prod[:, c0:c0 + n], rhs=rk[:, h, :], start=True, stop=True),
                             reads=[prodb, pb_], writes=[p_b])
                        P.op(ACT, lambda e, p_=p_, g=g, n=n: e.activation(out=GB[0:n, g, :], in_=p_[0:n, 0:66], func=AF.Copy), reads=[p_b], pwrites=[GBb])
                pyg, pygb = pygr.next()
                for g, (c0, n, t0) in enumerate(chs):
                    vtm = TM[0:n, g, 0, :]; ktm = TM[0:n, g, 1, :]; btm = TM[0:n, g, 2, :]
                    LakT = MM[0:n, g, 0, 0:n]; MrkT = MM[0:n, g, 1, 0:n]; MrbT = MM[0:n, g, 3, 0:n]
                    TT_ = Pm[0:n, g, 0:n]
                    pw, pwb = ps1.next()
                    P.op(PE, lambda e, pw=pw, c0=c0, n=n: e.matmul(pw[0:n, 0:64], lhsT=at[:, c0:c0 + n], rhs=H[:, :], start=True, stop=False), reads=[atb, Hb], writes=[pwb])
                    P.op(PE, lambda e, pw=pw, n=n, LakT=LakT, vtm=vtm: e.matmul(pw[0:n, 0:64], lhsT=LakT, rhs=vtm, start=False, stop=True), reads=[MMb, TMb], writes=[pwb])
                    w0, w0b = w0r.next()
                    P.op(ACT, lambda e, w0=w0, pw=pw, n=n: e.activation(out=w0[0:n, :], in_=pw[0:n, 0:64], func=AF.Copy), reads=[pwb], writes=[w0b])
                    P.op(PE, lambda e, pw=pw, n=n, TT_=TT_, w0=w0: e.matmul(pw[0:n, 64:128], lhsT=TT_, rhs=w0[0:n, :], start=True, stop=True), reads=[Pmb, w0b], writes=[pwb])
                    u, ub_ = ur.next()
                    P.op(DVE, lambda e, u=u, pw=pw, n=n: e.tensor_copy(out=u[0:n, :], in_=pw[0:n, 64:128]), reads=[pwb], writes=[ub_])
                    if C.emit_out:
                        P.op(PE, lambda e, pyg=pyg, g=g, c0=c0, n=n: e.matmul(pyg[0:n, g * 64:(g + 1) * 64], lhsT=rt[:, c0:c0 + n], rhs=H[:, :], start=True, stop=False), reads=[rtb, Hb], writes=[pygb])
                        P.op(PE, lambda e, pyg=pyg, g=g, n=n, MrbT=MrbT, u=u: e.matmul(pyg[0:n, g * 64:(g + 1) * 64], lhsT=MrbT, rhs=u[0:n, :], start=False, stop=False), reads=[MMb, ub_], writes=[pygb])
                        P.op(PE, lambda e, pyg=pyg, g=g, n=n, MrkT=MrkT, vtm=vtm: e.matmul(pyg[0:n, g * 64:(g + 1) * 64], lhsT=MrkT, rhs=vtm, start=False, stop=True), reads=[MMb, TMb], writes=[pygb])
                    ph, phb = ps1.next()
                    P.op(PE, lambda e, ph=ph: e.matmul(ph[0:64, 0:64], lhsT=ident[0:64, 0:64], rhs=H[:, :], start=True, stop=False), reads=[idb, Hb], writes=[phb])
                    P.op(PE, lambda e, ph=ph, n=n, btm=btm, u=u: e.matmul(ph[0:64, 0:64], lhsT=btm, rhs=u[0:n, :], start=False, stop=False), reads=[TMb, ub_], writes=[phb])
                    P.op(PE, lambda e, ph=ph, n=n, ktm=ktm, vtm=vtm: e.matmul(ph[0:64, 0:64], lhsT=ktm, rhs=vtm, start=False, stop=True), reads=[TMb], writes=[phb])
                    ce = c0 + n - 1
                    P.op(DVE, lambda e, ph=ph, ce=ce: e.tensor_scalar(out=H[:], in0=ph[0:64, 0:64], scalar1=G1[:, ce:ce + 1], scalar2=None, op0=ALU.mult),
                         reads=[phb, G1b], writes=[Hb])
                if C.emit_out:
                    t0g = chs[0][2]
                    YG = pyg[0:n, 0:ng * 64].rearrange("p (g d) -> p g d", g=ng)
                    bc = lambda ap, n=n, ng=ng: ap.unsqueeze(2).broadcast_to([n, ng, 64])
                    P.op(DVE, lambda e, YG=YG, n=n, ng=ng: e.tensor_reduce(out=ST[0:n, 0, 0:ng], in_=YG, axis=AX.X, op=ALU.add), reads=[pygb], writes=[STb])
                    P.op(ACT, lambda e, YG=YG, n=n, ng=ng: e.activation(out=SQ[0:n, 0:ng, :], in_=YG, func=AF.Square), reads=[pygb], writes=[SQb])
                    P.op(DVE, lambda e, n=n, ng=ng: e.tensor_reduce(out=ST[0:n, 1, 0:ng], in_=SQ[0:n, 0:ng, :], axis=AX.X, op=ALU.add), reads=[SQb, STb], writes=[STb])
                    P.op(DVE, lambda e, n=n, ng=ng: e.tensor_scalar(out=ST[0:n, 2, 0:ng], in0=ST[0:n, 0, 0:ng], scalar1=1.0 / 64, scalar2=None, op0=ALU.mult), reads=[STb], writes=[STb])
                    P.op(DVE, lambda e, n=n, ng=ng: e.tensor_tensor(out=ST[0:n, 3, 0:ng], in0=ST[0:n, 2, 0:ng], in1=ST[0:n, 2, 0:ng], op=ALU.mult), reads=[STb], writes=[STb])
                    P.op(DVE, lambda e, n=n, ng=ng: e.tensor_scalar(out=ST[0:n, 4, 0:ng], in0=ST[0:n, 1, 0:ng], scalar1=1.0 / 64, scalar2=64e-5, op0=ALU.mult, op1=ALU.add),
                         reads=[STb], writes=[STb])
                    P.op(DVE, lambda e, n=n, ng=ng: e.tensor_tensor(out=ST[0:n, 4, 0:ng], in0=ST[0:n, 4, 0:ng], in1=ST[0:n, 3, 0:ng], op=ALU.subtract), reads=[STb], writes=[STb])
                    P.op(POOL, lambda e, n=n, ng=ng: e.tensor_tensor(out=ST[0:n, 5, 0:ng], in0=ST[0:n, 4, 0:ng], in1=mh[0:n, 0:ng], op=ALU.pow), reads=[STb, mhb], writes=[STb])
                    P.op(DVE, lambda e, YG=YG, n=n, ng=ng, bc=bc: e.tensor_tensor(out=T1[0:n, 0:ng, :], in0=YG, in1=bc(ST[0:n, 2, 0:ng]), op=ALU.subtract),
                         reads=[pygb, STb], writes=[T1b])
                    P.op(DVE, lambda e, n=n, ng=ng, bc=bc: e.tensor_tensor(out=T1[0:n, 0:ng, :], in0=T1[0:n, 0:ng, :], in1=bc(ST[0:n, 5, 0:ng]), op=ALU.mult),
                         reads=[T1b, STb], writes=[T1b])
                    P.op(DVE, lambda e, n=n, ng=ng, h=h: e.tensor_tensor(out=T1[0:n, 0:ng, :], in0=T1[0:n, 0:ng, :],
                                                                      in1=lnw[0:n, h * 64:(h + 1) * 64].unsqueeze(1).broadcast_to([n, ng, 64]), op=ALU.mult),
                         reads=[T1b, pb_], writes=[T1b])
                    P.op(DVE, lambda e, n=n, ng=ng, h=h: e.tensor_tensor(out=T1[0:n, 0:ng, :], in0=T1[0:n, 0:ng, :],
                                                                      in1=lnb[0:n, h * 64:(h + 1) * 64].unsqueeze(1).broadcast_to([n, ng, 64]), op=ALU.add),
                         reads=[T1b, pb_], writes=[T1b])
                    P.op(DVE, lambda e, n=n, ng=ng: e.tensor_tensor(out=SQ[0:n, 0:ng, :], in0=TM[0:n, 0:ng, 0, :], in1=GB[0:n, 0:ng, 64:65].broadcast_to([n, ng, 64]), op=ALU.mult),
                         reads=[TMb, GBb, SQb], writes=[SQb])
                    P.op(DVE, lambda e, n=n, ng=ng: e.tensor_tensor(out=T1[0:n, 0:ng, :], in0=T1[0:n, 0:ng, :], in1=SQ[0:n, 0:ng, :], op=ALU.add), reads=[T1b, SQb], writes=[T1b])
                    P.op(DVE, lambda e, n=n, ng=ng: e.tensor_tensor(out=YO[0:n, 0:ng, :], in0=T1[0:n, 0:ng, :], in1=GB[0:n, 0:ng, 0:64], op=ALU.mult),
                         reads=[T1b, GBb], writes=[YOb])
                    P.dma(SP, y[t0g:t0g + ng * n, h * 64:(h + 1) * 64].rearrange("(g p) d -> p g d", p=n), YO[0:n, 0:ng, :], reads=[YOb], pwrites=[yb])
        P.dma(POOL, prm["sA_out"][h], H[:], reads=[Hb], pwrites=[K.sob])


def mixer_rwkv2(C, st, pf, pfb, y, yb, prm, K):
    P = C.P
    ones, onesb, ident, idb, flagE, fb = K.ones, K.onesb, K.ident, K.idb, K.flagE, K.fb
    mask5, m5b = K.mask5, K.m5b
    muA = sbuf(C, st, "bmuA", [64, 3, 8]); muL = sbuf(C, st, "bmuL", [96, 3]); w2 = sbuf(C, st, "bw2", [32, 512]); a2 = sbuf(C, st, "ba2", [32, 512])
    g2 = sbuf(C, st, "bg2", [96, 512]); ch = sbuf(C, st, "bch", [64, 5, 8]); rk = sbuf(C, st, "brk", [64, 8, 2])
    lnw = sbuf(C, st, "blnw", [64, 512]); lnb = sbuf(C, st, "blnb", [64, 512])
    pb_ = Buf()
    for t, n_ in ((muA, "rw_muA"), (muL, "rw_muL"), (w2, "rw_w2"), (a2, "rw_a2"), (g2, "rw_g2"), (rk, "rw_rk"), (lnw, "rw_lnw_bc"), (lnb, "rw_lnb_bc")):
        P.dma(SP, t[:], prm[n_], pwrites=[pb_])
    P.dma(SP, ch[:, 0:4, :], prm["rw_ch"], pwrites=[pb_])
    P.op(DVE, lambda e: e.tensor_scalar(out=ch[:, 4, :], in0=ch[:, 3, :], scalar1=-1.0, scalar2=1.0, op0=ALU.mult, op1=ALU.add), reads=[pb_], writes=[pb_])
    WM = 128
    W1M = WM + 1
    mh = sbuf(C, st, "bmh", [64, 8 * WM], F32); mhb = Buf()
    P.op(POOL, lambda e: e.memset(mh[:], -0.5), writes=[mhb])
    names = ["pr", "pk", "pv", "xr", "xk", "xv", "sgz", "asig", "kkn", "t1", "rel", "G1", "G2"]
    X = {nm: sbuf(C, st, "bX" + nm, [64, 8, W1M]) for nm in names}
    Xb = {nm: Buf() for nm in names}
    Ssc = sbuf(C, st, "bSsc", [64, 1 + 8 * WM]); Sscb = Buf()
    P.op(DVE, lambda e: e.memset(Ssc[:, 0:1], 0.0), writes=[Sscb])
    lraw = sbuf(C, st, "blraw", [96, 3, W1M]); lrawb = Buf()
    thw = sbuf(C, st, "bthw", [32, WM]); xal = sbuf(C, st, "bxal", [32, WM]); sg = sbuf(C, st, "bsg", [96, WM])
    thwb, xalb, sgb = Buf(), Buf(), Buf()
    hs = sbuf(C, st, "bhs", [96, 2, 8]); hsb = Buf()
    H = sbuf(C, st, "bH", [64, 8, 64]); Hb = Buf()
    Hin = sbuf(C, st, "bHin", [64, 8, 64]); Hinb = Buf()
    NQ = 16
    TM = sbuf(C, st, "bTM", [64, 3, NQ, 64]); TMb = Buf()
    MM = sbuf(C, st, "bMM", [64, 5, NQ, 64]); MMb = Buf()
    NN = [sbuf(C, st, f"bNN{i}", [64, NQ, 2, 64]) for i in range(2)]; NNb = [Buf(), Buf()]
    Pm = sbuf(C, st, "bPm", [64, NQ, 64]); Pmb = Buf()
    W0s = sbuf(C, st, "bW0", [64, 8, 64]); W0b = Buf()
    Us = sbuf(C, st, "bUs", [64, 8, 64]); Usb = Buf()
    GBs = sbuf(C, st, "bGB", [64, 528]); GBb = Buf()
    T1 = sbuf(C, st, "bT1", [64, 8, 64]); T1b = Buf()
    SQ = sbuf(C, st, "bSQ", [64, 8, 64]); SQb = Buf()
    YO = sbuf(C, st, "bYO", [64, 8, 64], BF16); YOb = Buf()
    ST = sbuf(C, st, "bST", [64, 6, 8]); STb = Buf()
    bank = [psum(C, st, f"bpb{i}", [128, 512], F32) for i in range(8)]
    bkb = [Buf() for _ in range(8)]
    P.op(DVE, lambda e: e.memset(H[:], 0.0), writes=[Hb])

    def bc8(ap, W):
        return ap.unsqueeze(2).broadcast_to([64, 8, W])

    def vop(fn, reads, writes, pwrites=()):
        P.op(DVE, fn, reads=reads, writes=writes, pwrites=pwrites)

    Fl = sbuf(C, st, "bFl", [64, 8 * WM]); Flb = Buf()

    scs = [(3, 16, 0, 16)] + [(22 + 128 * i, 128, 16 + 128 * i, 64) for i in range(16)]
    def do_sc(sci, c0, W, t0, n):
        W1 = W + 1
        nch = W // n
        nq = 8 * nch
        cur = lambda nm: X[nm][:, :, 1:W1]
        prev = lambda nm: X[nm][:, :, 0:W]
        P.phase = "rwkv_pre"
        for i, nm in enumerate(("pr", "pk", "pv")):
            P.dma(SP, X[nm][:, :, 0:W1], pf[i * 512:(i + 1) * 512, c0 - 1:c0 + W].rearrange("(h d) c -> d h c", d=64), reads=[pfb], writes=[Xb[nm]])
        for j, (r0, nr) in enumerate(((12 * 128, 32), (12 * 128 + 32, 32), (13 * 128, 96))):
            P.dma(SP, lraw[0:nr, j, 0:W1], pf[r0:r0 + nr, c0 - 1:c0 + W], reads=[pfb], writes=[lrawb])
        if sci == 0:
            for nm in ("pr", "pk", "pv"):
                vop(lambda e, nm=nm: e.memset(X[nm][:, :, 0:1], 0.0), [Xb[nm]], [Xb[nm]])
            vop(lambda e: e.memset(lraw[:, :, 0:1], 0.0), [lrawb], [lrawb])
        if sci == 1:
            for i, nm in enumerate(("pr", "pk", "pv")):
                P.dma(SP, hs[0:64, 0, :], pf[i * 512:(i + 1) * 512, 18:19].rearrange("(h d) c -> d (h c)", d=64), reads=[pfb], writes=[hsb], allow_slow_non_contiguous=True)
                P.dma(SP, hs[0:64, 1, :], prm["hist_in"][i * 512:(i + 1) * 512, 2:3].rearrange("(h d) c -> d (h c)", d=64), reads=[hsb], writes=[hsb], allow_slow_non_contiguous=True)
                vop(lambda e, nm=nm: e.scalar_tensor_tensor(out=X[nm][:, :, 0:1], in0=hs[0:64, 0, :].unsqueeze(2), scalar=flagE[0:64, 0:1], in1=hs[0:64, 1, :].unsqueeze(2),
                                                            op0=ALU.mult, op1=ALU.add), [hsb, fb, Xb[nm]], [Xb[nm]])
            for j, (r0, nr) in enumerate(((12 * 128, 32), (12 * 128 + 32, 32), (13 * 128, 96))):
                P.dma(SP, hs[0:nr, 0, 0:1], pf[r0:r0 + nr, 18:19], reads=[pfb, hsb], writes=[hsb], allow_slow_non_contiguous=True)
                P.dma(SP, hs[0:nr, 1, 0:1], prm["hist_in"][r0:r0 + nr, 2:3], reads=[hsb], writes=[hsb], allow_slow_non_contiguous=True)
                vop(lambda e, j=j, nr=nr: e.scalar_tensor_tensor(out=lraw[0:nr, j, 0:1], in0=hs[0:nr, 0, 0:1], scalar=flagE[0:nr, 0:1], in1=hs[0:nr, 1, 0:1],
                                                                 op0=ALU.mult, op1=ALU.add), [hsb, fb, lrawb], [lrawb])
            P.dma(SP, Hin[:], prm["sA_in"].rearrange("h k v -> k h v"), writes=[Hinb])
            vop(lambda e: e.scalar_tensor_tensor(out=H[:], in0=H[:], scalar=flagE[0:64, 0:1], in1=Hin[:], op0=ALU.mult, op1=ALU.add), [Hb, Hinb, fb], [Hb])
        for i, (src, dst) in enumerate((("pr", "xr"), ("pk", "xk"), ("pv", "xv"))):
            vop(lambda e, src=src, dst=dst: e.tensor_tensor(out=cur(dst), in0=prev(src), in1=cur(src), op=ALU.subtract), [Xb[src]], [Xb[dst]])
            vop(lambda e, dst=dst, i=i: e.tensor_tensor(out=cur(dst), in0=cur(dst), in1=bc8(muA[:, i, :], W), op=ALU.mult), [Xb[dst], pb_], [Xb[dst]])
            vop(lambda e, src=src, dst=dst: e.tensor_tensor(out=cur(dst), in0=cur(dst), in1=cur(src), op=ALU.add), [Xb[dst], Xb[src]], [Xb[dst]])
        for j, (dst, dstb, nr, fn) in enumerate(((thw, thwb, 32, AF.Tanh), (xal, xalb, 32, None), (sg, sgb, 96, AF.Sigmoid))):
            vop(lambda e, dst=dst, nr=nr, j=j: e.tensor_tensor(out=dst[0:nr, 0:W], in0=lraw[0:nr, j, 0:W], in1=lraw[0:nr, j, 1:W1], op=ALU.subtract), [lrawb], [dstb])
            vop(lambda e, dst=dst, nr=nr, j=j: e.scalar_tensor_tensor(out=dst[0:nr, 0:W], in0=dst[0:nr, 0:W], scalar=muL[0:nr, j:j + 1], in1=lraw[0:nr, j, 1:W1],
                                                                    op0=ALU.mult, op1=ALU.add), [lrawb, dstb, pb_], [dstb])
            if fn is not None:
                P.op(ACT, lambda e, dst=dst, nr=nr, fn=fn: e.activation(out=dst[0:nr, 0:W], in_=dst[0:nr, 0:W], func=fn), reads=[dstb], writes=[dstb])
        for (wt_, src, srcb, dst, chi, b0) in ((w2, thw, thwb, "sgz", 0, 0), (a2, xal, xalb, "asig", 1, 2)):
            for h in range(8):
                bk = b0 + (h * W) // 512
                off = (h * W) % 512
                P.op(PE, lambda e, bk=bk, off=off, wt_=wt_, src=src, h=h: e.matmul(bank[bk][0:64, off:off + W], lhsT=wt_[0:32, h * 64:(h + 1) * 64], rhs=src[0:32, 0:W],
                                                                                  start=True, stop=True), reads=[pb_, srcb], writes=[bkb[bk]])
            nb = (8 * W + 511) // 512
            for b in range(nb):
                h0 = b * (512 // W) if W >= 64 else 0
                nh = (512 // W) if W >= 64 else 8
                vop(lambda e, b=b, b0=b0, dst=dst, chi=chi, h0=h0, nh=nh: e.tensor_tensor(
                    out=X[dst][:, h0:h0 + nh, 1:W1], in0=bank[b0 + b][0:64, 0:nh * W].rearrange("p (h w) -> p h w", h=nh),
                    in1=ch[:, chi, h0:h0 + nh].unsqueeze(2).broadcast_to([64, nh, W]), op=ALU.add), [bkb[b0 + b], pb_], [Xb[dst]])
            P.op(ACT, lambda e, dst=dst: e.activation(out=cur(dst), in_=cur(dst), func=AF.Sigmoid), reads=[Xb[dst]], writes=[Xb[dst]])
        vop(lambda e: e.tensor_tensor(out=cur("kkn"), in0=cur("xk"), in1=bc8(ch[:, 2, :], W), op=ALU.mult), [Xb["xk"], pb_], [Xb["kkn"]])
        vop(lambda e: e.tensor_tensor(out=Fl[:, 0:8 * W].rearrange("p (h w) -> p h w", h=8), in0=cur("kkn"), in1=cur("kkn"), op=ALU.mult), [Xb["kkn"]], [Flb])
        nb = (8 * W + 511) // 512
        for b in range(nb):
            nn_ = min(512, 8 * W - b * 512)
            P.op(PE, lambda e, b=b, nn_=nn_: e.matmul(bank[4 + b][0:64, 0:nn_], lhsT=ones[0:64, 0:64], rhs=Fl[:, b * 512:b * 512 + nn_], start=True, stop=True),
                 reads=[onesb, Flb], writes=[bkb[4 + b]])
        for b in range(nb):
            nn_ = min(512, 8 * W - b * 512)
            vop(lambda e, b=b, nn_=nn_: e.tensor_scalar(out=Fl[:, b * 512:b * 512 + nn_], in0=bank[4 + b][0:64, 0:nn_],
                                                        scalar1=1e-24, scalar2=None, op0=ALU.max), [bkb[4 + b], Flb], [Flb])
        relf = Fl[:, 0:8 * W]
        P.op(POOL, lambda e, relf=relf: e.tensor_tensor(out=relf, in0=relf, in1=mh[:, 0:8 * W], op=ALU.pow), reads=[Flb, mhb], writes=[Flb])
        vop(lambda e, relf=relf: e.tensor_tensor(out=cur("kkn"), in0=cur("kkn"), in1=relf.rearrange("p (h w) -> p h w", h=8), op=ALU.mult),
            [Xb["kkn"], Flb], [Xb["kkn"]])
        vop(lambda e: e.tensor_tensor(out=cur("t1"), in0=cur("asig"), in1=bc8(ch[:, 3, :], W), op=ALU.mult), [Xb["asig"], pb_], [Xb["t1"]])
        vop(lambda e: e.tensor_tensor(out=cur("t1"), in0=cur("t1"), in1=bc8(ch[:, 4, :], W), op=ALU.add), [Xb["t1"], pb_], [Xb["t1"]])
        vop(lambda e: e.tensor_tensor(out=cur("xk"), in0=cur("xk"), in1=cur("t1"), op=ALU.mult), [Xb["xk"], Xb["t1"]], [Xb["xk"]])
        vop(lambda e: e.tensor_tensor(out=cur("asig"), in0=cur("asig"), in1=cur("kkn"), op=ALU.mult), [Xb["asig"], Xb["kkn"]], [Xb["asig"]])
        vop(lambda e: e.tensor_tensor(out=cur("t1"), in0=cur("xr"), in1=cur("xk"), op=ALU.mult), [Xb["xr"], Xb["xk"], Xb["t1"]], [Xb["t1"]])
        vop(lambda e: e.tensor_tensor(out=cur("pr"), in0=cur("t1"), in1=bc8(rk[:, :, 0], W), op=ALU.mult), [Xb["t1"], pb_, Xb["pr"], Xb["xr"]], [Xb["pr"]])
        vop(lambda e: e.tensor_copy(out=Fl[:, 0:8 * W].rearrange("p (h w) -> p h w", h=8), in_=cur("sgz")), [Xb["sgz"], Flb], [Flb])
        vop(lambda e: e.tensor_tensor_scan(out=Ssc[:, 1:1 + 8 * W], data0=ones[0:64, 0:8 * W], data1=Fl[:, 0:8 * W], initial=0.0, op0=ALU.mult, op1=ALU.add),
            [Flb, Sscb, onesb], [Sscb])
        vop(lambda e: e.tensor_tensor(out=cur("rel").rearrange("p h (c j) -> p h c j", j=n),
                                      in0=Ssc[:, 1:1 + 8 * W].rearrange("p (h c j) -> p h c j", h=8, j=n),
                                      in1=Ssc[:, 0:8 * W].rearrange("p (h c j) -> p h c j", h=8, j=n)[:, :, :, 0:1].broadcast_to([64, 8, nch, n]), op=ALU.subtract),
            [Sscb, Xb["rel"]], [Xb["rel"]])
        P.op(ACT, lambda e: e.activation(out=cur("G1"), in_=cur("rel"), func=AF.Exp, scale=-LDK), reads=[Xb["rel"]], writes=[Xb["G1"]])
        P.op(ACT, lambda e: e.activation(out=cur("G2"), in_=cur("rel"), func=AF.Exp, scale=LDK), reads=[Xb["rel"]], writes=[Xb["G2"]])
        vop(lambda e: e.tensor_tensor(out=cur("rel"), in0=cur("rel"), in1=cur("sgz"), op=ALU.subtract), [Xb["rel"], Xb["sgz"]], [Xb["rel"]])
        P.op(ACT, lambda e: e.activation(out=cur("rel"), in_=cur("rel"), func=AF.Exp, scale=-LDK), reads=[Xb["rel"]], writes=[Xb["rel"]])
        vop(lambda e: e.tensor_tensor(out=cur("xr"), in0=cur("xr"), in1=cur("G1"), op=ALU.mult), [Xb["xr"], Xb["G1"]], [Xb["xr"]])
        vop(lambda e: e.tensor_tensor(out=cur("xk"), in0=cur("xk"), in1=cur("G2"), op=ALU.mult), [Xb["xk"], Xb["G2"]], [Xb["xk"]])
        vop(lambda e: e.tensor_tensor(out=cur("asig"), in0=cur("asig"), in1=cur("G2"), op=ALU.mult), [Xb["asig"], Xb["G2"]], [Xb["asig"]])
        vop(lambda e: e.scalar_tensor_tensor(out=cur("kkn"), in0=cur("kkn"), scalar=-1.0, in1=cur("rel"), op0=ALU.mult, op1=ALU.mult),
            [Xb["kkn"], Xb["rel"]], [Xb["kkn"]])
        RT, KT, BT, AT, XV, PRK, G1 = "xr", "xk", "asig", "kkn", "xv", "pr", "G1"
        col = lambda nm, h, c: X[nm][:, h, 1 + c * n:1 + (c + 1) * n]
        qi = lambda h, c: h * nch + c
        P.phase = "rwkv_gram"
        for a, nm in enumerate((XV, KT, BT)):
            for h in range(8):
                for c in range(nch):
                    q = qi(h, c)
                    bk, off = (q * 64) // 512, (q * 64) % 512
                    P.op(PE, lambda e, bk=bk, off=off, nm=nm, h=h, c=c: e.transpose(out=bank[bk][0:n, off:off + 64], in_=col(nm, h, c), identity=ident[0:64, 0:64]),
                         reads=[Xb[nm], idb], writes=[bkb[bk]])
            for b in range((nq * 64 + 511) // 512):
                qn = min(8, nq - b * 8)
                P.op(ACT, lambda e, a=a, b=b, qn=qn: e.activation(out=TM[0:n, a, b * 8:b * 8 + qn, :], in_=bank[b][0:n, 0:qn * 64].rearrange("p (q d) -> p q d", q=qn),
                                                               func=AF.Copy), reads=[bkb[b]], pwrites=[TMb])
        pairs = ((KT, AT), (KT, RT), (BT, AT), (BT, RT), (AT, BT))
        for j, (l_, r_) in enumerate(pairs):
            b0 = 4 if j % 2 else 0
            for h in range(8):
                for c in range(nch):
                    q = qi(h, c)
                    bk, off = b0 + (q * 64) // 512, (q * 64) % 512
                    P.op(PE, lambda e, bk=bk, off=off, l_=l_, r_=r_, h=h, c=c: e.matmul(bank[bk][0:n, off:off + n], lhsT=col(l_, h, c), rhs=col(r_, h, c), start=True, stop=True),
                         reads=[Xb[l_], Xb[r_]], writes=[bkb[bk]])
            for b in range((nq * 64 + 511) // 512):
                qn = min(8, nq - b * 8)
                vop(lambda e, j=j, b=b, b0=b0, qn=qn: e.tensor_tensor(out=MM[0:n, j, b * 8:b * 8 + qn, 0:n],
                                                                    in0=bank[b0 + b][0:n, 0:qn * 64].rearrange("p (q d) -> p q d", q=qn)[:, :, 0:n],
                                                                    in1=mask5[0:n, j, 0:n].unsqueeze(1).broadcast_to([n, qn, n]), op=ALU.mult),
                    [bkb[b0 + b], m5b], [], pwrites=[MMb])
        P.phase = "rwkv_inv"
        vop(lambda e: e.tensor_tensor(out=Pm[0:n, 0:nq, 0:n], in0=MM[0:n, 2, 0:nq, 0:n], in1=ident[0:n, 0:n].unsqueeze(1).broadcast_to([n, nq, n]), op=ALU.add),
            [MMb, idb], [Pmb])
        curN = lambda q: MM[0:n, 2, q, 0:n]
        curNT = lambda q: MM[0:n, 4, q, 0:n]
        curb = MMb
        nlev = 5 if n == 64 else 3
        for lev in range(nlev):
            nn, nnb = NN[lev % 2], NNb[lev % 2]
            for q in range(nq):
                bk, off = (q * 128) // 512, (q * 128) % 512
                P.op(PE, lambda e, bk=bk, off=off, a_=curNT(q), b_=curN(q): e.matmul(bank[bk][0:n, off:off + n], lhsT=a_, rhs=b_, start=True, stop=True), reads=[curb], writes=[bkb[bk]])
                P.op(PE, lambda e, bk=bk, off=off, a_=curN(q), b_=curNT(q): e.matmul(bank[bk][0:n, off + 64:off + 64 + n], lhsT=a_, rhs=b_, start=True, stop=True), reads=[curb], writes=[bkb[bk]])
            for b in range((nq * 128 + 511) // 512):
                qn = min(4, nq - b * 4)
                P.op(ACT, lambda e, nn=nn, b=b, qn=qn: e.activation(out=nn[0:n, b * 4:b * 4 + qn, :, 0:n],
                                                                 in_=bank[b][0:n, 0:qn * 128].rearrange("p (q j d) -> p q j d", q=qn, j=2)[:, :, :, 0:n], func=AF.Copy),
                     reads=[bkb[b]], pwrites=[nnb])
            curN = lambda q, nn=nn: nn[0:n, q, 0, 0:n]
            curNT = lambda q, nn=nn: nn[0:n, q, 1, 0:n]
            curb = nnb
            for q in range(nq):
                bk, off = 4 + (q * 64) // 512, (q * 64) % 512
                P.op(PE, lambda e, bk=bk, off=off, a_=curNT(q), q=q: e.matmul(bank[bk][0:n, off:off + n], lhsT=a_, rhs=Pm[0:n, q, 0:n], start=True, stop=True), reads=[curb, Pmb], writes=[bkb[bk]])
            for b in range((nq * 64 + 511) // 512):
                qn = min(8, nq - b * 8)
                vop(lambda e, b=b, qn=qn: e.tensor_tensor(out=Pm[0:n, b * 8:b * 8 + qn, 0:n], in0=Pm[0:n, b * 8:b * 8 + qn, 0:n],
                                                          in1=bank[4 + b][0:n, 0:qn * 64].rearrange("p (q d) -> p q d", q=qn)[:, :, 0:n], op=ALU.add),
                    [bkb[4 + b], Pmb], [Pmb])
        P.phase = "rwkv_chain"
        for c in range(nch):
            tc0 = t0 + c * n
            for h in range(8):
                q = qi(h, c)
                P.op(PE, lambda e, h=h, c=c: e.matmul(bank[0][0:n, h * 64:(h + 1) * 64], lhsT=col(AT, h, c), rhs=H[:, h, :], start=True, stop=False), reads=[Xb[AT], Hb], writes=[bkb[0]])
                P.op(PE, lambda e, h=h, q=q: e.matmul(bank[0][0:n, h * 64:(h + 1) * 64], lhsT=MM[0:n, 0, q, 0:n], rhs=TM[0:n, 0, q, :], start=False, stop=True), reads=[MMb, TMb], writes=[bkb[0]])
            P.op(ACT, lambda e: e.activation(out=W0s[0:n, :, :], in_=bank[0][0:n, 0:512].rearrange("p (h d) -> p h d", h=8), func=AF.Copy), reads=[bkb[0]], writes=[W0b])
            for h in range(8):
                q = qi(h, c)
                P.op(PE, lambda e, h=h, q=q: e.matmul(bank[1][0:n, h * 64:(h + 1) * 64], lhsT=Pm[0:n, q, 0:n], rhs=W0s[0:n, h, :], start=True, stop=True), reads=[Pmb, W0b], writes=[bkb[1]])
            vop(lambda e: e.tensor_copy(out=Us[0:n, :, :], in_=bank[1][0:n, 0:512].rearrange("p (h d) -> p h d", h=8)), [bkb[1]], [Usb])
            if C.emit_out:
                for h in range(8):
                    q = qi(h, c)
                    P.op(PE, lambda e, h=h, c=c: e.matmul(bank[2][0:n, h * 64:(h + 1) * 64], lhsT=col(RT, h, c), rhs=H[:, h, :], start=True, stop=False), reads=[Xb[RT], Hb], writes=[bkb[2]])
                    P.op(PE, lambda e, h=h, q=q: e.matmul(bank[2][0:n, h * 64:(h + 1) * 64], lhsT=MM[0:n, 3, q, 0:n], rhs=Us[0:n, h, :], start=False, stop=False), reads=[MMb, Usb], writes=[bkb[2]])
                    P.op(PE, lambda e, h=h, q=q: e.matmul(bank[2][0:n, h * 64:(h + 1) * 64], lhsT=MM[0:n, 1, q, 0:n], rhs=TM[0:n, 0, q, :], start=False, stop=True), reads=[MMb, TMb], writes=[bkb[2]])
            for h in range(8):
                q = qi(h, c)
                P.op(PE, lambda e, h=h: e.matmul(bank[3][0:64, h * 64:(h + 1) * 64], lhsT=ident[0:64, 0:64], rhs=H[:, h, :], start=True, stop=False), reads=[idb, Hb], writes=[bkb[3]])
                P.op(PE, lambda e, h=h, q=q: e.matmul(bank[3][0:64, h * 64:(h + 1) * 64], lhsT=TM[0:n, 2, q, :], rhs=Us[0:n, h, :], start=False, stop=False), reads=[TMb, Usb], writes=[bkb[3]])
                P.op(PE, lambda e, h=h, q=q: e.matmul(bank[3][0:64, h * 64:(h + 1) * 64], lhsT=TM[0:n, 1, q, :], rhs=TM[0:n, 0, q, :], start=False, stop=True), reads=[TMb], writes=[bkb[3]])
            ce = 1 + (c + 1) * n - 1
            vop(lambda e, ce=ce: e.tensor_tensor(out=H[:], in0=bank[3][0:64, 0:512].rearrange("p (h d) -> p h d", h=8),
                                                 in1=X[G1][:, :, ce:ce + 1].broadcast_to([64, 8, 64]), op=ALU.mult), [bkb[3], Xb[G1]], [Hb])
            if C.emit_out:
                P.op(PE, lambda e, c=c: e.matmul(bank[4][0:n, 0:512], lhsT=sg[0:96, c * n:(c + 1) * n], rhs=g2[0:96, :], start=True, stop=True), reads=[sgb, pb_], writes=[bkb[4]])
                for h in range(8):
                    P.op(PE, lambda e, h=h, c=c: e.matmul(bank[5][0:n, 2 * h:2 * h + 2], lhsT=col(PRK, h, c), rhs=ones[0:64, 0:2], start=True, stop=True), reads=[Xb[PRK], onesb], writes=[bkb[5]])
                P.op(ACT, lambda e: e.activation(out=GBs[0:n, 0:512], in_=bank[4][0:n, 0:512], func=AF.Copy), reads=[bkb[4]], writes=[GBb])
                P.op(ACT, lambda e: e.activation(out=GBs[0:n, 512:528], in_=bank[5][0:n, 0:16], func=AF.Copy), reads=[bkb[5], GBb], writes=[GBb])
                YG = bank[2][0:n, 0:512].rearrange("p (h d) -> p h d", h=8)
                bcn = lambda ap: ap.unsqueeze(2).broadcast_to([n, 8, 64])
                vop(lambda e, YG=YG: e.tensor_reduce(out=ST[0:n, 0, :], in_=YG, axis=AX.X, op=ALU.add), [bkb[2]], [STb])
                P.op(ACT, lambda e, YG=YG: e.activation(out=SQ[0:n, :, :], in_=YG, func=AF.Square), reads=[bkb[2]], writes=[SQb])
                vop(lambda e: e.tensor_reduce(out=ST[0:n, 1, :], in_=SQ[0:n, :, :], axis=AX.X, op=ALU.add), [SQb, STb], [STb])
                vop(lambda e: e.tensor_scalar(out=ST[0:n, 2, :], in0=ST[0:n, 0, :], scalar1=1.0 / 64, scalar2=None, op0=ALU.mult), [STb], [STb])
                vop(lambda e: e.tensor_tensor(out=ST[0:n, 3, :], in0=ST[0:n, 2, :], in1=ST[0:n, 2, :], op=ALU.mult), [STb], [STb])
                vop(lambda e: e.tensor_scalar(out=ST[0:n, 4, :], in0=ST[0:n, 1, :], scalar1=1.0 / 64, scalar2=64e-5, op0=ALU.mult, op1=ALU.add), [STb], [STb])
                vop(lambda e: e.tensor_tensor(out=ST[0:n, 4, :], in0=ST[0:n, 4, :], in1=ST[0:n, 3, :], op=ALU.subtract), [STb], [STb])
                P.op(POOL, lambda e: e.tensor_tensor(out=ST[0:n, 5, :], in0=ST[0:n, 4, :], in1=mh[0:n, 0:8], op=ALU.pow), reads=[STb, mhb], writes=[STb])
                vop(lambda e, YG=YG, bcn=bcn: e.tensor_tensor(out=T1[0:n, :, :], in0=YG, in1=bcn(ST[0:n, 2, :]), op=ALU.subtract), [bkb[2], STb], [T1b])
                vop(lambda e, bcn=bcn: e.tensor_tensor(out=T1[0:n, :, :], in0=T1[0:n, :, :], in1=bcn(ST[0:n, 5, :]), op=ALU.mult), [T1b, STb], [T1b])
                vop(lambda e: e.tensor_tensor(out=T1[0:n, :, :], in0=T1[0:n, :, :], in1=lnw[0:n, :].rearrange("p (h d) -> p h d", h=8), op=ALU.mult), [T1b, pb_], [T1b])
                vop(lambda e: e.tensor_tensor(out=T1[0:n, :, :], in0=T1[0:n, :, :], in1=lnb[0:n, :].rearrange("p (h d) -> p h d", h=8), op=ALU.add), [T1b, pb_], [T1b])
                vtm_c = TM[0:n, 0, 0:nq, :].rearrange("p (h c) d -> p h c d", c=nch)[:, :, c, :]
                bs_c = GBs[0:n, 512:528].rearrange("p (h t) -> p h t", t=2)[:, :, 0:1].broadcast_to([n, 8, 64])
                vop(lambda e, vtm_c=vtm_c, bs_c=bs_c: e.tensor_tensor(out=SQ[0:n, :, :], in0=vtm_c, in1=bs_c, op=ALU.mult), [TMb, GBb, SQb], [SQb])
                vop(lambda e: e.tensor_tensor(out=T1[0:n, :, :], in0=T1[0:n, :, :], in1=SQ[0:n, :, :], op=ALU.add), [T1b, SQb], [T1b])
                vop(lambda e: e.tensor_tensor(out=YO[0:n, :, :], in0=T1[0:n, :, :], in1=GBs[0:n, 0:512].rearrange("p (h d) -> p h d", h=8), op=ALU.mult), [T1b, GBb], [YOb])
                P.dma(SP, y[tc0:tc0 + n, 0:512], YO[0:n, :, :].rearrange("p h d -> p (h d)"), reads=[YOb], pwrites=[yb])
    for sci, (c0, W, t0, n) in enumerate(scs):
        do_sc(sci, c0, W, t0, n)
    P.dma(POOL, prm["sA_out"].rearrange("h k v -> k h v"), H[:], reads=[Hb], pwrites=[K.sob])


def mixer_rwkv3(C, st, pf, pfb, y, yb, prm, K):
    P = C.P
    ones, onesb, ident, idb, flagE, fb = K.ones, K.onesb, K.ident, K.idb, K.flagE, K.fb
    mask5, m5b = K.mask5, K.m5b
    muA = sbuf(C, st, "cmuA", [64, 3, 8]); muL = sbuf(C, st, "cmuL", [96, 3]); w2 = sbuf(C, st, "cw2", [32, 512]); a2 = sbuf(C, st, "ca2", [32, 512])
    g2 = sbuf(C, st, "cg2", [96, 512]); ch = sbuf(C, st, "cch", [64, 5, 8]); rk = sbuf(C, st, "crk", [64, 8, 2])
    lnw = sbuf(C, st, "clnw", [64, 512]); lnb = sbuf(C, st, "clnb", [64, 512])
    pb_ = Buf()
    for t, n_ in ((muA, "rw_muA"), (muL, "rw_muL"), (w2, "rw_w2"), (a2, "rw_a2"), (g2, "rw_g2"), (rk, "rw_rk"), (lnw, "rw_lnw_bc"), (lnb, "rw_lnb_bc")):
        P.dma(SP, t[:], prm[n_], pwrites=[pb_])
    P.dma(SP, ch[:, 0:4, :], prm["rw_ch"], pwrites=[pb_])
    P.op(DVE, lambda e: e.tensor_scalar(out=ch[:, 4, :], in0=ch[:, 3, :], scalar1=-1.0, scalar2=1.0, op0=ALU.mult, op1=ALU.add), reads=[pb_], writes=[pb_])
    WM = 128
    W1M = WM + 1
    mh = sbuf(C, st, "cmh", [64, 8 * WM], F32); mhb = Buf()
    P.op(POOL, lambda e: e.memset(mh[:], -0.5), writes=[mhb])
    names = ["pr", "pk", "pv", "xr", "xk", "xv", "sgz", "asig", "kkn", "t1", "rel", "G1", "G2"]
    X = {nm: sbuf(C, st, "cX" + nm, [64, 8, W1M]) for nm in names}
    Xb = {nm: Buf() for nm in names}
    Ssc = sbuf(C, st, "cSsc", [64, 1 + 8 * WM]); Sscb = Buf()
    P.op(DVE, lambda e: e.memset(Ssc[:, 0:1], 0.0), writes=[Sscb])
    lraw = sbuf(C, st, "clraw", [96, 3, W1M]); lrawb = Buf()
    thw = sbuf(C, st, "cthw", [32, WM]); xal = sbuf(C, st, "cxal", [32, WM]); sg = sbuf(C, st, "csg", [96, WM])
    thwb, xalb, sgb = Buf(), Buf(), Buf()
    hs = sbuf(C, st, "chs", [96, 2, 8]); hsb = Buf()
    H = sbuf(C, st, "cH", [64, 8, 64]); Hb = Buf()
    Hin = sbuf(C, st, "cHin", [64, 8, 64]); Hinb = Buf()
    NQ = 16
    TM = sbuf(C, st, "cTM", [64, 3, NQ, 64], BF16); TMb = Buf()
    MM = sbuf(C, st, "cMM", [64, 5, NQ, 64], BF16); MMb = Buf()
    NN = [sbuf(C, st, f"cNN{i}", [64, NQ, 2, 64], BF16) for i in range(2)]; NNb = [Buf(), Buf()]
    Pm = sbuf(C, st, "cPm", [64, NQ, 64], BF16); Pmb = Buf()
    W0s = sbuf(C, st, "cW0", [64, 8, 64], BF16); W0b = Buf()
    Us = sbuf(C, st, "cUs", [64, 8, 64], BF16); Usb = Buf()
    GBs = sbuf(C, st, "cGB", [64, 528]); GBb = Buf()
    T1 = sbuf(C, st, "cT1", [64, 8, 64]); T1b = Buf()
    SQ = sbuf(C, st, "cSQ", [64, 8, 64]); SQb = Buf()
    YO = sbuf(C, st, "cYO", [64, 8, 64], BF16); YOb = Buf()
    ST = sbuf(C, st, "cST", [64, 6, 8]); STb = Buf()
    bank = [psum(C, st, f"cpb{i}", [128, 512], F32) for i in range(8)]
    bkb = [Buf() for _ in range(8)]
    P.op(DVE, lambda e: e.memset(H[:], 0.0), writes=[Hb])
    H16 = sbuf(C, st, "cH16", [64, 8, 64], BF16); H16b = Buf()
    P.op(DVE, lambda e: e.memset(H16[:], 0.0), writes=[H16b])
    XBF = {nm: sbuf(C, st, "cXB" + nm, [64, 8, W1M], BF16) for nm in ("xr", "xk", "asig", "kkn")}
    XBFb = {nm: Buf() for nm in XBF}

    def bc8(ap, W):
        return ap.unsqueeze(2).broadcast_to([64, 8, W])

    def vop(fn, reads, writes, pwrites=()):
        P.op(DVE, fn, reads=reads, writes=writes, pwrites=pwrites)

    Fl = sbuf(C, st, "cFl", [64, 8 * WM]); Flb = Buf()

    scs = [(3, 16, 0, 16)] + [(22 + 128 * i, 128, 16 + 128 * i, 64) for i in range(16)]
    def do_sc(sci, c0, W, t0, n):
        W1 = W + 1
        nch = W // n
        nq = 8 * nch
        cur = lambda nm: X[nm][:, :, 1:W1]
        prev = lambda nm: X[nm][:, :, 0:W]
        P.phase = "rwkv_pre"
        for i, nm in enumerate(("pr", "pk", "pv")):
            P.dma(SP, X[nm][:, :, 0:W1], pf[i * 512:(i + 1) * 512, c0 - 1:c0 + W].rearrange("(h d) c -> d h c", d=64), reads=[pfb], writes=[Xb[nm]])
        for j, (r0, nr) in enumerate(((12 * 128, 32), (12 * 128 + 32, 32), (13 * 128, 96))):
            P.dma(SP, lraw[0:nr, j, 0:W1], pf[r0:r0 + nr, c0 - 1:c0 + W], reads=[pfb], writes=[lrawb])
        if sci == 0:
            for nm in ("pr", "pk", "pv"):
                vop(lambda e, nm=nm: e.memset(X[nm][:, :, 0:1], 0.0), [Xb[nm]], [Xb[nm]])
            vop(lambda e: e.memset(lraw[:, :, 0:1], 0.0), [lrawb], [lrawb])
        if sci == 1:
            for i, nm in enumerate(("pr", "pk", "pv")):
                P.dma(SP, hs[0:64, 0, :], pf[i * 512:(i + 1) * 512, 18:19].rearrange("(h d) c -> d (h c)", d=64), reads=[pfb], writes=[hsb], allow_slow_non_contiguous=True)
                P.dma(SP, hs[0:64, 1, :], prm["hist_in"][i * 512:(i + 1) * 512, 2:3].rearrange("(h d) c -> d (h c)", d=64), reads=[hsb], writes=[hsb], allow_slow_non_contiguous=True)
                vop(lambda e, nm=nm: e.scalar_tensor_tensor(out=X[nm][:, :, 0:1], in0=hs[0:64, 0, :].unsqueeze(2), scalar=flagE[0:64, 0:1], in1=hs[0:64, 1, :].unsqueeze(2),
                                                            op0=ALU.mult, op1=ALU.add), [hsb, fb, Xb[nm]], [Xb[nm]])
            for j, (r0, nr) in enumerate(((12 * 128, 32), (12 * 128 + 32, 32), (13 * 128, 96))):
                P.dma(SP, hs[0:nr, 0, 0:1], pf[r0:r0 + nr, 18:19], reads=[pfb, hsb], writes=[hsb], allow_slow_non_contiguous=True)
                P.dma(SP, hs[0:nr, 1, 0:1], prm["hist_in"][r0:r0 + nr, 2:3], reads=[hsb], writes=[hsb], allow_slow_non_contiguous=True)
                vop(lambda e, j=j, nr=nr: e.scalar_tensor_tensor(out=lraw[0:nr, j, 0:1], in0=hs[0:nr, 0, 0:1], scalar=flagE[0:nr, 0:1], in1=hs[0:nr, 1, 0:1],
                                                                 op0=ALU.mult, op1=ALU.add), [hsb, fb, lrawb], [lrawb])
            P.dma(SP, Hin[:], prm["sA_in"].rearrange("h k v -> k h v"), writes=[Hinb])
            vop(lambda e: e.scalar_tensor_tensor(out=H[:], in0=H[:], scalar=flagE[0:64, 0:1], in1=Hin[:], op0=ALU.mult, op1=ALU.add), [Hb, Hinb, fb], [Hb])
            P.op(ACT, lambda e: e.activation(out=H16[:], in_=H[:], func=AF.Copy), reads=[Hb], writes=[H16b])
        for i, (src, dst) in enumerate((("pr", "xr"), ("pk", "xk"), ("pv", "xv"))):
            vop(lambda e, src=src, dst=dst: e.tensor_tensor(out=cur(dst), in0=prev(src), in1=cur(src), op=ALU.subtract), [Xb[src]], [Xb[dst]])
            vop(lambda e, dst=dst, i=i: e.tensor_tensor(out=cur(dst), in0=cur(dst), in1=bc8(muA[:, i, :], W), op=ALU.mult), [Xb[dst], pb_], [Xb[dst]])
            vop(lambda e, src=src, dst=dst: e.tensor_tensor(out=cur(dst), in0=cur(dst), in1=cur(src), op=ALU.add), [Xb[dst], Xb[src]], [Xb[dst]])
        for j, (dst, dstb, nr, fn) in enumerate(((thw, thwb, 32, AF.Tanh), (xal, xalb, 32, None), (sg, sgb, 96, AF.Sigmoid))):
            vop(lambda e, dst=dst, nr=nr, j=j: e.tensor_tensor(out=dst[0:nr, 0:W], in0=lraw[0:nr, j, 0:W], in1=lraw[0:nr, j, 1:W1], op=ALU.subtract), [lrawb], [dstb])
            vop(lambda e, dst=dst, nr=nr, j=j: e.scalar_tensor_tensor(out=dst[0:nr, 0:W], in0=dst[0:nr, 0:W], scalar=muL[0:nr, j:j + 1], in1=lraw[0:nr, j, 1:W1],
                                                                    op0=ALU.mult, op1=ALU.add), [lrawb, dstb, pb_], [dstb])
            if fn is not None:
                P.op(ACT, lambda e, dst=dst, nr=nr, fn=fn: e.activation(out=dst[0:nr, 0:W], in_=dst[0:nr, 0:W], func=fn), reads=[dstb], writes=[dstb])
        for (wt_, src, srcb, dst, chi, b0) in ((w2, thw, thwb, "sgz", 0, 0), (a2, xal, xalb, "asig", 1, 2)):
            for h in range(8):
                bk = b0 + (h * W) // 512
                off = (h * W) % 512
                P.op(PE, lambda e, bk=bk, off=off, wt_=wt_, src=src, h=h: e.matmul(bank[bk][0:64, off:off + W], lhsT=wt_[0:32, h * 64:(h + 1) * 64], rhs=src[0:32, 0:W],
                                                                                  start=True, stop=True), reads=[pb_, srcb], writes=[bkb[bk]])
            nb = (8 * W + 511) // 512
            for b in range(nb):
                h0 = b * (512 // W) if W >= 64 else 0
                nh = (512 // W) if W >= 64 else 8
                vop(lambda e, b=b, b0=b0, dst=dst, chi=chi, h0=h0, nh=nh: e.tensor_tensor(
                    out=X[dst][:, h0:h0 + nh, 1:W1], in0=bank[b0 + b][0:64, 0:nh * W].rearrange("p (h w) -> p h w", h=nh),
                    in1=ch[:, chi, h0:h0 + nh].unsqueeze(2).broadcast_to([64, nh, W]), op=ALU.add), [bkb[b0 + b], pb_], [Xb[dst]])
            P.op(ACT, lambda e, dst=dst: e.activation(out=cur(dst), in_=cur(dst), func=AF.Sigmoid), reads=[Xb[dst]], writes=[Xb[dst]])
        vop(lambda e: e.tensor_tensor(out=cur("kkn"), in0=cur("xk"), in1=bc8(ch[:, 2, :], W), op=ALU.mult), [Xb["xk"], pb_], [Xb["kkn"]])
        vop(lambda e: e.tensor_tensor(out=Fl[:, 0:8 * W].rearrange("p (h w) -> p h w", h=8), in0=cur("kkn"), in1=cur("kkn"), op=ALU.mult), [Xb["kkn"]], [Flb])
        nb = (8 * W + 511) // 512
        for b in range(nb):
            nn_ = min(512, 8 * W - b * 512)
            P.op(PE, lambda e, b=b, nn_=nn_: e.matmul(bank[4 + b][0:64, 0:nn_], lhsT=ones[0:64, 0:64], rhs=Fl[:, b * 512:b * 512 + nn_], start=True, stop=True),
                 reads=[onesb, Flb], writes=[bkb[4 + b]])
        for b in range(nb):
            nn_ = min(512, 8 * W - b * 512)
            vop(lambda e, b=b, nn_=nn_: e.tensor_scalar(out=Fl[:, b * 512:b * 512 + nn_], in0=bank[4 + b][0:64, 0:nn_],
                                                        scalar1=1e-24, scalar2=None, op0=ALU.max), [bkb[4 + b], Flb], [Flb])
        relf = Fl[:, 0:8 * W]
        P.op(ACT, lambda e, relf=relf: e.activation(out=relf, in_=relf, func=AF.Sqrt), reads=[Flb], writes=[Flb])
        vop(lambda e, relf=relf: e.reciprocal(out=relf, in_=relf), [Flb], [Flb])
        vop(lambda e, relf=relf: e.tensor_tensor(out=cur("kkn"), in0=cur("kkn"), in1=relf.rearrange("p (h w) -> p h w", h=8), op=ALU.mult),
            [Xb["kkn"], Flb], [Xb["kkn"]])
        vop(lambda e: e.tensor_tensor(out=cur("t1"), in0=cur("asig"), in1=bc8(ch[:, 3, :], W), op=ALU.mult), [Xb["asig"], pb_], [Xb["t1"]])
        vop(lambda e: e.tensor_tensor(out=cur("t1"), in0=cur("t1"), in1=bc8(ch[:, 4, :], W), op=ALU.add), [Xb["t1"], pb_], [Xb["t1"]])
        vop(lambda e: e.tensor_tensor(out=cur("xk"), in0=cur("xk"), in1=cur("t1"), op=ALU.mult), [Xb["xk"], Xb["t1"]], [Xb["xk"]])
        vop(lambda e: e.tensor_tensor(out=cur("asig"), in0=cur("asig"), in1=cur("kkn"), op=ALU.mult), [Xb["asig"], Xb["kkn"]], [Xb["asig"]])
        vop(lambda e: e.tensor_tensor(out=cur("t1"), in0=cur("xr"), in1=cur("xk"), op=ALU.mult), [Xb["xr"], Xb["xk"], Xb["t1"]], [Xb["t1"]])
        vop(lambda e: e.tensor_tensor(out=cur("pr"), in0=cur("t1"), in1=bc8(rk[:, :, 0], W), op=ALU.mult), [Xb["t1"], pb_, Xb["pr"], Xb["xr"]], [Xb["pr"]])
        vop(lambda e: e.tensor_copy(out=Fl[:, 0:8 * W].rearrange("p (h w) -> p h w", h=8), in_=cur("sgz")), [Xb["sgz"], Flb], [Flb])
        vop(lambda e: e.tensor_tensor_scan(out=Ssc[:, 1:1 + 8 * W], data0=ones[0:64, 0:8 * W], data1=Fl[:, 0:8 * W], initial=0.0, op0=ALU.mult, op1=ALU.add),
            [Flb, Sscb, onesb], [Sscb])
        vop(lambda e: e.tensor_tensor(out=cur("rel").rearrange("p h (c j) -> p h c j", j=n),
                                      in0=Ssc[:, 1:1 + 8 * W].rearrange("p (h c j) -> p h c j", h=8, j=n),
                                      in1=Ssc[:, 0:8 * W].rearrange("p (h c j) -> p h c j", h=8, j=n)[:, :, :, 0:1].broadcast_to([64, 8, nch, n]), op=ALU.subtract),
            [Sscb, Xb["rel"]], [Xb["rel"]])
        P.op(ACT, lambda e: e.activation(out=cur("G1"), in_=cur("rel"), func=AF.Exp, scale=-LDK), reads=[Xb["rel"]], writes=[Xb["G1"]])
        P.op(ACT, lambda e: e.activation(out=cur("G2"), in_=cur("rel"), func=AF.Exp, scale=LDK), reads=[Xb["rel"]], writes=[Xb["G2"]])
        vop(lambda e: e.tensor_tensor(out=cur("rel"), in0=cur("rel"), in1=cur("sgz"), op=ALU.subtract), [Xb["rel"], Xb["sgz"]], [Xb["rel"]])
        P.op(ACT, lambda e: e.activation(out=cur("rel"), in_=cur("rel"), func=AF.Exp, scale=-LDK), reads=[Xb["rel"]], writes=[Xb["rel"]])
        vop(lambda e: e.tensor_tensor(out=cur("xr"), in0=cur("xr"), in1=cur("G1"), op=ALU.mult), [Xb["xr"], Xb["G1"]], [Xb["xr"]])
        vop(lambda e: e.tensor_tensor(out=cur("xk"), in0=cur("xk"), in1=cur("G2"), op=ALU.mult), [Xb["xk"], Xb["G2"]], [Xb["xk"]])
        vop(lambda e: e.tensor_tensor(out=cur("asig"), in0=cur("asig"), in1=cur("G2"), op=ALU.mult), [Xb["asig"], Xb["G2"]], [Xb["asig"]])
        vop(lambda e: e.scalar_tensor_tensor(out=cur("kkn"), in0=cur("kkn"), scalar=-1.0, in1=cur("rel"), op0=ALU.mult, op1=ALU.mult),
            [Xb["kkn"], Xb["rel"]], [Xb["kkn"]])
        for nm in ("xr", "xk", "asig", "kkn"):
            P.op(ACT, lambda e, nm=nm: e.activation(out=XBF[nm][:, :, 1:W1], in_=cur(nm), func=AF.Copy), reads=[Xb[nm]], writes=[XBFb[nm]])
        RT, KT, BT, AT, XV, PRK, G1 = "xr", "xk", "asig", "kkn", "xv", "pr", "G1"
        colb = lambda nm, h, c: XBF[nm][:, h, 1 + c * n:1 + (c + 1) * n]
        col = lambda nm, h, c: X[nm][:, h, 1 + c * n:1 + (c + 1) * n]
        qi = lambda h, c: h * nch + c
        P.phase = "rwkv_gram"
        for a, nm in enumerate((XV, KT, BT)):
            for h in range(8):
                for c in range(nch):
                    q = qi(h, c)
                    bk, off = (q * 64) // 512, (q * 64) % 512
                    P.op(PE, lambda e, bk=bk, off=off, nm=nm, h=h, c=c: e.transpose(out=bank[bk][0:n, off:off + 64], in_=col(nm, h, c), identity=ident[0:64, 0:64]),
                         reads=[Xb[nm], idb], writes=[bkb[bk]])
            for b in range((nq * 64 + 511) // 512):
                qn = min(8, nq - b * 8)
                P.op(ACT, lambda e, a=a, b=b, qn=qn: e.activation(out=TM[0:n, a, b * 8:b * 8 + qn, :], in_=bank[b][0:n, 0:qn * 64].rearrange("p (q d) -> p q d", q=qn),
                                                               func=AF.Copy), reads=[bkb[b]], pwrites=[TMb])
        pairs = ((KT, AT), (KT, RT), (BT, AT), (BT, RT), (AT, BT))
        for j, (l_, r_) in enumerate(pairs):
            b0 = 4 if j % 2 else 0
            for h in range(8):
                for c in range(nch):
                    q = qi(h, c)
                    bk, off = b0 + (q * 64) // 512, (q * 64) % 512
                    P.op(PE, lambda e, bk=bk, off=off, l_=l_, r_=r_, h=h, c=c: e.matmul(bank[bk][0:n, off:off + n], lhsT=colb(l_, h, c), rhs=colb(r_, h, c), start=True, stop=True),
                         reads=[XBFb[l_], XBFb[r_]], writes=[bkb[bk]])
            for b in range((nq * 64 + 511) // 512):
                qn = min(8, nq - b * 8)
                vop(lambda e, j=j, b=b, b0=b0, qn=qn: e.tensor_tensor(out=MM[0:n, j, b * 8:b * 8 + qn, 0:n],
                                                                    in0=bank[b0 + b][0:n, 0:qn * 64].rearrange("p (q d) -> p q d", q=qn)[:, :, 0:n],
                                                                    in1=mask5[0:n, j, 0:n].unsqueeze(1).broadcast_to([n, qn, n]), op=ALU.mult),
                    [bkb[b0 + b], m5b], [], pwrites=[MMb])
        P.phase = "rwkv_inv"
        vop(lambda e: e.tensor_tensor(out=Pm[0:n, 0:nq, 0:n], in0=MM[0:n, 2, 0:nq, 0:n], in1=ident[0:n, 0:n].unsqueeze(1).broadcast_to([n, nq, n]), op=ALU.add),
            [MMb, idb], [Pmb])
        curN = lambda q: MM[0:n, 2, q, 0:n]
        curNT = lambda q: MM[0:n, 4, q, 0:n]
        curb = MMb
        nlev = 5 if n == 64 else 3
        for lev in range(nlev):
            nn, nnb = NN[lev % 2], NNb[lev % 2]
            for q in range(nq):
                bk, off = (q * 128) // 512, (q * 128) % 512
                P.op(PE, lambda e, bk=bk, off=off, a_=curNT(q), b_=curN(q): e.matmul(bank[bk][0:n, off:off + n], lhsT=a_, rhs=b_, start=True, stop=True), reads=[curb], writes=[bkb[bk]])
                P.op(PE, lambda e, bk=bk, off=off, a_=curN(q), b_=curNT(q): e.matmul(bank[bk][0:n, off + 64:off + 64 + n], lhsT=a_, rhs=b_, start=True, stop=True), reads=[curb], writes=[bkb[bk]])
            for b in range((nq * 128 + 511) // 512):
                qn = min(4, nq - b * 4)
                P.op(ACT, lambda e, nn=nn, b=b, qn=qn: e.activation(out=nn[0:n, b * 4:b * 4 + qn, :, 0:n],
                                                                 in_=bank[b][0:n, 0:qn * 128].rearrange("p (q j d) -> p q j d", q=qn, j=2)[:, :, :, 0:n], func=AF.Copy),
                     reads=[bkb[b]], pwrites=[nnb])
            curN = lambda q, nn=nn: nn[0:n, q, 0, 0:n]
            curNT = lambda q, nn=nn: nn[0:n, q, 1, 0:n]
            curb = nnb
            for q in range(nq):
                bk, off = 4 + (q * 64) // 512, (q * 64) % 512
                P.op(PE, lambda e, bk=bk, off=off, a_=curNT(q), q=q: e.matmul(bank[bk][0:n, off:off + n], lhsT=a_, rhs=Pm[0:n, q, 0:n], start=True, stop=True), reads=[curb, Pmb], writes=[bkb[bk]])
            for b in range((nq * 64 + 511) // 512):
                qn = min(8, nq - b * 8)
                vop(lambda e, b=b, qn=qn: e.tensor_tensor(out=Pm[0:n, b * 8:b * 8 + qn, 0:n], in0=Pm[0:n, b * 8:b * 8 + qn, 0:n],
                                                          in1=bank[4 + b][0:n, 0:qn * 64].rearrange("p (q d) -> p q d", q=qn)[:, :, 0:n], op=ALU.add),
                    [bkb[4 + b], Pmb], [Pmb])
        P.phase = "rwkv_chain"
        for c in range(nch):
            tc0 = t0 + c * n
            for h in range(8):
                q = qi(h, c)
                P.op(PE, lambda e, h=h, c=c: e.matmul(bank[0][0:n, h * 64:(h + 1) * 64], lhsT=colb(AT, h, c), rhs=H16[:, h, :], start=True, stop=False), reads=[XBFb[AT], H16b], writes=[bkb[0]])
                P.op(PE, lambda e, h=h, q=q: e.matmul(bank[0][0:n, h * 64:(h + 1) * 64], lhsT=MM[0:n, 0, q, 0:n], rhs=TM[0:n, 0, q, :], start=False, stop=True), reads=[MMb, TMb], writes=[bkb[0]])
            P.op(ACT, lambda e: e.activation(out=W0s[0:n, :, :], in_=bank[0][0:n, 0:512].rearrange("p (h d) -> p h d", h=8), func=AF.Copy), reads=[bkb[0]], writes=[W0b])
            for h in range(8):
                q = qi(h, c)
                P.op(PE, lambda e, h=h, q=q: e.matmul(bank[1][0:n, h * 64:(h + 1) * 64], lhsT=Pm[0:n, q, 0:n], rhs=W0s[0:n, h, :], start=True, stop=True), reads=[Pmb, W0b], writes=[bkb[1]])
            vop(lambda e: e.tensor_copy(out=Us[0:n, :, :], in_=bank[1][0:n, 0:512].rearrange("p (h d) -> p h d", h=8)), [bkb[1]], [Usb])
            if C.emit_out:
                for h in range(8):
                    q = qi(h, c)
                    P.op(PE, lambda e, h=h, c=c: e.matmul(bank[2][0:n, h * 64:(h + 1) * 64], lhsT=colb(RT, h, c), rhs=H16[:, h, :], start=True, stop=False), reads=[XBFb[RT], H16b], writes=[bkb[2]])
                    P.op(PE, lambda e, h=h, q=q: e.matmul(bank[2][0:n, h * 64:(h + 1) * 64], lhsT=MM[0:n, 3, q, 0:n], rhs=Us[0:n, h, :], start=False, stop=False), reads=[MMb, Usb], writes=[bkb[2]])
                    P.op(PE, lambda e, h=h, q=q: e.matmul(bank[2][0:n, h * 64:(h + 1) * 64], lhsT=MM[0:n, 1, q, 0:n], rhs=TM[0:n, 0, q, :], start=False, stop=True), reads=[MMb, TMb], writes=[bkb[2]])
            for h in range(8):
                q = qi(h, c)
                P.op(PE, lambda e, h=h, q=q: e.matmul(bank[3][0:64, h * 64:(h + 1) * 64], lhsT=TM[0:n, 2, q, :], rhs=Us[0:n, h, :], start=True, stop=False), reads=[TMb, Usb], writes=[bkb[3]])
                P.op(PE, lambda e, h=h, q=q: e.matmul(bank[3][0:64, h * 64:(h + 1) * 64], lhsT=TM[0:n, 1, q, :], rhs=TM[0:n, 0, q, :], start=False, stop=True), reads=[TMb], writes=[bkb[3]])
            ce = 1 + (c + 1) * n - 1
            vop(lambda e: e.tensor_tensor(out=H[:], in0=H[:], in1=bank[3][0:64, 0:512].rearrange("p (h d) -> p h d", h=8), op=ALU.add), [bkb[3], Hb], [Hb])
            vop(lambda e, ce=ce: e.tensor_tensor(out=H[:], in0=H[:], in1=X[G1][:, :, ce:ce + 1].broadcast_to([64, 8, 64]), op=ALU.mult), [Hb, Xb[G1]], [Hb])
            P.op(ACT, lambda e: e.activation(out=H16[:], in_=H[:], func=AF.Copy), reads=[Hb], writes=[H16b])
            if C.emit_out:
                P.op(PE, lambda e, c=c: e.matmul(bank[4][0:n, 0:512], lhsT=sg[0:96, c * n:(c + 1) * n], rhs=g2[0:96, :], start=True, stop=True), reads=[sgb, pb_], writes=[bkb[4]])
                for h in range(8):
                    P.op(PE, lambda e, h=h, c=c: e.matmul(bank[5][0:n, 2 * h:2 * h + 2], lhsT=col(PRK, h, c), rhs=ones[0:64, 0:2], start=True, stop=True), reads=[Xb[PRK], onesb], writes=[bkb[5]])
                P.op(ACT, lambda e: e.activation(out=GBs[0:n, 0:512], in_=bank[4][0:n, 0:512], func=AF.Copy), reads=[bkb[4]], writes=[GBb])
                P.op(ACT, lambda e: e.activation(out=GBs[0:n, 512:528], in_=bank[5][0:n, 0:16], func=AF.Copy), reads=[bkb[5], GBb], writes=[GBb])
                YG = bank[2][0:n, 0:512].rearrange("p (h d) -> p h d", h=8)
                bcn = lambda ap: ap.unsqueeze(2).broadcast_to([n, 8, 64])
                vop(lambda e, YG=YG: e.tensor_reduce(out=ST[0:n, 0, :], in_=YG, axis=AX.X, op=ALU.add), [bkb[2]], [STb])
                P.op(ACT, lambda e, YG=YG: e.activation(out=SQ[0:n, :, :], in_=YG, func=AF.Square), reads=[bkb[2]], writes=[SQb])
                vop(lambda e: e.tensor_reduce(out=ST[0:n, 1, :], in_=SQ[0:n, :, :], axis=AX.X, op=ALU.add), [SQb, STb], [STb])
                vop(lambda e: e.tensor_scalar(out=ST[0:n, 2, :], in0=ST[0:n, 0, :], scalar1=1.0 / 64, scalar2=None, op0=ALU.mult), [STb], [STb])
                vop(lambda e: e.tensor_tensor(out=ST[0:n, 3, :], in0=ST[0:n, 2, :], in1=ST[0:n, 2, :], op=ALU.mult), [STb], [STb])
                vop(lambda e: e.tensor_scalar(out=ST[0:n, 4, :], in0=ST[0:n, 1, :], scalar1=1.0 / 64, scalar2=64e-5, op0=ALU.mult, op1=ALU.add), [STb], [STb])
                vop(lambda e: e.tensor_tensor(out=ST[0:n, 4, :], in0=ST[0:n, 4, :], in1=ST[0:n, 3, :], op=ALU.subtract), [STb], [STb])
                P.op(POOL, lambda e: e.tensor_tensor(out=ST[0:n, 5, :], in0=ST[0:n, 4, :], in1=mh[0:n, 0:8], op=ALU.pow), reads=[STb, mhb], writes=[STb])
                vop(lambda e, YG=YG, bcn=bcn: e.tensor_tensor(out=T1[0:n, :, :], in0=YG, in1=bcn(ST[0:n, 2, :]), op=ALU.subtract), [bkb[2], STb], [T1b])
                vop(lambda e, bcn=bcn: e.tensor_tensor(out=T1[0:n, :, :], in0=T1[0:n, :, :], in1=bcn(ST[0:n, 5, :]), op=ALU.mult), [T1b, STb], [T1b])
                vop(lambda e: e.tensor_tensor(out=T1[0:n, :, :], in0=T1[0:n, :, :], in1=lnw[0:n, :].rearrange("p (h d) -> p h d", h=8), op=ALU.mult), [T1b, pb_], [T1b])
                vop(lambda e: e.tensor_tensor(out=T1[0:n, :, :], in0=T1[0:n, :, :], in1=lnb[0:n, :].rearrange("p (h d) -> p h d", h=8), op=ALU.add), [T1b, pb_], [T1b])
                vtm_c = TM[0:n, 0, 0:nq, :].rearrange("p (h c) d -> p h c d", c=nch)[:, :, c, :]
                bs_c = GBs[0:n, 512:528].rearrange("p (h t) -> p h t", t=2)[:, :, 0:1].broadcast_to([n, 8, 64])
                vop(lambda e, vtm_c=vtm_c, bs_c=bs_c: e.tensor_tensor(out=SQ[0:n, :, :], in0=vtm_c, in1=bs_c, op=ALU.mult), [TMb, GBb, SQb], [SQb])
                vop(lambda e: e.tensor_tensor(out=T1[0:n, :, :], in0=T1[0:n, :, :], in1=SQ[0:n, :, :], op=ALU.add), [T1b, SQb], [T1b])
                vop(lambda e: e.tensor_tensor(out=YO[0:n, :, :], in0=T1[0:n, :, :], in1=GBs[0:n, 0:512].rearrange("p (h d) -> p h d", h=8), op=ALU.mult), [T1b, GBb], [YOb])
                P.dma(SP, y[tc0:tc0 + n, 0:512], YO[0:n, :, :].rearrange("p h d -> p (h d)"), reads=[YOb], pwrites=[yb])
    for sci, (c0, W, t0, n) in enumerate(scs):
        do_sc(sci, c0, W, t0, n)
    P.dma(POOL, prm["sA_out"].rearrange("h k v -> k h v"), H[:], reads=[Hb], pwrites=[K.sob])


def mixer_gla2(C, st, pf, pfb, pt, ptb, y, yb, prm, K):
    P = C.P
    ones, onesb, ident, idb, mask_i, mib, flagE, fb = K.ones, K.onesb, K.ident, K.idb, K.mask_i, K.mib, K.flagE, K.fb
    a2 = sbuf(C, st, "dga2", [32, 256]); a2b = Buf()
    P.op(DVE, lambda e: e.memset(a2[:], 0.0), writes=[a2b])
    P.dma(SP, a2[0:16, :], prm["gla_a2"], reads=[a2b], writes=[a2b])
    nab = sbuf(C, st, "dgnab", [64, 4]); nabb = Buf()
    nbc = sbuf(C, st, "dgnbc", [64, 128]); nbcb = Buf()
    P.dma(SP, nab[:], prm["gla_ab"], writes=[nabb])
    P.op(DVE, lambda e: e.tensor_scalar(out=nab[:], in0=nab[:], scalar1=-1.0, scalar2=None, op0=ALU.mult), reads=[nabb], writes=[nabb])
    P.dma(SP, nbc[:], prm["gla_normbc"], writes=[nbcb])
    WM = 512
    q = sbuf(C, st, "dgq", [64, 4, WM]); k = sbuf(C, st, "dgk", [64, 4, WM]); rel = sbuf(C, st, "dgrel", [64, 4, WM])
    e1 = sbuf(C, st, "dge1", [64, 4, WM]); e2 = sbuf(C, st, "dge2", [64, 4, WM])
    Fl = sbuf(C, st, "dgFl", [64, 4 * WM]); Ssc = sbuf(C, st, "dgSsc", [64, 1 + 4 * WM]); xa = sbuf(C, st, "dgxa", [32, WM])
    qb, kb_, relb, e1b, e2b, Flb, Sscb, xab = [Buf() for _ in range(8)]
    P.op(DVE, lambda e: e.memset(Ssc[:, 0:1], 0.0), writes=[Sscb])
    S = sbuf(C, st, "dgS", [64, 4, 128]); Sb = Buf()
    Sin = sbuf(C, st, "dgSin", [64, 4, 128]); Sinb = Buf()
    P.op(DVE, lambda e: e.memset(S[:], 0.0), writes=[Sb])
    bank = [psum(C, st, f"dgb{i}", [128, 512], F32) for i in range(8)]
    bkb = [Buf() for _ in range(8)]
    vr = Ring([sbuf(C, st, f"dgv{i}", [64, 1024], F32) for i in range(3)])
    ktr = Ring([sbuf(C, st, f"dgkt{i}", [64, 256], F32) for i in range(2)])
    scr = Ring([sbuf(C, st, f"dgsc{i}", [64, 4, 64], F32) for i in range(2)])
    t1r = Ring([sbuf(C, st, f"dgt1{i}", [64, 4, 128], F32) for i in range(2)])
    yor = Ring([sbuf(C, st, f"dgyo{i}", [64, 4, 128], BF16) for i in range(2)])
    str_ = Ring([sbuf(C, st, f"dgst{i}", [64, 3, 4], F32) for i in range(2)])
    junk = sbuf(C, st, "dgjunk", [64, 4, 128], F32); jb = Buf()
    mh = sbuf(C, st, "dgmh", [64, 4], F32); mhb = Buf()
    P.op(POOL, lambda e: e.memset(mh[:], -0.5), writes=[mhb])

    def vop(fn, reads, writes, pwrites=()):
        P.op(DVE, fn, reads=reads, writes=writes, pwrites=pwrites)

    scs = [(3, 16, 0, 16)] + [(22 + 512 * i, 512, 16 + 512 * i, 64) for i in range(4)]
    cnt = [0]

    def do_sc(sci, c0, W, t0, n):
        nch = W // n
        P.dma(SP, q[:, :, 0:W], pf[14 * 128:14 * 128 + 256, c0:c0 + W].rearrange("(h d) c -> d h c", d=64), reads=[pfb], writes=[qb])
        P.dma(SP, k[:, :, 0:W], pf[16 * 128:16 * 128 + 256, c0:c0 + W].rearrange("(h d) c -> d h c", d=64), reads=[pfb], writes=[kb_])
        P.dma(SP, xa[:, 0:W], pf[18 * 128:18 * 128 + 32, c0:c0 + W], reads=[pfb], writes=[xab])
        if sci == 1:
            P.dma(SP, Sin[:], prm["sB_in"].rearrange("h k v -> k h v"), writes=[Sinb])
            vop(lambda e: e.scalar_tensor_tensor(out=S[:], in0=S[:], scalar=flagE[0:64, 0:1], in1=Sin[:], op0=ALU.mult, op1=ALU.add), [Sb, Sinb, fb], [Sb])
        for h in range(4):
            bk, off = (h * W) // 512, (h * W) % 512
            P.op(PE, lambda e, bk=bk, off=off, h=h: e.matmul(bank[bk][0:64, off:off + W], lhsT=a2[0:32, h * 64:(h + 1) * 64], rhs=xa[0:32, 0:W], start=True, stop=True),
                 reads=[a2b, xab], writes=[bkb[bk]])
            P.op(ACT, lambda e, bk=bk, off=off, h=h: e.activation(out=Fl[:, h * W:(h + 1) * W], in_=bank[bk][0:64, off:off + W], func=AF.Exp, scale=-1.0, bias=nab[:, h:h + 1]),
                 reads=[bkb[bk], nabb], pwrites=[Flb])
        P.op(ACT, lambda e: e.activation(out=Fl[:, 0:4 * W], in_=Fl[:, 0:4 * W], func=AF.Ln, bias=1.0), reads=[Flb], writes=[Flb])
        vop(lambda e: e.tensor_tensor_scan(out=Ssc[:, 1:1 + 4 * W], data0=ones[0:64, 0:4 * W], data1=Fl[:, 0:4 * W], initial=0.0, op0=ALU.mult, op1=ALU.add),
            [Flb, Sscb, onesb], [Sscb])
        vop(lambda e: e.tensor_tensor(out=rel[:, :, 0:W].rearrange("p h (c j) -> p h c j", j=n),
                                      in0=Ssc[:, 1:1 + 4 * W].rearrange("p (h c j) -> p h c j", h=4, j=n),
                                      in1=Ssc[:, 0:4 * W].rearrange("p (h c j) -> p h c j", h=4, j=n)[:, :, :, 0:1].broadcast_to([64, 4, nch, n]), op=ALU.subtract),
            [Sscb], [relb])
        P.op(ACT, lambda e: e.activation(out=e1[:, :, 0:W], in_=rel[:, :, 0:W], func=AF.Exp, scale=-1.0 / 16), reads=[relb], writes=[e1b])
        P.op(ACT, lambda e: e.activation(out=e2[:, :, 0:W], in_=rel[:, :, 0:W], func=AF.Exp, scale=1.0 / 16), reads=[relb], writes=[e2b])
        vop(lambda e: e.scalar_tensor_tensor(out=q[:, :, 0:W], in0=q[:, :, 0:W], scalar=0.125, in1=e1[:, :, 0:W], op0=ALU.mult, op1=ALU.mult), [qb, e1b], [qb])
        vop(lambda e: e.tensor_tensor(out=k[:, :, 0:W], in0=k[:, :, 0:W], in1=e2[:, :, 0:W], op=ALU.mult), [kb_, e2b], [kb_])
        for c in range(nch):
            tc0 = t0 + c * n
            bA, bO, bS = (4, 5, 6) if cnt[0] % 2 == 0 else (1, 2, 3)
            cnt[0] += 1
            vt, vb = vr.next()
            P.dma(SP, vt[0:n, :], pt[tc0:tc0 + n, 0:1024], reads=[ptb], writes=[vb])
            for h in range(4):
                P.op(PE, lambda e, h=h, c=c, bA=bA: e.transpose(out=bank[bA][0:n, h * 64:(h + 1) * 64], in_=k[:, h, c * n:(c + 1) * n], identity=ident[0:64, 0:64]),
                     reads=[kb_, idb], writes=[bkb[bA]])
            for h in range(4):
                P.op(PE, lambda e, h=h, c=c, bA=bA: e.matmul(bank[bA][0:n, 256 + h * 64:256 + h * 64 + n], lhsT=k[:, h, c * n:(c + 1) * n], rhs=q[:, h, c * n:(c + 1) * n],
                                                          start=True, stop=True), reads=[kb_, qb], writes=[bkb[bA]])
            kt, ktb = ktr.next(); sc, scb = scr.next()
            P.op(ACT, lambda e, kt=kt, bA=bA: e.activation(out=kt[0:n, :], in_=bank[bA][0:n, 0:256], func=AF.Copy), reads=[bkb[bA]], writes=[ktb])
            vop(lambda e, sc=sc, bA=bA: e.tensor_tensor(out=sc[0:n, :, 0:n], in0=bank[bA][0:n, 256:512].rearrange("p (h d) -> p h d", h=4)[:, :, 0:n],
                                                      in1=mask_i[0:n, 0:n].unsqueeze(1).broadcast_to([n, 4, n]), op=ALU.mult), [bkb[bA], mib], [scb])
            for h in range(4):
                P.op(PE, lambda e, h=h, c=c, bO=bO: e.matmul(bank[bO][0:n, h * 128:(h + 1) * 128], lhsT=q[:, h, c * n:(c + 1) * n], rhs=S[:, h, :], start=True, stop=False),
                     reads=[qb, Sb], writes=[bkb[bO]])
                P.op(PE, lambda e, h=h, sc=sc, vt=vt, bO=bO: e.matmul(bank[bO][0:n, h * 128:(h + 1) * 128], lhsT=sc[0:n, h, 0:n], rhs=vt[0:n, h * 128:(h + 1) * 128], start=False, stop=True),
                     reads=[scb, vb], writes=[bkb[bO]])
            for h in range(4):
                P.op(PE, lambda e, h=h, bS=bS: e.matmul(bank[bS][0:64, h * 128:(h + 1) * 128], lhsT=ident[0:64, 0:64], rhs=S[:, h, :], start=True, stop=False),
                     reads=[idb, Sb], writes=[bkb[bS]])
                P.op(PE, lambda e, h=h, kt=kt, vt=vt, bS=bS: e.matmul(bank[bS][0:64, h * 128:(h + 1) * 128], lhsT=kt[0:n, h * 64:(h + 1) * 64], rhs=vt[0:n, h * 128:(h + 1) * 128], start=False, stop=True),
                     reads=[ktb, vb], writes=[bkb[bS]])
            ce = (c + 1) * n - 1
            vop(lambda e, ce=ce, bS=bS: e.tensor_tensor(out=S[:], in0=bank[bS][0:64, 0:512].rearrange("p (h d) -> p h d", h=4),
                                                      in1=e1[:, :, ce:ce + 1].broadcast_to([64, 4, 128]), op=ALU.mult), [bkb[bS], e1b], [Sb])
            if C.emit_out:
                s_, sb2 = str_.next(); t1, t1b = t1r.next(); yo, yob = yor.next()
                P.op(ACT, lambda e, t1=t1, bO=bO: e.activation(out=t1[0:n, :, :], in_=bank[bO][0:n, 0:512].rearrange("p (h d) -> p h d", h=4), func=AF.Copy),
                     reads=[bkb[bO]], writes=[t1b])
                vop(lambda e, t1=t1: e.tensor_tensor(out=junk[0:n, :, :], in0=t1[0:n, :, :], in1=t1[0:n, :, :], op=ALU.mult), [t1b], [jb])
                vop(lambda e, s_=s_: e.tensor_reduce(out=s_[0:n, 0, :], in_=junk[0:n, :, :], axis=AX.X, op=ALU.add), [jb], [sb2])
                vop(lambda e, s_=s_: e.tensor_scalar(out=s_[0:n, 1, :], in0=s_[0:n, 0, :], scalar1=1.0 / 128, scalar2=EPS, op0=ALU.mult, op1=ALU.add), [sb2], [sb2])
                P.op(POOL, lambda e, s_=s_: e.tensor_tensor(out=s_[0:n, 2, :], in0=s_[0:n, 1, :], in1=mh[0:n, :], op=ALU.pow), reads=[sb2, mhb], writes=[sb2])
                vop(lambda e, t1=t1, s_=s_: e.tensor_tensor(out=t1[0:n, :, :], in0=t1[0:n, :, :], in1=s_[0:n, 2, :].unsqueeze(2).broadcast_to([n, 4, 128]), op=ALU.mult),
                    [t1b, sb2], [t1b])
                vop(lambda e, t1=t1: e.tensor_tensor(out=t1[0:n, :, :], in0=t1[0:n, :, :], in1=nbc[0:n, :].unsqueeze(1).broadcast_to([n, 4, 128]), op=ALU.mult),
                    [t1b, nbcb], [t1b])
                vop(lambda e, yo=yo, t1=t1, vt=vt: e.tensor_tensor(out=yo[0:n, :, :], in0=t1[0:n, :, :], in1=vt[0:n, 512:1024].rearrange("p (h d) -> p h d", h=4), op=ALU.mult),
                    [t1b, vb], [yob])
                P.dma(SP, y[tc0:tc0 + n, 512:1024], yo[0:n, :, :].rearrange("p h d -> p (h d)"), reads=[yob], pwrites=[yb])

    for sci, (c0, W, t0, n) in enumerate(scs):
        do_sc(sci, c0, W, t0, n)
    P.dma(POOL, prm["sB_out"].rearrange("h k v -> k h v"), S[:], reads=[Sb], pwrites=[K.sob])


def run_gens(gens):
    gens = list(gens)
    while gens:
        for g_ in list(gens):
            try:
                next(g_)
            except StopIteration:
                gens.remove(g_)


import contextlib
import numpy as np

PRM_SHAPES = {
    "gla_a2": [16, 256], "gla_ab": [64, 4], "gla_normbc": [64, 128],
    "ml_cw": [128, 8, 4], "ml_cb": [128, 8], "ml_ib": [4, 1], "ml_fb": [4, 1], "ml_normbc": [64, 1024], "onehot": [4, 4, 128],
    "rw_muA": [64, 3, 8], "rw_muL": [96, 3], "rw_w2": [32, 512], "rw_a2": [32, 512], "rw_g2": [96, 512], "rw_ch": [64, 4, 8],
    "rw_rk": [64, 8, 2], "rw_lnw_bc": [64, 512], "rw_lnb_bc": [64, 512],
    "sA_in": [8, 64, 64], "sB_in": [4, 64, 128], "sC_in": [4, 128, 257], "mC_in": [4, 1], "hist_in": [NFMB * 128, 3],
    "flagE": [128, 1], "mask_i": [64, 64], "mask5": [64, 5, 64],
}
OUT_SHAPES = {"sA_out": [8, 64, 64], "sB_out": [4, 64, 128], "sC_out": [4, 128, 257], "mC_out": [4, 1], "hist_out": [NFMB * 128, 3]}


def host_consts():
    j = np.arange(64)
    mi = (j[None, :] >= j[:, None]).astype(np.float32)
    ms = (j[None, :] > j[:, None]).astype(np.float32)
    ml = (j[None, :] < j[:, None]).astype(np.float32)
    mask5 = np.stack([ms, mi, ms, mi, ml], 1)
    oh = np.zeros((4, 4, 128), np.float32)
    for h in range(4):
        oh[h, h, :] = 1.0
    return {"mask_i": mi, "mask5": np.ascontiguousarray(mask5), "onehot": oh}


def host_layer_params(z, l):
    f = lambda a: np.ascontiguousarray(a, dtype=np.float32)
    chT = lambda v: f(v.reshape(8, 64).T)
    mu = z["rw_mu"][l]
    d = {}
    d["gla_a2"] = f(z["gla_a2"][l]); d["gla_ab"] = f(z["gla_ab"][l].reshape(4, 64).T)
    d["gla_normbc"] = f(np.broadcast_to(z["gla_norm"][l], (64, 128)))
    cw = z["ml_conv_w"][l]
    d["ml_cw"] = f(cw.reshape(4, 8, 128).transpose(2, 1, 0)); d["ml_cb"] = f(z["ml_conv_b"][l].reshape(8, 128).T)
    d["ml_ib"] = f(z["ml_ib"][l].reshape(4, 1)); d["ml_fb"] = f(z["ml_fb"][l].reshape(4, 1))
    d["ml_normbc"] = f(np.broadcast_to(z["ml_norm"][l], (64, 1024)))
    d["rw_muA"] = f(np.stack([chT(mu[0:512]), chT(mu[512:1024]), chT(mu[1024:1536])], 1))
    muL = np.zeros((96, 3), np.float32); muL[0:32, 0] = mu[1536:1568]; muL[0:32, 1] = mu[1568:1600]; muL[0:96, 2] = mu[1600:1696]
    d["rw_muL"] = muL
    d["rw_w2"] = f(z["rw_w2"][l]); d["rw_a2"] = f(z["rw_a2"][l]); d["rw_g2"] = f(z["rw_g2"][l])
    d["rw_ch"] = f(np.stack([chT(z["rw_w0"][l]), chT(z["rw_a0"][l]), chT(z["rw_kk"][l]), chT(z["rw_ka"][l])], 1))
    rk = z["rw_rk"][l]
    d["rw_rk"] = f(np.stack([rk.T, rk.T], 2))
    d["rw_lnw_bc"] = f(np.broadcast_to(z["rw_ln_w"][l], (64, 512))); d["rw_lnb_bc"] = f(np.broadcast_to(z["rw_ln_b"][l], (64, 512)))
    return d


def host_layer_weights(z, l):
    return {"win": hp.prep_win(z["w_in"][l]), "wout": hp.prep_sq(z["w_out"][l], 4), "w1": hp.prep_sq(z["ffn_w1"][l], 11),
            "w3": hp.prep_sq(z["ffn_w3"][l], 11), "w2": hp.prep_w2(z["ffn_w2"][l]), "g1": hp.gT(z["norm_mix"][l]), "g2": hp.gT(z["norm_ffn"][l])}


def build_layer(debug=False, emit_out=True, do_final=True):
    nc = bass.Bass("TRN2", target_bir_lowering=False)
    C = Ctx(); C.nc = nc; C.P = Prog(nc, same_engine_sync=True); C.emit_out = emit_out
    C.P.scopes = False
    P = C.P
    dr = lambda n, s, dt=F32, kind="ExternalInput": nc.dram_tensor(n, s, dt, kind=kind).ap()
    hin = dr("hin", [NTOK, D])
    win = dr("win", [13, 128, 8192]); wout = dr("wout", [4, 128, 8192])
    w1 = dr("w1", [11, 128, 8192]); w3 = dr("w3", [11, 128, 8192]); w2 = dr("w2", [4, 4, 128, 11 * 512])
    g1 = dr("g1", [128, 16]); g2 = dr("g2", [128, 16]); gf = dr("gf", [128, D])
    prm = {k: dr(k, s) for k, s in PRM_SHAPES.items()}
    for k, s in OUT_SHAPES.items():
        prm[k] = dr(k, s, F32, "ExternalOutput")
    dk = "ExternalOutput" if debug else "Internal"
    pf = dr("pf", [NFMB * 128, TP], F32, dk)
    pt = dr("pt", [NTOK, NTMC], F32, dk)
    y = dr("y", [NTOK, D], BF16, dk)
    hmid = dr("hmid", [NTOK, D], F32, dk)
    hout = dr("hout", [NTOK, D], F32, "ExternalOutput")
    out = dr("out", [NTOK - 16, D], F32, "ExternalOutput")
    aT = dr("aT", [5, 128, 44, 512], BF16, "Internal")
    hb, pfb, ptb, yb, hmb, hob, ob, ab = [Buf() for _ in range(8)]
    K = Ctx(); K.sob = Buf()
    with contextlib.ExitStack() as st0:
        K.ident = sbuf(C, st0, "ident", [128, 128], F32); identb = sbuf(C, st0, "identb", [128, 128], BF16)
        g1t = sbuf(C, st0, "g1t", [128, 16]); g2t = sbuf(C, st0, "g2t", [128, 16])
        K.idb, idbb, g1b, g2b = [Buf() for _ in range(4)]
        P.op(POOL, lambda e: e.memset(K.ident[:], 1.0), writes=[K.idb])
        P.op(POOL, lambda e: e.affine_select(out=K.ident[:], in_=K.ident[:], pattern=[[-1, 128]], base=0, channel_multiplier=1,
                                             compare_op=ALU.is_equal, fill=0.0), reads=[K.idb], writes=[K.idb])
        P.op(POOL, lambda e: e.tensor_copy(out=identb[:], in_=K.ident[:]), reads=[K.idb], writes=[idbb])
        P.dma(SP, g1t[:], g1, writes=[g1b]); P.dma(SP, g2t[:], g2, writes=[g2b])
        with contextlib.ExitStack() as st1:
            uT = sbuf(C, st1, "uT", [128, 16, NTOK], BF16); ub = Buf()
            pst = Ring([psum(C, st1, f"pst{i}", [128, 1024], BF16) for i in range(2)])
            psm = Ring([psum(C, st1, f"psm{i}", [128, 512], F32) for i in range(6)])
            with contextlib.ExitStack() as st:
                P.phase = "norm"
                phase_norm(C, st, hin, hb, g1t, g1b, uT, ub, pst, identb, idbb)
            P.barrier()
            with contextlib.ExitStack() as st:
                P.phase = "proj"
                phase_proj(C, st, uT, ub, win, pf, pfb, pt, ptb, psm, prm["hist_out"], K.sob)
            P.barrier()
        with contextlib.ExitStack() as st1:
            K.ones = sbuf(C, st1, "ones", [64, TP]); K.onesb = Buf()
            K.mask_i = sbuf(C, st1, "mask_i", [64, 64]); K.mib = Buf()
            K.mask5 = sbuf(C, st1, "mask5", [64, 5, 64]); K.m5b = Buf()
            K.flagE = sbuf(C, st1, "flagE", [128, 1]); K.fb = Buf()
            P.op(POOL, lambda e: e.memset(K.ones[:], 1.0), writes=[K.onesb])
            P.dma(SP, K.mask_i[:], prm["mask_i"], writes=[K.mib]); P.dma(SP, K.mask5[:], prm["mask5"], writes=[K.m5b])
            P.dma(SP, K.flagE[:], prm["flagE"], writes=[K.fb])
            with contextlib.ExitStack() as st:
                P.phase = "prepass"
                gate_prepass(C, st, pt, ptb)
            P.barrier()
            with contextlib.ExitStack() as st:
                P.phase = "glamlstm"
                run_gens([mixer_gla(C, st, pf, pfb, pt, ptb, y, yb, prm, K, 1, 3), mixer_mlstm(C, st, pf, pfb, pt, ptb, y, yb, prm, K, 3, 1)])
            P.barrier()
            if True:
              with contextlib.ExitStack() as st:
                P.phase = "rwkv"
                mixer_rwkv3(C, st, pf, pfb, y, yb, prm, K)
            P.barrier()
        if emit_out:
            with contextlib.ExitStack() as st1:
                uT = sbuf(C, st1, "uT2", [128, 16, NTOK], BF16); ub = Buf()
                pst = Ring([psum(C, st1, f"pst{i}", [128, 1024], BF16) for i in range(2)])
                psm = Ring([psum(C, st1, f"psm{i}", [128, 512], F32) for i in range(6)])
                with contextlib.ExitStack() as st:
                    P.phase = "wout"
                    phase_wout(C, st, y, yb, hin, hb, hmid, hmb, wout, uT, ub, psm, pst, identb, idbb)
                P.barrier()
                with contextlib.ExitStack() as st:
                    P.phase = "norm"
                    phase_norm(C, st, hmid, hmb, g2t, g2b, uT, ub, pst, identb, idbb)
                P.barrier()
                with contextlib.ExitStack() as st:
                    P.phase = "ffn1"
                    phase_ffn1(C, st, uT, ub, w1, w3, aT, ab, psm)
                P.barrier()
    if emit_out:
        with contextlib.ExitStack() as st:
            psm = Ring([psum(C, st, f"psn{i}", [128, 512], F32) for i in range(6)])
            P.phase = "ffn2"
            phase_ffn2(C, st, aT, ab, w2, hmid, hmb, hout, hob, psm)
        P.barrier()
        if do_final:
            with contextlib.ExitStack() as st:
                gft = sbuf(C, st, "gft2", [128, D]); gfb = Buf()
                P.dma(SP, gft[:], gf, writes=[gfb])
                P.phase = "final"
                phase_final_norm(C, st, hout, hob, gft, gfb, out, ob)
    fin = [K.sob, hob, ob]
    if debug:
        fin += [pfb, ptb, yb, hmb]
    P.finish(fin)
    P.emit()
    C.counts = {e: (len(P.ops[e]), sum(1 for o in P.ops[e] if o.signal)) for e in ENGS}
    return nc, C


import contextlib
import numpy as np

LAYER_KEYS = ["gla_a2", "gla_ab", "gla_normbc", "ml_cw", "ml_cb", "ml_ib", "ml_fb", "ml_normbc", "rw_muA", "rw_muL", "rw_w2", "rw_a2",
              "rw_g2", "rw_ch", "rw_rk", "rw_lnw_bc", "rw_lnb_bc"]
STATE_KEYS = ["sA", "sB", "sC", "mC", "hist"]


def emit_half(C, K, T, l, half):
    P = C.P
    hin, hinb = T["hin"][(l, half)]
    hout, houtb = T["hout"][(l, half)]
    prm = {k: T["lp"][k][l] for k in LAYER_KEYS}
    for k in ("mask_i", "mask5", "onehot"):
        prm[k] = T["const"][k]
    prm["flagE"] = T["flag1"] if half == 0 else T["flag0"]
    for k in STATE_KEYS:
        prm[k + "_in"] = T["zstate"][k] if half == 0 else T["state"][k][l]
        prm[k + "_out"] = T["state"][k][l] if half == 0 else T["sdump"][k]
    K.sob = T["stateb"][l] if half == 0 else T["sdumpb"]
    K.fb = Buf()
    win, wout, w1, w3, w2 = T["win"][l], T["wout"][l], T["w1"][l], T["w3"][l], T["w2"][l]
    pf, pfb, pt, ptb, y, yb, hmid, hmb, aT, ab = T["pf"], T["pfb"], T["pt"], T["ptb"], T["y"], T["yb"], T["hmid"], T["hmb"], T["aT"], T["ab"]
    identb, idbb = K.identb, K.idbb
    with contextlib.ExitStack() as st1:
        uT = sbuf(C, st1, "uT", [128, 16, NTOK], BF16); ub = Buf()
        pst = Ring([psum(C, st1, f"pst{i}", [128, 1024], BF16) for i in range(2)])
        psm = Ring([psum(C, st1, f"psm{i}", [128, 512], F32) for i in range(6)])
        with contextlib.ExitStack() as st:
            phase_norm(C, st, hin, hinb, K.g1t[l], K.g1b, uT, ub, pst, identb, idbb)
        P.barrier()
        with contextlib.ExitStack() as st:
            phase_proj(C, st, uT, ub, win, pf, pfb, pt, ptb, psm, prm["hist_out"], K.sob)
        P.barrier()
    with contextlib.ExitStack() as st1:
        K.flagE = sbuf(C, st1, "flagE", [128, 1])
        P.dma(SP, K.flagE[:], prm["flagE"], writes=[K.fb])
        K.ones = sbuf(C, st1, "ones", [64, TP]); K.onesb = Buf()
        K.mask_i = sbuf(C, st1, "mask_i", [64, 64]); K.mib = Buf()
        K.mask5 = sbuf(C, st1, "mask5", [64, 5, 64]); K.m5b = Buf()
        P.op(POOL, lambda e: e.memset(K.ones[:], 1.0), writes=[K.onesb])
        P.dma(SP, K.mask_i[:], T["const"]["mask_i"], writes=[K.mib]); P.dma(SP, K.mask5[:], T["const"]["mask5"], writes=[K.m5b])
        with contextlib.ExitStack() as st:
            gate_prepass(C, st, pt, ptb)
        P.barrier()
        with contextlib.ExitStack() as st:
            run_gens([mixer_gla(C, st, pf, pfb, pt, ptb, y, yb, prm, K, 1, 3), mixer_mlstm(C, st, pf, pfb, pt, ptb, y, yb, prm, K, 3, 1)])
        P.barrier()
        with contextlib.ExitStack() as st:
            mixer_rwkv3(C, st, pf, pfb, y, yb, prm, K)
        P.barrier()
    with contextlib.ExitStack() as st1:
        uT = sbuf(C, st1, "uT2", [128, 16, NTOK], BF16); ub = Buf()
        pst = Ring([psum(C, st1, f"pst{i}", [128, 1024], BF16) for i in range(2)])
        psm = Ring([psum(C, st1, f"psm{i}", [128, 512], F32) for i in range(6)])
        with contextlib.ExitStack() as st:
            phase_wout(C, st, y, yb, hin, hinb, hmid, hmb, wout, uT, ub, psm, pst, identb, idbb)
        P.barrier()
        with contextlib.ExitStack() as st:
            phase_norm(C, st, hmid, hmb, K.g2t[l], K.g1b, uT, ub, pst, identb, idbb)
        P.barrier()
        with contextlib.ExitStack() as st:
            phase_ffn1(C, st, uT, ub, w1, w3, aT, ab, psm)
        P.barrier()
    with contextlib.ExitStack() as st:
        psm = Ring([psum(C, st, f"psn{i}", [128, 512], F32) for i in range(6)])
        phase_ffn2(C, st, aT, ab, w2, hmid, hmb, hout, houtb, psm)
    P.barrier()
    if l == 1:
        with contextlib.ExitStack() as st:
            gft = sbuf(C, st, "gft2", [128, D]); gfb = Buf()
            P.dma(SP, gft[:], T["gf"], writes=[gfb])
            phase_final_norm(C, st, hout, houtb, gft, gfb, T["out"][half], T["outb"])
        P.barrier()


def build_fused(nlayers=2, halves=(0, 1)):
    nc = bass.Bass("TRN2", target_bir_lowering=False)
    C = Ctx(); C.nc = nc; C.P = Prog(nc); C.emit_out = True
    P = C.P
    dr = lambda n, s, dt=F32, kind="ExternalInput": nc.dram_tensor(n, s, dt, kind=kind).ap()
    T = {}
    xin = [dr("xE", [NTOK, D]), dr("xO", [NTOK, D])]
    T["win"] = dr("win", [2, 13, 128, 8192]); T["wout"] = dr("wout", [2, 4, 128, 8192])
    T["w1"] = dr("w1", [2, 11, 128, 8192]); T["w3"] = dr("w3", [2, 11, 128, 8192]); T["w2"] = dr("w2", [2, 4, 4, 128, 11 * 512])
    g1 = dr("g1", [2, 128, 16]); g2 = dr("g2", [2, 128, 16]); T["gf"] = dr("gf", [128, D])
    T["lp"] = {k: dr(k, [2] + PRM_SHAPES[k]) for k in LAYER_KEYS}
    T["const"] = {k: dr(k, PRM_SHAPES[k]) for k in ("mask_i", "mask5", "onehot")}
    T["flag1"] = dr("flag1", [128, 1]); T["flag0"] = dr("flag0", [128, 1])
    T["zstate"] = {k: dr("z_" + k, PRM_SHAPES[k + "_in"]) for k in STATE_KEYS}
    T["state"] = {k: dr("st_" + k, [2] + PRM_SHAPES[k + "_in"], F32, "Internal") for k in STATE_KEYS}
    T["sdump"] = {k: dr("sd_" + k, PRM_SHAPES[k + "_in"], F32, "Internal") for k in STATE_KEYS}
    T["stateb"] = [Buf(), Buf()]; T["sdumpb"] = Buf()
    T["pf"] = dr("pf", [NFMB * 128, TP], F32, "Internal"); T["pt"] = dr("pt", [NTOK, NTMC], F32, "Internal")
    T["y"] = dr("y", [NTOK, D], BF16, "Internal"); T["hmid"] = dr("hmid", [NTOK, D], F32, "Internal")
    T["aT"] = dr("aT", [5, 128, 44, 512], BF16, "Internal")
    for k in ("pfb", "ptb", "yb", "hmb", "ab", "outb"):
        T[k] = Buf()
    h1 = [dr("h1E", [NTOK, D], F32, "Internal"), dr("h1O", [NTOK, D], F32, "Internal")]
    h2 = [dr("h2E", [NTOK, D], F32, "Internal"), dr("h2O", [NTOK, D], F32, "Internal")]
    T["out"] = [dr("outE", [2048, D], F32, "ExternalOutput"), dr("outO", [2048, D], F32, "ExternalOutput")]
    xb = [Buf(), Buf()]; h1b = [Buf(), Buf()]; h2b = [Buf(), Buf()]
    T["hin"] = {(0, 0): (xin[0], xb[0]), (0, 1): (xin[1], xb[1]), (1, 0): (h1[0], h1b[0]), (1, 1): (h1[1], h1b[1])}
    T["hout"] = {(0, 0): (h1[0], h1b[0]), (0, 1): (h1[1], h1b[1]), (1, 0): (h2[0], h2b[0]), (1, 1): (h2[1], h2b[1])}
    K = Ctx()
    with contextlib.ExitStack() as st0:
        K.ident = sbuf(C, st0, "ident", [128, 128], F32); K.identb = sbuf(C, st0, "identb", [128, 128], BF16)
        K.g1t = [sbuf(C, st0, f"g1t{l}", [128, 16]) for l in range(2)]; K.g2t = [sbuf(C, st0, f"g2t{l}", [128, 16]) for l in range(2)]
        K.idb, K.idbb, K.g1b = Buf(), Buf(), Buf()
        P.op(POOL, lambda e: e.memset(K.ident[:], 1.0), writes=[K.idb])
        P.op(POOL, lambda e: e.affine_select(out=K.ident[:], in_=K.ident[:], pattern=[[-1, 128]], base=0, channel_multiplier=1,
                                             compare_op=ALU.is_equal, fill=0.0), reads=[K.idb], writes=[K.idb])
        P.op(POOL, lambda e: e.tensor_copy(out=K.identb[:], in_=K.ident[:]), reads=[K.idb], writes=[K.idbb])
        for l in range(2):
            P.dma(SP, K.g1t[l][:], g1[l], pwrites=[K.g1b]); P.dma(SP, K.g2t[l][:], g2[l], pwrites=[K.g1b])
        for l in range(nlayers):
            for half in halves:
                emit_half(C, K, T, l, half)
    P.finish([T["outb"], T["sdumpb"], T["stateb"][0], T["stateb"][1]])
    P.emit()
    C.counts = {e: (len(P.ops[e]), sum(1 for o in P.ops[e] if o.signal)) for e in ENGS}
    return nc, C


from concourse.bass_utils import run_bass_kernel_spmd

_PROG = {}


def kernel(**z):
    x = np.asarray(z["x"], np.float32)
    meta = np.asarray(z["meta_tokens"], np.float32)
    if "nc" not in _PROG:
        _PROG["nc"] = build_fused()[0]
    nc = _PROG["nc"]
    shared = {}
    shared.update(host_consts())
    shared["gf"] = np.ascontiguousarray(np.broadcast_to(np.asarray(z["norm_final"], np.float32), (128, D)))
    Ws = [host_layer_weights(z, l) for l in range(2)]
    for k in ("win", "wout", "w1", "w3", "w2", "g1", "g2"):
        shared[k] = np.stack([Ws[0][k], Ws[1][k]])
    Ps = [host_layer_params(z, l) for l in range(2)]
    for k in LAYER_KEYS:
        shared[k] = np.stack([Ps[0][k], Ps[1][k]])
    shared["flag1"] = np.ones((128, 1), np.float32)
    shared["flag0"] = np.zeros((128, 1), np.float32)
    for k in STATE_KEYS:
        shared["z_" + k] = np.zeros(PRM_SHAPES[k + "_in"], np.float32)
    in_maps = []
    for c in range(8):
        b = c % 4
        im = dict(shared)
        im["xE"] = np.ascontiguousarray(np.concatenate([meta, x[b, :2048]], 0))
        im["xO"] = np.ascontiguousarray(np.concatenate([meta, x[b, 2048:]], 0))
        in_maps.append(im)
    res = run_bass_kernel_spmd(nc, in_maps, core_ids=list(range(8))).results
    out = np.zeros((4, 4096, D), np.float32)
    for b in range(4):
        out[b, :2048] = np.asarray(res[b]["outE"], np.float32)
        out[b, 2048:] = np.asarray(res[b]["outO"], np.float32)
    return out
```

```python
import contextlib
import numpy as np
import concourse.bass as bass
import concourse.mybir as mybir

F32 = mybir.dt.float32
BF16 = mybir.dt.bfloat16
AF = mybir.ActivationFunctionType
ALU = mybir.AluOpType
AX = mybir.AxisListType

PE, ACT, DVE, POOL, SP = "pe", "act", "dve", "pool", "sp"
ENGS = [PE, ACT, DVE, POOL, SP]
SEG = 30000
NSLOT = 6


class Buf:
    __slots__ = ("name", "w", "ws", "rs")

    def __init__(self, name=""):
        self.name = name
        self.w = None
        self.ws = {}
        self.rs = {}


def _key(o):
    return (o.eng, o.slot if o.dma else None)


class Op:
    __slots__ = ("eng", "fn", "deps", "dma", "idx", "signal", "ev", "slot", "name")

    def __init__(self, eng, fn, dma):
        self.eng = eng
        self.fn = fn
        self.dma = dma
        self.deps = []
        self.signal = False
        self.ev = None
        self.slot = None
        self.name = ""


class Prog:
    def __init__(self, nc, same_engine_sync=True):
        self.nc = nc
        self.ops = {e: [] for e in ENGS}
        self.same = same_engine_sync
        self.ndma = {e: 0 for e in ENGS}
        self.final_deps = []
        self.pending_barrier = None
        self.scopes = False
        self.phase = ""

    def op(self, eng, fn, reads=(), writes=(), dma=False, name="", pwrites=()):
        o = Op(eng, fn, dma)
        o.name = getattr(self, "phase", "")
        if dma:
            o.slot = self.ndma[eng] % NSLOT
            self.ndma[eng] += 1
            o.signal = True
        o.idx = len(self.ops[eng])
        deps = []
        for r in reads:
            if r.w is not None:
                deps.append(r.w)
            deps.extend(r.ws.values())
        for w in writes:
            if w.w is not None:
                deps.append(w.w)
            deps.extend(w.ws.values())
            deps.extend(w.rs.values())
        for w in pwrites:
            if w.w is not None:
                deps.append(w.w)
            deps.extend(w.rs.values())
        if self.pending_barrier and self.pending_barrier.get(eng):
            deps.extend(self.pending_barrier[eng])
            self.pending_barrier[eng] = []
        best = {}
        for d in deps:
            if d is o:
                continue
            if d.eng == eng and not d.dma:
                if eng == PE or not self.same:
                    continue
            k = _key(d)
            if k not in best or best[k].idx < d.idx:
                best[k] = d
        for d in best.values():
            o.deps.append(d)
            d.signal = True
        for w in writes:
            w.w = o
            w.ws = {}
            w.rs = {}
        for w in pwrites:
            w.ws[_key(o)] = o
        for r in reads:
            r.rs[_key(o)] = o
        self.ops[eng].append(o)
        return o

    def dma(self, eng, out, in_, reads=(), writes=(), pwrites=(), **kw):
        return self.op(eng, lambda e: e.dma_start(out=out, in_=in_, **kw), reads, writes, dma=True, pwrites=pwrites)

    def finish(self, bufs):
        for b in bufs:
            for o in ([b.w] if b.w is not None else []) + list(b.ws.values()):
                self.final_deps.append(o)
                o.signal = True

    def emit(self):
        nc = self.nc
        with contextlib.ExitStack() as st:
            csem = {}
            for e in (PE, ACT, DVE, POOL):
                n = sum(1 for o in self.ops[e] if o.signal and not o.dma)
                nseg = n // SEG + 1
                csem[e] = [st.enter_context(nc.semaphore(f"c_{e}_{i}")) for i in range(nseg)]
            dsem = {}
            for e in (ACT, POOL, SP):
                if self.ndma[e] > 0:
                    dsem[e] = [st.enter_context(nc.semaphore(f"d_{e}_{i}")) for i in range(NSLOT)]
            for e in ENGS:
                cnt = 0
                dcur = [0] * NSLOT
                for o in self.ops[e]:
                    if o.dma:
                        prev = dcur[o.slot]
                        dcur[o.slot] += 16
                        o.ev = (dsem[e][o.slot], dcur[o.slot], prev)
                    elif o.signal:
                        seg, v = divmod(cnt, SEG)
                        o.ev = (csem[e][seg], v + 1, None)
                        cnt += 1
            block = st.enter_context(nc.Block())
            handles = {PE: block.tensor, ACT: block.scalar, DVE: block.vector,
                       POOL: block.gpsimd, SP: block.sync}
            for e in ENGS:
                ops = self.ops[e]
                fdeps = self.final_deps if e == SP else []
                if not ops and not fdeps:
                    continue

                def body(eng, ops=ops, fdeps=fdeps):
                    known = {}

                    def wait(sem, val):
                        k = id(sem)
                        if known.get(k, 0) >= val:
                            return
                        eng.wait_ge(sem, val)
                        known[k] = val

                    cur_ph, sid = None, None
                    for o in ops:
                        if self.scopes and o.name != cur_ph:
                            if cur_ph:
                                nc.leave_named_scope(cur_ph, sid, False)
                            cur_ph = o.name
                            if cur_ph:
                                sid, _ = nc.enter_named_scope(cur_ph, False)
                        for d in o.deps:
                            wait(d.ev[0], d.ev[1])
                        if o.dma and o.ev[2] > 0:
                            wait(o.ev[0], o.ev[2])
                        ins = o.fn(eng)
                        if o.dma:
                            ins.then_inc(o.ev[0], 16)
                        elif o.signal:
                            ins.then_inc(o.ev[0], 1)
                    if self.scopes and cur_ph:
                        nc.leave_named_scope(cur_ph, sid, False)
                    for d in fdeps:
                        wait(d.ev[0], d.ev[1])

                handles[e](body)


def _barrier(self):
    lasts = []
    for e in ENGS:
        ops = self.ops[e]
        if not ops:
            continue
        for o in reversed(ops):
            if not o.dma:
                lasts.append(o)
                break
        seen = set()
        for o in reversed(ops):
            if o.dma and o.slot not in seen:
                seen.add(o.slot)
                lasts.append(o)
            if len(seen) == NSLOT:
                break
    for o in lasts:
        o.signal = True
    self.pending_barrier = {e: list(lasts) for e in ENGS}


Prog.barrier = _barrier


class Ring:
    def __init__(self, tiles):
        self.tiles = tiles
        self.bufs = [Buf() for _ in tiles]
        self.i = 0

    def next(self):
        k = self.i % len(self.tiles)
        self.i += 1
        return self.tiles[k], self.bufs[k]


class _HP:
    pass
hp = _HP()


import numpy as np

A0, B0, C0 = 0, 1696, 3248


def fm_blocks():
    blks = []
    for seg in range(3):
        for i in range(4):
            blks.append(list(range(A0 + seg * 512 + i * 128, A0 + seg * 512 + (i + 1) * 128)))
    blks.append(list(range(A0 + 1536, A0 + 1600)))
    blks.append(list(range(A0 + 1600, A0 + 1696)))
    for seg in range(2):
        for i in range(2):
            blks.append(list(range(B0 + seg * 256 + i * 128, B0 + seg * 256 + (i + 1) * 128)))
    blks.append(list(range(B0 + 1024, B0 + 1040)))
    for seg in range(2):
        for i in range(4):
            blks.append(list(range(C0 + seg * 512 + i * 128, C0 + seg * 512 + (i + 1) * 128)))
    blks.append(list(range(C0 + 2048, C0 + 2056)))
    assert len(blks) == 28
    return blks


def tm_cols():
    cols = []
    cols += list(range(B0 + 512, B0 + 1024))
    cols += list(range(B0 + 1040, B0 + 1552))
    cols += list(range(C0 + 1024, C0 + 2048))
    cols += list(range(C0 + 2056, C0 + 3080))
    assert len(cols) == 3072
    return cols


def tile_k(W, ncol=512):
    K = W.shape[0]
    return np.ascontiguousarray(W.reshape(K // 128, 128, ncol).transpose(1, 0, 2).reshape(128, (K // 128) * ncol))


def prep_win(w):
    blks = fm_blocks()
    out = np.zeros((13, 128, 8192), np.float32)
    for wi in range(7):
        Wt = np.zeros((2048, 512), np.float32)
        for bi in range(4):
            cols = blks[wi * 4 + bi]
            Wt[:, bi * 128:bi * 128 + len(cols)] = w[:, cols]
        out[wi] = tile_k(Wt)
    tc = tm_cols()
    for ci in range(6):
        out[7 + ci] = tile_k(w[:, tc[ci * 512:(ci + 1) * 512]])
    return out


def prep_sq(w, ncb):
    return np.stack([tile_k(w[:, i * 512:(i + 1) * 512]) for i in range(ncb)])


def prep_w2(w):
    out = np.zeros((4, 4, 128, 11 * 512), np.float32)
    for cb in range(4):
        for pc in range(4):
            out[cb, pc] = tile_k(w[pc * 1408:(pc + 1) * 1408, cb * 512:(cb + 1) * 512])
    return out


def gT(g):
    return np.ascontiguousarray(g.reshape(16, 128).T)


for _n in ['fm_blocks','tm_cols','tile_k','prep_win','prep_sq','prep_w2','gT']:
    setattr(hp, _n, globals()[_n])


import contextlib

D = 2048
NTOK = 2064
NMETA = 16
TP = 2070
DFF = 5632
NFMB = 28
NTMC = 3072
EPS = 1e-6

TG = [(0, 16)] + [(16 + 512 * i, 512) for i in range(4)]
TT = [(0, 16)] + [(16 + 128 * i, 128) for i in range(16)]


def pfcol(t):
    return 3 + t if t < 16 else t + 6


class Ctx:
    pass


_uid = [0]


def sbuf(C, st, name, shape, dt=F32):
    _uid[0] += 1
    return st.enter_context(C.nc.sbuf_tensor(f"{name}_{_uid[0]}", shape, dt))


def psum(C, st, name, shape, dt=F32):
    _uid[0] += 1
    return st.enter_context(C.nc.psum_tensor(f"{name}_{_uid[0]}", shape, dt))


def make_wloader(C, st, n_stage=2, n_wb=2, stage_elems=8192):
    stage = Ring([sbuf(C, st, f"wst{i}", [128, stage_elems], F32) for i in range(n_stage)])
    return stage


def load_w(C, stage, dst_ap, dst_buf, src_ap, nelem, cast_eng=POOL):
    P = C.P
    stt, stb = stage.next()
    P.dma(SP, stt[:, 0:nelem], src_ap, writes=[stb])
    P.op(cast_eng, lambda e: e.tensor_copy(out=dst_ap, in_=stt[:, 0:nelem]), reads=[stb], writes=[dst_buf])


def phase_norm(C, st, hsrc, hbuf, gT, gbuf, uT, ubuf, ps_t, ident_bf, idbuf):
    P = C.P
    hring = Ring([sbuf(C, st, f"nh{i}", [128, D], F32) for i in range(2)])
    hnring = Ring([sbuf(C, st, f"nhn{i}", [128, D], BF16) for i in range(2)])
    junk = sbuf(C, st, "njunk", [128, D], BF16)
    jb = Buf()
    stat = Ring([sbuf(C, st, f"nst{i}", [128, 4], F32) for i in range(2)])
    mh = sbuf(C, st, "nmh", [128, 1], F32)
    mhb = Buf()
    P.op(POOL, lambda e: e.memset(mh[:], -0.5), writes=[mhb])
    for (t0, nt) in TT:
        ht, hb = hring.next()
        hn, hnb = hnring.next()
        s, sb_ = stat.next()
        P.dma(SP, ht[0:nt, :], hsrc[t0:t0 + nt, :], reads=[hbuf], writes=[hb])
        P.op(ACT, lambda e, ht=ht, s=s, nt=nt: e.activation(out=junk[0:nt, :], in_=ht[0:nt, :], func=AF.Square,
                                                            accum_out=s[0:nt, 0:1]), reads=[hb], writes=[jb, sb_])
        P.op(DVE, lambda e, s=s, nt=nt: e.tensor_scalar(out=s[0:nt, 1:2], in0=s[0:nt, 0:1], scalar1=1.0 / D, scalar2=EPS,
                                                        op0=ALU.mult, op1=ALU.add), reads=[sb_], writes=[sb_])
        P.op(POOL, lambda e, s=s, nt=nt: e.tensor_tensor(out=s[0:nt, 2:3], in0=s[0:nt, 1:2], in1=mh[0:nt, :], op=ALU.pow),
             reads=[sb_, mhb], writes=[sb_])
        P.op(DVE, lambda e, ht=ht, hn=hn, s=s, nt=nt: e.tensor_scalar(out=hn[0:nt, :], in0=ht[0:nt, :], scalar1=s[0:nt, 2:3],
                                                                      scalar2=None, op0=ALU.mult), reads=[hb, sb_], writes=[hnb])
        for half in range(2):
            pt_, ptb = ps_t.next()
            for k in range(8):
                kb = half * 8 + k
                P.op(PE, lambda e, pt_=pt_, hn=hn, kb=kb, k=k, nt=nt: e.transpose(
                    out=pt_[:, k * 128:k * 128 + nt], in_=hn[0:nt, kb * 128:(kb + 1) * 128], identity=ident_bf[0:nt, 0:nt]),
                    reads=[hnb, idbuf], writes=[ptb])
            eng = DVE if half == 0 else POOL
            if half == 0:
                P.op(DVE, lambda e, pt_=pt_, nt=nt, t0=t0, half=half: e.tensor_tensor(
                    out=uT[:, half * 8:half * 8 + 8, t0:t0 + nt],
                    in0=pt_[:].rearrange("p (k t) -> p k t", k=8)[:, :, 0:nt],
                    in1=gT[:, half * 8:half * 8 + 8].unsqueeze(2).broadcast_to([128, 8, nt]), op=ALU.mult),
                    reads=[ptb, gbuf], pwrites=[ubuf])
            else:
                P.op(DVE, lambda e, pt_=pt_, nt=nt, t0=t0, half=half: e.tensor_tensor(
                    out=uT[:, half * 8:half * 8 + 8, t0:t0 + nt],
                    in0=pt_[:].rearrange("p (k t) -> p k t", k=8)[:, :, 0:nt],
                    in1=gT[:, half * 8:half * 8 + 8].unsqueeze(2).broadcast_to([128, 8, nt]), op=ALU.mult),
                    reads=[ptb, gbuf], pwrites=[ubuf])


def phase_proj(C, st, uT, ubuf, w_dram, pf, pfbuf, pt, ptbuf, ps_mm, hist_out=None, hob=None):
    P = C.P
    stage = make_wloader(C, st)
    wb = Ring([sbuf(C, st, f"pwb{i}", [128, 16, 512], BF16) for i in range(2)])
    ev = Ring([sbuf(C, st, f"pev{i}", [128, 512], F32) for i in range(4)])
    cnt = 0
    for wi in range(13):
        wt, wbuf = wb.next()
        load_w(C, stage, wt[:].rearrange("p k c -> p (k c)"), wbuf, w_dram[wi], 8192)
        if wi < 7:
            for bi in range(4):
                blk = wi * 4 + bi
                for (t0, nt) in TG:
                    pm, pmb = ps_mm.next()
                    for kb in range(16):
                        P.op(PE, lambda e, pm=pm, wt=wt, kb=kb, bi=bi, t0=t0, nt=nt: e.matmul(
                            pm[:, 0:nt], lhsT=wt[:, kb, bi * 128:(bi + 1) * 128], rhs=uT[:, kb, t0:t0 + nt],
                            start=(kb == 0), stop=(kb == 15)), reads=[wbuf, ubuf], writes=[pmb])
                    et, eb = ev.next()
                    if cnt % 2 == 0:
                        P.op(ACT, lambda e, et=et, pm=pm, nt=nt: e.activation(out=et[:, 0:nt], in_=pm[:, 0:nt], func=AF.Copy),
                             reads=[pmb], writes=[eb])
                    else:
                        P.op(DVE, lambda e, et=et, pm=pm, nt=nt: e.tensor_copy(out=et[:, 0:nt], in_=pm[:, 0:nt]),
                             reads=[pmb], writes=[eb])
                    cnt += 1
                    c0 = pfcol(t0)
                    P.dma(POOL, pf[blk * 128:(blk + 1) * 128, c0:c0 + nt], et[:, 0:nt], reads=[eb], pwrites=[pfbuf])
                    if hist_out is not None and t0 + nt == NTOK:
                        P.dma(POOL, hist_out[blk * 128:(blk + 1) * 128, :], et[:, nt - 3:nt], reads=[eb], pwrites=[hob])
        else:
            ci = wi - 7
            for (t0, nt) in TT:
                pm, pmb = ps_mm.next()
                for kb in range(16):
                    P.op(PE, lambda e, pm=pm, wt=wt, kb=kb, t0=t0, nt=nt: e.matmul(
                        pm[0:nt, :], lhsT=uT[:, kb, t0:t0 + nt], rhs=wt[:, kb, :],
                        start=(kb == 0), stop=(kb == 15)), reads=[wbuf, ubuf], writes=[pmb])
                et, eb = ev.next()
                gfn = None
                if gfn is not None:
                    P.op(ACT, lambda e, et=et, pm=pm, nt=nt, gfn=gfn: e.activation(out=et[0:nt, :], in_=pm[0:nt, :], func=gfn),
                         reads=[pmb], writes=[eb])
                elif cnt % 2 == 0:
                    P.op(ACT, lambda e, et=et, pm=pm, nt=nt: e.activation(out=et[0:nt, :], in_=pm[0:nt, :], func=AF.Copy),
                         reads=[pmb], writes=[eb])
                else:
                    P.op(DVE, lambda e, et=et, pm=pm, nt=nt: e.tensor_copy(out=et[0:nt, :], in_=pm[0:nt, :]),
                         reads=[pmb], writes=[eb])
                cnt += 1
                P.dma(POOL, pt[t0:t0 + nt, ci * 512:(ci + 1) * 512], et[0:nt, :], reads=[eb], pwrites=[ptbuf])


def phase_wout(C, st, y, ybuf, hsrc, hbuf, hdst, hdbuf, w_dram, uT, ubuf, ps_mm, ps_t, ident_bf, idbuf):
    P = C.P
    yr = Ring([sbuf(C, st, f"oy{i}", [128, D], BF16) for i in range(2)])
    for (t0, nt) in TT:
        yt, yb = yr.next()
        P.dma(SP, yt[0:nt, :], y[t0:t0 + nt, :], reads=[ybuf], writes=[yb])
        for half in range(2):
            pt_, ptb = ps_t.next()
            for k in range(8):
                kb = half * 8 + k
                P.op(PE, lambda e, pt_=pt_, yt=yt, kb=kb, k=k, nt=nt: e.transpose(
                    out=pt_[:, k * 128:k * 128 + nt], in_=yt[0:nt, kb * 128:(kb + 1) * 128], identity=ident_bf[0:nt, 0:nt]),
                    reads=[yb, idbuf], writes=[ptb])
            eng = ACT if half == 0 else DVE
            if half == 0:
                P.op(ACT, lambda e, pt_=pt_, nt=nt, t0=t0, half=half: e.activation(
                    out=uT[:, half * 8:half * 8 + 8, t0:t0 + nt],
                    in_=pt_[:].rearrange("p (k t) -> p k t", k=8)[:, :, 0:nt], func=AF.Copy), reads=[ptb], pwrites=[ubuf])
            else:
                P.op(DVE, lambda e, pt_=pt_, nt=nt, t0=t0, half=half: e.tensor_copy(
                    out=uT[:, half * 8:half * 8 + 8, t0:t0 + nt],
                    in_=pt_[:].rearrange("p (k t) -> p k t", k=8)[:, :, 0:nt]), reads=[ptb], pwrites=[ubuf])
    stage = make_wloader(C, st)
    wb = Ring([sbuf(C, st, f"owb{i}", [128, 16, 512], BF16) for i in range(2)])
    hr = Ring([sbuf(C, st, f"ohr{i}", [128, 512], F32) for i in range(3)])
    ev = Ring([sbuf(C, st, f"oev{i}", [128, 512], F32) for i in range(3)])
    nxt = wb.next()
    load_w(C, stage, nxt[0][:].rearrange("p k c -> p (k c)"), nxt[1], w_dram[0], 8192)
    for ci in range(4):
        wt, wbuf = nxt
        if ci + 1 < 4:
            nxt = wb.next()
            load_w(C, stage, nxt[0][:].rearrange("p k c -> p (k c)"), nxt[1], w_dram[ci + 1], 8192)
        for (t0, nt) in TT:
            ho, hob = hr.next()
            P.dma(SP, ho[0:nt, :], hsrc[t0:t0 + nt, ci * 512:(ci + 1) * 512], reads=[hbuf], writes=[hob])
            pm, pmb = ps_mm.next()
            for kb in range(16):
                P.op(PE, lambda e, pm=pm, wt=wt, kb=kb, t0=t0, nt=nt: e.matmul(
                    pm[0:nt, :], lhsT=uT[:, kb, t0:t0 + nt], rhs=wt[:, kb, :],
                    start=(kb == 0), stop=(kb == 15)), reads=[wbuf, ubuf], writes=[pmb])
            et, eb = ev.next()
            P.op(DVE, lambda e, et=et, pm=pm, ho=ho, nt=nt: e.tensor_tensor(out=et[0:nt, :], in0=pm[0:nt, :], in1=ho[0:nt, :],
                                                                            op=ALU.add), reads=[pmb, hob], writes=[eb])
            P.dma(POOL, hdst[t0:t0 + nt, ci * 512:(ci + 1) * 512], et[0:nt, :], reads=[eb], pwrites=[hdbuf])


def phase_ffn1(C, st, uT, ubuf, w1_dram, w3_dram, aT, abuf, ps_mm):
    P = C.P
    stage = make_wloader(C, st)
    w1b = Ring([sbuf(C, st, f"f1w{i}", [128, 16, 512], BF16) for i in range(2)])
    w3b = Ring([sbuf(C, st, f"f3w{i}", [128, 16, 512], BF16) for i in range(2)])
    sg = Ring([sbuf(C, st, f"fsg{i}", [128, 512], F32) for i in range(3)])
    av = Ring([sbuf(C, st, f"fav{i}", [128, 512], BF16) for i in range(3)])
    for gi in range(11):
        w1t, w1buf = w1b.next()
        w3t, w3buf = w3b.next()
        load_w(C, stage, w1t[:].rearrange("p k c -> p (k c)"), w1buf, w1_dram[gi], 8192)
        load_w(C, stage, w3t[:].rearrange("p k c -> p (k c)"), w3buf, w3_dram[gi], 8192)
        for bi in range(4):
            j = gi * 4 + bi
            for gidx, (t0, nt) in enumerate(TG):
                pa, pab = ps_mm.next()
                for kb in range(16):
                    P.op(PE, lambda e, pa=pa, w1t=w1t, kb=kb, bi=bi, t0=t0, nt=nt: e.matmul(
                        pa[:, 0:nt], lhsT=w1t[:, kb, bi * 128:(bi + 1) * 128], rhs=uT[:, kb, t0:t0 + nt],
                        start=(kb == 0), stop=(kb == 15)), reads=[w1buf, ubuf], writes=[pab])
                pb_, pbb = ps_mm.next()
                for kb in range(16):
                    P.op(PE, lambda e, pb_=pb_, w3t=w3t, kb=kb, bi=bi, t0=t0, nt=nt: e.matmul(
                        pb_[:, 0:nt], lhsT=w3t[:, kb, bi * 128:(bi + 1) * 128], rhs=uT[:, kb, t0:t0 + nt],
                        start=(kb == 0), stop=(kb == 15)), reads=[w3buf, ubuf], writes=[pbb])
                s, sb_ = sg.next()
                a, ab_ = av.next()
                P.op(ACT, lambda e, s=s, pa=pa, nt=nt: e.activation(out=s[:, 0:nt], in_=pa[:, 0:nt], func=AF.Silu),
                     reads=[pab], writes=[sb_])
                P.op(DVE, lambda e, a=a, s=s, pb_=pb_, nt=nt: e.tensor_tensor(out=a[:, 0:nt], in0=pb_[:, 0:nt], in1=s[:, 0:nt],
                                                                              op=ALU.mult), reads=[pbb, sb_], writes=[ab_])
                P.dma(POOL, aT[gidx, :, j, 0:nt], a[:, 0:nt], reads=[ab_], pwrites=[abuf])


def phase_ffn2(C, st, aT, abuf, w2_dram, hsrc, hbuf, hdst, hdbuf, ps_mm):
    P = C.P
    stage = Ring([sbuf(C, st, f"gst{i}", [128, 11 * 512], F32) for i in range(2)])
    w2b = Ring([sbuf(C, st, f"gw{i}", [128, 44, 512], BF16) for i in range(1)])
    ar = Ring([sbuf(C, st, f"gar{i}", [128, 44, 512], BF16) for i in range(2)])
    hr = Ring([sbuf(C, st, f"ghr{i}", [128, 512], F32) for i in range(3)])
    ev = Ring([sbuf(C, st, f"gev{i}", [128, 512], F32) for i in range(3)])
    for cb in range(4):
        wt, wbuf = w2b.next()
        for pc in range(4):
            load_w(C, stage, wt[:, pc * 11:(pc + 1) * 11, :].rearrange("p k c -> p (k c)"), wbuf, w2_dram[cb, pc], 11 * 512)
        for gidx, (g0, gn) in enumerate(TG):
            at, atb = ar.next()
            P.dma(SP, at[:, :, 0:gn], aT[gidx, :, :, 0:gn], reads=[abuf], writes=[atb])
            for s0 in range(0, gn, 128):
                nt = min(128, gn - s0)
                t0 = g0 + s0
                ho, hob = hr.next()
                P.dma(SP, ho[0:nt, :], hsrc[t0:t0 + nt, cb * 512:(cb + 1) * 512], reads=[hbuf], writes=[hob])
                pm, pmb = ps_mm.next()
                for j in range(44):
                    P.op(PE, lambda e, pm=pm, at=at, wt=wt, j=j, s0=s0, nt=nt: e.matmul(
                        pm[0:nt, :], lhsT=at[:, j, s0:s0 + nt], rhs=wt[:, j, :],
                        start=(j == 0), stop=(j == 43)), reads=[wbuf, atb], writes=[pmb])
                et, eb = ev.next()
                P.op(DVE, lambda e, et=et, pm=pm, ho=ho, nt=nt: e.tensor_tensor(out=et[0:nt, :], in0=pm[0:nt, :], in1=ho[0:nt, :],
                                                                                op=ALU.add), reads=[pmb, hob], writes=[eb])
                P.dma(POOL, hdst[t0:t0 + nt, cb * 512:(cb + 1) * 512], et[0:nt, :], reads=[eb], pwrites=[hdbuf])


def phase_final_norm(C, st, hsrc, hbuf, gbc, gbcb, out, obuf):
    P = C.P
    hring = Ring([sbuf(C, st, f"zh{i}", [128, D], F32) for i in range(2)])
    oring = Ring([sbuf(C, st, f"zo{i}", [128, D], F32) for i in range(2)])
    junk = sbuf(C, st, "zjunk", [128, D], BF16)
    jb = Buf()
    stat = Ring([sbuf(C, st, f"zst{i}", [128, 4], F32) for i in range(2)])
    mh = sbuf(C, st, "zmh", [128, 1], F32)
    mhb = Buf()
    P.op(POOL, lambda e: e.memset(mh[:], -0.5), writes=[mhb])
    for (t0, nt) in TT[1:]:
        ht, hb = hring.next()
        ot, ob = oring.next()
        s, sb_ = stat.next()
        P.dma(SP, ht[0:nt, :], hsrc[t0:t0 + nt, :], reads=[hbuf], writes=[hb])
        P.op(ACT, lambda e, ht=ht, s=s, nt=nt: e.activation(out=junk[0:nt, :], in_=ht[0:nt, :], func=AF.Square,
                                                            accum_out=s[0:nt, 0:1]), reads=[hb], writes=[jb, sb_])
        P.op(DVE, lambda e, s=s, nt=nt: e.tensor_scalar(out=s[0:nt, 1:2], in0=s[0:nt, 0:1], scalar1=1.0 / D, scalar2=EPS,
                                                        op0=ALU.mult, op1=ALU.add), reads=[sb_], writes=[sb_])
        P.op(POOL, lambda e, s=s, nt=nt: e.tensor_tensor(out=s[0:nt, 2:3], in0=s[0:nt, 1:2], in1=mh[0:nt, :], op=ALU.pow),
             reads=[sb_, mhb], writes=[sb_])
        P.op(DVE, lambda e, ht=ht, ot=ot, s=s, nt=nt: e.scalar_tensor_tensor(
            out=ot[0:nt, :], in0=ht[0:nt, :], scalar=s[0:nt, 2:3], in1=gbc[0:nt, :], op0=ALU.mult, op1=ALU.mult),
            reads=[hb, sb_, gbcb], writes=[ob])
        P.dma(POOL, out[t0 - 16:t0 - 16 + nt, :], ot[0:nt, :], reads=[ob], pwrites=[obuf])


import contextlib

SEGS = [(3, 16, 0), (22, 2048, 16)]
LDK = 0.6065306597126334


def chunks_of(seg):
    c0, ncol, t0 = seg
    if ncol == 16:
        return [(c0, 16, t0)]
    return [(c0 + 64 * i, 64, t0 + 64 * i) for i in range(ncol // 64)]


def fix_gap(C, x, xb, hist_src, flagE, fb, tmp_ring, rows):
    P = C.P
    t, tb = tmp_ring.next()
    P.dma(SP, t[0:rows, 0:3], hist_src, writes=[tb])
    P.op(DVE, lambda e: e.scalar_tensor_tensor(out=x[0:rows, 19:22], in0=x[0:rows, 16:19], scalar=flagE[0:rows, 0:1],
                                               in1=t[0:rows, 0:3], op0=ALU.mult, op1=ALU.add), reads=[xb, tb, fb], writes=[xb])


def chunk_rel(C, out, ob, src, sb_, rows):
    P = C.P
    P.op(DVE, lambda e: e.tensor_copy(out=out[0:rows, 3:19], in_=src[0:rows, 3:19]), reads=[sb_], writes=[ob])
    P.op(DVE, lambda e: e.tensor_copy(out=out[0:rows, 22:86], in_=src[0:rows, 22:86]), reads=[sb_], writes=[ob])
    o3 = out[0:rows, 86:2070].rearrange("p (c j) -> p c j", j=64)
    s3 = src[0:rows, 86:2070].rearrange("p (c j) -> p c j", j=64)
    pv = src[0:rows, 22:2006].rearrange("p (c j) -> p c j", j=64)[:, :, 63:64].broadcast_to([rows, 31, 64])
    P.op(DVE, lambda e: e.tensor_tensor(out=o3, in0=s3, in1=pv, op=ALU.subtract), reads=[sb_], writes=[ob])


def gate_prepass(C, st, pt, ptb):
    P = C.P
    r = Ring([sbuf(C, st, f"gp{i}", [128, 1536], F32) for i in range(3)])
    for (t0, nt) in TT:
        t, tb = r.next()
        P.dma(SP, t[0:nt, 0:512], pt[t0:t0 + nt, 512:1024], reads=[ptb], writes=[tb])
        P.dma(SP, t[0:nt, 512:1536], pt[t0:t0 + nt, 2048:3072], reads=[ptb], writes=[tb])
        P.op(ACT, lambda e, t=t, nt=nt: e.activation(out=t[0:nt, 0:512], in_=t[0:nt, 0:512], func=AF.Silu), reads=[tb], writes=[tb])
        P.op(ACT, lambda e, t=t, nt=nt: e.activation(out=t[0:nt, 512:1536], in_=t[0:nt, 512:1536], func=AF.Sigmoid),
             reads=[tb], writes=[tb])
        P.dma(POOL, pt[t0:t0 + nt, 512:1024], t[0:nt, 0:512], reads=[tb], writes=[ptb])
        P.dma(POOL, pt[t0:t0 + nt, 2048:3072], t[0:nt, 512:1536], reads=[tb], writes=[ptb])


def mixer_gla(C, st, pf, pfb, pt, ptb, y, yb, prm, K, npsA=2, npsB=6):
    P = C.P
    ones, onesb, ident, idb, mask_i, mib, flagE, fb = K.ones, K.onesb, K.ident, K.idb, K.mask_i, K.mib, K.flagE, K.fb
    a2 = sbuf(C, st, "ga2", [32, 256]); a2b = Buf()
    P.op(DVE, lambda e: e.memset(a2[:], 0.0), writes=[a2b])
    nab = sbuf(C, st, "gnab", [64, 4]); nabb = Buf()
    nbc = sbuf(C, st, "gnbc", [64, 128]); nbcb = Buf()
    P.dma(SP, a2[0:16, :], prm["gla_a2"], reads=[a2b], writes=[a2b])
    P.dma(SP, nab[:], prm["gla_ab"], writes=[nabb])
    P.op(DVE, lambda e: e.tensor_scalar(out=nab[:], in0=nab[:], scalar1=-1.0, scalar2=None, op0=ALU.mult), reads=[nabb], writes=[nabb])
    P.dma(SP, nbc[:], prm["gla_normbc"], writes=[nbcb])
    xa = sbuf(C, st, "gxa", [32, TP]); xab = Buf()
    P.dma(SP, xa[:], pf[18 * 128:18 * 128 + 32, :], reads=[pfb], writes=[xab])
    q = sbuf(C, st, "gq", [64, TP]); k = sbuf(C, st, "gk", [64, TP]); sp = sbuf(C, st, "gsp", [64, TP])
    spc = sbuf(C, st, "gspc", [64, TP]); e1 = sbuf(C, st, "ge1", [64, TP]); e2 = sbuf(C, st, "ge2", [64, TP])
    qb, kb_, spb, spcb, e1b, e2b = [Buf() for _ in range(6)]
    S = sbuf(C, st, "gS", [64, 128]); Sb = Buf()
    S16 = sbuf(C, st, "gS16", [64, 128], BF16); S16b = Buf()
    qbf = sbuf(C, st, "gqbf", [64, TP], BF16); kbf = sbuf(C, st, "gkbf", [64, TP], BF16); qbfb, kbfb = Buf(), Buf()
    v16r = Ring([sbuf(C, st, f"gv16{i}", [64, 128], BF16) for i in range(3)])
    Sin = sbuf(C, st, "gSin", [64, 128]); Sinb = Buf()
    psA = Ring([psum(C, st, f"gpa{i}", [128, 512], F32) for i in range(npsA)])
    psB = Ring([psum(C, st, f"gpb{i}", [128, 512], F32) for i in range(npsB)])
    vr = Ring([sbuf(C, st, f"gv{i}", [64, 256], F32) for i in range(3)])
    ktr = Ring([sbuf(C, st, f"gkt{i}", [64, 64], BF16) for i in range(2)])
    scr = Ring([sbuf(C, st, f"gsc{i}", [64, 64], BF16) for i in range(2)])
    str_ = Ring([sbuf(C, st, f"gst{i}", [64, 4], F32) for i in range(2)])
    junk = sbuf(C, st, "gjunk", [64, 128], F32); jb = Buf()
    mh = sbuf(C, st, "gmh", [64, 1], F32); mhb = Buf()
    P.op(POOL, lambda e: e.memset(mh[:], -0.5), writes=[mhb])
    t1r = Ring([sbuf(C, st, f"gt1{i}", [64, 128], F32) for i in range(2)])
    yor = Ring([sbuf(C, st, f"gyo{i}", [64, 128], BF16) for i in range(2)])
    for h in range(4):
        r0 = (14 + h // 2) * 128 + (h % 2) * 64
        r1 = (16 + h // 2) * 128 + (h % 2) * 64
        P.dma(SP, q[:], pf[r0:r0 + 64, :], reads=[pfb], writes=[qb])
        P.dma(SP, k[:], pf[r1:r1 + 64, :], reads=[pfb], writes=[kb_])
        for c0 in range(3, TP, 512):
            n = min(512, TP - c0)
            pa, pab = psA.next()
            P.op(PE, lambda e, pa=pa, c0=c0, n=n, h=h: e.matmul(pa[0:64, 0:n], lhsT=a2[0:32, h * 64:(h + 1) * 64], rhs=xa[0:32, c0:c0 + n],
                                                               start=True, stop=True), reads=[a2b, xab], writes=[pab])
            P.op(ACT, lambda e, pa=pa, c0=c0, n=n, h=h: e.activation(out=sp[:, c0:c0 + n], in_=pa[0:64, 0:n], func=AF.Exp, scale=-1.0,
                                                                    bias=nab[:, h:h + 1]), reads=[pab, nabb], writes=[spb])
        P.op(ACT, lambda e: e.activation(out=sp[:, 3:TP], in_=sp[:, 3:TP], func=AF.Ln, bias=1.0), reads=[spb], writes=[spb])
        for (c0, ncol, t0) in SEGS:
            P.op(DVE, lambda e, c0=c0, ncol=ncol: e.tensor_tensor_scan(out=spc[:, c0:c0 + ncol], data0=ones[0:64, c0:c0 + ncol],
                                                                       data1=sp[:, c0:c0 + ncol], initial=0.0, op0=ALU.mult, op1=ALU.add),
                 reads=[spb, onesb], writes=[spcb])
        chunk_rel(C, sp, spb, spc, spcb, 64)
        P.op(ACT, lambda e: e.activation(out=e1[:, 3:TP], in_=sp[:, 3:TP], func=AF.Exp, scale=-1.0 / 16), reads=[spb], writes=[e1b])
        P.op(ACT, lambda e: e.activation(out=e2[:, 3:TP], in_=sp[:, 3:TP], func=AF.Exp, scale=1.0 / 16), reads=[spb], writes=[e2b])
        P.op(DVE, lambda e: e.scalar_tensor_tensor(out=q[:, 3:TP], in0=q[:, 3:TP], scalar=0.125, in1=e1[:, 3:TP], op0=ALU.mult,
                                                   op1=ALU.mult), reads=[qb, e1b], writes=[qb])
        P.op(DVE, lambda e: e.tensor_tensor(out=k[:, 3:TP], in0=k[:, 3:TP], in1=e2[:, 3:TP], op=ALU.mult), reads=[kb_, e2b], writes=[kb_])
        P.op(ACT, lambda e: e.activation(out=qbf[:, 3:TP], in_=q[:, 3:TP], func=AF.Copy), reads=[qb], writes=[qbfb])
        P.op(ACT, lambda e: e.activation(out=kbf[:, 3:TP], in_=k[:, 3:TP], func=AF.Copy), reads=[kb_], writes=[kbfb])
        P.op(DVE, lambda e: e.memset(S[:], 0.0), writes=[Sb])
        P.op(DVE, lambda e: e.memset(S16[:], 0.0), writes=[S16b])
        for si, seg in enumerate(SEGS):
            if si == 1:
                P.dma(SP, Sin[:], prm["sB_in"][h], writes=[Sinb])
                P.op(DVE, lambda e: e.scalar_tensor_tensor(out=S[:], in0=S[:], scalar=flagE[0:64, 0:1], in1=Sin[:], op0=ALU.mult,
                                                           op1=ALU.add), reads=[Sb, Sinb, fb], writes=[Sb])
                P.op(ACT, lambda e: e.activation(out=S16[:], in_=S[:], func=AF.Copy), reads=[Sb], writes=[S16b])
            for (c0, n, t0) in chunks_of(seg):
                vt, vb = vr.next()
                P.dma(SP, vt[0:n, 0:128], pt[t0:t0 + n, h * 128:(h + 1) * 128], reads=[ptb], writes=[vb])
                P.dma(SP, vt[0:n, 128:256], pt[t0:t0 + n, 512 + h * 128:512 + (h + 1) * 128], reads=[ptb], writes=[vb])
                pb, pbb = psB.next()
                P.op(PE, lambda e, pb=pb, c0=c0, n=n: e.transpose(out=pb[0:n, 0:64], in_=k[:, c0:c0 + n], identity=ident[0:64, 0:64]),
                     reads=[kb_, idb], writes=[pbb])
                P.op(PE, lambda e, pb=pb, c0=c0, n=n: e.matmul(pb[0:n, 64:64 + n], lhsT=kbf[:, c0:c0 + n], rhs=qbf[:, c0:c0 + n], start=True, stop=True),
                     reads=[kbfb, qbfb], writes=[pbb])
                v16, v16b = v16r.next()
                P.op(ACT, lambda e, v16=v16, vt=vt, n=n: e.activation(out=v16[0:n, :], in_=vt[0:n, 0:128], func=AF.Copy), reads=[vb], writes=[v16b])
                kt, ktb = ktr.next()
                sc, scb = scr.next()
                P.op(ACT, lambda e, kt=kt, pb=pb, n=n: e.activation(out=kt[0:n, :], in_=pb[0:n, 0:64], func=AF.Copy), reads=[pbb], writes=[ktb])
                P.op(DVE, lambda e, sc=sc, pb=pb, n=n: e.tensor_tensor(out=sc[0:n, 0:n], in0=pb[0:n, 64:64 + n], in1=mask_i[0:n, 0:n], op=ALU.mult),
                     reads=[pbb, mib], writes=[scb])
                po, pob = psB.next()
                P.op(PE, lambda e, po=po, c0=c0, n=n: e.matmul(po[0:n, 0:128], lhsT=qbf[:, c0:c0 + n], rhs=S16[:, :], start=True, stop=False),
                     reads=[qbfb, S16b], writes=[pob])
                P.op(PE, lambda e, po=po, sc=sc, v16=v16, n=n: e.matmul(po[0:n, 0:128], lhsT=sc[0:n, 0:n], rhs=v16[0:n, :], start=False, stop=True),
                     reads=[scb, v16b], writes=[pob])
                pc, pcb = psB.next()
                P.op(PE, lambda e, pc=pc, kt=kt, v16=v16, n=n: e.matmul(pc[0:64, 0:128], lhsT=kt[0:n, 0:64], rhs=v16[0:n, :], start=True, stop=True),
                     reads=[ktb, v16b], writes=[pcb])
                ce = c0 + n - 1
                P.op(DVE, lambda e, pc=pc: e.tensor_tensor(out=S[:], in0=S[:], in1=pc[0:64, 0:128], op=ALU.add), reads=[pcb, Sb], writes=[Sb])
                P.op(DVE, lambda e, ce=ce: e.tensor_scalar(out=S[:], in0=S[:], scalar1=e1[:, ce:ce + 1], scalar2=None, op0=ALU.mult),
                     reads=[Sb, e1b], writes=[Sb])
                P.op(ACT, lambda e: e.activation(out=S16[:], in_=S[:], func=AF.Copy), reads=[Sb], writes=[S16b])
                if C.emit_out and not False:
                    s_, sb2 = str_.next()
                    t1, t1b = t1r.next()
                    yo, yob = yor.next()
                    P.op(ACT, lambda e, t1=t1, po=po, n=n: e.activation(out=t1[0:n, :], in_=po[0:n, 0:128], func=AF.Copy), reads=[pob], writes=[t1b])
                    P.op(DVE, lambda e, t1=t1, n=n: e.tensor_tensor(out=junk[0:n, :], in0=t1[0:n, :], in1=t1[0:n, :], op=ALU.mult), reads=[t1b], writes=[jb])
                    P.op(DVE, lambda e, s_=s_, n=n: e.tensor_reduce(out=s_[0:n, 0:1], in_=junk[0:n, :], axis=AX.X, op=ALU.add), reads=[jb], writes=[sb2])
                    P.op(DVE, lambda e, s_=s_, n=n: e.tensor_scalar(out=s_[0:n, 1:2], in0=s_[0:n, 0:1], scalar1=1.0 / 128, scalar2=EPS, op0=ALU.mult,
                                                                   op1=ALU.add), reads=[sb2], writes=[sb2])
                    P.op(POOL, lambda e, s_=s_, n=n: e.tensor_tensor(out=s_[0:n, 2:3], in0=s_[0:n, 1:2], in1=mh[0:n, :], op=ALU.pow),
                         reads=[sb2, mhb], writes=[sb2])
                    P.op(DVE, lambda e, t1=t1, s_=s_, n=n: e.scalar_tensor_tensor(out=t1[0:n, :], in0=t1[0:n, :], scalar=s_[0:n, 2:3],
                                                                               in1=nbc[0:n, :], op0=ALU.mult, op1=ALU.mult),
                         reads=[sb2, nbcb, t1b], writes=[t1b])
                    P.op(DVE, lambda e, yo=yo, t1=t1, vt=vt, n=n: e.tensor_tensor(out=yo[0:n, :], in0=t1[0:n, :], in1=vt[0:n, 128:256], op=ALU.mult),
                         reads=[t1b, vb], writes=[yob])
                    P.dma(SP, y[t0:t0 + n, 512 + h * 128:512 + (h + 1) * 128], yo[0:n, :], reads=[yob], pwrites=[yb])
                yield
        P.dma(POOL, prm["sB_out"][h], S[:], reads=[Sb], pwrites=[K.sob])


def mixer_mlstm(C, st, pf, pfb, pt, ptb, y, yb, prm, K, npsA=4, npsG=2):
    P = C.P
    ones, onesb, ident, idb, mask_i, mib, flagE, fb = K.ones, K.onesb, K.ident, K.idb, K.mask_i, K.mib, K.flagE, K.fb
    cw = sbuf(C, st, "mcw", [128, 8, 4]); cb = sbuf(C, st, "mcb", [128, 8]); cwb = Buf()
    ib = sbuf(C, st, "mib", [4, 2]); fbb = sbuf(C, st, "mfb", [4, 2]); gbb = Buf()
    nbc = sbuf(C, st, "mnbc", [64, 1024]); nbcb = Buf()
    oh = sbuf(C, st, "moh", [4, 4, 128]); ohb = Buf()
    P.dma(SP, cw[:], prm["ml_cw"], writes=[cwb]); P.dma(SP, cb[:], prm["ml_cb"], writes=[cwb])
    P.dma(SP, ib[:, 0:1], prm["ml_ib"], writes=[gbb]); P.dma(SP, fbb[:, 0:1], prm["ml_fb"], writes=[gbb])
    P.dma(SP, nbc[:], prm["ml_normbc"], writes=[nbcb]); P.dma(SP, oh[:], prm["onehot"], writes=[ohb])
    P.op(DVE, lambda e: e.tensor_scalar(out=ib[:, 1:2], in0=ib[:, 0:1], scalar1=1.0 / 15, scalar2=None, op0=ALU.mult), reads=[gbb], writes=[gbb])
    P.op(DVE, lambda e: e.tensor_scalar(out=fbb[:, 1:2], in0=fbb[:, 0:1], scalar1=1.0 / 15, scalar2=None, op0=ALU.mult), reads=[gbb], writes=[gbb])
    gi = sbuf(C, st, "mgi", [4, TP]); gf = sbuf(C, st, "mgf", [4, TP]); SPc = sbuf(C, st, "mSP", [4, TP]); av = sbuf(C, st, "mav", [4, TP])
    MU = sbuf(C, st, "mMU", [4, TP]); MUS = sbuf(C, st, "mMUS", [4, TP])
    G8 = sbuf(C, st, "mG8", [36, TP]); FE = sbuf(C, st, "mFE", [4, 64]); min_ = sbuf(C, st, "mmin", [4, 4])
    gib, gfb_, SPb, avb, MUb, MUSb, G8b, FEb, minb = [Buf() for _ in range(9)]
    P.dma(SP, gi[:], pf[27 * 128:27 * 128 + 4, :], reads=[pfb], writes=[gib])
    P.dma(SP, gf[:], pf[27 * 128 + 4:27 * 128 + 8, :], reads=[pfb], writes=[gfb_])
    P.op(DVE, lambda e: e.memset(G8[:], 0.0), writes=[G8b])
    P.op(ACT, lambda e: e.activation(out=gi[:], in_=gi[:], func=AF.Tanh, scale=1.0 / 15, bias=ib[:, 1:2]), reads=[gib, gbb], writes=[gib])
    P.op(ACT, lambda e: e.activation(out=gf[:], in_=gf[:], func=AF.Tanh, scale=1.0 / 15, bias=fbb[:, 1:2]), reads=[gfb_, gbb], writes=[gfb_])
    P.op(ACT, lambda e: e.activation(out=gf[:], in_=gf[:], func=AF.Exp, scale=-15.0), reads=[gfb_], writes=[gfb_])
    P.op(ACT, lambda e: e.activation(out=gf[:], in_=gf[:], func=AF.Ln, bias=1.0), reads=[gfb_], writes=[gfb_])
    P.dma(SP, min_[:, 0:1], prm["mC_in"], writes=[minb])
    for si, (c0, ncol, t0) in enumerate(SEGS):
        P.op(DVE, lambda e, c0=c0, ncol=ncol: e.tensor_tensor_scan(out=SPc[:, c0:c0 + ncol], data0=ones[0:4, c0:c0 + ncol], data1=gf[:, c0:c0 + ncol],
                                                                   initial=0.0, op0=ALU.mult, op1=ALU.add), reads=[gfb_, onesb], writes=[SPb])
        P.op(DVE, lambda e, c0=c0, ncol=ncol: e.scalar_tensor_tensor(out=av[:, c0:c0 + ncol], in0=gi[:, c0:c0 + ncol], scalar=15.0,
                                                                     in1=SPc[:, c0:c0 + ncol], op0=ALU.mult, op1=ALU.add), reads=[gib, SPb], writes=[avb])
        if si == 0:
            P.op(DVE, lambda e, c0=c0, ncol=ncol: e.tensor_tensor_scan(out=MU[:, c0:c0 + ncol], data0=av[:, c0:c0 + ncol], data1=av[:, c0:c0 + ncol],
                                                                       initial=0.0, op0=ALU.max, op1=ALU.max), reads=[avb], writes=[MUb])
            P.op(DVE, lambda e: e.memset(MUS[:, 3:19], 0.0), writes=[MUSb])
            P.op(DVE, lambda e: e.tensor_tensor(out=min_[:, 1:2], in0=MU[:, 18:19], in1=SPc[:, 18:19], op=ALU.subtract), reads=[MUb, SPb, minb], writes=[minb])
            P.op(DVE, lambda e: e.scalar_tensor_tensor(out=min_[:, 2:3], in0=min_[:, 1:2], scalar=flagE[0:4, 0:1], in1=min_[:, 0:1], op0=ALU.mult,
                                                       op1=ALU.add), reads=[minb, fb], writes=[minb])
        else:
            P.op(DVE, lambda e, c0=c0, ncol=ncol: e.tensor_tensor_scan(out=MU[:, c0:c0 + ncol], data0=av[:, c0:c0 + ncol], data1=av[:, c0:c0 + ncol],
                                                                       initial=min_[:, 2:3], op0=ALU.max, op1=ALU.max), reads=[avb, minb], writes=[MUb])
            P.op(DVE, lambda e: e.tensor_copy(out=MUS[:, 22:86], in_=min_[:, 2:3].broadcast_to([4, 64])), reads=[minb], writes=[MUSb])
            P.op(DVE, lambda e: e.tensor_copy(out=MUS[:, 86:2070].rearrange("p (c j) -> p c j", j=64),
                                              in_=MU[:, 22:2006].rearrange("p (c j) -> p c j", j=64)[:, :, 63:64].broadcast_to([4, 31, 64])),
                 reads=[MUb], writes=[MUSb])
    P.op(DVE, lambda e: e.tensor_tensor(out=av[:, 3:TP], in0=av[:, 3:TP], in1=MUS[:, 3:TP], op=ALU.subtract), reads=[avb, MUSb], writes=[avb])
    P.op(ACT, lambda e: e.activation(out=G8[0:4, 3:TP], in_=av[:, 3:TP], func=AF.Exp), reads=[avb, G8b], writes=[G8b])
    P.op(DVE, lambda e: e.tensor_tensor(out=av[:, 3:TP], in0=SPc[:, 3:TP], in1=MUS[:, 3:TP], op=ALU.subtract), reads=[SPb, MUSb, G8b], writes=[avb])
    P.op(ACT, lambda e: e.activation(out=G8[32:36, 3:TP], in_=av[:, 3:TP], func=AF.Exp), reads=[avb, G8b], writes=[G8b])
    P.op(DVE, lambda e: e.tensor_tensor(out=FE[:, 0:1], in0=MUS[:, 3:4], in1=MU[:, 18:19], op=ALU.subtract), reads=[MUSb, MUb], writes=[FEb])
    P.op(DVE, lambda e: e.tensor_tensor(out=FE[:, 1:33], in0=MUS[:, 22:2070].rearrange("p (c j) -> p c j", j=64)[:, :, 0],
                                        in1=MU[:, 22:2070].rearrange("p (c j) -> p c j", j=64)[:, :, 63], op=ALU.subtract),
         reads=[MUSb, MUb, FEb], writes=[FEb])
    P.op(ACT, lambda e: e.activation(out=FE[:, 0:33], in_=FE[:, 0:33], func=AF.Exp), reads=[FEb], writes=[FEb])
    P.op(DVE, lambda e: e.tensor_tensor(out=min_[:, 3:4], in0=MU[:, TP - 1:TP], in1=SPc[:, TP - 1:TP], op=ALU.subtract), reads=[MUb, SPb, minb], writes=[minb])
    P.dma(POOL, prm["mC_out"], min_[:, 3:4], reads=[minb], pwrites=[K.sob])
    xq = sbuf(C, st, "mxq", [128, TP]); xk = sbuf(C, st, "mxk", [128, TP]); q = sbuf(C, st, "mq", [128, TP]); k = sbuf(C, st, "mk", [128, TP])
    xqb, xkb, qb, kb_ = [Buf() for _ in range(4)]
    tmpr = Ring([sbuf(C, st, f"mtmp{i}", [128, 4], F32) for i in range(2)])
    CX = sbuf(C, st, "mCX", [128, 257]); CXb = Buf()
    CX16 = sbuf(C, st, "mCX16", [128, 258], BF16); CX16b = Buf()
    qbf = sbuf(C, st, "mqbf", [128, TP], BF16); kbf = sbuf(C, st, "mkbf", [128, TP], BF16); qbfb, kbfb = Buf(), Buf()
    CXin = sbuf(C, st, "mCXin", [128, 257]); CXinb = Buf()
    FB = sbuf(C, st, "mFB", [128, 64]); FBb = Buf()
    psA = Ring([psum(C, st, f"mpa{i}", [128, 512], F32) for i in range(npsA)])
    psG = Ring([psum(C, st, f"mpg{i}", [128, 512], F32) for i in range(npsG)])
    vr = Ring([sbuf(C, st, f"mv{i}", [64, 512], F32) for i in range(3)])
    vxr = Ring([sbuf(C, st, f"mvx{i}", [64, 258], BF16) for i in range(2)])
    for _t, _b in zip(vxr.tiles, vxr.bufs):
        P.op(DVE, lambda e, _t=_t: e.memset(_t[:], 0.0), writes=[_b])
    ktr = Ring([sbuf(C, st, f"mkt{i}", [64, 128], BF16) for i in range(2)])
    scr = Ring([sbuf(C, st, f"msc{i}", [64, 64], BF16) for i in range(2)])
    gtr = Ring([sbuf(C, st, f"mgt{i}", [64, 36], F32) for i in range(2)])
    str_ = Ring([sbuf(C, st, f"mst{i}", [64, 8], F32) for i in range(2)])
    junk = sbuf(C, st, "mjunk", [64, 256], F32); jb = Buf()
    mh = sbuf(C, st, "mmh", [64, 1], F32); mhb = Buf()
    P.op(POOL, lambda e: e.memset(mh[:], -0.5), writes=[mhb])
    t1r = Ring([sbuf(C, st, f"mt1{i}", [64, 256], F32) for i in range(2)])
    yor = Ring([sbuf(C, st, f"myo{i}", [64, 256], BF16) for i in range(2)])
    for h in range(4):
        P.dma(SP, xq[:], pf[(19 + h) * 128:(20 + h) * 128, :], reads=[pfb], writes=[xqb])
        P.dma(SP, xk[:], pf[(23 + h) * 128:(24 + h) * 128, :], reads=[pfb], writes=[xkb])
        P.op(DVE, lambda e: e.memset(xq[:, 0:3], 0.0), reads=[xqb], writes=[xqb])
        P.op(DVE, lambda e: e.memset(xk[:, 0:3], 0.0), reads=[xkb], writes=[xkb])
        fix_gap(C, xq, xqb, prm["hist_in"][(19 + h) * 128:(20 + h) * 128, :], flagE, fb, tmpr, 128)
        fix_gap(C, xk, xkb, prm["hist_in"][(23 + h) * 128:(24 + h) * 128, :], flagE, fb, tmpr, 128)
        for (x, xb, o, ob, j) in ((xq, xqb, q, qb, h), (xk, xkb, k, kb_, 4 + h)):
            P.op(DVE, lambda e, x=x, o=o, j=j: e.tensor_scalar(out=o[:, 3:TP], in0=x[:, 0:TP - 3], scalar1=cw[:, j, 0:1], scalar2=cb[:, j:j + 1],
                                                               op0=ALU.mult, op1=ALU.add), reads=[xb, cwb], writes=[ob])
            for tap in range(1, 4):
                P.op(DVE, lambda e, x=x, o=o, j=j, tap=tap: e.scalar_tensor_tensor(out=o[:, 3:TP], in0=x[:, tap:TP - 3 + tap], scalar=cw[:, j, tap:tap + 1],
                                                                                 in1=o[:, 3:TP], op0=ALU.mult, op1=ALU.add), reads=[xb, cwb, ob], writes=[ob])
            P.op(ACT, lambda e, o=o: e.activation(out=o[:, 3:TP], in_=o[:, 3:TP], func=AF.Silu), reads=[ob], writes=[ob])
        P.op(DVE, lambda e: e.tensor_scalar(out=k[:, 3:TP], in0=k[:, 3:TP], scalar1=128 ** -0.5, scalar2=None, op0=ALU.mult), reads=[kb_], writes=[kb_])
        P.op(ACT, lambda e: e.activation(out=qbf[:, 3:TP], in_=q[:, 3:TP], func=AF.Copy), reads=[qb], writes=[qbfb])
        P.op(ACT, lambda e: e.activation(out=kbf[:, 3:TP], in_=k[:, 3:TP], func=AF.Copy), reads=[kb_], writes=[kbfb])
        pg, pgb = psG.next()
        P.op(PE, lambda e, pg=pg, h=h: e.matmul(pg[:, 0:33], lhsT=oh[0:4, h, :], rhs=FE[0:4, 0:33], start=True, stop=True), reads=[ohb, FEb], writes=[pgb])
        P.op(ACT, lambda e, pg=pg: e.activation(out=FB[:, 0:33], in_=pg[:, 0:33], func=AF.Copy), reads=[pgb], writes=[FBb])
        P.op(DVE, lambda e: e.memset(CX[:], 0.0), writes=[CXb])
        P.op(DVE, lambda e: e.memset(CX16[:], 0.0), writes=[CX16b])
        ci = 0
        for si, seg in enumerate(SEGS):
            if si == 1:
                P.dma(SP, CXin[:], prm["sC_in"][h], writes=[CXinb])
                P.op(DVE, lambda e: e.scalar_tensor_tensor(out=CX[:], in0=CX[:], scalar=flagE[:, 0:1], in1=CXin[:], op0=ALU.mult, op1=ALU.add),
                     reads=[CXb, CXinb, fb], writes=[CXb])
                P.op(ACT, lambda e: e.activation(out=CX16[:, 0:257], in_=CX[:], func=AF.Copy), reads=[CXb, CX16b], writes=[CX16b])
            for (c0, n, t0) in chunks_of(seg):
                vt, vb = vr.next()
                P.dma(SP, vt[0:n, 0:256], pt[t0:t0 + n, 1024 + h * 256:1024 + (h + 1) * 256], reads=[ptb], writes=[vb])
                P.dma(SP, vt[0:n, 256:512], pt[t0:t0 + n, 2048 + h * 256:2048 + (h + 1) * 256], reads=[ptb], writes=[vb])
                pa, pab = psA.next()
                P.op(PE, lambda e, pa=pa, c0=c0, n=n: e.transpose(out=pa[0:n, 0:128], in_=k[:, c0:c0 + n], identity=ident[:, :]), reads=[kb_, idb], writes=[pab])
                P.op(PE, lambda e, pa=pa, c0=c0, n=n: e.matmul(pa[0:n, 128:128 + n], lhsT=kbf[:, c0:c0 + n], rhs=qbf[:, c0:c0 + n], start=True, stop=True),
                     reads=[kbfb, qbfb], writes=[pab])
                P.op(PE, lambda e, pa=pa, c0=c0, n=n: e.transpose(out=pa[0:n, 192:228], in_=G8[0:36, c0:c0 + n], identity=ident[0:36, 0:36]),
                     reads=[G8b, idb], writes=[pab])
                kt, ktb = ktr.next(); sc, scb = scr.next(); gt, gtb = gtr.next()
                P.op(ACT, lambda e, kt=kt, pa=pa, n=n: e.activation(out=kt[0:n, :], in_=pa[0:n, 0:128], func=AF.Copy), reads=[pab], writes=[ktb])
                P.op(DVE, lambda e, sc=sc, pa=pa, n=n: e.tensor_tensor(out=sc[0:n, 0:n], in0=pa[0:n, 128:128 + n], in1=mask_i[0:n, 0:n], op=ALU.mult),
                     reads=[pab, mib], writes=[scb])
                P.op(ACT, lambda e, gt=gt, pa=pa, n=n: e.activation(out=gt[0:n, :], in_=pa[0:n, 192:228], func=AF.Copy), reads=[pab], writes=[gtb])
                vx, vxb = vxr.next()
                P.op(DVE, lambda e, vx=vx, vt=vt, gt=gt, n=n, h=h: e.tensor_scalar(out=vx[0:n, 0:256], in0=vt[0:n, 0:256], scalar1=gt[0:n, h:h + 1], scalar2=None,
                                                                                 op0=ALU.mult), reads=[vb, gtb], writes=[vxb])
                P.op(ACT, lambda e, vx=vx, gt=gt, n=n, h=h: e.activation(out=vx[0:n, 256:257], in_=gt[0:n, h:h + 1], func=AF.Copy), reads=[gtb, vxb], writes=[vxb])
                pn, pnb = psA.next()
                P.op(PE, lambda e, pn=pn, c0=c0, n=n: e.matmul(pn[0:n, 0:258], lhsT=qbf[:, c0:c0 + n], rhs=CX16[:, :], start=True, stop=False), reads=[qbfb, CX16b], writes=[pnb])
                P.op(PE, lambda e, pn=pn, sc=sc, vx=vx, n=n: e.matmul(pn[0:n, 0:258], lhsT=sc[0:n, 0:n], rhs=vx[0:n, :], start=False, stop=True),
                     reads=[scb, vxb], writes=[pnb])
                pc, pcb = psA.next()
                P.op(PE, lambda e, pc=pc, kt=kt, vx=vx, n=n: e.matmul(pc[:, 0:258], lhsT=kt[0:n, :], rhs=vx[0:n, :], start=True, stop=True),
                     reads=[ktb, vxb], writes=[pcb])
                P.op(DVE, lambda e, pc=pc: e.tensor_tensor(out=CX[:], in0=CX[:], in1=pc[:, 0:257], op=ALU.add), reads=[pcb, CXb], writes=[CXb])
                P.op(DVE, lambda e, ci=ci: e.tensor_scalar(out=CX[:], in0=CX[:], scalar1=FB[:, ci:ci + 1], scalar2=None, op0=ALU.mult),
                     reads=[CXb, FBb], writes=[CXb])
                P.op(ACT, lambda e: e.activation(out=CX16[:, 0:257], in_=CX[:], func=AF.Copy), reads=[CXb, CX16b], writes=[CX16b])
                if C.emit_out:
                    s_, sb2 = str_.next()
                    P.op(ACT, lambda e, s_=s_, pn=pn, n=n: e.activation(out=s_[0:n, 0:1], in_=pn[0:n, 256:257], func=AF.Abs),
                         reads=[pnb], writes=[sb2])
                    P.op(DVE, lambda e, s_=s_, gt=gt, n=n, h=h: e.tensor_tensor(out=s_[0:n, 0:1], in0=s_[0:n, 0:1], in1=gt[0:n, 32 + h:33 + h], op=ALU.max),
                         reads=[sb2, gtb], writes=[sb2])
                    P.op(DVE, lambda e, s_=s_, n=n: e.reciprocal(out=s_[0:n, 1:2], in_=s_[0:n, 0:1]), reads=[sb2], writes=[sb2])
                    P.op(ACT, lambda e, s_=s_, pn=pn, n=n: e.activation(out=junk[0:n, :], in_=pn[0:n, 0:256], func=AF.Square, scale=s_[0:n, 1:2],
                                                                       accum_out=s_[0:n, 2:3]), reads=[pnb, sb2], writes=[jb, sb2])
                    P.op(DVE, lambda e, s_=s_, n=n: e.tensor_scalar(out=s_[0:n, 3:4], in0=s_[0:n, 2:3], scalar1=1.0 / 256, scalar2=EPS, op0=ALU.mult,
                                                                   op1=ALU.add), reads=[sb2], writes=[sb2])
                    P.op(POOL, lambda e, s_=s_, n=n: e.tensor_tensor(out=s_[0:n, 4:5], in0=s_[0:n, 3:4], in1=mh[0:n, :], op=ALU.pow), reads=[sb2, mhb], writes=[sb2])
                    P.op(DVE, lambda e, s_=s_, n=n: e.tensor_tensor(out=s_[0:n, 5:6], in0=s_[0:n, 4:5], in1=s_[0:n, 1:2], op=ALU.mult), reads=[sb2], writes=[sb2])
                    t1, t1b = t1r.next(); yo, yob = yor.next()
                    P.op(DVE, lambda e, t1=t1, pn=pn, s_=s_, n=n, h=h: e.scalar_tensor_tensor(out=t1[0:n, :], in0=pn[0:n, 0:256], scalar=s_[0:n, 5:6],
                                                                                          in1=nbc[0:n, h * 256:(h + 1) * 256], op0=ALU.mult, op1=ALU.mult),
                         reads=[pnb, sb2, nbcb], writes=[t1b])
                    P.op(DVE, lambda e, yo=yo, t1=t1, vt=vt, n=n: e.tensor_tensor(out=yo[0:n, :], in0=t1[0:n, :], in1=vt[0:n, 256:512], op=ALU.mult),
                         reads=[t1b, vb], writes=[yob])
                    P.dma(POOL, y[t0:t0 + n, 1024 + h * 256:1024 + (h + 1) * 256], yo[0:n, :], reads=[yob], pwrites=[yb])
                ci += 1
                yield
        P.dma(POOL, prm["sC_out"][h], CX[:], reads=[CXb], pwrites=[K.sob])


def mixer_rwkv(C, st, pf, pfb, y, yb, prm, K):
    P = C.P
    ones, onesb, ident, idb, flagE, fb = K.ones, K.onesb, K.ident, K.idb, K.flagE, K.fb
    mask5, m5b = K.mask5, K.m5b
    muA = sbuf(C, st, "amuA", [64, 3, 8]); muL = sbuf(C, st, "amuL", [96, 3]); w2 = sbuf(C, st, "aw2", [32, 512]); a2 = sbuf(C, st, "aa2", [32, 512])
    g2 = sbuf(C, st, "ag2", [96, 512]); ch = sbuf(C, st, "ach", [64, 5, 8]); rk = sbuf(C, st, "ark", [64, 8, 2])
    lnw = sbuf(C, st, "alnw", [64, 512]); lnb = sbuf(C, st, "alnb", [64, 512])
    pb_ = Buf()
    for t, n_ in ((muA, "rw_muA"), (muL, "rw_muL"), (w2, "rw_w2"), (a2, "rw_a2"), (g2, "rw_g2"), (rk, "rw_rk"), (lnw, "rw_lnw_bc"), (lnb, "rw_lnb_bc")):
        P.dma(SP, t[:], prm[n_], pwrites=[pb_])
    P.dma(SP, ch[:, 0:4, :], prm["rw_ch"], pwrites=[pb_])
    P.op(DVE, lambda e: e.tensor_scalar(out=ch[:, 4, :], in0=ch[:, 3, :], scalar1=-1.0, scalar2=1.0, op0=ALU.mult, op1=ALU.add), reads=[pb_], writes=[pb_])
    mh = sbuf(C, st, "amh", [64, TP], F32); mhb = Buf()
    P.op(POOL, lambda e: e.memset(mh[:], -0.5), writes=[mhb])
    tmpr = Ring([sbuf(C, st, f"atmp{i}", [128, 4], F32) for i in range(2)])
    raw = sbuf(C, st, "araw", [96, TP]); rawb = Buf()
    thw = sbuf(C, st, "athw", [32, TP]); xal = sbuf(C, st, "axal", [32, TP]); sg = sbuf(C, st, "asg", [96, TP])
    thwb, xalb, sgb = Buf(), Buf(), Buf()
    for (dst, dstb, r0, nr, mcol, fn) in ((thw, thwb, 12 * 128, 32, 0, AF.Tanh), (xal, xalb, 12 * 128 + 32, 32, 1, None), (sg, sgb, 13 * 128, 96, 2, AF.Sigmoid)):
        P.dma(SP, raw[0:nr, :], pf[r0:r0 + nr, :], reads=[pfb], writes=[rawb])
        P.op(DVE, lambda e, nr=nr: e.memset(raw[0:nr, 0:3], 0.0), reads=[rawb], writes=[rawb])
        fix_gap(C, raw, rawb, prm["hist_in"][r0:r0 + nr, :], flagE, fb, tmpr, nr)
        P.op(DVE, lambda e, dst=dst, nr=nr: e.tensor_tensor(out=dst[0:nr, 3:TP], in0=raw[0:nr, 2:TP - 1], in1=raw[0:nr, 3:TP], op=ALU.subtract),
             reads=[rawb], writes=[dstb])
        P.op(DVE, lambda e, dst=dst, nr=nr, mcol=mcol: e.scalar_tensor_tensor(out=dst[0:nr, 3:TP], in0=dst[0:nr, 3:TP], scalar=muL[0:nr, mcol:mcol + 1],
                                                                            in1=raw[0:nr, 3:TP], op0=ALU.mult, op1=ALU.add), reads=[rawb, dstb, pb_], writes=[dstb])
        if fn is not None:
            P.op(ACT, lambda e, dst=dst, nr=nr, fn=fn: e.activation(out=dst[0:nr, 3:TP], in_=dst[0:nr, 3:TP], func=fn), reads=[dstb], writes=[dstb])
    A = [sbuf(C, st, f"aA{i}", [64, TP]) for i in range(10)]
    Ab = [Buf() for _ in range(10)]
    H = sbuf(C, st, "aH", [64, 64]); Hb = Buf()
    Hin = sbuf(C, st, "aHin", [64, 64]); Hinb = Buf()
    G = 8
    MM = sbuf(C, st, "aMM", [64, G, 5, 64]); MMb = Buf()
    TM = sbuf(C, st, "aTM", [64, G, 3, 64]); TMb = Buf()
    NN = [sbuf(C, st, f"aNN{i}", [64, G, 2, 64]) for i in range(2)]; NNb = [Buf(), Buf()]
    Pm = sbuf(C, st, "aPm", [64, G, 64]); Pmb = Buf()
    GB = sbuf(C, st, "aGB", [64, G, 66]); GBb = Buf()
    ps1 = Ring([psum(C, st, f"ap1{i}", [128, 512], F32) for i in range(3)])
    pygr = Ring([psum(C, st, f"apy{i}", [128, 512], F32) for i in range(2)])
    ps2 = Ring([psum(C, st, f"ap2{i}", [128, 512], F32) for i in range(3)])
    w0r = Ring([sbuf(C, st, f"aw0{i}", [64, 64], F32) for i in range(2)])
    ur = Ring([sbuf(C, st, f"au{i}", [64, 64], F32) for i in range(2)])
    str_ = Ring([sbuf(C, st, f"ast{i}", [64, 8], F32) for i in range(2)])
    junk = sbuf(C, st, "ajunk", [64, 64], F32); jb = Buf()
    T1 = sbuf(C, st, "aT1", [64, G, 64]); T1b = Buf()
    SQ = sbuf(C, st, "aSQ", [64, G, 64]); SQb = Buf()
    YO = sbuf(C, st, "aYO", [64, G, 64], BF16); YOb = Buf()
    ST = sbuf(C, st, "aST", [64, 6, G]); STb = Buf()
    for h in range(8):
        rows = [(sg_ * 4 + h // 2) * 128 + (h % 2) * 64 for sg_ in range(3)]
        for i in range(3):
            P.dma(SP, A[i][:], pf[rows[i]:rows[i] + 64, :], reads=[pfb], writes=[Ab[i]])
            P.op(DVE, lambda e, i=i: e.memset(A[i][:, 0:3], 0.0), reads=[Ab[i]], writes=[Ab[i]])
            fix_gap(C, A[i], Ab[i], prm["hist_in"][rows[i]:rows[i] + 64, :], flagE, fb, tmpr, 64)
            P.op(DVE, lambda e, i=i: e.tensor_tensor(out=A[3 + i][:, 3:TP], in0=A[i][:, 2:TP - 1], in1=A[i][:, 3:TP], op=ALU.subtract),
                 reads=[Ab[i]], writes=[Ab[3 + i]])
            P.op(DVE, lambda e, i=i, h=h: e.scalar_tensor_tensor(out=A[3 + i][:, 3:TP], in0=A[3 + i][:, 3:TP], scalar=muA[:, i, h:h + 1], in1=A[i][:, 3:TP],
                                                                op0=ALU.mult, op1=ALU.add), reads=[Ab[i], Ab[3 + i], pb_], writes=[Ab[3 + i]])
        xr, xk, xv = A[3], A[4], A[5]
        for c0 in range(3, TP, 512):
            n = min(512, TP - c0)
            p_, p_b = ps1.next()
            P.op(PE, lambda e, p_=p_, c0=c0, n=n, h=h: e.matmul(p_[0:64, 0:n], lhsT=w2[0:32, h * 64:(h + 1) * 64], rhs=thw[0:32, c0:c0 + n], start=True, stop=True),
                 reads=[pb_, thwb], writes=[p_b])
            P.op(ACT, lambda e, p_=p_, c0=c0, n=n, h=h: e.activation(out=A[0][:, c0:c0 + n], in_=p_[0:64, 0:n], func=AF.Sigmoid, bias=ch[:, 0, h:h + 1]),
                 reads=[p_b, pb_], writes=[Ab[0]])
            p_, p_b = ps1.next()
            P.op(PE, lambda e, p_=p_, c0=c0, n=n, h=h: e.matmul(p_[0:64, 0:n], lhsT=a2[0:32, h * 64:(h + 1) * 64], rhs=xal[0:32, c0:c0 + n], start=True, stop=True),
                 reads=[pb_, xalb], writes=[p_b])
            P.op(ACT, lambda e, p_=p_, c0=c0, n=n, h=h: e.activation(out=A[1][:, c0:c0 + n], in_=p_[0:64, 0:n], func=AF.Sigmoid, bias=ch[:, 1, h:h + 1]),
                 reads=[p_b, pb_], writes=[Ab[1]])
        P.op(DVE, lambda e, h=h: e.tensor_scalar(out=A[2][:, 3:TP], in0=xk[:, 3:TP], scalar1=ch[:, 2, h:h + 1], scalar2=None, op0=ALU.mult),
             reads=[Ab[4], pb_], writes=[Ab[2]])
        P.op(DVE, lambda e: e.tensor_tensor(out=A[6][:, 3:TP], in0=A[2][:, 3:TP], in1=A[2][:, 3:TP], op=ALU.mult), reads=[Ab[2]], writes=[Ab[6]])
        for c0 in range(3, TP, 512):
            n = min(512, TP - c0)
            p_, p_b = ps1.next()
            P.op(PE, lambda e, p_=p_, c0=c0, n=n: e.matmul(p_[0:64, 0:n], lhsT=ones[0:64, 0:64], rhs=A[6][:, c0:c0 + n], start=True, stop=True),
                 reads=[onesb, Ab[6]], writes=[p_b])
            P.op(DVE, lambda e, p_=p_, c0=c0, n=n: e.tensor_scalar(out=A[8][:, c0:c0 + n], in0=p_[0:64, 0:n], scalar1=1e-24, scalar2=None, op0=ALU.max),
                 reads=[p_b], writes=[Ab[8]])
        P.op(POOL, lambda e: e.tensor_tensor(out=A[8][:, 3:TP], in0=A[8][:, 3:TP], in1=mh[:, 3:TP], op=ALU.pow), reads=[Ab[8], mhb], writes=[Ab[8]])
        P.op(DVE, lambda e: e.tensor_tensor(out=A[2][:, 3:TP], in0=A[2][:, 3:TP], in1=A[8][:, 3:TP], op=ALU.mult), reads=[Ab[2], Ab[8]], writes=[Ab[2]])
        P.op(DVE, lambda e, h=h: e.tensor_scalar(out=A[6][:, 3:TP], in0=A[1][:, 3:TP], scalar1=ch[:, 3, h:h + 1], scalar2=ch[:, 4, h:h + 1], op0=ALU.mult,
                                                op1=ALU.add), reads=[Ab[1], pb_], writes=[Ab[6]])
        P.op(DVE, lambda e: e.tensor_tensor(out=xk[:, 3:TP], in0=xk[:, 3:TP], in1=A[6][:, 3:TP], op=ALU.mult), reads=[Ab[4], Ab[6]], writes=[Ab[4]])
        P.op(DVE, lambda e: e.tensor_tensor(out=A[1][:, 3:TP], in0=A[1][:, 3:TP], in1=A[2][:, 3:TP], op=ALU.mult), reads=[Ab[1], Ab[2]], writes=[Ab[1]])
        P.op(DVE, lambda e: e.tensor_tensor(out=A[6][:, 3:TP], in0=xr[:, 3:TP], in1=xk[:, 3:TP], op=ALU.mult), reads=[Ab[3], Ab[4]], writes=[Ab[6]])
        for (c0, ncol, t0) in SEGS:
            P.op(DVE, lambda e, c0=c0, ncol=ncol: e.tensor_tensor_scan(out=A[7][:, c0:c0 + ncol], data0=ones[0:64, c0:c0 + ncol], data1=A[0][:, c0:c0 + ncol],
                                                                       initial=0.0, op0=ALU.mult, op1=ALU.add), reads=[Ab[0], onesb], writes=[Ab[7]])
        P.op(DVE, lambda e: e.memset(A[8][:, 19:22], 0.0), reads=[Ab[8]], writes=[Ab[8]])
        chunk_rel(C, A[8], Ab[8], A[7], Ab[7], 64)
        P.op(ACT, lambda e: e.activation(out=A[7][:, 3:TP], in_=A[8][:, 3:TP], func=AF.Exp, scale=-LDK), reads=[Ab[8]], writes=[Ab[7]])
        P.op(ACT, lambda e: e.activation(out=A[9][:, 3:TP], in_=A[8][:, 3:TP], func=AF.Exp, scale=LDK), reads=[Ab[8]], writes=[Ab[9]])
        P.op(DVE, lambda e: e.tensor_tensor(out=A[8][:, 3:TP], in0=A[8][:, 3:TP], in1=A[0][:, 3:TP], op=ALU.subtract), reads=[Ab[8], Ab[0]], writes=[Ab[8]])
        P.op(ACT, lambda e: e.activation(out=A[8][:, 3:TP], in_=A[8][:, 3:TP], func=AF.Exp, scale=-LDK), reads=[Ab[8]], writes=[Ab[8]])
        P.op(DVE, lambda e: e.tensor_tensor(out=xr[:, 3:TP], in0=xr[:, 3:TP], in1=A[7][:, 3:TP], op=ALU.mult), reads=[Ab[3], Ab[7]], writes=[Ab[3]])
        P.op(DVE, lambda e: e.tensor_tensor(out=xk[:, 3:TP], in0=xk[:, 3:TP], in1=A[9][:, 3:TP], op=ALU.mult), reads=[Ab[4], Ab[9]], writes=[Ab[4]])
        P.op(DVE, lambda e: e.tensor_tensor(out=A[1][:, 3:TP], in0=A[1][:, 3:TP], in1=A[9][:, 3:TP], op=ALU.mult), reads=[Ab[1], Ab[9]], writes=[Ab[1]])
        P.op(DVE, lambda e: e.scalar_tensor_tensor(out=A[2][:, 3:TP], in0=A[2][:, 3:TP], scalar=-1.0, in1=A[8][:, 3:TP], op0=ALU.mult, op1=ALU.mult),
             reads=[Ab[2], Ab[8]], writes=[Ab[2]])
        rt, kt_, bt, at, prod, G1 = A[3], A[4], A[1], A[2], A[6], A[7]
        rtb, ktb_, btb, atb, prodb, G1b = Ab[3], Ab[4], Ab[1], Ab[2], Ab[6], Ab[7]
        xvb = Ab[5]
        P.op(DVE, lambda e: e.memset(H[:], 0.0), writes=[Hb])
        for si, seg in enumerate(SEGS):
            if si == 1:
                P.dma(SP, Hin[:], prm["sA_in"][h], writes=[Hinb])
                P.op(DVE, lambda e: e.scalar_tensor_tensor(out=H[:], in0=H[:], scalar=flagE[0:64, 0:1], in1=Hin[:], op0=ALU.mult, op1=ALU.add),
                     reads=[Hb, Hinb, fb], writes=[Hb])
            chs_all = chunks_of(seg)
            for g0 in range(0, len(chs_all), G):
                chs = chs_all[g0:g0 + G]
                ng = len(chs)
                n = chs[0][1]
                nlev = 5 if n == 64 else 3
                for g, (c0, n, t0) in enumerate(chs):
                    p_, p_b = ps1.next()
                    for j, (src, srcb) in enumerate(((xv, xvb), (kt_, ktb_), (bt, btb))):
                        P.op(PE, lambda e, p_=p_, src=src, c0=c0, n=n, j=j: e.transpose(out=p_[0:n, j * 64:(j + 1) * 64], in_=src[:, c0:c0 + n], identity=ident[0:64, 0:64]),
                             reads=[srcb, idb], writes=[p_b])
                    P.op(ACT, lambda e, p_=p_, g=g, n=n: e.activation(out=TM[0:n, g, :, :], in_=p_[0:n, 0:192].rearrange("p (j d) -> p j d", j=3), func=AF.Copy),
                         reads=[p_b], pwrites=[TMb])
                    q_, q_b = ps1.next()
                    pairs = ((kt_, ktb_, at, atb), (kt_, ktb_, rt, rtb), (bt, btb, at, atb), (bt, btb, rt, rtb), (at, atb, bt, btb))
                    for j, (l, lb, r, rb) in enumerate(pairs):
                        P.op(PE, lambda e, q_=q_, l=l, r=r, c0=c0, n=n, j=j: e.matmul(q_[0:n, j * 64:j * 64 + n], lhsT=l[:, c0:c0 + n], rhs=r[:, c0:c0 + n], start=True, stop=True),
                             reads=[lb, rb], writes=[q_b])
                    P.op(DVE, lambda e, q_=q_, g=g, n=n: e.tensor_tensor(out=MM[0:n, g, :, 0:n], in0=q_[0:n, 0:320].rearrange("p (j d) -> p j d", j=5)[:, :, 0:n],
                                                                       in1=mask5[0:n, :, 0:n], op=ALU.mult), reads=[q_b, m5b], pwrites=[MMb])
                P.op(DVE, lambda e, ng=ng, n=n: e.tensor_tensor(out=Pm[0:n, 0:ng, 0:n], in0=MM[0:n, 0:ng, 2, 0:n],
                                                                in1=ident[0:n, 0:n].unsqueeze(1).broadcast_to([n, ng, n]), op=ALU.add),
                     reads=[MMb, idb], writes=[Pmb])
                curN = lambda g, n=n: MM[0:n, g, 2, 0:n]
                curNT = lambda g, n=n: MM[0:n, g, 4, 0:n]
                curb = MMb
                for lev in range(nlev):
                    nn, nnb = NN[lev % 2], NNb[lev % 2]
                    for g4 in range(0, ng, 4):
                        m4 = min(4, ng - g4)
                        p2, p2b = ps2.next()
                        for g in range(g4, g4 + m4):
                            gg = g - g4
                            P.op(PE, lambda e, p2=p2, gg=gg, n=n, a_=curNT(g), b_=curN(g): e.matmul(p2[0:n, gg * 128:gg * 128 + n], lhsT=a_, rhs=b_, start=True, stop=True),
                                 reads=[curb], writes=[p2b])
                            P.op(PE, lambda e, p2=p2, gg=gg, n=n, a_=curN(g), b_=curNT(g): e.matmul(p2[0:n, gg * 128 + 64:gg * 128 + 64 + n], lhsT=a_, rhs=b_, start=True, stop=True),
                                 reads=[curb], writes=[p2b])
                        P.op(ACT, lambda e, p2=p2, nn=nn, g4=g4, m4=m4, n=n: e.activation(out=nn[0:n, g4:g4 + m4, :, 0:n],
                                                                                 in_=p2[0:n, 0:m4 * 128].rearrange("p (g j d) -> p g j d", g=m4, j=2)[:, :, :, 0:n], func=AF.Copy),
                             reads=[p2b], pwrites=[nnb])
                    curN = lambda g, nn=nn, n=n: nn[0:n, g, 0, 0:n]
                    curNT = lambda g, nn=nn, n=n: nn[0:n, g, 1, 0:n]
                    curb = nnb
                    p1, p1b = ps1.next()
                    for g in range(ng):
                        P.op(PE, lambda e, p1=p1, g=g, n=n, a_=curNT(g): e.matmul(p1[0:n, g * 64:g * 64 + n], lhsT=a_, rhs=Pm[0:n, g, 0:n], start=True, stop=True),
                             reads=[curb, Pmb], writes=[p1b])
                    P.op(DVE, lambda e, p1=p1, ng=ng, n=n: e.tensor_tensor(out=Pm[0:n, 0:ng, 0:n], in0=Pm[0:n, 0:ng, 0:n],
                                                                         in1=p1[0:n, 0:ng * 64].rearrange("p (g d) -> p g d", g=ng)[:, :, 0:n], op=ALU.add),
                         reads=[p1b, Pmb], writes=[Pmb])
                if C.emit_out:
                    for g, (c0, n, t0) in enumerate(chs):
                        p_, p_b = ps1.next()
                        P.op(PE, lambda e, p_=p_, c0=c0, n=n, h=h: e.matmul(p_[0:n, 0:64], lhsT=sg[0:96, c0:c0 + n], rhs=g2[0:96, h * 64:(h + 1) * 64], start=True, stop=True),
                             reads=[sgb, pb_], writes=[p_b])
                        P.op(PE, lambda e, p_=p_, c0=c0, n=n, h=h: e.matmul(p_[0:n, 64:66], lhsT=prod[:, c0:c0 + n], rhs=rk[:, h, :], start=True, stop=True),
                             reads=[prodb, pb_], writes=[p_b])
                        P.op(ACT, lambda e, p_=p_, g=g, n=n: e.activation(out=GB[0:n, g, :], in_=p_[0:n, 0:66], func=AF.Copy), reads=[p_b], pwrites=[GBb])
                pyg, pygb = pygr.next()
                for g, (c0, n, t0) in enumerate(chs):
                    vtm = TM[0:n, g, 0, :]; ktm = TM[0:n, g, 1, :]; btm = TM[0:n, g, 2, :]
                    LakT = MM[0:n, g, 0, 0:n]; MrkT = MM[0:n, g, 1, 0:n]; MrbT = MM[0:n, g, 3, 0:n]
                    TT_ = Pm[0:n, g, 0:n]
                    pw, pwb = ps1.next()
                    P.op(PE, lambda e, pw=pw, c0=c0, n=n: e.matmul(pw[0:n, 0:64], lhsT=at[:, c0:c0 + n], rhs=H[:, :], start=True, stop=False), reads=[atb, Hb], writes=[pwb])
                    P.op(PE, lambda e, pw=pw, n=n, LakT=LakT, vtm=vtm: e.matmul(pw[0:n, 0:64], lhsT=LakT, rhs=vtm, start=False, stop=True), reads=[MMb, TMb], writes=[pwb])
                    w0, w0b = w0r.next()
                    P.op(ACT, lambda e, w0=w0, pw=pw, n=n: e.activation(out=w0[0:n, :], in_=pw[0:n, 0:64], func=AF.Copy), reads=[pwb], writes=[w0b])
                    P.op(PE, lambda e, pw=pw, n=n, TT_=TT_, w0=w0: e.matmul(pw[0:n, 64:128], lhsT=TT_, rhs=w0[0:n, :], start=True, stop=True), reads=[Pmb, w0b], writes=[pwb])
                    u, ub_ = ur.next()
                    P.op(DVE, lambda e, u=u, pw=pw, n=n: e.tensor_copy(out=u[0:n, :], in_=pw[0:n, 64:128]), reads=[pwb], writes=[ub_])
                    if C.emit_out:
                        P.op(PE, lambda e, pyg=pyg, g=g, c0=c0, n=n: e.matmul(pyg[0:n, g * 64:(g + 1) * 64], lhsT=rt[:, c0:c0 + n], rhs=H[:, :], start=True, stop=False), reads=[rtb, Hb], writes=[pygb])
                        P.op(PE, lambda e, pyg=pyg, g=g, n=n, MrbT=MrbT, u=u: e.matmul(pyg[0:n, g * 64:(g + 1) * 64], lhsT=MrbT, rhs=u[0:n, :], start=False, stop=False), reads=[MMb, ub_], writes=[pygb])
                        P.op(PE, lambda e, pyg=pyg, g=g, n=n, MrkT=MrkT, vtm=vtm: e.matmul(pyg[0:n, g * 64:(g + 1) * 64], lhsT=MrkT, rhs=vtm, start=False, stop=True), reads=[MMb, TMb], writes=[pygb])
                    ph, phb = ps1.next()
                    P.op(PE, lambda e, ph=ph: e.matmul(ph[0:64, 0:64], lhsT=ident[0:64, 0:64], rhs=H[:, :], start=True, stop=False), reads=[idb, Hb], writes=[phb])
                    P.op(PE, lambda e, ph=ph, n=n, btm=btm, u=u: e.matmul(ph[0:64, 0:64], lhsT=btm, rhs=u[0:n, :], start=False, stop=False), reads=[TMb, ub_], writes=[phb])
                    P.op(PE, lambda e, ph=ph, n=n, ktm=ktm, vtm=vtm: e.matmul(ph[0:64, 0:64], lhsT=ktm, rhs=vtm, start=False, stop=True), reads=[TMb], writes=[phb])
                    ce = c0 + n - 1
                    P.op(DVE, lambda e, ph=ph, ce=ce: e.tensor_scalar(out=H[:], in0=ph[0:64, 0:64], scalar1=G1[:, ce:ce + 1], scalar2=None, op0=ALU.mult),
                         reads=[phb, G1b], writes=[Hb])
                if C.emit_out:
                    t0g = chs[0][2]
                    YG = pyg[0:n, 0:ng * 64].rearrange("p (g d) -> p g d", g=ng)
                    bc = lambda ap, n=n, ng=ng: ap.unsqueeze(2).broadcast_to([n, ng, 64])
                    P.op(DVE, lambda e, YG=YG, n=n, ng=ng: e.tensor_reduce(out=ST[0:n, 0, 0:ng], in_=YG, axis=AX.X, op=ALU.add), reads=[pygb], writes=[STb])
                    P.op(ACT, lambda e, YG=YG, n=n, ng=ng: e.activation(out=SQ[0:n, 0:ng, :], in_=YG, func=AF.Square), reads=[pygb], writes=[SQb])
                    P.op(DVE, lambda e, n=n, ng=ng: e.tensor_reduce(out=ST[0:n, 1, 0:ng], in_=SQ[0:n, 0:ng, :], axis=AX.X, op=ALU.add), reads=[SQb, STb], writes=[STb])
                    P.op(DVE, lambda e, n=n, ng=ng: e.tensor_scalar(out=ST[0:n, 2, 0:ng], in0=ST[0:n, 0, 0:ng], scalar1=1.0 / 64, scalar2=None, op0=ALU.mult), reads=[STb], writes=[STb])
                    P.op(DVE, lambda e, n=n, ng=ng: e.tensor_tensor(out=ST[0:n, 3, 0:ng], in0=ST[0:n, 2, 0:ng], in1=ST[0:n, 2, 0:ng], op=ALU.mult), reads=[STb], writes=[STb])
                    P.op(DVE, lambda e, n=n, ng=ng: e.tensor_scalar(out=ST[0:n, 4, 0:ng], in0=ST[0:n, 1, 0:ng], scalar1=1.0 / 64, scalar2=64e-5, op0=ALU.mult, op1=ALU.add),
                         reads=[STb], writes=[STb])
                    P.op(DVE, lambda e, n=n, ng=ng: e.tensor_tensor(out=ST[0:n, 4, 0:ng], in0=ST[0:n, 4, 0:ng], in1=ST[0:n, 3, 0:ng], op=ALU.subtract), reads=[STb], writes=[STb])
                    P.op(POOL, lambda e, n=n, ng=ng: e.tensor_tensor(out=ST[0:n, 5, 0:ng], in0=ST[0:n, 4, 0:ng], in1=mh[0:n, 0:ng], op=ALU.pow), reads=[STb, mhb], writes=[STb])
                    P.op(DVE, lambda e, YG=YG, n=n, ng=ng, bc=bc: e.tensor_tensor(out=T1[0:n, 0:ng, :], in0=YG, in1=bc(ST[0:n, 2, 0:ng]), op=ALU.subtract),
                         reads=[pygb, STb], writes=[T1b])
                    P.op(DVE, lambda e, n=n, ng=ng, bc=bc: e.tensor_tensor(out=T1[0:n, 0:ng, :], in0=T1[0:n, 0:ng, :], in1=bc(ST[0:n, 5, 0:ng]), op=ALU.mult),
                         reads=[T1b, STb], writes=[T1b])
                    P.op(DVE, lambda e, n=n, ng=ng, h=h: e.tensor_tensor(out=T1[0:n, 0:ng, :], in0=T1[0:n, 0:ng, :],
                                                                      in1=lnw[0:n, h * 64:(h + 1) * 64].unsqueeze(1).broadcast_to([n, ng, 64]), op=ALU.mult),
                         reads=[T1b, pb_], writes=[T1b])
                    P.op(DVE, lambda e, n=n, ng=ng, h=h: e.tensor_tensor(out=T1[0:n, 0:ng, :], in0=T1[0:n, 0:ng, :],
                                                                      in1=lnb[0:n, h * 64:(h + 1) * 64].unsqueeze(1).broadcast_to([n, ng, 64]), op=ALU.add),
                         reads=[T1b, pb_], writes=[T1b])
                    P.op(DVE, lambda e, n=n, ng=ng: e.tensor_tensor(out=SQ[0:n, 0:ng, :], in0=TM[0:n, 0:ng, 0, :], in1=GB[0:n, 0:ng, 64:65].broadcast_to([n, ng, 64]), op=ALU.mult),
                         reads=[TMb, GBb, SQb], writes=[SQb])
                    P.op(DVE, lambda e, n=n, ng=ng: e.tensor_tensor(out=T1[0:n, 0:ng, :], in0=T1[0:n, 0:ng, :], in1=SQ[0:n, 0:ng, :], op=ALU.add), reads=[T1b, SQb], writes=[T1b])
                    P.op(DVE, lambda e, n=n, ng=ng: e.tensor_tensor(out=YO[0:n, 0:ng, :], in0=T1[0:n, 0:ng, :], in1=GB[0:n, 0:ng, 0:64], op=ALU.mult),
                         reads=[T1b, GBb], writes=[YOb])
                    P.dma(SP, y[t0g:t0g + ng * n, h * 64:(h + 1) * 64].rearrange("(g p) d -> p g d", p=n), YO[0:n, 0:ng, :], reads=[YOb], pwrites=[yb])
        P.dma(POOL, prm["sA_out"][h], H[:], reads=[Hb], pwrites=[K.sob])


def mixer_rwkv2(C, st, pf, pfb, y, yb, prm, K):
    P = C.P
    ones, onesb, ident, idb, flagE, fb = K.ones, K.onesb, K.ident, K.idb, K.flagE, K.fb
    mask5, m5b = K.mask5, K.m5b
    muA = sbuf(C, st, "bmuA", [64, 3, 8]); muL = sbuf(C, st, "bmuL", [96, 3]); w2 = sbuf(C, st, "bw2", [32, 512]); a2 = sbuf(C, st, "ba2", [32, 512])
    g2 = sbuf(C, st, "bg2", [96, 512]); ch = sbuf(C, st, "bch", [64, 5, 8]); rk = sbuf(C, st, "brk", [64, 8, 2])
    lnw = sbuf(C, st, "blnw", [64, 512]); lnb = sbuf(C, st, "blnb", [64, 512])
    pb_ = Buf()
    for t, n_ in ((muA, "rw_muA"), (muL, "rw_muL"), (w2, "rw_w2"), (a2, "rw_a2"), (g2, "rw_g2"), (rk, "rw_rk"), (lnw, "rw_lnw_bc"), (lnb, "rw_lnb_bc")):
        P.dma(SP, t[:], prm[n_], pwrites=[pb_])
    P.dma(SP, ch[:, 0:4, :], prm["rw_ch"], pwrites=[pb_])
    P.op(DVE, lambda e: e.tensor_scalar(out=ch[:, 4, :], in0=ch[:, 3, :], scalar1=-1.0, scalar2=1.0, op0=ALU.mult, op1=ALU.add), reads=[pb_], writes=[pb_])
    WM = 128
    W1M = WM + 1
    mh = sbuf(C, st, "bmh", [64, 8 * WM], F32); mhb = Buf()
    P.op(POOL, lambda e: e.memset(mh[:], -0.5), writes=[mhb])
    names = ["pr", "pk", "pv", "xr", "xk", "xv", "sgz", "asig", "kkn", "t1", "rel", "G1", "G2"]
    X = {nm: sbuf(C, st, "bX" + nm, [64, 8, W1M]) for nm in names}
    Xb = {nm: Buf() for nm in names}
    Ssc = sbuf(C, st, "bSsc", [64, 1 + 8 * WM]); Sscb = Buf()
    P.op(DVE, lambda e: e.memset(Ssc[:, 0:1], 0.0), writes=[Sscb])
    lraw = sbuf(C, st, "blraw", [96, 3, W1M]); lrawb = Buf()
    thw = sbuf(C, st, "bthw", [32, WM]); xal = sbuf(C, st, "bxal", [32, WM]); sg = sbuf(C, st, "bsg", [96, WM])
    thwb, xalb, sgb = Buf(), Buf(), Buf()
    hs = sbuf(C, st, "bhs", [96, 2, 8]); hsb = Buf()
    H = sbuf(C, st, "bH", [64, 8, 64]); Hb = Buf()
    Hin = sbuf(C, st, "bHin", [64, 8, 64]); Hinb = Buf()
    NQ = 16
    TM = sbuf(C, st, "bTM", [64, 3, NQ, 64]); TMb = Buf()
    MM = sbuf(C, st, "bMM", [64, 5, NQ, 64]); MMb = Buf()
    NN = [sbuf(C, st, f"bNN{i}", [64, NQ, 2, 64]) for i in range(2)]; NNb = [Buf(), Buf()]
    Pm = sbuf(C, st, "bPm", [64, NQ, 64]); Pmb = Buf()
    W0s = sbuf(C, st, "bW0", [64, 8, 64]); W0b = Buf()
    Us = sbuf(C, st, "bUs", [64, 8, 64]); Usb = Buf()
    GBs = sbuf(C, st, "bGB", [64, 528]); GBb = Buf()
    T1 = sbuf(C, st, "bT1", [64, 8, 64]); T1b = Buf()
    SQ = sbuf(C, st, "bSQ", [64, 8, 64]); SQb = Buf()
    YO = sbuf(C, st, "bYO", [64, 8, 64], BF16); YOb = Buf()
    ST = sbuf(C, st, "bST", [64, 6, 8]); STb = Buf()
    bank = [psum(C, st, f"bpb{i}", [128, 512], F32) for i in range(8)]
    bkb = [Buf() for _ in range(8)]
    P.op(DVE, lambda e: e.memset(H[:], 0.0), writes=[Hb])

    def bc8(ap, W):
        return ap.unsqueeze(2).broadcast_to([64, 8, W])

    def vop(fn, reads, writes, pwrites=()):
        P.op(DVE, fn, reads=reads, writes=writes, pwrites=pwrites)

    Fl = sbuf(C, st, "bFl", [64, 8 * WM]); Flb = Buf()

    scs = [(3, 16, 0, 16)] + [(22 + 128 * i, 128, 16 + 128 * i, 64) for i in range(16)]
    def do_sc(sci, c0, W, t0, n):
        W1 = W + 1
        nch = W // n
        nq = 8 * nch
        cur = lambda nm: X[nm][:, :, 1:W1]
        prev = lambda nm: X[nm][:, :, 0:W]
        P.phase = "rwkv_pre"
        for i, nm in enumerate(("pr", "pk", "pv")):
            P.dma(SP, X[nm][:, :, 0:W1], pf[i * 512:(i + 1) * 512, c0 - 1:c0 + W].rearrange("(h d) c -> d h c", d=64), reads=[pfb], writes=[Xb[nm]])
        for j, (r0, nr) in enumerate(((12 * 128, 32), (12 * 128 + 32, 32), (13 * 128, 96))):
            P.dma(SP, lraw[0:nr, j, 0:W1], pf[r0:r0 + nr, c0 - 1:c0 + W], reads=[pfb], writes=[lrawb])
        if sci == 0:
            for nm in ("pr", "pk", "pv"):
                vop(lambda e, nm=nm: e.memset(X[nm][:, :, 0:1], 0.0), [Xb[nm]], [Xb[nm]])
            vop(lambda e: e.memset(lraw[:, :, 0:1], 0.0), [lrawb], [lrawb])
        if sci == 1:
            for i, nm in enumerate(("pr", "pk", "pv")):
                P.dma(SP, hs[0:64, 0, :], pf[i * 512:(i + 1) * 512, 18:19].rearrange("(h d) c -> d (h c)", d=64), reads=[pfb], writes=[hsb], allow_slow_non_contiguous=True)
                P.dma(SP, hs[0:64, 1, :], prm["hist_in"][i * 512:(i + 1) * 512, 2:3].rearrange("(h d) c -> d (h c)", d=64), reads=[hsb], writes=[hsb], allow_slow_non_contiguous=True)
                vop(lambda e, nm=nm: e.scalar_tensor_tensor(out=X[nm][:, :, 0:1], in0=hs[0:64, 0, :].unsqueeze(2), scalar=flagE[0:64, 0:1], in1=hs[0:64, 1, :].unsqueeze(2),
                                                            op0=ALU.mult, op1=ALU.add), [hsb, fb, Xb[nm]], [Xb[nm]])
            for j, (r0, nr) in enumerate(((12 * 128, 32), (12 * 128 + 32, 32), (13 * 128, 96))):
                P.dma(SP, hs[0:nr, 0, 0:1], pf[r0:r0 + nr, 18:19], reads=[pfb, hsb], writes=[hsb], allow_slow_non_contiguous=True)
                P.dma(SP, hs[0:nr, 1, 0:1], prm["hist_in"][r0:r0 + nr, 2:3], reads=[hsb], writes=[hsb], allow_slow_non_contiguous=True)
                vop(lambda e, j=j, nr=nr: e.scalar_tensor_tensor(out=lraw[0:nr, j, 0:1], in0=hs[0:nr, 0, 0:1], scalar=flagE[0:nr, 0:1], in1=hs[0:nr, 1, 0:1],
                                                                 op0=ALU.mult, op1=ALU.add), [hsb, fb, lrawb], [lrawb])
            P.dma(SP, Hin[:], prm["sA_in"].rearrange("h k v -> k h v"), writes=[Hinb])
            vop(lambda e: e.scalar_tensor_tensor(out=H[:], in0=H[:], scalar=flagE[0:64, 0:1], in1=Hin[:], op0=ALU.mult, op1=ALU.add), [Hb, Hinb, fb], [Hb])
        for i, (src, dst) in enumerate((("pr", "xr"), ("pk", "xk"), ("pv", "xv"))):
            vop(lambda e, src=src, dst=dst: e.tensor_tensor(out=cur(dst), in0=prev(src), in1=cur(src), op=ALU.subtract), [Xb[src]], [Xb[dst]])
            vop(lambda e, dst=dst, i=i: e.tensor_tensor(out=cur(dst), in0=cur(dst), in1=bc8(muA[:, i, :], W), op=ALU.mult), [Xb[dst], pb_], [Xb[dst]])
            vop(lambda e, src=src, dst=dst: e.tensor_tensor(out=cur(dst), in0=cur(dst), in1=cur(src), op=ALU.add), [Xb[dst], Xb[src]], [Xb[dst]])
        for j, (dst, dstb, nr, fn) in enumerate(((thw, thwb, 32, AF.Tanh), (xal, xalb, 32, None), (sg, sgb, 96, AF.Sigmoid))):
            vop(lambda e, dst=dst, nr=nr, j=j: e.tensor_tensor(out=dst[0:nr, 0:W], in0=lraw[0:nr, j, 0:W], in1=lraw[0:nr, j, 1:W1], op=ALU.subtract), [lrawb], [dstb])
            vop(lambda e, dst=dst, nr=nr, j=j: e.scalar_tensor_tensor(out=dst[0:nr, 0:W], in0=dst[0:nr, 0:W], scalar=muL[0:nr, j:j + 1], in1=lraw[0:nr, j, 1:W1],
                                                                    op0=ALU.mult, op1=ALU.add), [lrawb, dstb, pb_], [dstb])
            if fn is not None:
                P.op(ACT, lambda e, dst=dst, nr=nr, fn=fn: e.activation(out=dst[0:nr, 0:W], in_=dst[0:nr, 0:W], func=fn), reads=[dstb], writes=[dstb])
        for (wt_, src, srcb, dst, chi, b0) in ((w2, thw, thwb, "sgz", 0, 0), (a2, xal, xalb, "asig", 1, 2)):
            for h in range(8):
                bk = b0 + (h * W) // 512
                off = (h * W) % 512
                P.op(PE, lambda e, bk=bk, off=off, wt_=wt_, src=src, h=h: e.matmul(bank[bk][0:64, off:off + W], lhsT=wt_[0:32, h * 64:(h + 1) * 64], rhs=src[0:32, 0:W],
                                                                                  start=True, stop=True), reads=[pb_, srcb], writes=[bkb[bk]])
            nb = (8 * W + 511) // 512
            for b in range(nb):
                h0 = b * (512 // W) if W >= 64 else 0
                nh = (512 // W) if W >= 64 else 8
                vop(lambda e, b=b, b0=b0, dst=dst, chi=chi, h0=h0, nh=nh: e.tensor_tensor(
                    out=X[dst][:, h0:h0 + nh, 1:W1], in0=bank[b0 + b][0:64, 0:nh * W].rearrange("p (h w) -> p h w", h=nh),
                    in1=ch[:, chi, h0:h0 + nh].unsqueeze(2).broadcast_to([64, nh, W]), op=ALU.add), [bkb[b0 + b], pb_], [Xb[dst]])
            P.op(ACT, lambda e, dst=dst: e.activation(out=cur(dst), in_=cur(dst), func=AF.Sigmoid), reads=[Xb[dst]], writes=[Xb[dst]])
        vop(lambda e: e.tensor_tensor(out=cur("kkn"), in0=cur("xk"), in1=bc8(ch[:, 2, :], W), op=ALU.mult), [Xb["xk"], pb_], [Xb["kkn"]])
        vop(lambda e: e.tensor_tensor(out=Fl[:, 0:8 * W].rearrange("p (h w) -> p h w", h=8), in0=cur("kkn"), in1=cur("kkn"), op=ALU.mult), [Xb["kkn"]], [Flb])
        nb = (8 * W + 511) // 512
        for b in range(nb):
            nn_ = min(512, 8 * W - b * 512)
            P.op(PE, lambda e, b=b, nn_=nn_: e.matmul(bank[4 + b][0:64, 0:nn_], lhsT=ones[0:64, 0:64], rhs=Fl[:, b * 512:b * 512 + nn_], start=True, stop=True),
                 reads=[onesb, Flb], writes=[bkb[4 + b]])
        for b in range(nb):
            nn_ = min(512, 8 * W - b * 512)
            vop(lambda e, b=b, nn_=nn_: e.tensor_scalar(out=Fl[:, b * 512:b * 512 + nn_], in0=bank[4 + b][0:64, 0:nn_],
                                                        scalar1=1e-24, scalar2=None, op0=ALU.max), [bkb[4 + b], Flb], [Flb])
        relf = Fl[:, 0:8 * W]
        P.op(POOL, lambda e, relf=relf: e.tensor_tensor(out=relf, in0=relf, in1=mh[:, 0:8 * W], op=ALU.pow), reads=[Flb, mhb], writes=[Flb])
        vop(lambda e, relf=relf: e.tensor_tensor(out=cur("kkn"), in0=cur("kkn"), in1=relf.rearrange("p (h w) -> p h w", h=8), op=ALU.mult),
            [Xb["kkn"], Flb], [Xb["kkn"]])
        vop(lambda e: e.tensor_tensor(out=cur("t1"), in0=cur("asig"), in1=bc8(ch[:, 3, :], W), op=ALU.mult), [Xb["asig"], pb_], [Xb["t1"]])
        vop(lambda e: e.tensor_tensor(out=cur("t1"), in0=cur("t1"), in1=bc8(ch[:, 4, :], W), op=ALU.add), [Xb["t1"], pb_], [Xb["t1"]])
        vop(lambda e: e.tensor_tensor(out=cur("xk"), in0=cur("xk"), in1=cur("t1"), op=ALU.mult), [Xb["xk"], Xb["t1"]], [Xb["xk"]])
        vop(lambda e: e.tensor_tensor(out=cur("asig"), in0=cur("asig"), in1=cur("kkn"), op=ALU.mult), [Xb["asig"], Xb["kkn"]], [Xb["asig"]])
        vop(lambda e: e.tensor_tensor(out=cur("t1"), in0=cur("xr"), in1=cur("xk"), op=ALU.mult), [Xb["xr"], Xb["xk"], Xb["t1"]], [Xb["t1"]])
        vop(lambda e: e.tensor_tensor(out=cur("pr"), in0=cur("t1"), in1=bc8(rk[:, :, 0], W), op=ALU.mult), [Xb["t1"], pb_, Xb["pr"], Xb["xr"]], [Xb["pr"]])
        vop(lambda e: e.tensor_copy(out=Fl[:, 0:8 * W].rearrange("p (h w) -> p h w", h=8), in_=cur("sgz")), [Xb["sgz"], Flb], [Flb])
        vop(lambda e: e.tensor_tensor_scan(out=Ssc[:, 1:1 + 8 * W], data0=ones[0:64, 0:8 * W], data1=Fl[:, 0:8 * W], initial=0.0, op0=ALU.mult, op1=ALU.add),
            [Flb, Sscb, onesb], [Sscb])
        vop(lambda e: e.tensor_tensor(out=cur("rel").rearrange("p h (c j) -> p h c j", j=n),
                                      in0=Ssc[:, 1:1 + 8 * W].rearrange("p (h c j) -> p h c j", h=8, j=n),
                                      in1=Ssc[:, 0:8 * W].rearrange("p (h c j) -> p h c j", h=8, j=n)[:, :, :, 0:1].broadcast_to([64, 8, nch, n]), op=ALU.subtract),
            [Sscb, Xb["rel"]], [Xb["rel"]])
        P.op(ACT, lambda e: e.activation(out=cur("G1"), in_=cur("rel"), func=AF.Exp, scale=-LDK), reads=[Xb["rel"]], writes=[Xb["G1"]])
        P.op(ACT, lambda e: e.activation(out=cur("G2"), in_=cur("rel"), func=AF.Exp, scale=LDK), reads=[Xb["rel"]], writes=[Xb["G2"]])
        vop(lambda e: e.tensor_tensor(out=cur("rel"), in0=cur("rel"), in1=cur("sgz"), op=ALU.subtract), [Xb["rel"], Xb["sgz"]], [Xb["rel"]])
        P.op(ACT, lambda e: e.activation(out=cur("rel"), in_=cur("rel"), func=AF.Exp, scale=-LDK), reads=[Xb["rel"]], writes=[Xb["rel"]])
        vop(lambda e: e.tensor_tensor(out=cur("xr"), in0=cur("xr"), in1=cur("G1"), op=ALU.mult), [Xb["xr"], Xb["G1"]], [Xb["xr"]])
        vop(lambda e: e.tensor_tensor(out=cur("xk"), in0=cur("xk"), in1=cur("G2"), op=ALU.mult), [Xb["xk"], Xb["G2"]], [Xb["xk"]])
        vop(lambda e: e.tensor_tensor(out=cur("asig"), in0=cur("asig"), in1=cur("G2"), op=ALU.mult), [Xb["asig"], Xb["G2"]], [Xb["asig"]])
        vop(lambda e: e.scalar_tensor_tensor(out=cur("kkn"), in0=cur("kkn"), scalar=-1.0, in1=cur("rel"), op0=ALU.mult, op1=ALU.mult),
            [Xb["kkn"], Xb["rel"]], [Xb["kkn"]])
        RT, KT, BT, AT, XV, PRK, G1 = "xr", "xk", "asig", "kkn", "xv", "pr", "G1"
        col = lambda nm, h, c: X[nm][:, h, 1 + c * n:1 + (c + 1) * n]
        qi = lambda h, c: h * nch + c
        P.phase = "rwkv_gram"
        for a, nm in enumerate((XV, KT, BT)):
            for h in range(8):
                for c in range(nch):
                    q = qi(h, c)
                    bk, off = (q * 64) // 512, (q * 64) % 512
                    P.op(PE, lambda e, bk=bk, off=off, nm=nm, h=h, c=c: e.transpose(out=bank[bk][0:n, off:off + 64], in_=col(nm, h, c), identity=ident[0:64, 0:64]),
                         reads=[Xb[nm], idb], writes=[bkb[bk]])
            for b in range((nq * 64 + 511) // 512):
                qn = min(8, nq - b * 8)
                P.op(ACT, lambda e, a=a, b=b, qn=qn: e.activation(out=TM[0:n, a, b * 8:b * 8 + qn, :], in_=bank[b][0:n, 0:qn * 64].rearrange("p (q d) -> p q d", q=qn),
                                                               func=AF.Copy), reads=[bkb[b]], pwrites=[TMb])
        pairs = ((KT, AT), (KT, RT), (BT, AT), (BT, RT), (AT, BT))
        for j, (l_, r_) in enumerate(pairs):
            b0 = 4 if j % 2 else 0
            for h in range(8):
                for c in range(nch):
                    q = qi(h, c)
                    bk, off = b0 + (q * 64) // 512, (q * 64) % 512
                    P.op(PE, lambda e, bk=bk, off=off, l_=l_, r_=r_, h=h, c=c: e.matmul(bank[bk][0:n, off:off + n], lhsT=col(l_, h, c), rhs=col(r_, h, c), start=True, stop=True),
                         reads=[Xb[l_], Xb[r_]], writes=[bkb[bk]])
            for b in range((nq * 64 + 511) // 512):
                qn = min(8, nq - b * 8)
                vop(lambda e, j=j, b=b, b0=b0, qn=qn: e.tensor_tensor(out=MM[0:n, j, b * 8:b * 8 + qn, 0:n],
                                                                    in0=bank[b0 + b][0:n, 0:qn * 64].rearrange("p (q d) -> p q d", q=qn)[:, :, 0:n],
                                                                    in1=mask5[0:n, j, 0:n].unsqueeze(1).broadcast_to([n, qn, n]), op=ALU.mult),
                    [bkb[b0 + b], m5b], [], pwrites=[MMb])
        P.phase = "rwkv_inv"
        vop(lambda e: e.tensor_tensor(out=Pm[0:n, 0:nq, 0:n], in0=MM[0:n, 2, 0:nq, 0:n], in1=ident[0:n, 0:n].unsqueeze(1).broadcast_to([n, nq, n]), op=ALU.add),
            [MMb, idb], [Pmb])
        curN = lambda q: MM[0:n, 2, q, 0:n]
        curNT = lambda q: MM[0:n, 4, q, 0:n]
        curb = MMb
        nlev = 5 if n == 64 else 3
        for lev in range(nlev):
            nn, nnb = NN[lev % 2], NNb[lev % 2]
            for q in range(nq):
                bk, off = (q * 128) // 512, (q * 128) % 512
                P.op(PE, lambda e, bk=bk, off=off, a_=curNT(q), b_=curN(q): e.matmul(bank[bk][0:n, off:off + n], lhsT=a_, rhs=b_, start=True, stop=True), reads=[curb], writes=[bkb[bk]])
                P.op(PE, lambda e, bk=bk, off=off, a_=curN(q), b_=curNT(q): e.matmul(bank[bk][0:n, off + 64:off + 64 + n], lhsT=a_, rhs=b_, start=True, stop=True), reads=[curb], writes=[bkb[bk]])
            for b in range((nq * 128 + 511) // 512):
                qn = min(4, nq - b * 4)
                P.op(ACT, lambda e, nn=nn, b=b, qn=qn: e.activation(out=nn[0:n, b * 4:b * 4 + qn, :, 0:n],
                                                                 in_=bank[b][0:n, 0:qn * 128].rearrange("p (q j d) -> p q j d", q=qn, j=2)[:, :, :, 0:n], func=AF.Copy),
                     reads=[bkb[b]], pwrites=[nnb])
            curN = lambda q, nn=nn: nn[0:n, q, 0, 0:n]
            curNT = lambda q, nn=nn: nn[0:n, q, 1, 0:n]
            curb = nnb
            for q in range(nq):
                bk, off = 4 + (q * 64) // 512, (q * 64) % 512
                P.op(PE, lambda e, bk=bk, off=off, a_=curNT(q), q=q: e.matmul(bank[bk][0:n, off:off + n], lhsT=a_, rhs=Pm[0:n, q, 0:n], start=True, stop=True), reads=[curb, Pmb], writes=[bkb[bk]])
            for b in range((nq * 64 + 511) // 512):
                qn = min(8, nq - b * 8)
                vop(lambda e, b=b, qn=qn: e.tensor_tensor(out=Pm[0:n, b * 8:b * 8 + qn, 0:n], in0=Pm[0:n, b * 8:b * 8 + qn, 0:n],
                                                          in1=bank[4 + b][0:n, 0:qn * 64].rearrange("p (q d) -> p q d", q=qn)[:, :, 0:n], op=ALU.add),
                    [bkb[4 + b], Pmb], [Pmb])
        P.phase = "rwkv_chain"
        for c in range(nch):
            tc0 = t0 + c * n
            for h in range(8):
                q = qi(h, c)
                P.op(PE, lambda e, h=h, c=c: e.matmul(bank[0][0:n, h * 64:(h + 1) * 64], lhsT=col(AT, h, c), rhs=H[:, h, :], start=True, stop=False), reads=[Xb[AT], Hb], writes=[bkb[0]])
                P.op(PE, lambda e, h=h, q=q: e.matmul(bank[0][0:n, h * 64:(h + 1) * 64], lhsT=MM[0:n, 0, q, 0:n], rhs=TM[0:n, 0, q, :], start=False, stop=True), reads=[MMb, TMb], writes=[bkb[0]])
            P.op(ACT, lambda e: e.activation(out=W0s[0:n, :, :], in_=bank[0][0:n, 0:512].rearrange("p (h d) -> p h d", h=8), func=AF.Copy), reads=[bkb[0]], writes=[W0b])
            for h in range(8):
                q = qi(h, c)
                P.op(PE, lambda e, h=h, q=q: e.matmul(bank[1][0:n, h * 64:(h + 1) * 64], lhsT=Pm[0:n, q, 0:n], rhs=W0s[0:n, h, :], start=True, stop=True), reads=[Pmb, W0b], writes=[bkb[1]])
            vop(lambda e: e.tensor_copy(out=Us[0:n, :, :], in_=bank[1][0:n, 0:512].rearrange("p (h d) -> p h d", h=8)), [bkb[1]], [Usb])
            if C.emit_out:
                for h in range(8):
                    q = qi(h, c)
                    P.op(PE, lambda e, h=h, c=c: e.matmul(bank[2][0:n, h * 64:(h + 1) * 64], lhsT=col(RT, h, c), rhs=H[:, h, :], start=True, stop=False), reads=[Xb[RT], Hb], writes=[bkb[2]])
                    P.op(PE, lambda e, h=h, q=q: e.matmul(bank[2][0:n, h * 64:(h + 1) * 64], lhsT=MM[0:n, 3, q, 0:n], rhs=Us[0:n, h, :], start=False, stop=False), reads=[MMb, Usb], writes=[bkb[2]])
                    P.op(PE, lambda e, h=h, q=q: e.matmul(bank[2][0:n, h * 64:(h + 1) * 64], lhsT=MM[0:n, 1, q, 0:n], rhs=TM[0:n, 0, q, :], start=False, stop=True), reads=[MMb, TMb], writes=[bkb[2]])
            for h in range(8):
                q = qi(h, c)
                P.op(PE, lambda e, h=h: e.matmul(bank[3][0:64, h * 64:(h + 1) * 64], lhsT=ident[0:64, 0:64], rhs=H[:, h, :], start=True, stop=False), reads=[idb, Hb], writes=[bkb[3]])
                P.op(PE, lambda e, h=h, q=q: e.matmul(bank[3][0:64, h * 64:(h + 1) * 64], lhsT=TM[0:n, 2, q, :], rhs=Us[0:n, h, :], start=False, stop=False), reads=[TMb, Usb], writes=[bkb[3]])
                P.op(PE, lambda e, h=h, q=q: e.matmul(bank[3][0:64, h * 64:(h + 1) * 64], lhsT=TM[0:n, 1, q, :], rhs=TM[0:n, 0, q, :], start=False, stop=True), reads=[TMb], writes=[bkb[3]])
            ce = 1 + (c + 1) * n - 1
            vop(lambda e, ce=ce: e.tensor_tensor(out=H[:], in0=bank[3][0:64, 0:512].rearrange("p (h d) -> p h d", h=8),
                                                 in1=X[G1][:, :, ce:ce + 1].broadcast_to([64, 8, 64]), op=ALU.mult), [bkb[3], Xb[G1]], [Hb])
            if C.emit_out:
                P.op(PE, lambda e, c=c: e.matmul(bank[4][0:n, 0:512], lhsT=sg[0:96, c * n:(c + 1) * n], rhs=g2[0:96, :], start=True, stop=True), reads=[sgb, pb_], writes=[bkb[4]])
                for h in range(8):
                    P.op(PE, lambda e, h=h, c=c: e.matmul(bank[5][0:n, 2 * h:2 * h + 2], lhsT=col(PRK, h, c), rhs=ones[0:64, 0:2], start=True, stop=True), reads=[Xb[PRK], onesb], writes=[bkb[5]])
                P.op(ACT, lambda e: e.activation(out=GBs[0:n, 0:512], in_=bank[4][0:n, 0:512], func=AF.Copy), reads=[bkb[4]], writes=[GBb])
                P.op(ACT, lambda e: e.activation(out=GBs[0:n, 512:528], in_=bank[5][0:n, 0:16], func=AF.Copy), reads=[bkb[5], GBb], writes=[GBb])
                YG = bank[2][0:n, 0:512].rearrange("p (h d) -> p h d", h=8)
                bcn = lambda ap: ap.unsqueeze(2).broadcast_to([n, 8, 64])
                vop(lambda e, YG=YG: e.tensor_reduce(out=ST[0:n, 0, :], in_=YG, axis=AX.X, op=ALU.add), [bkb[2]], [STb])
                P.op(ACT, lambda e, YG=YG: e.activation(out=SQ[0:n, :, :], in_=YG, func=AF.Square), reads=[bkb[2]], writes=[SQb])
                vop(lambda e: e.tensor_reduce(out=ST[0:n, 1, :], in_=SQ[0:n, :, :], axis=AX.X, op=ALU.add), [SQb, STb], [STb])
                vop(lambda e: e.tensor_scalar(out=ST[0:n, 2, :], in0=ST[0:n, 0, :], scalar1=1.0 / 64, scalar2=None, op0=ALU.mult), [STb], [STb])
                vop(lambda e: e.tensor_tensor(out=ST[0:n, 3, :], in0=ST[0:n, 2, :], in1=ST[0:n, 2, :], op=ALU.mult), [STb], [STb])
                vop(lambda e: e.tensor_scalar(out=ST[0:n, 4, :], in0=ST[0:n, 1, :], scalar1=1.0 / 64, scalar2=64e-5, op0=ALU.mult, op1=ALU.add), [STb], [STb])
                vop(lambda e: e.tensor_tensor(out=ST[0:n, 4, :], in0=ST[0:n, 4, :], in1=ST[0:n, 3, :], op=ALU.subtract), [STb], [STb])
                P.op(POOL, lambda e: e.tensor_tensor(out=ST[0:n, 5, :], in0=ST[0:n, 4, :], in1=mh[0:n, 0:8], op=ALU.pow), reads=[STb, mhb], writes=[STb])
                vop(lambda e, YG=YG, bcn=bcn: e.tensor_tensor(out=T1[0:n, :, :], in0=YG, in1=bcn(ST[0:n, 2, :]), op=ALU.subtract), [bkb[2], STb], [T1b])
                vop(lambda e, bcn=bcn: e.tensor_tensor(out=T1[0:n, :, :], in0=T1[0:n, :, :], in1=bcn(ST[0:n, 5, :]), op=ALU.mult), [T1b, STb], [T1b])
                vop(lambda e: e.tensor_tensor(out=T1[0:n, :, :], in0=T1[0:n, :, :], in1=lnw[0:n, :].rearrange("p (h d) -> p h d", h=8), op=ALU.mult), [T1b, pb_], [T1b])
                vop(lambda e: e.tensor_tensor(out=T1[0:n, :, :], in0=T1[0:n, :, :], in1=lnb[0:n, :].rearrange("p (h d) -> p h d", h=8), op=ALU.add), [T1b, pb_], [T1b])
                vtm_c = TM[0:n, 0, 0:nq, :].rearrange("p (h c) d -> p h c d", c=nch)[:, :, c, :]
                bs_c = GBs[0:n, 512:528].rearrange("p (h t) -> p h t", t=2)[:, :, 0:1].broadcast_to([n, 8, 64])
                vop(lambda e, vtm_c=vtm_c, bs_c=bs_c: e.tensor_tensor(out=SQ[0:n, :, :], in0=vtm_c, in1=bs_c, op=ALU.mult), [TMb, GBb, SQb], [SQb])
                vop(lambda e: e.tensor_tensor(out=T1[0:n, :, :], in0=T1[0:n, :, :], in1=SQ[0:n, :, :], op=ALU.add), [T1b, SQb], [T1b])
                vop(lambda e: e.tensor_tensor(out=YO[0:n, :, :], in0=T1[0:n, :, :], in1=GBs[0:n, 0:512].rearrange("p (h d) -> p h d", h=8), op=ALU.mult), [T1b, GBb], [YOb])
                P.dma(SP, y[tc0:tc0 + n, 0:512], YO[0:n, :, :].rearrange("p h d -> p (h d)"), reads=[YOb], pwrites=[yb])
    for sci, (c0, W, t0, n) in enumerate(scs):
        do_sc(sci, c0, W, t0, n)
    P.dma(POOL, prm["sA_out"].rearrange("h k v -> k h v"), H[:], reads=[Hb], pwrites=[K.sob])


def mixer_rwkv3(C, st, pf, pfb, y, yb, prm, K):
    P = C.P
    ones, onesb, ident, idb, flagE, fb = K.ones, K.onesb, K.ident, K.idb, K.flagE, K.fb
    mask5, m5b = K.mask5, K.m5b
    muA = sbuf(C, st, "cmuA", [64, 3, 8]); muL = sbuf(C, st, "cmuL", [96, 3]); w2 = sbuf(C, st, "cw2", [32, 512]); a2 = sbuf(C, st, "ca2", [32, 512])
    g2 = sbuf(C, st, "cg2", [96, 512]); ch = sbuf(C, st, "cch", [64, 5, 8]); rk = sbuf(C, st, "crk", [64, 8, 2])
    lnw = sbuf(C, st, "clnw", [64, 512]); lnb = sbuf(C, st, "clnb", [64, 512])
    pb_ = Buf()
    for t, n_ in ((muA, "rw_muA"), (muL, "rw_muL"), (w2, "rw_w2"), (a2, "rw_a2"), (g2, "rw_g2"), (rk, "rw_rk"), (lnw, "rw_lnw_bc"), (lnb, "rw_lnb_bc")):
        P.dma(SP, t[:], prm[n_], pwrites=[pb_])
    P.dma(SP, ch[:, 0:4, :], prm["rw_ch"], pwrites=[pb_])
    P.op(DVE, lambda e: e.tensor_scalar(out=ch[:, 4, :], in0=ch[:, 3, :], scalar1=-1.0, scalar2=1.0, op0=ALU.mult, op1=ALU.add), reads=[pb_], writes=[pb_])
    WM = 128
    W1M = WM + 1
    mh = sbuf(C, st, "cmh", [64, 8 * WM], F32); mhb = Buf()
    P.op(POOL, lambda e: e.memset(mh[:], -0.5), writes=[mhb])
    names = ["pr", "pk", "pv", "xr", "xk", "xv", "sgz", "asig", "kkn", "t1", "rel", "G1", "G2"]
    X = {nm: sbuf(C, st, "cX" + nm, [64, 8, W1M]) for nm in names}
    Xb = {nm: Buf() for nm in names}
    Ssc = sbuf(C, st, "cSsc", [64, 1 + 8 * WM]); Sscb = Buf()
    P.op(DVE, lambda e: e.memset(Ssc[:, 0:1], 0.0), writes=[Sscb])
    lraw = sbuf(C, st, "clraw", [96, 3, W1M]); lrawb = Buf()
    thw = sbuf(C, st, "cthw", [32, WM]); xal = sbuf(C, st, "cxal", [32, WM]); sg = sbuf(C, st, "csg", [96, WM])
    thwb, xalb, sgb = Buf(), Buf(), Buf()
    hs = sbuf(C, st, "chs", [96, 2, 8]); hsb = Buf()
    H = sbuf(C, st, "cH", [64, 8, 64]); Hb = Buf()
    Hin = sbuf(C, st, "cHin", [64, 8, 64]); Hinb = Buf()
    NQ = 16
    TM = sbuf(C, st, "cTM", [64, 3, NQ, 64], BF16); TMb = Buf()
    MM = sbuf(C, st, "cMM", [64, 5, NQ, 64], BF16); MMb = Buf()
    NN = [sbuf(C, st, f"cNN{i}", [64, NQ, 2, 64], BF16) for i in range(2)]; NNb = [Buf(), Buf()]
    Pm = sbuf(C, st, "cPm", [64, NQ, 64], BF16); Pmb = Buf()
    W0s = sbuf(C, st, "cW0", [64, 8, 64], BF16); W0b = Buf()
    Us = sbuf(C, st, "cUs", [64, 8, 64], BF16); Usb = Buf()
    GBs = sbuf(C, st, "cGB", [64, 528]); GBb = Buf()
    T1 = sbuf(C, st, "cT1", [64, 8, 64]); T1b = Buf()
    SQ = sbuf(C, st, "cSQ", [64, 8, 64]); SQb = Buf()
    YO = sbuf(C, st, "cYO", [64, 8, 64], BF16); YOb = Buf()
    ST = sbuf(C, st, "cST", [64, 6, 8]); STb = Buf()
    bank = [psum(C, st, f"cpb{i}", [128, 512], F32) for i in range(8)]
    bkb = [Buf() for _ in range(8)]
    P.op(DVE, lambda e: e.memset(H[:], 0.0), writes=[Hb])
    H16 = sbuf(C, st, "cH16", [64, 8, 64], BF16); H16b = Buf()
    P.op(DVE, lambda e: e.memset(H16[:], 0.0), writes=[H16b])
    XBF = {nm: sbuf(C, st, "cXB" + nm, [64, 8, W1M], BF16) for nm in ("xr", "xk", "asig", "kkn")}
    XBFb = {nm: Buf() for nm in XBF}

    def bc8(ap, W):
        return ap.unsqueeze(2).broadcast_to([64, 8, W])

    def vop(fn, reads, writes, pwrites=()):
        P.op(DVE, fn, reads=reads, writes=writes, pwrites=pwrites)

    Fl = sbuf(C, st, "cFl", [64, 8 * WM]); Flb = Buf()

    scs = [(3, 16, 0, 16)] + [(22 + 128 * i, 128, 16 + 128 * i, 64) for i in range(16)]
    def do_sc(sci, c0, W, t0, n):
        W1 = W + 1
        nch = W // n
        nq = 8 * nch
        cur = lambda nm: X[nm][:, :, 1:W1]
        prev = lambda nm: X[nm][:, :, 0:W]
        P.phase = "rwkv_pre"
        for i, nm in enumerate(("pr", "pk", "pv")):
            P.dma(SP, X[nm][:, :, 0:W1], pf[i * 512:(i + 1) * 512, c0 - 1:c0 + W].rearrange("(h d) c -> d h c", d=64), reads=[pfb], writes=[Xb[nm]])
        for j, (r0, nr) in enumerate(((12 * 128, 32), (12 * 128 + 32, 32), (13 * 128, 96))):
            P.dma(SP, lraw[0:nr, j, 0:W1], pf[r0:r0 + nr, c0 - 1:c0 + W], reads=[pfb], writes=[lrawb])
        if sci == 0:
            for nm in ("pr", "pk", "pv"):
                vop(lambda e, nm=nm: e.memset(X[nm][:, :, 0:1], 0.0), [Xb[nm]], [Xb[nm]])
            vop(lambda e: e.memset(lraw[:, :, 0:1], 0.0), [lrawb], [lrawb])
        if sci == 1:
            for i, nm in enumerate(("pr", "pk", "pv")):
                P.dma(SP, hs[0:64, 0, :], pf[i * 512:(i + 1) * 512, 18:19].rearrange("(h d) c -> d (h c)", d=64), reads=[pfb], writes=[hsb], allow_slow_non_contiguous=True)
                P.dma(SP, hs[0:64, 1, :], prm["hist_in"][i * 512:(i + 1) * 512, 2:3].rearrange("(h d) c -> d (h c)", d=64), reads=[hsb], writes=[hsb], allow_slow_non_contiguous=True)
                vop(lambda e, nm=nm: e.scalar_tensor_tensor(out=X[nm][:, :, 0:1], in0=hs[0:64, 0, :].unsqueeze(2), scalar=flagE[0:64, 0:1], in1=hs[0:64, 1, :].unsqueeze(2),
                                                            op0=ALU.mult, op1=ALU.add), [hsb, fb, Xb[nm]], [Xb[nm]])
            for j, (r0, nr) in enumerate(((12 * 128, 32), (12 * 128 + 32, 32), (13 * 128, 96))):
                P.dma(SP, hs[0:nr, 0, 0:1], pf[r0:r0 + nr, 18:19], reads=[pfb, hsb], writes=[hsb], allow_slow_non_contiguous=True)
                P.dma(SP, hs[0:nr, 1, 0:1], prm["hist_in"][r0:r0 + nr, 2:3], reads=[hsb], writes=[hsb], allow_slow_non_contiguous=True)
                vop(lambda e, j=j, nr=nr: e.scalar_tensor_tensor(out=lraw[0:nr, j, 0:1], in0=hs[0:nr, 0, 0:1], scalar=flagE[0:nr, 0:1], in1=hs[0:nr, 1, 0:1],
                                                                 op0=ALU.mult, op1=ALU.add), [hsb, fb, lrawb], [lrawb])
            P.dma(SP, Hin[:], prm["sA_in"].rearrange("h k v -> k h v"), writes=[Hinb])
            vop(lambda e: e.scalar_tensor_tensor(out=H[:], in0=H[:], scalar=flagE[0:64, 0:1], in1=Hin[:], op0=ALU.mult, op1=ALU.add), [Hb, Hinb, fb], [Hb])
            P.op(ACT, lambda e: e.activation(out=H16[:], in_=H[:], func=AF.Copy), reads=[Hb], writes=[H16b])
        for i, (src, dst) in enumerate((("pr", "xr"), ("pk", "xk"), ("pv", "xv"))):
            vop(lambda e, src=src, dst=dst: e.tensor_tensor(out=cur(dst), in0=prev(src), in1=cur(src), op=ALU.subtract), [Xb[src]], [Xb[dst]])
            vop(lambda e, dst=dst, i=i: e.tensor_tensor(out=cur(dst), in0=cur(dst), in1=bc8(muA[:, i, :], W), op=ALU.mult), [Xb[dst], pb_], [Xb[dst]])
            vop(lambda e, src=src, dst=dst: e.tensor_tensor(out=cur(dst), in0=cur(dst), in1=cur(src), op=ALU.add), [Xb[dst], Xb[src]], [Xb[dst]])
        for j, (dst, dstb, nr, fn) in enumerate(((thw, thwb, 32, AF.Tanh), (xal, xalb, 32, None), (sg, sgb, 96, AF.Sigmoid))):
            vop(lambda e, dst=dst, nr=nr, j=j: e.tensor_tensor(out=dst[0:nr, 0:W], in0=lraw[0:nr, j, 0:W], in1=lraw[0:nr, j, 1:W1], op=ALU.subtract), [lrawb], [dstb])
            vop(lambda e, dst=dst, nr=nr, j=j: e.scalar_tensor_tensor(out=dst[0:nr, 0:W], in0=dst[0:nr, 0:W], scalar=muL[0:nr, j:j + 1], in1=lraw[0:nr, j, 1:W1],
                                                                    op0=ALU.mult, op1=ALU.add), [lrawb, dstb, pb_], [dstb])
            if fn is not None:
                P.op(ACT, lambda e, dst=dst, nr=nr, fn=fn: e.activation(out=dst[0:nr, 0:W], in_=dst[0:nr, 0:W], func=fn), reads=[dstb], writes=[dstb])
        for (wt_, src, srcb, dst, chi, b0) in ((w2, thw, thwb, "sgz", 0, 0), (a2, xal, xalb, "asig", 1, 2)):
            for h in range(8):
                bk = b0 + (h * W) // 512
                off = (h * W) % 512
                P.op(PE, lambda e, bk=bk, off=off, wt_=wt_, src=src, h=h: e.matmul(bank[bk][0:64, off:off + W], lhsT=wt_[0:32, h * 64:(h + 1) * 64], rhs=src[0:32, 0:W],
                                                                                  start=True, stop=True), reads=[pb_, srcb], writes=[bkb[bk]])
            nb = (8 * W + 511) // 512
            for b in range(nb):
                h0 = b * (512 // W) if W >= 64 else 0
                nh = (512 // W) if W >= 64 else 8
                vop(lambda e, b=b, b0=b0, dst=dst, chi=chi, h0=h0, nh=nh: e.tensor_tensor(
                    out=X[dst][:, h0:h0 + nh, 1:W1], in0=bank[b0 + b][0:64, 0:nh * W].rearrange("p (h w) -> p h w", h=nh),
                    in1=ch[:, chi, h0:h0 + nh].unsqueeze(2).broadcast_to([64, nh, W]), op=ALU.add), [bkb[b0 + b], pb_], [Xb[dst]])
            P.op(ACT, lambda e, dst=dst: e.activation(out=cur(dst), in_=cur(dst), func=AF.Sigmoid), reads=[Xb[dst]], writes=[Xb[dst]])
        vop(lambda e: e.tensor_tensor(out=cur("kkn"), in0=cur("xk"), in1=bc8(ch[:, 2, :], W), op=ALU.mult), [Xb["xk"], pb_], [Xb["kkn"]])
        vop(lambda e: e.tensor_tensor(out=Fl[:, 0:8 * W].rearrange("p (h w) -> p h w", h=8), in0=cur("kkn"), in1=cur("kkn"), op=ALU.mult), [Xb["kkn"]], [Flb])
        nb = (8 * W + 511) // 512
        for b in range(nb):
            nn_ = min(512, 8 * W - b * 512)
            P.op(PE, lambda e, b=b, nn_=nn_: e.matmul(bank[4 + b][0:64, 0:nn_], lhsT=ones[0:64, 0:64], rhs=Fl[:, b * 512:b * 512 + nn_], start=True, stop=True),
                 reads=[onesb, Flb], writes=[bkb[4 + b]])
        for b in range(nb):
            nn_ = min(512, 8 * W - b * 512)
            vop(lambda e, b=b, nn_=nn_: e.tensor_scalar(out=Fl[:, b * 512:b * 512 + nn_], in0=bank[4 + b][0:64, 0:nn_],
                                                        scalar1=1e-24, scalar2=None, op0=ALU.max), [bkb[4 + b], Flb], [Flb])
        relf = Fl[:, 0:8 * W]
        P.op(ACT, lambda e, relf=relf: e.activation(out=relf, in_=relf, func=AF.Sqrt), reads=[Flb], writes=[Flb])
        vop(lambda e, relf=relf: e.reciprocal(out=relf, in_=relf), [Flb], [Flb])
        vop(lambda e, relf=relf: e.tensor_tensor(out=cur("kkn"), in0=cur("kkn"), in1=relf.rearrange("p (h w) -> p h w", h=8), op=ALU.mult),
            [Xb["kkn"], Flb], [Xb["kkn"]])
        vop(lambda e: e.tensor_tensor(out=cur("t1"), in0=cur("asig"), in1=bc8(ch[:, 3, :], W), op=ALU.mult), [Xb["asig"], pb_], [Xb["t1"]])
        vop(lambda e: e.tensor_tensor(out=cur("t1"), in0=cur("t1"), in1=bc8(ch[:, 4, :], W), op=ALU.add), [Xb["t1"], pb_], [Xb["t1"]])
        vop(lambda e: e.tensor_tensor(out=cur("xk"), in0=cur("xk"), in1=cur("t1"), op=ALU.mult), [Xb["xk"], Xb["t1"]], [Xb["xk"]])
        vop(lambda e: e.tensor_tensor(out=cur("asig"), in0=cur("asig"), in1=cur("kkn"), op=ALU.mult), [Xb["asig"], Xb["kkn"]], [Xb["asig"]])
        vop(lambda e: e.tensor_tensor(out=cur("t1"), in0=cur("xr"), in1=cur("xk"), op=ALU.mult), [Xb["xr"], Xb["xk"], Xb["t1"]], [Xb["t1"]])
        vop(lambda e: e.tensor_tensor(out=cur("pr"), in0=cur("t1"), in1=bc8(rk[:, :, 0], W), op=ALU.mult), [Xb["t1"], pb_, Xb["pr"], Xb["xr"]], [Xb["pr"]])
        vop(lambda e: e.tensor_copy(out=Fl[:, 0:8 * W].rearrange("p (h w) -> p h w", h=8), in_=cur("sgz")), [Xb["sgz"], Flb], [Flb])
        vop(lambda e: e.tensor_tensor_scan(out=Ssc[:, 1:1 + 8 * W], data0=ones[0:64, 0:8 * W], data1=Fl[:, 0:8 * W], initial=0.0, op0=ALU.mult, op1=ALU.add),
            [Flb, Sscb, onesb], [Sscb])
        vop(lambda e: e.tensor_tensor(out=cur("rel").rearrange("p h (c j) -> p h c j", j=n),
                                      in0=Ssc[:, 1:1 + 8 * W].rearrange("p (h c j) -> p h c j", h=8, j=n),
                                      in1=Ssc[:, 0:8 * W].rearrange("p (h c j) -> p h c j", h=8, j=n)[:, :, :, 0:1].broadcast_to([64, 8, nch, n]), op=ALU.subtract),
            [Sscb, Xb["rel"]], [Xb["rel"]])
        P.op(ACT, lambda e: e.activation(out=cur("G1"), in_=cur("rel"), func=AF.Exp, scale=-LDK), reads=[Xb["rel"]], writes=[Xb["G1"]])
        P.op(ACT, lambda e: e.activation(out=cur("G2"), in_=cur("rel"), func=AF.Exp, scale=LDK), reads=[Xb["rel"]], writes=[Xb["G2"]])
        vop(lambda e: e.tensor_tensor(out=cur("rel"), in0=cur("rel"), in1=cur("sgz"), op=ALU.subtract), [Xb["rel"], Xb["sgz"]], [Xb["rel"]])
        P.op(ACT, lambda e: e.activation(out=cur("rel"), in_=cur("rel"), func=AF.Exp, scale=-LDK), reads=[Xb["rel"]], writes=[Xb["rel"]])
        vop(lambda e: e.tensor_tensor(out=cur("xr"), in0=cur("xr"), in1=cur("G1"), op=ALU.mult), [Xb["xr"], Xb["G1"]], [Xb["xr"]])
        vop(lambda e: e.tensor_tensor(out=cur("xk"), in0=cur("xk"), in1=cur("G2"), op=ALU.mult), [Xb["xk"], Xb["G2"]], [Xb["xk"]])
        vop(lambda e: e.tensor_tensor(out=cur("asig"), in0=cur("asig"), in1=cur("G2"), op=ALU.mult), [Xb["asig"], Xb["G2"]], [Xb["asig"]])
        vop(lambda e: e.scalar_tensor_tensor(out=cur("kkn"), in0=cur("kkn"), scalar=-1.0, in1=cur("rel"), op0=ALU.mult, op1=ALU.mult),
            [Xb["kkn"], Xb["rel"]], [Xb["kkn"]])
        for nm in ("xr", "xk", "asig", "kkn"):
            P.op(ACT, lambda e, nm=nm: e.activation(out=XBF[nm][:, :, 1:W1], in_=cur(nm), func=AF.Copy), reads=[Xb[nm]], writes=[XBFb[nm]])
        RT, KT, BT, AT, XV, PRK, G1 = "xr", "xk", "asig", "kkn", "xv", "pr", "G1"
        colb = lambda nm, h, c: XBF[nm][:, h, 1 + c * n:1 + (c + 1) * n]
        col = lambda nm, h, c: X[nm][:, h, 1 + c * n:1 + (c + 1) * n]
        qi = lambda h, c: h * nch + c
        P.phase = "rwkv_gram"
        for a, nm in enumerate((XV, KT, BT)):
            for h in range(8):
                for c in range(nch):
                    q = qi(h, c)
                    bk, off = (q * 64) // 512, (q * 64) % 512
                    P.op(PE, lambda e, bk=bk, off=off, nm=nm, h=h, c=c: e.transpose(out=bank[bk][0:n, off:off + 64], in_=col(nm, h, c), identity=ident[0:64, 0:64]),
                         reads=[Xb[nm], idb], writes=[bkb[bk]])
            for b in range((nq * 64 + 511) // 512):
                qn = min(8, nq - b * 8)
                P.op(ACT, lambda e, a=a, b=b, qn=qn: e.activation(out=TM[0:n, a, b * 8:b * 8 + qn, :], in_=bank[b][0:n, 0:qn * 64].rearrange("p (q d) -> p q d", q=qn),
                                                               func=AF.Copy), reads=[bkb[b]], pwrites=[TMb])
        pairs = ((KT, AT), (KT, RT), (BT, AT), (BT, RT), (AT, BT))
        for j, (l_, r_) in enumerate(pairs):
            b0 = 4 if j % 2 else 0
            for h in range(8):
                for c in range(nch):
                    q = qi(h, c)
                    bk, off = b0 + (q * 64) // 512, (q * 64) % 512
                    P.op(PE, lambda e, bk=bk, off=off, l_=l_, r_=r_, h=h, c=c: e.matmul(bank[bk][0:n, off:off + n], lhsT=colb(l_, h, c), rhs=colb(r_, h, c), start=True, stop=True),
                         reads=[XBFb[l_], XBFb[r_]], writes=[bkb[bk]])
            for b in range((nq * 64 + 511) // 512):
                qn = min(8, nq - b * 8)
                vop(lambda e, j=j, b=b, b0=b0, qn=qn: e.tensor_tensor(out=MM[0:n, j, b * 8:b * 8 + qn, 0:n],
                                                                    in0=bank[b0 + b][0:n, 0:qn * 64].rearrange("p (q d) -> p q d", q=qn)[:, :, 0:n],
                                                                    in1=mask5[0:n, j, 0:n].unsqueeze(1).broadcast_to([n, qn, n]), op=ALU.mult),
                    [bkb[b0 + b], m5b], [], pwrites=[MMb])
        P.phase = "rwkv_inv"
        vop(lambda e: e.tensor_tensor(out=Pm[0:n, 0:nq, 0:n], in0=MM[0:n, 2, 0:nq, 0:n], in1=ident[0:n, 0:n].unsqueeze(1).broadcast_to([n, nq, n]), op=ALU.add),
            [MMb, idb], [Pmb])
        curN = lambda q: MM[0:n, 2, q, 0:n]
        curNT = lambda q: MM[0:n, 4, q, 0:n]
        curb = MMb
        nlev = 5 if n == 64 else 3
        for lev in range(nlev):
            nn, nnb = NN[lev % 2], NNb[lev % 2]
            for q in range(nq):
                bk, off = (q * 128) // 512, (q * 128) % 512
                P.op(PE, lambda e, bk=bk, off=off, a_=curNT(q), b_=curN(q): e.matmul(bank[bk][0:n, off:off + n], lhsT=a_, rhs=b_, start=True, stop=True), reads=[curb], writes=[bkb[bk]])
                P.op(PE, lambda e, bk=bk, off=off, a_=curN(q), b_=curNT(q): e.matmul(bank[bk][0:n, off + 64:off + 64 + n], lhsT=a_, rhs=b_, start=True, stop=True), reads=[curb], writes=[bkb[bk]])
            for b in range((nq * 128 + 511) // 512):
                qn = min(4, nq - b * 4)
                P.op(ACT, lambda e, nn=nn, b=b, qn=qn: e.activation(out=nn[0:n, b * 4:b * 4 + qn, :, 0:n],
                                                                 in_=bank[b][0:n, 0:qn * 128].rearrange("p (q j d) -> p q j d", q=qn, j=2)[:, :, :, 0:n], func=AF.Copy),
                     reads=[bkb[b]], pwrites=[nnb])
            curN = lambda q, nn=nn: nn[0:n, q, 0, 0:n]
            curNT = lambda q, nn=nn: nn[0:n, q, 1, 0:n]
            curb = nnb
            for q in range(nq):
                bk, off = 4 + (q * 64) // 512, (q * 64) % 512
                P.op(PE, lambda e, bk=bk, off=off, a_=curNT(q), q=q: e.matmul(bank[bk][0:n, off:off + n], lhsT=a_, rhs=Pm[0:n, q, 0:n], start=True, stop=True), reads=[curb, Pmb], writes=[bkb[bk]])
            for b in range((nq * 64 + 511) // 512):
                qn = min(8, nq - b * 8)
                vop(lambda e, b=b, qn=qn: e.tensor_tensor(out=Pm[0:n, b * 8:b * 8 + qn, 0:n], in0=Pm[0:n, b * 8:b * 8 + qn, 0:n],
                                                          in1=bank[4 + b][0:n, 0:qn * 64].rearrange("p (q d) -> p q d", q=qn)[:, :, 0:n], op=ALU.add),
                    [bkb[4 + b], Pmb], [Pmb])
        P.phase = "rwkv_chain"
        for c in range(nch):
            tc0 = t0 + c * n
            for h in range(8):
                q = qi(h, c)
                P.op(PE, lambda e, h=h, c=c: e.matmul(bank[0][0:n, h * 64:(h + 1) * 64], lhsT=colb(AT, h, c), rhs=H16[:, h, :], start=True, stop=False), reads=[XBFb[AT], H16b], writes=[bkb[0]])
                P.op(PE, lambda e, h=h, q=q: e.matmul(bank[0][0:n, h * 64:(h + 1) * 64], lhsT=MM[0:n, 0, q, 0:n], rhs=TM[0:n, 0, q, :], start=False, stop=True), reads=[MMb, TMb], writes=[bkb[0]])
            P.op(ACT, lambda e: e.activation(out=W0s[0:n, :, :], in_=bank[0][0:n, 0:512].rearrange("p (h d) -> p h d", h=8), func=AF.Copy), reads=[bkb[0]], writes=[W0b])
            for h in range(8):
                q = qi(h, c)
                P.op(PE, lambda e, h=h, q=q: e.matmul(bank[1][0:n, h * 64:(h + 1) * 64], lhsT=Pm[0:n, q, 0:n], rhs=W0s[0:n, h, :], start=True, stop=True), reads=[Pmb, W0b], writes=[bkb[1]])
            vop(lambda e: e.tensor_copy(out=Us[0:n, :, :], in_=bank[1][0:n, 0:512].rearrange("p (h d) -> p h d", h=8)), [bkb[1]], [Usb])
            if C.emit_out:
                for h in range(8):
                    q = qi(h, c)
                    P.op(PE, lambda e, h=h, c=c: e.matmul(bank[2][0:n, h * 64:(h + 1) * 64], lhsT=colb(RT, h, c), rhs=H16[:, h, :], start=True, stop=False), reads=[XBFb[RT], H16b], writes=[bkb[2]])
                    P.op(PE, lambda e, h=h, q=q: e.matmul(bank[2][0:n, h * 64:(h + 1) * 64], lhsT=MM[0:n, 3, q, 0:n], rhs=Us[0:n, h, :], start=False, stop=False), reads=[MMb, Usb], writes=[bkb[2]])
                    P.op(PE, lambda e, h=h, q=q: e.matmul(bank[2][0:n, h * 64:(h + 1) * 64], lhsT=MM[0:n, 1, q, 0:n], rhs=TM[0:n, 0, q, :], start=False, stop=True), reads=[MMb, TMb], writes=[bkb[2]])
            for h in range(8):
                q = qi(h, c)
                P.op(PE, lambda e, h=h, q=q: e.matmul(bank[3][0:64, h * 64:(h + 1) * 64], lhsT=TM[0:n, 2, q, :], rhs=Us[0:n, h, :], start=True, stop=False), reads=[TMb, Usb], writes=[bkb[3]])
                P.op(PE, lambda e, h=h, q=q: e.matmul(bank[3][0:64, h * 64:(h + 1) * 64], lhsT=TM[0:n, 1, q, :], rhs=TM[0:n, 0, q, :], start=False, stop=True), reads=[TMb], writes=[bkb[3]])
            ce = 1 + (c + 1) * n - 1
            vop(lambda e: e.tensor_tensor(out=H[:], in0=H[:], in1=bank[3][0:64, 0:512].rearrange("p (h d) -> p h d", h=8), op=ALU.add), [bkb[3], Hb], [Hb])
            vop(lambda e, ce=ce: e.tensor_tensor(out=H[:], in0=H[:], in1=X[G1][:, :, ce:ce + 1].broadcast_to([64, 8, 64]), op=ALU.mult), [Hb, Xb[G1]], [Hb])
            P.op(ACT, lambda e: e.activation(out=H16[:], in_=H[:], func=AF.Copy), reads=[Hb], writes=[H16b])
            if C.emit_out:
                P.op(PE, lambda e, c=c: e.matmul(bank[4][0:n, 0:512], lhsT=sg[0:96, c * n:(c + 1) * n], rhs=g2[0:96, :], start=True, stop=True), reads=[sgb, pb_], writes=[bkb[4]])
                for h in range(8):
                    P.op(PE, lambda e, h=h, c=c: e.matmul(bank[5][0:n, 2 * h:2 * h + 2], lhsT=col(PRK, h, c), rhs=ones[0:64, 0:2], start=True, stop=True), reads=[Xb[PRK], onesb], writes=[bkb[5]])
                P.op(ACT, lambda e: e.activation(out=GBs[0:n, 0:512], in_=bank[4][0:n, 0:512], func=AF.Copy), reads=[bkb[4]], writes=[GBb])
                P.op(ACT, lambda e: e.activation(out=GBs[0:n, 512:528], in_=bank[5][0:n, 0:16], func=AF.Copy), reads=[bkb[5], GBb], writes=[GBb])
                YG = bank[2][0:n, 0:512].rearrange("p (h d) -> p h d", h=8)
                bcn = lambda ap: ap.unsqueeze(2).broadcast_to([n, 8, 64])
                vop(lambda e, YG=YG: e.tensor_reduce(out=ST[0:n, 0, :], in_=YG, axis=AX.X, op=ALU.add), [bkb[2]], [STb])
                P.op(ACT, lambda e, YG=YG: e.activation(out=SQ[0:n, :, :], in_=YG, func=AF.Square), reads=[bkb[2]], writes=[SQb])
                vop(lambda e: e.tensor_reduce(out=ST[0:n, 1, :], in_=SQ[0:n, :, :], axis=AX.X, op=ALU.add), [SQb, STb], [STb])
                vop(lambda e: e.tensor_scalar(out=ST[0:n, 2, :], in0=ST[0:n, 0, :], scalar1=1.0 / 64, scalar2=None, op0=ALU.mult), [STb], [STb])
                vop(lambda e: e.tensor_tensor(out=ST[0:n, 3, :], in0=ST[0:n, 2, :], in1=ST[0:n, 2, :], op=ALU.mult), [STb], [STb])
                vop(lambda e: e.tensor_scalar(out=ST[0:n, 4, :], in0=ST[0:n, 1, :], scalar1=1.0 / 64, scalar2=64e-5, op0=ALU.mult, op1=ALU.add), [STb], [STb])
                vop(lambda e: e.tensor_tensor(out=ST[0:n, 4, :], in0=ST[0:n, 4, :], in1=ST[0:n, 3, :], op=ALU.subtract), [STb], [STb])
                P.op(POOL, lambda e: e.tensor_tensor(out=ST[0:n, 5, :], in0=ST[0:n, 4, :], in1=mh[0:n, 0:8], op=ALU.pow), reads=[STb, mhb], writes=[STb])
                vop(lambda e, YG=YG, bcn=bcn: e.tensor_tensor(out=T1[0:n, :, :], in0=YG, in1=bcn(ST[0:n, 2, :]), op=ALU.subtract), [bkb[2], STb], [T1b])
                vop(lambda e, bcn=bcn: e.tensor_tensor(out=T1[0:n, :, :], in0=T1[0:n, :, :], in1=bcn(ST[0:n, 5, :]), op=ALU.mult), [T1b, STb], [T1b])
                vop(lambda e: e.tensor_tensor(out=T1[0:n, :, :], in0=T1[0:n, :, :], in1=lnw[0:n, :].rearrange("p (h d) -> p h d", h=8), op=ALU.mult), [T1b, pb_], [T1b])
                vop(lambda e: e.tensor_tensor(out=T1[0:n, :, :], in0=T1[0:n, :, :], in1=lnb[0:n, :].rearrange("p (h d) -> p h d", h=8), op=ALU.add), [T1b, pb_], [T1b])
                vtm_c = TM[0:n, 0, 0:nq, :].rearrange("p (h c) d -> p h c d", c=nch)[:, :, c, :]
                bs_c = GBs[0:n, 512:528].rearrange("p (h t) -> p h t", t=2)[:, :, 0:1].broadcast_to([n, 8, 64])
                vop(lambda e, vtm_c=vtm_c, bs_c=bs_c: e.tensor_tensor(out=SQ[0:n, :, :], in0=vtm_c, in1=bs_c, op=ALU.mult), [TMb, GBb, SQb], [SQb])
                vop(lambda e: e.tensor_tensor(out=T1[0:n, :, :], in0=T1[0:n, :, :], in1=SQ[0:n, :, :], op=ALU.add), [T1b, SQb], [T1b])
                vop(lambda e: e.tensor_tensor(out=YO[0:n, :, :], in0=T1[0:n, :, :], in1=GBs[0:n, 0:512].rearrange("p (h d) -> p h d", h=8), op=ALU.mult), [T1b, GBb], [YOb])
                P.dma(SP, y[tc0:tc0 + n, 0:512], YO[0:n, :, :].rearrange("p h d -> p (h d)"), reads=[YOb], pwrites=[yb])
    for sci, (c0, W, t0, n) in enumerate(scs):
        do_sc(sci, c0, W, t0, n)
    P.dma(POOL, prm["sA_out"].rearrange("h k v -> k h v"), H[:], reads=[Hb], pwrites=[K.sob])


def mixer_gla2(C, st, pf, pfb, pt, ptb, y, yb, prm, K):
    P = C.P
    ones, onesb, ident, idb, mask_i, mib, flagE, fb = K.ones, K.onesb, K.ident, K.idb, K.mask_i, K.mib, K.flagE, K.fb
    a2 = sbuf(C, st, "dga2", [32, 256]); a2b = Buf()
    P.op(DVE, lambda e: e.memset(a2[:], 0.0), writes=[a2b])
    P.dma(SP, a2[0:16, :], prm["gla_a2"], reads=[a2b], writes=[a2b])
    nab = sbuf(C, st, "dgnab", [64, 4]); nabb = Buf()
    nbc = sbuf(C, st, "dgnbc", [64, 128]); nbcb = Buf()
    P.dma(SP, nab[:], prm["gla_ab"], writes=[nabb])
    P.op(DVE, lambda e: e.tensor_scalar(out=nab[:], in0=nab[:], scalar1=-1.0, scalar2=None, op0=ALU.mult), reads=[nabb], writes=[nabb])
    P.dma(SP, nbc[:], prm["gla_normbc"], writes=[nbcb])
    WM = 512
    q = sbuf(C, st, "dgq", [64, 4, WM]); k = sbuf(C, st, "dgk", [64, 4, WM]); rel = sbuf(C, st, "dgrel", [64, 4, WM])
    e1 = sbuf(C, st, "dge1", [64, 4, WM]); e2 = sbuf(C, st, "dge2", [64, 4, WM])
    Fl = sbuf(C, st, "dgFl", [64, 4 * WM]); Ssc = sbuf(C, st, "dgSsc", [64, 1 + 4 * WM]); xa = sbuf(C, st, "dgxa", [32, WM])
    qb, kb_, relb, e1b, e2b, Flb, Sscb, xab = [Buf() for _ in range(8)]
    P.op(DVE, lambda e: e.memset(Ssc[:, 0:1], 0.0), writes=[Sscb])
    S = sbuf(C, st, "dgS", [64, 4, 128]); Sb = Buf()
    Sin = sbuf(C, st, "dgSin", [64, 4, 128]); Sinb = Buf()
    P.op(DVE, lambda e: e.memset(S[:], 0.0), writes=[Sb])
    bank = [psum(C, st, f"dgb{i}", [128, 512], F32) for i in range(8)]
    bkb = [Buf() for _ in range(8)]
    vr = Ring([sbuf(C, st, f"dgv{i}", [64, 1024], F32) for i in range(3)])
    ktr = Ring([sbuf(C, st, f"dgkt{i}", [64, 256], F32) for i in range(2)])
    scr = Ring([sbuf(C, st, f"dgsc{i}", [64, 4, 64], F32) for i in range(2)])
    t1r = Ring([sbuf(C, st, f"dgt1{i}", [64, 4, 128], F32) for i in range(2)])
    yor = Ring([sbuf(C, st, f"dgyo{i}", [64, 4, 128], BF16) for i in range(2)])
    str_ = Ring([sbuf(C, st, f"dgst{i}", [64, 3, 4], F32) for i in range(2)])
    junk = sbuf(C, st, "dgjunk", [64, 4, 128], F32); jb = Buf()
    mh = sbuf(C, st, "dgmh", [64, 4], F32); mhb = Buf()
    P.op(POOL, lambda e: e.memset(mh[:], -0.5), writes=[mhb])

    def vop(fn, reads, writes, pwrites=()):
        P.op(DVE, fn, reads=reads, writes=writes, pwrites=pwrites)

    scs = [(3, 16, 0, 16)] + [(22 + 512 * i, 512, 16 + 512 * i, 64) for i in range(4)]
    cnt = [0]

    def do_sc(sci, c0, W, t0, n):
        nch = W // n
        P.dma(SP, q[:, :, 0:W], pf[14 * 128:14 * 128 + 256, c0:c0 + W].rearrange("(h d) c -> d h c", d=64), reads=[pfb], writes=[qb])
        P.dma(SP, k[:, :, 0:W], pf[16 * 128:16 * 128 + 256, c0:c0 + W].rearrange("(h d) c -> d h c", d=64), reads=[pfb], writes=[kb_])
        P.dma(SP, xa[:, 0:W], pf[18 * 128:18 * 128 + 32, c0:c0 + W], reads=[pfb], writes=[xab])
        if sci == 1:
            P.dma(SP, Sin[:], prm["sB_in"].rearrange("h k v -> k h v"), writes=[Sinb])
            vop(lambda e: e.scalar_tensor_tensor(out=S[:], in0=S[:], scalar=flagE[0:64, 0:1], in1=Sin[:], op0=ALU.mult, op1=ALU.add), [Sb, Sinb, fb], [Sb])
        STOP = 9
        if STOP <= 1:
            return
        for h in range(4):
            bk, off = (h * W) // 512, (h * W) % 512
            P.op(PE, lambda e, bk=bk, off=off, h=h: e.matmul(bank[bk][0:64, off:off + W], lhsT=a2[0:32, h * 64:(h + 1) * 64], rhs=xa[0:32, 0:W], start=True, stop=True),
                 reads=[a2b, xab], writes=[bkb[bk]])
            P.op(ACT, lambda e, bk=bk, off=off, h=h: e.activation(out=Fl[:, h * W:(h + 1) * W], in_=bank[bk][0:64, off:off + W], func=AF.Exp, scale=-1.0, bias=nab[:, h:h + 1]),
                 reads=[bkb[bk], nabb], pwrites=[Flb])
        if STOP <= 2:
            return
        P.op(ACT, lambda e: e.activation(out=Fl[:, 0:4 * W], in_=Fl[:, 0:4 * W], func=AF.Ln, bias=1.0), reads=[Flb], writes=[Flb])
        vop(lambda e: e.tensor_tensor_scan(out=Ssc[:, 1:1 + 4 * W], data0=ones[0:64, 0:4 * W], data1=Fl[:, 0:4 * W], initial=0.0, op0=ALU.mult, op1=ALU.add),
            [Flb, Sscb, onesb], [Sscb])
        vop(lambda e: e.tensor_tensor(out=rel[:, :, 0:W].rearrange("p h (c j) -> p h c j", j=n),
                                      in0=Ssc[:, 1:1 + 4 * W].rearrange("p (h c j) -> p h c j", h=4, j=n),
                                      in1=Ssc[:, 0:4 * W].rearrange("p (h c j) -> p h c j", h=4, j=n)[:, :, :, 0:1].broadcast_to([64, 4, nch, n]), op=ALU.subtract),
            [Sscb], [relb])
        if STOP <= 3:
            return
        P.op(ACT, lambda e: e.activation(out=e1[:, :, 0:W], in_=rel[:, :, 0:W], func=AF.Exp, scale=-1.0 / 16), reads=[relb], writes=[e1b])
        P.op(ACT, lambda e: e.activation(out=e2[:, :, 0:W], in_=rel[:, :, 0:W], func=AF.Exp, scale=1.0 / 16), reads=[relb], writes=[e2b])
        if STOP <= 4:
            return
        vop(lambda e: e.scalar_tensor_tensor(out=q[:, :, 0:W], in0=q[:, :, 0:W], scalar=0.125, in1=e1[:, :, 0:W], op0=ALU.mult, op1=ALU.mult), [qb, e1b], [qb])
        if STOP <= 5:
            return
        vop(lambda e: e.scalar_tensor_tensor(out=k[:, :, 0:W], in0=k[:, :, 0:W], scalar=1.0, in1=e2[:, :, 0:W], op0=ALU.mult, op1=ALU.mult), [kb_, e2b], [kb_])
        for c in range(0 if None else nch):
            tc0 = t0 + c * n
            bA, bO, bS = (4, 5, 6) if cnt[0] % 2 == 0 else (1, 2, 3)
            cnt[0] += 1
            vt, vb = vr.next()
            P.dma(SP, vt[0:n, :], pt[tc0:tc0 + n, 0:1024], reads=[ptb], writes=[vb])
            for h in range(4):
                P.op(PE, lambda e, h=h, c=c, bA=bA: e.transpose(out=bank[bA][0:n, h * 64:(h + 1) * 64], in_=k[:, h, c * n:(c + 1) * n], identity=ident[0:64, 0:64]),
                     reads=[kb_, idb], writes=[bkb[bA]])
            for h in range(4):
                P.op(PE, lambda e, h=h, c=c, bA=bA: e.matmul(bank[bA][0:n, 256 + h * 64:256 + h * 64 + n], lhsT=k[:, h, c * n:(c + 1) * n], rhs=q[:, h, c * n:(c + 1) * n],
                                                          start=True, stop=True), reads=[kb_, qb], writes=[bkb[bA]])
            kt, ktb = ktr.next(); sc, scb = scr.next()
            P.op(ACT, lambda e, kt=kt, bA=bA: e.activation(out=kt[0:n, :], in_=bank[bA][0:n, 0:256], func=AF.Copy), reads=[bkb[bA]], writes=[ktb])
            vop(lambda e, sc=sc, bA=bA: e.tensor_tensor(out=sc[0:n, :, 0:n], in0=bank[bA][0:n, 256:512].rearrange("p (h d) -> p h d", h=4)[:, :, 0:n],
                                                      in1=mask_i[0:n, 0:n].unsqueeze(1).broadcast_to([n, 4, n]), op=ALU.mult), [bkb[bA], mib], [scb])
            for h in range(4):
                P.op(PE, lambda e, h=h, c=c, bO=bO: e.matmul(bank[bO][0:n, h * 128:(h + 1) * 128], lhsT=q[:, h, c * n:(c + 1) * n], rhs=S[:, h, :], start=True, stop=False),
                     reads=[qb, Sb], writes=[bkb[bO]])
                P.op(PE, lambda e, h=h, sc=sc, vt=vt, bO=bO: e.matmul(bank[bO][0:n, h * 128:(h + 1) * 128], lhsT=sc[0:n, h, 0:n], rhs=vt[0:n, h * 128:(h + 1) * 128], start=False, stop=True),
                     reads=[scb, vb], writes=[bkb[bO]])
            for h in range(4):
                P.op(PE, lambda e, h=h, bS=bS: e.matmul(bank[bS][0:64, h * 128:(h + 1) * 128], lhsT=ident[0:64, 0:64], rhs=S[:, h, :], start=True, stop=False),
                     reads=[idb, Sb], writes=[bkb[bS]])
                P.op(PE, lambda e, h=h, kt=kt, vt=vt, bS=bS: e.matmul(bank[bS][0:64, h * 128:(h + 1) * 128], lhsT=kt[0:n, h * 64:(h + 1) * 64], rhs=vt[0:n, h * 128:(h + 1) * 128], start=False, stop=True),
                     reads=[ktb, vb], writes=[bkb[bS]])
            ce = (c + 1) * n - 1
            vop(lambda e, ce=ce, bS=bS: e.tensor_tensor(out=S[:], in0=bank[bS][0:64, 0:512].rearrange("p (h d) -> p h d", h=4),
                                                      in1=e1[:, :, ce:ce + 1].broadcast_to([64, 4, 128]), op=ALU.mult), [bkb[bS], e1b], [Sb])
            if C.emit_out and not False:
                s_, sb2 = str_.next(); t1, t1b = t1r.next(); yo, yob = yor.next()
                P.op(ACT, lambda e, t1=t1, bO=bO: e.activation(out=t1[0:n, :, :], in_=bank[bO][0:n, 0:512].rearrange("p (h d) -> p h d", h=4), func=AF.Copy),
                     reads=[bkb[bO]], writes=[t1b])
                vop(lambda e, t1=t1: e.scalar_tensor_tensor(out=junk[0:n, :, :], in0=t1[0:n, :, :], scalar=1.0, in1=t1[0:n, :, :], op0=ALU.mult, op1=ALU.mult), [t1b], [jb])
                vop(lambda e, s_=s_: e.tensor_reduce(out=s_[0:n, 0, :], in_=junk[0:n, :, :], axis=AX.X, op=ALU.add), [jb], [sb2])
                vop(lambda e, s_=s_: e.tensor_scalar(out=s_[0:n, 1, :], in0=s_[0:n, 0, :], scalar1=1.0 / 128, scalar2=EPS, op0=ALU.mult, op1=ALU.add), [sb2], [sb2])
                P.op(POOL, lambda e, s_=s_: e.tensor_tensor(out=s_[0:n, 2, :], in0=s_[0:n, 1, :], in1=mh[0:n, :], op=ALU.pow), reads=[sb2, mhb], writes=[sb2])
                vop(lambda e, t1=t1, s_=s_: e.scalar_tensor_tensor(out=t1[0:n, :, :], in0=t1[0:n, :, :], scalar=1.0, in1=s_[0:n, 2, :].unsqueeze(2).broadcast_to([n, 4, 128]), op0=ALU.mult, op1=ALU.mult),
                    [t1b, sb2], [t1b])
                vop(lambda e, t1=t1: e.scalar_tensor_tensor(out=t1[0:n, :, :], in0=t1[0:n, :, :], scalar=1.0, in1=nbc[0:n, :].unsqueeze(1).broadcast_to([n, 4, 128]), op0=ALU.mult, op1=ALU.mult),
                    [t1b, nbcb], [t1b])
                vop(lambda e, yo=yo, t1=t1, vt=vt: e.scalar_tensor_tensor(out=yo[0:n, :, :], in0=t1[0:n, :, :], scalar=1.0, in1=vt[0:n, 512:1024].rearrange("p (h d) -> p h d", h=4), op0=ALU.mult, op1=ALU.mult),
                    [t1b, vb], [yob])
                P.dma(SP, y[tc0:tc0 + n, 512:1024], yo[0:n, :, :].rearrange("p h d -> p (h d)"), reads=[yob], pwrites=[yb])

    for sci, (c0, W, t0, n) in enumerate(scs):
        do_sc(sci, c0, W, t0, n)
    P.dma(POOL, prm["sB_out"].rearrange("h k v -> k h v"), S[:], reads=[Sb], pwrites=[K.sob])


def run_gens(gens):
    gens = list(gens)
    while gens:
        for g_ in list(gens):
            try:
                next(g_)
            except StopIteration:
                gens.remove(g_)


def mixer_gla_f32(C, st, pf, pfb, pt, ptb, y, yb, prm, K, npsA=2, npsB=6):
    P = C.P
    ones, onesb, ident, idb, mask_i, mib, flagE, fb = K.ones, K.onesb, K.ident, K.idb, K.mask_i, K.mib, K.flagE, K.fb
    a2 = sbuf(C, st, "fga2", [32, 256]); a2b = Buf()
    P.op(DVE, lambda e: e.memset(a2[:], 0.0), writes=[a2b])
    nab = sbuf(C, st, "fgnab", [64, 4]); nabb = Buf()
    nbc = sbuf(C, st, "fgnbc", [64, 128]); nbcb = Buf()
    P.dma(SP, a2[0:16, :], prm["gla_a2"], reads=[a2b], writes=[a2b])
    P.dma(SP, nab[:], prm["gla_ab"], writes=[nabb])
    P.op(DVE, lambda e: e.tensor_scalar(out=nab[:], in0=nab[:], scalar1=-1.0, scalar2=None, op0=ALU.mult), reads=[nabb], writes=[nabb])
    P.dma(SP, nbc[:], prm["gla_normbc"], writes=[nbcb])
    xa = sbuf(C, st, "fgxa", [32, TP]); xab = Buf()
    P.dma(SP, xa[:], pf[18 * 128:18 * 128 + 32, :], reads=[pfb], writes=[xab])
    q = sbuf(C, st, "fgq", [64, TP]); k = sbuf(C, st, "fgk", [64, TP]); sp = sbuf(C, st, "fgsp", [64, TP])
    spc = sbuf(C, st, "fgspc", [64, TP]); e1 = sbuf(C, st, "fge1", [64, TP]); e2 = sbuf(C, st, "fge2", [64, TP])
    qb, kb_, spb, spcb, e1b, e2b = [Buf() for _ in range(6)]
    S = sbuf(C, st, "fgS", [64, 128]); Sb = Buf()
    Sin = sbuf(C, st, "fgSin", [64, 128]); Sinb = Buf()
    psA = Ring([psum(C, st, f"fgpa{i}", [128, 512], F32) for i in range(npsA)])
    psB = Ring([psum(C, st, f"fgpb{i}", [128, 512], F32) for i in range(npsB)])
    vr = Ring([sbuf(C, st, f"fgv{i}", [64, 256], F32) for i in range(3)])
    ktr = Ring([sbuf(C, st, f"fgkt{i}", [64, 64], F32) for i in range(2)])
    scr = Ring([sbuf(C, st, f"fgsc{i}", [64, 64], F32) for i in range(2)])
    str_ = Ring([sbuf(C, st, f"fgst{i}", [64, 4], F32) for i in range(2)])
    junk = sbuf(C, st, "fgjunk", [64, 128], F32); jb = Buf()
    mh = sbuf(C, st, "fgmh", [64, 1], F32); mhb = Buf()
    P.op(POOL, lambda e: e.memset(mh[:], -0.5), writes=[mhb])
    t1r = Ring([sbuf(C, st, f"fgt1{i}", [64, 128], F32) for i in range(2)])
    yor = Ring([sbuf(C, st, f"fgyo{i}", [64, 128], BF16) for i in range(2)])
    for h in range(4):
        r0 = (14 + h // 2) * 128 + (h % 2) * 64
        r1 = (16 + h // 2) * 128 + (h % 2) * 64
        P.dma(SP, q[:], pf[r0:r0 + 64, :], reads=[pfb], writes=[qb])
        P.dma(SP, k[:], pf[r1:r1 + 64, :], reads=[pfb], writes=[kb_])
        for c0 in range(3, TP, 512):
            n = min(512, TP - c0)
            pa, pab = psA.next()
            P.op(PE, lambda e, pa=pa, c0=c0, n=n, h=h: e.matmul(pa[0:64, 0:n], lhsT=a2[0:32, h * 64:(h + 1) * 64], rhs=xa[0:32, c0:c0 + n],
                                                               start=True, stop=True), reads=[a2b, xab], writes=[pab])
            P.op(ACT, lambda e, pa=pa, c0=c0, n=n, h=h: e.activation(out=sp[:, c0:c0 + n], in_=pa[0:64, 0:n], func=AF.Exp, scale=-1.0,
                                                                    bias=nab[:, h:h + 1]), reads=[pab, nabb], writes=[spb])
        P.op(ACT, lambda e: e.activation(out=sp[:, 3:TP], in_=sp[:, 3:TP], func=AF.Ln, bias=1.0), reads=[spb], writes=[spb])
        for (c0, ncol, t0) in SEGS:
            P.op(DVE, lambda e, c0=c0, ncol=ncol: e.tensor_tensor_scan(out=spc[:, c0:c0 + ncol], data0=ones[0:64, c0:c0 + ncol],
                                                                       data1=sp[:, c0:c0 + ncol], initial=0.0, op0=ALU.mult, op1=ALU.add),
                 reads=[spb, onesb], writes=[spcb])
        chunk_rel(C, sp, spb, spc, spcb, 64)
        P.op(ACT, lambda e: e.activation(out=e1[:, 3:TP], in_=sp[:, 3:TP], func=AF.Exp, scale=-1.0 / 16), reads=[spb], writes=[e1b])
        P.op(ACT, lambda e: e.activation(out=e2[:, 3:TP], in_=sp[:, 3:TP], func=AF.Exp, scale=1.0 / 16), reads=[spb], writes=[e2b])
        P.op(DVE, lambda e: e.scalar_tensor_tensor(out=q[:, 3:TP], in0=q[:, 3:TP], scalar=0.125, in1=e1[:, 3:TP], op0=ALU.mult,
                                                   op1=ALU.mult), reads=[qb, e1b], writes=[qb])
        P.op(DVE, lambda e: e.tensor_tensor(out=k[:, 3:TP], in0=k[:, 3:TP], in1=e2[:, 3:TP], op=ALU.mult), reads=[kb_, e2b], writes=[kb_])
        P.op(DVE, lambda e: e.memset(S[:], 0.0), writes=[Sb])
        for si, seg in enumerate(SEGS):
            if si == 1:
                P.dma(SP, Sin[:], prm["sB_in"][h], writes=[Sinb])
                P.op(DVE, lambda e: e.scalar_tensor_tensor(out=S[:], in0=S[:], scalar=flagE[0:64, 0:1], in1=Sin[:], op0=ALU.mult,
                                                           op1=ALU.add), reads=[Sb, Sinb, fb], writes=[Sb])
            for (c0, n, t0) in chunks_of(seg):
                vt, vb = vr.next()
                P.dma(SP, vt[0:n, 0:128], pt[t0:t0 + n, h * 128:(h + 1) * 128], reads=[ptb], writes=[vb])
                P.dma(SP, vt[0:n, 128:256], pt[t0:t0 + n, 512 + h * 128:512 + (h + 1) * 128], reads=[ptb], writes=[vb])
                pb, pbb = psB.next()
                P.op(PE, lambda e, pb=pb, c0=c0, n=n: e.transpose(out=pb[0:n, 0:64], in_=k[:, c0:c0 + n], identity=ident[0:64, 0:64]),
                     reads=[kb_, idb], writes=[pbb])
                P.op(PE, lambda e, pb=pb, c0=c0, n=n: e.matmul(pb[0:n, 64:64 + n], lhsT=k[:, c0:c0 + n], rhs=q[:, c0:c0 + n], start=True, stop=True),
                     reads=[kb_, qb], writes=[pbb])
                kt, ktb = ktr.next()
                sc, scb = scr.next()
                P.op(ACT, lambda e, kt=kt, pb=pb, n=n: e.activation(out=kt[0:n, :], in_=pb[0:n, 0:64], func=AF.Copy), reads=[pbb], writes=[ktb])
                P.op(DVE, lambda e, sc=sc, pb=pb, n=n: e.tensor_tensor(out=sc[0:n, 0:n], in0=pb[0:n, 64:64 + n], in1=mask_i[0:n, 0:n], op=ALU.mult),
                     reads=[pbb, mib], writes=[scb])
                po, pob = psB.next()
                P.op(PE, lambda e, po=po, c0=c0, n=n: e.matmul(po[0:n, 0:128], lhsT=q[:, c0:c0 + n], rhs=S[:, :], start=True, stop=False),
                     reads=[qb, Sb], writes=[pob])
                P.op(PE, lambda e, po=po, sc=sc, vt=vt, n=n: e.matmul(po[0:n, 0:128], lhsT=sc[0:n, 0:n], rhs=vt[0:n, 0:128], start=False, stop=True),
                     reads=[scb, vb], writes=[pob])
                pc, pcb = psB.next()
                P.op(PE, lambda e, pc=pc: e.matmul(pc[0:64, 0:128], lhsT=ident[0:64, 0:64], rhs=S[:, :], start=True, stop=False),
                     reads=[idb, Sb], writes=[pcb])
                P.op(PE, lambda e, pc=pc, kt=kt, vt=vt, n=n: e.matmul(pc[0:64, 0:128], lhsT=kt[0:n, 0:64], rhs=vt[0:n, 0:128], start=False, stop=True),
                     reads=[ktb, vb], writes=[pcb])
                ce = c0 + n - 1
                P.op(DVE, lambda e, pc=pc, ce=ce: e.tensor_scalar(out=S[:], in0=pc[0:64, 0:128], scalar1=e1[:, ce:ce + 1], scalar2=None, op0=ALU.mult),
                     reads=[pcb, e1b], writes=[Sb])
                if C.emit_out and not False:
                    s_, sb2 = str_.next()
                    t1, t1b = t1r.next()
                    yo, yob = yor.next()
                    P.op(ACT, lambda e, t1=t1, po=po, n=n: e.activation(out=t1[0:n, :], in_=po[0:n, 0:128], func=AF.Copy), reads=[pob], writes=[t1b])
                    P.op(DVE, lambda e, t1=t1, n=n: e.tensor_tensor(out=junk[0:n, :], in0=t1[0:n, :], in1=t1[0:n, :], op=ALU.mult), reads=[t1b], writes=[jb])
                    P.op(DVE, lambda e, s_=s_, n=n: e.tensor_reduce(out=s_[0:n, 0:1], in_=junk[0:n, :], axis=AX.X, op=ALU.add), reads=[jb], writes=[sb2])
                    P.op(DVE, lambda e, s_=s_, n=n: e.tensor_scalar(out=s_[0:n, 1:2], in0=s_[0:n, 0:1], scalar1=1.0 / 128, scalar2=EPS, op0=ALU.mult,
                                                                   op1=ALU.add), reads=[sb2], writes=[sb2])
                    P.op(POOL, lambda e, s_=s_, n=n: e.tensor_tensor(out=s_[0:n, 2:3], in0=s_[0:n, 1:2], in1=mh[0:n, :], op=ALU.pow),
                         reads=[sb2, mhb], writes=[sb2])
                    P.op(DVE, lambda e, t1=t1, s_=s_, n=n: e.scalar_tensor_tensor(out=t1[0:n, :], in0=t1[0:n, :], scalar=s_[0:n, 2:3],
                                                                               in1=nbc[0:n, :], op0=ALU.mult, op1=ALU.mult),
                         reads=[sb2, nbcb, t1b], writes=[t1b])
                    P.op(DVE, lambda e, yo=yo, t1=t1, vt=vt, n=n: e.tensor_tensor(out=yo[0:n, :], in0=t1[0:n, :], in1=vt[0:n, 128:256], op=ALU.mult),
                         reads=[t1b, vb], writes=[yob])
                    P.dma(SP, y[t0:t0 + n, 512 + h * 128:512 + (h + 1) * 128], yo[0:n, :], reads=[yob], pwrites=[yb])
                yield
        P.dma(POOL, prm["sB_out"][h], S[:], reads=[Sb], pwrites=[K.sob])


import contextlib
import numpy as np

PRM_SHAPES = {
    "gla_a2": [16, 256], "gla_ab": [64, 4], "gla_normbc": [64, 128],
    "ml_cw": [128, 8, 4], "ml_cb": [128, 8], "ml_ib": [4, 1], "ml_fb": [4, 1], "ml_normbc": [64, 1024], "onehot": [4, 4, 128],
    "rw_muA": [64, 3, 8], "rw_muL": [96, 3], "rw_w2": [32, 512], "rw_a2": [32, 512], "rw_g2": [96, 512], "rw_ch": [64, 4, 8],
    "rw_rk": [64, 8, 2], "rw_lnw_bc": [64, 512], "rw_lnb_bc": [64, 512],
    "sA_in": [8, 64, 64], "sB_in": [4, 64, 128], "sC_in": [4, 128, 257], "mC_in": [4, 1], "hist_in": [NFMB * 128, 3],
    "flagE": [128, 1], "mask_i": [64, 64], "mask5": [64, 5, 64],
}
OUT_SHAPES = {"sA_out": [8, 64, 64], "sB_out": [4, 64, 128], "sC_out": [4, 128, 257], "mC_out": [4, 1], "hist_out": [NFMB * 128, 3]}


def host_consts():
    j = np.arange(64)
    mi = (j[None, :] >= j[:, None]).astype(np.float32)
    ms = (j[None, :] > j[:, None]).astype(np.float32)
    ml = (j[None, :] < j[:, None]).astype(np.float32)
    mask5 = np.stack([ms, mi, ms, mi, ml], 1)
    oh = np.zeros((4, 4, 128), np.float32)
    for h in range(4):
        oh[h, h, :] = 1.0
    return {"mask_i": mi, "mask5": np.ascontiguousarray(mask5), "onehot": oh}


def host_layer_params(z, l):
    f = lambda a: np.ascontiguousarray(a, dtype=np.float32)
    chT = lambda v: f(v.reshape(8, 64).T)
    mu = z["rw_mu"][l]
    d = {}
    d["gla_a2"] = f(z["gla_a2"][l]); d["gla_ab"] = f(z["gla_ab"][l].reshape(4, 64).T)
    d["gla_normbc"] = f(np.broadcast_to(z["gla_norm"][l], (64, 128)))
    cw = z["ml_conv_w"][l]
    d["ml_cw"] = f(cw.reshape(4, 8, 128).transpose(2, 1, 0)); d["ml_cb"] = f(z["ml_conv_b"][l].reshape(8, 128).T)
    d["ml_ib"] = f(z["ml_ib"][l].reshape(4, 1)); d["ml_fb"] = f(z["ml_fb"][l].reshape(4, 1))
    d["ml_normbc"] = f(np.broadcast_to(z["ml_norm"][l], (64, 1024)))
    d["rw_muA"] = f(np.stack([chT(mu[0:512]), chT(mu[512:1024]), chT(mu[1024:1536])], 1))
    muL = np.zeros((96, 3), np.float32); muL[0:32, 0] = mu[1536:1568]; muL[0:32, 1] = mu[1568:1600]; muL[0:96, 2] = mu[1600:1696]
    d["rw_muL"] = muL
    d["rw_w2"] = f(z["rw_w2"][l]); d["rw_a2"] = f(z["rw_a2"][l]); d["rw_g2"] = f(z["rw_g2"][l])
    d["rw_ch"] = f(np.stack([chT(z["rw_w0"][l]), chT(z["rw_a0"][l]), chT(z["rw_kk"][l]), chT(z["rw_ka"][l])], 1))
    rk = z["rw_rk"][l]
    d["rw_rk"] = f(np.stack([rk.T, rk.T], 2))
    d["rw_lnw_bc"] = f(np.broadcast_to(z["rw_ln_w"][l], (64, 512))); d["rw_lnb_bc"] = f(np.broadcast_to(z["rw_ln_b"][l], (64, 512)))
    return d


def host_layer_weights(z, l):
    return {"win": hp.prep_win(z["w_in"][l]), "wout": hp.prep_sq(z["w_out"][l], 4), "w1": hp.prep_sq(z["ffn_w1"][l], 11),
            "w3": hp.prep_sq(z["ffn_w3"][l], 11), "w2": hp.prep_w2(z["ffn_w2"][l]), "g1": hp.gT(z["norm_mix"][l]), "g2": hp.gT(z["norm_ffn"][l])}


def build_layer(debug=False, emit_out=True, do_final=True):
    nc = bass.Bass("TRN2", target_bir_lowering=False)
    C = Ctx(); C.nc = nc; C.P = Prog(nc, same_engine_sync=True); C.emit_out = emit_out
    C.P.scopes = False
    P = C.P
    dr = lambda n, s, dt=F32, kind="ExternalInput": nc.dram_tensor(n, s, dt, kind=kind).ap()
    hin = dr("hin", [NTOK, D])
    win = dr("win", [13, 128, 8192]); wout = dr("wout", [4, 128, 8192])
    w1 = dr("w1", [11, 128, 8192]); w3 = dr("w3", [11, 128, 8192]); w2 = dr("w2", [4, 4, 128, 11 * 512])
    g1 = dr("g1", [128, 16]); g2 = dr("g2", [128, 16]); gf = dr("gf", [128, D])
    prm = {k: dr(k, s) for k, s in PRM_SHAPES.items()}
    for k, s in OUT_SHAPES.items():
        prm[k] = dr(k, s, F32, "ExternalOutput")
    dk = "ExternalOutput" if debug else "Internal"
    pf = dr("pf", [NFMB * 128, TP], F32, dk)
    pt = dr("pt", [NTOK, NTMC], F32, dk)
    y = dr("y", [NTOK, D], BF16, dk)
    hmid = dr("hmid", [NTOK, D], F32, dk)
    hout = dr("hout", [NTOK, D], F32, "ExternalOutput")
    out = dr("out", [NTOK - 16, D], F32, "ExternalOutput")
    aT = dr("aT", [5, 128, 44, 512], BF16, "Internal")
    hb, pfb, ptb, yb, hmb, hob, ob, ab = [Buf() for _ in range(8)]
    K = Ctx(); K.sob = Buf()
    with contextlib.ExitStack() as st0:
        K.ident = sbuf(C, st0, "ident", [128, 128], F32); identb = sbuf(C, st0, "identb", [128, 128], BF16)
        g1t = sbuf(C, st0, "g1t", [128, 16]); g2t = sbuf(C, st0, "g2t", [128, 16])
        K.idb, idbb, g1b, g2b = [Buf() for _ in range(4)]
        P.op(POOL, lambda e: e.memset(K.ident[:], 1.0), writes=[K.idb])
        P.op(POOL, lambda e: e.affine_select(out=K.ident[:], in_=K.ident[:], pattern=[[-1, 128]], base=0, channel_multiplier=1,
                                             compare_op=ALU.is_equal, fill=0.0), reads=[K.idb], writes=[K.idb])
        P.op(POOL, lambda e: e.tensor_copy(out=identb[:], in_=K.ident[:]), reads=[K.idb], writes=[idbb])
        P.dma(SP, g1t[:], g1, writes=[g1b]); P.dma(SP, g2t[:], g2, writes=[g2b])
        with contextlib.ExitStack() as st1:
            uT = sbuf(C, st1, "uT", [128, 16, NTOK], BF16); ub = Buf()
            pst = Ring([psum(C, st1, f"pst{i}", [128, 1024], BF16) for i in range(2)])
            psm = Ring([psum(C, st1, f"psm{i}", [128, 512], F32) for i in range(6)])
            with contextlib.ExitStack() as st:
                P.phase = "norm"
                phase_norm(C, st, hin, hb, g1t, g1b, uT, ub, pst, identb, idbb)
            P.barrier()
            with contextlib.ExitStack() as st:
                P.phase = "proj"
                phase_proj(C, st, uT, ub, win, pf, pfb, pt, ptb, psm, prm["hist_out"], K.sob)
            P.barrier()
        with contextlib.ExitStack() as st1:
            K.ones = sbuf(C, st1, "ones", [64, TP]); K.onesb = Buf()
            K.mask_i = sbuf(C, st1, "mask_i", [64, 64]); K.mib = Buf()
            K.mask5 = sbuf(C, st1, "mask5", [64, 5, 64]); K.m5b = Buf()
            K.flagE = sbuf(C, st1, "flagE", [128, 1]); K.fb = Buf()
            P.op(POOL, lambda e: e.memset(K.ones[:], 1.0), writes=[K.onesb])
            P.dma(SP, K.mask_i[:], prm["mask_i"], writes=[K.mib]); P.dma(SP, K.mask5[:], prm["mask5"], writes=[K.m5b])
            P.dma(SP, K.flagE[:], prm["flagE"], writes=[K.fb])
            with contextlib.ExitStack() as st:
                P.phase = "prepass"
                gate_prepass(C, st, pt, ptb)
            P.barrier()
            with contextlib.ExitStack() as st:
                P.phase = "gla"
                if None:
                    with contextlib.ExitStack() as st2:
                        mixer_gla2(C, st2, pf, pfb, pt, ptb, y, yb, prm, K)
                    P.barrier()
                    P.phase = "mlstm"
                    run_gens([mixer_mlstm(C, st, pf, pfb, pt, ptb, y, yb, prm, K, 4, 2)])
                else:
                    if None:
                        with contextlib.ExitStack() as st2:
                            run_gens([mixer_gla_f32(C, st2, pf, pfb, pt, ptb, y, yb, prm, K)])
                        P.barrier()
                        P.phase = "mlstm"
                        with contextlib.ExitStack() as st2:
                            run_gens([mixer_mlstm(C, st2, pf, pfb, pt, ptb, y, yb, prm, K)])
                    else:
                        run_gens([mixer_gla(C, st, pf, pfb, pt, ptb, y, yb, prm, K, 1, 3), mixer_mlstm(C, st, pf, pfb, pt, ptb, y, yb, prm, K, 3, 1)])
            P.barrier()
            if True:
              with contextlib.ExitStack() as st:
                P.phase = "rwkv"
                mixer_rwkv3(C, st, pf, pfb, y, yb, prm, K)
            P.barrier()
        if emit_out:
            with contextlib.ExitStack() as st1:
                uT = sbuf(C, st1, "uT2", [128, 16, NTOK], BF16); ub = Buf()
                pst = Ring([psum(C, st1, f"pst{i}", [128, 1024], BF16) for i in range(2)])
                psm = Ring([psum(C, st1, f"psm{i}", [128, 512], F32) for i in range(6)])
                with contextlib.ExitStack() as st:
                    P.phase = "wout"
                    phase_wout(C, st, y, yb, hin, hb, hmid, hmb, wout, uT, ub, psm, pst, identb, idbb)
                P.barrier()
                with contextlib.ExitStack() as st:
                    P.phase = "norm"
                    phase_norm(C, st, hmid, hmb, g2t, g2b, uT, ub, pst, identb, idbb)
                P.barrier()
                with contextlib.ExitStack() as st:
                    P.phase = "ffn1"
                    phase_ffn1(C, st, uT, ub, w1, w3, aT, ab, psm)
                P.barrier()
    if emit_out:
        with contextlib.ExitStack() as st:
            psm = Ring([psum(C, st, f"psn{i}", [128, 512], F32) for i in range(6)])
            P.phase = "ffn2"
            phase_ffn2(C, st, aT, ab, w2, hmid, hmb, hout, hob, psm)
        P.barrier()
        if do_final:
            with contextlib.ExitStack() as st:
                gft = sbuf(C, st, "gft2", [128, D]); gfb = Buf()
                P.dma(SP, gft[:], gf, writes=[gfb])
                P.phase = "final"
                phase_final_norm(C, st, hout, hob, gft, gfb, out, ob)
    fin = [K.sob, hob, ob]
    if debug:
        fin += [pfb, ptb, yb, hmb]
    P.finish(fin)
    P.emit()
    C.counts = {e: (len(P.ops[e]), sum(1 for o in P.ops[e] if o.signal)) for e in ENGS}
    return nc, C


import contextlib
import numpy as np

LAYER_KEYS = ["gla_a2", "gla_ab", "gla_normbc", "ml_cw", "ml_cb", "ml_ib", "ml_fb", "ml_normbc", "rw_muA", "rw_muL", "rw_w2", "rw_a2",
              "rw_g2", "rw_ch", "rw_rk", "rw_lnw_bc", "rw_lnb_bc"]
STATE_KEYS = ["sA", "sB", "sC", "mC", "hist"]


def emit_half(C, K, T, l, half):
    P = C.P
    hin, hinb = T["hin"][(l, half)]
    hout, houtb = T["hout"][(l, half)]
    prm = {k: T["lp"][k][l] for k in LAYER_KEYS}
    for k in ("mask_i", "mask5", "onehot"):
        prm[k] = T["const"][k]
    prm["flagE"] = T["flag1"] if half == 0 else T["flag0"]
    for k in STATE_KEYS:
        prm[k + "_in"] = T["zstate"][k] if half == 0 else T["state"][k][l]
        prm[k + "_out"] = T["state"][k][l] if half == 0 else T["sdump"][k]
    K.sob = T["stateb"][l] if half == 0 else T["sdumpb"]
    K.fb = Buf()
    win, wout, w1, w3, w2 = T["win"][l], T["wout"][l], T["w1"][l], T["w3"][l], T["w2"][l]
    pf, pfb, pt, ptb, y, yb, hmid, hmb, aT, ab = T["pf"], T["pfb"], T["pt"], T["ptb"], T["y"], T["yb"], T["hmid"], T["hmb"], T["aT"], T["ab"]
    identb, idbb = K.identb, K.idbb
    with contextlib.ExitStack() as st1:
        uT = sbuf(C, st1, "uT", [128, 16, NTOK], BF16); ub = Buf()
        pst = Ring([psum(C, st1, f"pst{i}", [128, 1024], BF16) for i in range(2)])
        psm = Ring([psum(C, st1, f"psm{i}", [128, 512], F32) for i in range(6)])
        with contextlib.ExitStack() as st:
            phase_norm(C, st, hin, hinb, K.g1t[l], K.g1b, uT, ub, pst, identb, idbb)
        P.barrier()
        with contextlib.ExitStack() as st:
            phase_proj(C, st, uT, ub, win, pf, pfb, pt, ptb, psm, prm["hist_out"], K.sob)
        P.barrier()
    with contextlib.ExitStack() as st1:
        K.flagE = sbuf(C, st1, "flagE", [128, 1])
        P.dma(SP, K.flagE[:], prm["flagE"], writes=[K.fb])
        K.ones = sbuf(C, st1, "ones", [64, TP]); K.onesb = Buf()
        K.mask_i = sbuf(C, st1, "mask_i", [64, 64]); K.mib = Buf()
        K.mask5 = sbuf(C, st1, "mask5", [64, 5, 64]); K.m5b = Buf()
        P.op(POOL, lambda e: e.memset(K.ones[:], 1.0), writes=[K.onesb])
        P.dma(SP, K.mask_i[:], T["const"]["mask_i"], writes=[K.mib]); P.dma(SP, K.mask5[:], T["const"]["mask5"], writes=[K.m5b])
        with contextlib.ExitStack() as st:
            gate_prepass(C, st, pt, ptb)
        P.barrier()
        with contextlib.ExitStack() as st:
            run_gens([mixer_gla_f32(C, st, pf, pfb, pt, ptb, y, yb, prm, K)])
        P.barrier()
        with contextlib.ExitStack() as st:
            run_gens([mixer_mlstm(C, st, pf, pfb, pt, ptb, y, yb, prm, K)])
        P.barrier()
        with contextlib.ExitStack() as st:
            mixer_rwkv3(C, st, pf, pfb, y, yb, prm, K)
        P.barrier()
    with contextlib.ExitStack() as st1:
        uT = sbuf(C, st1, "uT2", [128, 16, NTOK], BF16); ub = Buf()
        pst = Ring([psum(C, st1, f"pst{i}", [128, 1024], BF16) for i in range(2)])
        psm = Ring([psum(C, st1, f"psm{i}", [128, 512], F32) for i in range(6)])
        with contextlib.ExitStack() as st:
            phase_wout(C, st, y, yb, hin, hinb, hmid, hmb, wout, uT, ub, psm, pst, identb, idbb)
        P.barrier()
        with contextlib.ExitStack() as st:
            phase_norm(C, st, hmid, hmb, K.g2t[l], K.g1b, uT, ub, pst, identb, idbb)
        P.barrier()
        with contextlib.ExitStack() as st:
            phase_ffn1(C, st, uT, ub, w1, w3, aT, ab, psm)
        P.barrier()
    with contextlib.ExitStack() as st:
        psm = Ring([psum(C, st, f"psn{i}", [128, 512], F32) for i in range(6)])
        phase_ffn2(C, st, aT, ab, w2, hmid, hmb, hout, houtb, psm)
    P.barrier()
    if l == 1:
        with contextlib.ExitStack() as st:
            gft = sbuf(C, st, "gft2", [128, D]); gfb = Buf()
            P.dma(SP, gft[:], T["gf"], writes=[gfb])
            phase_final_norm(C, st, hout, houtb, gft, gfb, T["out"][half], T["outb"])
        P.barrier()


def build_fused(nlayers=2, halves=(0, 1)):
    nc = bass.Bass("TRN2", target_bir_lowering=False)
    C = Ctx(); C.nc = nc; C.P = Prog(nc); C.emit_out = True
    P = C.P
    dr = lambda n, s, dt=F32, kind="ExternalInput": nc.dram_tensor(n, s, dt, kind=kind).ap()
    T = {}
    xin = [dr("xE", [NTOK, D]), dr("xO", [NTOK, D])]
    T["win"] = dr("win", [2, 13, 128, 8192]); T["wout"] = dr("wout", [2, 4, 128, 8192])
    T["w1"] = dr("w1", [2, 11, 128, 8192]); T["w3"] = dr("w3", [2, 11, 128, 8192]); T["w2"] = dr("w2", [2, 4, 4, 128, 11 * 512])
    g1 = dr("g1", [2, 128, 16]); g2 = dr("g2", [2, 128, 16]); T["gf"] = dr("gf", [128, D])
    T["lp"] = {k: dr(k, [2] + PRM_SHAPES[k]) for k in LAYER_KEYS}
    T["const"] = {k: dr(k, PRM_SHAPES[k]) for k in ("mask_i", "mask5", "onehot")}
    T["flag1"] = dr("flag1", [128, 1]); T["flag0"] = dr("flag0", [128, 1])
    T["zstate"] = {k: dr("z_" + k, PRM_SHAPES[k + "_in"]) for k in STATE_KEYS}
    T["state"] = {k: dr("st_" + k, [2] + PRM_SHAPES[k + "_in"], F32, "Internal") for k in STATE_KEYS}
    T["sdump"] = {k: dr("sd_" + k, PRM_SHAPES[k + "_in"], F32, "Internal") for k in STATE_KEYS}
    T["stateb"] = [Buf(), Buf()]; T["sdumpb"] = Buf()
    T["pf"] = dr("pf", [NFMB * 128, TP], F32, "Internal"); T["pt"] = dr("pt", [NTOK, NTMC], F32, "Internal")
    T["y"] = dr("y", [NTOK, D], BF16, "Internal"); T["hmid"] = dr("hmid", [NTOK, D], F32, "Internal")
    T["aT"] = dr("aT", [5, 128, 44, 512], BF16, "Internal")
    for k in ("pfb", "ptb", "yb", "hmb", "ab", "outb"):
        T[k] = Buf()
    h1 = [dr("h1E", [NTOK, D], F32, "Internal"), dr("h1O", [NTOK, D], F32, "Internal")]
    h2 = [dr("h2E", [NTOK, D], F32, "Internal"), dr("h2O", [NTOK, D], F32, "Internal")]
    T["out"] = [dr("outE", [2048, D], F32, "ExternalOutput"), dr("outO", [2048, D], F32, "ExternalOutput")]
    xb = [Buf(), Buf()]; h1b = [Buf(), Buf()]; h2b = [Buf(), Buf()]
    T["hin"] = {(0, 0): (xin[0], xb[0]), (0, 1): (xin[1], xb[1]), (1, 0): (h1[0], h1b[0]), (1, 1): (h1[1], h1b[1])}
    T["hout"] = {(0, 0): (h1[0], h1b[0]), (0, 1): (h1[1], h1b[1]), (1, 0): (h2[0], h2b[0]), (1, 1): (h2[1], h2b[1])}
    K = Ctx()
    with contextlib.ExitStack() as st0:
        K.ident = sbuf(C, st0, "ident", [128, 128], F32); K.identb = sbuf(C, st0, "identb", [128, 128], BF16)
        K.g1t = [sbuf(C, st0, f"g1t{l}", [128, 16]) for l in range(2)]; K.g2t = [sbuf(C, st0, f"g2t{l}", [128, 16]) for l in range(2)]
        K.idb, K.idbb, K.g1b = Buf(), Buf(), Buf()
        P.op(POOL, lambda e: e.memset(K.ident[:], 1.0), writes=[K.idb])
        P.op(POOL, lambda e: e.affine_select(out=K.ident[:], in_=K.ident[:], pattern=[[-1, 128]], base=0, channel_multiplier=1,
                                             compare_op=ALU.is_equal, fill=0.0), reads=[K.idb], writes=[K.idb])
        P.op(POOL, lambda e: e.tensor_copy(out=K.identb[:], in_=K.ident[:]), reads=[K.idb], writes=[K.idbb])
        for l in range(2):
            P.dma(SP, K.g1t[l][:], g1[l], pwrites=[K.g1b]); P.dma(SP, K.g2t[l][:], g2[l], pwrites=[K.g1b])
        for l in range(nlayers):
            for half in halves:
                emit_half(C, K, T, l, half)
    P.finish([T["outb"], T["sdumpb"], T["stateb"][0], T["stateb"][1]])
    P.emit()
    C.counts = {e: (len(P.ops[e]), sum(1 for o in P.ops[e] if o.signal)) for e in ENGS}
    return nc, C


from concourse.bass_utils import run_bass_kernel_spmd

_PROG = {}


def kernel(**z):
    x = np.asarray(z["x"], np.float32)
    meta = np.asarray(z["meta_tokens"], np.float32)
    if "nc" not in _PROG:
        _PROG["nc"] = build_fused()[0]
    nc = _PROG["nc"]
    shared = {}
    shared.update(host_consts())
    shared["gf"] = np.ascontiguousarray(np.broadcast_to(np.asarray(z["norm_final"], np.float32), (128, D)))
    Ws = [host_layer_weights(z, l) for l in range(2)]
    for k in ("win", "wout", "w1", "w3", "w2", "g1", "g2"):
        shared[k] = np.stack([Ws[0][k], Ws[1][k]])
    Ps = [host_layer_params(z, l) for l in range(2)]
    for k in LAYER_KEYS:
        shared[k] = np.stack([Ps[0][k], Ps[1][k]])
    shared["flag1"] = np.ones((128, 1), np.float32)
    shared["flag0"] = np.zeros((128, 1), np.float32)
    for k in STATE_KEYS:
        shared["z_" + k] = np.zeros(PRM_SHAPES[k + "_in"], np.float32)
    in_maps = []
    for c in range(8):
        b = c % 4
        im = dict(shared)
        im["xE"] = np.ascontiguousarray(np.concatenate([meta, x[b, :2048]], 0))
        im["xO"] = np.ascontiguousarray(np.concatenate([meta, x[b, 2048:]], 0))
        in_maps.append(im)
    res = run_bass_kernel_spmd(nc, in_maps, core_ids=list(range(8))).results
    out = np.zeros((4, 4096, D), np.float32)
    for b in range(4):
        out[b, :2048] = np.asarray(res[b]["outE"], np.float32)
        out[b, 2048:] = np.asarray(res[b]["outO"], np.float32)
    return out
```

```python
import contextlib
import numpy as np
import concourse.bass as bass
import concourse.mybir as mybir

F32 = mybir.dt.float32
BF16 = mybir.dt.bfloat16
AF = mybir.ActivationFunctionType
ALU = mybir.AluOpType
AX = mybir.AxisListType

PE, ACT, DVE, POOL, SP = "pe", "act", "dve", "pool", "sp"
ENGS = [PE, ACT, DVE, POOL, SP]
SEG = 30000
NSLOT = 6


class Buf:
    __slots__ = ("name", "w", "ws", "rs")

    def __init__(self, name=""):
        self.name = name
        self.w = None
        self.ws = {}
        self.rs = {}


def _key(o):
    return (o.eng, o.slot if o.dma else None)


class Op:
    __slots__ = ("eng", "fn", "deps", "dma", "idx", "signal", "ev", "slot", "name")

    def __init__(self, eng, fn, dma):
        self.eng = eng
        self.fn = fn
        self.dma = dma
        self.deps = []
        self.signal = False
        self.ev = None
        self.slot = None
        self.name = ""


class Prog:
    def __init__(self, nc, same_engine_sync=True):
        self.nc = nc
        self.ops = {e: [] for e in ENGS}
        self.same = same_engine_sync
        self.ndma = {e: 0 for e in ENGS}
        self.final_deps = []
        self.pending_barrier = None
        self.scopes = False
        self.phase = ""

    def op(self, eng, fn, reads=(), writes=(), dma=False, name="", pwrites=()):
        o = Op(eng, fn, dma)
        o.name = getattr(self, "phase", "")
        if dma:
            o.slot = self.ndma[eng] % NSLOT
            self.ndma[eng] += 1
            o.signal = True
        o.idx = len(self.ops[eng])
        deps = []
        for r in reads:
            if r.w is not None:
                deps.append(r.w)
            deps.extend(r.ws.values())
        for w in writes:
            if w.w is not None:
                deps.append(w.w)
            deps.extend(w.ws.values())
            deps.extend(w.rs.values())
        for w in pwrites:
            if w.w is not None:
                deps.append(w.w)
            deps.extend(w.rs.values())
        if self.pending_barrier and self.pending_barrier.get(eng):
            deps.extend(self.pending_barrier[eng])
            self.pending_barrier[eng] = []
        best = {}
        for d in deps:
            if d is o:
                continue
            if d.eng == eng and not d.dma:
                if eng == PE or not self.same:
                    continue
            k = _key(d)
            if k not in best or best[k].idx < d.idx:
                best[k] = d
        for d in best.values():
            o.deps.append(d)
            d.signal = True
        for w in writes:
            w.w = o
            w.ws = {}
            w.rs = {}
        for w in pwrites:
            w.ws[_key(o)] = o
        for r in reads:
            r.rs[_key(o)] = o
        self.ops[eng].append(o)
        return o

    def dma(self, eng, out, in_, reads=(), writes=(), pwrites=(), **kw):
        return self.op(eng, lambda e: e.dma_start(out=out, in_=in_, **kw), reads, writes, dma=True, pwrites=pwrites)

    def finish(self, bufs):
        for b in bufs:
            for o in ([b.w] if b.w is not None else []) + list(b.ws.values()):
                self.final_deps.append(o)
                o.signal = True

    def emit(self):
        nc = self.nc
        with contextlib.ExitStack() as st:
            csem = {}
            for e in (PE, ACT, DVE, POOL):
                n = sum(1 for o in self.ops[e] if o.signal and not o.dma)
                nseg = n // SEG + 1
                csem[e] = [st.enter_context(nc.semaphore(f"c_{e}_{i}")) for i in range(nseg)]
            dsem = {}
            for e in (ACT, POOL, SP):
                if self.ndma[e] > 0:
                    dsem[e] = [st.enter_context(nc.semaphore(f"d_{e}_{i}")) for i in range(NSLOT)]
            for e in ENGS:
                cnt = 0
                dcur = [0] * NSLOT
                for o in self.ops[e]:
                    if o.dma:
                        prev = dcur[o.slot]
                        dcur[o.slot] += 16
                        o.ev = (dsem[e][o.slot], dcur[o.slot], prev)
                    elif o.signal:
                        seg, v = divmod(cnt, SEG)
                        o.ev = (csem[e][seg], v + 1, None)
                        cnt += 1
            block = st.enter_context(nc.Block())
            handles = {PE: block.tensor, ACT: block.scalar, DVE: block.vector,
                       POOL: block.gpsimd, SP: block.sync}
            for e in ENGS:
                ops = self.ops[e]
                fdeps = self.final_deps if e == SP else []
                if not ops and not fdeps:
                    continue

                def body(eng, ops=ops, fdeps=fdeps):
                    known = {}

                    def wait(sem, val):
                        k = id(sem)
                        if known.get(k, 0) >= val:
                            return
                        eng.wait_ge(sem, val)
                        known[k] = val

                    cur_ph, sid = None, None
                    for o in ops:
                        if self.scopes and o.name != cur_ph:
                            if cur_ph:
                                nc.leave_named_scope(cur_ph, sid, False)
                            cur_ph = o.name
                            if cur_ph:
                                sid, _ = nc.enter_named_scope(cur_ph, False)
                        for d in o.deps:
                            wait(d.ev[0], d.ev[1])
                        if o.dma and o.ev[2] > 0:
                            wait(o.ev[0], o.ev[2])
                        ins = o.fn(eng)
                        if o.dma:
                            ins.then_inc(o.ev[0], 16)
                        elif o.signal:
                            ins.then_inc(o.ev[0], 1)
                    if self.scopes and cur_ph:
                        nc.leave_named_scope(cur_ph, sid, False)
                    for d in fdeps:
                        wait(d.ev[0], d.ev[1])

                handles[e](body)


def _barrier(self):
    lasts = []
    for e in ENGS:
        ops = self.ops[e]
        if not ops:
            continue
        for o in reversed(ops):
            if not o.dma:
                lasts.append(o)
                break
        seen = set()
        for o in reversed(ops):
            if o.dma and o.slot not in seen:
                seen.add(o.slot)
                lasts.append(o)
            if len(seen) == NSLOT:
                break
    for o in lasts:
        o.signal = True
    self.pending_barrier = {e: list(lasts) for e in ENGS}


Prog.barrier = _barrier


class Ring:
    def __init__(self, tiles):
        self.tiles = tiles
        self.bufs = [Buf() for _ in tiles]
        self.i = 0

    def next(self):
        k = self.i % len(self.tiles)
        self.i += 1
        return self.tiles[k], self.bufs[k]


class _HP:
    pass
hp = _HP()


import numpy as np

A0, B0, C0 = 0, 1696, 3248


def fm_blocks():
    blks = []
    for seg in range(3):
        for i in range(4):
            blks.append(list(range(A0 + seg * 512 + i * 128, A0 + seg * 512 + (i + 1) * 128)))
    blks.append(list(range(A0 + 1536, A0 + 1600)))
    blks.append(list(range(A0 + 1600, A0 + 1696)))
    for seg in range(2):
        for i in range(2):
            blks.append(list(range(B0 + seg * 256 + i * 128, B0 + seg * 256 + (i + 1) * 128)))
    blks.append(list(range(B0 + 1024, B0 + 1040)))
    for seg in range(2):
        for i in range(4):
            blks.append(list(range(C0 + seg * 512 + i * 128, C0 + seg * 512 + (i + 1) * 128)))
    blks.append(list(range(C0 + 2048, C0 + 2056)))
    assert len(blks) == 28
    return blks


def tm_cols():
    cols = []
    cols += list(range(B0 + 512, B0 + 1024))
    cols += list(range(B0 + 1040, B0 + 1552))
    cols += list(range(C0 + 1024, C0 + 2048))
    cols += list(range(C0 + 2056, C0 + 3080))
    assert len(cols) == 3072
    return cols


def tile_k(W, ncol=512):
    K = W.shape[0]
    return np.ascontiguousarray(W.reshape(K // 128, 128, ncol).transpose(1, 0, 2).reshape(128, (K // 128) * ncol))


def prep_win(w):
    blks = fm_blocks()
    out = np.zeros((13, 128, 8192), np.float32)
    for wi in range(7):
        Wt = np.zeros((2048, 512), np.float32)
        for bi in range(4):
            cols = blks[wi * 4 + bi]
            Wt[:, bi * 128:bi * 128 + len(cols)] = w[:, cols]
        out[wi] = tile_k(Wt)
    tc = tm_cols()
    for ci in range(6):
        out[7 + ci] = tile_k(w[:, tc[ci * 512:(ci + 1) * 512]])
    return out


def prep_sq(w, ncb):
    return np.stack([tile_k(w[:, i * 512:(i + 1) * 512]) for i in range(ncb)])


def prep_w2(w):
    out = np.zeros((4, 4, 128, 11 * 512), np.float32)
    for cb in range(4):
        for pc in range(4):
            out[cb, pc] = tile_k(w[pc * 1408:(pc + 1) * 1408, cb * 512:(cb + 1) * 512])
    return out


def gT(g):
    return np.ascontiguousarray(g.reshape(16, 128).T)


for _n in ['fm_blocks','tm_cols','tile_k','prep_win','prep_sq','prep_w2','gT']:
    setattr(hp, _n, globals()[_n])


import contextlib

D = 2048
NTOK = 2064
NMETA = 16
TP = 2070
DFF = 5632
NFMB = 28
NTMC = 3072
EPS = 1e-6

TG = [(0, 16)] + [(16 + 512 * i, 512) for i in range(4)]
TT = [(0, 16)] + [(16 + 128 * i, 128) for i in range(16)]


def pfcol(t):
    return 3 + t if t < 16 else t + 6


class Ctx:
    pass


_uid = [0]


def sbuf(C, st, name, shape, dt=F32):
    _uid[0] += 1
    return st.enter_context(C.nc.sbuf_tensor(f"{name}_{_uid[0]}", shape, dt))


def psum(C, st, name, shape, dt=F32):
    _uid[0] += 1
    return st.enter_context(C.nc.psum_tensor(f"{name}_{_uid[0]}", shape, dt))


def make_wloader(C, st, n_stage=2, n_wb=2, stage_elems=8192):
    stage = Ring([sbuf(C, st, f"wst{i}", [128, stage_elems], F32) for i in range(n_stage)])
    return stage


def load_w(C, stage, dst_ap, dst_buf, src_ap, nelem, cast_eng=POOL):
    P = C.P
    stt, stb = stage.next()
    P.dma(SP, stt[:, 0:nelem], src_ap, writes=[stb])
    P.op(cast_eng, lambda e: e.tensor_copy(out=dst_ap, in_=stt[:, 0:nelem]), reads=[stb], writes=[dst_buf])


def phase_norm(C, st, hsrc, hbuf, gT, gbuf, uT, ubuf, ps_t, ident_bf, idbuf):
    P = C.P
    hring = Ring([sbuf(C, st, f"nh{i}", [128, D], F32) for i in range(2)])
    hnring = Ring([sbuf(C, st, f"nhn{i}", [128, D], BF16) for i in range(2)])
    junk = sbuf(C, st, "njunk", [128, D], BF16)
    jb = Buf()
    stat = Ring([sbuf(C, st, f"nst{i}", [128, 4], F32) for i in range(2)])
    mh = sbuf(C, st, "nmh", [128, 1], F32)
    mhb = Buf()
    P.op(POOL, lambda e: e.memset(mh[:], -0.5), writes=[mhb])
    for (t0, nt) in TT:
        ht, hb = hring.next()
        hn, hnb = hnring.next()
        s, sb_ = stat.next()
        P.dma(SP, ht[0:nt, :], hsrc[t0:t0 + nt, :], reads=[hbuf], writes=[hb])
        P.op(ACT, lambda e, ht=ht, s=s, nt=nt: e.activation(out=junk[0:nt, :], in_=ht[0:nt, :], func=AF.Square,
                                                            accum_out=s[0:nt, 0:1]), reads=[hb], writes=[jb, sb_])
        P.op(DVE, lambda e, s=s, nt=nt: e.tensor_scalar(out=s[0:nt, 1:2], in0=s[0:nt, 0:1], scalar1=1.0 / D, scalar2=EPS,
                                                        op0=ALU.mult, op1=ALU.add), reads=[sb_], writes=[sb_])
        P.op(POOL, lambda e, s=s, nt=nt: e.tensor_tensor(out=s[0:nt, 2:3], in0=s[0:nt, 1:2], in1=mh[0:nt, :], op=ALU.pow),
             reads=[sb_, mhb], writes=[sb_])
        P.op(DVE, lambda e, ht=ht, hn=hn, s=s, nt=nt: e.tensor_scalar(out=hn[0:nt, :], in0=ht[0:nt, :], scalar1=s[0:nt, 2:3],
                                                                      scalar2=None, op0=ALU.mult), reads=[hb, sb_], writes=[hnb])
        for half in range(2):
            pt_, ptb = ps_t.next()
            for k in range(8):
                kb = half * 8 + k
                P.op(PE, lambda e, pt_=pt_, hn=hn, kb=kb, k=k, nt=nt: e.transpose(
                    out=pt_[:, k * 128:k * 128 + nt], in_=hn[0:nt, kb * 128:(kb + 1) * 128], identity=ident_bf[0:nt, 0:nt]),
                    reads=[hnb, idbuf], writes=[ptb])
            eng = DVE if half == 0 else POOL
            if half == 0:
                P.op(DVE, lambda e, pt_=pt_, nt=nt, t0=t0, half=half: e.tensor_tensor(
                    out=uT[:, half * 8:half * 8 + 8, t0:t0 + nt],
                    in0=pt_[:].rearrange("p (k t) -> p k t", k=8)[:, :, 0:nt],
                    in1=gT[:, half * 8:half * 8 + 8].unsqueeze(2).broadcast_to([128, 8, nt]), op=ALU.mult),
                    reads=[ptb, gbuf], pwrites=[ubuf])
            else:
                P.op(DVE, lambda e, pt_=pt_, nt=nt, t0=t0, half=half: e.tensor_tensor(
                    out=uT[:, half * 8:half * 8 + 8, t0:t0 + nt],
                    in0=pt_[:].rearrange("p (k t) -> p k t", k=8)[:, :, 0:nt],
                    in1=gT[:, half * 8:half * 8 + 8].unsqueeze(2).broadcast_to([128, 8, nt]), op=ALU.mult),
                    reads=[ptb, gbuf], pwrites=[ubuf])


def phase_proj(C, st, uT, ubuf, w_dram, pf, pfbuf, pt, ptbuf, ps_mm, hist_out=None, hob=None):
    P = C.P
    stage = make_wloader(C, st)
    wb = Ring([sbuf(C, st, f"pwb{i}", [128, 16, 512], BF16) for i in range(2)])
    ev = Ring([sbuf(C, st, f"pev{i}", [128, 512], F32) for i in range(4)])
    cnt = 0
    for wi in range(13):
        wt, wbuf = wb.next()
        load_w(C, stage, wt[:].rearrange("p k c -> p (k c)"), wbuf, w_dram[wi], 8192)
        if wi < 7:
            for bi in range(4):
                blk = wi * 4 + bi
                for (t0, nt) in TG:
                    pm, pmb = ps_mm.next()
                    for kb in range(16):
                        P.op(PE, lambda e, pm=pm, wt=wt, kb=kb, bi=bi, t0=t0, nt=nt: e.matmul(
                            pm[:, 0:nt], lhsT=wt[:, kb, bi * 128:(bi + 1) * 128], rhs=uT[:, kb, t0:t0 + nt],
                            start=(kb == 0), stop=(kb == 15)), reads=[wbuf, ubuf], writes=[pmb])
                    et, eb = ev.next()
                    if cnt % 2 == 0:
                        P.op(ACT, lambda e, et=et, pm=pm, nt=nt: e.activation(out=et[:, 0:nt], in_=pm[:, 0:nt], func=AF.Copy),
                             reads=[pmb], writes=[eb])
                    else:
                        P.op(DVE, lambda e, et=et, pm=pm, nt=nt: e.tensor_copy(out=et[:, 0:nt], in_=pm[:, 0:nt]),
                             reads=[pmb], writes=[eb])
                    cnt += 1
                    c0 = pfcol(t0)
                    P.dma(POOL, pf[blk * 128:(blk + 1) * 128, c0:c0 + nt], et[:, 0:nt], reads=[eb], pwrites=[pfbuf])
                    if hist_out is not None and t0 + nt == NTOK:
                        P.dma(POOL, hist_out[blk * 128:(blk + 1) * 128, :], et[:, nt - 3:nt], reads=[eb], pwrites=[hob])
        else:
            ci = wi - 7
            for (t0, nt) in TT:
                pm, pmb = ps_mm.next()
                for kb in range(16):
                    P.op(PE, lambda e, pm=pm, wt=wt, kb=kb, t0=t0, nt=nt: e.matmul(
                        pm[0:nt, :], lhsT=uT[:, kb, t0:t0 + nt], rhs=wt[:, kb, :],
                        start=(kb == 0), stop=(kb == 15)), reads=[wbuf, ubuf], writes=[pmb])
                et, eb = ev.next()
                gfn = None
                if gfn is not None:
                    P.op(ACT, lambda e, et=et, pm=pm, nt=nt, gfn=gfn: e.activation(out=et[0:nt, :], in_=pm[0:nt, :], func=gfn),
                         reads=[pmb], writes=[eb])
                elif cnt % 2 == 0:
                    P.op(ACT, lambda e, et=et, pm=pm, nt=nt: e.activation(out=et[0:nt, :], in_=pm[0:nt, :], func=AF.Copy),
                         reads=[pmb], writes=[eb])
                else:
                    P.op(DVE, lambda e, et=et, pm=pm, nt=nt: e.tensor_copy(out=et[0:nt, :], in_=pm[0:nt, :]),
                         reads=[pmb], writes=[eb])
                cnt += 1
                P.dma(POOL, pt[t0:t0 + nt, ci * 512:(ci + 1) * 512], et[0:nt, :], reads=[eb], pwrites=[ptbuf])


def phase_wout(C, st, y, ybuf, hsrc, hbuf, hdst, hdbuf, w_dram, uT, ubuf, ps_mm, ps_t, ident_bf, idbuf):
    P = C.P
    yr = Ring([sbuf(C, st, f"oy{i}", [128, D], BF16) for i in range(2)])
    for (t0, nt) in TT:
        yt, yb = yr.next()
        P.dma(SP, yt[0:nt, :], y[t0:t0 + nt, :], reads=[ybuf], writes=[yb])
        for half in range(2):
            pt_, ptb = ps_t.next()
            for k in range(8):
                kb = half * 8 + k
                P.op(PE, lambda e, pt_=pt_, yt=yt, kb=kb, k=k, nt=nt: e.transpose(
                    out=pt_[:, k * 128:k * 128 + nt], in_=yt[0:nt, kb * 128:(kb + 1) * 128], identity=ident_bf[0:nt, 0:nt]),
                    reads=[yb, idbuf], writes=[ptb])
            eng = ACT if half == 0 else DVE
            if half == 0:
                P.op(ACT, lambda e, pt_=pt_, nt=nt, t0=t0, half=half: e.activation(
                    out=uT[:, half * 8:half * 8 + 8, t0:t0 + nt],
                    in_=pt_[:].rearrange("p (k t) -> p k t", k=8)[:, :, 0:nt], func=AF.Copy), reads=[ptb], pwrites=[ubuf])
            else:
                P.op(DVE, lambda e, pt_=pt_, nt=nt, t0=t0, half=half: e.tensor_copy(
                    out=uT[:, half * 8:half * 8 + 8, t0:t0 + nt],
                    in_=pt_[:].rearrange("p (k t) -> p k t", k=8)[:, :, 0:nt]), reads=[ptb], pwrites=[ubuf])
    stage = make_wloader(C, st)
    wb = Ring([sbuf(C, st, f"owb{i}", [128, 16, 512], BF16) for i in range(2)])
    hr = Ring([sbuf(C, st, f"ohr{i}", [128, 512], F32) for i in range(3)])
    ev = Ring([sbuf(C, st, f"oev{i}", [128, 512], F32) for i in range(3)])
    nxt = wb.next()
    load_w(C, stage, nxt[0][:].rearrange("p k c -> p (k c)"), nxt[1], w_dram[0], 8192)
    for ci in range(4):
        wt, wbuf = nxt
        if ci + 1 < 4:
            nxt = wb.next()
            load_w(C, stage, nxt[0][:].rearrange("p k c -> p (k c)"), nxt[1], w_dram[ci + 1], 8192)
        for (t0, nt) in TT:
            ho, hob = hr.next()
            P.dma(SP, ho[0:nt, :], hsrc[t0:t0 + nt, ci * 512:(ci + 1) * 512], reads=[hbuf], writes=[hob])
            pm, pmb = ps_mm.next()
            for kb in range(16):
                P.op(PE, lambda e, pm=pm, wt=wt, kb=kb, t0=t0, nt=nt: e.matmul(
                    pm[0:nt, :], lhsT=uT[:, kb, t0:t0 + nt], rhs=wt[:, kb, :],
                    start=(kb == 0), stop=(kb == 15)), reads=[wbuf, ubuf], writes=[pmb])
            et, eb = ev.next()
            P.op(DVE, lambda e, et=et, pm=pm, ho=ho, nt=nt: e.tensor_tensor(out=et[0:nt, :], in0=pm[0:nt, :], in1=ho[0:nt, :],
                                                                            op=ALU.add), reads=[pmb, hob], writes=[eb])
            P.dma(POOL, hdst[t0:t0 + nt, ci * 512:(ci + 1) * 512], et[0:nt, :], reads=[eb], pwrites=[hdbuf])


def phase_ffn1(C, st, uT, ubuf, w1_dram, w3_dram, aT, abuf, ps_mm):
    P = C.P
    stage = make_wloader(C, st)
    w1b = Ring([sbuf(C, st, f"f1w{i}", [128, 16, 512], BF16) for i in range(2)])
    w3b = Ring([sbuf(C, st, f"f3w{i}", [128, 16, 512], BF16) for i in range(2)])
    sg = Ring([sbuf(C, st, f"fsg{i}", [128, 512], F32) for i in range(3)])
    av = Ring([sbuf(C, st, f"fav{i}", [128, 512], BF16) for i in range(3)])
    for gi in range(11):
        w1t, w1buf = w1b.next()
        w3t, w3buf = w3b.next()
        load_w(C, stage, w1t[:].rearrange("p k c -> p (k c)"), w1buf, w1_dram[gi], 8192)
        load_w(C, stage, w3t[:].rearrange("p k c -> p (k c)"), w3buf, w3_dram[gi], 8192)
        for bi in range(4):
            j = gi * 4 + bi
            for gidx, (t0, nt) in enumerate(TG):
                pa, pab = ps_mm.next()
                for kb in range(16):
                    P.op(PE, lambda e, pa=pa, w1t=w1t, kb=kb, bi=bi, t0=t0, nt=nt: e.matmul(
                        pa[:, 0:nt], lhsT=w1t[:, kb, bi * 128:(bi + 1) * 128], rhs=uT[:, kb, t0:t0 + nt],
                        start=(kb == 0), stop=(kb == 15)), reads=[w1buf, ubuf], writes=[pab])
                pb_, pbb = ps_mm.next()
                for kb in range(16):
                    P.op(PE, lambda e, pb_=pb_, w3t=w3t, kb=kb, bi=bi, t0=t0, nt=nt: e.matmul(
                        pb_[:, 0:nt], lhsT=w3t[:, kb, bi * 128:(bi + 1) * 128], rhs=uT[:, kb, t0:t0 + nt],
                        start=(kb == 0), stop=(kb == 15)), reads=[w3buf, ubuf], writes=[pbb])
                s, sb_ = sg.next()
                a, ab_ = av.next()
                P.op(ACT, lambda e, s=s, pa=pa, nt=nt: e.activation(out=s[:, 0:nt], in_=pa[:, 0:nt], func=AF.Silu),
                     reads=[pab], writes=[sb_])
                P.op(DVE, lambda e, a=a, s=s, pb_=pb_, nt=nt: e.tensor_tensor(out=a[:, 0:nt], in0=pb_[:, 0:nt], in1=s[:, 0:nt],
                                                                              op=ALU.mult), reads=[pbb, sb_], writes=[ab_])
                P.dma(POOL, aT[gidx, :, j, 0:nt], a[:, 0:nt], reads=[ab_], pwrites=[abuf])


def phase_ffn2(C, st, aT, abuf, w2_dram, hsrc, hbuf, hdst, hdbuf, ps_mm):
    P = C.P
    stage = Ring([sbuf(C, st, f"gst{i}", [128, 11 * 512], F32) for i in range(2)])
    w2b = Ring([sbuf(C, st, f"gw{i}", [128, 44, 512], BF16) for i in range(1)])
    ar = Ring([sbuf(C, st, f"gar{i}", [128, 44, 512], BF16) for i in range(2)])
    hr = Ring([sbuf(C, st, f"ghr{i}", [128, 512], F32) for i in range(3)])
    ev = Ring([sbuf(C, st, f"gev{i}", [128, 512], F32) for i in range(3)])
    for cb in range(4):
        wt, wbuf = w2b.next()
        for pc in range(4):
            load_w(C, stage, wt[:, pc * 11:(pc + 1) * 11, :].rearrange("p k c -> p (k c)"), wbuf, w2_dram[cb, pc], 11 * 512)
        for gidx, (g0, gn) in enumerate(TG):
            at, atb = ar.next()
            P.dma(SP, at[:, :, 0:gn], aT[gidx, :, :, 0:gn], reads=[abuf], writes=[atb])
            for s0 in range(0, gn, 128):
                nt = min(128, gn - s0)
                t0 = g0 + s0
                ho, hob = hr.next()
                P.dma(SP, ho[0:nt, :], hsrc[t0:t0 + nt, cb * 512:(cb + 1) * 512], reads=[hbuf], writes=[hob])
                pm, pmb = ps_mm.next()
                for j in range(44):
                    P.op(PE, lambda e, pm=pm, at=at, wt=wt, j=j, s0=s0, nt=nt: e.matmul(
                        pm[0:nt, :], lhsT=at[:, j, s0:s0 + nt], rhs=wt[:, j, :],
                        start=(j == 0), stop=(j == 43)), reads=[wbuf, atb], writes=[pmb])
                et, eb = ev.next()
                P.op(DVE, lambda e, et=et, pm=pm, ho=ho, nt=nt: e.tensor_tensor(out=et[0:nt, :], in0=pm[0:nt, :], in1=ho[0:nt, :],
                                                                                op=ALU.add), reads=[pmb, hob], writes=[eb])
                P.dma(POOL, hdst[t0:t0 + nt, cb * 512:(cb + 1) * 512], et[0:nt, :], reads=[eb], pwrites=[hdbuf])


def phase_final_norm(C, st, hsrc, hbuf, gbc, gbcb, out, obuf):
    P = C.P
    hring = Ring([sbuf(C, st, f"zh{i}", [128, D], F32) for i in range(2)])
    oring = Ring([sbuf(C, st, f"zo{i}", [128, D], F32) for i in range(2)])
    junk = sbuf(C, st, "zjunk", [128, D], BF16)
    jb = Buf()
    stat = Ring([sbuf(C, st, f"zst{i}", [128, 4], F32) for i in range(2)])
    mh = sbuf(C, st, "zmh", [128, 1], F32)
    mhb = Buf()
    P.op(POOL, lambda e: e.memset(mh[:], -0.5), writes=[mhb])
    for (t0, nt) in TT[1:]:
        ht, hb = hring.next()
        ot, ob = oring.next()
        s, sb_ = stat.next()
        P.dma(SP, ht[0:nt, :], hsrc[t0:t0 + nt, :], reads=[hbuf], writes=[hb])
        P.op(ACT, lambda e, ht=ht, s=s, nt=nt: e.activation(out=junk[0:nt, :], in_=ht[0:nt, :], func=AF.Square,
                                                            accum_out=s[0:nt, 0:1]), reads=[hb], writes=[jb, sb_])
        P.op(DVE, lambda e, s=s, nt=nt: e.tensor_scalar(out=s[0:nt, 1:2], in0=s[0:nt, 0:1], scalar1=1.0 / D, scalar2=EPS,
                                                        op0=ALU.mult, op1=ALU.add), reads=[sb_], writes=[sb_])
        P.op(POOL, lambda e, s=s, nt=nt: e.tensor_tensor(out=s[0:nt, 2:3], in0=s[0:nt, 1:2], in1=mh[0:nt, :], op=ALU.pow),
             reads=[sb_, mhb], writes=[sb_])
        P.op(DVE, lambda e, ht=ht, ot=ot, s=s, nt=nt: e.scalar_tensor_tensor(
            out=ot[0:nt, :], in0=ht[0:nt, :], scalar=s[0:nt, 2:3], in1=gbc[0:nt, :], op0=ALU.mult, op1=ALU.mult),
            reads=[hb, sb_, gbcb], writes=[ob])
        P.dma(POOL, out[t0 - 16:t0 - 16 + nt, :], ot[0:nt, :], reads=[ob], pwrites=[obuf])


import contextlib

SEGS = [(3, 16, 0), (22, 2048, 16)]
LDK = 0.6065306597126334


def chunks_of(seg):
    c0, ncol, t0 = seg
    if ncol == 16:
        return [(c0, 16, t0)]
    return [(c0 + 64 * i, 64, t0 + 64 * i) for i in range(ncol // 64)]


def fix_gap(C, x, xb, hist_src, flagE, fb, tmp_ring, rows):
    P = C.P
    t, tb = tmp_ring.next()
    P.dma(SP, t[0:rows, 0:3], hist_src, writes=[tb])
    P.op(DVE, lambda e: e.scalar_tensor_tensor(out=x[0:rows, 19:22], in0=x[0:rows, 16:19], scalar=flagE[0:rows, 0:1],
                                               in1=t[0:rows, 0:3], op0=ALU.mult, op1=ALU.add), reads=[xb, tb, fb], writes=[xb])


def chunk_rel(C, out, ob, src, sb_, rows):
    P = C.P
    P.op(DVE, lambda e: e.tensor_copy(out=out[0:rows, 3:19], in_=src[0:rows, 3:19]), reads=[sb_], writes=[ob])
    P.op(DVE, lambda e: e.tensor_copy(out=out[0:rows, 22:86], in_=src[0:rows, 22:86]), reads=[sb_], writes=[ob])
    o3 = out[0:rows, 86:2070].rearrange("p (c j) -> p c j", j=64)
    s3 = src[0:rows, 86:2070].rearrange("p (c j) -> p c j", j=64)
    pv = src[0:rows, 22:2006].rearrange("p (c j) -> p c j", j=64)[:, :, 63:64].broadcast_to([rows, 31, 64])
    P.op(DVE, lambda e: e.tensor_tensor(out=o3, in0=s3, in1=pv, op=ALU.subtract), reads=[sb_], writes=[ob])


def gate_prepass(C, st, pt, ptb):
    P = C.P
    r = Ring([sbuf(C, st, f"gp{i}", [128, 1536], F32) for i in range(3)])
    for (t0, nt) in TT:
        t, tb = r.next()
        P.dma(SP, t[0:nt, 0:512], pt[t0:t0 + nt, 512:1024], reads=[ptb], writes=[tb])
        P.dma(SP, t[0:nt, 512:1536], pt[t0:t0 + nt, 2048:3072], reads=[ptb], writes=[tb])
        P.op(ACT, lambda e, t=t, nt=nt: e.activation(out=t[0:nt, 0:512], in_=t[0:nt, 0:512], func=AF.Silu), reads=[tb], writes=[tb])
        P.op(ACT, lambda e, t=t, nt=nt: e.activation(out=t[0:nt, 512:1536], in_=t[0:nt, 512:1536], func=AF.Sigmoid),
             reads=[tb], writes=[tb])
        P.dma(POOL, pt[t0:t0 + nt, 512:1024], t[0:nt, 0:512], reads=[tb], writes=[ptb])
        P.dma(POOL, pt[t0:t0 + nt, 2048:3072], t[0:nt, 512:1536], reads=[tb], writes=[ptb])


def mixer_gla(C, st, pf, pfb, pt, ptb, y, yb, prm, K, npsA=2, npsB=6):
    P = C.P
    ones, onesb, ident, idb, mask_i, mib, flagE, fb = K.ones, K.onesb, K.ident, K.idb, K.mask_i, K.mib, K.flagE, K.fb
    a2 = sbuf(C, st, "ga2", [32, 256]); a2b = Buf()
    P.op(DVE, lambda e: e.memset(a2[:], 0.0), writes=[a2b])
    nab = sbuf(C, st, "gnab", [64, 4]); nabb = Buf()
    nbc = sbuf(C, st, "gnbc", [64, 128]); nbcb = Buf()
    P.dma(SP, a2[0:16, :], prm["gla_a2"], reads=[a2b], writes=[a2b])
    P.dma(SP, nab[:], prm["gla_ab"], writes=[nabb])
    P.op(DVE, lambda e: e.tensor_scalar(out=nab[:], in0=nab[:], scalar1=-1.0, scalar2=None, op0=ALU.mult), reads=[nabb], writes=[nabb])
    P.dma(SP, nbc[:], prm["gla_normbc"], writes=[nbcb])
    xa = sbuf(C, st, "gxa", [32, TP]); xab = Buf()
    P.dma(SP, xa[:], pf[18 * 128:18 * 128 + 32, :], reads=[pfb], writes=[xab])
    q = sbuf(C, st, "gq", [64, TP]); k = sbuf(C, st, "gk", [64, TP]); sp = sbuf(C, st, "gsp", [64, TP])
    spc = sbuf(C, st, "gspc", [64, TP]); e1 = sbuf(C, st, "ge1", [64, TP]); e2 = sbuf(C, st, "ge2", [64, TP])
    qb, kb_, spb, spcb, e1b, e2b = [Buf() for _ in range(6)]
    S = sbuf(C, st, "gS", [64, 128]); Sb = Buf()
    S16 = sbuf(C, st, "gS16", [64, 128], BF16); S16b = Buf()
    qbf = sbuf(C, st, "gqbf", [64, TP], BF16); kbf = sbuf(C, st, "gkbf", [64, TP], BF16); qbfb, kbfb = Buf(), Buf()
    v16r = Ring([sbuf(C, st, f"gv16{i}", [64, 128], BF16) for i in range(3)])
    Sin = sbuf(C, st, "gSin", [64, 128]); Sinb = Buf()
    psA = Ring([psum(C, st, f"gpa{i}", [128, 512], F32) for i in range(npsA)])
    psB = Ring([psum(C, st, f"gpb{i}", [128, 512], F32) for i in range(npsB)])
    vr = Ring([sbuf(C, st, f"gv{i}", [64, 256], F32) for i in range(3)])
    ktr = Ring([sbuf(C, st, f"gkt{i}", [64, 64], BF16) for i in range(2)])
    scr = Ring([sbuf(C, st, f"gsc{i}", [64, 64], BF16) for i in range(2)])
    str_ = Ring([sbuf(C, st, f"gst{i}", [64, 4], F32) for i in range(2)])
    junk = sbuf(C, st, "gjunk", [64, 128], F32); jb = Buf()
    mh = sbuf(C, st, "gmh", [64, 1], F32); mhb = Buf()
    P.op(POOL, lambda e: e.memset(mh[:], -0.5), writes=[mhb])
    t1r = Ring([sbuf(C, st, f"gt1{i}", [64, 128], F32) for i in range(2)])
    yor = Ring([sbuf(C, st, f"gyo{i}", [64, 128], BF16) for i in range(2)])
    for h in range(4):
        r0 = (14 + h // 2) * 128 + (h % 2) * 64
        r1 = (16 + h // 2) * 128 + (h % 2) * 64
        P.dma(SP, q[:], pf[r0:r0 + 64, :], reads=[pfb], writes=[qb])
        P.dma(SP, k[:], pf[r1:r1 + 64, :], reads=[pfb], writes=[kb_])
        for c0 in range(3, TP, 512):
            n = min(512, TP - c0)
            pa, pab = psA.next()
            P.op(PE, lambda e, pa=pa, c0=c0, n=n, h=h: e.matmul(pa[0:64, 0:n], lhsT=a2[0:32, h * 64:(h + 1) * 64], rhs=xa[0:32, c0:c0 + n],
                                                               start=True, stop=True), reads=[a2b, xab], writes=[pab])
            P.op(ACT, lambda e, pa=pa, c0=c0, n=n, h=h: e.activation(out=sp[:, c0:c0 + n], in_=pa[0:64, 0:n], func=AF.Exp, scale=-1.0,
                                                                    bias=nab[:, h:h + 1]), reads=[pab, nabb], writes=[spb])
        P.op(ACT, lambda e: e.activation(out=sp[:, 3:TP], in_=sp[:, 3:TP], func=AF.Ln, bias=1.0), reads=[spb], writes=[spb])
        for (c0, ncol, t0) in SEGS:
            P.op(DVE, lambda e, c0=c0, ncol=ncol: e.tensor_tensor_scan(out=spc[:, c0:c0 + ncol], data0=ones[0:64, c0:c0 + ncol],
                                                                       data1=sp[:, c0:c0 + ncol], initial=0.0, op0=ALU.mult, op1=ALU.add),
                 reads=[spb, onesb], writes=[spcb])
        chunk_rel(C, sp, spb, spc, spcb, 64)
        P.op(ACT, lambda e: e.activation(out=e1[:, 3:TP], in_=sp[:, 3:TP], func=AF.Exp, scale=-1.0 / 16), reads=[spb], writes=[e1b])
        P.op(ACT, lambda e: e.activation(out=e2[:, 3:TP], in_=sp[:, 3:TP], func=AF.Exp, scale=1.0 / 16), reads=[spb], writes=[e2b])
        P.op(DVE, lambda e: e.scalar_tensor_tensor(out=q[:, 3:TP], in0=q[:, 3:TP], scalar=0.125, in1=e1[:, 3:TP], op0=ALU.mult,
                                                   op1=ALU.mult), reads=[qb, e1b], writes=[qb])
        P.op(DVE, lambda e: e.tensor_tensor(out=k[:, 3:TP], in0=k[:, 3:TP], in1=e2[:, 3:TP], op=ALU.mult), reads=[kb_, e2b], writes=[kb_])
        P.op(ACT, lambda e: e.activation(out=qbf[:, 3:TP], in_=q[:, 3:TP], func=AF.Copy), reads=[qb], writes=[qbfb])
        P.op(ACT, lambda e: e.activation(out=kbf[:, 3:TP], in_=k[:, 3:TP], func=AF.Copy), reads=[kb_], writes=[kbfb])
        P.op(DVE, lambda e: e.memset(S[:], 0.0), writes=[Sb])
        P.op(DVE, lambda e: e.memset(S16[:], 0.0), writes=[S16b])
        for si, seg in enumerate(SEGS):
            if si == 1:
                P.dma(SP, Sin[:], prm["sB_in"][h], writes=[Sinb])
                P.op(DVE, lambda e: e.scalar_tensor_tensor(out=S[:], in0=S[:], scalar=flagE[0:64, 0:1], in1=Sin[:], op0=ALU.mult,
                                                           op1=ALU.add), reads=[Sb, Sinb, fb], writes=[Sb])
                P.op(ACT, lambda e: e.activation(out=S16[:], in_=S[:], func=AF.Copy), reads=[Sb], writes=[S16b])
            for (c0, n, t0) in chunks_of(seg):
                vt, vb = vr.next()
                P.dma(SP, vt[0:n, 0:128], pt[t0:t0 + n, h * 128:(h + 1) * 128], reads=[ptb], writes=[vb])
                P.dma(SP, vt[0:n, 128:256], pt[t0:t0 + n, 512 + h * 128:512 + (h + 1) * 128], reads=[ptb], writes=[vb])
                pb, pbb = psB.next()
                P.op(PE, lambda e, pb=pb, c0=c0, n=n: e.transpose(out=pb[0:n, 0:64], in_=k[:, c0:c0 + n], identity=ident[0:64, 0:64]),
                     reads=[kb_, idb], writes=[pbb])
                P.op(PE, lambda e, pb=pb, c0=c0, n=n: e.matmul(pb[0:n, 64:64 + n], lhsT=kbf[:, c0:c0 + n], rhs=qbf[:, c0:c0 + n], start=True, stop=True),
                     reads=[kbfb, qbfb], writes=[pbb])
                v16, v16b = v16r.next()
                P.op(ACT, lambda e, v16=v16, vt=vt, n=n: e.activation(out=v16[0:n, :], in_=vt[0:n, 0:128], func=AF.Copy), reads=[vb], writes=[v16b])
                kt, ktb = ktr.next()
                sc, scb = scr.next()
                P.op(ACT, lambda e, kt=kt, pb=pb, n=n: e.activation(out=kt[0:n, :], in_=pb[0:n, 0:64], func=AF.Copy), reads=[pbb], writes=[ktb])
                P.op(DVE, lambda e, sc=sc, pb=pb, n=n: e.tensor_tensor(out=sc[0:n, 0:n], in0=pb[0:n, 64:64 + n], in1=mask_i[0:n, 0:n], op=ALU.mult),
                     reads=[pbb, mib], writes=[scb])
                po, pob = psB.next()
                P.op(PE, lambda e, po=po, c0=c0, n=n: e.matmul(po[0:n, 0:128], lhsT=qbf[:, c0:c0 + n], rhs=S16[:, :], start=True, stop=False),
                     reads=[qbfb, S16b], writes=[pob])
                P.op(PE, lambda e, po=po, sc=sc, v16=v16, n=n: e.matmul(po[0:n, 0:128], lhsT=sc[0:n, 0:n], rhs=v16[0:n, :], start=False, stop=True),
                     reads=[scb, v16b], writes=[pob])
                pc, pcb = psB.next()
                P.op(PE, lambda e, pc=pc, kt=kt, v16=v16, n=n: e.matmul(pc[0:64, 0:128], lhsT=kt[0:n, 0:64], rhs=v16[0:n, :], start=True, stop=True),
                     reads=[ktb, v16b], writes=[pcb])
                ce = c0 + n - 1
                P.op(DVE, lambda e, pc=pc: e.tensor_tensor(out=S[:], in0=S[:], in1=pc[0:64, 0:128], op=ALU.add), reads=[pcb, Sb], writes=[Sb])
                P.op(DVE, lambda e, ce=ce: e.tensor_scalar(out=S[:], in0=S[:], scalar1=e1[:, ce:ce + 1], scalar2=None, op0=ALU.mult),
                     reads=[Sb, e1b], writes=[Sb])
                P.op(ACT, lambda e: e.activation(out=S16[:], in_=S[:], func=AF.Copy), reads=[Sb], writes=[S16b])
                if C.emit_out and not False:
                    s_, sb2 = str_.next()
                    t1, t1b = t1r.next()
                    yo, yob = yor.next()
                    P.op(ACT, lambda e, t1=t1, po=po, n=n: e.activation(out=t1[0:n, :], in_=po[0:n, 0:128], func=AF.Copy), reads=[pob], writes=[t1b])
                    P.op(DVE, lambda e, t1=t1, n=n: e.tensor_tensor(out=junk[0:n, :], in0=t1[0:n, :], in1=t1[0:n, :], op=ALU.mult), reads=[t1b], writes=[jb])
                    P.op(DVE, lambda e, s_=s_, n=n: e.tensor_reduce(out=s_[0:n, 0:1], in_=junk[0:n, :], axis=AX.X, op=ALU.add), reads=[jb], writes=[sb2])
                    P.op(DVE, lambda e, s_=s_, n=n: e.tensor_scalar(out=s_[0:n, 1:2], in0=s_[0:n, 0:1], scalar1=1.0 / 128, scalar2=EPS, op0=ALU.mult,
                                                                   op1=ALU.add), reads=[sb2], writes=[sb2])
                    P.op(POOL, lambda e, s_=s_, n=n: e.tensor_tensor(out=s_[0:n, 2:3], in0=s_[0:n, 1:2], in1=mh[0:n, :], op=ALU.pow),
                         reads=[sb2, mhb], writes=[sb2])
                    P.op(DVE, lambda e, t1=t1, s_=s_, n=n: e.scalar_tensor_tensor(out=t1[0:n, :], in0=t1[0:n, :], scalar=s_[0:n, 2:3],
                                                                               in1=nbc[0:n, :], op0=ALU.mult, op1=ALU.mult),
                         reads=[sb2, nbcb, t1b], writes=[t1b])
                    P.op(DVE, lambda e, yo=yo, t1=t1, vt=vt, n=n: e.tensor_tensor(out=yo[0:n, :], in0=t1[0:n, :], in1=vt[0:n, 128:256], op=ALU.mult),
                         reads=[t1b, vb], writes=[yob])
                    P.dma(SP, y[t0:t0 + n, 512 + h * 128:512 + (h + 1) * 128], yo[0:n, :], reads=[yob], pwrites=[yb])
                yield
        P.dma(POOL, prm["sB_out"][h], S[:], reads=[Sb], pwrites=[K.sob])


def mixer_mlstm(C, st, pf, pfb, pt, ptb, y, yb, prm, K, npsA=4, npsG=2):
    P = C.P
    ones, onesb, ident, idb, mask_i, mib, flagE, fb = K.ones, K.onesb, K.ident, K.idb, K.mask_i, K.mib, K.flagE, K.fb
    cw = sbuf(C, st, "mcw", [128, 8, 4]); cb = sbuf(C, st, "mcb", [128, 8]); cwb = Buf()
    ib = sbuf(C, st, "mib", [4, 2]); fbb = sbuf(C, st, "mfb", [4, 2]); gbb = Buf()
    nbc = sbuf(C, st, "mnbc", [64, 1024]); nbcb = Buf()
    oh = sbuf(C, st, "moh", [4, 4, 128]); ohb = Buf()
    P.dma(SP, cw[:], prm["ml_cw"], writes=[cwb]); P.dma(SP, cb[:], prm["ml_cb"], writes=[cwb])
    P.dma(SP, ib[:, 0:1], prm["ml_ib"], writes=[gbb]); P.dma(SP, fbb[:, 0:1], prm["ml_fb"], writes=[gbb])
    P.dma(SP, nbc[:], prm["ml_normbc"], writes=[nbcb]); P.dma(SP, oh[:], prm["onehot"], writes=[ohb])
    P.op(DVE, lambda e: e.tensor_scalar(out=ib[:, 1:2], in0=ib[:, 0:1], scalar1=1.0 / 15, scalar2=None, op0=ALU.mult), reads=[gbb], writes=[gbb])
    P.op(DVE, lambda e: e.tensor_scalar(out=fbb[:, 1:2], in0=fbb[:, 0:1], scalar1=1.0 / 15, scalar2=None, op0=ALU.mult), reads=[gbb], writes=[gbb])
    gi = sbuf(C, st, "mgi", [4, TP]); gf = sbuf(C, st, "mgf", [4, TP]); SPc = sbuf(C, st, "mSP", [4, TP]); av = sbuf(C, st, "mav", [4, TP])
    MU = sbuf(C, st, "mMU", [4, TP]); MUS = sbuf(C, st, "mMUS", [4, TP])
    G8 = sbuf(C, st, "mG8", [36, TP]); FE = sbuf(C, st, "mFE", [4, 64]); min_ = sbuf(C, st, "mmin", [4, 4])
    gib, gfb_, SPb, avb, MUb, MUSb, G8b, FEb, minb = [Buf() for _ in range(9)]
    P.dma(SP, gi[:], pf[27 * 128:27 * 128 + 4, :], reads=[pfb], writes=[gib])
    P.dma(SP, gf[:], pf[27 * 128 + 4:27 * 128 + 8, :], reads=[pfb], writes=[gfb_])
    P.op(DVE, lambda e: e.memset(G8[:], 0.0), writes=[G8b])
    P.op(ACT, lambda e: e.activation(out=gi[:], in_=gi[:], func=AF.Tanh, scale=1.0 / 15, bias=ib[:, 1:2]), reads=[gib, gbb], writes=[gib])
    P.op(ACT, lambda e: e.activation(out=gf[:], in_=gf[:], func=AF.Tanh, scale=1.0 / 15, bias=fbb[:, 1:2]), reads=[gfb_, gbb], writes=[gfb_])
    P.op(ACT, lambda e: e.activation(out=gf[:], in_=gf[:], func=AF.Exp, scale=-15.0), reads=[gfb_], writes=[gfb_])
    P.op(ACT, lambda e: e.activation(out=gf[:], in_=gf[:], func=AF.Ln, bias=1.0), reads=[gfb_], writes=[gfb_])
    P.dma(SP, min_[:, 0:1], prm["mC_in"], writes=[minb])
    for si, (c0, ncol, t0) in enumerate(SEGS):
        P.op(DVE, lambda e, c0=c0, ncol=ncol: e.tensor_tensor_scan(out=SPc[:, c0:c0 + ncol], data0=ones[0:4, c0:c0 + ncol], data1=gf[:, c0:c0 + ncol],
                                                                   initial=0.0, op0=ALU.mult, op1=ALU.add), reads=[gfb_, onesb], writes=[SPb])
        P.op(DVE, lambda e, c0=c0, ncol=ncol: e.scalar_tensor_tensor(out=av[:, c0:c0 + ncol], in0=gi[:, c0:c0 + ncol], scalar=15.0,
                                                                     in1=SPc[:, c0:c0 + ncol], op0=ALU.mult, op1=ALU.add), reads=[gib, SPb], writes=[avb])
        if si == 0:
            P.op(DVE, lambda e, c0=c0, ncol=ncol: e.tensor_tensor_scan(out=MU[:, c0:c0 + ncol], data0=av[:, c0:c0 + ncol], data1=av[:, c0:c0 + ncol],
                                                                       initial=0.0, op0=ALU.max, op1=ALU.max), reads=[avb], writes=[MUb])
            P.op(DVE, lambda e: e.memset(MUS[:, 3:19], 0.0), writes=[MUSb])
            P.op(DVE, lambda e: e.tensor_tensor(out=min_[:, 1:2], in0=MU[:, 18:19], in1=SPc[:, 18:19], op=ALU.subtract), reads=[MUb, SPb, minb], writes=[minb])
            P.op(DVE, lambda e: e.scalar_tensor_tensor(out=min_[:, 2:3], in0=min_[:, 1:2], scalar=flagE[0:4, 0:1], in1=min_[:, 0:1], op0=ALU.mult,
                                                       op1=ALU.add), reads=[minb, fb], writes=[minb])
        else:
            P.op(DVE, lambda e, c0=c0, ncol=ncol: e.tensor_tensor_scan(out=MU[:, c0:c0 + ncol], data0=av[:, c0:c0 + ncol], data1=av[:, c0:c0 + ncol],
                                                                       initial=min_[:, 2:3], op0=ALU.max, op1=ALU.max), reads=[avb, minb], writes=[MUb])
            P.op(DVE, lambda e: e.tensor_copy(out=MUS[:, 22:86], in_=min_[:, 2:3].broadcast_to([4, 64])), reads=[minb], writes=[MUSb])
            P.op(DVE, lambda e: e.tensor_copy(out=MUS[:, 86:2070].rearrange("p (c j) -> p c j", j=64),
                                              in_=MU[:, 22:2006].rearrange("p (c j) -> p c j", j=64)[:, :, 63:64].broadcast_to([4, 31, 64])),
                 reads=[MUb], writes=[MUSb])
    P.op(DVE, lambda e: e.tensor_tensor(out=av[:, 3:TP], in0=av[:, 3:TP], in1=MUS[:, 3:TP], op=ALU.subtract), reads=[avb, MUSb], writes=[avb])
    P.op(ACT, lambda e: e.activation(out=G8[0:4, 3:TP], in_=av[:, 3:TP], func=AF.Exp), reads=[avb, G8b], writes=[G8b])
    P.op(DVE, lambda e: e.tensor_tensor(out=av[:, 3:TP], in0=SPc[:, 3:TP], in1=MUS[:, 3:TP], op=ALU.subtract), reads=[SPb, MUSb, G8b], writes=[avb])
    P.op(ACT, lambda e: e.activation(out=G8[32:36, 3:TP], in_=av[:, 3:TP], func=AF.Exp), reads=[avb, G8b], writes=[G8b])
    P.op(DVE, lambda e: e.tensor_tensor(out=FE[:, 0:1], in0=MUS[:, 3:4], in1=MU[:, 18:19], op=ALU.subtract), reads=[MUSb, MUb], writes=[FEb])
    P.op(DVE, lambda e: e.tensor_tensor(out=FE[:, 1:33], in0=MUS[:, 22:2070].rearrange("p (c j) -> p c j", j=64)[:, :, 0],
                                        in1=MU[:, 22:2070].rearrange("p (c j) -> p c j", j=64)[:, :, 63], op=ALU.subtract),
         reads=[MUSb, MUb, FEb], writes=[FEb])
    P.op(ACT, lambda e: e.activation(out=FE[:, 0:33], in_=FE[:, 0:33], func=AF.Exp), reads=[FEb], writes=[FEb])
    P.op(DVE, lambda e: e.tensor_tensor(out=min_[:, 3:4], in0=MU[:, TP - 1:TP], in1=SPc[:, TP - 1:TP], op=ALU.subtract), reads=[MUb, SPb, minb], writes=[minb])
    P.dma(POOL, prm["mC_out"], min_[:, 3:4], reads=[minb], pwrites=[K.sob])
    xq = sbuf(C, st, "mxq", [128, TP]); xk = sbuf(C, st, "mxk", [128, TP]); q = sbuf(C, st, "mq", [128, TP]); k = sbuf(C, st, "mk", [128, TP])
    xqb, xkb, qb, kb_ = [Buf() for _ in range(4)]
    tmpr = Ring([sbuf(C, st, f"mtmp{i}", [128, 4], F32) for i in range(2)])
    CX = sbuf(C, st, "mCX", [128, 257]); CXb = Buf()
    CX16 = sbuf(C, st, "mCX16", [128, 258], BF16); CX16b = Buf()
    qbf = sbuf(C, st, "mqbf", [128, TP], BF16); kbf = sbuf(C, st, "mkbf", [128, TP], BF16); qbfb, kbfb = Buf(), Buf()
    CXin = sbuf(C, st, "mCXin", [128, 257]); CXinb = Buf()
    FB = sbuf(C, st, "mFB", [128, 64]); FBb = Buf()
    psA = Ring([psum(C, st, f"mpa{i}", [128, 512], F32) for i in range(npsA)])
    psG = Ring([psum(C, st, f"mpg{i}", [128, 512], F32) for i in range(npsG)])
    vr = Ring([sbuf(C, st, f"mv{i}", [64, 512], F32) for i in range(3)])
    vxr = Ring([sbuf(C, st, f"mvx{i}", [64, 258], BF16) for i in range(2)])
    for _t, _b in zip(vxr.tiles, vxr.bufs):
        P.op(DVE, lambda e, _t=_t: e.memset(_t[:], 0.0), writes=[_b])
    ktr = Ring([sbuf(C, st, f"mkt{i}", [64, 128], BF16) for i in range(2)])
    scr = Ring([sbuf(C, st, f"msc{i}", [64, 64], BF16) for i in range(2)])
    gtr = Ring([sbuf(C, st, f"mgt{i}", [64, 36], F32) for i in range(2)])
    str_ = Ring([sbuf(C, st, f"mst{i}", [64, 8], F32) for i in range(2)])
    junk = sbuf(C, st, "mjunk", [64, 256], F32); jb = Buf()
    mh = sbuf(C, st, "mmh", [64, 1], F32); mhb = Buf()
    P.op(POOL, lambda e: e.memset(mh[:], -0.5), writes=[mhb])
    t1r = Ring([sbuf(C, st, f"mt1{i}", [64, 256], F32) for i in range(2)])
    yor = Ring([sbuf(C, st, f"myo{i}", [64, 256], BF16) for i in range(2)])
    for h in range(4):
        P.dma(SP, xq[:], pf[(19 + h) * 128:(20 + h) * 128, :], reads=[pfb], writes=[xqb])
        P.dma(SP, xk[:], pf[(23 + h) * 128:(24 + h) * 128, :], reads=[pfb], writes=[xkb])
        P.op(DVE, lambda e: e.memset(xq[:, 0:3], 0.0), reads=[xqb], writes=[xqb])
        P.op(DVE, lambda e: e.memset(xk[:, 0:3], 0.0), reads=[xkb], writes=[xkb])
        fix_gap(C, xq, xqb, prm["hist_in"][(19 + h) * 128:(20 + h) * 128, :], flagE, fb, tmpr, 128)
        fix_gap(C, xk, xkb, prm["hist_in"][(23 + h) * 128:(24 + h) * 128, :], flagE, fb, tmpr, 128)
        for (x, xb, o, ob, j) in ((xq, xqb, q, qb, h), (xk, xkb, k, kb_, 4 + h)):
            P.op(DVE, lambda e, x=x, o=o, j=j: e.tensor_scalar(out=o[:, 3:TP], in0=x[:, 0:TP - 3], scalar1=cw[:, j, 0:1], scalar2=cb[:, j:j + 1],
                                                               op0=ALU.mult, op1=ALU.add), reads=[xb, cwb], writes=[ob])
            for tap in range(1, 4):
                P.op(DVE, lambda e, x=x, o=o, j=j, tap=tap: e.scalar_tensor_tensor(out=o[:, 3:TP], in0=x[:, tap:TP - 3 + tap], scalar=cw[:, j, tap:tap + 1],
                                                                                 in1=o[:, 3:TP], op0=ALU.mult, op1=ALU.add), reads=[xb, cwb, ob], writes=[ob])
            P.op(ACT, lambda e, o=o: e.activation(out=o[:, 3:TP], in_=o[:, 3:TP], func=AF.Silu), reads=[ob], writes=[ob])
        P.op(DVE, lambda e: e.tensor_scalar(out=k[:, 3:TP], in0=k[:, 3:TP], scalar1=128 ** -0.5, scalar2=None, op0=ALU.mult), reads=[kb_], writes=[kb_])
        P.op(ACT, lambda e: e.activation(out=qbf[:, 3:TP], in_=q[:, 3:TP], func=AF.Copy), reads=[qb], writes=[qbfb])
        P.op(ACT, lambda e: e.activation(out=kbf[:, 3:TP], in_=k[:, 3:TP], func=AF.Copy), reads=[kb_], writes=[kbfb])
        pg, pgb = psG.next()
        P.op(PE, lambda e, pg=pg, h=h: e.matmul(pg[:, 0:33], lhsT=oh[0:4, h, :], rhs=FE[0:4, 0:33], start=True, stop=True), reads=[ohb, FEb], writes=[pgb])
        P.op(ACT, lambda e, pg=pg: e.activation(out=FB[:, 0:33], in_=pg[:, 0:33], func=AF.Copy), reads=[pgb], writes=[FBb])
        P.op(DVE, lambda e: e.memset(CX[:], 0.0), writes=[CXb])
        P.op(DVE, lambda e: e.memset(CX16[:], 0.0), writes=[CX16b])
        ci = 0
        for si, seg in enumerate(SEGS):
            if si == 1:
                P.dma(SP, CXin[:], prm["sC_in"][h], writes=[CXinb])
                P.op(DVE, lambda e: e.scalar_tensor_tensor(out=CX[:], in0=CX[:], scalar=flagE[:, 0:1], in1=CXin[:], op0=ALU.mult, op1=ALU.add),
                     reads=[CXb, CXinb, fb], writes=[CXb])
                P.op(ACT, lambda e: e.activation(out=CX16[:, 0:257], in_=CX[:], func=AF.Copy), reads=[CXb, CX16b], writes=[CX16b])
            for (c0, n, t0) in chunks_of(seg):
                vt, vb = vr.next()
                P.dma(SP, vt[0:n, 0:256], pt[t0:t0 + n, 1024 + h * 256:1024 + (h + 1) * 256], reads=[ptb], writes=[vb])
                P.dma(SP, vt[0:n, 256:512], pt[t0:t0 + n, 2048 + h * 256:2048 + (h + 1) * 256], reads=[ptb], writes=[vb])
                pa, pab = psA.next()
                P.op(PE, lambda e, pa=pa, c0=c0, n=n: e.transpose(out=pa[0:n, 0:128], in_=k[:, c0:c0 + n], identity=ident[:, :]), reads=[kb_, idb], writes=[pab])
                P.op(PE, lambda e, pa=pa, c0=c0, n=n: e.matmul(pa[0:n, 128:128 + n], lhsT=kbf[:, c0:c0 + n], rhs=qbf[:, c0:c0 + n], start=True, stop=True),
                     reads=[kbfb, qbfb], writes=[pab])
                P.op(PE, lambda e, pa=pa, c0=c0, n=n: e.transpose(out=pa[0:n, 192:228], in_=G8[0:36, c0:c0 + n], identity=ident[0:36, 0:36]),
                     reads=[G8b, idb], writes=[pab])
                kt, ktb = ktr.next(); sc, scb = scr.next(); gt, gtb = gtr.next()
                P.op(ACT, lambda e, kt=kt, pa=pa, n=n: e.activation(out=kt[0:n, :], in_=pa[0:n, 0:128], func=AF.Copy), reads=[pab], writes=[ktb])
                P.op(DVE, lambda e, sc=sc, pa=pa, n=n: e.tensor_tensor(out=sc[0:n, 0:n], in0=pa[0:n, 128:128 + n], in1=mask_i[0:n, 0:n], op=ALU.mult),
                     reads=[pab, mib], writes=[scb])
                P.op(ACT, lambda e, gt=gt, pa=pa, n=n: e.activation(out=gt[0:n, :], in_=pa[0:n, 192:228], func=AF.Copy), reads=[pab], writes=[gtb])
                vx, vxb = vxr.next()
                P.op(DVE, lambda e, vx=vx, vt=vt, gt=gt, n=n, h=h: e.tensor_scalar(out=vx[0:n, 0:256], in0=vt[0:n, 0:256], scalar1=gt[0:n, h:h + 1], scalar2=None,
                                                                                 op0=ALU.mult), reads=[vb, gtb], writes=[vxb])
                P.op(ACT, lambda e, vx=vx, gt=gt, n=n, h=h: e.activation(out=vx[0:n, 256:257], in_=gt[0:n, h:h + 1], func=AF.Copy), reads=[gtb, vxb], writes=[vxb])
                pn, pnb = psA.next()
                P.op(PE, lambda e, pn=pn, c0=c0, n=n: e.matmul(pn[0:n, 0:258], lhsT=qbf[:, c0:c0 + n], rhs=CX16[:, :], start=True, stop=False), reads=[qbfb, CX16b], writes=[pnb])
                P.op(PE, lambda e, pn=pn, sc=sc, vx=vx, n=n: e.matmul(pn[0:n, 0:258], lhsT=sc[0:n, 0:n], rhs=vx[0:n, :], start=False, stop=True),
                     reads=[scb, vxb], writes=[pnb])
                pc, pcb = psA.next()
                P.op(PE, lambda e, pc=pc, kt=kt, vx=vx, n=n: e.matmul(pc[:, 0:258], lhsT=kt[0:n, :], rhs=vx[0:n, :], start=True, stop=True),
                     reads=[ktb, vxb], writes=[pcb])
                P.op(DVE, lambda e, pc=pc: e.tensor_tensor(out=CX[:], in0=CX[:], in1=pc[:, 0:257], op=ALU.add), reads=[pcb, CXb], writes=[CXb])
                P.op(DVE, lambda e, ci=ci: e.tensor_scalar(out=CX[:], in0=CX[:], scalar1=FB[:, ci:ci + 1], scalar2=None, op0=ALU.mult),
                     reads=[CXb, FBb], writes=[CXb])
                P.op(ACT, lambda e: e.activation(out=CX16[:, 0:257], in_=CX[:], func=AF.Copy), reads=[CXb, CX16b], writes=[CX16b])
                if C.emit_out:
                    s_, sb2 = str_.next()
                    P.op(ACT, lambda e, s_=s_, pn=pn, n=n: e.activation(out=s_[0:n, 0:1], in_=pn[0:n, 256:257], func=AF.Abs),
                         reads=[pnb], writes=[sb2])
                    P.op(DVE, lambda e, s_=s_, gt=gt, n=n, h=h: e.tensor_tensor(out=s_[0:n, 0:1], in0=s_[0:n, 0:1], in1=gt[0:n, 32 + h:33 + h], op=ALU.max),
                         reads=[sb2, gtb], writes=[sb2])
                    P.op(DVE, lambda e, s_=s_, n=n: e.reciprocal(out=s_[0:n, 1:2], in_=s_[0:n, 0:1]), reads=[sb2], writes=[sb2])
                    P.op(ACT, lambda e, s_=s_, pn=pn, n=n: e.activation(out=junk[0:n, :], in_=pn[0:n, 0:256], func=AF.Square, scale=s_[0:n, 1:2],
                                                                       accum_out=s_[0:n, 2:3]), reads=[pnb, sb2], writes=[jb, sb2])
                    P.op(DVE, lambda e, s_=s_, n=n: e.tensor_scalar(out=s_[0:n, 3:4], in0=s_[0:n, 2:3], scalar1=1.0 / 256, scalar2=EPS, op0=ALU.mult,
                                                                   op1=ALU.add), reads=[sb2], writes=[sb2])
                    P.op(POOL, lambda e, s_=s_, n=n: e.tensor_tensor(out=s_[0:n, 4:5], in0=s_[0:n, 3:4], in1=mh[0:n, :], op=ALU.pow), reads=[sb2, mhb], writes=[sb2])
                    P.op(DVE, lambda e, s_=s_, n=n: e.tensor_tensor(out=s_[0:n, 5:6], in0=s_[0:n, 4:5], in1=s_[0:n, 1:2], op=ALU.mult), reads=[sb2], writes=[sb2])
                    t1, t1b = t1r.next(); yo, yob = yor.next()
                    P.op(DVE, lambda e, t1=t1, pn=pn, s_=s_, n=n, h=h: e.scalar_tensor_tensor(out=t1[0:n, :], in0=pn[0:n, 0:256], scalar=s_[0:n, 5:6],
                                                                                          in1=nbc[0:n, h * 256:(h + 1) * 256], op0=ALU.mult, op1=ALU.mult),
                         reads=[pnb, sb2, nbcb], writes=[t1b])
                    P.op(DVE, lambda e, yo=yo, t1=t1, vt=vt, n=n: e.tensor_tensor(out=yo[0:n, :], in0=t1[0:n, :], in1=vt[0:n, 256:512], op=ALU.mult),
                         reads=[t1b, vb], writes=[yob])
                    P.dma(POOL, y[t0:t0 + n, 1024 + h * 256:1024 + (h + 1) * 256], yo[0:n, :], reads=[yob], pwrites=[yb])
                ci += 1
                yield
        P.dma(POOL, prm["sC_out"][h], CX[:], reads=[CXb], pwrites=[K.sob])


def mixer_rwkv(C, st, pf, pfb, y, yb, prm, K):
    P = C.P
    ones, onesb, ident, idb, flagE, fb = K.ones, K.onesb, K.ident, K.idb, K.flagE, K.fb
    mask5, m5b = K.mask5, K.m5b
    muA = sbuf(C, st, "amuA", [64, 3, 8]); muL = sbuf(C, st, "amuL", [96, 3]); w2 = sbuf(C, st, "aw2", [32, 512]); a2 = sbuf(C, st, "aa2", [32, 512])
    g2 = sbuf(C, st, "ag2", [96, 512]); ch = sbuf(C, st, "ach", [64, 5, 8]); rk = sbuf(C, st, "ark", [64, 8, 2])
    lnw = sbuf(C, st, "alnw", [64, 512]); lnb = sbuf(C, st, "alnb", [64, 512])
    pb_ = Buf()
    for t, n_ in ((muA, "rw_muA"), (muL, "rw_muL"), (w2, "rw_w2"), (a2, "rw_a2"), (g2, "rw_g2"), (rk, "rw_rk"), (lnw, "rw_lnw_bc"), (lnb, "rw_lnb_bc")):
        P.dma(SP, t[:], prm[n_], pwrites=[pb_])
    P.dma(SP, ch[:, 0:4, :], prm["rw_ch"], pwrites=[pb_])
    P.op(DVE, lambda e: e.tensor_scalar(out=ch[:, 4, :], in0=ch[:, 3, :], scalar1=-1.0, scalar2=1.0, op0=ALU.mult, op1=ALU.add), reads=[pb_], writes=[pb_])
    mh = sbuf(C, st, "amh", [64, TP], F32); mhb = Buf()
    P.op(POOL, lambda e: e.memset(mh[:], -0.5), writes=[mhb])
    tmpr = Ring([sbuf(C, st, f"atmp{i}", [128, 4], F32) for i in range(2)])
    raw = sbuf(C, st, "araw", [96, TP]); rawb = Buf()
    thw = sbuf(C, st, "athw", [32, TP]); xal = sbuf(C, st, "axal", [32, TP]); sg = sbuf(C, st, "asg", [96, TP])
    thwb, xalb, sgb = Buf(), Buf(), Buf()
    for (dst, dstb, r0, nr, mcol, fn) in ((thw, thwb, 12 * 128, 32, 0, AF.Tanh), (xal, xalb, 12 * 128 + 32, 32, 1, None), (sg, sgb, 13 * 128, 96, 2, AF.Sigmoid)):
        P.dma(SP, raw[0:nr, :], pf[r0:r0 + nr, :], reads=[pfb], writes=[rawb])
        P.op(DVE, lambda e, nr=nr: e.memset(raw[0:nr, 0:3], 0.0), reads=[rawb], writes=[rawb])
        fix_gap(C, raw, rawb, prm["hist_in"][r0:r0 + nr, :], flagE, fb, tmpr, nr)
        P.op(DVE, lambda e, dst=dst, nr=nr: e.tensor_tensor(out=dst[0:nr, 3:TP], in0=raw[0:nr, 2:TP - 1], in1=raw[0:nr, 3:TP], op=ALU.subtract),
             reads=[rawb], writes=[dstb])
        P.op(DVE, lambda e, dst=dst, nr=nr, mcol=mcol: e.scalar_tensor_tensor(out=dst[0:nr, 3:TP], in0=dst[0:nr, 3:TP], scalar=muL[0:nr, mcol:mcol + 1],
                                                                            in1=raw[0:nr, 3:TP], op0=ALU.mult, op1=ALU.add), reads=[rawb, dstb, pb_], writes=[dstb])
        if fn is not None:
            P.op(ACT, lambda e, dst=dst, nr=nr, fn=fn: e.activation(out=dst[0:nr, 3:TP], in_=dst[0:nr, 3:TP], func=fn), reads=[dstb], writes=[dstb])
    A = [sbuf(C, st, f"aA{i}", [64, TP]) for i in range(10)]
    Ab = [Buf() for _ in range(10)]
    H = sbuf(C, st, "aH", [64, 64]); Hb = Buf()
    Hin = sbuf(C, st, "aHin", [64, 64]); Hinb = Buf()
    G = 8
    MM = sbuf(C, st, "aMM", [64, G, 5, 64]); MMb = Buf()
    TM = sbuf(C, st, "aTM", [64, G, 3, 64]); TMb = Buf()
    NN = [sbuf(C, st, f"aNN{i}", [64, G, 2, 64]) for i in range(2)]; NNb = [Buf(), Buf()]
    Pm = sbuf(C, st, "aPm", [64, G, 64]); Pmb = Buf()
    GB = sbuf(C, st, "aGB", [64, G, 66]); GBb = Buf()
    ps1 = Ring([psum(C, st, f"ap1{i}", [128, 512], F32) for i in range(3)])
    pygr = Ring([psum(C, st, f"apy{i}", [128, 512], F32) for i in range(2)])
    ps2 = Ring([psum(C, st, f"ap2{i}", [128, 512], F32) for i in range(3)])
    w0r = Ring([sbuf(C, st, f"aw0{i}", [64, 64], F32) for i in range(2)])
    ur = Ring([sbuf(C, st, f"au{i}", [64, 64], F32) for i in range(2)])
    str_ = Ring([sbuf(C, st, f"ast{i}", [64, 8], F32) for i in range(2)])
    junk = sbuf(C, st, "ajunk", [64, 64], F32); jb = Buf()
    T1 = sbuf(C, st, "aT1", [64, G, 64]); T1b = Buf()
    SQ = sbuf(C, st, "aSQ", [64, G, 64]); SQb = Buf()
    YO = sbuf(C, st, "aYO", [64, G, 64], BF16); YOb = Buf()
    ST = sbuf(C, st, "aST", [64, 6, G]); STb = Buf()
    for h in range(8):
        rows = [(sg_ * 4 + h // 2) * 128 + (h % 2) * 64 for sg_ in range(3)]
        for i in range(3):
            P.dma(SP, A[i][:], pf[rows[i]:rows[i] + 64, :], reads=[pfb], writes=[Ab[i]])
            P.op(DVE, lambda e, i=i: e.memset(A[i][:, 0:3], 0.0), reads=[Ab[i]], writes=[Ab[i]])
            fix_gap(C, A[i], Ab[i], prm["hist_in"][rows[i]:rows[i] + 64, :], flagE, fb, tmpr, 64)
            P.op(DVE, lambda e, i=i: e.tensor_tensor(out=A[3 + i][:, 3:TP], in0=A[i][:, 2:TP - 1], in1=A[i][:, 3:TP], op=ALU.subtract),
                 reads=[Ab[i]], writes=[Ab[3 + i]])
            P.op(DVE, lambda e, i=i, h=h: e.scalar_tensor_tensor(out=A[3 + i][:, 3:TP], in0=A[3 + i][:, 3:TP], scalar=muA[:, i, h:h + 1], in1=A[i][:, 3:TP],
                                                                op0=ALU.mult, op1=ALU.add), reads=[Ab[i], Ab[3 + i], pb_], writes=[Ab[3 + i]])
        xr, xk, xv = A[3], A[4], A[5]
        for c0 in range(3, TP, 512):
            n = min(512, TP - c0)
            p_, p_b = ps1.next()
            P.op(PE, lambda e, p_=p_, c0=c0, n=n, h=h: e.matmul(p_[0:64, 0:n], lhsT=w2[0:32, h * 64:(h + 1) * 64], rhs=thw[0:32, c0:c0 + n], start=True, stop=True),
                 reads=[pb_, thwb], writes=[p_b])
            P.op(ACT, lambda e, p_=p_, c0=c0, n=n, h=h: e.activation(out=A[0][:, c0:c0 + n], in_=p_[0:64, 0:n], func=AF.Sigmoid, bias=ch[:, 0, h:h + 1]),
                 reads=[p_b, pb_], writes=[Ab[0]])
            p_, p_b = ps1.next()
            P.op(PE, lambda e, p_=p_, c0=c0, n=n, h=h: e.matmul(p_[0:64, 0:n], lhsT=a2[0:32, h * 64:(h + 1) * 64], rhs=xal[0:32, c0:c0 + n], start=True, stop=True),
                 reads=[pb_, xalb], writes=[p_b])
            P.op(ACT, lambda e, p_=p_, c0=c0, n=n, h=h: e.activation(out=A[1][:, c0:c0 + n], in_=p_[0:64, 0:n], func=AF.Sigmoid, bias=ch[:, 1, h:h + 1]),
                 reads=[p_b, pb_], writes=[Ab[1]])
        P.op(DVE, lambda e, h=h: e.tensor_scalar(out=A[2][:, 3:TP], in0=xk[:, 3:TP], scalar1=ch[:, 2, h:h + 1], scalar2=None, op0=ALU.mult),
             reads=[Ab[4], pb_], writes=[Ab[2]])
        P.op(DVE, lambda e: e.tensor_tensor(out=A[6][:, 3:TP], in0=A[2][:, 3:TP], in1=A[2][:, 3:TP], op=ALU.mult), reads=[Ab[2]], writes=[Ab[6]])
        for c0 in range(3, TP, 512):
            n = min(512, TP - c0)
            p_, p_b = ps1.next()
            P.op(PE, lambda e, p_=p_, c0=c0, n=n: e.matmul(p_[0:64, 0:n], lhsT=ones[0:64, 0:64], rhs=A[6][:, c0:c0 + n], start=True, stop=True),
                 reads=[onesb, Ab[6]], writes=[p_b])
            P.op(DVE, lambda e, p_=p_, c0=c0, n=n: e.tensor_scalar(out=A[8][:, c0:c0 + n], in0=p_[0:64, 0:n], scalar1=1e-24, scalar2=None, op0=ALU.max),
                 reads=[p_b], writes=[Ab[8]])
        P.op(POOL, lambda e: e.tensor_tensor(out=A[8][:, 3:TP], in0=A[8][:, 3:TP], in1=mh[:, 3:TP], op=ALU.pow), reads=[Ab[8], mhb], writes=[Ab[8]])
        P.op(DVE, lambda e: e.tensor_tensor(out=A[2][:, 3:TP], in0=A[2][:, 3:TP], in1=A[8][:, 3:TP], op=ALU.mult), reads=[Ab[2], Ab[8]], writes=[Ab[2]])
        P.op(DVE, lambda e, h=h: e.tensor_scalar(out=A[6][:, 3:TP], in0=A[1][:, 3:TP], scalar1=ch[:, 3, h:h + 1], scalar2=ch[:, 4, h:h + 1], op0=ALU.mult,
                                                op1=ALU.add), reads=[Ab[1], pb_], writes=[Ab[6]])
        P.op(DVE, lambda e: e.tensor_tensor(out=xk[:, 3:TP], in0=xk[:, 3:TP], in1=A[6][:, 3:TP], op=ALU.mult), reads=[Ab[4], Ab[6]], writes=[Ab[4]])
        P.op(DVE, lambda e: e.tensor_tensor(out=A[1][:, 3:TP], in0=A[1][:, 3:TP], in1=A[2][:, 3:TP], op=ALU.mult), reads=[Ab[1], Ab[2]], writes=[Ab[1]])
        P.op(DVE, lambda e: e.tensor_tensor(out=A[6][:, 3:TP], in0=xr[:, 3:TP], in1=xk[:, 3:TP], op=ALU.mult), reads=[Ab[3], Ab[4]], writes=[Ab[6]])
        for (c0, ncol, t0) in SEGS:
            P.op(DVE, lambda e, c0=c0, ncol=ncol: e.tensor_tensor_scan(out=A[7][:, c0:c0 + ncol], data0=ones[0:64, c0:c0 + ncol], data1=A[0][:, c0:c0 + ncol],
                                                                       initial=0.0, op0=ALU.mult, op1=ALU.add), reads=[Ab[0], onesb], writes=[Ab[7]])
        P.op(DVE, lambda e: e.memset(A[8][:, 19:22], 0.0), reads=[Ab[8]], writes=[Ab[8]])
        chunk_rel(C, A[8], Ab[8], A[7], Ab[7], 64)
        P.op(ACT, lambda e: e.activation(out=A[7][:, 3:TP], in_=A[8][:, 3:TP], func=AF.Exp, scale=-LDK), reads=[Ab[8]], writes=[Ab[7]])
        P.op(ACT, lambda e: e.activation(out=A[9][:, 3:TP], in_=A[8][:, 3:TP], func=AF.Exp, scale=LDK), reads=[Ab[8]], writes=[Ab[9]])
        P.op(DVE, lambda e: e.tensor_tensor(out=A[8][:, 3:TP], in0=A[8][:, 3:TP], in1=A[0][:, 3:TP], op=ALU.subtract), reads=[Ab[8], Ab[0]], writes=[Ab[8]])
        P.op(ACT, lambda e: e.activation(out=A[8][:, 3:TP], in_=A[8][:, 3:TP], func=AF.Exp, scale=-LDK), reads=[Ab[8]], writes=[Ab[8]])
        P.op(DVE, lambda e: e.tensor_tensor(out=xr[:, 3:TP], in0=xr[:, 3:TP], in1=A[7][:, 3:TP], op=ALU.mult), reads=[Ab[3], Ab[7]], writes=[Ab[3]])
        P.op(DVE, lambda e: e.tensor_tensor(out=xk[:, 3:TP], in0=xk[:, 3:TP], in1=A[9][:, 3:TP], op=ALU.mult), reads=[Ab[4], Ab[9]], writes=[Ab[4]])
        P.op(DVE, lambda e: e.tensor_tensor(out=A[1][:, 3:TP], in0=A[1][:, 3:TP], in1=A[9][:, 3:TP], op=ALU.mult), reads=[Ab[1], Ab[9]], writes=[Ab[1]])
        P.op(DVE, lambda e: e.scalar_tensor_tensor(out=A[2][:, 3:TP], in0=A[2][:, 3:TP], scalar=-1.0, in1=A[8][:, 3:TP], op0=ALU.mult, op1=ALU.mult),
             reads=[Ab[2], Ab[8]], writes=[Ab[2]])
        rt, kt_, bt, at, prod, G1 = A[3], A[4], A[1], A[2], A[6], A[7]
        rtb, ktb_, btb, atb, prodb, G1b = Ab[3], Ab[4], Ab[1], Ab[2], Ab[6], Ab[7]
        xvb = Ab[5]
        P.op(DVE, lambda e: e.memset(H[:], 0.0), writes=[Hb])
        for si, seg in enumerate(SEGS):
            if si == 1:
                P.dma(SP, Hin[:], prm["sA_in"][h], writes=[Hinb])
                P.op(DVE, lambda e: e.scalar_tensor_tensor(out=H[:], in0=H[:], scalar=flagE[0:64, 0:1], in1=Hin[:], op0=ALU.mult, op1=ALU.add),
                     reads=[Hb, Hinb, fb], writes=[Hb])
            chs_all = chunks_of(seg)
            for g0 in range(0, len(chs_all), G):
                chs = chs_all[g0:g0 + G]
                ng = len(chs)
                n = chs[0][1]
                nlev = 5 if n == 64 else 3
                for g, (c0, n, t0) in enumerate(chs):
                    p_, p_b = ps1.next()
                    for j, (src, srcb) in enumerate(((xv, xvb), (kt_, ktb_), (bt, btb))):
                        P.op(PE, lambda e, p_=p_, src=src, c0=c0, n=n, j=j: e.transpose(out=p_[0:n, j * 64:(j + 1) * 64], in_=src[:, c0:c0 + n], identity=ident[0:64, 0:64]),
                             reads=[srcb, idb], writes=[p_b])
                    P.op(ACT, lambda e, p_=p_, g=g, n=n: e.activation(out=TM[0:n, g, :, :], in_=p_[0:n, 0:192].rearrange("p (j d) -> p j d", j=3), func=AF.Copy),
                         reads=[p_b], pwrites=[TMb])
                    q_, q_b = ps1.next()
                    pairs = ((kt_, ktb_, at, atb), (kt_, ktb_, rt, rtb), (bt, btb, at, atb), (bt, btb, rt, rtb), (at, atb, bt, btb))
                    for j, (l, lb, r, rb) in enumerate(pairs):
                        P.op(PE, lambda e, q_=q_, l=l, r=r, c0=c0, n=n, j=j: e.matmul(q_[0:n, j * 64:j * 64 + n], lhsT=l[:, c0:c0 + n], rhs=r[:, c0:c0 + n], start=True, stop=True),
                             reads=[lb, rb], writes=[q_b])
                    P.op(DVE, lambda e, q_=q_, g=g, n=n: e.tensor_tensor(out=MM[0:n, g, :, 0:n], in0=q_[0:n, 0:320].rearrange("p (j d) -> p j d", j=5)[:, :, 0:n],
                                                                       in1=mask5[0:n, :, 0:n], op=ALU.mult), reads=[q_b, m5b], pwrites=[MMb])
                P.op(DVE, lambda e, ng=ng, n=n: e.tensor_tensor(out=Pm[0:n, 0:ng, 0:n], in0=MM[0:n, 0:ng, 2, 0:n],
                                                                in1=ident[0:n, 0:n].unsqueeze(1).broadcast_to([n, ng, n]), op=ALU.add),
                     reads=[MMb, idb], writes=[Pmb])
                curN = lambda g, n=n: MM[0:n, g, 2, 0:n]
                curNT = lambda g, n=n: MM[0:n, g, 4, 0:n]
                curb = MMb
                for lev in range(nlev):
                    nn, nnb = NN[lev % 2], NNb[lev % 2]
                    for g4 in range(0, ng, 4):
                        m4 = min(4, ng - g4)
                        p2, p2b = ps2.next()
                        for g in range(g4, g4 + m4):
                            gg = g - g4
                            P.op(PE, lambda e, p2=p2, gg=gg, n=n, a_=curNT(g), b_=curN(g): e.matmul(p2[0:n, gg * 128:gg * 128 + n], lhsT=a_, rhs=b_, start=True, stop=True),
                                 reads=[curb], writes=[p2b])
                            P.op(PE, lambda e, p2=p2, gg=gg, n=n, a_=curN(g), b_=curNT(g): e.matmul(p2[0:n, gg * 128 + 64:gg * 128 + 64 + n], lhsT=a_, rhs=b_, start=True, stop=True),
                                 reads=[curb], writes=[p2b])
                        P.op(ACT, lambda e, p2=p2, nn=nn, g4=g4, m4=m4, n=n: e.activation(out=nn[0:n, g4:g4 + m4, :, 0:n],
                                                                                 in_=p2[0:n, 0:m4 * 128].rearrange("p (g j d) -> p g j d", g=m4, j=2)[:, :, :, 0:n], func=AF.Copy),
                             reads=[p2b], pwrites=[nnb])
                    curN = lambda g, nn=nn, n=n: nn[0:n, g, 0, 0:n]
                    curNT = lambda g, nn=nn, n=n: nn[0:n, g, 1, 0:n]
                    curb = nnb
                    p1, p1b = ps1.next()
                    for g in range(ng):
                        P.op(PE, lambda e, p1=p1, g=g, n=n, a_=curNT(g): e.matmul(p1[0:n, g * 64:g * 64 + n], lhsT=a_, rhs=Pm[0:n, g, 0:n], start=True, stop=True),
                             reads=[curb, Pmb], writes=[p1b])
                    P.op(DVE, lambda e, p1=p1, ng=ng, n=n: e.tensor_tensor(out=Pm[0:n, 0:ng, 0:n], in0=Pm[0:n, 0:ng, 0:n],
                                                                         in1=p1[0:n, 0:ng * 64].rearrange("p (g d) -> p g d", g=ng)[:, :, 0:n], op=ALU.add),
                         reads=[p1b, Pmb], writes=[Pmb])
                if C.emit_out:
                    for g, (c0, n, t0) in enumerate(chs):
                        p_, p_b = ps1.next()
                        P.op(PE, lambda e, p_=p_, c0=c0, n=n, h=h: e.matmul(p_[0:n, 0:64], lhsT=sg[0:96, c0:c0 + n], rhs=g2[0:96, h * 64:(h + 1) * 64], start=True, stop=True),
                             reads=[sgb, pb_], writes=[p_b])
                        P.op(PE, lambda e, p_=p_, c0=c0, n=n, h=h: e.matmul(p_[0:n, 64:66], lhsT=prod[:, c0:c0 + n], rhs=rk[:, h, :], start=True, stop=True),
                             reads=[prodb, pb_], writes=[p_b])
                        P.op(ACT, lambda e, p_=p_, g=g, n=n: e.activation(out=GB[0:n, g, :], in_=p_[0:n, 0:66], func=AF.Copy), reads=[p_b], pwrites=[GBb])
                pyg, pygb = pygr.next()
                for g, (c0, n, t0) in enumerate(chs):
                    vtm = TM[0:n, g, 0, :]; ktm = TM[0:n, g, 1, :]; btm = TM[0:n, g, 2, :]
                    LakT = MM[0:n, g, 0, 0:n]; MrkT = MM[0:n, g, 1, 0:n]; MrbT = MM[0:n, g, 3, 0:n]
                    TT_ = Pm[0:n, g, 0:n]
                    pw, pwb = ps1.next()
                    P.op(PE, lambda e, pw=pw, c0=c0, n=n: e.matmul(pw[0:n, 0:64], lhsT=at[:, c0:c0 + n], rhs=H[:, :], start=True, stop=False), reads=[atb, Hb], writes=[pwb])
                    P.op(PE, lambda e, pw=pw, n=n, LakT=LakT, vtm=vtm: e.matmul(pw[0:n, 0:64], lhsT=LakT, rhs=vtm, start=False, stop=True), reads=[MMb, TMb], writes=[pwb])
                    w0, w0b = w0r.next()
                    P.op(ACT, lambda e, w0=w0, pw=pw, n=n: e.activation(out=w0[0:n, :], in_=pw[0:n, 0:64], func=AF.Copy), reads=[pwb], writes=[w0b])
                    P.op(PE, lambda e, pw=pw, n=n, TT_=TT_, w0=w0: e.matmul(pw[0:n, 64:128], lhsT=TT_, rhs=w0[0:n, :], start=True, stop=True), reads=[Pmb, w0b], writes=[pwb])
                    u, ub_ = ur.next()
                    P.op(DVE, lambda e, u=u, pw=pw, n=n: e.tensor_copy(out=u[0:n, :], in_=pw[0:n, 64:128]), reads=[pwb], writes=[ub_])
                    if C.emit_out:
                        P.op(PE, lambda e, pyg=pyg, g=g, c0=c0, n=n: e.matmul(pyg[0:n, g * 64:(g + 1) * 64], lhsT=rt[:, c0:c0 + n], rhs=H[:, :], start=True, stop=False), reads=[rtb, Hb], writes=[pygb])
                        P.op(PE, lambda e, pyg=pyg, g=g, n=n, MrbT=MrbT, u=u: e.matmul(pyg[0:n, g * 64:(g + 1) * 64], lhsT=MrbT, rhs=u[0:n, :], start=False, stop=False), reads=[MMb, ub_], writes=[pygb])
                        P.op(PE, lambda e, pyg=pyg, g=g, n=n, MrkT=MrkT, vtm=vtm: e.matmul(pyg[0:n, g * 64:(g + 1) * 64], lhsT=MrkT, rhs=vtm, start=False, stop=True), reads=[MMb, TMb], writes=[pygb])
                    ph, phb = ps1.next()
                    P.op(PE, lambda e, ph=ph: e.matmul(ph[0:64, 0:64], lhsT=ident[0:64, 0:64], rhs=H[:, :], start=True, stop=False), reads=[idb, Hb], writes=[phb])
                    P.op(PE, lambda e, ph=ph, n=n, btm=btm, u=u: e.matmul(ph[0:64, 0:64], lhsT=btm, rhs=u[0:n, :], start=False, stop=False), reads=[TMb, ub_], writes=[phb])
                    P.op(PE, lambda e, ph=ph, n=n, ktm=ktm, vtm=vtm: e.matmul(ph[0:64, 0:64], lhsT=ktm, rhs=vtm, start=False, stop=True), reads=[TMb], writes=[phb])
                    ce = c0 + n - 1
                    P.op(DVE, lambda e, ph=ph, ce=ce: e.tensor_scalar(out=H[:], in0=ph[0:64, 0:64], scalar1=G1[:, ce:ce + 1], scalar2=None, op0=ALU.mult),
                         reads=[phb, G1b], writes=[Hb])
                if C.emit_out:
                    t0g = chs[0][2]
                    YG = pyg[0:n, 0:ng * 64].rearrange("p (g d) -> p g d", g=ng)
                    bc = lambda ap, n=n, ng=ng: ap.unsqueeze(2).broadcast_to([n, ng, 64])
                    P.op(DVE, lambda e, YG=YG, n=n, ng=ng: e.tensor_reduce(out=ST[0:n, 0, 0:ng], in_=YG, axis=AX.X, op=ALU.add), reads=[pygb], writes=[STb])
                    P.op(ACT, lambda e, YG=YG, n=n, ng=ng: e.activation(out=SQ[0:n, 0:ng, :], in_=YG, func=AF.Square), reads=[pygb], writes=[SQb])
                    P.op(DVE, lambda e, n=n, ng=ng: e.tensor_reduce(out=ST[0:n, 1, 0:ng], in_=SQ[0:n, 0:ng, :], axis=AX.X, op=ALU.add), reads=[SQb, STb], writes=[STb])
                    P.op(DVE, lambda e, n=n, ng=ng: e.tensor_scalar(out=ST[0:n, 2, 0:ng], in0=ST[0:n, 0, 0:ng], scalar1=1.0 / 64, scalar2=None, op0=ALU.mult), reads=[STb], writes=[STb])
                    P.op(DVE, lambda e, n=n, ng=ng: e.tensor_tensor(out=ST[0:n, 3, 0:ng], in0=ST[0:n, 2, 0:ng], in1=ST[0:n, 2, 0:ng], op=ALU.mult), reads=[STb], writes=[STb])
                    P.op(DVE, lambda e, n=n, ng=ng: e.tensor_scalar(out=ST[0:n, 4, 0:ng], in0=ST[0:n, 1, 0:ng], scalar1=1.0 / 64, scalar2=64e-5, op0=ALU.mult, op1=ALU.add),
                         reads=[STb], writes=[STb])
                    P.op(DVE, lambda e, n=n, ng=ng: e.tensor_tensor(out=ST[0:n, 4, 0:ng], in0=ST[0:n, 4, 0:ng], in1=ST[0:n, 3, 0:ng], op=ALU.subtract), reads=[STb], writes=[STb])
                    P.op(POOL, lambda e, n=n, ng=ng: e.tensor_tensor(out=ST[0:n, 5, 0:ng], in0=ST[0:n, 4, 0:ng], in1=mh[0:n, 0:ng], op=ALU.pow), reads=[STb, mhb], writes=[STb])
                    P.op(DVE, lambda e, YG=YG, n=n, ng=ng, bc=bc: e.tensor_tensor(out=T1[0:n, 0:ng, :], in0=YG, in1=bc(ST[0:n, 2, 0:ng]), op=ALU.subtract),
                         reads=[pygb, STb], writes=[T1b])
                    P.op(DVE, lambda e, n=n, ng=ng, bc=bc: e.tensor_tensor(out=T1[0:n, 0:ng, :], in0=T1[0:n, 0:ng, :], in1=bc(ST[0:n, 5, 0:ng]), op=ALU.mult),
                         reads=[T1b, STb], writes=[T1b])
                    P.op(DVE, lambda e, n=n, ng=ng, h=h: e.tensor_tensor(out=T1[0:n, 0:ng, :], in0=T1[0:n, 0:ng, :],
                                                                      in1=lnw[0:n, h * 64:(h + 1) * 64].unsqueeze(1).broadcast_to([n, ng, 64]), op=ALU.mult),
                         reads=[T1b, pb_], writes=[T1b])
                    P.op(DVE, lambda e, n=n, ng=ng, h=h: e.tensor_tensor(out=T1[0:n, 0:ng, :], in0=T1[0:n, 0:ng, :],
                                                                      in1=lnb[0:n, h * 64:(h + 1) * 64].unsqueeze(1).broadcast_to([n, ng, 64]), op=ALU.add),
                         reads=[T1b, pb_], writes=[T1b])
                    P.op(DVE, lambda e, n=n, ng=ng: e.tensor_tensor(out=SQ[0:n, 0:ng, :], in0=TM[0:n, 0:ng, 0, :], in1=GB[0:n, 0:ng, 64:65].broadcast_to([n, ng, 64]), op=ALU.mult),
                         reads=[TMb, GBb, SQb], writes=[SQb])
                    P.op(DVE, lambda e, n=n, ng=ng: e.tensor_tensor(out=T1[0:n, 0:ng, :], in0=T1[0:n, 0:ng, :], in1=SQ[0:n, 0:ng, :], op=ALU.add), reads=[T1b, SQb], writes=[T1b])
                    P.op(DVE, lambda e, n=n, ng=ng: e.tensor_tensor(out=YO[0:n, 0:ng, :], in0=T1[0:n, 0:ng, :], in1=GB[0:n, 0:ng, 0:64], op=ALU.mult),
                         reads=[T1b, GBb], writes=[YOb])
                    P.dma(SP, y[t0g:t0g + ng * n, h * 64:(h + 1) * 64].rearrange("(g p) d -> p g d", p=n), YO[0:n, 0:ng, :], reads=[YOb], pwrites=[yb])
        P.dma(POOL, prm["sA_out"][h], H[:], reads=[Hb], pwrites=[K.sob])


def mixer_rwkv2(C, st, pf, pfb, y, yb, prm, K):
    P = C.P
    ones, onesb, ident, idb, flagE, fb = K.ones, K.onesb, K.ident, K.idb, K.flagE, K.fb
    mask5, m5b = K.mask5, K.m5b
    muA = sbuf(C, st, "bmuA", [64, 3, 8]); muL = sbuf(C, st, "bmuL", [96, 3]); w2 = sbuf(C, st, "bw2", [32, 512]); a2 = sbuf(C, st, "ba2", [32, 512])
    g2 = sbuf(C, st, "bg2", [96, 512]); ch = sbuf(C, st, "bch", [64, 5, 8]); rk = sbuf(C, st, "brk", [64, 8, 2])
    lnw = sbuf(C, st, "blnw", [64, 512]); lnb = sbuf(C, st, "blnb", [64, 512])
    pb_ = Buf()
    for t, n_ in ((muA, "rw_muA"), (muL, "rw_muL"), (w2, "rw_w2"), (a2, "rw_a2"), (g2, "rw_g2"), (rk, "rw_rk"), (lnw, "rw_lnw_bc"), (lnb, "rw_lnb_bc")):
        P.dma(SP, t[:], prm[n_], pwrites=[pb_])
    P.dma(SP, ch[:, 0:4, :], prm["rw_ch"], pwrites=[pb_])
    P.op(DVE, lambda e: e.tensor_scalar(out=ch[:, 4, :], in0=ch[:, 3, :], scalar1=-1.0, scalar2=1.0, op0=ALU.mult, op1=ALU.add), reads=[pb_], writes=[pb_])
    WM = 128
    W1M = WM + 1
    mh = sbuf(C, st, "bmh", [64, 8 * WM], F32); mhb = Buf()
    P.op(POOL, lambda e: e.memset(mh[:], -0.5), writes=[mhb])
    names = ["pr", "pk", "pv", "xr", "xk", "xv", "sgz", "asig", "kkn", "t1", "rel", "G1", "G2"]
    X = {nm: sbuf(C, st, "bX" + nm, [64, 8, W1M]) for nm in names}
    Xb = {nm: Buf() for nm in names}
    Ssc = sbuf(C, st, "bSsc", [64, 1 + 8 * WM]); Sscb = Buf()
    P.op(DVE, lambda e: e.memset(Ssc[:, 0:1], 0.0), writes=[Sscb])
    lraw = sbuf(C, st, "blraw", [96, 3, W1M]); lrawb = Buf()
    thw = sbuf(C, st, "bthw", [32, WM]); xal = sbuf(C, st, "bxal", [32, WM]); sg = sbuf(C, st, "bsg", [96, WM])
    thwb, xalb, sgb = Buf(), Buf(), Buf()
    hs = sbuf(C, st, "bhs", [96, 2, 8]); hsb = Buf()
    H = sbuf(C, st, "bH", [64, 8, 64]); Hb = Buf()
    Hin = sbuf(C, st, "bHin", [64, 8, 64]); Hinb = Buf()
    NQ = 16
    TM = sbuf(C, st, "bTM", [64, 3, NQ, 64]); TMb = Buf()
    MM = sbuf(C, st, "bMM", [64, 5, NQ, 64]); MMb = Buf()
    NN = [sbuf(C, st, f"bNN{i}", [64, NQ, 2, 64]) for i in range(2)]; NNb = [Buf(), Buf()]
    Pm = sbuf(C, st, "bPm", [64, NQ, 64]); Pmb = Buf()
    W0s = sbuf(C, st, "bW0", [64, 8, 64]); W0b = Buf()
    Us = sbuf(C, st, "bUs", [64, 8, 64]); Usb = Buf()
    GBs = sbuf(C, st, "bGB", [64, 528]); GBb = Buf()
    T1 = sbuf(C, st, "bT1", [64, 8, 64]); T1b = Buf()
    SQ = sbuf(C, st, "bSQ", [64, 8, 64]); SQb = Buf()
    YO = sbuf(C, st, "bYO", [64, 8, 64], BF16); YOb = Buf()
    ST = sbuf(C, st, "bST", [64, 6, 8]); STb = Buf()
    bank = [psum(C, st, f"bpb{i}", [128, 512], F32) for i in range(8)]
    bkb = [Buf() for _ in range(8)]
    P.op(DVE, lambda e: e.memset(H[:], 0.0), writes=[Hb])

    def bc8(ap, W):
        return ap.unsqueeze(2).broadcast_to([64, 8, W])

    def vop(fn, reads, writes, pwrites=()):
        P.op(DVE, fn, reads=reads, writes=writes, pwrites=pwrites)

    Fl = sbuf(C, st, "bFl", [64, 8 * WM]); Flb = Buf()

    scs = [(3, 16, 0, 16)] + [(22 + 128 * i, 128, 16 + 128 * i, 64) for i in range(16)]
    def do_sc(sci, c0, W, t0, n):
        W1 = W + 1
        nch = W // n
        nq = 8 * nch
        cur = lambda nm: X[nm][:, :, 1:W1]
        prev = lambda nm: X[nm][:, :, 0:W]
        P.phase = "rwkv_pre"
        for i, nm in enumerate(("pr", "pk", "pv")):
            P.dma(SP, X[nm][:, :, 0:W1], pf[i * 512:(i + 1) * 512, c0 - 1:c0 + W].rearrange("(h d) c -> d h c", d=64), reads=[pfb], writes=[Xb[nm]])
        for j, (r0, nr) in enumerate(((12 * 128, 32), (12 * 128 + 32, 32), (13 * 128, 96))):
            P.dma(SP, lraw[0:nr, j, 0:W1], pf[r0:r0 + nr, c0 - 1:c0 + W], reads=[pfb], writes=[lrawb])
        if sci == 0:
            for nm in ("pr", "pk", "pv"):
                vop(lambda e, nm=nm: e.memset(X[nm][:, :, 0:1], 0.0), [Xb[nm]], [Xb[nm]])
            vop(lambda e: e.memset(lraw[:, :, 0:1], 0.0), [lrawb], [lrawb])
        if sci == 1:
            for i, nm in enumerate(("pr", "pk", "pv")):
                P.dma(SP, hs[0:64, 0, :], pf[i * 512:(i + 1) * 512, 18:19].rearrange("(h d) c -> d (h c)", d=64), reads=[pfb], writes=[hsb], allow_slow_non_contiguous=True)
                P.dma(SP, hs[0:64, 1, :], prm["hist_in"][i * 512:(i + 1) * 512, 2:3].rearrange("(h d) c -> d (h c)", d=64), reads=[hsb], writes=[hsb], allow_slow_non_contiguous=True)
                vop(lambda e, nm=nm: e.scalar_tensor_tensor(out=X[nm][:, :, 0:1], in0=hs[0:64, 0, :].unsqueeze(2), scalar=flagE[0:64, 0:1], in1=hs[0:64, 1, :].unsqueeze(2),
                                                            op0=ALU.mult, op1=ALU.add), [hsb, fb, Xb[nm]], [Xb[nm]])
            for j, (r0, nr) in enumerate(((12 * 128, 32), (12 * 128 + 32, 32), (13 * 128, 96))):
                P.dma(SP, hs[0:nr, 0, 0:1], pf[r0:r0 + nr, 18:19], reads=[pfb, hsb], writes=[hsb], allow_slow_non_contiguous=True)
                P.dma(SP, hs[0:nr, 1, 0:1], prm["hist_in"][r0:r0 + nr, 2:3], reads=[hsb], writes=[hsb], allow_slow_non_contiguous=True)
                vop(lambda e, j=j, nr=nr: e.scalar_tensor_tensor(out=lraw[0:nr, j, 0:1], in0=hs[0:nr, 0, 0:1], scalar=flagE[0:nr, 0:1], in1=hs[0:nr, 1, 0:1],
                                                                 op0=ALU.mult, op1=ALU.add), [hsb, fb, lrawb], [lrawb])
            P.dma(SP, Hin[:], prm["sA_in"].rearrange("h k v -> k h v"), writes=[Hinb])
            vop(lambda e: e.scalar_tensor_tensor(out=H[:], in0=H[:], scalar=flagE[0:64, 0:1], in1=Hin[:], op0=ALU.mult, op1=ALU.add), [Hb, Hinb, fb], [Hb])
        for i, (src, dst) in enumerate((("pr", "xr"), ("pk", "xk"), ("pv", "xv"))):
            vop(lambda e, src=src, dst=dst: e.tensor_tensor(out=cur(dst), in0=prev(src), in1=cur(src), op=ALU.subtract), [Xb[src]], [Xb[dst]])
            vop(lambda e, dst=dst, i=i: e.tensor_tensor(out=cur(dst), in0=cur(dst), in1=bc8(muA[:, i, :], W), op=ALU.mult), [Xb[dst], pb_], [Xb[dst]])
            vop(lambda e, src=src, dst=dst: e.tensor_tensor(out=cur(dst), in0=cur(dst), in1=cur(src), op=ALU.add), [Xb[dst], Xb[src]], [Xb[dst]])
        for j, (dst, dstb, nr, fn) in enumerate(((thw, thwb, 32, AF.Tanh), (xal, xalb, 32, None), (sg, sgb, 96, AF.Sigmoid))):
            vop(lambda e, dst=dst, nr=nr, j=j: e.tensor_tensor(out=dst[0:nr, 0:W], in0=lraw[0:nr, j, 0:W], in1=lraw[0:nr, j, 1:W1], op=ALU.subtract), [lrawb], [dstb])
            vop(lambda e, dst=dst, nr=nr, j=j: e.scalar_tensor_tensor(out=dst[0:nr, 0:W], in0=dst[0:nr, 0:W], scalar=muL[0:nr, j:j + 1], in1=lraw[0:nr, j, 1:W1],
                                                                    op0=ALU.mult, op1=ALU.add), [lrawb, dstb, pb_], [dstb])
            if fn is not None:
                P.op(ACT, lambda e, dst=dst, nr=nr, fn=fn: e.activation(out=dst[0:nr, 0:W], in_=dst[0:nr, 0:W], func=fn), reads=[dstb], writes=[dstb])
        for (wt_, src, srcb, dst, chi, b0) in ((w2, thw, thwb, "sgz", 0, 0), (a2, xal, xalb, "asig", 1, 2)):
            for h in range(8):
                bk = b0 + (h * W) // 512
                off = (h * W) % 512
                P.op(PE, lambda e, bk=bk, off=off, wt_=wt_, src=src, h=h: e.matmul(bank[bk][0:64, off:off + W], lhsT=wt_[0:32, h * 64:(h + 1) * 64], rhs=src[0:32, 0:W],
                                                                                  start=True, stop=True), reads=[pb_, srcb], writes=[bkb[bk]])
            nb = (8 * W + 511) // 512
            for b in range(nb):
                h0 = b * (512 // W) if W >= 64 else 0
                nh = (512 // W) if W >= 64 else 8
                vop(lambda e, b=b, b0=b0, dst=dst, chi=chi, h0=h0, nh=nh: e.tensor_tensor(
                    out=X[dst][:, h0:h0 + nh, 1:W1], in0=bank[b0 + b][0:64, 0:nh * W].rearrange("p (h w) -> p h w", h=nh),
                    in1=ch[:, chi, h0:h0 + nh].unsqueeze(2).broadcast_to([64, nh, W]), op=ALU.add), [bkb[b0 + b], pb_], [Xb[dst]])
            P.op(ACT, lambda e, dst=dst: e.activation(out=cur(dst), in_=cur(dst), func=AF.Sigmoid), reads=[Xb[dst]], writes=[Xb[dst]])
        vop(lambda e: e.tensor_tensor(out=cur("kkn"), in0=cur("xk"), in1=bc8(ch[:, 2, :], W), op=ALU.mult), [Xb["xk"], pb_], [Xb["kkn"]])
        vop(lambda e: e.tensor_tensor(out=Fl[:, 0:8 * W].rearrange("p (h w) -> p h w", h=8), in0=cur("kkn"), in1=cur("kkn"), op=ALU.mult), [Xb["kkn"]], [Flb])
        nb = (8 * W + 511) // 512
        for b in range(nb):
            nn_ = min(512, 8 * W - b * 512)
            P.op(PE, lambda e, b=b, nn_=nn_: e.matmul(bank[4 + b][0:64, 0:nn_], lhsT=ones[0:64, 0:64], rhs=Fl[:, b * 512:b * 512 + nn_], start=True, stop=True),
                 reads=[onesb, Flb], writes=[bkb[4 + b]])
        for b in range(nb):
            nn_ = min(512, 8 * W - b * 512)
            vop(lambda e, b=b, nn_=nn_: e.tensor_scalar(out=Fl[:, b * 512:b * 512 + nn_], in0=bank[4 + b][0:64, 0:nn_],
                                                        scalar1=1e-24, scalar2=None, op0=ALU.max), [bkb[4 + b], Flb], [Flb])
        relf = Fl[:, 0:8 * W]
        P.op(POOL, lambda e, relf=relf: e.tensor_tensor(out=relf, in0=relf, in1=mh[:, 0:8 * W], op=ALU.pow), reads=[Flb, mhb], writes=[Flb])
        vop(lambda e, relf=relf: e.tensor_tensor(out=cur("kkn"), in0=cur("kkn"), in1=relf.rearrange("p (h w) -> p h w", h=8), op=ALU.mult),
            [Xb["kkn"], Flb], [Xb["kkn"]])
        vop(lambda e: e.tensor_tensor(out=cur("t1"), in0=cur("asig"), in1=bc8(ch[:, 3, :], W), op=ALU.mult), [Xb["asig"], pb_], [Xb["t1"]])
        vop(lambda e: e.tensor_tensor(out=cur("t1"), in0=cur("t1"), in1=bc8(ch[:, 4, :], W), op=ALU.add), [Xb["t1"], pb_], [Xb["t1"]])
        vop(lambda e: e.tensor_tensor(out=cur("xk"), in0=cur("xk"), in1=cur("t1"), op=ALU.mult), [Xb["xk"], Xb["t1"]], [Xb["xk"]])
        vop(lambda e: e.tensor_tensor(out=cur("asig"), in0=cur("asig"), in1=cur("kkn"), op=ALU.mult), [Xb["asig"], Xb["kkn"]], [Xb["asig"]])
        vop(lambda e: e.tensor_tensor(out=cur("t1"), in0=cur("xr"), in1=cur("xk"), op=ALU.mult), [Xb["xr"], Xb["xk"], Xb["t1"]], [Xb["t1"]])
        vop(lambda e: e.tensor_tensor(out=cur("pr"), in0=cur("t1"), in1=bc8(rk[:, :, 0], W), op=ALU.mult), [Xb["t1"], pb_, Xb["pr"], Xb["xr"]], [Xb["pr"]])
        vop(lambda e: e.tensor_copy(out=Fl[:, 0:8 * W].rearrange("p (h w) -> p h w", h=8), in_=cur("sgz")), [Xb["sgz"], Flb], [Flb])
        vop(lambda e: e.tensor_tensor_scan(out=Ssc[:, 1:1 + 8 * W], data0=ones[0:64, 0:8 * W], data1=Fl[:, 0:8 * W], initial=0.0, op0=ALU.mult, op1=ALU.add),
            [Flb, Sscb, onesb], [Sscb])
        vop(lambda e: e.tensor_tensor(out=cur("rel").rearrange("p h (c j) -> p h c j", j=n),
                                      in0=Ssc[:, 1:1 + 8 * W].rearrange("p (h c j) -> p h c j", h=8, j=n),
                                      in1=Ssc[:, 0:8 * W].rearrange("p (h c j) -> p h c j", h=8, j=n)[:, :, :, 0:1].broadcast_to([64, 8, nch, n]), op=ALU.subtract),
            [Sscb, Xb["rel"]], [Xb["rel"]])
        P.op(ACT, lambda e: e.activation(out=cur("G1"), in_=cur("rel"), func=AF.Exp, scale=-LDK), reads=[Xb["rel"]], writes=[Xb["G1"]])
        P.op(ACT, lambda e: e.activation(out=cur("G2"), in_=cur("rel"), func=AF.Exp, scale=LDK), reads=[Xb["rel"]], writes=[Xb["G2"]])
        vop(lambda e: e.tensor_tensor(out=cur("rel"), in0=cur("rel"), in1=cur("sgz"), op=ALU.subtract), [Xb["rel"], Xb["sgz"]], [Xb["rel"]])
        P.op(ACT, lambda e: e.activation(out=cur("rel"), in_=cur("rel"), func=AF.Exp, scale=-LDK), reads=[Xb["rel"]], writes=[Xb["rel"]])
        vop(lambda e: e.tensor_tensor(out=cur("xr"), in0=cur("xr"), in1=cur("G1"), op=ALU.mult), [Xb["xr"], Xb["G1"]], [Xb["xr"]])
        vop(lambda e: e.tensor_tensor(out=cur("xk"), in0=cur("xk"), in1=cur("G2"), op=ALU.mult), [Xb["xk"], Xb["G2"]], [Xb["xk"]])
        vop(lambda e: e.tensor_tensor(out=cur("asig"), in0=cur("asig"), in1=cur("G2"), op=ALU.mult), [Xb["asig"], Xb["G2"]], [Xb["asig"]])
        vop(lambda e: e.scalar_tensor_tensor(out=cur("kkn"), in0=cur("kkn"), scalar=-1.0, in1=cur("rel"), op0=ALU.mult, op1=ALU.mult),
            [Xb["kkn"], Xb["rel"]], [Xb["kkn"]])
        RT, KT, BT, AT, XV, PRK, G1 = "xr", "xk", "asig", "kkn", "xv", "pr", "G1"
        col = lambda nm, h, c: X[nm][:, h, 1 + c * n:1 + (c + 1) * n]
        qi = lambda h, c: h * nch + c
        P.phase = "rwkv_gram"
        for a, nm in enumerate((XV, KT, BT)):
            for h in range(8):
                for c in range(nch):
                    q = qi(h, c)
                    bk, off = (q * 64) // 512, (q * 64) % 512
                    P.op(PE, lambda e, bk=bk, off=off, nm=nm, h=h, c=c: e.transpose(out=bank[bk][0:n, off:off + 64], in_=col(nm, h, c), identity=ident[0:64, 0:64]),
                         reads=[Xb[nm], idb], writes=[bkb[bk]])
            for b in range((nq * 64 + 511) // 512):
                qn = min(8, nq - b * 8)
                P.op(ACT, lambda e, a=a, b=b, qn=qn: e.activation(out=TM[0:n, a, b * 8:b * 8 + qn, :], in_=bank[b][0:n, 0:qn * 64].rearrange("p (q d) -> p q d", q=qn),
                                                               func=AF.Copy), reads=[bkb[b]], pwrites=[TMb])
        pairs = ((KT, AT), (KT, RT), (BT, AT), (BT, RT), (AT, BT))
        for j, (l_, r_) in enumerate(pairs):
            b0 = 4 if j % 2 else 0
            for h in range(8):
                for c in range(nch):
                    q = qi(h, c)
                    bk, off = b0 + (q * 64) // 512, (q * 64) % 512
                    P.op(PE, lambda e, bk=bk, off=off, l_=l_, r_=r_, h=h, c=c: e.matmul(bank[bk][0:n, off:off + n], lhsT=col(l_, h, c), rhs=col(r_, h, c), start=True, stop=True),
                         reads=[Xb[l_], Xb[r_]], writes=[bkb[bk]])
            for b in range((nq * 64 + 511) // 512):
                qn = min(8, nq - b * 8)
                vop(lambda e, j=j, b=b, b0=b0, qn=qn: e.tensor_tensor(out=MM[0:n, j, b * 8:b * 8 + qn, 0:n],
                                                                    in0=bank[b0 + b][0:n, 0:qn * 64].rearrange("p (q d) -> p q d", q=qn)[:, :, 0:n],
                                                                    in1=mask5[0:n, j, 0:n].unsqueeze(1).broadcast_to([n, qn, n]), op=ALU.mult),
                    [bkb[b0 + b], m5b], [], pwrites=[MMb])
        P.phase = "rwkv_inv"
        vop(lambda e: e.tensor_tensor(out=Pm[0:n, 0:nq, 0:n], in0=MM[0:n, 2, 0:nq, 0:n], in1=ident[0:n, 0:n].unsqueeze(1).broadcast_to([n, nq, n]), op=ALU.add),
            [MMb, idb], [Pmb])
        curN = lambda q: MM[0:n, 2, q, 0:n]
        curNT = lambda q: MM[0:n, 4, q, 0:n]
        curb = MMb
        nlev = 5 if n == 64 else 3
        for lev in range(nlev):
            nn, nnb = NN[lev % 2], NNb[lev % 2]
            for q in range(nq):
                bk, off = (q * 128) // 512, (q * 128) % 512
                P.op(PE, lambda e, bk=bk, off=off, a_=curNT(q), b_=curN(q): e.matmul(bank[bk][0:n, off:off + n], lhsT=a_, rhs=b_, start=True, stop=True), reads=[curb], writes=[bkb[bk]])
                P.op(PE, lambda e, bk=bk, off=off, a_=curN(q), b_=curNT(q): e.matmul(bank[bk][0:n, off + 64:off + 64 + n], lhsT=a_, rhs=b_, start=True, stop=True), reads=[curb], writes=[bkb[bk]])
            for b in range((nq * 128 + 511) // 512):
                qn = min(4, nq - b * 4)
                P.op(ACT, lambda e, nn=nn, b=b, qn=qn: e.activation(out=nn[0:n, b * 4:b * 4 + qn, :, 0:n],
                                                                 in_=bank[b][0:n, 0:qn * 128].rearrange("p (q j d) -> p q j d", q=qn, j=2)[:, :, :, 0:n], func=AF.Copy),
                     reads=[bkb[b]], pwrites=[nnb])
            curN = lambda q, nn=nn: nn[0:n, q, 0, 0:n]
            curNT = lambda q, nn=nn: nn[0:n, q, 1, 0:n]
            curb = nnb
            for q in range(nq):
                bk, off = 4 + (q * 64) // 512, (q * 64) % 512
                P.op(PE, lambda e, bk=bk, off=off, a_=curNT(q), q=q: e.matmul(bank[bk][0:n, off:off + n], lhsT=a_, rhs=Pm[0:n, q, 0:n], start=True, stop=True), reads=[curb, Pmb], writes=[bkb[bk]])
            for b in range((nq * 64 + 511) // 512):
                qn = min(8, nq - b * 8)
                vop(lambda e, b=b, qn=qn: e.tensor_tensor(out=Pm[0:n, b * 8:b * 8 + qn, 0:n], in0=Pm[0:n, b * 8:b * 8 + qn, 0:n],
                                                          in1=bank[4 + b][0:n, 0:qn * 64].rearrange("p (q d) -> p q d", q=qn)[:, :, 0:n], op=ALU.add),
                    [bkb[4 + b], Pmb], [Pmb])
        P.phase = "rwkv_chain"
        for c in range(nch):
            tc0 = t0 + c * n
            for h in range(8):
                q = qi(h, c)
                P.op(PE, lambda e, h=h, c=c: e.matmul(bank[0][0:n, h * 64:(h + 1) * 64], lhsT=col(AT, h, c), rhs=H[:, h, :], start=True, stop=False), reads=[Xb[AT], Hb], writes=[bkb[0]])
                P.op(PE, lambda e, h=h, q=q: e.matmul(bank[0][0:n, h * 64:(h + 1) * 64], lhsT=MM[0:n, 0, q, 0:n], rhs=TM[0:n, 0, q, :], start=False, stop=True), reads=[MMb, TMb], writes=[bkb[0]])
            P.op(ACT, lambda e: e.activation(out=W0s[0:n, :, :], in_=bank[0][0:n, 0:512].rearrange("p (h d) -> p h d", h=8), func=AF.Copy), reads=[bkb[0]], writes=[W0b])
            for h in range(8):
                q = qi(h, c)
                P.op(PE, lambda e, h=h, q=q: e.matmul(bank[1][0:n, h * 64:(h + 1) * 64], lhsT=Pm[0:n, q, 0:n], rhs=W0s[0:n, h, :], start=True, stop=True), reads=[Pmb, W0b], writes=[bkb[1]])
            vop(lambda e: e.tensor_copy(out=Us[0:n, :, :], in_=bank[1][0:n, 0:512].rearrange("p (h d) -> p h d", h=8)), [bkb[1]], [Usb])
            if C.emit_out:
                for h in range(8):
                    q = qi(h, c)
                    P.op(PE, lambda e, h=h, c=c: e.matmul(bank[2][0:n, h * 64:(h + 1) * 64], lhsT=col(RT, h, c), rhs=H[:, h, :], start=True, stop=False), reads=[Xb[RT], Hb], writes=[bkb[2]])
                    P.op(PE, lambda e, h=h, q=q: e.matmul(bank[2][0:n, h * 64:(h + 1) * 64], lhsT=MM[0:n, 3, q, 0:n], rhs=Us[0:n, h, :], start=False, stop=False), reads=[MMb, Usb], writes=[bkb[2]])
                    P.op(PE, lambda e, h=h, q=q: e.matmul(bank[2][0:n, h * 64:(h + 1) * 64], lhsT=MM[0:n, 1, q, 0:n], rhs=TM[0:n, 0, q, :], start=False, stop=True), reads=[MMb, TMb], writes=[bkb[2]])
            for h in range(8):
                q = qi(h, c)
                P.op(PE, lambda e, h=h: e.matmul(bank[3][0:64, h * 64:(h + 1) * 64], lhsT=ident[0:64, 0:64], rhs=H[:, h, :], start=True, stop=False), reads=[idb, Hb], writes=[bkb[3]])
                P.op(PE, lambda e, h=h, q=q: e.matmul(bank[3][0:64, h * 64:(h + 1) * 64], lhsT=TM[0:n, 2, q, :], rhs=Us[0:n, h, :], start=False, stop=False), reads=[TMb, Usb], writes=[bkb[3]])
                P.op(PE, lambda e, h=h, q=q: e.matmul(bank[3][0:64, h * 64:(h + 1) * 64], lhsT=TM[0:n, 1, q, :], rhs=TM[0:n, 0, q, :], start=False, stop=True), reads=[TMb], writes=[bkb[3]])
            ce = 1 + (c + 1) * n - 1
            vop(lambda e, ce=ce: e.tensor_tensor(out=H[:], in0=bank[3][0:64, 0:512].rearrange("p (h d) -> p h d", h=8),
                                                 in1=X[G1][:, :, ce:ce + 1].broadcast_to([64, 8, 64]), op=ALU.mult), [bkb[3], Xb[G1]], [Hb])
            if C.emit_out:
                P.op(PE, lambda e, c=c: e.matmul(bank[4][0:n, 0:512], lhsT=sg[0:96, c * n:(c + 1) * n], rhs=g2[0:96, :], start=True, stop=True), reads=[sgb, pb_], writes=[bkb[4]])
                for h in range(8):
                    P.op(PE, lambda e, h=h, c=c: e.matmul(bank[5][0:n, 2 * h:2 * h + 2], lhsT=col(PRK, h, c), rhs=ones[0:64, 0:2], start=True, stop=True), reads=[Xb[PRK], onesb], writes=[bkb[5]])
                P.op(ACT, lambda e: e.activation(out=GBs[0:n, 0:512], in_=bank[4][0:n, 0:512], func=AF.Copy), reads=[bkb[4]], writes=[GBb])
                P.op(ACT, lambda e: e.activation(out=GBs[0:n, 512:528], in_=bank[5][0:n, 0:16], func=AF.Copy), reads=[bkb[5], GBb], writes=[GBb])
                YG = bank[2][0:n, 0:512].rearrange("p (h d) -> p h d", h=8)
                bcn = lambda ap: ap.unsqueeze(2).broadcast_to([n, 8, 64])
                vop(lambda e, YG=YG: e.tensor_reduce(out=ST[0:n, 0, :], in_=YG, axis=AX.X, op=ALU.add), [bkb[2]], [STb])
                P.op(ACT, lambda e, YG=YG: e.activation(out=SQ[0:n, :, :], in_=YG, func=AF.Square), reads=[bkb[2]], writes=[SQb])
                vop(lambda e: e.tensor_reduce(out=ST[0:n, 1, :], in_=SQ[0:n, :, :], axis=AX.X, op=ALU.add), [SQb, STb], [STb])
                vop(lambda e: e.tensor_scalar(out=ST[0:n, 2, :], in0=ST[0:n, 0, :], scalar1=1.0 / 64, scalar2=None, op0=ALU.mult), [STb], [STb])
                vop(lambda e: e.tensor_tensor(out=ST[0:n, 3, :], in0=ST[0:n, 2, :], in1=ST[0:n, 2, :], op=ALU.mult), [STb], [STb])
                vop(lambda e: e.tensor_scalar(out=ST[0:n, 4, :], in0=ST[0:n, 1, :], scalar1=1.0 / 64, scalar2=64e-5, op0=ALU.mult, op1=ALU.add), [STb], [STb])
                vop(lambda e: e.tensor_tensor(out=ST[0:n, 4, :], in0=ST[0:n, 4, :], in1=ST[0:n, 3, :], op=ALU.subtract), [STb], [STb])
                P.op(POOL, lambda e: e.tensor_tensor(out=ST[0:n, 5, :], in0=ST[0:n, 4, :], in1=mh[0:n, 0:8], op=ALU.pow), reads=[STb, mhb], writes=[STb])
                vop(lambda e, YG=YG, bcn=bcn: e.tensor_tensor(out=T1[0:n, :, :], in0=YG, in1=bcn(ST[0:n, 2, :]), op=ALU.subtract), [bkb[2], STb], [T1b])
                vop(lambda e, bcn=bcn: e.tensor_tensor(out=T1[0:n, :, :], in0=T1[0:n, :, :], in1=bcn(ST[0:n, 5, :]), op=ALU.mult), [T1b, STb], [T1b])
                vop(lambda e: e.tensor_tensor(out=T1[0:n, :, :], in0=T1[0:n, :, :], in1=lnw[0:n, :].rearrange("p (h d) -> p h d", h=8), op=ALU.mult), [T1b, pb_], [T1b])
                vop(lambda e: e.tensor_tensor(out=T1[0:n, :, :], in0=T1[0:n, :, :], in1=lnb[0:n, :].rearrange("p (h d) -> p h d", h=8), op=ALU.add), [T1b, pb_], [T1b])
                vtm_c = TM[0:n, 0, 0:nq, :].rearrange("p (h c) d -> p h c d", c=nch)[:, :, c, :]
                bs_c = GBs[0:n, 512:528].rearrange("p (h t) -> p h t", t=2)[:, :, 0:1].broadcast_to([n, 8, 64])
                vop(lambda e, vtm_c=vtm_c, bs_c=bs_c: e.tensor_tensor(out=SQ[0:n, :, :], in0=vtm_c, in1=bs_c, op=ALU.mult), [TMb, GBb, SQb], [SQb])
                vop(lambda e: e.tensor_tensor(out=T1[0:n, :, :], in0=T1[0:n, :, :], in1=SQ[0:n, :, :], op=ALU.add), [T1b, SQb], [T1b])
                vop(lambda e: e.tensor_tensor(out=YO[0:n, :, :], in0=T1[0:n, :, :], in1=GBs[0:n, 0:512].rearrange("p (h d) -> p h d", h=8), op=ALU.mult), [T1b, GBb], [YOb])
                P.dma(SP, y[tc0:tc0 + n, 0:512], YO[0:n, :, :].rearrange("p h d -> p (h d)"), reads=[YOb], pwrites=[yb])
    for sci, (c0, W, t0, n) in enumerate(scs):
        do_sc(sci, c0, W, t0, n)
    P.dma(POOL, prm["sA_out"].rearrange("h k v -> k h v"), H[:], reads=[Hb], pwrites=[K.sob])


def mixer_rwkv3(C, st, pf, pfb, y, yb, prm, K):
    P = C.P
    ones, onesb, ident, idb, flagE, fb = K.ones, K.onesb, K.ident, K.idb, K.flagE, K.fb
    mask5, m5b = K.mask5, K.m5b
    muA = sbuf(C, st, "cmuA", [64, 3, 8]); muL = sbuf(C, st, "cmuL", [96, 3]); w2 = sbuf(C, st, "cw2", [32, 512]); a2 = sbuf(C, st, "ca2", [32, 512])
    g2 = sbuf(C, st, "cg2", [96, 512]); ch = sbuf(C, st, "cch", [64, 5, 8]); rk = sbuf(C, st, "crk", [64, 8, 2])
    lnw = sbuf(C, st, "clnw", [64, 512]); lnb = sbuf(C, st, "clnb", [64, 512])
    pb_ = Buf()
    for t, n_ in ((muA, "rw_muA"), (muL, "rw_muL"), (w2, "rw_w2"), (a2, "rw_a2"), (g2, "rw_g2"), (rk, "rw_rk"), (lnw, "rw_lnw_bc"), (lnb, "rw_lnb_bc")):
        P.dma(SP, t[:], prm[n_], pwrites=[pb_])
    P.dma(SP, ch[:, 0:4, :], prm["rw_ch"], pwrites=[pb_])
    P.op(DVE, lambda e: e.tensor_scalar(out=ch[:, 4, :], in0=ch[:, 3, :], scalar1=-1.0, scalar2=1.0, op0=ALU.mult, op1=ALU.add), reads=[pb_], writes=[pb_])
    WM = 128
    W1M = WM + 1
    mh = sbuf(C, st, "cmh", [64, 8 * WM], F32); mhb = Buf()
    P.op(POOL, lambda e: e.memset(mh[:], -0.5), writes=[mhb])
    names = ["pr", "pk", "pv", "xr", "xk", "xv", "sgz", "asig", "kkn", "t1", "rel", "G1", "G2"]
    X = {nm: sbuf(C, st, "cX" + nm, [64, 8, W1M]) for nm in names}
    Xb = {nm: Buf() for nm in names}
    Ssc = sbuf(C, st, "cSsc", [64, 1 + 8 * WM]); Sscb = Buf()
    P.op(DVE, lambda e: e.memset(Ssc[:, 0:1], 0.0), writes=[Sscb])
    lraw = sbuf(C, st, "clraw", [96, 3, W1M]); lrawb = Buf()
    thw = sbuf(C, st, "cthw", [32, WM]); xal = sbuf(C, st, "cxal", [32, WM]); sg = sbuf(C, st, "csg", [96, WM])
    thwb, xalb, sgb = Buf(), Buf(), Buf()
    hs = sbuf(C, st, "chs", [96, 2, 8]); hsb = Buf()
    H = sbuf(C, st, "cH", [64, 8, 64]); Hb = Buf()
    Hin = sbuf(C, st, "cHin", [64, 8, 64]); Hinb = Buf()
    NQ = 16
    TM = sbuf(C, st, "cTM", [64, 3, NQ, 64], BF16); TMb = Buf()
    MM = sbuf(C, st, "cMM", [64, 5, NQ, 64], BF16); MMb = Buf()
    NN = [sbuf(C, st, f"cNN{i}", [64, NQ, 2, 64], BF16) for i in range(2)]; NNb = [Buf(), Buf()]
    Pm = sbuf(C, st, "cPm", [64, NQ, 64], BF16); Pmb = Buf()
    W0s = sbuf(C, st, "cW0", [64, 8, 64], BF16); W0b = Buf()
    Us = sbuf(C, st, "cUs", [64, 8, 64], BF16); Usb = Buf()
    GBs = sbuf(C, st, "cGB", [64, 528]); GBb = Buf()
    T1 = sbuf(C, st, "cT1", [64, 8, 64]); T1b = Buf()
    SQ = sbuf(C, st, "cSQ", [64, 8, 64]); SQb = Buf()
    YO = sbuf(C, st, "cYO", [64, 8, 64], BF16); YOb = Buf()
    ST = sbuf(C, st, "cST", [64, 6, 8]); STb = Buf()
    bank = [psum(C, st, f"cpb{i}", [128, 512], F32) for i in range(8)]
    bkb = [Buf() for _ in range(8)]
    P.op(DVE, lambda e: e.memset(H[:], 0.0), writes=[Hb])
    H16 = sbuf(C, st, "cH16", [64, 8, 64], BF16); H16b = Buf()
    P.op(DVE, lambda e: e.memset(H16[:], 0.0), writes=[H16b])
    XBF = {nm: sbuf(C, st, "cXB" + nm, [64, 8, W1M], BF16) for nm in ("xr", "xk", "asig", "kkn")}
    XBFb = {nm: Buf() for nm in XBF}

    def bc8(ap, W):
        return ap.unsqueeze(2).broadcast_to([64, 8, W])

    def vop(fn, reads, writes, pwrites=()):
        P.op(DVE, fn, reads=reads, writes=writes, pwrites=pwrites)

    Fl = sbuf(C, st, "cFl", [64, 8 * WM]); Flb = Buf()

    scs = [(3, 16, 0, 16)] + [(22 + 128 * i, 128, 16 + 128 * i, 64) for i in range(16)]
    def do_sc(sci, c0, W, t0, n):
        W1 = W + 1
        nch = W // n
        nq = 8 * nch
        cur = lambda nm: X[nm][:, :, 1:W1]
        prev = lambda nm: X[nm][:, :, 0:W]
        P.phase = "rwkv_pre"
        for i, nm in enumerate(("pr", "pk", "pv")):
            P.dma(SP, X[nm][:, :, 0:W1], pf[i * 512:(i + 1) * 512, c0 - 1:c0 + W].rearrange("(h d) c -> d h c", d=64), reads=[pfb], writes=[Xb[nm]])
        for j, (r0, nr) in enumerate(((12 * 128, 32), (12 * 128 + 32, 32), (13 * 128, 96))):
            P.dma(SP, lraw[0:nr, j, 0:W1], pf[r0:r0 + nr, c0 - 1:c0 + W], reads=[pfb], writes=[lrawb])
        if sci == 0:
            for nm in ("pr", "pk", "pv"):
                vop(lambda e, nm=nm: e.memset(X[nm][:, :, 0:1], 0.0), [Xb[nm]], [Xb[nm]])
            vop(lambda e: e.memset(lraw[:, :, 0:1], 0.0), [lrawb], [lrawb])
        if sci == 1:
            for i, nm in enumerate(("pr", "pk", "pv")):
                P.dma(SP, hs[0:64, 0, :], pf[i * 512:(i + 1) * 512, 18:19].rearrange("(h d) c -> d (h c)", d=64), reads=[pfb], writes=[hsb], allow_slow_non_contiguous=True)
                P.dma(SP, hs[0:64, 1, :], prm["hist_in"][i * 512:(i + 1) * 512, 2:3].rearrange("(h d) c -> d (h c)", d=64), reads=[hsb], writes=[hsb], allow_slow_non_contiguous=True)
                vop(lambda e, nm=nm: e.scalar_tensor_tensor(out=X[nm][:, :, 0:1], in0=hs[0:64, 0, :].unsqueeze(2), scalar=flagE[0:64, 0:1], in1=hs[0:64, 1, :].unsqueeze(2),
                                                            op0=ALU.mult, op1=ALU.add), [hsb, fb, Xb[nm]], [Xb[nm]])
            for j, (r0, nr) in enumerate(((12 * 128, 32), (12 * 128 + 32, 32), (13 * 128, 96))):
                P.dma(SP, hs[0:nr, 0, 0:1], pf[r0:r0 + nr, 18:19], reads=[pfb, hsb], writes=[hsb], allow_slow_non_contiguous=True)
                P.dma(SP, hs[0:nr, 1, 0:1], prm["hist_in"][r0:r0 + nr, 2:3], reads=[hsb], writes=[hsb], allow_slow_non_contiguous=True)
                vop(lambda e, j=j, nr=nr: e.scalar_tensor_tensor(out=lraw[0:nr, j, 0:1], in0=hs[0:nr, 0, 0:1], scalar=flagE[0:nr, 0:1], in1=hs[0:nr, 1, 0:1],
                                                                 op0=ALU.mult, op1=ALU.add), [hsb, fb, lrawb], [lrawb])
            P.dma(SP, Hin[:], prm["sA_in"].rearrange("h k v -> k h v"), writes=[Hinb])
            vop(lambda e: e.scalar_tensor_tensor(out=H[:], in0=H[:], scalar=flagE[0:64, 0:1], in1=Hin[:], op0=ALU.mult, op1=ALU.add), [Hb, Hinb, fb], [Hb])
            P.op(ACT, lambda e: e.activation(out=H16[:], in_=H[:], func=AF.Copy), reads=[Hb], writes=[H16b])
        for i, (src, dst) in enumerate((("pr", "xr"), ("pk", "xk"), ("pv", "xv"))):
            vop(lambda e, src=src, dst=dst: e.tensor_tensor(out=cur(dst), in0=prev(src), in1=cur(src), op=ALU.subtract), [Xb[src]], [Xb[dst]])
            vop(lambda e, dst=dst, i=i: e.tensor_tensor(out=cur(dst), in0=cur(dst), in1=bc8(muA[:, i, :], W), op=ALU.mult), [Xb[dst], pb_], [Xb[dst]])
            vop(lambda e, src=src, dst=dst: e.tensor_tensor(out=cur(dst), in0=cur(dst), in1=cur(src), op=ALU.add), [Xb[dst], Xb[src]], [Xb[dst]])
        for j, (dst, dstb, nr, fn) in enumerate(((thw, thwb, 32, AF.Tanh), (xal, xalb, 32, None), (sg, sgb, 96, AF.Sigmoid))):
            vop(lambda e, dst=dst, nr=nr, j=j: e.tensor_tensor(out=dst[0:nr, 0:W], in0=lraw[0:nr, j, 0:W], in1=lraw[0:nr, j, 1:W1], op=ALU.subtract), [lrawb], [dstb])
            vop(lambda e, dst=dst, nr=nr, j=j: e.scalar_tensor_tensor(out=dst[0:nr, 0:W], in0=dst[0:nr, 0:W], scalar=muL[0:nr, j:j + 1], in1=lraw[0:nr, j, 1:W1],
                                                                    op0=ALU.mult, op1=ALU.add), [lrawb, dstb, pb_], [dstb])
            if fn is not None:
                P.op(ACT, lambda e, dst=dst, nr=nr, fn=fn: e.activation(out=dst[0:nr, 0:W], in_=dst[0:nr, 0:W], func=fn), reads=[dstb], writes=[dstb])
        for (wt_, src, srcb, dst, chi, b0) in ((w2, thw, thwb, "sgz", 0, 0), (a2, xal, xalb, "asig", 1, 2)):
            for h in range(8):
                bk = b0 + (h * W) // 512
                off = (h * W) % 512
                P.op(PE, lambda e, bk=bk, off=off, wt_=wt_, src=src, h=h: e.matmul(bank[bk][0:64, off:off + W], lhsT=wt_[0:32, h * 64:(h + 1) * 64], rhs=src[0:32, 0:W],
                                                                                  start=True, stop=True), reads=[pb_, srcb], writes=[bkb[bk]])
            nb = (8 * W + 511) // 512
            for b in range(nb):
                h0 = b * (512 // W) if W >= 64 else 0
                nh = (512 // W) if W >= 64 else 8
                vop(lambda e, b=b, b0=b0, dst=dst, chi=chi, h0=h0, nh=nh: e.tensor_tensor(
                    out=X[dst][:, h0:h0 + nh, 1:W1], in0=bank[b0 + b][0:64, 0:nh * W].rearrange("p (h w) -> p h w", h=nh),
                    in1=ch[:, chi, h0:h0 + nh].unsqueeze(2).broadcast_to([64, nh, W]), op=ALU.add), [bkb[b0 + b], pb_], [Xb[dst]])
            P.op(ACT, lambda e, dst=dst: e.activation(out=cur(dst), in_=cur(dst), func=AF.Sigmoid), reads=[Xb[dst]], writes=[Xb[dst]])
        vop(lambda e: e.tensor_tensor(out=cur("kkn"), in0=cur("xk"), in1=bc8(ch[:, 2, :], W), op=ALU.mult), [Xb["xk"], pb_], [Xb["kkn"]])
        vop(lambda e: e.tensor_tensor(out=Fl[:, 0:8 * W].rearrange("p (h w) -> p h w", h=8), in0=cur("kkn"), in1=cur("kkn"), op=ALU.mult), [Xb["kkn"]], [Flb])
        nb = (8 * W + 511) // 512
        for b in range(nb):
            nn_ = min(512, 8 * W - b * 512)
            P.op(PE, lambda e, b=b, nn_=nn_: e.matmul(bank[4 + b][0:64, 0:nn_], lhsT=ones[0:64, 0:64], rhs=Fl[:, b * 512:b * 512 + nn_], start=True, stop=True),
                 reads=[onesb, Flb], writes=[bkb[4 + b]])
        for b in range(nb):
            nn_ = min(512, 8 * W - b * 512)
            vop(lambda e, b=b, nn_=nn_: e.tensor_scalar(out=Fl[:, b * 512:b * 512 + nn_], in0=bank[4 + b][0:64, 0:nn_],
                                                        scalar1=1e-24, scalar2=None, op0=ALU.max), [bkb[4 + b], Flb], [Flb])
        relf = Fl[:, 0:8 * W]
        P.op(ACT, lambda e, relf=relf: e.activation(out=relf, in_=relf, func=AF.Sqrt), reads=[Flb], writes=[Flb])
        vop(lambda e, relf=relf: e.reciprocal(out=relf, in_=relf), [Flb], [Flb])
        vop(lambda e, relf=relf: e.tensor_tensor(out=cur("kkn"), in0=cur("kkn"), in1=relf.rearrange("p (h w) -> p h w", h=8), op=ALU.mult),
            [Xb["kkn"], Flb], [Xb["kkn"]])
        vop(lambda e: e.tensor_tensor(out=cur("t1"), in0=cur("asig"), in1=bc8(ch[:, 3, :], W), op=ALU.mult), [Xb["asig"], pb_], [Xb["t1"]])
        vop(lambda e: e.tensor_tensor(out=cur("t1"), in0=cur("t1"), in1=bc8(ch[:, 4, :], W), op=ALU.add), [Xb["t1"], pb_], [Xb["t1"]])
        vop(lambda e: e.tensor_tensor(out=cur("xk"), in0=cur("xk"), in1=cur("t1"), op=ALU.mult), [Xb["xk"], Xb["t1"]], [Xb["xk"]])
        vop(lambda e: e.tensor_tensor(out=cur("asig"), in0=cur("asig"), in1=cur("kkn"), op=ALU.mult), [Xb["asig"], Xb["kkn"]], [Xb["asig"]])
        vop(lambda e: e.tensor_tensor(out=cur("t1"), in0=cur("xr"), in1=cur("xk"), op=ALU.mult), [Xb["xr"], Xb["xk"], Xb["t1"]], [Xb["t1"]])
        vop(lambda e: e.tensor_tensor(out=cur("pr"), in0=cur("t1"), in1=bc8(rk[:, :, 0], W), op=ALU.mult), [Xb["t1"], pb_, Xb["pr"], Xb["xr"]], [Xb["pr"]])
        vop(lambda e: e.tensor_copy(out=Fl[:, 0:8 * W].rearrange("p (h w) -> p h w", h=8), in_=cur("sgz")), [Xb["sgz"], Flb], [Flb])
        vop(lambda e: e.tensor_tensor_scan(out=Ssc[:, 1:1 + 8 * W], data0=ones[0:64, 0:8 * W], data1=Fl[:, 0:8 * W], initial=0.0, op0=ALU.mult, op1=ALU.add),
            [Flb, Sscb, onesb], [Sscb])
        vop(lambda e: e.tensor_tensor(out=cur("rel").rearrange("p h (c j) -> p h c j", j=n),
                                      in0=Ssc[:, 1:1 + 8 * W].rearrange("p (h c j) -> p h c j", h=8, j=n),
                                      in1=Ssc[:, 0:8 * W].rearrange("p (h c j) -> p h c j", h=8, j=n)[:, :, :, 0:1].broadcast_to([64, 8, nch, n]), op=ALU.subtract),
            [Sscb, Xb["rel"]], [Xb["rel"]])
        P.op(ACT, lambda e: e.activation(out=cur("G1"), in_=cur("rel"), func=AF.Exp, scale=-LDK), reads=[Xb["rel"]], writes=[Xb["G1"]])
        P.op(ACT, lambda e: e.activation(out=cur("G2"), in_=cur("rel"), func=AF.Exp, scale=LDK), reads=[Xb["rel"]], writes=[Xb["G2"]])
        vop(lambda e: e.tensor_tensor(out=cur("rel"), in0=cur("rel"), in1=cur("sgz"), op=ALU.subtract), [Xb["rel"], Xb["sgz"]], [Xb["rel"]])
        P.op(ACT, lambda e: e.activation(out=cur("rel"), in_=cur("rel"), func=AF.Exp, scale=-LDK), reads=[Xb["rel"]], writes=[Xb["rel"]])
        vop(lambda e: e.tensor_tensor(out=cur("xr"), in0=cur("xr"), in1=cur("G1"), op=ALU.mult), [Xb["xr"], Xb["G1"]], [Xb["xr"]])
        vop(lambda e: e.tensor_tensor(out=cur("xk"), in0=cur("xk"), in1=cur("G2"), op=ALU.mult), [Xb["xk"], Xb["G2"]], [Xb["xk"]])
        vop(lambda e: e.tensor_tensor(out=cur("asig"), in0=cur("asig"), in1=cur("G2"), op=ALU.mult), [Xb["asig"], Xb["G2"]], [Xb["asig"]])
        vop(lambda e: e.scalar_tensor_tensor(out=cur("kkn"), in0=cur("kkn"), scalar=-1.0, in1=cur("rel"), op0=ALU.mult, op1=ALU.mult),
            [Xb["kkn"], Xb["rel"]], [Xb["kkn"]])
        for nm in ("xr", "xk", "asig", "kkn"):
            P.op(ACT, lambda e, nm=nm: e.activation(out=XBF[nm][:, :, 1:W1], in_=cur(nm), func=AF.Copy), reads=[Xb[nm]], writes=[XBFb[nm]])
        RT, KT, BT, AT, XV, PRK, G1 = "xr", "xk", "asig", "kkn", "xv", "pr", "G1"
        colb = lambda nm, h, c: XBF[nm][:, h, 1 + c * n:1 + (c + 1) * n]
        col = lambda nm, h, c: X[nm][:, h, 1 + c * n:1 + (c + 1) * n]
        qi = lambda h, c: h * nch + c
        P.phase = "rwkv_gram"
        for a, nm in enumerate((XV, KT, BT)):
            for h in range(8):
                for c in range(nch):
                    q = qi(h, c)
                    bk, off = (q * 64) // 512, (q * 64) % 512
                    P.op(PE, lambda e, bk=bk, off=off, nm=nm, h=h, c=c: e.transpose(out=bank[bk][0:n, off:off + 64], in_=col(nm, h, c), identity=ident[0:64, 0:64]),
                         reads=[Xb[nm], idb], writes=[bkb[bk]])
            for b in range((nq * 64 + 511) // 512):
                qn = min(8, nq - b * 8)
                P.op(ACT, lambda e, a=a, b=b, qn=qn: e.activation(out=TM[0:n, a, b * 8:b * 8 + qn, :], in_=bank[b][0:n, 0:qn * 64].rearrange("p (q d) -> p q d", q=qn),
                                                               func=AF.Copy), reads=[bkb[b]], pwrites=[TMb])
        pairs = ((KT, AT), (KT, RT), (BT, AT), (BT, RT), (AT, BT))
        for j, (l_, r_) in enumerate(pairs):
            b0 = 4 if j % 2 else 0
            for h in range(8):
                for c in range(nch):
                    q = qi(h, c)
                    bk, off = b0 + (q * 64) // 512, (q * 64) % 512
                    P.op(PE, lambda e, bk=bk, off=off, l_=l_, r_=r_, h=h, c=c: e.matmul(bank[bk][0:n, off:off + n], lhsT=colb(l_, h, c), rhs=colb(r_, h, c), start=True, stop=True),
                         reads=[XBFb[l_], XBFb[r_]], writes=[bkb[bk]])
            for b in range((nq * 64 + 511) // 512):
                qn = min(8, nq - b * 8)
                vop(lambda e, j=j, b=b, b0=b0, qn=qn: e.tensor_tensor(out=MM[0:n, j, b * 8:b * 8 + qn, 0:n],
                                                                    in0=bank[b0 + b][0:n, 0:qn * 64].rearrange("p (q d) -> p q d", q=qn)[:, :, 0:n],
                                                                    in1=mask5[0:n, j, 0:n].unsqueeze(1).broadcast_to([n, qn, n]), op=ALU.mult),
                    [bkb[b0 + b], m5b], [], pwrites=[MMb])
        P.phase = "rwkv_inv"
        vop(lambda e: e.tensor_tensor(out=Pm[0:n, 0:nq, 0:n], in0=MM[0:n, 2, 0:nq, 0:n], in1=ident[0:n, 0:n].unsqueeze(1).broadcast_to([n, nq, n]), op=ALU.add),
            [MMb, idb], [Pmb])
        curN = lambda q: MM[0:n, 2, q, 0:n]
        curNT = lambda q: MM[0:n, 4, q, 0:n]
        curb = MMb
        nlev = 5 if n == 64 else 3
        for lev in range(nlev):
            nn, nnb = NN[lev % 2], NNb[lev % 2]
            for q in range(nq):
                bk, off = (q * 128) // 512, (q * 128) % 512
                P.op(PE, lambda e, bk=bk, off=off, a_=curNT(q), b_=curN(q): e.matmul(bank[bk][0:n, off:off + n], lhsT=a_, rhs=b_, start=True, stop=True), reads=[curb], writes=[bkb[bk]])
                P.op(PE, lambda e, bk=bk, off=off, a_=curN(q), b_=curNT(q): e.matmul(bank[bk][0:n, off + 64:off + 64 + n], lhsT=a_, rhs=b_, start=True, stop=True), reads=[curb], writes=[bkb[bk]])
            for b in range((nq * 128 + 511) // 512):
                qn = min(4, nq - b * 4)
                P.op(ACT, lambda e, nn=nn, b=b, qn=qn: e.activation(out=nn[0:n, b * 4:b * 4 + qn, :, 0:n],
                                                                 in_=bank[b][0:n, 0:qn * 128].rearrange("p (q j d) -> p q j d", q=qn, j=2)[:, :, :, 0:n], func=AF.Copy),
                     reads=[bkb[b]], pwrites=[nnb])
            curN = lambda q, nn=nn: nn[0:n, q, 0, 0:n]
            curNT = lambda q, nn=nn: nn[0:n, q, 1, 0:n]
            curb = nnb
            for q in range(nq):
                bk, off = 4 + (q * 64) // 512, (q * 64) % 512
                P.op(PE, lambda e, bk=bk, off=off, a_=curNT(q), q=q: e.matmul(bank[bk][0:n, off:off + n], lhsT=a_, rhs=Pm[0:n, q, 0:n], start=True, stop=True), reads=[curb, Pmb], writes=[bkb[bk]])
            for b in range((nq * 64 + 511) // 512):
                qn = min(8, nq - b * 8)
                vop(lambda e, b=b, qn=qn: e.tensor_tensor(out=Pm[0:n, b * 8:b * 8 + qn, 0:n], in0=Pm[0:n, b * 8:b * 8 + qn, 0:n],
                                                          in1=bank[4 + b][0:n, 0:qn * 64].rearrange("p (q d) -> p q d", q=qn)[:, :, 0:n], op=ALU.add),
                    [bkb[4 + b], Pmb], [Pmb])
        P.phase = "rwkv_chain"
        for c in range(nch):
            tc0 = t0 + c * n
            for h in range(8):
                q = qi(h, c)
                P.op(PE, lambda e, h=h, c=c: e.matmul(bank[0][0:n, h * 64:(h + 1) * 64], lhsT=colb(AT, h, c), rhs=H16[:, h, :], start=True, stop=False), reads=[XBFb[AT], H16b], writes=[bkb[0]])
                P.op(PE, lambda e, h=h, q=q: e.matmul(bank[0][0:n, h * 64:(h + 1) * 64], lhsT=MM[0:n, 0, q, 0:n], rhs=TM[0:n, 0, q, :], start=False, stop=True), reads=[MMb, TMb], writes=[bkb[0]])
            P.op(ACT, lambda e: e.activation(out=W0s[0:n, :, :], in_=bank[0][0:n, 0:512].rearrange("p (h d) -> p h d", h=8), func=AF.Copy), reads=[bkb[0]], writes=[W0b])
            for h in range(8):
                q = qi(h, c)
                P.op(PE, lambda e, h=h, q=q: e.matmul(bank[1][0:n, h * 64:(h + 1) * 64], lhsT=Pm[0:n, q, 0:n], rhs=W0s[0:n, h, :], start=True, stop=True), reads=[Pmb, W0b], writes=[bkb[1]])
            vop(lambda e: e.tensor_copy(out=Us[0:n, :, :], in_=bank[1][0:n, 0:512].rearrange("p (h d) -> p h d", h=8)), [bkb[1]], [Usb])
            if C.emit_out:
                for h in range(8):
                    q = qi(h, c)
                    P.op(PE, lambda e, h=h, c=c: e.matmul(bank[2][0:n, h * 64:(h + 1) * 64], lhsT=colb(RT, h, c), rhs=H16[:, h, :], start=True, stop=False), reads=[XBFb[RT], H16b], writes=[bkb[2]])
                    P.op(PE, lambda e, h=h, q=q: e.matmul(bank[2][0:n, h * 64:(h + 1) * 64], lhsT=MM[0:n, 3, q, 0:n], rhs=Us[0:n, h, :], start=False, stop=False), reads=[MMb, Usb], writes=[bkb[2]])
                    P.op(PE, lambda e, h=h, q=q: e.matmul(bank[2][0:n, h * 64:(h + 1) * 64], lhsT=MM[0:n, 1, q, 0:n], rhs=TM[0:n, 0, q, :], start=False, stop=True), reads=[MMb, TMb], writes=[bkb[2]])
            for h in range(8):
                q = qi(h, c)
                P.op(PE, lambda e, h=h, q=q: e.matmul(bank[3][0:64, h * 64:(h + 1) * 64], lhsT=TM[0:n, 2, q, :], rhs=Us[0:n, h, :], start=True, stop=False), reads=[TMb, Usb], writes=[bkb[3]])
                P.op(PE, lambda e, h=h, q=q: e.matmul(bank[3][0:64, h * 64:(h + 1) * 64], lhsT=TM[0:n, 1, q, :], rhs=TM[0:n, 0, q, :], start=False, stop=True), reads=[TMb], writes=[bkb[3]])
            ce = 1 + (c + 1) * n - 1
            vop(lambda e: e.tensor_tensor(out=H[:], in0=H[:], in1=bank[3][0:64, 0:512].rearrange("p (h d) -> p h d", h=8), op=ALU.add), [bkb[3], Hb], [Hb])
            vop(lambda e, ce=ce: e.tensor_tensor(out=H[:], in0=H[:], in1=X[G1][:, :, ce:ce + 1].broadcast_to([64, 8, 64]), op=ALU.mult), [Hb, Xb[G1]], [Hb])
            P.op(ACT, lambda e: e.activation(out=H16[:], in_=H[:], func=AF.Copy), reads=[Hb], writes=[H16b])
            if C.emit_out:
                P.op(PE, lambda e, c=c: e.matmul(bank[4][0:n, 0:512], lhsT=sg[0:96, c * n:(c + 1) * n], rhs=g2[0:96, :], start=True, stop=True), reads=[sgb, pb_], writes=[bkb[4]])
                for h in range(8):
                    P.op(PE, lambda e, h=h, c=c: e.matmul(bank[5][0:n, 2 * h:2 * h + 2], lhsT=col(PRK, h, c), rhs=ones[0:64, 0:2], start=True, stop=True), reads=[Xb[PRK], onesb], writes=[bkb[5]])
                P.op(ACT, lambda e: e.activation(out=GBs[0:n, 0:512], in_=bank[4][0:n, 0:512], func=AF.Copy), reads=[bkb[4]], writes=[GBb])
                P.op(ACT, lambda e: e.activation(out=GBs[0:n, 512:528], in_=bank[5][0:n, 0:16], func=AF.Copy), reads=[bkb[5], GBb], writes=[GBb])
                YG = bank[2][0:n, 0:512].rearrange("p (h d) -> p h d", h=8)
                bcn = lambda ap: ap.unsqueeze(2).broadcast_to([n, 8, 64])
                vop(lambda e, YG=YG: e.tensor_reduce(out=ST[0:n, 0, :], in_=YG, axis=AX.X, op=ALU.add), [bkb[2]], [STb])
                P.op(ACT, lambda e, YG=YG: e.activation(out=SQ[0:n, :, :], in_=YG, func=AF.Square), reads=[bkb[2]], writes=[SQb])
                vop(lambda e: e.tensor_reduce(out=ST[0:n, 1, :], in_=SQ[0:n, :, :], axis=AX.X, op=ALU.add), [SQb, STb], [STb])
                vop(lambda e: e.tensor_scalar(out=ST[0:n, 2, :], in0=ST[0:n, 0, :], scalar1=1.0 / 64, scalar2=None, op0=ALU.mult), [STb], [STb])
                vop(lambda e: e.tensor_tensor(out=ST[0:n, 3, :], in0=ST[0:n, 2, :], in1=ST[0:n, 2, :], op=ALU.mult), [STb], [STb])
                vop(lambda e: e.tensor_scalar(out=ST[0:n, 4, :], in0=ST[0:n, 1, :], scalar1=1.0 / 64, scalar2=64e-5, op0=ALU.mult, op1=ALU.add), [STb], [STb])
                vop(lambda e: e.tensor_tensor(out=ST[0:n, 4, :], in0=ST[0:n, 4, :], in1=ST[0:n, 3, :], op=ALU.subtract), [STb], [STb])
                P.op(POOL, lambda e: e.tensor_tensor(out=ST[0:n, 5, :], in0=ST[0:n, 4, :], in1=mh[0:n, 0:8], op=ALU.pow), reads=[STb, mhb], writes=[STb])
                vop(lambda e, YG=YG, bcn=bcn: e.tensor_tensor(out=T1[0:n, :, :], in0=YG, in1=bcn(ST[0:n, 2, :]), op=ALU.subtract), [bkb[2], STb], [T1b])
                vop(lambda e, bcn=bcn: e.tensor_tensor(out=T1[0:n, :, :], in0=T1[0:n, :, :], in1=bcn(ST[0:n, 5, :]), op=ALU.mult), [T1b, STb], [T1b])
                vop(lambda e: e.tensor_tensor(out=T1[0:n, :, :], in0=T1[0:n, :, :], in1=lnw[0:n, :].rearrange("p (h d) -> p h d", h=8), op=ALU.mult), [T1b, pb_], [T1b])
                vop(lambda e: e.tensor_tensor(out=T1[0:n, :, :], in0=T1[0:n, :, :], in1=lnb[0:n, :].rearrange("p (h d) -> p h d", h=8), op=ALU.add), [T1b, pb_], [T1b])
                vtm_c = TM[0:n, 0, 0:nq, :].rearrange("p (h c) d -> p h c d", c=nch)[:, :, c, :]
                bs_c = GBs[0:n, 512:528].rearrange("p (h t) -> p h t", t=2)[:, :, 0:1].broadcast_to([n, 8, 64])
                vop(lambda e, vtm_c=vtm_c, bs_c=bs_c: e.tensor_tensor(out=SQ[0:n, :, :], in0=vtm_c, in1=bs_c, op=ALU.mult), [TMb, GBb, SQb], [SQb])
                vop(lambda e: e.tensor_tensor(out=T1[0:n, :, :], in0=T1[0:n, :, :], in1=SQ[0:n, :, :], op=ALU.add), [T1b, SQb], [T1b])
                vop(lambda e: e.tensor_tensor(out=YO[0:n, :, :], in0=T1[0:n, :, :], in1=GBs[0:n, 0:512].rearrange("p (h d) -> p h d", h=8), op=ALU.mult), [T1b, GBb], [YOb])
                P.dma(SP, y[tc0:tc0 + n, 0:512], YO[0:n, :, :].rearrange("p h d -> p (h d)"), reads=[YOb], pwrites=[yb])
    for sci, (c0, W, t0, n) in enumerate(scs):
        do_sc(sci, c0, W, t0, n)
    P.dma(POOL, prm["sA_out"].rearrange("h k v -> k h v"), H[:], reads=[Hb], pwrites=[K.sob])


def mixer_gla2(C, st, pf, pfb, pt, ptb, y, yb, prm, K):
    P = C.P
    ones, onesb, ident, idb, mask_i, mib, flagE, fb = K.ones, K.onesb, K.ident, K.idb, K.mask_i, K.mib, K.flagE, K.fb
    a2 = sbuf(C, st, "dga2", [32, 256]); a2b = Buf()
    P.op(DVE, lambda e: e.memset(a2[:], 0.0), writes=[a2b])
    P.dma(SP, a2[0:16, :], prm["gla_a2"], reads=[a2b], writes=[a2b])
    nab = sbuf(C, st, "dgnab", [64, 4]); nabb = Buf()
    nbc = sbuf(C, st, "dgnbc", [64, 128]); nbcb = Buf()
    P.dma(SP, nab[:], prm["gla_ab"], writes=[nabb])
    P.op(DVE, lambda e: e.tensor_scalar(out=nab[:], in0=nab[:], scalar1=-1.0, scalar2=None, op0=ALU.mult), reads=[nabb], writes=[nabb])
    P.dma(SP, nbc[:], prm["gla_normbc"], writes=[nbcb])
    WM = 512
    q = sbuf(C, st, "dgq", [64, 4, WM]); k = sbuf(C, st, "dgk", [64, 4, WM]); rel = sbuf(C, st, "dgrel", [64, 4, WM])
    e1 = sbuf(C, st, "dge1", [64, 4, WM]); e2 = sbuf(C, st, "dge2", [64, 4, WM])
    Fl = sbuf(C, st, "dgFl", [64, 4 * WM]); Ssc = sbuf(C, st, "dgSsc", [64, 1 + 4 * WM]); xa = sbuf(C, st, "dgxa", [32, WM])
    qb, kb_, relb, e1b, e2b, Flb, Sscb, xab = [Buf() for _ in range(8)]
    P.op(DVE, lambda e: e.memset(Ssc[:, 0:1], 0.0), writes=[Sscb])
    S = sbuf(C, st, "dgS", [64, 4, 128]); Sb = Buf()
    Sin = sbuf(C, st, "dgSin", [64, 4, 128]); Sinb = Buf()
    P.op(DVE, lambda e: e.memset(S[:], 0.0), writes=[Sb])
    bank = [psum(C, st, f"dgb{i}", [128, 512], F32) for i in range(8)]
    bkb = [Buf() for _ in range(8)]
    vr = Ring([sbuf(C, st, f"dgv{i}", [64, 1024], F32) for i in range(3)])
    ktr = Ring([sbuf(C, st, f"dgkt{i}", [64, 256], F32) for i in range(2)])
    scr = Ring([sbuf(C, st, f"dgsc{i}", [64, 4, 64], F32) for i in range(2)])
    t1r = Ring([sbuf(C, st, f"dgt1{i}", [64, 4, 128], F32) for i in range(2)])
    yor = Ring([sbuf(C, st, f"dgyo{i}", [64, 4, 128], BF16) for i in range(2)])
    str_ = Ring([sbuf(C, st, f"dgst{i}", [64, 3, 4], F32) for i in range(2)])
    junk = sbuf(C, st, "dgjunk", [64, 4, 128], F32); jb = Buf()
    mh = sbuf(C, st, "dgmh", [64, 4], F32); mhb = Buf()
    P.op(POOL, lambda e: e.memset(mh[:], -0.5), writes=[mhb])

    def vop(fn, reads, writes, pwrites=()):
        P.op(DVE, fn, reads=reads, writes=writes, pwrites=pwrites)

    scs = [(3, 16, 0, 16)] + [(22 + 512 * i, 512, 16 + 512 * i, 64) for i in range(4)]
    cnt = [0]

    def do_sc(sci, c0, W, t0, n):
        nch = W // n
        P.dma(SP, q[:, :, 0:W], pf[14 * 128:14 * 128 + 256, c0:c0 + W].rearrange("(h d) c -> d h c", d=64), reads=[pfb], writes=[qb])
        P.dma(SP, k[:, :, 0:W], pf[16 * 128:16 * 128 + 256, c0:c0 + W].rearrange("(h d) c -> d h c", d=64), reads=[pfb], writes=[kb_])
        P.dma(SP, xa[:, 0:W], pf[18 * 128:18 * 128 + 32, c0:c0 + W], reads=[pfb], writes=[xab])
        if sci == 1:
            P.dma(SP, Sin[:], prm["sB_in"].rearrange("h k v -> k h v"), writes=[Sinb])
            vop(lambda e: e.scalar_tensor_tensor(out=S[:], in0=S[:], scalar=flagE[0:64, 0:1], in1=Sin[:], op0=ALU.mult, op1=ALU.add), [Sb, Sinb, fb], [Sb])
        STOP = 9
        if STOP <= 1:
            return
        for h in range(4):
            bk, off = (h * W) // 512, (h * W) % 512
            P.op(PE, lambda e, bk=bk, off=off, h=h: e.matmul(bank[bk][0:64, off:off + W], lhsT=a2[0:32, h * 64:(h + 1) * 64], rhs=xa[0:32, 0:W], start=True, stop=True),
                 reads=[a2b, xab], writes=[bkb[bk]])
            P.op(ACT, lambda e, bk=bk, off=off, h=h: e.activation(out=Fl[:, h * W:(h + 1) * W], in_=bank[bk][0:64, off:off + W], func=AF.Exp, scale=-1.0, bias=nab[:, h:h + 1]),
                 reads=[bkb[bk], nabb], pwrites=[Flb])
        if STOP <= 2:
            return
        P.op(ACT, lambda e: e.activation(out=Fl[:, 0:4 * W], in_=Fl[:, 0:4 * W], func=AF.Ln, bias=1.0), reads=[Flb], writes=[Flb])
        vop(lambda e: e.tensor_tensor_scan(out=Ssc[:, 1:1 + 4 * W], data0=ones[0:64, 0:4 * W], data1=Fl[:, 0:4 * W], initial=0.0, op0=ALU.mult, op1=ALU.add),
            [Flb, Sscb, onesb], [Sscb])
        vop(lambda e: e.tensor_tensor(out=rel[:, :, 0:W].rearrange("p h (c j) -> p h c j", j=n),
                                      in0=Ssc[:, 1:1 + 4 * W].rearrange("p (h c j) -> p h c j", h=4, j=n),
                                      in1=Ssc[:, 0:4 * W].rearrange("p (h c j) -> p h c j", h=4, j=n)[:, :, :, 0:1].broadcast_to([64, 4, nch, n]), op=ALU.subtract),
            [Sscb], [relb])
        if STOP <= 3:
            return
        P.op(ACT, lambda e: e.activation(out=e1[:, :, 0:W], in_=rel[:, :, 0:W], func=AF.Exp, scale=-1.0 / 16), reads=[relb], writes=[e1b])
        P.op(ACT, lambda e: e.activation(out=e2[:, :, 0:W], in_=rel[:, :, 0:W], func=AF.Exp, scale=1.0 / 16), reads=[relb], writes=[e2b])
        if STOP <= 4:
            return
        vop(lambda e: e.scalar_tensor_tensor(out=q[:, :, 0:W], in0=q[:, :, 0:W], scalar=0.125, in1=e1[:, :, 0:W], op0=ALU.mult, op1=ALU.mult), [qb, e1b], [qb])
        if STOP <= 5:
            return
        vop(lambda e: e.scalar_tensor_tensor(out=k[:, :, 0:W], in0=k[:, :, 0:W], scalar=1.0, in1=e2[:, :, 0:W], op0=ALU.mult, op1=ALU.mult), [kb_, e2b], [kb_])
        for c in range(0 if None else nch):
            tc0 = t0 + c * n
            bA, bO, bS = (4, 5, 6) if cnt[0] % 2 == 0 else (1, 2, 3)
            cnt[0] += 1
            vt, vb = vr.next()
            P.dma(SP, vt[0:n, :], pt[tc0:tc0 + n, 0:1024], reads=[ptb], writes=[vb])
            for h in range(4):
                P.op(PE, lambda e, h=h, c=c, bA=bA: e.transpose(out=bank[bA][0:n, h * 64:(h + 1) * 64], in_=k[:, h, c * n:(c + 1) * n], identity=ident[0:64, 0:64]),
                     reads=[kb_, idb], writes=[bkb[bA]])
            for h in range(4):
                P.op(PE, lambda e, h=h, c=c, bA=bA: e.matmul(bank[bA][0:n, 256 + h * 64:256 + h * 64 + n], lhsT=k[:, h, c * n:(c + 1) * n], rhs=q[:, h, c * n:(c + 1) * n],
                                                          start=True, stop=True), reads=[kb_, qb], writes=[bkb[bA]])
            kt, ktb = ktr.next(); sc, scb = scr.next()
            P.op(ACT, lambda e, kt=kt, bA=bA: e.activation(out=kt[0:n, :], in_=bank[bA][0:n, 0:256], func=AF.Copy), reads=[bkb[bA]], writes=[ktb])
            vop(lambda e, sc=sc, bA=bA: e.tensor_tensor(out=sc[0:n, :, 0:n], in0=bank[bA][0:n, 256:512].rearrange("p (h d) -> p h d", h=4)[:, :, 0:n],
                                                      in1=mask_i[0:n, 0:n].unsqueeze(1).broadcast_to([n, 4, n]), op=ALU.mult), [bkb[bA], mib], [scb])
            for h in range(4):
                P.op(PE, lambda e, h=h, c=c, bO=bO: e.matmul(bank[bO][0:n, h * 128:(h + 1) * 128], lhsT=q[:, h, c * n:(c + 1) * n], rhs=S[:, h, :], start=True, stop=False),
                     reads=[qb, Sb], writes=[bkb[bO]])
                P.op(PE, lambda e, h=h, sc=sc, vt=vt, bO=bO: e.matmul(bank[bO][0:n, h * 128:(h + 1) * 128], lhsT=sc[0:n, h, 0:n], rhs=vt[0:n, h * 128:(h + 1) * 128], start=False, stop=True),
                     reads=[scb, vb], writes=[bkb[bO]])
            for h in range(4):
                P.op(PE, lambda e, h=h, bS=bS: e.matmul(bank[bS][0:64, h * 128:(h + 1) * 128], lhsT=ident[0:64, 0:64], rhs=S[:, h, :], start=True, stop=False),
                     reads=[idb, Sb], writes=[bkb[bS]])
                P.op(PE, lambda e, h=h, kt=kt, vt=vt, bS=bS: e.matmul(bank[bS][0:64, h * 128:(h + 1) * 128], lhsT=kt[0:n, h * 64:(h + 1) * 64], rhs=vt[0:n, h * 128:(h + 1) * 128], start=False, stop=True),
                     reads=[ktb, vb], writes=[bkb[bS]])
            ce = (c + 1) * n - 1
            vop(lambda e, ce=ce, bS=bS: e.tensor_tensor(out=S[:], in0=bank[bS][0:64, 0:512].rearrange("p (h d) -> p h d", h=4),
                                                      in1=e1[:, :, ce:ce + 1].broadcast_to([64, 4, 128]), op=ALU.mult), [bkb[bS], e1b], [Sb])
            if C.emit_out and not False:
                s_, sb2 = str_.next(); t1, t1b = t1r.next(); yo, yob = yor.next()
                P.op(ACT, lambda e, t1=t1, bO=bO: e.activation(out=t1[0:n, :, :], in_=bank[bO][0:n, 0:512].rearrange("p (h d) -> p h d", h=4), func=AF.Copy),
                     reads=[bkb[bO]], writes=[t1b])
                vop(lambda e, t1=t1: e.scalar_tensor_tensor(out=junk[0:n, :, :], in0=t1[0:n, :, :], scalar=1.0, in1=t1[0:n, :, :], op0=ALU.mult, op1=ALU.mult), [t1b], [jb])
                vop(lambda e, s_=s_: e.tensor_reduce(out=s_[0:n, 0, :], in_=junk[0:n, :, :], axis=AX.X, op=ALU.add), [jb], [sb2])
                vop(lambda e, s_=s_: e.tensor_scalar(out=s_[0:n, 1, :], in0=s_[0:n, 0, :], scalar1=1.0 / 128, scalar2=EPS, op0=ALU.mult, op1=ALU.add), [sb2], [sb2])
                P.op(POOL, lambda e, s_=s_: e.tensor_tensor(out=s_[0:n, 2, :], in0=s_[0:n, 1, :], in1=mh[0:n, :], op=ALU.pow), reads=[sb2, mhb], writes=[sb2])
                vop(lambda e, t1=t1, s_=s_: e.scalar_tensor_tensor(out=t1[0:n, :, :], in0=t1[0:n, :, :], scalar=1.0, in1=s_[0:n, 2, :].unsqueeze(2).broadcast_to([n, 4, 128]), op0=ALU.mult, op1=ALU.mult),
                    [t1b, sb2], [t1b])
                vop(lambda e, t1=t1: e.scalar_tensor_tensor(out=t1[0:n, :, :], in0=t1[0:n, :, :], scalar=1.0, in1=nbc[0:n, :].unsqueeze(1).broadcast_to([n, 4, 128]), op0=ALU.mult, op1=ALU.mult),
                    [t1b, nbcb], [t1b])
                vop(lambda e, yo=yo, t1=t1, vt=vt: e.scalar_tensor_tensor(out=yo[0:n, :, :], in0=t1[0:n, :, :], scalar=1.0, in1=vt[0:n, 512:1024].rearrange("p (h d) -> p h d", h=4), op0=ALU.mult, op1=ALU.mult),
                    [t1b, vb], [yob])
                P.dma(SP, y[tc0:tc0 + n, 512:1024], yo[0:n, :, :].rearrange("p h d -> p (h d)"), reads=[yob], pwrites=[yb])

    for sci, (c0, W, t0, n) in enumerate(scs):
        do_sc(sci, c0, W, t0, n)
    P.dma(POOL, prm["sB_out"].rearrange("h k v -> k h v"), S[:], reads=[Sb], pwrites=[K.sob])


def run_gens(gens):
    gens = list(gens)
    while gens:
        for g_ in list(gens):
            try:
                next(g_)
            except StopIteration:
                gens.remove(g_)


def mixer_gla_f32(C, st, pf, pfb, pt, ptb, y, yb, prm, K, npsA=2, npsB=6):
    P = C.P
    ones, onesb, ident, idb, mask_i, mib, flagE, fb = K.ones, K.onesb, K.ident, K.idb, K.mask_i, K.mib, K.flagE, K.fb
    a2 = sbuf(C, st, "fga2", [32, 256]); a2b = Buf()
    P.op(DVE, lambda e: e.memset(a2[:], 0.0), writes=[a2b])
    nab = sbuf(C, st, "fgnab", [64, 4]); nabb = Buf()
    nbc = sbuf(C, st, "fgnbc", [64, 128]); nbcb = Buf()
    P.dma(SP, a2[0:16, :], prm["gla_a2"], reads=[a2b], writes=[a2b])
    P.dma(SP, nab[:], prm["gla_ab"], writes=[nabb])
    P.op(DVE, lambda e: e.tensor_scalar(out=nab[:], in0=nab[:], scalar1=-1.0, scalar2=None, op0=ALU.mult), reads=[nabb], writes=[nabb])
    P.dma(SP, nbc[:], prm["gla_normbc"], writes=[nbcb])
    xa = sbuf(C, st, "fgxa", [32, TP]); xab = Buf()
    P.dma(SP, xa[:], pf[18 * 128:18 * 128 + 32, :], reads=[pfb], writes=[xab])
    q = sbuf(C, st, "fgq", [64, TP]); k = sbuf(C, st, "fgk", [64, TP]); sp = sbuf(C, st, "fgsp", [64, TP])
    spc = sbuf(C, st, "fgspc", [64, TP]); e1 = sbuf(C, st, "fge1", [64, TP]); e2 = sbuf(C, st, "fge2", [64, TP])
    qb, kb_, spb, spcb, e1b, e2b = [Buf() for _ in range(6)]
    S = sbuf(C, st, "fgS", [64, 128]); Sb = Buf()
    Sin = sbuf(C, st, "fgSin", [64, 128]); Sinb = Buf()
    psA = Ring([psum(C, st, f"fgpa{i}", [128, 512], F32) for i in range(npsA)])
    psB = Ring([psum(C, st, f"fgpb{i}", [128, 512], F32) for i in range(npsB)])
    vr = Ring([sbuf(C, st, f"fgv{i}", [64, 256], F32) for i in range(3)])
    ktr = Ring([sbuf(C, st, f"fgkt{i}", [64, 64], F32) for i in range(2)])
    scr = Ring([sbuf(C, st, f"fgsc{i}", [64, 64], F32) for i in range(2)])
    str_ = Ring([sbuf(C, st, f"fgst{i}", [64, 4], F32) for i in range(2)])
    junk = sbuf(C, st, "fgjunk", [64, 128], F32); jb = Buf()
    mh = sbuf(C, st, "fgmh", [64, 1], F32); mhb = Buf()
    P.op(POOL, lambda e: e.memset(mh[:], -0.5), writes=[mhb])
    t1r = Ring([sbuf(C, st, f"fgt1{i}", [64, 128], F32) for i in range(2)])
    yor = Ring([sbuf(C, st, f"fgyo{i}", [64, 128], BF16) for i in range(2)])
    for h in range(4):
        r0 = (14 + h // 2) * 128 + (h % 2) * 64
        r1 = (16 + h // 2) * 128 + (h % 2) * 64
        P.dma(SP, q[:], pf[r0:r0 + 64, :], reads=[pfb], writes=[qb])
        P.dma(SP, k[:], pf[r1:r1 + 64, :], reads=[pfb], writes=[kb_])
        for c0 in range(3, TP, 512):
            n = min(512, TP - c0)
            pa, pab = psA.next()
            P.op(PE, lambda e, pa=pa, c0=c0, n=n, h=h: e.matmul(pa[0:64, 0:n], lhsT=a2[0:32, h * 64:(h + 1) * 64], rhs=xa[0:32, c0:c0 + n],
                                                               start=True, stop=True), reads=[a2b, xab], writes=[pab])
            P.op(ACT, lambda e, pa=pa, c0=c0, n=n, h=h: e.activation(out=sp[:, c0:c0 + n], in_=pa[0:64, 0:n], func=AF.Exp, scale=-1.0,
                                                                    bias=nab[:, h:h + 1]), reads=[pab, nabb], writes=[spb])
        P.op(ACT, lambda e: e.activation(out=sp[:, 3:TP], in_=sp[:, 3:TP], func=AF.Ln, bias=1.0), reads=[spb], writes=[spb])
        for (c0, ncol, t0) in SEGS:
            P.op(DVE, lambda e, c0=c0, ncol=ncol: e.tensor_tensor_scan(out=spc[:, c0:c0 + ncol], data0=ones[0:64, c0:c0 + ncol],
                                                                       data1=sp[:, c0:c0 + ncol], initial=0.0, op0=ALU.mult, op1=ALU.add),
                 reads=[spb, onesb], writes=[spcb])
        chunk_rel(C, sp, spb, spc, spcb, 64)
        P.op(ACT, lambda e: e.activation(out=e1[:, 3:TP], in_=sp[:, 3:TP], func=AF.Exp, scale=-1.0 / 16), reads=[spb], writes=[e1b])
        P.op(ACT, lambda e: e.activation(out=e2[:, 3:TP], in_=sp[:, 3:TP], func=AF.Exp, scale=1.0 / 16), reads=[spb], writes=[e2b])
        P.op(DVE, lambda e: e.scalar_tensor_tensor(out=q[:, 3:TP], in0=q[:, 3:TP], scalar=0.125, in1=e1[:, 3:TP], op0=ALU.mult,
                                                   op1=ALU.mult), reads=[qb, e1b], writes=[qb])
        P.op(DVE, lambda e: e.tensor_tensor(out=k[:, 3:TP], in0=k[:, 3:TP], in1=e2[:, 3:TP], op=ALU.mult), reads=[kb_, e2b], writes=[kb_])
        P.op(DVE, lambda e: e.memset(S[:], 0.0), writes=[Sb])
        for si, seg in enumerate(SEGS):
            if si == 1:
                P.dma(SP, Sin[:], prm["sB_in"][h], writes=[Sinb])
                P.op(DVE, lambda e: e.scalar_tensor_tensor(out=S[:], in0=S[:], scalar=flagE[0:64, 0:1], in1=Sin[:], op0=ALU.mult,
                                                           op1=ALU.add), reads=[Sb, Sinb, fb], writes=[Sb])
            for (c0, n, t0) in chunks_of(seg):
                vt, vb = vr.next()
                P.dma(SP, vt[0:n, 0:128], pt[t0:t0 + n, h * 128:(h + 1) * 128], reads=[ptb], writes=[vb])
                P.dma(SP, vt[0:n, 128:256], pt[t0:t0 + n, 512 + h * 128:512 + (h + 1) * 128], reads=[ptb], writes=[vb])
                pb, pbb = psB.next()
                P.op(PE, lambda e, pb=pb, c0=c0, n=n: e.transpose(out=pb[0:n, 0:64], in_=k[:, c0:c0 + n], identity=ident[0:64, 0:64]),
                     reads=[kb_, idb], writes=[pbb])
                P.op(PE, lambda e, pb=pb, c0=c0, n=n: e.matmul(pb[0:n, 64:64 + n], lhsT=k[:, c0:c0 + n], rhs=q[:, c0:c0 + n], start=True, stop=True),
                     reads=[kb_, qb], writes=[pbb])
                kt, ktb = ktr.next()
                sc, scb = scr.next()
                P.op(ACT, lambda e, kt=kt, pb=pb, n=n: e.activation(out=kt[0:n, :], in_=pb[0:n, 0:64], func=AF.Copy), reads=[pbb], writes=[ktb])
                P.op(DVE, lambda e, sc=sc, pb=pb, n=n: e.tensor_tensor(out=sc[0:n, 0:n], in0=pb[0:n, 64:64 + n], in1=mask_i[0:n, 0:n], op=ALU.mult),
                     reads=[pbb, mib], writes=[scb])
                po, pob = psB.next()
                P.op(PE, lambda e, po=po, c0=c0, n=n: e.matmul(po[0:n, 0:128], lhsT=q[:, c0:c0 + n], rhs=S[:, :], start=True, stop=False),
                     reads=[qb, Sb], writes=[pob])
                P.op(PE, lambda e, po=po, sc=sc, vt=vt, n=n: e.matmul(po[0:n, 0:128], lhsT=sc[0:n, 0:n], rhs=vt[0:n, 0:128], start=False, stop=True),
                     reads=[scb, vb], writes=[pob])
                pc, pcb = psB.next()
                P.op(PE, lambda e, pc=pc: e.matmul(pc[0:64, 0:128], lhsT=ident[0:64, 0:64], rhs=S[:, :], start=True, stop=False),
                     reads=[idb, Sb], writes=[pcb])
                P.op(PE, lambda e, pc=pc, kt=kt, vt=vt, n=n: e.matmul(pc[0:64, 0:128], lhsT=kt[0:n, 0:64], rhs=vt[0:n, 0:128], start=False, stop=True),
                     reads=[ktb, vb], writes=[pcb])
                ce = c0 + n - 1
                P.op(DVE, lambda e, pc=pc, ce=ce: e.tensor_scalar(out=S[:], in0=pc[0:64, 0:128], scalar1=e1[:, ce:ce + 1], scalar2=None, op0=ALU.mult),
                     reads=[pcb, e1b], writes=[Sb])
                if C.emit_out and not False:
                    s_, sb2 = str_.next()
                    t1, t1b = t1r.next()
                    yo, yob = yor.next()
                    P.op(ACT, lambda e, t1=t1, po=po, n=n: e.activation(out=t1[0:n, :], in_=po[0:n, 0:128], func=AF.Copy), reads=[pob], writes=[t1b])
                    P.op(ACT, lambda e, t1=t1, s_=s_, n=n: e.activation(out=junk[0:n, :], in_=t1[0:n, :], func=AF.Square, accum_out=s_[0:n, 0:1]),
                         reads=[t1b], writes=[jb, sb2])
                    P.op(DVE, lambda e, s_=s_, n=n: e.tensor_scalar(out=s_[0:n, 1:2], in0=s_[0:n, 0:1], scalar1=1.0 / 128, scalar2=EPS, op0=ALU.mult,
                                                                   op1=ALU.add), reads=[sb2], writes=[sb2])
                    P.op(POOL, lambda e, s_=s_, n=n: e.tensor_tensor(out=s_[0:n, 2:3], in0=s_[0:n, 1:2], in1=mh[0:n, :], op=ALU.pow),
                         reads=[sb2, mhb], writes=[sb2])
                    P.op(DVE, lambda e, t1=t1, s_=s_, n=n: e.scalar_tensor_tensor(out=t1[0:n, :], in0=t1[0:n, :], scalar=s_[0:n, 2:3],
                                                                               in1=nbc[0:n, :], op0=ALU.mult, op1=ALU.mult),
                         reads=[sb2, nbcb, t1b], writes=[t1b])
                    P.op(DVE, lambda e, yo=yo, t1=t1, vt=vt, n=n: e.tensor_tensor(out=yo[0:n, :], in0=t1[0:n, :], in1=vt[0:n, 128:256], op=ALU.mult),
                         reads=[t1b, vb], writes=[yob])
                    P.dma(SP, y[t0:t0 + n, 512 + h * 128:512 + (h + 1) * 128], yo[0:n, :], reads=[yob], pwrites=[yb])
                yield
        P.dma(POOL, prm["sB_out"][h], S[:], reads=[Sb], pwrites=[K.sob])


import contextlib
import numpy as np

PRM_SHAPES = {
    "gla_a2": [16, 256], "gla_ab": [64, 4], "gla_normbc": [64, 128],
    "ml_cw": [128, 8, 4], "ml_cb": [128, 8], "ml_ib": [4, 1], "ml_fb": [4, 1], "ml_normbc": [64, 1024], "onehot": [4, 4, 128],
    "rw_muA": [64, 3, 8], "rw_muL": [96, 3], "rw_w2": [32, 512], "rw_a2": [32, 512], "rw_g2": [96, 512], "rw_ch": [64, 4, 8],
    "rw_rk": [64, 8, 2], "rw_lnw_bc": [64, 512], "rw_lnb_bc": [64, 512],
    "sA_in": [8, 64, 64], "sB_in": [4, 64, 128], "sC_in": [4, 128, 257], "mC_in": [4, 1], "hist_in": [NFMB * 128, 3],
    "flagE": [128, 1], "mask_i": [64, 64], "mask5": [64, 5, 64],
}
OUT_SHAPES = {"sA_out": [8, 64, 64], "sB_out": [4, 64, 128], "sC_out": [4, 128, 257], "mC_out": [4, 1], "hist_out": [NFMB * 128, 3]}


def host_consts():
    j = np.arange(64)
    mi = (j[None, :] >= j[:, None]).astype(np.float32)
    ms = (j[None, :] > j[:, None]).astype(np.float32)
    ml = (j[None, :] < j[:, None]).astype(np.float32)
    mask5 = np.stack([ms, mi, ms, mi, ml], 1)
    oh = np.zeros((4, 4, 128), np.float32)
    for h in range(4):
        oh[h, h, :] = 1.0
    return {"mask_i": mi, "mask5": np.ascontiguousarray(mask5), "onehot": oh}


def host_layer_params(z, l):
    f = lambda a: np.ascontiguousarray(a, dtype=np.float32)
    chT = lambda v: f(v.reshape(8, 64).T)
    mu = z["rw_mu"][l]
    d = {}
    d["gla_a2"] = f(z["gla_a2"][l]); d["gla_ab"] = f(z["gla_ab"][l].reshape(4, 64).T)
    d["gla_normbc"] = f(np.broadcast_to(z["gla_norm"][l], (64, 128)))
    cw = z["ml_conv_w"][l]
    d["ml_cw"] = f(cw.reshape(4, 8, 128).transpose(2, 1, 0)); d["ml_cb"] = f(z["ml_conv_b"][l].reshape(8, 128).T)
    d["ml_ib"] = f(z["ml_ib"][l].reshape(4, 1)); d["ml_fb"] = f(z["ml_fb"][l].reshape(4, 1))
    d["ml_normbc"] = f(np.broadcast_to(z["ml_norm"][l], (64, 1024)))
    d["rw_muA"] = f(np.stack([chT(mu[0:512]), chT(mu[512:1024]), chT(mu[1024:1536])], 1))
    muL = np.zeros((96, 3), np.float32); muL[0:32, 0] = mu[1536:1568]; muL[0:32, 1] = mu[1568:1600]; muL[0:96, 2] = mu[1600:1696]
    d["rw_muL"] = muL
    d["rw_w2"] = f(z["rw_w2"][l]); d["rw_a2"] = f(z["rw_a2"][l]); d["rw_g2"] = f(z["rw_g2"][l])
    d["rw_ch"] = f(np.stack([chT(z["rw_w0"][l]), chT(z["rw_a0"][l]), chT(z["rw_kk"][l]), chT(z["rw_ka"][l])], 1))
    rk = z["rw_rk"][l]
    d["rw_rk"] = f(np.stack([rk.T, rk.T], 2))
    d["rw_lnw_bc"] = f(np.broadcast_to(z["rw_ln_w"][l], (64, 512))); d["rw_lnb_bc"] = f(np.broadcast_to(z["rw_ln_b"][l], (64, 512)))
    return d


def host_layer_weights(z, l):
    return {"win": hp.prep_win(z["w_in"][l]), "wout": hp.prep_sq(z["w_out"][l], 4), "w1": hp.prep_sq(z["ffn_w1"][l], 11),
            "w3": hp.prep_sq(z["ffn_w3"][l], 11), "w2": hp.prep_w2(z["ffn_w2"][l]), "g1": hp.gT(z["norm_mix"][l]), "g2": hp.gT(z["norm_ffn"][l])}


def build_layer(debug=False, emit_out=True, do_final=True):
    nc = bass.Bass("TRN2", target_bir_lowering=False)
    C = Ctx(); C.nc = nc; C.P = Prog(nc, same_engine_sync=True); C.emit_out = emit_out
    C.P.scopes = False
    P = C.P
    dr = lambda n, s, dt=F32, kind="ExternalInput": nc.dram_tensor(n, s, dt, kind=kind).ap()
    hin = dr("hin", [NTOK, D])
    win = dr("win", [13, 128, 8192]); wout = dr("wout", [4, 128, 8192])
    w1 = dr("w1", [11, 128, 8192]); w3 = dr("w3", [11, 128, 8192]); w2 = dr("w2", [4, 4, 128, 11 * 512])
    g1 = dr("g1", [128, 16]); g2 = dr("g2", [128, 16]); gf = dr("gf", [128, D])
    prm = {k: dr(k, s) for k, s in PRM_SHAPES.items()}
    for k, s in OUT_SHAPES.items():
        prm[k] = dr(k, s, F32, "ExternalOutput")
    dk = "ExternalOutput" if debug else "Internal"
    pf = dr("pf", [NFMB * 128, TP], F32, dk)
    pt = dr("pt", [NTOK, NTMC], F32, dk)
    y = dr("y", [NTOK, D], BF16, dk)
    hmid = dr("hmid", [NTOK, D], F32, dk)
    hout = dr("hout", [NTOK, D], F32, "ExternalOutput")
    out = dr("out", [NTOK - 16, D], F32, "ExternalOutput")
    aT = dr("aT", [5, 128, 44, 512], BF16, "Internal")
    hb, pfb, ptb, yb, hmb, hob, ob, ab = [Buf() for _ in range(8)]
    K = Ctx(); K.sob = Buf()
    with contextlib.ExitStack() as st0:
        K.ident = sbuf(C, st0, "ident", [128, 128], F32); identb = sbuf(C, st0, "identb", [128, 128], BF16)
        g1t = sbuf(C, st0, "g1t", [128, 16]); g2t = sbuf(C, st0, "g2t", [128, 16])
        K.idb, idbb, g1b, g2b = [Buf() for _ in range(4)]
        P.op(POOL, lambda e: e.memset(K.ident[:], 1.0), writes=[K.idb])
        P.op(POOL, lambda e: e.affine_select(out=K.ident[:], in_=K.ident[:], pattern=[[-1, 128]], base=0, channel_multiplier=1,
                                             compare_op=ALU.is_equal, fill=0.0), reads=[K.idb], writes=[K.idb])
        P.op(POOL, lambda e: e.tensor_copy(out=identb[:], in_=K.ident[:]), reads=[K.idb], writes=[idbb])
        P.dma(SP, g1t[:], g1, writes=[g1b]); P.dma(SP, g2t[:], g2, writes=[g2b])
        with contextlib.ExitStack() as st1:
            uT = sbuf(C, st1, "uT", [128, 16, NTOK], BF16); ub = Buf()
            pst = Ring([psum(C, st1, f"pst{i}", [128, 1024], BF16) for i in range(2)])
            psm = Ring([psum(C, st1, f"psm{i}", [128, 512], F32) for i in range(6)])
            with contextlib.ExitStack() as st:
                P.phase = "norm"
                phase_norm(C, st, hin, hb, g1t, g1b, uT, ub, pst, identb, idbb)
            P.barrier()
            with contextlib.ExitStack() as st:
                P.phase = "proj"
                phase_proj(C, st, uT, ub, win, pf, pfb, pt, ptb, psm, prm["hist_out"], K.sob)
            P.barrier()
        with contextlib.ExitStack() as st1:
            K.ones = sbuf(C, st1, "ones", [64, TP]); K.onesb = Buf()
            K.mask_i = sbuf(C, st1, "mask_i", [64, 64]); K.mib = Buf()
            K.mask5 = sbuf(C, st1, "mask5", [64, 5, 64]); K.m5b = Buf()
            K.flagE = sbuf(C, st1, "flagE", [128, 1]); K.fb = Buf()
            P.op(POOL, lambda e: e.memset(K.ones[:], 1.0), writes=[K.onesb])
            P.dma(SP, K.mask_i[:], prm["mask_i"], writes=[K.mib]); P.dma(SP, K.mask5[:], prm["mask5"], writes=[K.m5b])
            P.dma(SP, K.flagE[:], prm["flagE"], writes=[K.fb])
            with contextlib.ExitStack() as st:
                P.phase = "prepass"
                gate_prepass(C, st, pt, ptb)
            P.barrier()
            with contextlib.ExitStack() as st:
                P.phase = "gla"
                if None:
                    with contextlib.ExitStack() as st2:
                        mixer_gla2(C, st2, pf, pfb, pt, ptb, y, yb, prm, K)
                    P.barrier()
                    P.phase = "mlstm"
                    run_gens([mixer_mlstm(C, st, pf, pfb, pt, ptb, y, yb, prm, K, 4, 2)])
                else:
                    if None:
                        with contextlib.ExitStack() as st2:
                            run_gens([mixer_gla_f32(C, st2, pf, pfb, pt, ptb, y, yb, prm, K)])
                        P.barrier()
                        P.phase = "mlstm"
                        with contextlib.ExitStack() as st2:
                            run_gens([mixer_mlstm(C, st2, pf, pfb, pt, ptb, y, yb, prm, K)])
                    else:
                        run_gens([mixer_gla(C, st, pf, pfb, pt, ptb, y, yb, prm, K, 1, 3), mixer_mlstm(C, st, pf, pfb, pt, ptb, y, yb, prm, K, 3, 1)])
            P.barrier()
            if True:
              with contextlib.ExitStack() as st:
                P.phase = "rwkv"
                mixer_rwkv3(C, st, pf, pfb, y, yb, prm, K)
            P.barrier()
        if emit_out:
            with contextlib.ExitStack() as st1:
                uT = sbuf(C, st1, "uT2", [128, 16, NTOK], BF16); ub = Buf()
                pst = Ring([psum(C, st1, f"pst{i}", [128, 1024], BF16) for i in range(2)])
                psm = Ring([psum(C, st1, f"psm{i}", [128, 512], F32) for i in range(6)])
                with contextlib.ExitStack() as st:
                    P.phase = "wout"
                    phase_wout(C, st, y, yb, hin, hb, hmid, hmb, wout, uT, ub, psm, pst, identb, idbb)
                P.barrier()
                with contextlib.ExitStack() as st:
                    P.phase = "norm"
                    phase_norm(C, st, hmid, hmb, g2t, g2b, uT, ub, pst, identb, idbb)
                P.barrier()
                with contextlib.ExitStack() as st:
                    P.phase = "ffn1"
                    phase_ffn1(C, st, uT, ub, w1, w3, aT, ab, psm)
                P.barrier()
    if emit_out:
        with contextlib.ExitStack() as st:
            psm = Ring([psum(C, st, f"psn{i}", [128, 512], F32) for i in range(6)])
            P.phase = "ffn2"
            phase_ffn2(C, st, aT, ab, w2, hmid, hmb, hout, hob, psm)
        P.barrier()
        if do_final:
            with contextlib.ExitStack() as st:
                gft = sbuf(C, st, "gft2", [128, D]); gfb = Buf()
                P.dma(SP, gft[:], gf, writes=[gfb])
                P.phase = "final"
                phase_final_norm(C, st, hout, hob, gft, gfb, out, ob)
    fin = [K.sob, hob, ob]
    if debug:
        fin += [pfb, ptb, yb, hmb]
    P.finish(fin)
    P.emit()
    C.counts = {e: (len(P.ops[e]), sum(1 for o in P.ops[e] if o.signal)) for e in ENGS}
    return nc, C


import contextlib
import numpy as np

LAYER_KEYS = ["gla_a2", "gla_ab", "gla_normbc", "ml_cw", "ml_cb", "ml_ib", "ml_fb", "ml_normbc", "rw_muA", "rw_muL", "rw_w2", "rw_a2",
              "rw_g2", "rw_ch", "rw_rk", "rw_lnw_bc", "rw_lnb_bc"]
STATE_KEYS = ["sA", "sB", "sC", "mC", "hist"]


def emit_half(C, K, T, l, half):
    P = C.P
    hin, hinb = T["hin"][(l, half)]
    hout, houtb = T["hout"][(l, half)]
    prm = {k: T["lp"][k][l] for k in LAYER_KEYS}
    for k in ("mask_i", "mask5", "onehot"):
        prm[k] = T["const"][k]
    prm["flagE"] = T["flag1"] if half == 0 else T["flag0"]
    for k in STATE_KEYS:
        prm[k + "_in"] = T["zstate"][k] if half == 0 else T["state"][k][l]
        prm[k + "_out"] = T["state"][k][l] if half == 0 else T["sdump"][k]
    K.sob = T["stateb"][l] if half == 0 else T["sdumpb"]
    K.fb = Buf()
    win, wout, w1, w3, w2 = T["win"][l], T["wout"][l], T["w1"][l], T["w3"][l], T["w2"][l]
    pf, pfb, pt, ptb, y, yb, hmid, hmb, aT, ab = T["pf"], T["pfb"], T["pt"], T["ptb"], T["y"], T["yb"], T["hmid"], T["hmb"], T["aT"], T["ab"]
    identb, idbb = K.identb, K.idbb
    with contextlib.ExitStack() as st1:
        uT = sbuf(C, st1, "uT", [128, 16, NTOK], BF16); ub = Buf()
        pst = Ring([psum(C, st1, f"pst{i}", [128, 1024], BF16) for i in range(2)])
        psm = Ring([psum(C, st1, f"psm{i}", [128, 512], F32) for i in range(6)])
        with contextlib.ExitStack() as st:
            phase_norm(C, st, hin, hinb, K.g1t[l], K.g1b, uT, ub, pst, identb, idbb)
        P.barrier()
        with contextlib.ExitStack() as st:
            phase_proj(C, st, uT, ub, win, pf, pfb, pt, ptb, psm, prm["hist_out"], K.sob)
        P.barrier()
    with contextlib.ExitStack() as st1:
        K.flagE = sbuf(C, st1, "flagE", [128, 1])
        P.dma(SP, K.flagE[:], prm["flagE"], writes=[K.fb])
        K.ones = sbuf(C, st1, "ones", [64, TP]); K.onesb = Buf()
        K.mask_i = sbuf(C, st1, "mask_i", [64, 64]); K.mib = Buf()
        K.mask5 = sbuf(C, st1, "mask5", [64, 5, 64]); K.m5b = Buf()
        P.op(POOL, lambda e: e.memset(K.ones[:], 1.0), writes=[K.onesb])
        P.dma(SP, K.mask_i[:], T["const"]["mask_i"], writes=[K.mib]); P.dma(SP, K.mask5[:], T["const"]["mask5"], writes=[K.m5b])
        with contextlib.ExitStack() as st:
            gate_prepass(C, st, pt, ptb)
        P.barrier()
        with contextlib.ExitStack() as st:
            run_gens([mixer_gla_f32(C, st, pf, pfb, pt, ptb, y, yb, prm, K)])
        P.barrier()
        with contextlib.ExitStack() as st:
            run_gens([mixer_mlstm(C, st, pf, pfb, pt, ptb, y, yb, prm, K)])
        P.barrier()
        with contextlib.ExitStack() as st:
            mixer_rwkv3(C, st, pf, pfb, y, yb, prm, K)
        P.barrier()
    with contextlib.ExitStack() as st1:
        uT = sbuf(C, st1, "uT2", [128, 16, NTOK], BF16); ub = Buf()
        pst = Ring([psum(C, st1, f"pst{i}", [128, 1024], BF16) for i in range(2)])
        psm = Ring([psum(C, st1, f"psm{i}", [128, 512], F32) for i in range(6)])
        with contextlib.ExitStack() as st:
            phase_wout(C, st, y, yb, hin, hinb, hmid, hmb, wout, uT, ub, psm, pst, identb, idbb)
        P.barrier()
        with contextlib.ExitStack() as st:
            phase_norm(C, st, hmid, hmb, K.g2t[l], K.g1b, uT, ub, pst, identb, idbb)
        P.barrier()
        with contextlib.ExitStack() as st:
            phase_ffn1(C, st, uT, ub, w1, w3, aT, ab, psm)
        P.barrier()
    with contextlib.ExitStack() as st:
        psm = Ring([psum(C, st, f"psn{i}", [128, 512], F32) for i in range(6)])
        phase_ffn2(C, st, aT, ab, w2, hmid, hmb, hout, houtb, psm)
    P.barrier()
    if l == 1:
        with contextlib.ExitStack() as st:
            gft = sbuf(C, st, "gft2", [128, D]); gfb = Buf()
            P.dma(SP, gft[:], T["gf"], writes=[gfb])
            phase_final_norm(C, st, hout, houtb, gft, gfb, T["out"][half], T["outb"])
        P.barrier()


def build_fused(nlayers=2, halves=(0, 1)):
    nc = bass.Bass("TRN2", target_bir_lowering=False)
    C = Ctx(); C.nc = nc; C.P = Prog(nc); C.emit_out = True
    P = C.P
    dr = lambda n, s, dt=F32, kind="ExternalInput": nc.dram_tensor(n, s, dt, kind=kind).ap()
    T = {}
    xin = [dr("xE", [NTOK, D]), dr("xO", [NTOK, D])]
    T["win"] = dr("win", [2, 13, 128, 8192]); T["wout"] = dr("wout", [2, 4, 128, 8192])
    T["w1"] = dr("w1", [2, 11, 128, 8192]); T["w3"] = dr("w3", [2, 11, 128, 8192]); T["w2"] = dr("w2", [2, 4, 4, 128, 11 * 512])
    g1 = dr("g1", [2, 128, 16]); g2 = dr("g2", [2, 128, 16]); T["gf"] = dr("gf", [128, D])
    T["lp"] = {k: dr(k, [2] + PRM_SHAPES[k]) for k in LAYER_KEYS}
    T["const"] = {k: dr(k, PRM_SHAPES[k]) for k in ("mask_i", "mask5", "onehot")}
    T["flag1"] = dr("flag1", [128, 1]); T["flag0"] = dr("flag0", [128, 1])
    T["zstate"] = {k: dr("z_" + k, PRM_SHAPES[k + "_in"]) for k in STATE_KEYS}
    T["state"] = {k: dr("st_" + k, [2] + PRM_SHAPES[k + "_in"], F32, "Internal") for k in STATE_KEYS}
    T["sdump"] = {k: dr("sd_" + k, PRM_SHAPES[k + "_in"], F32, "Internal") for k in STATE_KEYS}
    T["stateb"] = [Buf(), Buf()]; T["sdumpb"] = Buf()
    T["pf"] = dr("pf", [NFMB * 128, TP], F32, "Internal"); T["pt"] = dr("pt", [NTOK, NTMC], F32, "Internal")
    T["y"] = dr("y", [NTOK, D], BF16, "Internal"); T["hmid"] = dr("hmid", [NTOK, D], F32, "Internal")
    T["aT"] = dr("aT", [5, 128, 44, 512], BF16, "Internal")
    for k in ("pfb", "ptb", "yb", "hmb", "ab", "outb"):
        T[k] = Buf()
    h1 = [dr("h1E", [NTOK, D], F32, "Internal"), dr("h1O", [NTOK, D], F32, "Internal")]
    h2 = [dr("h2E", [NTOK, D], F32, "Internal"), dr("h2O", [NTOK, D], F32, "Internal")]
    T["out"] = [dr("outE", [2048, D], F32, "ExternalOutput"), dr("outO", [2048, D], F32, "ExternalOutput")]
    xb = [Buf(), Buf()]; h1b = [Buf(), Buf()]; h2b = [Buf(), Buf()]
    T["hin"] = {(0, 0): (xin[0], xb[0]), (0, 1): (xin[1], xb[1]), (1, 0): (h1[0], h1b[0]), (1, 1): (h1[1], h1b[1])}
    T["hout"] = {(0, 0): (h1[0], h1b[0]), (0, 1): (h1[1], h1b[1]), (1, 0): (h2[0], h2b[0]), (1, 1): (h2[1], h2b[1])}
    K = Ctx()
    with contextlib.ExitStack() as st0:
        K.ident = sbuf(C, st0, "ident", [128, 128], F32); K.identb = sbuf(C, st0, "identb", [128, 128], BF16)
        K.g1t = [sbuf(C, st0, f"g1t{l}", [128, 16]) for l in range(2)]; K.g2t = [sbuf(C, st0, f"g2t{l}", [128, 16]) for l in range(2)]
        K.idb, K.idbb, K.g1b = Buf(), Buf(), Buf()
        P.op(POOL, lambda e: e.memset(K.ident[:], 1.0), writes=[K.idb])
        P.op(POOL, lambda e: e.affine_select(out=K.ident[:], in_=K.ident[:], pattern=[[-1, 128]], base=0, channel_multiplier=1,
                                             compare_op=ALU.is_equal, fill=0.0), reads=[K.idb], writes=[K.idb])
        P.op(POOL, lambda e: e.tensor_copy(out=K.identb[:], in_=K.ident[:]), reads=[K.idb], writes=[K.idbb])
        for l in range(2):
            P.dma(SP, K.g1t[l][:], g1[l], pwrites=[K.g1b]); P.dma(SP, K.g2t[l][:], g2[l], pwrites=[K.g1b])
        for l in range(nlayers):
            for half in halves:
                emit_half(C, K, T, l, half)
    P.finish([T["outb"], T["sdumpb"], T["stateb"][0], T["stateb"][1]])
    P.emit()
    C.counts = {e: (len(P.ops[e]), sum(1 for o in P.ops[e] if o.signal)) for e in ENGS}
    return nc, C


from concourse.bass_utils import run_bass_kernel_spmd

_PROG = {}


def kernel(**z):
    x = np.asarray(z["x"], np.float32)
    meta = np.asarray(z["meta_tokens"], np.float32)
    if "nc" not in _PROG:
        _PROG["nc"] = build_fused()[0]
    nc = _PROG["nc"]
    shared = {}
    shared.update(host_consts())
    shared["gf"] = np.ascontiguousarray(np.broadcast_to(np.asarray(z["norm_final"], np.float32), (128, D)))
    Ws = [host_layer_weights(z, l) for l in range(2)]
    for k in ("win", "wout", "w1", "w3", "w2", "g1", "g2"):
        shared[k] = np.stack([Ws[0][k], Ws[1][k]])
    Ps = [host_layer_params(z, l) for l in range(2)]
    for k in LAYER_KEYS:
        shared[k] = np.stack([Ps[0][k], Ps[1][k]])
    shared["flag1"] = np.ones((128, 1), np.float32)
    shared["flag0"] = np.zeros((128, 1), np.float32)
    for k in STATE_KEYS:
        shared["z_" + k] = np.zeros(PRM_SHAPES[k + "_in"], np.float32)
    in_maps = []
    for c in range(8):
        b = c % 4
        im = dict(shared)
        im["xE"] = np.ascontiguousarray(np.concatenate([meta, x[b, :2048]], 0))
        im["xO"] = np.ascontiguousarray(np.concatenate([meta, x[b, 2048:]], 0))
        in_maps.append(im)
    res = run_bass_kernel_spmd(nc, in_maps, core_ids=list(range(8))).results
    out = np.zeros((4, 4096, D), np.float32)
    for b in range(4):
        out[b, :2048] = np.asarray(res[b]["outE"], np.float32)
        out[b, 2048:] = np.asarray(res[b]["outO"], np.float32)
    return out
```

```python
import contextlib
import numpy as np
import concourse.bass as bass
import concourse.mybir as mybir

F32 = mybir.dt.float32
BF16 = mybir.dt.bfloat16
AF = mybir.ActivationFunctionType
ALU = mybir.AluOpType
AX = mybir.AxisListType

PE, ACT, DVE, POOL, SP = "pe", "act", "dve", "pool", "sp"
ENGS = [PE, ACT, DVE, POOL, SP]
SEG = 30000
NSLOT = 6


class Buf:
    __slots__ = ("name", "w", "ws", "rs")

    def __init__(self, name=""):
        self.name = name
        self.w = None
        self.ws = {}
        self.rs = {}


def _key(o):
    return (o.eng, o.slot if o.dma else None)


class Op:
    __slots__ = ("eng", "fn", "deps", "dma", "idx", "signal", "ev", "slot", "name")

    def __init__(self, eng, fn, dma):
        self.eng = eng
        self.fn = fn
        self.dma = dma
        self.deps = []
        self.signal = False
        self.ev = None
        self.slot = None
        self.name = ""


class Prog:
    def __init__(self, nc, same_engine_sync=True):
        self.nc = nc
        self.ops = {e: [] for e in ENGS}
        self.same = same_engine_sync
        self.ndma = {e: 0 for e in ENGS}
        self.final_deps = []
        self.pending_barrier = None
        self.scopes = False
        self.phase = ""

    def op(self, eng, fn, reads=(), writes=(), dma=False, name="", pwrites=()):
        o = Op(eng, fn, dma)
        o.name = getattr(self, "phase", "")
        if dma:
            o.slot = self.ndma[eng] % NSLOT
            self.ndma[eng] += 1
            o.signal = True
        o.idx = len(self.ops[eng])
        deps = []
        for r in reads:
            if r.w is not None:
                deps.append(r.w)
            deps.extend(r.ws.values())
        for w in writes:
            if w.w is not None:
                deps.append(w.w)
            deps.extend(w.ws.values())
            deps.extend(w.rs.values())
        for w in pwrites:
            if w.w is not None:
                deps.append(w.w)
            deps.extend(w.rs.values())
        if self.pending_barrier and self.pending_barrier.get(eng):
            deps.extend(self.pending_barrier[eng])
            self.pending_barrier[eng] = []
        best = {}
        for d in deps:
            if d is o:
                continue
            if d.eng == eng and not d.dma:
                if eng == PE or not self.same:
                    continue
            k = _key(d)
            if k not in best or best[k].idx < d.idx:
                best[k] = d
        for d in best.values():
            o.deps.append(d)
            d.signal = True
        for w in writes:
            w.w = o
            w.ws = {}
            w.rs = {}
        for w in pwrites:
            w.ws[_key(o)] = o
        for r in reads:
            r.rs[_key(o)] = o
        self.ops[eng].append(o)
        return o

    def dma(self, eng, out, in_, reads=(), writes=(), pwrites=(), **kw):
        return self.op(eng, lambda e: e.dma_start(out=out, in_=in_, **kw), reads, writes, dma=True, pwrites=pwrites)

    def finish(self, bufs):
        for b in bufs:
            for o in ([b.w] if b.w is not None else []) + list(b.ws.values()):
                self.final_deps.append(o)
                o.signal = True

    def emit(self):
        nc = self.nc
        with contextlib.ExitStack() as st:
            csem = {}
            for e in (PE, ACT, DVE, POOL):
                n = sum(1 for o in self.ops[e] if o.signal and not o.dma)
                nseg = n // SEG + 1
                csem[e] = [st.enter_context(nc.semaphore(f"c_{e}_{i}")) for i in range(nseg)]
            dsem = {}
            for e in (ACT, POOL, SP):
                if self.ndma[e] > 0:
                    dsem[e] = [st.enter_context(nc.semaphore(f"d_{e}_{i}")) for i in range(NSLOT)]
            for e in ENGS:
                cnt = 0
                dcur = [0] * NSLOT
                for o in self.ops[e]:
                    if o.dma:
                        prev = dcur[o.slot]
                        dcur[o.slot] += 16
                        o.ev = (dsem[e][o.slot], dcur[o.slot], prev)
                    elif o.signal:
                        seg, v = divmod(cnt, SEG)
                        o.ev = (csem[e][seg], v + 1, None)
                        cnt += 1
            block = st.enter_context(nc.Block())
            handles = {PE: block.tensor, ACT: block.scalar, DVE: block.vector,
                       POOL: block.gpsimd, SP: block.sync}
            for e in ENGS:
                ops = self.ops[e]
                fdeps = self.final_deps if e == SP else []
                if not ops and not fdeps:
                    continue

                def body(eng, ops=ops, fdeps=fdeps):
                    known = {}

                    def wait(sem, val):
                        k = id(sem)
                        if known.get(k, 0) >= val:
                            return
                        eng.wait_ge(sem, val)
                        known[k] = val

                    cur_ph, sid = None, None
                    for o in ops:
                        if self.scopes and o.name != cur_ph:
                            if cur_ph:
                                nc.leave_named_scope(cur_ph, sid, False)
                            cur_ph = o.name
                            if cur_ph:
                                sid, _ = nc.enter_named_scope(cur_ph, False)
                        for d in o.deps:
                            wait(d.ev[0], d.ev[1])
                        if o.dma and o.ev[2] > 0:
                            wait(o.ev[0], o.ev[2])
                        ins = o.fn(eng)
                        if o.dma:
                            ins.then_inc(o.ev[0], 16)
                        elif o.signal:
                            ins.then_inc(o.ev[0], 1)
                    if self.scopes and cur_ph:
                        nc.leave_named_scope(cur_ph, sid, False)
                    for d in fdeps:
                        wait(d.ev[0], d.ev[1])

                handles[e](body)


def _barrier(self):
    lasts = []
    for e in ENGS:
        ops = self.ops[e]
        if not ops:
            continue
        for o in reversed(ops):
            if not o.dma:
                lasts.append(o)
                break
        seen = set()
        for o in reversed(ops):
            if o.dma and o.slot not in seen:
                seen.add(o.slot)
                lasts.append(o)
            if len(seen) == NSLOT:
                break
    for o in lasts:
        o.signal = True
    self.pending_barrier = {e: list(lasts) for e in ENGS}


Prog.barrier = _barrier


class Ring:
    def __init__(self, tiles):
        self.tiles = tiles
        self.bufs = [Buf() for _ in tiles]
        self.i = 0

    def next(self):
        k = self.i % len(self.tiles)
        self.i += 1
        return self.tiles[k], self.bufs[k]


class _HP:
    pass
hp = _HP()


import numpy as np

A0, B0, C0 = 0, 1696, 3248


def fm_blocks():
    blks = []
    for seg in range(3):
        for i in range(4):
            blks.append(list(range(A0 + seg * 512 + i * 128, A0 + seg * 512 + (i + 1) * 128)))
    blks.append(list(range(A0 + 1536, A0 + 1600)))
    blks.append(list(range(A0 + 1600, A0 + 1696)))
    for seg in range(2):
        for i in range(2):
            blks.append(list(range(B0 + seg * 256 + i * 128, B0 + seg * 256 + (i + 1) * 128)))
    blks.append(list(range(B0 + 1024, B0 + 1040)))
    for seg in range(2):
        for i in range(4):
            blks.append(list(range(C0 + seg * 512 + i * 128, C0 + seg * 512 + (i + 1) * 128)))
    blks.append(list(range(C0 + 2048, C0 + 2056)))
    assert len(blks) == 28
    return blks


def tm_cols():
    cols = []
    cols += list(range(B0 + 512, B0 + 1024))
    cols += list(range(B0 + 1040, B0 + 1552))
    cols += list(range(C0 + 1024, C0 + 2048))
    cols += list(range(C0 + 2056, C0 + 3080))
    assert len(cols) == 3072
    return cols


def tile_k(W, ncol=512):
    K = W.shape[0]
    return np.ascontiguousarray(W.reshape(K // 128, 128, ncol).transpose(1, 0, 2).reshape(128, (K // 128) * ncol))


def prep_win(w):
    blks = fm_blocks()
    out = np.zeros((13, 128, 8192), np.float32)
    for wi in range(7):
        Wt = np.zeros((2048, 512), np.float32)
        for bi in range(4):
            cols = blks[wi * 4 + bi]
            Wt[:, bi * 128:bi * 128 + len(cols)] = w[:, cols]
        out[wi] = tile_k(Wt)
    tc = tm_cols()
    for ci in range(6):
        out[7 + ci] = tile_k(w[:, tc[ci * 512:(ci + 1) * 512]])
    return out


def prep_sq(w, ncb):
    return np.stack([tile_k(w[:, i * 512:(i + 1) * 512]) for i in range(ncb)])


def prep_w2(w):
    out = np.zeros((4, 4, 128, 11 * 512), np.float32)
    for cb in range(4):
        for pc in range(4):
            out[cb, pc] = tile_k(w[pc * 1408:(pc + 1) * 1408, cb * 512:(cb + 1) * 512])
    return out


def gT(g):
    return np.ascontiguousarray(g.reshape(16, 128).T)


for _n in ['fm_blocks','tm_cols','tile_k','prep_win','prep_sq','prep_w2','gT']:
    setattr(hp, _n, globals()[_n])


import contextlib

D = 2048
NTOK = 2064
NMETA = 16
TP = 2070
DFF = 5632
NFMB = 28
NTMC = 3072
EPS = 1e-6

TG = [(0, 16)] + [(16 + 512 * i, 512) for i in range(4)]
TT = [(0, 16)] + [(16 + 128 * i, 128) for i in range(16)]


def pfcol(t):
    return 3 + t if t < 16 else t + 6


class Ctx:
    pass


_uid = [0]


def sbuf(C, st, name, shape, dt=F32):
    _uid[0] += 1
    return st.enter_context(C.nc.sbuf_tensor(f"{name}_{_uid[0]}", shape, dt))


def psum(C, st, name, shape, dt=F32):
    _uid[0] += 1
    return st.enter_context(C.nc.psum_tensor(f"{name}_{_uid[0]}", shape, dt))


def make_wloader(C, st, n_stage=2, n_wb=2, stage_elems=8192):
    stage = Ring([sbuf(C, st, f"wst{i}", [128, stage_elems], F32) for i in range(n_stage)])
    return stage


def load_w(C, stage, dst_ap, dst_buf, src_ap, nelem, cast_eng=POOL):
    P = C.P
    stt, stb = stage.next()
    P.dma(SP, stt[:, 0:nelem], src_ap, writes=[stb])
    P.op(cast_eng, lambda e: e.tensor_copy(out=dst_ap, in_=stt[:, 0:nelem]), reads=[stb], writes=[dst_buf])


def phase_norm(C, st, hsrc, hbuf, gT, gbuf, uT, ubuf, ps_t, ident_bf, idbuf):
    P = C.P
    hring = Ring([sbuf(C, st, f"nh{i}", [128, D], F32) for i in range(2)])
    hnring = Ring([sbuf(C, st, f"nhn{i}", [128, D], BF16) for i in range(2)])
    junk = sbuf(C, st, "njunk", [128, D], BF16)
    jb = Buf()
    stat = Ring([sbuf(C, st, f"nst{i}", [128, 4], F32) for i in range(2)])
    mh = sbuf(C, st, "nmh", [128, 1], F32)
    mhb = Buf()
    P.op(POOL, lambda e: e.memset(mh[:], -0.5), writes=[mhb])
    for (t0, nt) in TT:
        ht, hb = hring.next()
        hn, hnb = hnring.next()
        s, sb_ = stat.next()
        P.dma(SP, ht[0:nt, :], hsrc[t0:t0 + nt, :], reads=[hbuf], writes=[hb])
        P.op(ACT, lambda e, ht=ht, s=s, nt=nt: e.activation(out=junk[0:nt, :], in_=ht[0:nt, :], func=AF.Square,
                                                            accum_out=s[0:nt, 0:1]), reads=[hb], writes=[jb, sb_])
        P.op(DVE, lambda e, s=s, nt=nt: e.tensor_scalar(out=s[0:nt, 1:2], in0=s[0:nt, 0:1], scalar1=1.0 / D, scalar2=EPS,
                                                        op0=ALU.mult, op1=ALU.add), reads=[sb_], writes=[sb_])
        P.op(POOL, lambda e, s=s, nt=nt: e.tensor_tensor(out=s[0:nt, 2:3], in0=s[0:nt, 1:2], in1=mh[0:nt, :], op=ALU.pow),
             reads=[sb_, mhb], writes=[sb_])
        P.op(DVE, lambda e, ht=ht, hn=hn, s=s, nt=nt: e.tensor_scalar(out=hn[0:nt, :], in0=ht[0:nt, :], scalar1=s[0:nt, 2:3],
                                                                      scalar2=None, op0=ALU.mult), reads=[hb, sb_], writes=[hnb])
        for half in range(2):
            pt_, ptb = ps_t.next()
            for k in range(8):
                kb = half * 8 + k
                P.op(PE, lambda e, pt_=pt_, hn=hn, kb=kb, k=k, nt=nt: e.transpose(
                    out=pt_[:, k * 128:k * 128 + nt], in_=hn[0:nt, kb * 128:(kb + 1) * 128], identity=ident_bf[0:nt, 0:nt]),
                    reads=[hnb, idbuf], writes=[ptb])
            eng = DVE if half == 0 else POOL
            if half == 0:
                P.op(DVE, lambda e, pt_=pt_, nt=nt, t0=t0, half=half: e.tensor_tensor(
                    out=uT[:, half * 8:half * 8 + 8, t0:t0 + nt],
                    in0=pt_[:].rearrange("p (k t) -> p k t", k=8)[:, :, 0:nt],
                    in1=gT[:, half * 8:half * 8 + 8].unsqueeze(2).broadcast_to([128, 8, nt]), op=ALU.mult),
                    reads=[ptb, gbuf], pwrites=[ubuf])
            else:
                P.op(DVE, lambda e, pt_=pt_, nt=nt, t0=t0, half=half: e.tensor_tensor(
                    out=uT[:, half * 8:half * 8 + 8, t0:t0 + nt],
                    in0=pt_[:].rearrange("p (k t) -> p k t", k=8)[:, :, 0:nt],
                    in1=gT[:, half * 8:half * 8 + 8].unsqueeze(2).broadcast_to([128, 8, nt]), op=ALU.mult),
                    reads=[ptb, gbuf], pwrites=[ubuf])


def phase_proj(C, st, uT, ubuf, w_dram, pf, pfbuf, pt, ptbuf, ps_mm, hist_out=None, hob=None):
    P = C.P
    stage = make_wloader(C, st)
    wb = Ring([sbuf(C, st, f"pwb{i}", [128, 16, 512], BF16) for i in range(2)])
    ev = Ring([sbuf(C, st, f"pev{i}", [128, 512], F32) for i in range(4)])
    cnt = 0
    for wi in range(13):
        wt, wbuf = wb.next()
        load_w(C, stage, wt[:].rearrange("p k c -> p (k c)"), wbuf, w_dram[wi], 8192)
        if wi < 7:
            for bi in range(4):
                blk = wi * 4 + bi
                for (t0, nt) in TG:
                    pm, pmb = ps_mm.next()
                    for kb in range(16):
                        P.op(PE, lambda e, pm=pm, wt=wt, kb=kb, bi=bi, t0=t0, nt=nt: e.matmul(
                            pm[:, 0:nt], lhsT=wt[:, kb, bi * 128:(bi + 1) * 128], rhs=uT[:, kb, t0:t0 + nt],
                            start=(kb == 0), stop=(kb == 15)), reads=[wbuf, ubuf], writes=[pmb])
                    et, eb = ev.next()
                    if cnt % 2 == 0:
                        P.op(ACT, lambda e, et=et, pm=pm, nt=nt: e.activation(out=et[:, 0:nt], in_=pm[:, 0:nt], func=AF.Copy),
                             reads=[pmb], writes=[eb])
                    else:
                        P.op(DVE, lambda e, et=et, pm=pm, nt=nt: e.tensor_copy(out=et[:, 0:nt], in_=pm[:, 0:nt]),
                             reads=[pmb], writes=[eb])
                    cnt += 1
                    c0 = pfcol(t0)
                    P.dma(POOL, pf[blk * 128:(blk + 1) * 128, c0:c0 + nt], et[:, 0:nt], reads=[eb], pwrites=[pfbuf])
                    if hist_out is not None and t0 + nt == NTOK:
                        P.dma(POOL, hist_out[blk * 128:(blk + 1) * 128, :], et[:, nt - 3:nt], reads=[eb], pwrites=[hob])
        else:
            ci = wi - 7
            for (t0, nt) in TT:
                pm, pmb = ps_mm.next()
                for kb in range(16):
                    P.op(PE, lambda e, pm=pm, wt=wt, kb=kb, t0=t0, nt=nt: e.matmul(
                        pm[0:nt, :], lhsT=uT[:, kb, t0:t0 + nt], rhs=wt[:, kb, :],
                        start=(kb == 0), stop=(kb == 15)), reads=[wbuf, ubuf], writes=[pmb])
                et, eb = ev.next()
                gfn = None
                if gfn is not None:
                    P.op(ACT, lambda e, et=et, pm=pm, nt=nt, gfn=gfn: e.activation(out=et[0:nt, :], in_=pm[0:nt, :], func=gfn),
                         reads=[pmb], writes=[eb])
                elif cnt % 2 == 0:
                    P.op(ACT, lambda e, et=et, pm=pm, nt=nt: e.activation(out=et[0:nt, :], in_=pm[0:nt, :], func=AF.Copy),
                         reads=[pmb], writes=[eb])
                else:
                    P.op(DVE, lambda e, et=et, pm=pm, nt=nt: e.tensor_copy(out=et[0:nt, :], in_=pm[0:nt, :]),
                         reads=[pmb], writes=[eb])
                cnt += 1
                P.dma(POOL, pt[t0:t0 + nt, ci * 512:(ci + 1) * 512], et[0:nt, :], reads=[eb], pwrites=[ptbuf])


def phase_wout(C, st, y, ybuf, hsrc, hbuf, hdst, hdbuf, w_dram, uT, ubuf, ps_mm, ps_t, ident_bf, idbuf):
    P = C.P
    yr = Ring([sbuf(C, st, f"oy{i}", [128, D], BF16) for i in range(2)])
    for (t0, nt) in TT:
        yt, yb = yr.next()
        P.dma(SP, yt[0:nt, :], y[t0:t0 + nt, :], reads=[ybuf], writes=[yb])
        for half in range(2):
            pt_, ptb = ps_t.next()
            for k in range(8):
                kb = half * 8 + k
                P.op(PE, lambda e, pt_=pt_, yt=yt, kb=kb, k=k, nt=nt: e.transpose(
                    out=pt_[:, k * 128:k * 128 + nt], in_=yt[0:nt, kb * 128:(kb + 1) * 128], identity=ident_bf[0:nt, 0:nt]),
                    reads=[yb, idbuf], writes=[ptb])
            eng = ACT if half == 0 else DVE
            if half == 0:
                P.op(ACT, lambda e, pt_=pt_, nt=nt, t0=t0, half=half: e.activation(
                    out=uT[:, half * 8:half * 8 + 8, t0:t0 + nt],
                    in_=pt_[:].rearrange("p (k t) -> p k t", k=8)[:, :, 0:nt], func=AF.Copy), reads=[ptb], pwrites=[ubuf])
            else:
                P.op(DVE, lambda e, pt_=pt_, nt=nt, t0=t0, half=half: e.tensor_copy(
                    out=uT[:, half * 8:half * 8 + 8, t0:t0 + nt],
                    in_=pt_[:].rearrange("p (k t) -> p k t", k=8)[:, :, 0:nt]), reads=[ptb], pwrites=[ubuf])
    stage = make_wloader(C, st)
    wb = Ring([sbuf(C, st, f"owb{i}", [128, 16, 512], BF16) for i in range(2)])
    hr = Ring([sbuf(C, st, f"ohr{i}", [128, 512], F32) for i in range(3)])
    ev = Ring([sbuf(C, st, f"oev{i}", [128, 512], F32) for i in range(3)])
    nxt = wb.next()
    load_w(C, stage, nxt[0][:].rearrange("p k c -> p (k c)"), nxt[1], w_dram[0], 8192)
    for ci in range(4):
        wt, wbuf = nxt
        if ci + 1 < 4:
            nxt = wb.next()
            load_w(C, stage, nxt[0][:].rearrange("p k c -> p (k c)"), nxt[1], w_dram[ci + 1], 8192)
        for (t0, nt) in TT:
            ho, hob = hr.next()
            P.dma(SP, ho[0:nt, :], hsrc[t0:t0 + nt, ci * 512:(ci + 1) * 512], reads=[hbuf], writes=[hob])
            pm, pmb = ps_mm.next()
            for kb in range(16):
                P.op(PE, lambda e, pm=pm, wt=wt, kb=kb, t0=t0, nt=nt: e.matmul(
                    pm[0:nt, :], lhsT=uT[:, kb, t0:t0 + nt], rhs=wt[:, kb, :],
                    start=(kb == 0), stop=(kb == 15)), reads=[wbuf, ubuf], writes=[pmb])
            et, eb = ev.next()
            P.op(DVE, lambda e, et=et, pm=pm, ho=ho, nt=nt: e.tensor_tensor(out=et[0:nt, :], in0=pm[0:nt, :], in1=ho[0:nt, :],
                                                                            op=ALU.add), reads=[pmb, hob], writes=[eb])
            P.dma(POOL, hdst[t0:t0 + nt, ci * 512:(ci + 1) * 512], et[0:nt, :], reads=[eb], pwrites=[hdbuf])


def phase_ffn1(C, st, uT, ubuf, w1_dram, w3_dram, aT, abuf, ps_mm):
    P = C.P
    stage = make_wloader(C, st)
    w1b = Ring([sbuf(C, st, f"f1w{i}", [128, 16, 512], BF16) for i in range(2)])
    w3b = Ring([sbuf(C, st, f"f3w{i}", [128, 16, 512], BF16) for i in range(2)])
    sg = Ring([sbuf(C, st, f"fsg{i}", [128, 512], F32) for i in range(3)])
    av = Ring([sbuf(C, st, f"fav{i}", [128, 512], BF16) for i in range(3)])
    for gi in range(11):
        w1t, w1buf = w1b.next()
        w3t, w3buf = w3b.next()
        load_w(C, stage, w1t[:].rearrange("p k c -> p (k c)"), w1buf, w1_dram[gi], 8192)
        load_w(C, stage, w3t[:].rearrange("p k c -> p (k c)"), w3buf, w3_dram[gi], 8192)
        for bi in range(4):
            j = gi * 4 + bi
            for gidx, (t0, nt) in enumerate(TG):
                pa, pab = ps_mm.next()
                for kb in range(16):
                    P.op(PE, lambda e, pa=pa, w1t=w1t, kb=kb, bi=bi, t0=t0, nt=nt: e.matmul(
                        pa[:, 0:nt], lhsT=w1t[:, kb, bi * 128:(bi + 1) * 128], rhs=uT[:, kb, t0:t0 + nt],
                        start=(kb == 0), stop=(kb == 15)), reads=[w1buf, ubuf], writes=[pab])
                pb_, pbb = ps_mm.next()
                for kb in range(16):
                    P.op(PE, lambda e, pb_=pb_, w3t=w3t, kb=kb, bi=bi, t0=t0, nt=nt: e.matmul(
                        pb_[:, 0:nt], lhsT=w3t[:, kb, bi * 128:(bi + 1) * 128], rhs=uT[:, kb, t0:t0 + nt],
                        start=(kb == 0), stop=(kb == 15)), reads=[w3buf, ubuf], writes=[pbb])
                s, sb_ = sg.next()
                a, ab_ = av.next()
                P.op(ACT, lambda e, s=s, pa=pa, nt=nt: e.activation(out=s[:, 0:nt], in_=pa[:, 0:nt], func=AF.Silu),
                     reads=[pab], writes=[sb_])
                P.op(DVE, lambda e, a=a, s=s, pb_=pb_, nt=nt: e.tensor_tensor(out=a[:, 0:nt], in0=pb_[:, 0:nt], in1=s[:, 0:nt],
                                                                              op=ALU.mult), reads=[pbb, sb_], writes=[ab_])
                P.dma(POOL, aT[gidx, :, j, 0:nt], a[:, 0:nt], reads=[ab_], pwrites=[abuf])


def phase_ffn2(C, st, aT, abuf, w2_dram, hsrc, hbuf, hdst, hdbuf, ps_mm):
    P = C.P
    stage = Ring([sbuf(C, st, f"gst{i}", [128, 11 * 512], F32) for i in range(2)])
    w2b = Ring([sbuf(C, st, f"gw{i}", [128, 44, 512], BF16) for i in range(1)])
    ar = Ring([sbuf(C, st, f"gar{i}", [128, 44, 512], BF16) for i in range(2)])
    hr = Ring([sbuf(C, st, f"ghr{i}", [128, 512], F32) for i in range(3)])
    ev = Ring([sbuf(C, st, f"gev{i}", [128, 512], F32) for i in range(3)])
    for cb in range(4):
        wt, wbuf = w2b.next()
        for pc in range(4):
            load_w(C, stage, wt[:, pc * 11:(pc + 1) * 11, :].rearrange("p k c -> p (k c)"), wbuf, w2_dram[cb, pc], 11 * 512)
        for gidx, (g0, gn) in enumerate(TG):
            at, atb = ar.next()
            P.dma(SP, at[:, :, 0:gn], aT[gidx, :, :, 0:gn], reads=[abuf], writes=[atb])
            for s0 in range(0, gn, 128):
                nt = min(128, gn - s0)
                t0 = g0 + s0
                ho, hob = hr.next()
                P.dma(SP, ho[0:nt, :], hsrc[t0:t0 + nt, cb * 512:(cb + 1) * 512], reads=[hbuf], writes=[hob])
                pm, pmb = ps_mm.next()
                for j in range(44):
                    P.op(PE, lambda e, pm=pm, at=at, wt=wt, j=j, s0=s0, nt=nt: e.matmul(
                        pm[0:nt, :], lhsT=at[:, j, s0:s0 + nt], rhs=wt[:, j, :],
                        start=(j == 0), stop=(j == 43)), reads=[wbuf, atb], writes=[pmb])
                et, eb = ev.next()
                P.op(DVE, lambda e, et=et, pm=pm, ho=ho, nt=nt: e.tensor_tensor(out=et[0:nt, :], in0=pm[0:nt, :], in1=ho[0:nt, :],
                                                                                op=ALU.add), reads=[pmb, hob], writes=[eb])
                P.dma(POOL, hdst[t0:t0 + nt, cb * 512:(cb + 1) * 512], et[0:nt, :], reads=[eb], pwrites=[hdbuf])


def phase_final_norm(C, st, hsrc, hbuf, gbc, gbcb, out, obuf):
    P = C.P
    hring = Ring([sbuf(C, st, f"zh{i}", [128, D], F32) for i in range(2)])
    oring = Ring([sbuf(C, st, f"zo{i}", [128, D], F32) for i in range(2)])
    junk = sbuf(C, st, "zjunk", [128, D], BF16)
    jb = Buf()
    stat = Ring([sbuf(C, st, f"zst{i}", [128, 4], F32) for i in range(2)])
    mh = sbuf(C, st, "zmh", [128, 1], F32)
    mhb = Buf()
    P.op(POOL, lambda e: e.memset(mh[:], -0.5), writes=[mhb])
    for (t0, nt) in TT[1:]:
        ht, hb = hring.next()
        ot, ob = oring.next()
        s, sb_ = stat.next()
        P.dma(SP, ht[0:nt, :], hsrc[t0:t0 + nt, :], reads=[hbuf], writes=[hb])
        P.op(ACT, lambda e, ht=ht, s=s, nt=nt: e.activation(out=junk[0:nt, :], in_=ht[0:nt, :], func=AF.Square,
                                                            accum_out=s[0:nt, 0:1]), reads=[hb], writes=[jb, sb_])
        P.op(DVE, lambda e, s=s, nt=nt: e.tensor_scalar(out=s[0:nt, 1:2], in0=s[0:nt, 0:1], scalar1=1.0 / D, scalar2=EPS,
                                                        op0=ALU.mult, op1=ALU.add), reads=[sb_], writes=[sb_])
        P.op(POOL, lambda e, s=s, nt=nt: e.tensor_tensor(out=s[0:nt, 2:3], in0=s[0:nt, 1:2], in1=mh[0:nt, :], op=ALU.pow),
             reads=[sb_, mhb], writes=[sb_])
        P.op(DVE, lambda e, ht=ht, ot=ot, s=s, nt=nt: e.scalar_tensor_tensor(
            out=ot[0:nt, :], in0=ht[0:nt, :], scalar=s[0:nt, 2:3], in1=gbc[0:nt, :], op0=ALU.mult, op1=ALU.mult),
            reads=[hb, sb_, gbcb], writes=[ob])
        P.dma(POOL, out[t0 - 16:t0 - 16 + nt, :], ot[0:nt, :], reads=[ob], pwrites=[obuf])


import contextlib

SEGS = [(3, 16, 0), (22, 2048, 16)]
LDK = 0.6065306597126334


def chunks_of(seg):
    c0, ncol, t0 = seg
    if ncol == 16:
        return [(c0, 16, t0)]
    return [(c0 + 64 * i, 64, t0 + 64 * i) for i in range(ncol // 64)]


def fix_gap(C, x, xb, hist_src, flagE, fb, tmp_ring, rows):
    P = C.P
    t, tb = tmp_ring.next()
    P.dma(SP, t[0:rows, 0:3], hist_src, writes=[tb])
    P.op(DVE, lambda e: e.scalar_tensor_tensor(out=x[0:rows, 19:22], in0=x[0:rows, 16:19], scalar=flagE[0:rows, 0:1],
                                               in1=t[0:rows, 0:3], op0=ALU.mult, op1=ALU.add), reads=[xb, tb, fb], writes=[xb])


def chunk_rel(C, out, ob, src, sb_, rows):
    P = C.P
    P.op(DVE, lambda e: e.tensor_copy(out=out[0:rows, 3:19], in_=src[0:rows, 3:19]), reads=[sb_], writes=[ob])
    P.op(DVE, lambda e: e.tensor_copy(out=out[0:rows, 22:86], in_=src[0:rows, 22:86]), reads=[sb_], writes=[ob])
    o3 = out[0:rows, 86:2070].rearrange("p (c j) -> p c j", j=64)
    s3 = src[0:rows, 86:2070].rearrange("p (c j) -> p c j", j=64)
    pv = src[0:rows, 22:2006].rearrange("p (c j) -> p c j", j=64)[:, :, 63:64].broadcast_to([rows, 31, 64])
    P.op(DVE, lambda e: e.tensor_tensor(out=o3, in0=s3, in1=pv, op=ALU.subtract), reads=[sb_], writes=[ob])


def gate_prepass(C, st, pt, ptb):
    P = C.P
    r = Ring([sbuf(C, st, f"gp{i}", [128, 1536], F32) for i in range(3)])
    for (c0, c1, fn) in ((512, 1024, AF.Silu), (2048, 3072, AF.Sigmoid)):
        w = c1 - c0
        for (t0, nt) in TT:
            t, tb = r.next()
            P.dma(SP, t[0:nt, 0:w], pt[t0:t0 + nt, c0:c1], reads=[ptb], writes=[tb])
            P.op(ACT, lambda e, t=t, nt=nt, w=w, fn=fn: e.activation(out=t[0:nt, 0:w], in_=t[0:nt, 0:w], func=fn), reads=[tb], writes=[tb])
            P.dma(POOL, pt[t0:t0 + nt, c0:c1], t[0:nt, 0:w], reads=[tb], writes=[ptb])


def mixer_gla(C, st, pf, pfb, pt, ptb, y, yb, prm, K, npsA=2, npsB=6):
    P = C.P
    ones, onesb, ident, idb, mask_i, mib, flagE, fb = K.ones, K.onesb, K.ident, K.idb, K.mask_i, K.mib, K.flagE, K.fb
    a2 = sbuf(C, st, "ga2", [32, 256]); a2b = Buf()
    P.op(DVE, lambda e: e.memset(a2[:], 0.0), writes=[a2b])
    nab = sbuf(C, st, "gnab", [64, 4]); nabb = Buf()
    nbc = sbuf(C, st, "gnbc", [64, 128]); nbcb = Buf()
    P.dma(SP, a2[0:16, :], prm["gla_a2"], reads=[a2b], writes=[a2b])
    P.dma(SP, nab[:], prm["gla_ab"], writes=[nabb])
    P.op(DVE, lambda e: e.tensor_scalar(out=nab[:], in0=nab[:], scalar1=-1.0, scalar2=None, op0=ALU.mult), reads=[nabb], writes=[nabb])
    P.dma(SP, nbc[:], prm["gla_normbc"], writes=[nbcb])
    xa = sbuf(C, st, "gxa", [32, TP]); xab = Buf()
    P.dma(SP, xa[:], pf[18 * 128:18 * 128 + 32, :], reads=[pfb], writes=[xab])
    q = sbuf(C, st, "gq", [64, TP]); k = sbuf(C, st, "gk", [64, TP]); sp = sbuf(C, st, "gsp", [64, TP])
    spc = sbuf(C, st, "gspc", [64, TP]); e1 = sbuf(C, st, "ge1", [64, TP]); e2 = sbuf(C, st, "ge2", [64, TP])
    qb, kb_, spb, spcb, e1b, e2b = [Buf() for _ in range(6)]
    S = sbuf(C, st, "gS", [64, 128]); Sb = Buf()
    S16 = sbuf(C, st, "gS16", [64, 128], BF16); S16b = Buf()
    qbf = sbuf(C, st, "gqbf", [64, TP], BF16); kbf = sbuf(C, st, "gkbf", [64, TP], BF16); qbfb, kbfb = Buf(), Buf()
    v16r = Ring([sbuf(C, st, f"gv16{i}", [64, 128], BF16) for i in range(3)])
    Sin = sbuf(C, st, "gSin", [64, 128]); Sinb = Buf()
    psA = Ring([psum(C, st, f"gpa{i}", [128, 512], F32) for i in range(npsA)])
    psB = Ring([psum(C, st, f"gpb{i}", [128, 512], F32) for i in range(npsB)])
    vr = Ring([sbuf(C, st, f"gv{i}", [64, 256], F32) for i in range(3)])
    ktr = Ring([sbuf(C, st, f"gkt{i}", [64, 64], BF16) for i in range(2)])
    scr = Ring([sbuf(C, st, f"gsc{i}", [64, 64], BF16) for i in range(2)])
    str_ = Ring([sbuf(C, st, f"gst{i}", [64, 4], F32) for i in range(2)])
    junk = sbuf(C, st, "gjunk", [64, 128], F32); jb = Buf()
    mh = sbuf(C, st, "gmh", [64, 1], F32); mhb = Buf()
    P.op(POOL, lambda e: e.memset(mh[:], -0.5), writes=[mhb])
    t1r = Ring([sbuf(C, st, f"gt1{i}", [64, 128], F32) for i in range(2)])
    yor = Ring([sbuf(C, st, f"gyo{i}", [64, 128], BF16) for i in range(2)])
    for h in range(4):
        r0 = (14 + h // 2) * 128 + (h % 2) * 64
        r1 = (16 + h // 2) * 128 + (h % 2) * 64
        P.dma(SP, q[:], pf[r0:r0 + 64, :], reads=[pfb], writes=[qb])
        P.dma(SP, k[:], pf[r1:r1 + 64, :], reads=[pfb], writes=[kb_])
        for c0 in range(3, TP, 512):
            n = min(512, TP - c0)
            pa, pab = psA.next()
            P.op(PE, lambda e, pa=pa, c0=c0, n=n, h=h: e.matmul(pa[0:64, 0:n], lhsT=a2[0:32, h * 64:(h + 1) * 64], rhs=xa[0:32, c0:c0 + n],
                                                               start=True, stop=True), reads=[a2b, xab], writes=[pab])
            P.op(ACT, lambda e, pa=pa, c0=c0, n=n, h=h: e.activation(out=sp[:, c0:c0 + n], in_=pa[0:64, 0:n], func=AF.Exp, scale=-1.0,
                                                                    bias=nab[:, h:h + 1]), reads=[pab, nabb], writes=[spb])
        P.op(ACT, lambda e: e.activation(out=sp[:, 3:TP], in_=sp[:, 3:TP], func=AF.Ln, bias=1.0), reads=[spb], writes=[spb])
        for (c0, ncol, t0) in SEGS:
            P.op(DVE, lambda e, c0=c0, ncol=ncol: e.tensor_tensor_scan(out=spc[:, c0:c0 + ncol], data0=ones[0:64, c0:c0 + ncol],
                                                                       data1=sp[:, c0:c0 + ncol], initial=0.0, op0=ALU.mult, op1=ALU.add),
                 reads=[spb, onesb], writes=[spcb])
        chunk_rel(C, sp, spb, spc, spcb, 64)
        P.op(ACT, lambda e: e.activation(out=e1[:, 3:TP], in_=sp[:, 3:TP], func=AF.Exp, scale=-1.0 / 16), reads=[spb], writes=[e1b])
        P.op(ACT, lambda e: e.activation(out=e2[:, 3:TP], in_=sp[:, 3:TP], func=AF.Exp, scale=1.0 / 16), reads=[spb], writes=[e2b])
        P.op(DVE, lambda e: e.scalar_tensor_tensor(out=q[:, 3:TP], in0=q[:, 3:TP], scalar=0.125, in1=e1[:, 3:TP], op0=ALU.mult,
                                                   op1=ALU.mult), reads=[qb, e1b], writes=[qb])
        P.op(DVE, lambda e: e.tensor_tensor(out=k[:, 3:TP], in0=k[:, 3:TP], in1=e2[:, 3:TP], op=ALU.mult), reads=[kb_, e2b], writes=[kb_])
        P.op(ACT, lambda e: e.activation(out=qbf[:, 3:TP], in_=q[:, 3:TP], func=AF.Copy), reads=[qb], writes=[qbfb])
        P.op(ACT, lambda e: e.activation(out=kbf[:, 3:TP], in_=k[:, 3:TP], func=AF.Copy), reads=[kb_], writes=[kbfb])
        P.op(DVE, lambda e: e.memset(S[:], 0.0), writes=[Sb])
        P.op(DVE, lambda e: e.memset(S16[:], 0.0), writes=[S16b])
        for si, seg in enumerate(SEGS):
            if si == 1:
                P.dma(SP, Sin[:], prm["sB_in"][h], writes=[Sinb])
                P.op(DVE, lambda e: e.scalar_tensor_tensor(out=S[:], in0=S[:], scalar=flagE[0:64, 0:1], in1=Sin[:], op0=ALU.mult,
                                                           op1=ALU.add), reads=[Sb, Sinb, fb], writes=[Sb])
                P.op(ACT, lambda e: e.activation(out=S16[:], in_=S[:], func=AF.Copy), reads=[Sb], writes=[S16b])
            for (c0, n, t0) in chunks_of(seg):
                vt, vb = vr.next()
                P.dma(SP, vt[0:n, 0:128], pt[t0:t0 + n, h * 128:(h + 1) * 128], reads=[ptb], writes=[vb])
                P.dma(SP, vt[0:n, 128:256], pt[t0:t0 + n, 512 + h * 128:512 + (h + 1) * 128], reads=[ptb], writes=[vb])
                pb, pbb = psB.next()
                P.op(PE, lambda e, pb=pb, c0=c0, n=n: e.transpose(out=pb[0:n, 0:64], in_=k[:, c0:c0 + n], identity=ident[0:64, 0:64]),
                     reads=[kb_, idb], writes=[pbb])
                P.op(PE, lambda e, pb=pb, c0=c0, n=n: e.matmul(pb[0:n, 64:64 + n], lhsT=kbf[:, c0:c0 + n], rhs=qbf[:, c0:c0 + n], start=True, stop=True),
                     reads=[kbfb, qbfb], writes=[pbb])
                v16, v16b = v16r.next()
                P.op(ACT, lambda e, v16=v16, vt=vt, n=n: e.activation(out=v16[0:n, :], in_=vt[0:n, 0:128], func=AF.Copy), reads=[vb], writes=[v16b])
                kt, ktb = ktr.next()
                sc, scb = scr.next()
                P.op(ACT, lambda e, kt=kt, pb=pb, n=n: e.activation(out=kt[0:n, :], in_=pb[0:n, 0:64], func=AF.Copy), reads=[pbb], writes=[ktb])
                P.op(DVE, lambda e, sc=sc, pb=pb, n=n: e.tensor_tensor(out=sc[0:n, 0:n], in0=pb[0:n, 64:64 + n], in1=mask_i[0:n, 0:n], op=ALU.mult),
                     reads=[pbb, mib], writes=[scb])
                po, pob = psB.next()
                P.op(PE, lambda e, po=po, c0=c0, n=n: e.matmul(po[0:n, 0:128], lhsT=qbf[:, c0:c0 + n], rhs=S16[:, :], start=True, stop=False),
                     reads=[qbfb, S16b], writes=[pob])
                P.op(PE, lambda e, po=po, sc=sc, v16=v16, n=n: e.matmul(po[0:n, 0:128], lhsT=sc[0:n, 0:n], rhs=v16[0:n, :], start=False, stop=True),
                     reads=[scb, v16b], writes=[pob])
                pc, pcb = psB.next()
                P.op(PE, lambda e, pc=pc, kt=kt, v16=v16, n=n: e.matmul(pc[0:64, 0:128], lhsT=kt[0:n, 0:64], rhs=v16[0:n, :], start=True, stop=True),
                     reads=[ktb, v16b], writes=[pcb])
                ce = c0 + n - 1
                P.op(DVE, lambda e, pc=pc: e.tensor_tensor(out=S[:], in0=S[:], in1=pc[0:64, 0:128], op=ALU.add), reads=[pcb, Sb], writes=[Sb])
                P.op(DVE, lambda e, ce=ce: e.tensor_scalar(out=S[:], in0=S[:], scalar1=e1[:, ce:ce + 1], scalar2=None, op0=ALU.mult),
                     reads=[Sb, e1b], writes=[Sb])
                P.op(ACT, lambda e: e.activation(out=S16[:], in_=S[:], func=AF.Copy), reads=[Sb], writes=[S16b])
                if C.emit_out and not False:
                    s_, sb2 = str_.next()
                    t1, t1b = t1r.next()
                    yo, yob = yor.next()
                    P.op(ACT, lambda e, t1=t1, po=po, n=n: e.activation(out=t1[0:n, :], in_=po[0:n, 0:128], func=AF.Copy), reads=[pob], writes=[t1b])
                    P.op(DVE, lambda e, t1=t1, n=n: e.tensor_tensor(out=junk[0:n, :], in0=t1[0:n, :], in1=t1[0:n, :], op=ALU.mult), reads=[t1b], writes=[jb])
                    P.op(DVE, lambda e, s_=s_, n=n: e.tensor_reduce(out=s_[0:n, 0:1], in_=junk[0:n, :], axis=AX.X, op=ALU.add), reads=[jb], writes=[sb2])
                    P.op(DVE, lambda e, s_=s_, n=n: e.tensor_scalar(out=s_[0:n, 1:2], in0=s_[0:n, 0:1], scalar1=1.0 / 128, scalar2=EPS, op0=ALU.mult,
                                                                   op1=ALU.add), reads=[sb2], writes=[sb2])
                    P.op(POOL, lambda e, s_=s_, n=n: e.tensor_tensor(out=s_[0:n, 2:3], in0=s_[0:n, 1:2], in1=mh[0:n, :], op=ALU.pow),
                         reads=[sb2, mhb], writes=[sb2])
                    P.op(DVE, lambda e, t1=t1, s_=s_, n=n: e.scalar_tensor_tensor(out=t1[0:n, :], in0=t1[0:n, :], scalar=s_[0:n, 2:3],
                                                                               in1=nbc[0:n, :], op0=ALU.mult, op1=ALU.mult),
                         reads=[sb2, nbcb, t1b], writes=[t1b])
                    P.op(DVE, lambda e, yo=yo, t1=t1, vt=vt, n=n: e.tensor_tensor(out=yo[0:n, :], in0=t1[0:n, :], in1=vt[0:n, 128:256], op=ALU.mult),
                         reads=[t1b, vb], writes=[yob])
                    P.dma(SP, y[t0:t0 + n, 512 + h * 128:512 + (h + 1) * 128], yo[0:n, :], reads=[yob], pwrites=[yb])
                yield
        P.dma(POOL, prm["sB_out"][h], S[:], reads=[Sb], pwrites=[K.sob])


def mixer_mlstm(C, st, pf, pfb, pt, ptb, y, yb, prm, K, npsA=4, npsG=2):
    P = C.P
    ones, onesb, ident, idb, mask_i, mib, flagE, fb = K.ones, K.onesb, K.ident, K.idb, K.mask_i, K.mib, K.flagE, K.fb
    cw = sbuf(C, st, "mcw", [128, 8, 4]); cb = sbuf(C, st, "mcb", [128, 8]); cwb = Buf()
    ib = sbuf(C, st, "mib", [4, 2]); fbb = sbuf(C, st, "mfb", [4, 2]); gbb = Buf()
    nbc = sbuf(C, st, "mnbc", [64, 1024]); nbcb = Buf()
    oh = sbuf(C, st, "moh", [4, 4, 128]); ohb = Buf()
    P.dma(SP, cw[:], prm["ml_cw"], writes=[cwb]); P.dma(SP, cb[:], prm["ml_cb"], writes=[cwb])
    P.dma(SP, ib[:, 0:1], prm["ml_ib"], writes=[gbb]); P.dma(SP, fbb[:, 0:1], prm["ml_fb"], writes=[gbb])
    P.dma(SP, nbc[:], prm["ml_normbc"], writes=[nbcb]); P.dma(SP, oh[:], prm["onehot"], writes=[ohb])
    P.op(DVE, lambda e: e.tensor_scalar(out=ib[:, 1:2], in0=ib[:, 0:1], scalar1=1.0 / 15, scalar2=None, op0=ALU.mult), reads=[gbb], writes=[gbb])
    P.op(DVE, lambda e: e.tensor_scalar(out=fbb[:, 1:2], in0=fbb[:, 0:1], scalar1=1.0 / 15, scalar2=None, op0=ALU.mult), reads=[gbb], writes=[gbb])
    gi = sbuf(C, st, "mgi", [4, TP]); gf = sbuf(C, st, "mgf", [4, TP]); SPc = sbuf(C, st, "mSP", [4, TP]); av = sbuf(C, st, "mav", [4, TP])
    MU = sbuf(C, st, "mMU", [4, TP]); MUS = sbuf(C, st, "mMUS", [4, TP])
    G8 = sbuf(C, st, "mG8", [36, TP]); FE = sbuf(C, st, "mFE", [4, 64]); min_ = sbuf(C, st, "mmin", [4, 4])
    gib, gfb_, SPb, avb, MUb, MUSb, G8b, FEb, minb = [Buf() for _ in range(9)]
    P.dma(SP, gi[:], pf[27 * 128:27 * 128 + 4, :], reads=[pfb], writes=[gib])
    P.dma(SP, gf[:], pf[27 * 128 + 4:27 * 128 + 8, :], reads=[pfb], writes=[gfb_])
    P.op(DVE, lambda e: e.memset(G8[:], 0.0), writes=[G8b])
    P.op(ACT, lambda e: e.activation(out=gi[:], in_=gi[:], func=AF.Tanh, scale=1.0 / 15, bias=ib[:, 1:2]), reads=[gib, gbb], writes=[gib])
    P.op(ACT, lambda e: e.activation(out=gf[:], in_=gf[:], func=AF.Tanh, scale=1.0 / 15, bias=fbb[:, 1:2]), reads=[gfb_, gbb], writes=[gfb_])
    P.op(ACT, lambda e: e.activation(out=gf[:], in_=gf[:], func=AF.Exp, scale=-15.0), reads=[gfb_], writes=[gfb_])
    P.op(ACT, lambda e: e.activation(out=gf[:], in_=gf[:], func=AF.Ln, bias=1.0), reads=[gfb_], writes=[gfb_])
    P.dma(SP, min_[:, 0:1], prm["mC_in"], writes=[minb])
    for si, (c0, ncol, t0) in enumerate(SEGS):
        P.op(DVE, lambda e, c0=c0, ncol=ncol: e.tensor_tensor_scan(out=SPc[:, c0:c0 + ncol], data0=ones[0:4, c0:c0 + ncol], data1=gf[:, c0:c0 + ncol],
                                                                   initial=0.0, op0=ALU.mult, op1=ALU.add), reads=[gfb_, onesb], writes=[SPb])
        P.op(DVE, lambda e, c0=c0, ncol=ncol: e.scalar_tensor_tensor(out=av[:, c0:c0 + ncol], in0=gi[:, c0:c0 + ncol], scalar=15.0,
                                                                     in1=SPc[:, c0:c0 + ncol], op0=ALU.mult, op1=ALU.add), reads=[gib, SPb], writes=[avb])
        if si == 0:
            P.op(DVE, lambda e, c0=c0, ncol=ncol: e.tensor_tensor_scan(out=MU[:, c0:c0 + ncol], data0=av[:, c0:c0 + ncol], data1=av[:, c0:c0 + ncol],
                                                                       initial=0.0, op0=ALU.max, op1=ALU.max), reads=[avb], writes=[MUb])
            P.op(DVE, lambda e: e.memset(MUS[:, 3:19], 0.0), writes=[MUSb])
            P.op(DVE, lambda e: e.tensor_tensor(out=min_[:, 1:2], in0=MU[:, 18:19], in1=SPc[:, 18:19], op=ALU.subtract), reads=[MUb, SPb, minb], writes=[minb])
            P.op(DVE, lambda e: e.scalar_tensor_tensor(out=min_[:, 2:3], in0=min_[:, 1:2], scalar=flagE[0:4, 0:1], in1=min_[:, 0:1], op0=ALU.mult,
                                                       op1=ALU.add), reads=[minb, fb], writes=[minb])
        else:
            P.op(DVE, lambda e, c0=c0, ncol=ncol: e.tensor_tensor_scan(out=MU[:, c0:c0 + ncol], data0=av[:, c0:c0 + ncol], data1=av[:, c0:c0 + ncol],
                                                                       initial=min_[:, 2:3], op0=ALU.max, op1=ALU.max), reads=[avb, minb], writes=[MUb])
            P.op(DVE, lambda e: e.tensor_copy(out=MUS[:, 22:86], in_=min_[:, 2:3].broadcast_to([4, 64])), reads=[minb], writes=[MUSb])
            P.op(DVE, lambda e: e.tensor_copy(out=MUS[:, 86:2070].rearrange("p (c j) -> p c j", j=64),
                                              in_=MU[:, 22:2006].rearrange("p (c j) -> p c j", j=64)[:, :, 63:64].broadcast_to([4, 31, 64])),
                 reads=[MUb], writes=[MUSb])
    P.op(DVE, lambda e: e.tensor_tensor(out=av[:, 3:TP], in0=av[:, 3:TP], in1=MUS[:, 3:TP], op=ALU.subtract), reads=[avb, MUSb], writes=[avb])
    P.op(ACT, lambda e: e.activation(out=G8[0:4, 3:TP], in_=av[:, 3:TP], func=AF.Exp), reads=[avb, G8b], writes=[G8b])
    P.op(DVE, lambda e: e.tensor_tensor(out=av[:, 3:TP], in0=SPc[:, 3:TP], in1=MUS[:, 3:TP], op=ALU.subtract), reads=[SPb, MUSb, G8b], writes=[avb])
    P.op(ACT, lambda e: e.activation(out=G8[32:36, 3:TP], in_=av[:, 3:TP], func=AF.Exp), reads=[avb, G8b], writes=[G8b])
    P.op(DVE, lambda e: e.tensor_tensor(out=FE[:, 0:1], in0=MUS[:, 3:4], in1=MU[:, 18:19], op=ALU.subtract), reads=[MUSb, MUb], writes=[FEb])
    P.op(DVE, lambda e: e.tensor_tensor(out=FE[:, 1:33], in0=MUS[:, 22:2070].rearrange("p (c j) -> p c j", j=64)[:, :, 0],
                                        in1=MU[:, 22:2070].rearrange("p (c j) -> p c j", j=64)[:, :, 63], op=ALU.subtract),
         reads=[MUSb, MUb, FEb], writes=[FEb])
    P.op(ACT, lambda e: e.activation(out=FE[:, 0:33], in_=FE[:, 0:33], func=AF.Exp), reads=[FEb], writes=[FEb])
    P.op(DVE, lambda e: e.tensor_tensor(out=min_[:, 3:4], in0=MU[:, TP - 1:TP], in1=SPc[:, TP - 1:TP], op=ALU.subtract), reads=[MUb, SPb, minb], writes=[minb])
    P.dma(POOL, prm["mC_out"], min_[:, 3:4], reads=[minb], pwrites=[K.sob])
    xq = sbuf(C, st, "mxq", [128, TP]); xk = sbuf(C, st, "mxk", [128, TP]); q = sbuf(C, st, "mq", [128, TP]); k = sbuf(C, st, "mk", [128, TP])
    xqb, xkb, qb, kb_ = [Buf() for _ in range(4)]
    tmpr = Ring([sbuf(C, st, f"mtmp{i}", [128, 4], F32) for i in range(2)])
    CX = sbuf(C, st, "mCX", [128, 257]); CXb = Buf()
    CX16 = sbuf(C, st, "mCX16", [128, 258], BF16); CX16b = Buf()
    qbf = sbuf(C, st, "mqbf", [128, TP], BF16); kbf = sbuf(C, st, "mkbf", [128, TP], BF16); qbfb, kbfb = Buf(), Buf()
    CXin = sbuf(C, st, "mCXin", [128, 257]); CXinb = Buf()
    FB = sbuf(C, st, "mFB", [128, 64]); FBb = Buf()
    psA = Ring([psum(C, st, f"mpa{i}", [128, 512], F32) for i in range(npsA)])
    psG = Ring([psum(C, st, f"mpg{i}", [128, 512], F32) for i in range(npsG)])
    vr = Ring([sbuf(C, st, f"mv{i}", [64, 512], F32) for i in range(3)])
    vxr = Ring([sbuf(C, st, f"mvx{i}", [64, 258], BF16) for i in range(2)])
    for _t, _b in zip(vxr.tiles, vxr.bufs):
        P.op(DVE, lambda e, _t=_t: e.memset(_t[:], 0.0), writes=[_b])
    ktr = Ring([sbuf(C, st, f"mkt{i}", [64, 128], BF16) for i in range(2)])
    scr = Ring([sbuf(C, st, f"msc{i}", [64, 64], BF16) for i in range(2)])
    gtr = Ring([sbuf(C, st, f"mgt{i}", [64, 36], F32) for i in range(2)])
    str_ = Ring([sbuf(C, st, f"mst{i}", [64, 8], F32) for i in range(2)])
    junk = sbuf(C, st, "mjunk", [64, 256], F32); jb = Buf()
    mh = sbuf(C, st, "mmh", [64, 1], F32); mhb = Buf()
    P.op(POOL, lambda e: e.memset(mh[:], -0.5), writes=[mhb])
    t1r = Ring([sbuf(C, st, f"mt1{i}", [64, 256], F32) for i in range(2)])
    yor = Ring([sbuf(C, st, f"myo{i}", [64, 256], BF16) for i in range(2)])
    for h in range(4):
        P.dma(SP, xq[:], pf[(19 + h) * 128:(20 + h) * 128, :], reads=[pfb], writes=[xqb])
        P.dma(SP, xk[:], pf[(23 + h) * 128:(24 + h) * 128, :], reads=[pfb], writes=[xkb])
        P.op(DVE, lambda e: e.memset(xq[:, 0:3], 0.0), reads=[xqb], writes=[xqb])
        P.op(DVE, lambda e: e.memset(xk[:, 0:3], 0.0), reads=[xkb], writes=[xkb])
        fix_gap(C, xq, xqb, prm["hist_in"][(19 + h) * 128:(20 + h) * 128, :], flagE, fb, tmpr, 128)
        fix_gap(C, xk, xkb, prm["hist_in"][(23 + h) * 128:(24 + h) * 128, :], flagE, fb, tmpr, 128)
        for (x, xb, o, ob, j) in ((xq, xqb, q, qb, h), (xk, xkb, k, kb_, 4 + h)):
            P.op(DVE, lambda e, x=x, o=o, j=j: e.tensor_scalar(out=o[:, 3:TP], in0=x[:, 0:TP - 3], scalar1=cw[:, j, 0:1], scalar2=cb[:, j:j + 1],
                                                               op0=ALU.mult, op1=ALU.add), reads=[xb, cwb], writes=[ob])
            for tap in range(1, 4):
                P.op(DVE, lambda e, x=x, o=o, j=j, tap=tap: e.scalar_tensor_tensor(out=o[:, 3:TP], in0=x[:, tap:TP - 3 + tap], scalar=cw[:, j, tap:tap + 1],
                                                                                 in1=o[:, 3:TP], op0=ALU.mult, op1=ALU.add), reads=[xb, cwb, ob], writes=[ob])
            P.op(ACT, lambda e, o=o: e.activation(out=o[:, 3:TP], in_=o[:, 3:TP], func=AF.Silu), reads=[ob], writes=[ob])
        P.op(DVE, lambda e: e.tensor_scalar(out=k[:, 3:TP], in0=k[:, 3:TP], scalar1=128 ** -0.5, scalar2=None, op0=ALU.mult), reads=[kb_], writes=[kb_])
        P.op(ACT, lambda e: e.activation(out=qbf[:, 3:TP], in_=q[:, 3:TP], func=AF.Copy), reads=[qb], writes=[qbfb])
        P.op(ACT, lambda e: e.activation(out=kbf[:, 3:TP], in_=k[:, 3:TP], func=AF.Copy), reads=[kb_], writes=[kbfb])
        pg, pgb = psG.next()
        P.op(PE, lambda e, pg=pg, h=h: e.matmul(pg[:, 0:33], lhsT=oh[0:4, h, :], rhs=FE[0:4, 0:33], start=True, stop=True), reads=[ohb, FEb], writes=[pgb])
        P.op(ACT, lambda e, pg=pg: e.activation(out=FB[:, 0:33], in_=pg[:, 0:33], func=AF.Copy), reads=[pgb], writes=[FBb])
        P.op(DVE, lambda e: e.memset(CX[:], 0.0), writes=[CXb])
        P.op(DVE, lambda e: e.memset(CX16[:], 0.0), writes=[CX16b])
        ci = 0
        for si, seg in enumerate(SEGS):
            if si == 1:
                P.dma(SP, CXin[:], prm["sC_in"][h], writes=[CXinb])
                P.op(DVE, lambda e: e.scalar_tensor_tensor(out=CX[:], in0=CX[:], scalar=flagE[:, 0:1], in1=CXin[:], op0=ALU.mult, op1=ALU.add),
                     reads=[CXb, CXinb, fb], writes=[CXb])
                P.op(ACT, lambda e: e.activation(out=CX16[:, 0:257], in_=CX[:], func=AF.Copy), reads=[CXb, CX16b], writes=[CX16b])
            for (c0, n, t0) in chunks_of(seg):
                vt, vb = vr.next()
                P.dma(SP, vt[0:n, 0:256], pt[t0:t0 + n, 1024 + h * 256:1024 + (h + 1) * 256], reads=[ptb], writes=[vb])
                P.dma(SP, vt[0:n, 256:512], pt[t0:t0 + n, 2048 + h * 256:2048 + (h + 1) * 256], reads=[ptb], writes=[vb])
                pa, pab = psA.next()
                P.op(PE, lambda e, pa=pa, c0=c0, n=n: e.transpose(out=pa[0:n, 0:128], in_=k[:, c0:c0 + n], identity=ident[:, :]), reads=[kb_, idb], writes=[pab])
                P.op(PE, lambda e, pa=pa, c0=c0, n=n: e.matmul(pa[0:n, 128:128 + n], lhsT=kbf[:, c0:c0 + n], rhs=qbf[:, c0:c0 + n], start=True, stop=True),
                     reads=[kbfb, qbfb], writes=[pab])
                P.op(PE, lambda e, pa=pa, c0=c0, n=n: e.transpose(out=pa[0:n, 192:228], in_=G8[0:36, c0:c0 + n], identity=ident[0:36, 0:36]),
                     reads=[G8b, idb], writes=[pab])
                kt, ktb = ktr.next(); sc, scb = scr.next(); gt, gtb = gtr.next()
                P.op(ACT, lambda e, kt=kt, pa=pa, n=n: e.activation(out=kt[0:n, :], in_=pa[0:n, 0:128], func=AF.Copy), reads=[pab], writes=[ktb])
                P.op(DVE, lambda e, sc=sc, pa=pa, n=n: e.tensor_tensor(out=sc[0:n, 0:n], in0=pa[0:n, 128:128 + n], in1=mask_i[0:n, 0:n], op=ALU.mult),
                     reads=[pab, mib], writes=[scb])
                P.op(ACT, lambda e, gt=gt, pa=pa, n=n: e.activation(out=gt[0:n, :], in_=pa[0:n, 192:228], func=AF.Copy), reads=[pab], writes=[gtb])
                vx, vxb = vxr.next()
                P.op(DVE, lambda e, vx=vx, vt=vt, gt=gt, n=n, h=h: e.tensor_scalar(out=vx[0:n, 0:256], in0=vt[0:n, 0:256], scalar1=gt[0:n, h:h + 1], scalar2=None,
                                                                                 op0=ALU.mult), reads=[vb, gtb], writes=[vxb])
                P.op(ACT, lambda e, vx=vx, gt=gt, n=n, h=h: e.activation(out=vx[0:n, 256:257], in_=gt[0:n, h:h + 1], func=AF.Copy), reads=[gtb, vxb], writes=[vxb])
                pn, pnb = psA.next()
                P.op(PE, lambda e, pn=pn, c0=c0, n=n: e.matmul(pn[0:n, 0:258], lhsT=qbf[:, c0:c0 + n], rhs=CX16[:, :], start=True, stop=False), reads=[qbfb, CX16b], writes=[pnb])
                P.op(PE, lambda e, pn=pn, sc=sc, vx=vx, n=n: e.matmul(pn[0:n, 0:258], lhsT=sc[0:n, 0:n], rhs=vx[0:n, :], start=False, stop=True),
                     reads=[scb, vxb], writes=[pnb])
                pc, pcb = psA.next()
                P.op(PE, lambda e, pc=pc, kt=kt, vx=vx, n=n: e.matmul(pc[:, 0:258], lhsT=kt[0:n, :], rhs=vx[0:n, :], start=True, stop=True),
                     reads=[ktb, vxb], writes=[pcb])
                P.op(DVE, lambda e, pc=pc: e.tensor_tensor(out=CX[:], in0=CX[:], in1=pc[:, 0:257], op=ALU.add), reads=[pcb, CXb], writes=[CXb])
                P.op(DVE, lambda e, ci=ci: e.tensor_scalar(out=CX[:], in0=CX[:], scalar1=FB[:, ci:ci + 1], scalar2=None, op0=ALU.mult),
                     reads=[CXb, FBb], writes=[CXb])
                P.op(ACT, lambda e: e.activation(out=CX16[:, 0:257], in_=CX[:], func=AF.Copy), reads=[CXb, CX16b], writes=[CX16b])
                if C.emit_out:
                    s_, sb2 = str_.next()
                    P.op(ACT, lambda e, s_=s_, pn=pn, n=n: e.activation(out=s_[0:n, 0:1], in_=pn[0:n, 256:257], func=AF.Abs),
                         reads=[pnb], writes=[sb2])
                    P.op(DVE, lambda e, s_=s_, gt=gt, n=n, h=h: e.tensor_tensor(out=s_[0:n, 0:1], in0=s_[0:n, 0:1], in1=gt[0:n, 32 + h:33 + h], op=ALU.max),
                         reads=[sb2, gtb], writes=[sb2])
                    P.op(DVE, lambda e, s_=s_, n=n: e.reciprocal(out=s_[0:n, 1:2], in_=s_[0:n, 0:1]), reads=[sb2], writes=[sb2])
                    P.op(ACT, lambda e, s_=s_, pn=pn, n=n: e.activation(out=junk[0:n, :], in_=pn[0:n, 0:256], func=AF.Square, scale=s_[0:n, 1:2],
                                                                       accum_out=s_[0:n, 2:3]), reads=[pnb, sb2], writes=[jb, sb2])
                    P.op(DVE, lambda e, s_=s_, n=n: e.tensor_scalar(out=s_[0:n, 3:4], in0=s_[0:n, 2:3], scalar1=1.0 / 256, scalar2=EPS, op0=ALU.mult,
                                                                   op1=ALU.add), reads=[sb2], writes=[sb2])
                    P.op(POOL, lambda e, s_=s_, n=n: e.tensor_tensor(out=s_[0:n, 4:5], in0=s_[0:n, 3:4], in1=mh[0:n, :], op=ALU.pow), reads=[sb2, mhb], writes=[sb2])
                    P.op(DVE, lambda e, s_=s_, n=n: e.tensor_tensor(out=s_[0:n, 5:6], in0=s_[0:n, 4:5], in1=s_[0:n, 1:2], op=ALU.mult), reads=[sb2], writes=[sb2])
                    t1, t1b = t1r.next(); yo, yob = yor.next()
                    P.op(DVE, lambda e, t1=t1, pn=pn, s_=s_, n=n, h=h: e.scalar_tensor_tensor(out=t1[0:n, :], in0=pn[0:n, 0:256], scalar=s_[0:n, 5:6],
                                                                                          in1=nbc[0:n, h * 256:(h + 1) * 256], op0=ALU.mult, op1=ALU.mult),
                         reads=[pnb, sb2, nbcb], writes=[t1b])
                    P.op(DVE, lambda e, yo=yo, t1=t1, vt=vt, n=n: e.tensor_tensor(out=yo[0:n, :], in0=t1[0:n, :], in1=vt[0:n, 256:512], op=ALU.mult),
                         reads=[t1b, vb], writes=[yob])
                    P.dma(POOL, y[t0:t0 + n, 1024 + h * 256:1024 + (h + 1) * 256], yo[0:n, :], reads=[yob], pwrites=[yb])
                ci += 1
                yield
        P.dma(POOL, prm["sC_out"][h], CX[:], reads=[CXb], pwrites=[K.sob])


def mixer_rwkv(C, st, pf, pfb, y, yb, prm, K):
    P = C.P
    ones, onesb, ident, idb, flagE, fb = K.ones, K.onesb, K.ident, K.idb, K.flagE, K.fb
    mask5, m5b = K.mask5, K.m5b
    muA = sbuf(C, st, "amuA", [64, 3, 8]); muL = sbuf(C, st, "amuL", [96, 3]); w2 = sbuf(C, st, "aw2", [32, 512]); a2 = sbuf(C, st, "aa2", [32, 512])
    g2 = sbuf(C, st, "ag2", [96, 512]); ch = sbuf(C, st, "ach", [64, 5, 8]); rk = sbuf(C, st, "ark", [64, 8, 2])
    lnw = sbuf(C, st, "alnw", [64, 512]); lnb = sbuf(C, st, "alnb", [64, 512])
    pb_ = Buf()
    for t, n_ in ((muA, "rw_muA"), (muL, "rw_muL"), (w2, "rw_w2"), (a2, "rw_a2"), (g2, "rw_g2"), (rk, "rw_rk"), (lnw, "rw_lnw_bc"), (lnb, "rw_lnb_bc")):
        P.dma(SP, t[:], prm[n_], pwrites=[pb_])
    P.dma(SP, ch[:, 0:4, :], prm["rw_ch"], pwrites=[pb_])
    P.op(DVE, lambda e: e.tensor_scalar(out=ch[:, 4, :], in0=ch[:, 3, :], scalar1=-1.0, scalar2=1.0, op0=ALU.mult, op1=ALU.add), reads=[pb_], writes=[pb_])
    mh = sbuf(C, st, "amh", [64, TP], F32); mhb = Buf()
    P.op(POOL, lambda e: e.memset(mh[:], -0.5), writes=[mhb])
    tmpr = Ring([sbuf(C, st, f"atmp{i}", [128, 4], F32) for i in range(2)])
    raw = sbuf(C, st, "araw", [96, TP]); rawb = Buf()
    thw = sbuf(C, st, "athw", [32, TP]); xal = sbuf(C, st, "axal", [32, TP]); sg = sbuf(C, st, "asg", [96, TP])
    thwb, xalb, sgb = Buf(), Buf(), Buf()
    for (dst, dstb, r0, nr, mcol, fn) in ((thw, thwb, 12 * 128, 32, 0, AF.Tanh), (xal, xalb, 12 * 128 + 32, 32, 1, None), (sg, sgb, 13 * 128, 96, 2, AF.Sigmoid)):
        P.dma(SP, raw[0:nr, :], pf[r0:r0 + nr, :], reads=[pfb], writes=[rawb])
        P.op(DVE, lambda e, nr=nr: e.memset(raw[0:nr, 0:3], 0.0), reads=[rawb], writes=[rawb])
        fix_gap(C, raw, rawb, prm["hist_in"][r0:r0 + nr, :], flagE, fb, tmpr, nr)
        P.op(DVE, lambda e, dst=dst, nr=nr: e.tensor_tensor(out=dst[0:nr, 3:TP], in0=raw[0:nr, 2:TP - 1], in1=raw[0:nr, 3:TP], op=ALU.subtract),
             reads=[rawb], writes=[dstb])
        P.op(DVE, lambda e, dst=dst, nr=nr, mcol=mcol: e.scalar_tensor_tensor(out=dst[0:nr, 3:TP], in0=dst[0:nr, 3:TP], scalar=muL[0:nr, mcol:mcol + 1],
                                                                            in1=raw[0:nr, 3:TP], op0=ALU.mult, op1=ALU.add), reads=[rawb, dstb, pb_], writes=[dstb])
        if fn is not None:
            P.op(ACT, lambda e, dst=dst, nr=nr, fn=fn: e.activation(out=dst[0:nr, 3:TP], in_=dst[0:nr, 3:TP], func=fn), reads=[dstb], writes=[dstb])
    A = [sbuf(C, st, f"aA{i}", [64, TP]) for i in range(10)]
    Ab = [Buf() for _ in range(10)]
    H = sbuf(C, st, "aH", [64, 64]); Hb = Buf()
    Hin = sbuf(C, st, "aHin", [64, 64]); Hinb = Buf()
    G = 8
    MM = sbuf(C, st, "aMM", [64, G, 5, 64]); MMb = Buf()
    TM = sbuf(C, st, "aTM", [64, G, 3, 64]); TMb = Buf()
    NN = [sbuf(C, st, f"aNN{i}", [64, G, 2, 64]) for i in range(2)]; NNb = [Buf(), Buf()]
    Pm = sbuf(C, st, "aPm", [64, G, 64]); Pmb = Buf()
    GB = sbuf(C, st, "aGB", [64, G, 66]); GBb = Buf()
    ps1 = Ring([psum(C, st, f"ap1{i}", [128, 512], F32) for i in range(3)])
    pygr = Ring([psum(C, st, f"apy{i}", [128, 512], F32) for i in range(2)])
    ps2 = Ring([psum(C, st, f"ap2{i}", [128, 512], F32) for i in range(3)])
    w0r = Ring([sbuf(C, st, f"aw0{i}", [64, 64], F32) for i in range(2)])
    ur = Ring([sbuf(C, st, f"au{i}", [64, 64], F32) for i in range(2)])
    str_ = Ring([sbuf(C, st, f"ast{i}", [64, 8], F32) for i in range(2)])
    junk = sbuf(C, st, "ajunk", [64, 64], F32); jb = Buf()
    T1 = sbuf(C, st, "aT1", [64, G, 64]); T1b = Buf()
    SQ = sbuf(C, st, "aSQ", [64, G, 64]); SQb = Buf()
    YO = sbuf(C, st, "aYO", [64, G, 64], BF16); YOb = Buf()
    ST = sbuf(C, st, "aST", [64, 6, G]); STb = Buf()
    for h in range(8):
        rows = [(sg_ * 4 + h // 2) * 128 + (h % 2) * 64 for sg_ in range(3)]
        for i in range(3):
            P.dma(SP, A[i][:], pf[rows[i]:rows[i] + 64, :], reads=[pfb], writes=[Ab[i]])
            P.op(DVE, lambda e, i=i: e.memset(A[i][:, 0:3], 0.0), reads=[Ab[i]], writes=[Ab[i]])
            fix_gap(C, A[i], Ab[i], prm["hist_in"][rows[i]:rows[i] + 64, :], flagE, fb, tmpr, 64)
            P.op(DVE, lambda e, i=i: e.tensor_tensor(out=A[3 + i][:, 3:TP], in0=A[i][:, 2:TP - 1], in1=A[i][:, 3:TP], op=ALU.subtract),
                 reads=[Ab[i]], writes=[Ab[3 + i]])
            P.op(DVE, lambda e, i=i, h=h: e.scalar_tensor_tensor(out=A[3 + i][:, 3:TP], in0=A[3 + i][:, 3:TP], scalar=muA[:, i, h:h + 1], in1=A[i][:, 3:TP],
                                                                op0=ALU.mult, op1=ALU.add), reads=[Ab[i], Ab[3 + i], pb_], writes=[Ab[3 + i]])
        xr, xk, xv = A[3], A[4], A[5]
        for c0 in range(3, TP, 512):
            n = min(512, TP - c0)
            p_, p_b = ps1.next()
            P.op(PE, lambda e, p_=p_, c0=c0, n=n, h=h: e.matmul(p_[0:64, 0:n], lhsT=w2[0:32, h * 64:(h + 1) * 64], rhs=thw[0:32, c0:c0 + n], start=True, stop=True),
                 reads=[pb_, thwb], writes=[p_b])
            P.op(ACT, lambda e, p_=p_, c0=c0, n=n, h=h: e.activation(out=A[0][:, c0:c0 + n], in_=p_[0:64, 0:n], func=AF.Sigmoid, bias=ch[:, 0, h:h + 1]),
                 reads=[p_b, pb_], writes=[Ab[0]])
            p_, p_b = ps1.next()
            P.op(PE, lambda e, p_=p_, c0=c0, n=n, h=h: e.matmul(p_[0:64, 0:n], lhsT=a2[0:32, h * 64:(h + 1) * 64], rhs=xal[0:32, c0:c0 + n], start=True, stop=True),
                 reads=[pb_, xalb], writes=[p_b])
            P.op(ACT, lambda e, p_=p_, c0=c0, n=n, h=h: e.activation(out=A[1][:, c0:c0 + n], in_=p_[0:64, 0:n], func=AF.Sigmoid, bias=ch[:, 1, h:h + 1]),
                 reads=[p_b, pb_], writes=[Ab[1]])
        P.op(DVE, lambda e, h=h: e.tensor_scalar(out=A[2][:, 3:TP], in0=xk[:, 3:TP], scalar1=ch[:, 2, h:h + 1], scalar2=None, op0=ALU.mult),
             reads=[Ab[4], pb_], writes=[Ab[2]])
        P.op(DVE, lambda e: e.tensor_tensor(out=A[6][:, 3:TP], in0=A[2][:, 3:TP], in1=A[2][:, 3:TP], op=ALU.mult), reads=[Ab[2]], writes=[Ab[6]])
        for c0 in range(3, TP, 512):
            n = min(512, TP - c0)
            p_, p_b = ps1.next()
            P.op(PE, lambda e, p_=p_, c0=c0, n=n: e.matmul(p_[0:64, 0:n], lhsT=ones[0:64, 0:64], rhs=A[6][:, c0:c0 + n], start=True, stop=True),
                 reads=[onesb, Ab[6]], writes=[p_b])
            P.op(DVE, lambda e, p_=p_, c0=c0, n=n: e.tensor_scalar(out=A[8][:, c0:c0 + n], in0=p_[0:64, 0:n], scalar1=1e-24, scalar2=None, op0=ALU.max),
                 reads=[p_b], writes=[Ab[8]])
        P.op(POOL, lambda e: e.tensor_tensor(out=A[8][:, 3:TP], in0=A[8][:, 3:TP], in1=mh[:, 3:TP], op=ALU.pow), reads=[Ab[8], mhb], writes=[Ab[8]])
        P.op(DVE, lambda e: e.tensor_tensor(out=A[2][:, 3:TP], in0=A[2][:, 3:TP], in1=A[8][:, 3:TP], op=ALU.mult), reads=[Ab[2], Ab[8]], writes=[Ab[2]])
        P.op(DVE, lambda e, h=h: e.tensor_scalar(out=A[6][:, 3:TP], in0=A[1][:, 3:TP], scalar1=ch[:, 3, h:h + 1], scalar2=ch[:, 4, h:h + 1], op0=ALU.mult,
                                                op1=ALU.add), reads=[Ab[1], pb_], writes=[Ab[6]])
        P.op(DVE, lambda e: e.tensor_tensor(out=xk[:, 3:TP], in0=xk[:, 3:TP], in1=A[6][:, 3:TP], op=ALU.mult), reads=[Ab[4], Ab[6]], writes=[Ab[4]])
        P.op(DVE, lambda e: e.tensor_tensor(out=A[1][:, 3:TP], in0=A[1][:, 3:TP], in1=A[2][:, 3:TP], op=ALU.mult), reads=[Ab[1], Ab[2]], writes=[Ab[1]])
        P.op(DVE, lambda e: e.tensor_tensor(out=A[6][:, 3:TP], in0=xr[:, 3:TP], in1=xk[:, 3:TP], op=ALU.mult), reads=[Ab[3], Ab[4]], writes=[Ab[6]])
        for (c0, ncol, t0) in SEGS:
            P.op(DVE, lambda e, c0=c0, ncol=ncol: e.tensor_tensor_scan(out=A[7][:, c0:c0 + ncol], data0=ones[0:64, c0:c0 + ncol], data1=A[0][:, c0:c0 + ncol],
                                                                       initial=0.0, op0=ALU.mult, op1=ALU.add), reads=[Ab[0], onesb], writes=[Ab[7]])
        P.op(DVE, lambda e: e.memset(A[8][:, 19:22], 0.0), reads=[Ab[8]], writes=[Ab[8]])
        chunk_rel(C, A[8], Ab[8], A[7], Ab[7], 64)
        P.op(ACT, lambda e: e.activation(out=A[7][:, 3:TP], in_=A[8][:, 3:TP], func=AF.Exp, scale=-LDK), reads=[Ab[8]], writes=[Ab[7]])
        P.op(ACT, lambda e: e.activation(out=A[9][:, 3:TP], in_=A[8][:, 3:TP], func=AF.Exp, scale=LDK), reads=[Ab[8]], writes=[Ab[9]])
        P.op(DVE, lambda e: e.tensor_tensor(out=A[8][:, 3:TP], in0=A[8][:, 3:TP], in1=A[0][:, 3:TP], op=ALU.subtract), reads=[Ab[8], Ab[0]], writes=[Ab[8]])
        P.op(ACT, lambda e: e.activation(out=A[8][:, 3:TP], in_=A[8][:, 3:TP], func=AF.Exp, scale=-LDK), reads=[Ab[8]], writes=[Ab[8]])
        P.op(DVE, lambda e: e.tensor_tensor(out=xr[:, 3:TP], in0=xr[:, 3:TP], in1=A[7][:, 3:TP], op=ALU.mult), reads=[Ab[3], Ab[7]], writes=[Ab[3]])
        P.op(DVE, lambda e: e.tensor_tensor(out=xk[:, 3:TP], in0=xk[:, 3:TP], in1=A[9][:, 3:TP], op=ALU.mult), reads=[Ab[4], Ab[9]], writes=[Ab[4]])
        P.op(DVE, lambda e: e.tensor_tensor(out=A[1][:, 3:TP], in0=A[1][:, 3:TP], in1=A[9][:, 3:TP], op=ALU.mult), reads=[Ab[1], Ab[9]], writes=[Ab[1]])
        P.op(DVE, lambda e: e.scalar_tensor_tensor(out=A[2][:, 3:TP], in0=A[2][:, 3:TP], scalar=-1.0, in1=A[8][:, 3:TP], op0=ALU.mult, op1=ALU.mult),
             reads=[Ab[2], Ab[8]], writes=[Ab[2]])
        rt, kt_, bt, at, prod, G1 = A[3], A[4], A[1], A[2], A[6], A[7]
        rtb, ktb_, btb, atb, prodb, G1b = Ab[3], Ab[4], Ab[1], Ab[2], Ab[6], Ab[7]
        xvb = Ab[5]
        P.op(DVE, lambda e: e.memset(H[:], 0.0), writes=[Hb])
        for si, seg in enumerate(SEGS):
            if si == 1:
                P.dma(SP, Hin[:], prm["sA_in"][h], writes=[Hinb])
                P.op(DVE, lambda e: e.scalar_tensor_tensor(out=H[:], in0=H[:], scalar=flagE[0:64, 0:1], in1=Hin[:], op0=ALU.mult, op1=ALU.add),
                     reads=[Hb, Hinb, fb], writes=[Hb])
            chs_all = chunks_of(seg)
            for g0 in range(0, len(chs_all), G):
                chs = chs_all[g0:g0 + G]
                ng = len(chs)
                n = chs[0][1]
                nlev = 5 if n == 64 else 3
                for g, (c0, n, t0) in enumerate(chs):
                    p_, p_b = ps1.next()
                    for j, (src, srcb) in enumerate(((xv, xvb), (kt_, ktb_), (bt, btb))):
                        P.op(PE, lambda e, p_=p_, src=src, c0=c0, n=n, j=j: e.transpose(out=p_[0:n, j * 64:(j + 1) * 64], in_=src[:, c0:c0 + n], identity=ident[0:64, 0:64]),
                             reads=[srcb, idb], writes=[p_b])
                    P.op(ACT, lambda e, p_=p_, g=g, n=n: e.activation(out=TM[0:n, g, :, :], in_=p_[0:n, 0:192].rearrange("p (j d) -> p j d", j=3), func=AF.Copy),
                         reads=[p_b], pwrites=[TMb])
                    q_, q_b = ps1.next()
                    pairs = ((kt_, ktb_, at, atb), (kt_, ktb_, rt, rtb), (bt, btb, at, atb), (bt, btb, rt, rtb), (at, atb, bt, btb))
                    for j, (l, lb, r, rb) in enumerate(pairs):
                        P.op(PE, lambda e, q_=q_, l=l, r=r, c0=c0, n=n, j=j: e.matmul(q_[0:n, j * 64:j * 64 + n], lhsT=l[:, c0:c0 + n], rhs=r[:, c0:c0 + n], start=True, stop=True),
                             reads=[lb, rb], writes=[q_b])
                    P.op(DVE, lambda e, q_=q_, g=g, n=n: e.tensor_tensor(out=MM[0:n, g, :, 0:n], in0=q_[0:n, 0:320].rearrange("p (j d) -> p j d", j=5)[:, :, 0:n],
                                                                       in1=mask5[0:n, :, 0:n], op=ALU.mult), reads=[q_b, m5b], pwrites=[MMb])
                P.op(DVE, lambda e, ng=ng, n=n: e.tensor_tensor(out=Pm[0:n, 0:ng, 0:n], in0=MM[0:n, 0:ng, 2, 0:n],
                                                                in1=ident[0:n, 0:n].unsqueeze(1).broadcast_to([n, ng, n]), op=ALU.add),
                     reads=[MMb, idb], writes=[Pmb])
                curN = lambda g, n=n: MM[0:n, g, 2, 0:n]
                curNT = lambda g, n=n: MM[0:n, g, 4, 0:n]
                curb = MMb
                for lev in range(nlev):
                    nn, nnb = NN[lev % 2], NNb[lev % 2]
                    for g4 in range(0, ng, 4):
                        m4 = min(4, ng - g4)
                        p2, p2b = ps2.next()
                        for g in range(g4, g4 + m4):
                            gg = g - g4
                            P.op(PE, lambda e, p2=p2, gg=gg, n=n, a_=curNT(g), b_=curN(g): e.matmul(p2[0:n, gg * 128:gg * 128 + n], lhsT=a_, rhs=b_, start=True, stop=True),
                                 reads=[curb], writes=[p2b])
                            P.op(PE, lambda e, p2=p2, gg=gg, n=n, a_=curN(g), b_=curNT(g): e.matmul(p2[0:n, gg * 128 + 64:gg * 128 + 64 + n], lhsT=a_, rhs=b_, start=True, stop=True),
                                 reads=[curb], writes=[p2b])
                        P.op(ACT, lambda e, p2=p2, nn=nn, g4=g4, m4=m4, n=n: e.activation(out=nn[0:n, g4:g4 + m4, :, 0:n],
                                                                                 in_=p2[0:n, 0:m4 * 128].rearrange("p (g j d) -> p g j d", g=m4, j=2)[:, :, :, 0:n], func=AF.Copy),
                             reads=[p2b], pwrites=[nnb])
                    curN = lambda g, nn=nn, n=n: nn[0:n, g, 0, 0:n]
                    curNT = lambda g, nn=nn, n=n: nn[0:n, g, 1, 0:n]
                    curb = nnb
                    p1, p1b = ps1.next()
                    for g in range(ng):
                        P.op(PE, lambda e, p1=p1, g=g, n=n, a_=curNT(g): e.matmul(p1[0:n, g * 64:g * 64 + n], lhsT=a_, rhs=Pm[0:n, g, 0:n], start=True, stop=True),
                             reads=[curb, Pmb], writes=[p1b])
                    P.op(DVE, lambda e, p1=p1, ng=ng, n=n: e.tensor_tensor(out=Pm[0:n, 0:ng, 0:n], in0=Pm[0:n, 0:ng, 0:n],
                                                                         in1=p1[0:n, 0:ng * 64].rearrange("p (g d) -> p g d", g=ng)[:, :, 0:n], op=ALU.add),
                         reads=[p1b, Pmb], writes=[Pmb])
                if C.emit_out:
                    for g, (c0, n, t0) in enumerate(chs):
                        p_, p_b = ps1.next()
                        P.op(PE, lambda e, p_=p_, c0=c0, n=n, h=h: e.matmul(p_[0:n, 0:64], lhsT=sg[0:96, c0:c0 + n], rhs=g2[0:96, h * 64:(h + 1) * 64], start=True, stop=True),
                             reads=[sgb, pb_], writes=[p_b])
                        P.op(PE, lambda e, p_=p_, c0=c0, n=n, h=h: e.matmul(p_[0:n, 64:66], lhsT=prod[:, c0:c0 + n], rhs=rk[:, h, :], start=True, stop=True),
                             reads=[prodb, pb_], writes=[p_b])
                        P.op(ACT, lambda e, p_=p_, g=g, n=n: e.activation(out=GB[0:n, g, :], in_=p_[0:n, 0:66], func=AF.Copy), reads=[p_b], pwrites=[GBb])
                pyg, pygb = pygr.next()
                for g, (c0, n, t0) in enumerate(chs):
                    vtm = TM[0:n, g, 0, :]; ktm = TM[0:n, g, 1, :]; btm = TM[0:n, g, 2, :]
                    LakT = MM[0:n, g, 0, 0:n]; MrkT = MM[0:n, g, 1, 0:n]; MrbT = MM[0:n, g, 3, 0:n]
                    TT_ = Pm[0:n, g, 0:n]
                    pw, pwb = ps1.next()
                    P.op(PE, lambda e, pw=pw, c0=c0, n=n: e.matmul(pw[0:n, 0:64], lhsT=at[:, c0:c0 + n], rhs=H[:, :], start=True, stop=False), reads=[atb, Hb], writes=[pwb])
                    P.op(PE, lambda e, pw=pw, n=n, LakT=LakT, vtm=vtm: e.matmul(pw[0:n, 0:64], lhsT=LakT, rhs=vtm, start=False, stop=True), reads=[MMb, TMb], writes=[pwb])
                    w0, w0b = w0r.next()
                    P.op(ACT, lambda e, w0=w0, pw=pw, n=n: e.activation(out=w0[0:n, :], in_=pw[0:n, 0:64], func=AF.Copy), reads=[pwb], writes=[w0b])
                    P.op(PE, lambda e, pw=pw, n=n, TT_=TT_, w0=w0: e.matmul(pw[0:n, 64:128], lhsT=TT_, rhs=w0[0:n, :], start=True, stop=True), reads=[Pmb, w0b], writes=[pwb])
                    u, ub_ = ur.next()
                    P.op(DVE, lambda e, u=u, pw=pw, n=n: e.tensor_copy(out=u[0:n, :], in_=pw[0:n, 64:128]), reads=[pwb], writes=[ub_])
                    if C.emit_out:
                        P.op(PE, lambda e, pyg=pyg, g=g, c0=c0, n=n: e.matmul(pyg[0:n, g * 64:(g + 1) * 64], lhsT=rt[:, c0:c0 + n], rhs=H[:, :], start=True, stop=False), reads=[rtb, Hb], writes=[pygb])
                        P.op(PE, lambda e, pyg=pyg, g=g, n=n, MrbT=MrbT, u=u: e.matmul(pyg[0:n, g * 64:(g + 1) * 64], lhsT=MrbT, rhs=u[0:n, :], start=False, stop=False), reads=[MMb, ub_], writes=[pygb])
                        P.op(PE, lambda e, pyg=pyg, g=g, n=n, MrkT=MrkT, vtm=vtm: e.matmul(pyg[0:n, g * 64:(g + 1) * 64], lhsT=MrkT, rhs=vtm, start=False, stop=True), reads=[MMb, TMb], writes=[pygb])
                    ph, phb = ps1.next()
                    P.op(PE, lambda e, ph=ph: e.matmul(ph[0:64, 0:64], lhsT=ident[0:64, 0:64], rhs=H[:, :], start=True, stop=False), reads=[idb, Hb], writes=[phb])
                    P.op(PE, lambda e, ph=ph, n=n, btm=btm, u=u: e.matmul(ph[0:64, 0:64], lhsT=btm, rhs=u[0:n, :], start=False, stop=False), reads=[TMb, ub_], writes=[phb])
                    P.op(PE, lambda e, ph=ph, n=n, ktm=ktm, vtm=vtm: e.matmul(ph[0:64, 0:64], lhsT=ktm, rhs=vtm, start=False, stop=True), reads=[TMb], writes=[phb])
                    ce = c0 + n - 1
                    P.op(DVE, lambda e, ph=ph, ce=ce: e.tensor_scalar(out=H[:], in0=ph[0:64, 0:64], scalar1=G1[:, ce:ce + 1], scalar2=None, op0=ALU.mult),
                         reads=[phb, G1b], writes=[Hb])
                if C.emit_out:
                    t0g = chs[0][2]
                    YG = pyg[0:n, 0:ng * 64].rearrange("p (g d) -> p g d", g=ng)
                    bc = lambda ap, n=n, ng=ng: ap.unsqueeze(2).broadcast_to([n, ng, 64])
                    P.op(DVE, lambda e, YG=YG, n=n, ng=ng: e.tensor_reduce(out=ST[0:n, 0, 0:ng], in_=YG, axis=AX.X, op=ALU.add), reads=[pygb], writes=[STb])
                    P.op(ACT, lambda e, YG=YG, n=n, ng=ng: e.activation(out=SQ[0:n, 0:ng, :], in_=YG, func=AF.Square), reads=[pygb], writes=[SQb])
                    P.op(DVE, lambda e, n=n, ng=ng: e.tensor_reduce(out=ST[0:n, 1, 0:ng], in_=SQ[0:n, 0:ng, :], axis=AX.X, op=ALU.add), reads=[SQb, STb], writes=[STb])
                    P.op(DVE, lambda e, n=n, ng=ng: e.tensor_scalar(out=ST[0:n, 2, 0:ng], in0=ST[0:n, 0, 0:ng], scalar1=1.0 / 64, scalar2=None, op0=ALU.mult), reads=[STb], writes=[STb])
                    P.op(DVE, lambda e, n=n, ng=ng: e.tensor_tensor(out=ST[0:n, 3, 0:ng], in0=ST[0:n, 2, 0:ng], in1=ST[0:n, 2, 0:ng], op=ALU.mult), reads=[STb], writes=[STb])
                    P.op(DVE, lambda e, n=n, ng=ng: e.tensor_scalar(out=ST[0:n, 4, 0:ng], in0=ST[0:n, 1, 0:ng], scalar1=1.0 / 64, scalar2=64e-5, op0=ALU.mult, op1=ALU.add),
                         reads=[STb], writes=[STb])
                    P.op(DVE, lambda e, n=n, ng=ng: e.tensor_tensor(out=ST[0:n, 4, 0:ng], in0=ST[0:n, 4, 0:ng], in1=ST[0:n, 3, 0:ng], op=ALU.subtract), reads=[STb], writes=[STb])
                    P.op(POOL, lambda e, n=n, ng=ng: e.tensor_tensor(out=ST[0:n, 5, 0:ng], in0=ST[0:n, 4, 0:ng], in1=mh[0:n, 0:ng], op=ALU.pow), reads=[STb, mhb], writes=[STb])
                    P.op(DVE, lambda e, YG=YG, n=n, ng=ng, bc=bc: e.tensor_tensor(out=T1[0:n, 0:ng, :], in0=YG, in1=bc(ST[0:n, 2, 0:ng]), op=ALU.subtract),
                         reads=[pygb, STb], writes=[T1b])
                    P.op(DVE, lambda e, n=n, ng=ng, bc=bc: e.tensor_tensor(out=T1[0:n, 0:ng, :], in0=T1[0:n, 0:ng, :], in1=bc(ST[0:n, 5, 0:ng]), op=ALU.mult),
                         reads=[T1b, STb], writes=[T1b])
                    P.op(DVE, lambda e, n=n, ng=ng, h=h: e.tensor_tensor(out=T1[0:n, 0:ng, :], in0=T1[0:n, 0:ng, :],
                                                                      in1=lnw[0:n, h * 64:(h + 1) * 64].unsqueeze(1).broadcast_to([n, ng, 64]), op=ALU.mult),
                         reads=[T1b, pb_], writes=[T1b])
                    P.op(DVE, lambda e, n=n, ng=ng, h=h: e.tensor_tensor(out=T1[0:n, 0:ng, :], in0=T1[0:n, 0:ng, :],
                                                                      in1=lnb[0:n, h * 64:(h + 1) * 64].unsqueeze(1).broadcast_to([n, ng, 64]), op=ALU.add),
                         reads=[T1b, pb_], writes=[T1b])
                    P.op(DVE, lambda e, n=n, ng=ng: e.tensor_tensor(out=SQ[0:n, 0:ng, :], in0=TM[0:n, 0:ng, 0, :], in1=GB[0:n, 0:ng, 64:65].broadcast_to([n, ng, 64]), op=ALU.mult),
                         reads=[TMb, GBb, SQb], writes=[SQb])
                    P.op(DVE, lambda e, n=n, ng=ng: e.tensor_tensor(out=T1[0:n, 0:ng, :], in0=T1[0:n, 0:ng, :], in1=SQ[0:n, 0:ng, :], op=ALU.add), reads=[T1b, SQb], writes=[T1b])
                    P.op(DVE, lambda e, n=n, ng=ng: e.tensor_tensor(out=YO[0:n, 0:ng, :], in0=T1[0:n, 0:ng, :], in1=GB[0:n, 0:ng, 0:64], op=ALU.mult),
                         reads=[T1b, GBb], writes=[YOb])
                    P.dma(SP, y[t0g:t0g + ng * n, h * 64:(h + 1) * 64].rearrange("(g p) d -> p g d", p=n), YO[0:n, 0:ng, :], reads=[YOb], pwrites=[yb])
        P.dma(POOL, prm["sA_out"][h], H[:], reads=[Hb], pwrites=[K.sob])


def mixer_rwkv2(C, st, pf, pfb, y, yb, prm, K):
    P = C.P
    ones, onesb, ident, idb, flagE, fb = K.ones, K.onesb, K.ident, K.idb, K.flagE, K.fb
    mask5, m5b = K.mask5, K.m5b
    muA = sbuf(C, st, "bmuA", [64, 3, 8]); muL = sbuf(C, st, "bmuL", [96, 3]); w2 = sbuf(C, st, "bw2", [32, 512]); a2 = sbuf(C, st, "ba2", [32, 512])
    g2 = sbuf(C, st, "bg2", [96, 512]); ch = sbuf(C, st, "bch", [64, 5, 8]); rk = sbuf(C, st, "brk", [64, 8, 2])
    lnw = sbuf(C, st, "blnw", [64, 512]); lnb = sbuf(C, st, "blnb", [64, 512])
    pb_ = Buf()
    for t, n_ in ((muA, "rw_muA"), (muL, "rw_muL"), (w2, "rw_w2"), (a2, "rw_a2"), (g2, "rw_g2"), (rk, "rw_rk"), (lnw, "rw_lnw_bc"), (lnb, "rw_lnb_bc")):
        P.dma(SP, t[:], prm[n_], pwrites=[pb_])
    P.dma(SP, ch[:, 0:4, :], prm["rw_ch"], pwrites=[pb_])
    P.op(DVE, lambda e: e.tensor_scalar(out=ch[:, 4, :], in0=ch[:, 3, :], scalar1=-1.0, scalar2=1.0, op0=ALU.mult, op1=ALU.add), reads=[pb_], writes=[pb_])
    WM = 128
    W1M = WM + 1
    mh = sbuf(C, st, "bmh", [64, 8 * WM], F32); mhb = Buf()
    P.op(POOL, lambda e: e.memset(mh[:], -0.5), writes=[mhb])
    names = ["pr", "pk", "pv", "xr", "xk", "xv", "sgz", "asig", "kkn", "t1", "rel", "G1", "G2"]
    X = {nm: sbuf(C, st, "bX" + nm, [64, 8, W1M]) for nm in names}
    Xb = {nm: Buf() for nm in names}
    Ssc = sbuf(C, st, "bSsc", [64, 1 + 8 * WM]); Sscb = Buf()
    P.op(DVE, lambda e: e.memset(Ssc[:, 0:1], 0.0), writes=[Sscb])
    lraw = sbuf(C, st, "blraw", [96, 3, W1M]); lrawb = Buf()
    thw = sbuf(C, st, "bthw", [32, WM]); xal = sbuf(C, st, "bxal", [32, WM]); sg = sbuf(C, st, "bsg", [96, WM])
    thwb, xalb, sgb = Buf(), Buf(), Buf()
    hs = sbuf(C, st, "bhs", [96, 2, 8]); hsb = Buf()
    H = sbuf(C, st, "bH", [64, 8, 64]); Hb = Buf()
    Hin = sbuf(C, st, "bHin", [64, 8, 64]); Hinb = Buf()
    NQ = 16
    TM = sbuf(C, st, "bTM", [64, 3, NQ, 64]); TMb = Buf()
    MM = sbuf(C, st, "bMM", [64, 5, NQ, 64]); MMb = Buf()
    NN = [sbuf(C, st, f"bNN{i}", [64, NQ, 2, 64]) for i in range(2)]; NNb = [Buf(), Buf()]
    Pm = sbuf(C, st, "bPm", [64, NQ, 64]); Pmb = Buf()
    W0s = sbuf(C, st, "bW0", [64, 8, 64]); W0b = Buf()
    Us = sbuf(C, st, "bUs", [64, 8, 64]); Usb = Buf()
    GBs = sbuf(C, st, "bGB", [64, 528]); GBb = Buf()
    T1 = sbuf(C, st, "bT1", [64, 8, 64]); T1b = Buf()
    SQ = sbuf(C, st, "bSQ", [64, 8, 64]); SQb = Buf()
    YO = sbuf(C, st, "bYO", [64, 8, 64], BF16); YOb = Buf()
    ST = sbuf(C, st, "bST", [64, 6, 8]); STb = Buf()
    bank = [psum(C, st, f"bpb{i}", [128, 512], F32) for i in range(8)]
    bkb = [Buf() for _ in range(8)]
    P.op(DVE, lambda e: e.memset(H[:], 0.0), writes=[Hb])

    def bc8(ap, W):
        return ap.unsqueeze(2).broadcast_to([64, 8, W])

    def vop(fn, reads, writes, pwrites=()):
        P.op(DVE, fn, reads=reads, writes=writes, pwrites=pwrites)

    Fl = sbuf(C, st, "bFl", [64, 8 * WM]); Flb = Buf()

    scs = [(3, 16, 0, 16)] + [(22 + 128 * i, 128, 16 + 128 * i, 64) for i in range(16)]
    def do_sc(sci, c0, W, t0, n):
        W1 = W + 1
        nch = W // n
        nq = 8 * nch
        cur = lambda nm: X[nm][:, :, 1:W1]
        prev = lambda nm: X[nm][:, :, 0:W]
        P.phase = "rwkv_pre"
        for i, nm in enumerate(("pr", "pk", "pv")):
            P.dma(SP, X[nm][:, :, 0:W1], pf[i * 512:(i + 1) * 512, c0 - 1:c0 + W].rearrange("(h d) c -> d h c", d=64), reads=[pfb], writes=[Xb[nm]])
        for j, (r0, nr) in enumerate(((12 * 128, 32), (12 * 128 + 32, 32), (13 * 128, 96))):
            P.dma(SP, lraw[0:nr, j, 0:W1], pf[r0:r0 + nr, c0 - 1:c0 + W], reads=[pfb], writes=[lrawb])
        if sci == 0:
            for nm in ("pr", "pk", "pv"):
                vop(lambda e, nm=nm: e.memset(X[nm][:, :, 0:1], 0.0), [Xb[nm]], [Xb[nm]])
            vop(lambda e: e.memset(lraw[:, :, 0:1], 0.0), [lrawb], [lrawb])
        if sci == 1:
            for i, nm in enumerate(("pr", "pk", "pv")):
                P.dma(SP, hs[0:64, 0, :], pf[i * 512:(i + 1) * 512, 18:19].rearrange("(h d) c -> d (h c)", d=64), reads=[pfb], writes=[hsb], allow_slow_non_contiguous=True)
                P.dma(SP, hs[0:64, 1, :], prm["hist_in"][i * 512:(i + 1) * 512, 2:3].rearrange("(h d) c -> d (h c)", d=64), reads=[hsb], writes=[hsb], allow_slow_non_contiguous=True)
                vop(lambda e, nm=nm: e.scalar_tensor_tensor(out=X[nm][:, :, 0:1], in0=hs[0:64, 0, :].unsqueeze(2), scalar=flagE[0:64, 0:1], in1=hs[0:64, 1, :].unsqueeze(2),
                                                            op0=ALU.mult, op1=ALU.add), [hsb, fb, Xb[nm]], [Xb[nm]])
            for j, (r0, nr) in enumerate(((12 * 128, 32), (12 * 128 + 32, 32), (13 * 128, 96))):
                P.dma(SP, hs[0:nr, 0, 0:1], pf[r0:r0 + nr, 18:19], reads=[pfb, hsb], writes=[hsb], allow_slow_non_contiguous=True)
                P.dma(SP, hs[0:nr, 1, 0:1], prm["hist_in"][r0:r0 + nr, 2:3], reads=[hsb], writes=[hsb], allow_slow_non_contiguous=True)
                vop(lambda e, j=j, nr=nr: e.scalar_tensor_tensor(out=lraw[0:nr, j, 0:1], in0=hs[0:nr, 0, 0:1], scalar=flagE[0:nr, 0:1], in1=hs[0:nr, 1, 0:1],
                                                                 op0=ALU.mult, op1=ALU.add), [hsb, fb, lrawb], [lrawb])
            P.dma(SP, Hin[:], prm["sA_in"].rearrange("h k v -> k h v"), writes=[Hinb])
            vop(lambda e: e.scalar_tensor_tensor(out=H[:], in0=H[:], scalar=flagE[0:64, 0:1], in1=Hin[:], op0=ALU.mult, op1=ALU.add), [Hb, Hinb, fb], [Hb])
        for i, (src, dst) in enumerate((("pr", "xr"), ("pk", "xk"), ("pv", "xv"))):
            vop(lambda e, src=src, dst=dst: e.tensor_tensor(out=cur(dst), in0=prev(src), in1=cur(src), op=ALU.subtract), [Xb[src]], [Xb[dst]])
            vop(lambda e, dst=dst, i=i: e.tensor_tensor(out=cur(dst), in0=cur(dst), in1=bc8(muA[:, i, :], W), op=ALU.mult), [Xb[dst], pb_], [Xb[dst]])
            vop(lambda e, src=src, dst=dst: e.tensor_tensor(out=cur(dst), in0=cur(dst), in1=cur(src), op=ALU.add), [Xb[dst], Xb[src]], [Xb[dst]])
        for j, (dst, dstb, nr, fn) in enumerate(((thw, thwb, 32, AF.Tanh), (xal, xalb, 32, None), (sg, sgb, 96, AF.Sigmoid))):
            vop(lambda e, dst=dst, nr=nr, j=j: e.tensor_tensor(out=dst[0:nr, 0:W], in0=lraw[0:nr, j, 0:W], in1=lraw[0:nr, j, 1:W1], op=ALU.subtract), [lrawb], [dstb])
            vop(lambda e, dst=dst, nr=nr, j=j: e.scalar_tensor_tensor(out=dst[0:nr, 0:W], in0=dst[0:nr, 0:W], scalar=muL[0:nr, j:j + 1], in1=lraw[0:nr, j, 1:W1],
                                                                    op0=ALU.mult, op1=ALU.add), [lrawb, dstb, pb_], [dstb])
            if fn is not None:
                P.op(ACT, lambda e, dst=dst, nr=nr, fn=fn: e.activation(out=dst[0:nr, 0:W], in_=dst[0:nr, 0:W], func=fn), reads=[dstb], writes=[dstb])
        for (wt_, src, srcb, dst, chi, b0) in ((w2, thw, thwb, "sgz", 0, 0), (a2, xal, xalb, "asig", 1, 2)):
            for h in range(8):
                bk = b0 + (h * W) // 512
                off = (h * W) % 512
                P.op(PE, lambda e, bk=bk, off=off, wt_=wt_, src=src, h=h: e.matmul(bank[bk][0:64, off:off + W], lhsT=wt_[0:32, h * 64:(h + 1) * 64], rhs=src[0:32, 0:W],
                                                                                  start=True, stop=True), reads=[pb_, srcb], writes=[bkb[bk]])
            nb = (8 * W + 511) // 512
            for b in range(nb):
                h0 = b * (512 // W) if W >= 64 else 0
                nh = (512 // W) if W >= 64 else 8
                vop(lambda e, b=b, b0=b0, dst=dst, chi=chi, h0=h0, nh=nh: e.tensor_tensor(
                    out=X[dst][:, h0:h0 + nh, 1:W1], in0=bank[b0 + b][0:64, 0:nh * W].rearrange("p (h w) -> p h w", h=nh),
                    in1=ch[:, chi, h0:h0 + nh].unsqueeze(2).broadcast_to([64, nh, W]), op=ALU.add), [bkb[b0 + b], pb_], [Xb[dst]])
            P.op(ACT, lambda e, dst=dst: e.activation(out=cur(dst), in_=cur(dst), func=AF.Sigmoid), reads=[Xb[dst]], writes=[Xb[dst]])
        vop(lambda e: e.tensor_tensor(out=cur("kkn"), in0=cur("xk"), in1=bc8(ch[:, 2, :], W), op=ALU.mult), [Xb["xk"], pb_], [Xb["kkn"]])
        vop(lambda e: e.tensor_tensor(out=Fl[:, 0:8 * W].rearrange("p (h w) -> p h w", h=8), in0=cur("kkn"), in1=cur("kkn"), op=ALU.mult), [Xb["kkn"]], [Flb])
        nb = (8 * W + 511) // 512
        for b in range(nb):
            nn_ = min(512, 8 * W - b * 512)
            P.op(PE, lambda e, b=b, nn_=nn_: e.matmul(bank[4 + b][0:64, 0:nn_], lhsT=ones[0:64, 0:64], rhs=Fl[:, b * 512:b * 512 + nn_], start=True, stop=True),
                 reads=[onesb, Flb], writes=[bkb[4 + b]])
        for b in range(nb):
            nn_ = min(512, 8 * W - b * 512)
            vop(lambda e, b=b, nn_=nn_: e.tensor_scalar(out=Fl[:, b * 512:b * 512 + nn_], in0=bank[4 + b][0:64, 0:nn_],
                                                        scalar1=1e-24, scalar2=None, op0=ALU.max), [bkb[4 + b], Flb], [Flb])
        relf = Fl[:, 0:8 * W]
        P.op(POOL, lambda e, relf=relf: e.tensor_tensor(out=relf, in0=relf, in1=mh[:, 0:8 * W], op=ALU.pow), reads=[Flb, mhb], writes=[Flb])
        vop(lambda e, relf=relf: e.tensor_tensor(out=cur("kkn"), in0=cur("kkn"), in1=relf.rearrange("p (h w) -> p h w", h=8), op=ALU.mult),
            [Xb["kkn"], Flb], [Xb["kkn"]])
        vop(lambda e: e.tensor_tensor(out=cur("t1"), in0=cur("asig"), in1=bc8(ch[:, 3, :], W), op=ALU.mult), [Xb["asig"], pb_], [Xb["t1"]])
        vop(lambda e: e.tensor_tensor(out=cur("t1"), in0=cur("t1"), in1=bc8(ch[:, 4, :], W), op=ALU.add), [Xb["t1"], pb_], [Xb["t1"]])
        vop(lambda e: e.tensor_tensor(out=cur("xk"), in0=cur("xk"), in1=cur("t1"), op=ALU.mult), [Xb["xk"], Xb["t1"]], [Xb["xk"]])
        vop(lambda e: e.tensor_tensor(out=cur("asig"), in0=cur("asig"), in1=cur("kkn"), op=ALU.mult), [Xb["asig"], Xb["kkn"]], [Xb["asig"]])
        vop(lambda e: e.tensor_tensor(out=cur("t1"), in0=cur("xr"), in1=cur("xk"), op=ALU.mult), [Xb["xr"], Xb["xk"], Xb["t1"]], [Xb["t1"]])
        vop(lambda e: e.tensor_tensor(out=cur("pr"), in0=cur("t1"), in1=bc8(rk[:, :, 0], W), op=ALU.mult), [Xb["t1"], pb_, Xb["pr"], Xb["xr"]], [Xb["pr"]])
        vop(lambda e: e.tensor_copy(out=Fl[:, 0:8 * W].rearrange("p (h w) -> p h w", h=8), in_=cur("sgz")), [Xb["sgz"], Flb], [Flb])
        vop(lambda e: e.tensor_tensor_scan(out=Ssc[:, 1:1 + 8 * W], data0=ones[0:64, 0:8 * W], data1=Fl[:, 0:8 * W], initial=0.0, op0=ALU.mult, op1=ALU.add),
            [Flb, Sscb, onesb], [Sscb])
        vop(lambda e: e.tensor_tensor(out=cur("rel").rearrange("p h (c j) -> p h c j", j=n),
                                      in0=Ssc[:, 1:1 + 8 * W].rearrange("p (h c j) -> p h c j", h=8, j=n),
                                      in1=Ssc[:, 0:8 * W].rearrange("p (h c j) -> p h c j", h=8, j=n)[:, :, :, 0:1].broadcast_to([64, 8, nch, n]), op=ALU.subtract),
            [Sscb, Xb["rel"]], [Xb["rel"]])
        P.op(ACT, lambda e: e.activation(out=cur("G1"), in_=cur("rel"), func=AF.Exp, scale=-LDK), reads=[Xb["rel"]], writes=[Xb["G1"]])
        P.op(ACT, lambda e: e.activation(out=cur("G2"), in_=cur("rel"), func=AF.Exp, scale=LDK), reads=[Xb["rel"]], writes=[Xb["G2"]])
        vop(lambda e: e.tensor_tensor(out=cur("rel"), in0=cur("rel"), in1=cur("sgz"), op=ALU.subtract), [Xb["rel"], Xb["sgz"]], [Xb["rel"]])
        P.op(ACT, lambda e: e.activation(out=cur("rel"), in_=cur("rel"), func=AF.Exp, scale=-LDK), reads=[Xb["rel"]], writes=[Xb["rel"]])
        vop(lambda e: e.tensor_tensor(out=cur("xr"), in0=cur("xr"), in1=cur("G1"), op=ALU.mult), [Xb["xr"], Xb["G1"]], [Xb["xr"]])
        vop(lambda e: e.tensor_tensor(out=cur("xk"), in0=cur("xk"), in1=cur("G2"), op=ALU.mult), [Xb["xk"], Xb["G2"]], [Xb["xk"]])
        vop(lambda e: e.tensor_tensor(out=cur("asig"), in0=cur("asig"), in1=cur("G2"), op=ALU.mult), [Xb["asig"], Xb["G2"]], [Xb["asig"]])
        vop(lambda e: e.scalar_tensor_tensor(out=cur("kkn"), in0=cur("kkn"), scalar=-1.0, in1=cur("rel"), op0=ALU.mult, op1=ALU.mult),
            [Xb["kkn"], Xb["rel"]], [Xb["kkn"]])
        RT, KT, BT, AT, XV, PRK, G1 = "xr", "xk", "asig", "kkn", "xv", "pr", "G1"
        col = lambda nm, h, c: X[nm][:, h, 1 + c * n:1 + (c + 1) * n]
        qi = lambda h, c: h * nch + c
        P.phase = "rwkv_gram"
        for a, nm in enumerate((XV, KT, BT)):
            for h in range(8):
                for c in range(nch):
                    q = qi(h, c)
                    bk, off = (q * 64) // 512, (q * 64) % 512
                    P.op(PE, lambda e, bk=bk, off=off, nm=nm, h=h, c=c: e.transpose(out=bank[bk][0:n, off:off + 64], in_=col(nm, h, c), identity=ident[0:64, 0:64]),
                         reads=[Xb[nm], idb], writes=[bkb[bk]])
            for b in range((nq * 64 + 511) // 512):
                qn = min(8, nq - b * 8)
                P.op(ACT, lambda e, a=a, b=b, qn=qn: e.activation(out=TM[0:n, a, b * 8:b * 8 + qn, :], in_=bank[b][0:n, 0:qn * 64].rearrange("p (q d) -> p q d", q=qn),
                                                               func=AF.Copy), reads=[bkb[b]], pwrites=[TMb])
        pairs = ((KT, AT), (KT, RT), (BT, AT), (BT, RT), (AT, BT))
        for j, (l_, r_) in enumerate(pairs):
            b0 = 4 if j % 2 else 0
            for h in range(8):
                for c in range(nch):
                    q = qi(h, c)
                    bk, off = b0 + (q * 64) // 512, (q * 64) % 512
                    P.op(PE, lambda e, bk=bk, off=off, l_=l_, r_=r_, h=h, c=c: e.matmul(bank[bk][0:n, off:off + n], lhsT=col(l_, h, c), rhs=col(r_, h, c), start=True, stop=True),
                         reads=[Xb[l_], Xb[r_]], writes=[bkb[bk]])
            for b in range((nq * 64 + 511) // 512):
                qn = min(8, nq - b * 8)
                vop(lambda e, j=j, b=b, b0=b0, qn=qn: e.tensor_tensor(out=MM[0:n, j, b * 8:b * 8 + qn, 0:n],
                                                                    in0=bank[b0 + b][0:n, 0:qn * 64].rearrange("p (q d) -> p q d", q=qn)[:, :, 0:n],
                                                                    in1=mask5[0:n, j, 0:n].unsqueeze(1).broadcast_to([n, qn, n]), op=ALU.mult),
                    [bkb[b0 + b], m5b], [], pwrites=[MMb])
        P.phase = "rwkv_inv"
        vop(lambda e: e.tensor_tensor(out=Pm[0:n, 0:nq, 0:n], in0=MM[0:n, 2, 0:nq, 0:n], in1=ident[0:n, 0:n].unsqueeze(1).broadcast_to([n, nq, n]), op=ALU.add),
            [MMb, idb], [Pmb])
        curN = lambda q: MM[0:n, 2, q, 0:n]
        curNT = lambda q: MM[0:n, 4, q, 0:n]
        curb = MMb
        nlev = 5 if n == 64 else 3
        for lev in range(nlev):
            nn, nnb = NN[lev % 2], NNb[lev % 2]
            for q in range(nq):
                bk, off = (q * 128) // 512, (q * 128) % 512
                P.op(PE, lambda e, bk=bk, off=off, a_=curNT(q), b_=curN(q): e.matmul(bank[bk][0:n, off:off + n], lhsT=a_, rhs=b_, start=True, stop=True), reads=[curb], writes=[bkb[bk]])
                P.op(PE, lambda e, bk=bk, off=off, a_=curN(q), b_=curNT(q): e.matmul(bank[bk][0:n, off + 64:off + 64 + n], lhsT=a_, rhs=b_, start=True, stop=True), reads=[curb], writes=[bkb[bk]])
            for b in range((nq * 128 + 511) // 512):
                qn = min(4, nq - b * 4)
                P.op(ACT, lambda e, nn=nn, b=b, qn=qn: e.activation(out=nn[0:n, b * 4:b * 4 + qn, :, 0:n],
                                                                 in_=bank[b][0:n, 0:qn * 128].rearrange("p (q j d) -> p q j d", q=qn, j=2)[:, :, :, 0:n], func=AF.Copy),
                     reads=[bkb[b]], pwrites=[nnb])
            curN = lambda q, nn=nn: nn[0:n, q, 0, 0:n]
            curNT = lambda q, nn=nn: nn[0:n, q, 1, 0:n]
            curb = nnb
            for q in range(nq):
                bk, off = 4 + (q * 64) // 512, (q * 64) % 512
                P.op(PE, lambda e, bk=bk, off=off, a_=curNT(q), q=q: e.matmul(bank[bk][0:n, off:off + n], lhsT=a_, rhs=Pm[0:n, q, 0:n], start=True, stop=True), reads=[curb, Pmb], writes=[bkb[bk]])
            for b in range((nq * 64 + 511) // 512):
                qn = min(8, nq - b * 8)
                vop(lambda e, b=b, qn=qn: e.tensor_tensor(out=Pm[0:n, b * 8:b * 8 + qn, 0:n], in0=Pm[0:n, b * 8:b * 8 + qn, 0:n],
                                                          in1=bank[4 + b][0:n, 0:qn * 64].rearrange("p (q d) -> p q d", q=qn)[:, :, 0:n], op=ALU.add),
                    [bkb[4 + b], Pmb], [Pmb])
        P.phase = "rwkv_chain"
        for c in range(nch):
            tc0 = t0 + c * n
            for h in range(8):
                q = qi(h, c)
                P.op(PE, lambda e, h=h, c=c: e.matmul(bank[0][0:n, h * 64:(h + 1) * 64], lhsT=col(AT, h, c), rhs=H[:, h, :], start=True, stop=False), reads=[Xb[AT], Hb], writes=[bkb[0]])
                P.op(PE, lambda e, h=h, q=q: e.matmul(bank[0][0:n, h * 64:(h + 1) * 64], lhsT=MM[0:n, 0, q, 0:n], rhs=TM[0:n, 0, q, :], start=False, stop=True), reads=[MMb, TMb], writes=[bkb[0]])
            P.op(ACT, lambda e: e.activation(out=W0s[0:n, :, :], in_=bank[0][0:n, 0:512].rearrange("p (h d) -> p h d", h=8), func=AF.Copy), reads=[bkb[0]], writes=[W0b])
            for h in range(8):
                q = qi(h, c)
                P.op(PE, lambda e, h=h, q=q: e.matmul(bank[1][0:n, h * 64:(h + 1) * 64], lhsT=Pm[0:n, q, 0:n], rhs=W0s[0:n, h, :], start=True, stop=True), reads=[Pmb, W0b], writes=[bkb[1]])
            vop(lambda e: e.tensor_copy(out=Us[0:n, :, :], in_=bank[1][0:n, 0:512].rearrange("p (h d) -> p h d", h=8)), [bkb[1]], [Usb])
            if C.emit_out:
                for h in range(8):
                    q = qi(h, c)
                    P.op(PE, lambda e, h=h, c=c: e.matmul(bank[2][0:n, h * 64:(h + 1) * 64], lhsT=col(RT, h, c), rhs=H[:, h, :], start=True, stop=False), reads=[Xb[RT], Hb], writes=[bkb[2]])
                    P.op(PE, lambda e, h=h, q=q: e.matmul(bank[2][0:n, h * 64:(h + 1) * 64], lhsT=MM[0:n, 3, q, 0:n], rhs=Us[0:n, h, :], start=False, stop=False), reads=[MMb, Usb], writes=[bkb[2]])
                    P.op(PE, lambda e, h=h, q=q: e.matmul(bank[2][0:n, h * 64:(h + 1) * 64], lhsT=MM[0:n, 1, q, 0:n], rhs=TM[0:n, 0, q, :], start=False, stop=True), reads=[MMb, TMb], writes=[bkb[2]])
            for h in range(8):
                q = qi(h, c)
                P.op(PE, lambda e, h=h: e.matmul(bank[3][0:64, h * 64:(h + 1) * 64], lhsT=ident[0:64, 0:64], rhs=H[:, h, :], start=True, stop=False), reads=[idb, Hb], writes=[bkb[3]])
                P.op(PE, lambda e, h=h, q=q: e.matmul(bank[3][0:64, h * 64:(h + 1) * 64], lhsT=TM[0:n, 2, q, :], rhs=Us[0:n, h, :], start=False, stop=False), reads=[TMb, Usb], writes=[bkb[3]])
                P.op(PE, lambda e, h=h, q=q: e.matmul(bank[3][0:64, h * 64:(h + 1) * 64], lhsT=TM[0:n, 1, q, :], rhs=TM[0:n, 0, q, :], start=False, stop=True), reads=[TMb], writes=[bkb[3]])
            ce = 1 + (c + 1) * n - 1
            vop(lambda e, ce=ce: e.tensor_tensor(out=H[:], in0=bank[3][0:64, 0:512].rearrange("p (h d) -> p h d", h=8),
                                                 in1=X[G1][:, :, ce:ce + 1].broadcast_to([64, 8, 64]), op=ALU.mult), [bkb[3], Xb[G1]], [Hb])
            if C.emit_out:
                P.op(PE, lambda e, c=c: e.matmul(bank[4][0:n, 0:512], lhsT=sg[0:96, c * n:(c + 1) * n], rhs=g2[0:96, :], start=True, stop=True), reads=[sgb, pb_], writes=[bkb[4]])
                for h in range(8):
                    P.op(PE, lambda e, h=h, c=c: e.matmul(bank[5][0:n, 2 * h:2 * h + 2], lhsT=col(PRK, h, c), rhs=ones[0:64, 0:2], start=True, stop=True), reads=[Xb[PRK], onesb], writes=[bkb[5]])
                P.op(ACT, lambda e: e.activation(out=GBs[0:n, 0:512], in_=bank[4][0:n, 0:512], func=AF.Copy), reads=[bkb[4]], writes=[GBb])
                P.op(ACT, lambda e: e.activation(out=GBs[0:n, 512:528], in_=bank[5][0:n, 0:16], func=AF.Copy), reads=[bkb[5], GBb], writes=[GBb])
                YG = bank[2][0:n, 0:512].rearrange("p (h d) -> p h d", h=8)
                bcn = lambda ap: ap.unsqueeze(2).broadcast_to([n, 8, 64])
                vop(lambda e, YG=YG: e.tensor_reduce(out=ST[0:n, 0, :], in_=YG, axis=AX.X, op=ALU.add), [bkb[2]], [STb])
                P.op(ACT, lambda e, YG=YG: e.activation(out=SQ[0:n, :, :], in_=YG, func=AF.Square), reads=[bkb[2]], writes=[SQb])
                vop(lambda e: e.tensor_reduce(out=ST[0:n, 1, :], in_=SQ[0:n, :, :], axis=AX.X, op=ALU.add), [SQb, STb], [STb])
                vop(lambda e: e.tensor_scalar(out=ST[0:n, 2, :], in0=ST[0:n, 0, :], scalar1=1.0 / 64, scalar2=None, op0=ALU.mult), [STb], [STb])
                vop(lambda e: e.tensor_tensor(out=ST[0:n, 3, :], in0=ST[0:n, 2, :], in1=ST[0:n, 2, :], op=ALU.mult), [STb], [STb])
                vop(lambda e: e.tensor_scalar(out=ST[0:n, 4, :], in0=ST[0:n, 1, :], scalar1=1.0 / 64, scalar2=64e-5, op0=ALU.mult, op1=ALU.add), [STb], [STb])
                vop(lambda e: e.tensor_tensor(out=ST[0:n, 4, :], in0=ST[0:n, 4, :], in1=ST[0:n, 3, :], op=ALU.subtract), [STb], [STb])
                P.op(POOL, lambda e: e.tensor_tensor(out=ST[0:n, 5, :], in0=ST[0:n, 4, :], in1=mh[0:n, 0:8], op=ALU.pow), reads=[STb, mhb], writes=[STb])
                vop(lambda e, YG=YG, bcn=bcn: e.tensor_tensor(out=T1[0:n, :, :], in0=YG, in1=bcn(ST[0:n, 2, :]), op=ALU.subtract), [bkb[2], STb], [T1b])
                vop(lambda e, bcn=bcn: e.tensor_tensor(out=T1[0:n, :, :], in0=T1[0:n, :, :], in1=bcn(ST[0:n, 5, :]), op=ALU.mult), [T1b, STb], [T1b])
                vop(lambda e: e.tensor_tensor(out=T1[0:n, :, :], in0=T1[0:n, :, :], in1=lnw[0:n, :].rearrange("p (h d) -> p h d", h=8), op=ALU.mult), [T1b, pb_], [T1b])
                vop(lambda e: e.tensor_tensor(out=T1[0:n, :, :], in0=T1[0:n, :, :], in1=lnb[0:n, :].rearrange("p (h d) -> p h d", h=8), op=ALU.add), [T1b, pb_], [T1b])
                vtm_c = TM[0:n, 0, 0:nq, :].rearrange("p (h c) d -> p h c d", c=nch)[:, :, c, :]
                bs_c = GBs[0:n, 512:528].rearrange("p (h t) -> p h t", t=2)[:, :, 0:1].broadcast_to([n, 8, 64])
                vop(lambda e, vtm_c=vtm_c, bs_c=bs_c: e.tensor_tensor(out=SQ[0:n, :, :], in0=vtm_c, in1=bs_c, op=ALU.mult), [TMb, GBb, SQb], [SQb])
                vop(lambda e: e.tensor_tensor(out=T1[0:n, :, :], in0=T1[0:n, :, :], in1=SQ[0:n, :, :], op=ALU.add), [T1b, SQb], [T1b])
                vop(lambda e: e.tensor_tensor(out=YO[0:n, :, :], in0=T1[0:n, :, :], in1=GBs[0:n, 0:512].rearrange("p (h d) -> p h d", h=8), op=ALU.mult), [T1b, GBb], [YOb])
                P.dma(SP, y[tc0:tc0 + n, 0:512], YO[0:n, :, :].rearrange("p h d -> p (h d)"), reads=[YOb], pwrites=[yb])
    for sci, (c0, W, t0, n) in enumerate(scs):
        do_sc(sci, c0, W, t0, n)
    P.dma(POOL, prm["sA_out"].rearrange("h k v -> k h v"), H[:], reads=[Hb], pwrites=[K.sob])


def mixer_rwkv3(C, st, pf, pfb, y, yb, prm, K):
    P = C.P
    ones, onesb, ident, idb, flagE, fb = K.ones, K.onesb, K.ident, K.idb, K.flagE, K.fb
    mask5, m5b = K.mask5, K.m5b
    muA = sbuf(C, st, "cmuA", [64, 3, 8]); muL = sbuf(C, st, "cmuL", [96, 3]); w2 = sbuf(C, st, "cw2", [32, 512]); a2 = sbuf(C, st, "ca2", [32, 512])
    g2 = sbuf(C, st, "cg2", [96, 512]); ch = sbuf(C, st, "cch", [64, 5, 8]); rk = sbuf(C, st, "crk", [64, 8, 2])
    lnw = sbuf(C, st, "clnw", [64, 512]); lnb = sbuf(C, st, "clnb", [64, 512])
    pb_ = Buf()
    for t, n_ in ((muA, "rw_muA"), (muL, "rw_muL"), (w2, "rw_w2"), (a2, "rw_a2"), (g2, "rw_g2"), (rk, "rw_rk"), (lnw, "rw_lnw_bc"), (lnb, "rw_lnb_bc")):
        P.dma(SP, t[:], prm[n_], pwrites=[pb_])
    P.dma(SP, ch[:, 0:4, :], prm["rw_ch"], pwrites=[pb_])
    P.op(DVE, lambda e: e.tensor_scalar(out=ch[:, 4, :], in0=ch[:, 3, :], scalar1=-1.0, scalar2=1.0, op0=ALU.mult, op1=ALU.add), reads=[pb_], writes=[pb_])
    WM = 128
    W1M = WM + 1
    mh = sbuf(C, st, "cmh", [64, 8 * WM], F32); mhb = Buf()
    P.op(POOL, lambda e: e.memset(mh[:], -0.5), writes=[mhb])
    names = ["pr", "pk", "pv", "xr", "xk", "xv", "sgz", "asig", "kkn", "t1", "rel", "G1", "G2"]
    X = {nm: sbuf(C, st, "cX" + nm, [64, 8, W1M]) for nm in names}
    Xb = {nm: Buf() for nm in names}
    Ssc = sbuf(C, st, "cSsc", [64, 1 + 8 * WM]); Sscb = Buf()
    P.op(DVE, lambda e: e.memset(Ssc[:, 0:1], 0.0), writes=[Sscb])
    lraw = sbuf(C, st, "clraw", [96, 3, W1M]); lrawb = Buf()
    thw = sbuf(C, st, "cthw", [32, WM]); xal = sbuf(C, st, "cxal", [32, WM]); sg = sbuf(C, st, "csg", [96, WM])
    thwb, xalb, sgb = Buf(), Buf(), Buf()
    hs = sbuf(C, st, "chs", [96, 2, 8]); hsb = Buf()
    H = sbuf(C, st, "cH", [64, 8, 64]); Hb = Buf()
    Hin = sbuf(C, st, "cHin", [64, 8, 64]); Hinb = Buf()
    NQ = 16
    TM = sbuf(C, st, "cTM", [64, 3, NQ, 64], BF16); TMb = Buf()
    MM = sbuf(C, st, "cMM", [64, 5, NQ, 64], BF16); MMb = Buf()
    NN = [sbuf(C, st, f"cNN{i}", [64, NQ, 2, 64], BF16) for i in range(2)]; NNb = [Buf(), Buf()]
    Pm = sbuf(C, st, "cPm", [64, NQ, 64], BF16); Pmb = Buf()
    W0s = sbuf(C, st, "cW0", [64, 8, 64], BF16); W0b = Buf()
    Us = sbuf(C, st, "cUs", [64, 8, 64], BF16); Usb = Buf()
    GBs = sbuf(C, st, "cGB", [64, 528]); GBb = Buf()
    T1 = sbuf(C, st, "cT1", [64, 8, 64]); T1b = Buf()
    SQ = sbuf(C, st, "cSQ", [64, 8, 64]); SQb = Buf()
    YO = sbuf(C, st, "cYO", [64, 8, 64], BF16); YOb = Buf()
    ST = sbuf(C, st, "cST", [64, 6, 8]); STb = Buf()
    bank = [psum(C, st, f"cpb{i}", [128, 512], F32) for i in range(8)]
    bkb = [Buf() for _ in range(8)]
    P.op(DVE, lambda e: e.memset(H[:], 0.0), writes=[Hb])
    H16 = sbuf(C, st, "cH16", [64, 8, 64], BF16); H16b = Buf()
    P.op(DVE, lambda e: e.memset(H16[:], 0.0), writes=[H16b])
    XBF = {nm: sbuf(C, st, "cXB" + nm, [64, 8, W1M], BF16) for nm in ("xr", "xk", "asig", "kkn")}
    XBFb = {nm: Buf() for nm in XBF}

    def bc8(ap, W):
        return ap.unsqueeze(2).broadcast_to([64, 8, W])

    def vop(fn, reads, writes, pwrites=()):
        P.op(DVE, fn, reads=reads, writes=writes, pwrites=pwrites)

    Fl = sbuf(C, st, "cFl", [64, 8 * WM]); Flb = Buf()

    scs = [(3, 16, 0, 16)] + [(22 + 128 * i, 128, 16 + 128 * i, 64) for i in range(16)]
    def do_sc(sci, c0, W, t0, n):
        W1 = W + 1
        nch = W // n
        nq = 8 * nch
        cur = lambda nm: X[nm][:, :, 1:W1]
        prev = lambda nm: X[nm][:, :, 0:W]
        P.phase = "rwkv_pre"
        for i, nm in enumerate(("pr", "pk", "pv")):
            P.dma(SP, X[nm][:, :, 0:W1], pf[i * 512:(i + 1) * 512, c0 - 1:c0 + W].rearrange("(h d) c -> d h c", d=64), reads=[pfb], writes=[Xb[nm]])
        for j, (r0, nr) in enumerate(((12 * 128, 32), (12 * 128 + 32, 32), (13 * 128, 96))):
            P.dma(SP, lraw[0:nr, j, 0:W1], pf[r0:r0 + nr, c0 - 1:c0 + W], reads=[pfb], writes=[lrawb])
        if sci == 0:
            for nm in ("pr", "pk", "pv"):
                vop(lambda e, nm=nm: e.memset(X[nm][:, :, 0:1], 0.0), [Xb[nm]], [Xb[nm]])
            vop(lambda e: e.memset(lraw[:, :, 0:1], 0.0), [lrawb], [lrawb])
        if sci == 1:
            for i, nm in enumerate(("pr", "pk", "pv")):
                P.dma(SP, hs[0:64, 0, :], pf[i * 512:(i + 1) * 512, 18:19].rearrange("(h d) c -> d (h c)", d=64), reads=[pfb], writes=[hsb], allow_slow_non_contiguous=True)
                P.dma(SP, hs[0:64, 1, :], prm["hist_in"][i * 512:(i + 1) * 512, 2:3].rearrange("(h d) c -> d (h c)", d=64), reads=[hsb], writes=[hsb], allow_slow_non_contiguous=True)
                vop(lambda e, nm=nm: e.scalar_tensor_tensor(out=X[nm][:, :, 0:1], in0=hs[0:64, 0, :].unsqueeze(2), scalar=flagE[0:64, 0:1], in1=hs[0:64, 1, :].unsqueeze(2),
                                                            op0=ALU.mult, op1=ALU.add), [hsb, fb, Xb[nm]], [Xb[nm]])
            for j, (r0, nr) in enumerate(((12 * 128, 32), (12 * 128 + 32, 32), (13 * 128, 96))):
                P.dma(SP, hs[0:nr, 0, 0:1], pf[r0:r0 + nr, 18:19], reads=[pfb, hsb], writes=[hsb], allow_slow_non_contiguous=True)
                P.dma(SP, hs[0:nr, 1, 0:1], prm["hist_in"][r0:r0 + nr, 2:3], reads=[hsb], writes=[hsb], allow_slow_non_contiguous=True)
                vop(lambda e, j=j, nr=nr: e.scalar_tensor_tensor(out=lraw[0:nr, j, 0:1], in0=hs[0:nr, 0, 0:1], scalar=flagE[0:nr, 0:1], in1=hs[0:nr, 1, 0:1],
                                                                 op0=ALU.mult, op1=ALU.add), [hsb, fb, lrawb], [lrawb])
            P.dma(SP, Hin[:], prm["sA_in"].rearrange("h k v -> k h v"), writes=[Hinb])
            vop(lambda e: e.scalar_tensor_tensor(out=H[:], in0=H[:], scalar=flagE[0:64, 0:1], in1=Hin[:], op0=ALU.mult, op1=ALU.add), [Hb, Hinb, fb], [Hb])
            P.op(ACT, lambda e: e.activation(out=H16[:], in_=H[:], func=AF.Copy), reads=[Hb], writes=[H16b])
        for i, (src, dst) in enumerate((("pr", "xr"), ("pk", "xk"), ("pv", "xv"))):
            vop(lambda e, src=src, dst=dst: e.tensor_tensor(out=cur(dst), in0=prev(src), in1=cur(src), op=ALU.subtract), [Xb[src]], [Xb[dst]])
            vop(lambda e, dst=dst, i=i: e.tensor_tensor(out=cur(dst), in0=cur(dst), in1=bc8(muA[:, i, :], W), op=ALU.mult), [Xb[dst], pb_], [Xb[dst]])
            vop(lambda e, src=src, dst=dst: e.tensor_tensor(out=cur(dst), in0=cur(dst), in1=cur(src), op=ALU.add), [Xb[dst], Xb[src]], [Xb[dst]])
        for j, (dst, dstb, nr, fn) in enumerate(((thw, thwb, 32, AF.Tanh), (xal, xalb, 32, None), (sg, sgb, 96, AF.Sigmoid))):
            vop(lambda e, dst=dst, nr=nr, j=j: e.tensor_tensor(out=dst[0:nr, 0:W], in0=lraw[0:nr, j, 0:W], in1=lraw[0:nr, j, 1:W1], op=ALU.subtract), [lrawb], [dstb])
            vop(lambda e, dst=dst, nr=nr, j=j: e.scalar_tensor_tensor(out=dst[0:nr, 0:W], in0=dst[0:nr, 0:W], scalar=muL[0:nr, j:j + 1], in1=lraw[0:nr, j, 1:W1],
                                                                    op0=ALU.mult, op1=ALU.add), [lrawb, dstb, pb_], [dstb])
            if fn is not None:
                P.op(ACT, lambda e, dst=dst, nr=nr, fn=fn: e.activation(out=dst[0:nr, 0:W], in_=dst[0:nr, 0:W], func=fn), reads=[dstb], writes=[dstb])
        for (wt_, src, srcb, dst, chi, b0) in ((w2, thw, thwb, "sgz", 0, 0), (a2, xal, xalb, "asig", 1, 2)):
            for h in range(8):
                bk = b0 + (h * W) // 512
                off = (h * W) % 512
                P.op(PE, lambda e, bk=bk, off=off, wt_=wt_, src=src, h=h: e.matmul(bank[bk][0:64, off:off + W], lhsT=wt_[0:32, h * 64:(h + 1) * 64], rhs=src[0:32, 0:W],
                                                                                  start=True, stop=True), reads=[pb_, srcb], writes=[bkb[bk]])
            nb = (8 * W + 511) // 512
            for b in range(nb):
                h0 = b * (512 // W) if W >= 64 else 0
                nh = (512 // W) if W >= 64 else 8
                vop(lambda e, b=b, b0=b0, dst=dst, chi=chi, h0=h0, nh=nh: e.tensor_tensor(
                    out=X[dst][:, h0:h0 + nh, 1:W1], in0=bank[b0 + b][0:64, 0:nh * W].rearrange("p (h w) -> p h w", h=nh),
                    in1=ch[:, chi, h0:h0 + nh].unsqueeze(2).broadcast_to([64, nh, W]), op=ALU.add), [bkb[b0 + b], pb_], [Xb[dst]])
            P.op(ACT, lambda e, dst=dst: e.activation(out=cur(dst), in_=cur(dst), func=AF.Sigmoid), reads=[Xb[dst]], writes=[Xb[dst]])
        vop(lambda e: e.tensor_tensor(out=cur("kkn"), in0=cur("xk"), in1=bc8(ch[:, 2, :], W), op=ALU.mult), [Xb["xk"], pb_], [Xb["kkn"]])
        vop(lambda e: e.tensor_tensor(out=Fl[:, 0:8 * W].rearrange("p (h w) -> p h w", h=8), in0=cur("kkn"), in1=cur("kkn"), op=ALU.mult), [Xb["kkn"]], [Flb])
        nb = (8 * W + 511) // 512
        for b in range(nb):
            nn_ = min(512, 8 * W - b * 512)
            P.op(PE, lambda e, b=b, nn_=nn_: e.matmul(bank[4 + b][0:64, 0:nn_], lhsT=ones[0:64, 0:64], rhs=Fl[:, b * 512:b * 512 + nn_], start=True, stop=True),
                 reads=[onesb, Flb], writes=[bkb[4 + b]])
        for b in range(nb):
            nn_ = min(512, 8 * W - b * 512)
            vop(lambda e, b=b, nn_=nn_: e.tensor_scalar(out=Fl[:, b * 512:b * 512 + nn_], in0=bank[4 + b][0:64, 0:nn_],
                                                        scalar1=1e-24, scalar2=None, op0=ALU.max), [bkb[4 + b], Flb], [Flb])
        relf = Fl[:, 0:8 * W]
        P.op(ACT, lambda e, relf=relf: e.activation(out=relf, in_=relf, func=AF.Sqrt), reads=[Flb], writes=[Flb])
        vop(lambda e, relf=relf: e.reciprocal(out=relf, in_=relf), [Flb], [Flb])
        vop(lambda e, relf=relf: e.tensor_tensor(out=cur("kkn"), in0=cur("kkn"), in1=relf.rearrange("p (h w) -> p h w", h=8), op=ALU.mult),
            [Xb["kkn"], Flb], [Xb["kkn"]])
        vop(lambda e: e.tensor_tensor(out=cur("t1"), in0=cur("asig"), in1=bc8(ch[:, 3, :], W), op=ALU.mult), [Xb["asig"], pb_], [Xb["t1"]])
        vop(lambda e: e.tensor_tensor(out=cur("t1"), in0=cur("t1"), in1=bc8(ch[:, 4, :], W), op=ALU.add), [Xb["t1"], pb_], [Xb["t1"]])
        vop(lambda e: e.tensor_tensor(out=cur("xk"), in0=cur("xk"), in1=cur("t1"), op=ALU.mult), [Xb["xk"], Xb["t1"]], [Xb["xk"]])
        vop(lambda e: e.tensor_tensor(out=cur("asig"), in0=cur("asig"), in1=cur("kkn"), op=ALU.mult), [Xb["asig"], Xb["kkn"]], [Xb["asig"]])
        vop(lambda e: e.tensor_tensor(out=cur("t1"), in0=cur("xr"), in1=cur("xk"), op=ALU.mult), [Xb["xr"], Xb["xk"], Xb["t1"]], [Xb["t1"]])
        vop(lambda e: e.tensor_tensor(out=cur("pr"), in0=cur("t1"), in1=bc8(rk[:, :, 0], W), op=ALU.mult), [Xb["t1"], pb_, Xb["pr"], Xb["xr"]], [Xb["pr"]])
        vop(lambda e: e.tensor_copy(out=Fl[:, 0:8 * W].rearrange("p (h w) -> p h w", h=8), in_=cur("sgz")), [Xb["sgz"], Flb], [Flb])
        vop(lambda e: e.tensor_tensor_scan(out=Ssc[:, 1:1 + 8 * W], data0=ones[0:64, 0:8 * W], data1=Fl[:, 0:8 * W], initial=0.0, op0=ALU.mult, op1=ALU.add),
            [Flb, Sscb, onesb], [Sscb])
        vop(lambda e: e.tensor_tensor(out=cur("rel").rearrange("p h (c j) -> p h c j", j=n),
                                      in0=Ssc[:, 1:1 + 8 * W].rearrange("p (h c j) -> p h c j", h=8, j=n),
                                      in1=Ssc[:, 0:8 * W].rearrange("p (h c j) -> p h c j", h=8, j=n)[:, :, :, 0:1].broadcast_to([64, 8, nch, n]), op=ALU.subtract),
            [Sscb, Xb["rel"]], [Xb["rel"]])
        P.op(ACT, lambda e: e.activation(out=cur("G1"), in_=cur("rel"), func=AF.Exp, scale=-LDK), reads=[Xb["rel"]], writes=[Xb["G1"]])
        P.op(ACT, lambda e: e.activation(out=cur("G2"), in_=cur("rel"), func=AF.Exp, scale=LDK), reads=[Xb["rel"]], writes=[Xb["G2"]])
        vop(lambda e: e.tensor_tensor(out=cur("rel"), in0=cur("rel"), in1=cur("sgz"), op=ALU.subtract), [Xb["rel"], Xb["sgz"]], [Xb["rel"]])
        P.op(ACT, lambda e: e.activation(out=cur("rel"), in_=cur("rel"), func=AF.Exp, scale=-LDK), reads=[Xb["rel"]], writes=[Xb["rel"]])
        vop(lambda e: e.tensor_tensor(out=cur("xr"), in0=cur("xr"), in1=cur("G1"), op=ALU.mult), [Xb["xr"], Xb["G1"]], [Xb["xr"]])
        vop(lambda e: e.tensor_tensor(out=cur("xk"), in0=cur("xk"), in1=cur("G2"), op=ALU.mult), [Xb["xk"], Xb["G2"]], [Xb["xk"]])
        vop(lambda e: e.tensor_tensor(out=cur("asig"), in0=cur("asig"), in1=cur("G2"), op=ALU.mult), [Xb["asig"], Xb["G2"]], [Xb["asig"]])
        vop(lambda e: e.scalar_tensor_tensor(out=cur("kkn"), in0=cur("kkn"), scalar=-1.0, in1=cur("rel"), op0=ALU.mult, op1=ALU.mult),
            [Xb["kkn"], Xb["rel"]], [Xb["kkn"]])
        for nm in ("xr", "xk", "asig", "kkn"):
            P.op(ACT, lambda e, nm=nm: e.activation(out=XBF[nm][:, :, 1:W1], in_=cur(nm), func=AF.Copy), reads=[Xb[nm]], writes=[XBFb[nm]])
        RT, KT, BT, AT, XV, PRK, G1 = "xr", "xk", "asig", "kkn", "xv", "pr", "G1"
        colb = lambda nm, h, c: XBF[nm][:, h, 1 + c * n:1 + (c + 1) * n]
        col = lambda nm, h, c: X[nm][:, h, 1 + c * n:1 + (c + 1) * n]
        qi = lambda h, c: h * nch + c
        P.phase = "rwkv_gram"
        for a, nm in enumerate((XV, KT, BT)):
            for h in range(8):
                for c in range(nch):
                    q = qi(h, c)
                    bk, off = (q * 64) // 512, (q * 64) % 512
                    P.op(PE, lambda e, bk=bk, off=off, nm=nm, h=h, c=c: e.transpose(out=bank[bk][0:n, off:off + 64], in_=col(nm, h, c), identity=ident[0:64, 0:64]),
                         reads=[Xb[nm], idb], writes=[bkb[bk]])
            for b in range((nq * 64 + 511) // 512):
                qn = min(8, nq - b * 8)
                P.op(ACT, lambda e, a=a, b=b, qn=qn: e.activation(out=TM[0:n, a, b * 8:b * 8 + qn, :], in_=bank[b][0:n, 0:qn * 64].rearrange("p (q d) -> p q d", q=qn),
                                                               func=AF.Copy), reads=[bkb[b]], pwrites=[TMb])
        pairs = ((KT, AT), (KT, RT), (BT, AT), (BT, RT), (AT, BT))
        for j, (l_, r_) in enumerate(pairs):
            b0 = 4 if j % 2 else 0
            for h in range(8):
                for c in range(nch):
                    q = qi(h, c)
                    bk, off = b0 + (q * 64) // 512, (q * 64) % 512
                    P.op(PE, lambda e, bk=bk, off=off, l_=l_, r_=r_, h=h, c=c: e.matmul(bank[bk][0:n, off:off + n], lhsT=colb(l_, h, c), rhs=colb(r_, h, c), start=True, stop=True),
                         reads=[XBFb[l_], XBFb[r_]], writes=[bkb[bk]])
            for b in range((nq * 64 + 511) // 512):
                qn = min(8, nq - b * 8)
                vop(lambda e, j=j, b=b, b0=b0, qn=qn: e.tensor_tensor(out=MM[0:n, j, b * 8:b * 8 + qn, 0:n],
                                                                    in0=bank[b0 + b][0:n, 0:qn * 64].rearrange("p (q d) -> p q d", q=qn)[:, :, 0:n],
                                                                    in1=mask5[0:n, j, 0:n].unsqueeze(1).broadcast_to([n, qn, n]), op=ALU.mult),
                    [bkb[b0 + b], m5b], [], pwrites=[MMb])
        P.phase = "rwkv_inv"
        vop(lambda e: e.tensor_tensor(out=Pm[0:n, 0:nq, 0:n], in0=MM[0:n, 2, 0:nq, 0:n], in1=ident[0:n, 0:n].unsqueeze(1).broadcast_to([n, nq, n]), op=ALU.add),
            [MMb, idb], [Pmb])
        curN = lambda q: MM[0:n, 2, q, 0:n]
        curNT = lambda q: MM[0:n, 4, q, 0:n]
        curb = MMb
        nlev = 5 if n == 64 else 3
        for lev in range(nlev):
            nn, nnb = NN[lev % 2], NNb[lev % 2]
            for q in range(nq):
                bk, off = (q * 128) // 512, (q * 128) % 512
                P.op(PE, lambda e, bk=bk, off=off, a_=curNT(q), b_=curN(q): e.matmul(bank[bk][0:n, off:off + n], lhsT=a_, rhs=b_, start=True, stop=True), reads=[curb], writes=[bkb[bk]])
                P.op(PE, lambda e, bk=bk, off=off, a_=curN(q), b_=curNT(q): e.matmul(bank[bk][0:n, off + 64:off + 64 + n], lhsT=a_, rhs=b_, start=True, stop=True), reads=[curb], writes=[bkb[bk]])
            for b in range((nq * 128 + 511) // 512):
                qn = min(4, nq - b * 4)
                P.op(ACT, lambda e, nn=nn, b=b, qn=qn: e.activation(out=nn[0:n, b * 4:b * 4 + qn, :, 0:n],
                                                                 in_=bank[b][0:n, 0:qn * 128].rearrange("p (q j d) -> p q j d", q=qn, j=2)[:, :, :, 0:n], func=AF.Copy),
                     reads=[bkb[b]], pwrites=[nnb])
            curN = lambda q, nn=nn: nn[0:n, q, 0, 0:n]
            curNT = lambda q, nn=nn: nn[0:n, q, 1, 0:n]
            curb = nnb
            for q in range(nq):
                bk, off = 4 + (q * 64) // 512, (q * 64) % 512
                P.op(PE, lambda e, bk=bk, off=off, a_=curNT(q), q=q: e.matmul(bank[bk][0:n, off:off + n], lhsT=a_, rhs=Pm[0:n, q, 0:n], start=True, stop=True), reads=[curb, Pmb], writes=[bkb[bk]])
            for b in range((nq * 64 + 511) // 512):
                qn = min(8, nq - b * 8)
                vop(lambda e, b=b, qn=qn: e.tensor_tensor(out=Pm[0:n, b * 8:b * 8 + qn, 0:n], in0=Pm[0:n, b * 8:b * 8 + qn, 0:n],
                                                          in1=bank[4 + b][0:n, 0:qn * 64].rearrange("p (q d) -> p q d", q=qn)[:, :, 0:n], op=ALU.add),
                    [bkb[4 + b], Pmb], [Pmb])
        P.phase = "rwkv_chain"
        for c in range(nch):
            tc0 = t0 + c * n
            for h in range(8):
                q = qi(h, c)
                P.op(PE, lambda e, h=h, c=c: e.matmul(bank[0][0:n, h * 64:(h + 1) * 64], lhsT=colb(AT, h, c), rhs=H16[:, h, :], start=True, stop=False), reads=[XBFb[AT], H16b], writes=[bkb[0]])
                P.op(PE, lambda e, h=h, q=q: e.matmul(bank[0][0:n, h * 64:(h + 1) * 64], lhsT=MM[0:n, 0, q, 0:n], rhs=TM[0:n, 0, q, :], start=False, stop=True), reads=[MMb, TMb], writes=[bkb[0]])
            P.op(ACT, lambda e: e.activation(out=W0s[0:n, :, :], in_=bank[0][0:n, 0:512].rearrange("p (h d) -> p h d", h=8), func=AF.Copy), reads=[bkb[0]], writes=[W0b])
            for h in range(8):
                q = qi(h, c)
                P.op(PE, lambda e, h=h, q=q: e.matmul(bank[1][0:n, h * 64:(h + 1) * 64], lhsT=Pm[0:n, q, 0:n], rhs=W0s[0:n, h, :], start=True, stop=True), reads=[Pmb, W0b], writes=[bkb[1]])
            vop(lambda e: e.tensor_copy(out=Us[0:n, :, :], in_=bank[1][0:n, 0:512].rearrange("p (h d) -> p h d", h=8)), [bkb[1]], [Usb])
            if C.emit_out:
                for h in range(8):
                    q = qi(h, c)
                    P.op(PE, lambda e, h=h, c=c: e.matmul(bank[2][0:n, h * 64:(h + 1) * 64], lhsT=colb(RT, h, c), rhs=H16[:, h, :], start=True, stop=False), reads=[XBFb[RT], H16b], writes=[bkb[2]])
                    P.op(PE, lambda e, h=h, q=q: e.matmul(bank[2][0:n, h * 64:(h + 1) * 64], lhsT=MM[0:n, 3, q, 0:n], rhs=Us[0:n, h, :], start=False, stop=False), reads=[MMb, Usb], writes=[bkb[2]])
                    P.op(PE, lambda e, h=h, q=q: e.matmul(bank[2][0:n, h * 64:(h + 1) * 64], lhsT=MM[0:n, 1, q, 0:n], rhs=TM[0:n, 0, q, :], start=False, stop=True), reads=[MMb, TMb], writes=[bkb[2]])
            for h in range(8):
                q = qi(h, c)
                P.op(PE, lambda e, h=h, q=q: e.matmul(bank[3][0:64, h * 64:(h + 1) * 64], lhsT=TM[0:n, 2, q, :], rhs=Us[0:n, h, :], start=True, stop=False), reads=[TMb, Usb], writes=[bkb[3]])
                P.op(PE, lambda e, h=h, q=q: e.matmul(bank[3][0:64, h * 64:(h + 1) * 64], lhsT=TM[0:n, 1, q, :], rhs=TM[0:n, 0, q, :], start=False, stop=True), reads=[TMb], writes=[bkb[3]])
            ce = 1 + (c + 1) * n - 1
            vop(lambda e: e.tensor_tensor(out=H[:], in0=H[:], in1=bank[3][0:64, 0:512].rearrange("p (h d) -> p h d", h=8), op=ALU.add), [bkb[3], Hb], [Hb])
            vop(lambda e, ce=ce: e.tensor_tensor(out=H[:], in0=H[:], in1=X[G1][:, :, ce:ce + 1].broadcast_to([64, 8, 64]), op=ALU.mult), [Hb, Xb[G1]], [Hb])
            P.op(ACT, lambda e: e.activation(out=H16[:], in_=H[:], func=AF.Copy), reads=[Hb], writes=[H16b])
            if C.emit_out:
                P.op(PE, lambda e, c=c: e.matmul(bank[4][0:n, 0:512], lhsT=sg[0:96, c * n:(c + 1) * n], rhs=g2[0:96, :], start=True, stop=True), reads=[sgb, pb_], writes=[bkb[4]])
                for h in range(8):
                    P.op(PE, lambda e, h=h, c=c: e.matmul(bank[5][0:n, 2 * h:2 * h + 2], lhsT=col(PRK, h, c), rhs=ones[0:64, 0:2], start=True, stop=True), reads=[Xb[PRK], onesb], writes=[bkb[5]])
                P.op(ACT, lambda e: e.activation(out=GBs[0:n, 0:512], in_=bank[4][0:n, 0:512], func=AF.Copy), reads=[bkb[4]], writes=[GBb])
                P.op(ACT, lambda e: e.activation(out=GBs[0:n, 512:528], in_=bank[5][0:n, 0:16], func=AF.Copy), reads=[bkb[5], GBb], writes=[GBb])
                YG = bank[2][0:n, 0:512].rearrange("p (h d) -> p h d", h=8)
                bcn = lambda ap: ap.unsqueeze(2).broadcast_to([n, 8, 64])
                vop(lambda e, YG=YG: e.tensor_reduce(out=ST[0:n, 0, :], in_=YG, axis=AX.X, op=ALU.add), [bkb[2]], [STb])
                P.op(ACT, lambda e, YG=YG: e.activation(out=SQ[0:n, :, :], in_=YG, func=AF.Square), reads=[bkb[2]], writes=[SQb])
                vop(lambda e: e.tensor_reduce(out=ST[0:n, 1, :], in_=SQ[0:n, :, :], axis=AX.X, op=ALU.add), [SQb, STb], [STb])
                vop(lambda e: e.tensor_scalar(out=ST[0:n, 2, :], in0=ST[0:n, 0, :], scalar1=1.0 / 64, scalar2=None, op0=ALU.mult), [STb], [STb])
                vop(lambda e: e.tensor_tensor(out=ST[0:n, 3, :], in0=ST[0:n, 2, :], in1=ST[0:n, 2, :], op=ALU.mult), [STb], [STb])
                vop(lambda e: e.tensor_scalar(out=ST[0:n, 4, :], in0=ST[0:n, 1, :], scalar1=1.0 / 64, scalar2=64e-5, op0=ALU.mult, op1=ALU.add), [STb], [STb])
                vop(lambda e: e.tensor_tensor(out=ST[0:n, 4, :], in0=ST[0:n, 4, :], in1=ST[0:n, 3, :], op=ALU.subtract), [STb], [STb])
                P.op(POOL, lambda e: e.tensor_tensor(out=ST[0:n, 5, :], in0=ST[0:n, 4, :], in1=mh[0:n, 0:8], op=ALU.pow), reads=[STb, mhb], writes=[STb])
                vop(lambda e, YG=YG, bcn=bcn: e.tensor_tensor(out=T1[0:n, :, :], in0=YG, in1=bcn(ST[0:n, 2, :]), op=ALU.subtract), [bkb[2], STb], [T1b])
                vop(lambda e, bcn=bcn: e.tensor_tensor(out=T1[0:n, :, :], in0=T1[0:n, :, :], in1=bcn(ST[0:n, 5, :]), op=ALU.mult), [T1b, STb], [T1b])
                vop(lambda e: e.tensor_tensor(out=T1[0:n, :, :], in0=T1[0:n, :, :], in1=lnw[0:n, :].rearrange("p (h d) -> p h d", h=8), op=ALU.mult), [T1b, pb_], [T1b])
                vop(lambda e: e.tensor_tensor(out=T1[0:n, :, :], in0=T1[0:n, :, :], in1=lnb[0:n, :].rearrange("p (h d) -> p h d", h=8), op=ALU.add), [T1b, pb_], [T1b])
                vtm_c = TM[0:n, 0, 0:nq, :].rearrange("p (h c) d -> p h c d", c=nch)[:, :, c, :]
                bs_c = GBs[0:n, 512:528].rearrange("p (h t) -> p h t", t=2)[:, :, 0:1].broadcast_to([n, 8, 64])
                vop(lambda e, vtm_c=vtm_c, bs_c=bs_c: e.tensor_tensor(out=SQ[0:n, :, :], in0=vtm_c, in1=bs_c, op=ALU.mult), [TMb, GBb, SQb], [SQb])
                vop(lambda e: e.tensor_tensor(out=T1[0:n, :, :], in0=T1[0:n, :, :], in1=SQ[0:n, :, :], op=ALU.add), [T1b, SQb], [T1b])
                vop(lambda e: e.tensor_tensor(out=YO[0:n, :, :], in0=T1[0:n, :, :], in1=GBs[0:n, 0:512].rearrange("p (h d) -> p h d", h=8), op=ALU.mult), [T1b, GBb], [YOb])
                P.dma(SP, y[tc0:tc0 + n, 0:512], YO[0:n, :, :].rearrange("p h d -> p (h d)"), reads=[YOb], pwrites=[yb])
    for sci, (c0, W, t0, n) in enumerate(scs):
        do_sc(sci, c0, W, t0, n)
    P.dma(POOL, prm["sA_out"].rearrange("h k v -> k h v"), H[:], reads=[Hb], pwrites=[K.sob])


def mixer_gla2(C, st, pf, pfb, pt, ptb, y, yb, prm, K):
    P = C.P
    ones, onesb, ident, idb, mask_i, mib, flagE, fb = K.ones, K.onesb, K.ident, K.idb, K.mask_i, K.mib, K.flagE, K.fb
    a2 = sbuf(C, st, "dga2", [32, 256]); a2b = Buf()
    P.op(DVE, lambda e: e.memset(a2[:], 0.0), writes=[a2b])
    P.dma(SP, a2[0:16, :], prm["gla_a2"], reads=[a2b], writes=[a2b])
    nab = sbuf(C, st, "dgnab", [64, 4]); nabb = Buf()
    nbc = sbuf(C, st, "dgnbc", [64, 128]); nbcb = Buf()
    P.dma(SP, nab[:], prm["gla_ab"], writes=[nabb])
    P.op(DVE, lambda e: e.tensor_scalar(out=nab[:], in0=nab[:], scalar1=-1.0, scalar2=None, op0=ALU.mult), reads=[nabb], writes=[nabb])
    P.dma(SP, nbc[:], prm["gla_normbc"], writes=[nbcb])
    WM = 512
    q = sbuf(C, st, "dgq", [64, 4, WM]); k = sbuf(C, st, "dgk", [64, 4, WM]); rel = sbuf(C, st, "dgrel", [64, 4, WM])
    e1 = sbuf(C, st, "dge1", [64, 4, WM]); e2 = sbuf(C, st, "dge2", [64, 4, WM])
    Fl = sbuf(C, st, "dgFl", [64, 4 * WM]); Ssc = sbuf(C, st, "dgSsc", [64, 1 + 4 * WM]); xa = sbuf(C, st, "dgxa", [32, WM])
    qb, kb_, relb, e1b, e2b, Flb, Sscb, xab = [Buf() for _ in range(8)]
    P.op(DVE, lambda e: e.memset(Ssc[:, 0:1], 0.0), writes=[Sscb])
    S = sbuf(C, st, "dgS", [64, 4, 128]); Sb = Buf()
    Sin = sbuf(C, st, "dgSin", [64, 4, 128]); Sinb = Buf()
    P.op(DVE, lambda e: e.memset(S[:], 0.0), writes=[Sb])
    bank = [psum(C, st, f"dgb{i}", [128, 512], F32) for i in range(8)]
    bkb = [Buf() for _ in range(8)]
    vr = Ring([sbuf(C, st, f"dgv{i}", [64, 1024], F32) for i in range(3)])
    ktr = Ring([sbuf(C, st, f"dgkt{i}", [64, 256], F32) for i in range(2)])
    scr = Ring([sbuf(C, st, f"dgsc{i}", [64, 4, 64], F32) for i in range(2)])
    t1r = Ring([sbuf(C, st, f"dgt1{i}", [64, 4, 128], F32) for i in range(2)])
    yor = Ring([sbuf(C, st, f"dgyo{i}", [64, 4, 128], BF16) for i in range(2)])
    str_ = Ring([sbuf(C, st, f"dgst{i}", [64, 3, 4], F32) for i in range(2)])
    junk = sbuf(C, st, "dgjunk", [64, 4, 128], F32); jb = Buf()
    mh = sbuf(C, st, "dgmh", [64, 4], F32); mhb = Buf()
    P.op(POOL, lambda e: e.memset(mh[:], -0.5), writes=[mhb])

    def vop(fn, reads, writes, pwrites=()):
        P.op(DVE, fn, reads=reads, writes=writes, pwrites=pwrites)

    scs = [(3, 16, 0, 16)] + [(22 + 512 * i, 512, 16 + 512 * i, 64) for i in range(4)]
    cnt = [0]

    def do_sc(sci, c0, W, t0, n):
        nch = W // n
        P.dma(SP, q[:, :, 0:W], pf[14 * 128:14 * 128 + 256, c0:c0 + W].rearrange("(h d) c -> d h c", d=64), reads=[pfb], writes=[qb])
        P.dma(SP, k[:, :, 0:W], pf[16 * 128:16 * 128 + 256, c0:c0 + W].rearrange("(h d) c -> d h c", d=64), reads=[pfb], writes=[kb_])
        P.dma(SP, xa[:, 0:W], pf[18 * 128:18 * 128 + 32, c0:c0 + W], reads=[pfb], writes=[xab])
        if sci == 1:
            P.dma(SP, Sin[:], prm["sB_in"].rearrange("h k v -> k h v"), writes=[Sinb])
            vop(lambda e: e.scalar_tensor_tensor(out=S[:], in0=S[:], scalar=flagE[0:64, 0:1], in1=Sin[:], op0=ALU.mult, op1=ALU.add), [Sb, Sinb, fb], [Sb])
        STOP = 9
        if STOP <= 1:
            return
        for h in range(4):
            bk, off = (h * W) // 512, (h * W) % 512
            P.op(PE, lambda e, bk=bk, off=off, h=h: e.matmul(bank[bk][0:64, off:off + W], lhsT=a2[0:32, h * 64:(h + 1) * 64], rhs=xa[0:32, 0:W], start=True, stop=True),
                 reads=[a2b, xab], writes=[bkb[bk]])
            P.op(ACT, lambda e, bk=bk, off=off, h=h: e.activation(out=Fl[:, h * W:(h + 1) * W], in_=bank[bk][0:64, off:off + W], func=AF.Exp, scale=-1.0, bias=nab[:, h:h + 1]),
                 reads=[bkb[bk], nabb], pwrites=[Flb])
        if STOP <= 2:
            return
        P.op(ACT, lambda e: e.activation(out=Fl[:, 0:4 * W], in_=Fl[:, 0:4 * W], func=AF.Ln, bias=1.0), reads=[Flb], writes=[Flb])
        vop(lambda e: e.tensor_tensor_scan(out=Ssc[:, 1:1 + 4 * W], data0=ones[0:64, 0:4 * W], data1=Fl[:, 0:4 * W], initial=0.0, op0=ALU.mult, op1=ALU.add),
            [Flb, Sscb, onesb], [Sscb])
        vop(lambda e: e.tensor_tensor(out=rel[:, :, 0:W].rearrange("p h (c j) -> p h c j", j=n),
                                      in0=Ssc[:, 1:1 + 4 * W].rearrange("p (h c j) -> p h c j", h=4, j=n),
                                      in1=Ssc[:, 0:4 * W].rearrange("p (h c j) -> p h c j", h=4, j=n)[:, :, :, 0:1].broadcast_to([64, 4, nch, n]), op=ALU.subtract),
            [Sscb], [relb])
        if STOP <= 3:
            return
        P.op(ACT, lambda e: e.activation(out=e1[:, :, 0:W], in_=rel[:, :, 0:W], func=AF.Exp, scale=-1.0 / 16), reads=[relb], writes=[e1b])
        P.op(ACT, lambda e: e.activation(out=e2[:, :, 0:W], in_=rel[:, :, 0:W], func=AF.Exp, scale=1.0 / 16), reads=[relb], writes=[e2b])
        if STOP <= 4:
            return
        vop(lambda e: e.scalar_tensor_tensor(out=q[:, :, 0:W], in0=q[:, :, 0:W], scalar=0.125, in1=e1[:, :, 0:W], op0=ALU.mult, op1=ALU.mult), [qb, e1b], [qb])
        if STOP <= 5:
            return
        vop(lambda e: e.scalar_tensor_tensor(out=k[:, :, 0:W], in0=k[:, :, 0:W], scalar=1.0, in1=e2[:, :, 0:W], op0=ALU.mult, op1=ALU.mult), [kb_, e2b], [kb_])
        for c in range(0 if None else nch):
            tc0 = t0 + c * n
            bA, bO, bS = (4, 5, 6) if cnt[0] % 2 == 0 else (1, 2, 3)
            cnt[0] += 1
            vt, vb = vr.next()
            P.dma(SP, vt[0:n, :], pt[tc0:tc0 + n, 0:1024], reads=[ptb], writes=[vb])
            for h in range(4):
                P.op(PE, lambda e, h=h, c=c, bA=bA: e.transpose(out=bank[bA][0:n, h * 64:(h + 1) * 64], in_=k[:, h, c * n:(c + 1) * n], identity=ident[0:64, 0:64]),
                     reads=[kb_, idb], writes=[bkb[bA]])
            for h in range(4):
                P.op(PE, lambda e, h=h, c=c, bA=bA: e.matmul(bank[bA][0:n, 256 + h * 64:256 + h * 64 + n], lhsT=k[:, h, c * n:(c + 1) * n], rhs=q[:, h, c * n:(c + 1) * n],
                                                          start=True, stop=True), reads=[kb_, qb], writes=[bkb[bA]])
            kt, ktb = ktr.next(); sc, scb = scr.next()
            P.op(ACT, lambda e, kt=kt, bA=bA: e.activation(out=kt[0:n, :], in_=bank[bA][0:n, 0:256], func=AF.Copy), reads=[bkb[bA]], writes=[ktb])
            vop(lambda e, sc=sc, bA=bA: e.tensor_tensor(out=sc[0:n, :, 0:n], in0=bank[bA][0:n, 256:512].rearrange("p (h d) -> p h d", h=4)[:, :, 0:n],
                                                      in1=mask_i[0:n, 0:n].unsqueeze(1).broadcast_to([n, 4, n]), op=ALU.mult), [bkb[bA], mib], [scb])
            for h in range(4):
                P.op(PE, lambda e, h=h, c=c, bO=bO: e.matmul(bank[bO][0:n, h * 128:(h + 1) * 128], lhsT=q[:, h, c * n:(c + 1) * n], rhs=S[:, h, :], start=True, stop=False),
                     reads=[qb, Sb], writes=[bkb[bO]])
                P.op(PE, lambda e, h=h, sc=sc, vt=vt, bO=bO: e.matmul(bank[bO][0:n, h * 128:(h + 1) * 128], lhsT=sc[0:n, h, 0:n], rhs=vt[0:n, h * 128:(h + 1) * 128], start=False, stop=True),
                     reads=[scb, vb], writes=[bkb[bO]])
            for h in range(4):
                P.op(PE, lambda e, h=h, bS=bS: e.matmul(bank[bS][0:64, h * 128:(h + 1) * 128], lhsT=ident[0:64, 0:64], rhs=S[:, h, :], start=True, stop=False),
                     reads=[idb, Sb], writes=[bkb[bS]])
                P.op(PE, lambda e, h=h, kt=kt, vt=vt, bS=bS: e.matmul(bank[bS][0:64, h * 128:(h + 1) * 128], lhsT=kt[0:n, h * 64:(h + 1) * 64], rhs=vt[0:n, h * 128:(h + 1) * 128], start=False, stop=True),
                     reads=[ktb, vb], writes=[bkb[bS]])
            ce = (c + 1) * n - 1
            vop(lambda e, ce=ce, bS=bS: e.tensor_tensor(out=S[:], in0=bank[bS][0:64, 0:512].rearrange("p (h d) -> p h d", h=4),
                                                      in1=e1[:, :, ce:ce + 1].broadcast_to([64, 4, 128]), op=ALU.mult), [bkb[bS], e1b], [Sb])
            if C.emit_out and not False:
                s_, sb2 = str_.next(); t1, t1b = t1r.next(); yo, yob = yor.next()
                P.op(ACT, lambda e, t1=t1, bO=bO: e.activation(out=t1[0:n, :, :], in_=bank[bO][0:n, 0:512].rearrange("p (h d) -> p h d", h=4), func=AF.Copy),
                     reads=[bkb[bO]], writes=[t1b])
                vop(lambda e, t1=t1: e.scalar_tensor_tensor(out=junk[0:n, :, :], in0=t1[0:n, :, :], scalar=1.0, in1=t1[0:n, :, :], op0=ALU.mult, op1=ALU.mult), [t1b], [jb])
                vop(lambda e, s_=s_: e.tensor_reduce(out=s_[0:n, 0, :], in_=junk[0:n, :, :], axis=AX.X, op=ALU.add), [jb], [sb2])
                vop(lambda e, s_=s_: e.tensor_scalar(out=s_[0:n, 1, :], in0=s_[0:n, 0, :], scalar1=1.0 / 128, scalar2=EPS, op0=ALU.mult, op1=ALU.add), [sb2], [sb2])
                P.op(POOL, lambda e, s_=s_: e.tensor_tensor(out=s_[0:n, 2, :], in0=s_[0:n, 1, :], in1=mh[0:n, :], op=ALU.pow), reads=[sb2, mhb], writes=[sb2])
                vop(lambda e, t1=t1, s_=s_: e.scalar_tensor_tensor(out=t1[0:n, :, :], in0=t1[0:n, :, :], scalar=1.0, in1=s_[0:n, 2, :].unsqueeze(2).broadcast_to([n, 4, 128]), op0=ALU.mult, op1=ALU.mult),
                    [t1b, sb2], [t1b])
                vop(lambda e, t1=t1: e.scalar_tensor_tensor(out=t1[0:n, :, :], in0=t1[0:n, :, :], scalar=1.0, in1=nbc[0:n, :].unsqueeze(1).broadcast_to([n, 4, 128]), op0=ALU.mult, op1=ALU.mult),
                    [t1b, nbcb], [t1b])
                vop(lambda e, yo=yo, t1=t1, vt=vt: e.scalar_tensor_tensor(out=yo[0:n, :, :], in0=t1[0:n, :, :], scalar=1.0, in1=vt[0:n, 512:1024].rearrange("p (h d) -> p h d", h=4), op0=ALU.mult, op1=ALU.mult),
                    [t1b, vb], [yob])
                P.dma(SP, y[tc0:tc0 + n, 512:1024], yo[0:n, :, :].rearrange("p h d -> p (h d)"), reads=[yob], pwrites=[yb])

    for sci, (c0, W, t0, n) in enumerate(scs):
        do_sc(sci, c0, W, t0, n)
    P.dma(POOL, prm["sB_out"].rearrange("h k v -> k h v"), S[:], reads=[Sb], pwrites=[K.sob])


def run_gens(gens):
    gens = list(gens)
    while gens:
        for g_ in list(gens):
            try:
                next(g_)
            except StopIteration:
                gens.remove(g_)


def mixer_gla_f32(C, st, pf, pfb, pt, ptb, y, yb, prm, K, npsA=2, npsB=6):
    P = C.P
    ones, onesb, ident, idb, mask_i, mib, flagE, fb = K.ones, K.onesb, K.ident, K.idb, K.mask_i, K.mib, K.flagE, K.fb
    a2 = sbuf(C, st, "fga2", [32, 256]); a2b = Buf()
    P.op(DVE, lambda e: e.memset(a2[:], 0.0), writes=[a2b])
    nab = sbuf(C, st, "fgnab", [64, 4]); nabb = Buf()
    nbc = sbuf(C, st, "fgnbc", [64, 128]); nbcb = Buf()
    P.dma(SP, a2[0:16, :], prm["gla_a2"], reads=[a2b], writes=[a2b])
    P.dma(SP, nab[:], prm["gla_ab"], writes=[nabb])
    P.op(DVE, lambda e: e.tensor_scalar(out=nab[:], in0=nab[:], scalar1=-1.0, scalar2=None, op0=ALU.mult), reads=[nabb], writes=[nabb])
    P.dma(SP, nbc[:], prm["gla_normbc"], writes=[nbcb])
    xa = sbuf(C, st, "fgxa", [32, TP]); xab = Buf()
    P.dma(SP, xa[:], pf[18 * 128:18 * 128 + 32, :], reads=[pfb], writes=[xab])
    q = sbuf(C, st, "fgq", [64, TP]); k = sbuf(C, st, "fgk", [64, TP]); sp = sbuf(C, st, "fgsp", [64, TP])
    spc = sbuf(C, st, "fgspc", [64, TP]); e1 = sbuf(C, st, "fge1", [64, TP]); e2 = sbuf(C, st, "fge2", [64, TP])
    qb, kb_, spb, spcb, e1b, e2b = [Buf() for _ in range(6)]
    S = sbuf(C, st, "fgS", [64, 128]); Sb = Buf()
    Sin = sbuf(C, st, "fgSin", [64, 128]); Sinb = Buf()
    psA = Ring([psum(C, st, f"fgpa{i}", [128, 512], F32) for i in range(npsA)])
    psB = Ring([psum(C, st, f"fgpb{i}", [128, 512], F32) for i in range(npsB)])
    vr = Ring([sbuf(C, st, f"fgv{i}", [64, 256], F32) for i in range(3)])
    ktr = Ring([sbuf(C, st, f"fgkt{i}", [64, 64], F32) for i in range(2)])
    scr = Ring([sbuf(C, st, f"fgsc{i}", [64, 64], F32) for i in range(2)])
    str_ = Ring([sbuf(C, st, f"fgst{i}", [64, 4], F32) for i in range(2)])
    junk = sbuf(C, st, "fgjunk", [64, 128], F32); jb = Buf()
    mh = sbuf(C, st, "fgmh", [64, 1], F32); mhb = Buf()
    P.op(POOL, lambda e: e.memset(mh[:], -0.5), writes=[mhb])
    t1r = Ring([sbuf(C, st, f"fgt1{i}", [64, 128], F32) for i in range(2)])
    yor = Ring([sbuf(C, st, f"fgyo{i}", [64, 128], BF16) for i in range(2)])
    for h in range(4):
        r0 = (14 + h // 2) * 128 + (h % 2) * 64
        r1 = (16 + h // 2) * 128 + (h % 2) * 64
        P.dma(SP, q[:], pf[r0:r0 + 64, :], reads=[pfb], writes=[qb])
        P.dma(SP, k[:], pf[r1:r1 + 64, :], reads=[pfb], writes=[kb_])
        for c0 in range(3, TP, 512):
            n = min(512, TP - c0)
            pa, pab = psA.next()
            P.op(PE, lambda e, pa=pa, c0=c0, n=n, h=h: e.matmul(pa[0:64, 0:n], lhsT=a2[0:32, h * 64:(h + 1) * 64], rhs=xa[0:32, c0:c0 + n],
                                                               start=True, stop=True), reads=[a2b, xab], writes=[pab])
            P.op(ACT, lambda e, pa=pa, c0=c0, n=n, h=h: e.activation(out=sp[:, c0:c0 + n], in_=pa[0:64, 0:n], func=AF.Exp, scale=-1.0,
                                                                    bias=nab[:, h:h + 1]), reads=[pab, nabb], writes=[spb])
        P.op(ACT, lambda e: e.activation(out=sp[:, 3:TP], in_=sp[:, 3:TP], func=AF.Ln, bias=1.0), reads=[spb], writes=[spb])
        for (c0, ncol, t0) in SEGS:
            P.op(DVE, lambda e, c0=c0, ncol=ncol: e.tensor_tensor_scan(out=spc[:, c0:c0 + ncol], data0=ones[0:64, c0:c0 + ncol],
                                                                       data1=sp[:, c0:c0 + ncol], initial=0.0, op0=ALU.mult, op1=ALU.add),
                 reads=[spb, onesb], writes=[spcb])
        chunk_rel(C, sp, spb, spc, spcb, 64)
        P.op(ACT, lambda e: e.activation(out=e1[:, 3:TP], in_=sp[:, 3:TP], func=AF.Exp, scale=-1.0 / 16), reads=[spb], writes=[e1b])
        P.op(ACT, lambda e: e.activation(out=e2[:, 3:TP], in_=sp[:, 3:TP], func=AF.Exp, scale=1.0 / 16), reads=[spb], writes=[e2b])
        P.op(DVE, lambda e: e.scalar_tensor_tensor(out=q[:, 3:TP], in0=q[:, 3:TP], scalar=0.125, in1=e1[:, 3:TP], op0=ALU.mult,
                                                   op1=ALU.mult), reads=[qb, e1b], writes=[qb])
        P.op(DVE, lambda e: e.tensor_tensor(out=k[:, 3:TP], in0=k[:, 3:TP], in1=e2[:, 3:TP], op=ALU.mult), reads=[kb_, e2b], writes=[kb_])
        P.op(DVE, lambda e: e.memset(S[:], 0.0), writes=[Sb])
        for si, seg in enumerate(SEGS):
            if si == 1:
                P.dma(SP, Sin[:], prm["sB_in"][h], writes=[Sinb])
                P.op(DVE, lambda e: e.scalar_tensor_tensor(out=S[:], in0=S[:], scalar=flagE[0:64, 0:1], in1=Sin[:], op0=ALU.mult,
                                                           op1=ALU.add), reads=[Sb, Sinb, fb], writes=[Sb])
            for (c0, n, t0) in chunks_of(seg):
                vt, vb = vr.next()
                P.dma(SP, vt[0:n, 0:128], pt[t0:t0 + n, h * 128:(h + 1) * 128], reads=[ptb], writes=[vb])
                P.dma(SP, vt[0:n, 128:256], pt[t0:t0 + n, 512 + h * 128:512 + (h + 1) * 128], reads=[ptb], writes=[vb])
                pb, pbb = psB.next()
                P.op(PE, lambda e, pb=pb, c0=c0, n=n: e.transpose(out=pb[0:n, 0:64], in_=k[:, c0:c0 + n], identity=ident[0:64, 0:64]),
                     reads=[kb_, idb], writes=[pbb])
                P.op(PE, lambda e, pb=pb, c0=c0, n=n: e.matmul(pb[0:n, 64:64 + n], lhsT=k[:, c0:c0 + n], rhs=q[:, c0:c0 + n], start=True, stop=True),
                     reads=[kb_, qb], writes=[pbb])
                kt, ktb = ktr.next()
                sc, scb = scr.next()
                P.op(ACT, lambda e, kt=kt, pb=pb, n=n: e.activation(out=kt[0:n, :], in_=pb[0:n, 0:64], func=AF.Copy), reads=[pbb], writes=[ktb])
                P.op(DVE, lambda e, sc=sc, pb=pb, n=n: e.tensor_tensor(out=sc[0:n, 0:n], in0=pb[0:n, 64:64 + n], in1=mask_i[0:n, 0:n], op=ALU.mult),
                     reads=[pbb, mib], writes=[scb])
                po, pob = psB.next()
                P.op(PE, lambda e, po=po, c0=c0, n=n: e.matmul(po[0:n, 0:128], lhsT=q[:, c0:c0 + n], rhs=S[:, :], start=True, stop=False),
                     reads=[qb, Sb], writes=[pob])
                P.op(PE, lambda e, po=po, sc=sc, vt=vt, n=n: e.matmul(po[0:n, 0:128], lhsT=sc[0:n, 0:n], rhs=vt[0:n, 0:128], start=False, stop=True),
                     reads=[scb, vb], writes=[pob])
                pc, pcb = psB.next()
                P.op(PE, lambda e, pc=pc: e.matmul(pc[0:64, 0:128], lhsT=ident[0:64, 0:64], rhs=S[:, :], start=True, stop=False),
                     reads=[idb, Sb], writes=[pcb])
                P.op(PE, lambda e, pc=pc, kt=kt, vt=vt, n=n: e.matmul(pc[0:64, 0:128], lhsT=kt[0:n, 0:64], rhs=vt[0:n, 0:128], start=False, stop=True),
                     reads=[ktb, vb], writes=[pcb])
                ce = c0 + n - 1
                P.op(DVE, lambda e, pc=pc, ce=ce: e.tensor_scalar(out=S[:], in0=pc[0:64, 0:128], scalar1=e1[:, ce:ce + 1], scalar2=None, op0=ALU.mult),
                     reads=[pcb, e1b], writes=[Sb])
                if C.emit_out and not False:
                    s_, sb2 = str_.next()
                    t1, t1b = t1r.next()
                    yo, yob = yor.next()
                    P.op(ACT, lambda e, t1=t1, po=po, n=n: e.activation(out=t1[0:n, :], in_=po[0:n, 0:128], func=AF.Copy), reads=[pob], writes=[t1b])
                    P.op(ACT, lambda e, t1=t1, s_=s_, n=n: e.activation(out=junk[0:n, :], in_=t1[0:n, :], func=AF.Square, accum_out=s_[0:n, 0:1]),
                         reads=[t1b], writes=[jb, sb2])
                    P.op(DVE, lambda e, s_=s_, n=n: e.tensor_scalar(out=s_[0:n, 1:2], in0=s_[0:n, 0:1], scalar1=1.0 / 128, scalar2=EPS, op0=ALU.mult,
                                                                   op1=ALU.add), reads=[sb2], writes=[sb2])
                    P.op(POOL, lambda e, s_=s_, n=n: e.tensor_tensor(out=s_[0:n, 2:3], in0=s_[0:n, 1:2], in1=mh[0:n, :], op=ALU.pow),
                         reads=[sb2, mhb], writes=[sb2])
                    P.op(DVE, lambda e, t1=t1, s_=s_, n=n: e.scalar_tensor_tensor(out=t1[0:n, :], in0=t1[0:n, :], scalar=s_[0:n, 2:3],
                                                                               in1=nbc[0:n, :], op0=ALU.mult, op1=ALU.mult),
                         reads=[sb2, nbcb, t1b], writes=[t1b])
                    P.op(DVE, lambda e, yo=yo, t1=t1, vt=vt, n=n: e.tensor_tensor(out=yo[0:n, :], in0=t1[0:n, :], in1=vt[0:n, 128:256], op=ALU.mult),
                         reads=[t1b, vb], writes=[yob])
                    P.dma(SP, y[t0:t0 + n, 512 + h * 128:512 + (h + 1) * 128], yo[0:n, :], reads=[yob], pwrites=[yb])
                yield
        P.dma(POOL, prm["sB_out"][h], S[:], reads=[Sb], pwrites=[K.sob])


import contextlib
import numpy as np

PRM_SHAPES = {
    "gla_a2": [16, 256], "gla_ab": [64, 4], "gla_normbc": [64, 128],
    "ml_cw": [128, 8, 4], "ml_cb": [128, 8], "ml_ib": [4, 1], "ml_fb": [4, 1], "ml_normbc": [64, 1024], "onehot": [4, 4, 128],
    "rw_muA": [64, 3, 8], "rw_muL": [96, 3], "rw_w2": [32, 512], "rw_a2": [32, 512], "rw_g2": [96, 512], "rw_ch": [64, 4, 8],
    "rw_rk": [64, 8, 2], "rw_lnw_bc": [64, 512], "rw_lnb_bc": [64, 512],
    "sA_in": [8, 64, 64], "sB_in": [4, 64, 128], "sC_in": [4, 128, 257], "mC_in": [4, 1], "hist_in": [NFMB * 128, 3],
    "flagE": [128, 1], "mask_i": [64, 64], "mask5": [64, 5, 64],
}
OUT_SHAPES = {"sA_out": [8, 64, 64], "sB_out": [4, 64, 128], "sC_out": [4, 128, 257], "mC_out": [4, 1], "hist_out": [NFMB * 128, 3]}


def host_consts():
    j = np.arange(64)
    mi = (j[None, :] >= j[:, None]).astype(np.float32)
    ms = (j[None, :] > j[:, None]).astype(np.float32)
    ml = (j[None, :] < j[:, None]).astype(np.float32)
    mask5 = np.stack([ms, mi, ms, mi, ml], 1)
    oh = np.zeros((4, 4, 128), np.float32)
    for h in range(4):
        oh[h, h, :] = 1.0
    return {"mask_i": mi, "mask5": np.ascontiguousarray(mask5), "onehot": oh}


def host_layer_params(z, l):
    f = lambda a: np.ascontiguousarray(a, dtype=np.float32)
    chT = lambda v: f(v.reshape(8, 64).T)
    mu = z["rw_mu"][l]
    d = {}
    d["gla_a2"] = f(z["gla_a2"][l]); d["gla_ab"] = f(z["gla_ab"][l].reshape(4, 64).T)
    d["gla_normbc"] = f(np.broadcast_to(z["gla_norm"][l], (64, 128)))
    cw = z["ml_conv_w"][l]
    d["ml_cw"] = f(cw.reshape(4, 8, 128).transpose(2, 1, 0)); d["ml_cb"] = f(z["ml_conv_b"][l].reshape(8, 128).T)
    d["ml_ib"] = f(z["ml_ib"][l].reshape(4, 1)); d["ml_fb"] = f(z["ml_fb"][l].reshape(4, 1))
    d["ml_normbc"] = f(np.broadcast_to(z["ml_norm"][l], (64, 1024)))
    d["rw_muA"] = f(np.stack([chT(mu[0:512]), chT(mu[512:1024]), chT(mu[1024:1536])], 1))
    muL = np.zeros((96, 3), np.float32); muL[0:32, 0] = mu[1536:1568]; muL[0:32, 1] = mu[1568:1600]; muL[0:96, 2] = mu[1600:1696]
    d["rw_muL"] = muL
    d["rw_w2"] = f(z["rw_w2"][l]); d["rw_a2"] = f(z["rw_a2"][l]); d["rw_g2"] = f(z["rw_g2"][l])
    d["rw_ch"] = f(np.stack([chT(z["rw_w0"][l]), chT(z["rw_a0"][l]), chT(z["rw_kk"][l]), chT(z["rw_ka"][l])], 1))
    rk = z["rw_rk"][l]
    d["rw_rk"] = f(np.stack([rk.T, rk.T], 2))
    d["rw_lnw_bc"] = f(np.broadcast_to(z["rw_ln_w"][l], (64, 512))); d["rw_lnb_bc"] = f(np.broadcast_to(z["rw_ln_b"][l], (64, 512)))
    return d


def host_layer_weights(z, l):
    return {"win": hp.prep_win(z["w_in"][l]), "wout": hp.prep_sq(z["w_out"][l], 4), "w1": hp.prep_sq(z["ffn_w1"][l], 11),
            "w3": hp.prep_sq(z["ffn_w3"][l], 11), "w2": hp.prep_w2(z["ffn_w2"][l]), "g1": hp.gT(z["norm_mix"][l]), "g2": hp.gT(z["norm_ffn"][l])}


def build_layer(debug=False, emit_out=True, do_final=True):
    nc = bass.Bass("TRN2", target_bir_lowering=False)
    C = Ctx(); C.nc = nc; C.P = Prog(nc, same_engine_sync=True); C.emit_out = emit_out
    C.P.scopes = False
    P = C.P
    dr = lambda n, s, dt=F32, kind="ExternalInput": nc.dram_tensor(n, s, dt, kind=kind).ap()
    hin = dr("hin", [NTOK, D])
    win = dr("win", [13, 128, 8192]); wout = dr("wout", [4, 128, 8192])
    w1 = dr("w1", [11, 128, 8192]); w3 = dr("w3", [11, 128, 8192]); w2 = dr("w2", [4, 4, 128, 11 * 512])
    g1 = dr("g1", [128, 16]); g2 = dr("g2", [128, 16]); gf = dr("gf", [128, D])
    prm = {k: dr(k, s) for k, s in PRM_SHAPES.items()}
    for k, s in OUT_SHAPES.items():
        prm[k] = dr(k, s, F32, "ExternalOutput")
    dk = "ExternalOutput" if debug else "Internal"
    pf = dr("pf", [NFMB * 128, TP], F32, dk)
    pt = dr("pt", [NTOK, NTMC], F32, dk)
    y = dr("y", [NTOK, D], BF16, dk)
    hmid = dr("hmid", [NTOK, D], F32, dk)
    hout = dr("hout", [NTOK, D], F32, "ExternalOutput")
    out = dr("out", [NTOK - 16, D], F32, "ExternalOutput")
    aT = dr("aT", [5, 128, 44, 512], BF16, "Internal")
    hb, pfb, ptb, yb, hmb, hob, ob, ab = [Buf() for _ in range(8)]
    K = Ctx(); K.sob = Buf()
    with contextlib.ExitStack() as st0:
        K.ident = sbuf(C, st0, "ident", [128, 128], F32); identb = sbuf(C, st0, "identb", [128, 128], BF16)
        g1t = sbuf(C, st0, "g1t", [128, 16]); g2t = sbuf(C, st0, "g2t", [128, 16])
        K.idb, idbb, g1b, g2b = [Buf() for _ in range(4)]
        P.op(POOL, lambda e: e.memset(K.ident[:], 1.0), writes=[K.idb])
        P.op(POOL, lambda e: e.affine_select(out=K.ident[:], in_=K.ident[:], pattern=[[-1, 128]], base=0, channel_multiplier=1,
                                             compare_op=ALU.is_equal, fill=0.0), reads=[K.idb], writes=[K.idb])
        P.op(POOL, lambda e: e.tensor_copy(out=identb[:], in_=K.ident[:]), reads=[K.idb], writes=[idbb])
        P.dma(SP, g1t[:], g1, writes=[g1b]); P.dma(SP, g2t[:], g2, writes=[g2b])
        with contextlib.ExitStack() as st1:
            uT = sbuf(C, st1, "uT", [128, 16, NTOK], BF16); ub = Buf()
            pst = Ring([psum(C, st1, f"pst{i}", [128, 1024], BF16) for i in range(2)])
            psm = Ring([psum(C, st1, f"psm{i}", [128, 512], F32) for i in range(6)])
            with contextlib.ExitStack() as st:
                P.phase = "norm"
                phase_norm(C, st, hin, hb, g1t, g1b, uT, ub, pst, identb, idbb)
            P.barrier()
            with contextlib.ExitStack() as st:
                P.phase = "proj"
                phase_proj(C, st, uT, ub, win, pf, pfb, pt, ptb, psm, prm["hist_out"], K.sob)
            P.barrier()
        with contextlib.ExitStack() as st1:
            K.ones = sbuf(C, st1, "ones", [64, TP]); K.onesb = Buf()
            K.mask_i = sbuf(C, st1, "mask_i", [64, 64]); K.mib = Buf()
            K.mask5 = sbuf(C, st1, "mask5", [64, 5, 64]); K.m5b = Buf()
            K.flagE = sbuf(C, st1, "flagE", [128, 1]); K.fb = Buf()
            P.op(POOL, lambda e: e.memset(K.ones[:], 1.0), writes=[K.onesb])
            P.dma(SP, K.mask_i[:], prm["mask_i"], writes=[K.mib]); P.dma(SP, K.mask5[:], prm["mask5"], writes=[K.m5b])
            P.dma(SP, K.flagE[:], prm["flagE"], writes=[K.fb])
            with contextlib.ExitStack() as st:
                P.phase = "prepass"
                gate_prepass(C, st, pt, ptb)
            P.barrier()
            with contextlib.ExitStack() as st:
                P.phase = "gla"
                if None:
                    with contextlib.ExitStack() as st2:
                        mixer_gla2(C, st2, pf, pfb, pt, ptb, y, yb, prm, K)
                    P.barrier()
                    P.phase = "mlstm"
                    run_gens([mixer_mlstm(C, st, pf, pfb, pt, ptb, y, yb, prm, K, 4, 2)])
                else:
                    if None:
                        with contextlib.ExitStack() as st2:
                            run_gens([mixer_gla_f32(C, st2, pf, pfb, pt, ptb, y, yb, prm, K)])
                        P.barrier()
                        P.phase = "mlstm"
                        with contextlib.ExitStack() as st2:
                            run_gens([mixer_mlstm(C, st2, pf, pfb, pt, ptb, y, yb, prm, K)])
                    else:
                        run_gens([mixer_gla(C, st, pf, pfb, pt, ptb, y, yb, prm, K, 1, 3), mixer_mlstm(C, st, pf, pfb, pt, ptb, y, yb, prm, K, 3, 1)])
            P.barrier()
            if True:
              with contextlib.ExitStack() as st:
                P.phase = "rwkv"
                mixer_rwkv3(C, st, pf, pfb, y, yb, prm, K)
            P.barrier()
        if emit_out:
            with contextlib.ExitStack() as st1:
                uT = sbuf(C, st1, "uT2", [128, 16, NTOK], BF16); ub = Buf()
                pst = Ring([psum(C, st1, f"pst{i}", [128, 1024], BF16) for i in range(2)])
                psm = Ring([psum(C, st1, f"psm{i}", [128, 512], F32) for i in range(6)])
                with contextlib.ExitStack() as st:
                    P.phase = "wout"
                    phase_wout(C, st, y, yb, hin, hb, hmid, hmb, wout, uT, ub, psm, pst, identb, idbb)
                P.barrier()
                with contextlib.ExitStack() as st:
                    P.phase = "norm"
                    phase_norm(C, st, hmid, hmb, g2t, g2b, uT, ub, pst, identb, idbb)
                P.barrier()
                with contextlib.ExitStack() as st:
                    P.phase = "ffn1"
                    phase_ffn1(C, st, uT, ub, w1, w3, aT, ab, psm)
                P.barrier()
    if emit_out:
        with contextlib.ExitStack() as st:
            psm = Ring([psum(C, st, f"psn{i}", [128, 512], F32) for i in range(6)])
            P.phase = "ffn2"
            phase_ffn2(C, st, aT, ab, w2, hmid, hmb, hout, hob, psm)
        P.barrier()
        if do_final:
            with contextlib.ExitStack() as st:
                gft = sbuf(C, st, "gft2", [128, D]); gfb = Buf()
                P.dma(SP, gft[:], gf, writes=[gfb])
                P.phase = "final"
                phase_final_norm(C, st, hout, hob, gft, gfb, out, ob)
    fin = [K.sob, hob, ob]
    if debug:
        fin += [pfb, ptb, yb, hmb]
    P.finish(fin)
    P.emit()
    C.counts = {e: (len(P.ops[e]), sum(1 for o in P.ops[e] if o.signal)) for e in ENGS}
    return nc, C


import contextlib
import numpy as np

LAYER_KEYS = ["gla_a2", "gla_ab", "gla_normbc", "ml_cw", "ml_cb", "ml_ib", "ml_fb", "ml_normbc", "rw_muA", "rw_muL", "rw_w2", "rw_a2",
              "rw_g2", "rw_ch", "rw_rk", "rw_lnw_bc", "rw_lnb_bc"]
STATE_KEYS = ["sA", "sB", "sC", "mC", "hist"]


def emit_half(C, K, T, l, half):
    P = C.P
    hin, hinb = T["hin"][(l, half)]
    hout, houtb = T["hout"][(l, half)]
    prm = {k: T["lp"][k][l] for k in LAYER_KEYS}
    for k in ("mask_i", "mask5", "onehot"):
        prm[k] = T["const"][k]
    prm["flagE"] = T["flag1"] if half == 0 else T["flag0"]
    for k in STATE_KEYS:
        prm[k + "_in"] = T["zstate"][k] if half == 0 else T["state"][k][l]
        prm[k + "_out"] = T["state"][k][l] if half == 0 else T["sdump"][k]
    K.sob = T["stateb"][l] if half == 0 else T["sdumpb"]
    K.fb = Buf()
    win, wout, w1, w3, w2 = T["win"][l], T["wout"][l], T["w1"][l], T["w3"][l], T["w2"][l]
    pf, pfb, pt, ptb, y, yb, hmid, hmb, aT, ab = T["pf"], T["pfb"], T["pt"], T["ptb"], T["y"], T["yb"], T["hmid"], T["hmb"], T["aT"], T["ab"]
    identb, idbb = K.identb, K.idbb
    with contextlib.ExitStack() as st1:
        uT = sbuf(C, st1, "uT", [128, 16, NTOK], BF16); ub = Buf()
        pst = Ring([psum(C, st1, f"pst{i}", [128, 1024], BF16) for i in range(2)])
        psm = Ring([psum(C, st1, f"psm{i}", [128, 512], F32) for i in range(6)])
        with contextlib.ExitStack() as st:
            phase_norm(C, st, hin, hinb, K.g1t[l], K.g1b, uT, ub, pst, identb, idbb)
        P.barrier()
        with contextlib.ExitStack() as st:
            phase_proj(C, st, uT, ub, win, pf, pfb, pt, ptb, psm, prm["hist_out"], K.sob)
        P.barrier()
    with contextlib.ExitStack() as st1:
        K.flagE = sbuf(C, st1, "flagE", [128, 1])
        P.dma(SP, K.flagE[:], prm["flagE"], writes=[K.fb])
        K.ones = sbuf(C, st1, "ones", [64, TP]); K.onesb = Buf()
        K.mask_i = sbuf(C, st1, "mask_i", [64, 64]); K.mib = Buf()
        K.mask5 = sbuf(C, st1, "mask5", [64, 5, 64]); K.m5b = Buf()
        P.op(POOL, lambda e: e.memset(K.ones[:], 1.0), writes=[K.onesb])
        P.dma(SP, K.mask_i[:], T["const"]["mask_i"], writes=[K.mib]); P.dma(SP, K.mask5[:], T["const"]["mask5"], writes=[K.m5b])
        with contextlib.ExitStack() as st:
            gate_prepass(C, st, pt, ptb)
        P.barrier()
        with contextlib.ExitStack() as st:
            run_gens([mixer_gla_f32(C, st, pf, pfb, pt, ptb, y, yb, prm, K)])
        P.barrier()
        with contextlib.ExitStack() as st:
            run_gens([mixer_mlstm(C, st, pf, pfb, pt, ptb, y, yb, prm, K)])
        P.barrier()
        with contextlib.ExitStack() as st:
            mixer_rwkv3(C, st, pf, pfb, y, yb, prm, K)
        P.barrier()
    with contextlib.ExitStack() as st1:
        uT = sbuf(C, st1, "uT2", [128, 16, NTOK], BF16); ub = Buf()
        pst = Ring([psum(C, st1, f"pst{i}", [128, 1024], BF16) for i in range(2)])
        psm = Ring([psum(C, st1, f"psm{i}", [128, 512], F32) for i in range(6)])
        with contextlib.ExitStack() as st:
            phase_wout(C, st, y, yb, hin, hinb, hmid, hmb, wout, uT, ub, psm, pst, identb, idbb)
        P.barrier()
        with contextlib.ExitStack() as st:
            phase_norm(C, st, hmid, hmb, K.g2t[l], K.g1b, uT, ub, pst, identb, idbb)
        P.barrier()
        with contextlib.ExitStack() as st:
            phase_ffn1(C, st, uT, ub, w1, w3, aT, ab, psm)
        P.barrier()
    with contextlib.ExitStack() as st:
        psm = Ring([psum(C, st, f"psn{i}", [128, 512], F32) for i in range(6)])
        phase_ffn2(C, st, aT, ab, w2, hmid, hmb, hout, houtb, psm)
    P.barrier()
    if l == 1:
        with contextlib.ExitStack() as st:
            gft = sbuf(C, st, "gft2", [128, D]); gfb = Buf()
            P.dma(SP, gft[:], T["gf"], writes=[gfb])
            phase_final_norm(C, st, hout, houtb, gft, gfb, T["out"][half], T["outb"])
        P.barrier()


def build_fused(nlayers=2, halves=(0, 1)):
    nc = bass.Bass("TRN2", target_bir_lowering=False)
    C = Ctx(); C.nc = nc; C.P = Prog(nc); C.emit_out = True
    P = C.P
    dr = lambda n, s, dt=F32, kind="ExternalInput": nc.dram_tensor(n, s, dt, kind=kind).ap()
    T = {}
    xin = [dr("xE", [NTOK, D]), dr("xO", [NTOK, D])]
    T["win"] = dr("win", [2, 13, 128, 8192]); T["wout"] = dr("wout", [2, 4, 128, 8192])
    T["w1"] = dr("w1", [2, 11, 128, 8192]); T["w3"] = dr("w3", [2, 11, 128, 8192]); T["w2"] = dr("w2", [2, 4, 4, 128, 11 * 512])
    g1 = dr("g1", [2, 128, 16]); g2 = dr("g2", [2, 128, 16]); T["gf"] = dr("gf", [128, D])
    T["lp"] = {k: dr(k, [2] + PRM_SHAPES[k]) for k in LAYER_KEYS}
    T["const"] = {k: dr(k, PRM_SHAPES[k]) for k in ("mask_i", "mask5", "onehot")}
    T["flag1"] = dr("flag1", [128, 1]); T["flag0"] = dr("flag0", [128, 1])
    T["zstate"] = {k: dr("z_" + k, PRM_SHAPES[k + "_in"]) for k in STATE_KEYS}
    T["state"] = {k: dr("st_" + k, [2] + PRM_SHAPES[k + "_in"], F32, "Internal") for k in STATE_KEYS}
    T["sdump"] = {k: dr("sd_" + k, PRM_SHAPES[k + "_in"], F32, "Internal") for k in STATE_KEYS}
    T["stateb"] = [Buf(), Buf()]; T["sdumpb"] = Buf()
    T["pf"] = dr("pf", [NFMB * 128, TP], F32, "Internal"); T["pt"] = dr("pt", [NTOK, NTMC], F32, "Internal")
    T["y"] = dr("y", [NTOK, D], BF16, "Internal"); T["hmid"] = dr("hmid", [NTOK, D], F32, "Internal")
    T["aT"] = dr("aT", [5, 128, 44, 512], BF16, "Internal")
    for k in ("pfb", "ptb", "yb", "hmb", "ab", "outb"):
        T[k] = Buf()
    h1 = [dr("h1E", [NTOK, D], F32, "Internal"), dr("h1O", [NTOK, D], F32, "Internal")]
    h2 = [dr("h2E", [NTOK, D], F32, "Internal"), dr("h2O", [NTOK, D], F32, "Internal")]
    T["out"] = [dr("outE", [2048, D], F32, "ExternalOutput"), dr("outO", [2048, D], F32, "ExternalOutput")]
    xb = [Buf(), Buf()]; h1b = [Buf(), Buf()]; h2b = [Buf(), Buf()]
    T["hin"] = {(0, 0): (xin[0], xb[0]), (0, 1): (xin[1], xb[1]), (1, 0): (h1[0], h1b[0]), (1, 1): (h1[1], h1b[1])}
    T["hout"] = {(0, 0): (h1[0], h1b[0]), (0, 1): (h1[1], h1b[1]), (1, 0): (h2[0], h2b[0]), (1, 1): (h2[1], h2b[1])}
    K = Ctx()
    with contextlib.ExitStack() as st0:
        K.ident = sbuf(C, st0, "ident", [128, 128], F32); K.identb = sbuf(C, st0, "identb", [128, 128], BF16)
        K.g1t = [sbuf(C, st0, f"g1t{l}", [128, 16]) for l in range(2)]; K.g2t = [sbuf(C, st0, f"g2t{l}", [128, 16]) for l in range(2)]
        K.idb, K.idbb, K.g1b = Buf(), Buf(), Buf()
        P.op(POOL, lambda e: e.memset(K.ident[:], 1.0), writes=[K.idb])
        P.op(POOL, lambda e: e.affine_select(out=K.ident[:], in_=K.ident[:], pattern=[[-1, 128]], base=0, channel_multiplier=1,
                                             compare_op=ALU.is_equal, fill=0.0), reads=[K.idb], writes=[K.idb])
        P.op(POOL, lambda e: e.tensor_copy(out=K.identb[:], in_=K.ident[:]), reads=[K.idb], writes=[K.idbb])
        for l in range(2):
            P.dma(SP, K.g1t[l][:], g1[l], pwrites=[K.g1b]); P.dma(SP, K.g2t[l][:], g2[l], pwrites=[K.g1b])
        for l in range(nlayers):
            for half in halves:
                emit_half(C, K, T, l, half)
    P.finish([T["outb"], T["sdumpb"], T["stateb"][0], T["stateb"][1]])
    P.emit()
    C.counts = {e: (len(P.ops[e]), sum(1 for o in P.ops[e] if o.signal)) for e in ENGS}
    return nc, C


from concourse.bass_utils import run_bass_kernel_spmd

_PROG = {}


def kernel(**z):
    x = np.asarray(z["x"], np.float32)
    meta = np.asarray(z["meta_tokens"], np.float32)
    if "nc" not in _PROG:
        _PROG["nc"] = build_fused()[0]
    nc = _PROG["nc"]
    shared = {}
    shared.update(host_consts())
    shared["gf"] = np.ascontiguousarray(np.broadcast_to(np.asarray(z["norm_final"], np.float32), (128, D)))
    Ws = [host_layer_weights(z, l) for l in range(2)]
    for k in ("win", "wout", "w1", "w3", "w2", "g1", "g2"):
        shared[k] = np.stack([Ws[0][k], Ws[1][k]])
    Ps = [host_layer_params(z, l) for l in range(2)]
    for k in LAYER_KEYS:
        shared[k] = np.stack([Ps[0][k], Ps[1][k]])
    shared["flag1"] = np.ones((128, 1), np.float32)
    shared["flag0"] = np.zeros((128, 1), np.float32)
    for k in STATE_KEYS:
        shared["z_" + k] = np.zeros(PRM_SHAPES[k + "_in"], np.float32)
    in_maps = []
    for c in range(8):
        b = c % 4
        im = dict(shared)
        im["xE"] = np.ascontiguousarray(np.concatenate([meta, x[b, :2048]], 0))
        im["xO"] = np.ascontiguousarray(np.concatenate([meta, x[b, 2048:]], 0))
        in_maps.append(im)
    res = run_bass_kernel_spmd(nc, in_maps, core_ids=list(range(8))).results
    out = np.zeros((4, 4096, D), np.float32)
    for b in range(4):
        out[b, :2048] = np.asarray(res[b]["outE"], np.float32)
        out[b, 2048:] = np.asarray(res[b]["outO"], np.float32)
    return out
```
